# Optimizing a Trainium2 kernel written in Bass

```python
import math
import jax, jax.numpy as jnp
from jax import lax
import numpy as np

D_MODEL = 1024
BATCH = 2
SEQ = 8192
DEPTH = 2

CHUNK = 64
Q_BLOCK = 128
ROPE_THETA = 500000.0
ROPE_FRACTION = 4
NORM_EPS = 1e-6
N_MIXERS = 2
DIFF_HEAD_DIM = 64
DIFF_HEADS = D_MODEL // (2 * DIFF_HEAD_DIM)
DIFF_V_DIM = 2 * DIFF_HEAD_DIM
DSA_HEAD_DIM = 64
DSA_HEADS = D_MODEL // DSA_HEAD_DIM
DSA_Q_RANK = D_MODEL // 4
IDX_HEADS = 8
IDX_DIM = 64
TOPK_MAX = 256
D_FF = 4 * D_MODEL

N_DIFF_LAYERS = (DEPTH + 1) // 2
N_DSA_LAYERS = DEPTH // 2

kernel_name = "hybrid_diffattn_dsa_chunk_causal"


def rms_norm(x, g=None):
    xf = x.astype(jnp.float32)
    y = xf * lax.rsqrt(jnp.mean(xf * xf, axis=-1, keepdims=True) + NORM_EPS)
    if g is not None:
        y = y * g.astype(jnp.float32)
    return y.astype(x.dtype)


def rope_tables(positions, head_dim):
    rot = head_dim // ROPE_FRACTION
    inv = ROPE_THETA ** (-jnp.arange(0, rot, 2, dtype=jnp.float32) / rot)
    ang = positions.astype(jnp.float32)[..., None] * inv
    return jnp.cos(ang)[:, :, None, :], jnp.sin(ang)[:, :, None, :]


def apply_partial_rope(x, cos, sin):
    half = cos.shape[-1]
    rot = 2 * half
    x1 = x[..., :half].astype(jnp.float32)
    x2 = x[..., half:rot].astype(jnp.float32)
    r = jnp.concatenate([x1 * cos - x2 * sin, x2 * cos + x1 * sin], axis=-1).astype(x.dtype)
    return jnp.concatenate([r, x[..., rot:]], axis=-1)


def diff_attention(h, w_in, q_g, k_g, lam_q1, lam_k1, lam_q2, lam_k2, subln_g, w_out,
                   cos, sin, lam_init):
    B, S, _ = h.shape
    H, d = DIFF_HEADS, DIFF_HEAD_DIM
    proj = h @ w_in
    q, k, v = jnp.split(proj, [2 * H * d, 4 * H * d], axis=-1)
    q = apply_partial_rope(rms_norm(q.reshape(B, S, 2 * H, d), q_g), cos, sin) * (d ** -0.5)
    k = apply_partial_rope(rms_norm(k.reshape(B, S, 2 * H, d), k_g), cos, sin)
    q = q.reshape(B, S, H, 2, d)
    k = k.reshape(B, S, H, 2, d)
    v = v.reshape(B, S, H, DIFF_V_DIM)
    lam = (jnp.exp(jnp.sum(lam_q1.astype(jnp.float32) * lam_k1.astype(jnp.float32)))
           - jnp.exp(jnp.sum(lam_q2.astype(jnp.float32) * lam_k2.astype(jnp.float32)))
           + lam_init)
    n_blocks = S // Q_BLOCK
    key_chunk = jnp.arange(S) // CHUNK
    q_blocks = q.reshape(B, n_blocks, Q_BLOCK, H, 2, d).swapaxes(0, 1)

    def block(args):
        qb, start = args
        q_chunk = (start + jnp.arange(Q_BLOCK)) // CHUNK
        mask = key_chunk[None, :] <= q_chunk[:, None]
        s = jnp.einsum('bqhcd,bkhcd->bhcqk', qb, k).astype(jnp.float32)
        p = jax.nn.softmax(jnp.where(mask, s, -jnp.inf), axis=-1)
        a = p[:, :, 0] - lam * p[:, :, 1]
        return jnp.einsum('bhqk,bkhe->bqhe', a.astype(v.dtype), v)

    o = lax.map(block, (q_blocks, jnp.arange(n_blocks) * Q_BLOCK))
    o = o.swapaxes(0, 1).reshape(B, S, H, DIFF_V_DIM)
    o = rms_norm(o, subln_g) * (1.0 - lam_init)
    return o.reshape(B, S, H * DIFF_V_DIM) @ w_out


def dsa_attention(h, w_in, cq_g, w_uq, w_uq_idx, q_g, k_g, w_out, cos, sin):
    B, S, _ = h.shape
    H, d, R, HI, DI = DSA_HEADS, DSA_HEAD_DIM, DSA_Q_RANK, IDX_HEADS, IDX_DIM
    proj = h @ w_in
    c_q, k, v, k_idx, w_idx = jnp.split(
        proj, [R, R + H * d, R + 2 * H * d, R + 2 * H * d + DI], axis=-1)
    c_q = rms_norm(c_q, cq_g)
    q = (c_q @ w_uq).reshape(B, S, H, d)
    q_idx = apply_partial_rope((c_q @ w_uq_idx).reshape(B, S, HI, DI), cos, sin)
    k_idx = apply_partial_rope(rms_norm(k_idx)[:, :, None, :], cos, sin)[:, :, 0]
    w_idx = w_idx * ((HI ** -0.5) * (DI ** -0.5))
    q = apply_partial_rope(rms_norm(q, q_g), cos, sin) * (d ** -0.5)
    k = apply_partial_rope(rms_norm(k.reshape(B, S, H, d), k_g), cos, sin)
    v = v.reshape(B, S, H, d)
    topk = min(TOPK_MAX, S // 4)
    n_blocks = S // Q_BLOCK
    key_chunk = jnp.arange(S) // CHUNK
    qb_all = q.reshape(B, n_blocks, Q_BLOCK, H, d).swapaxes(0, 1)
    qib_all = q_idx.reshape(B, n_blocks, Q_BLOCK, HI, DI).swapaxes(0, 1)
    wb_all = w_idx.reshape(B, n_blocks, Q_BLOCK, HI).swapaxes(0, 1)
    gather = jax.vmap(lambda arr, ids: arr[ids])

    def block(args):
        qb, qib, wb, start = args
        q_chunk = (start + jnp.arange(Q_BLOCK)) // CHUNK
        mask = key_chunk[None, :] <= q_chunk[:, None]
        dots = jnp.einsum('bqhe,bke->bqhk', qib, k_idx).astype(jnp.float32)
        score = jnp.einsum('bqh,bqhk->bqk', wb.astype(jnp.float32), jax.nn.relu(dots))
        score = jnp.where(mask, score, -jnp.inf)
        vals, idx = lax.top_k(score, topk)
        valid = jnp.isfinite(vals)
        kg = gather(k, idx)
        vg = gather(v, idx)
        s = jnp.einsum('bqhd,bqkhd->bhqk', qb, kg).astype(jnp.float32)
        p = jax.nn.softmax(jnp.where(valid[:, None], s, -jnp.inf), axis=-1)
        return jnp.einsum('bhqk,bqkhd->bqhd', p.astype(vg.dtype), vg)

    o = lax.map(block, (qb_all, qib_all, wb_all, jnp.arange(n_blocks) * Q_BLOCK))
    o = o.swapaxes(0, 1).reshape(B, S, H * d)
    return o @ w_out


def sq_relu_mlp(h, w1, w2):
    u = jax.nn.relu(h @ w1)
    return (u * u) @ w2


def setup_inputs(seed: int = 0) -> dict:
    key = jax.random.key(seed)
    ks = iter(jax.random.split(key, 32))
    D = D_MODEL

    def nrm(shape, scale):
        return jax.random.normal(next(ks), shape, jnp.float32) * scale

    def gain(shape):
        return 1.0 + nrm(shape, 0.02)

    nA, nB = N_DIFF_LAYERS, N_DSA_LAYERS
    dsa_in = DSA_Q_RANK + 2 * DSA_HEADS * DSA_HEAD_DIM + IDX_DIM + IDX_HEADS
    x = nrm((BATCH, SEQ, D), 1.0)
    offset = jax.random.randint(next(ks), (BATCH, 1), 0, 4096, dtype=jnp.int32)
    positions = (offset + jnp.arange(SEQ, dtype=jnp.int32)[None, :]).astype(jnp.int32)
    return {
        "x": x,
        "positions": positions,
        "norm_mix": gain((DEPTH, D)),
        "norm_mlp": gain((DEPTH, D)),
        "mlp_w1": nrm((DEPTH, D, D_FF), D ** -0.5),
        "mlp_w2": nrm((DEPTH, D_FF, D), D_FF ** -0.5),
        "diff_w_in": nrm((nA, D, 6 * DIFF_HEADS * DIFF_HEAD_DIM), D ** -0.5),
        "diff_q_norm": gain((nA, DIFF_HEAD_DIM)),
        "diff_k_norm": gain((nA, DIFF_HEAD_DIM)),
        "diff_lam_q1": nrm((nA, DIFF_HEAD_DIM), 0.1),
        "diff_lam_k1": nrm((nA, DIFF_HEAD_DIM), 0.1),
        "diff_lam_q2": nrm((nA, DIFF_HEAD_DIM), 0.1),
        "diff_lam_k2": nrm((nA, DIFF_HEAD_DIM), 0.1),
        "diff_subln": gain((nA, DIFF_V_DIM)),
        "diff_w_out": nrm((nA, DIFF_HEADS * DIFF_V_DIM, D), (DIFF_HEADS * DIFF_V_DIM) ** -0.5),
        "dsa_w_in": nrm((nB, D, dsa_in), D ** -0.5),
        "dsa_cq_norm": gain((nB, DSA_Q_RANK)),
        "dsa_w_uq": nrm((nB, DSA_Q_RANK, DSA_HEADS * DSA_HEAD_DIM), DSA_Q_RANK ** -0.5),
        "dsa_w_uq_idx": nrm((nB, DSA_Q_RANK, IDX_HEADS * IDX_DIM), DSA_Q_RANK ** -0.5),
        "dsa_q_norm": gain((nB, DSA_HEAD_DIM)),
        "dsa_k_norm": gain((nB, DSA_HEAD_DIM)),
        "dsa_w_out": nrm((nB, DSA_HEADS * DSA_HEAD_DIM, D), (DSA_HEADS * DSA_HEAD_DIM) ** -0.5),
    }


def reference(x, positions, norm_mix, norm_mlp, mlp_w1, mlp_w2,
              diff_w_in, diff_q_norm, diff_k_norm, diff_lam_q1, diff_lam_k1,
              diff_lam_q2, diff_lam_k2, diff_subln, diff_w_out,
              dsa_w_in, dsa_cq_norm, dsa_w_uq, dsa_w_uq_idx, dsa_q_norm, dsa_k_norm,
              dsa_w_out):
    cos, sin = rope_tables(positions, DIFF_HEAD_DIM)
    for i in range(DEPTH):
        h = rms_norm(x, norm_mix[i])
        j = i // N_MIXERS
        if i % N_MIXERS == 0:
            lam_init = 0.8 - 0.6 * math.exp(-0.3 * i)
            mix = diff_attention(h, diff_w_in[j], diff_q_norm[j], diff_k_norm[j],
                                 diff_lam_q1[j], diff_lam_k1[j], diff_lam_q2[j],
                                 diff_lam_k2[j], diff_subln[j], diff_w_out[j],
                                 cos, sin, lam_init)
        else:
            mix = dsa_attention(h, dsa_w_in[j], dsa_cq_norm[j], dsa_w_uq[j],
                                dsa_w_uq_idx[j], dsa_q_norm[j], dsa_k_norm[j],
                                dsa_w_out[j], cos, sin)
        x = x + mix
        x = x + sq_relu_mlp(rms_norm(x, norm_mlp[i]), mlp_w1[i], mlp_w2[i])
    return x
```

```python
import math
from contextlib import ExitStack
import numpy as np
import ml_dtypes
import concourse.bass as bass
import concourse.mybir as mybir
from concourse.bass_utils import run_bass_kernel_spmd


F32 = mybir.dt.float32
BF16 = mybir.dt.bfloat16
I32 = mybir.dt.int32
AF = mybir.ActivationFunctionType
ALU = mybir.AluOpType
AX = mybir.AxisListType


class Dep:
    __slots__ = ("w", "r")

    def __init__(self):
        self.w = None
        self.r = {}


class Sched:
    ENG = ("pe", "act", "dve", "pool", "sp")

    def __init__(self, nc):
        self.nc = nc
        self.ops = {e: [] for e in self.ENG}
        self.cnt = {e: 0 for e in self.ENG}
        self.known = {e: {} for e in self.ENG}
        self.dma_cnt = {}
        self.stack = ExitStack()
        self.nt = 0

    def sb(self, shape, dtype, name=None):
        self.nt += 1
        name = "sb_" + (name or f"t{self.nt}")
        return self.stack.enter_context(self.nc.sbuf_tensor(name, list(shape), dtype))

    def ps(self, shape, dtype, name=None):
        self.nt += 1
        name = "ps_" + (name or f"p{self.nt}")
        return self.stack.enter_context(self.nc.psum_tensor(name, list(shape), dtype))

    def op(self, eng, fn, reads=(), writes=(), dma=None, sem_inc=16):
        waits = {}

        def need(ev, raw):
            if ev is None:
                return
            key, val = ev
            if key == eng:
                if eng == "pe" or not raw:
                    return
            if waits.get(key, 0) < val:
                waits[key] = val

        for d in reads:
            need(d.w, True)
        for d in writes:
            need(d.w, False)
            for ev in d.r.items():
                need(ev, False)
        kn = self.known[eng]
        wl = []
        for key, val in waits.items():
            if kn.get(key, 0) >= val:
                continue
            kn[key] = val
            wl.append((key, val))
        if dma is not None:
            n = self.dma_cnt.get(dma, 0) + sem_inc
            self.dma_cnt[dma] = n
            ev = (dma, n)
            inc = (dma, sem_inc)
        else:
            self.cnt[eng] += 1
            ev = (eng, self.cnt[eng])
            inc = (eng, 1)
        self.ops[eng].append((wl, fn, inc))
        for d in reads:
            if d.r.get(ev[0], 0) < ev[1]:
                d.r[ev[0]] = ev[1]
        for d in writes:
            d.w = ev
            d.r = {}
        return ev

    def final_wait(self, eng, deps):
        waits = {}
        for d in deps:
            for ev in ([d.w] if d.w else []) + list(d.r.items()):
                if waits.get(ev[0], 0) < ev[1]:
                    waits[ev[0]] = ev[1]
        self.ops[eng].append((list(waits.items()), None, None))

    def barrier(self):
        waits = {e: self.cnt[e] for e in self.ENG if self.cnt[e] > 0}
        for k, n in self.dma_cnt.items():
            waits[k] = n
        for e in self.ENG:
            kn = self.known[e]
            wl = []
            for key, val in waits.items():
                if key == e or kn.get(key, 0) >= val:
                    continue
                kn[key] = val
                wl.append((key, val))
            self.ops[e].append((wl, None, None))

    def final_events(self, eng, evs):
        waits = {}
        for ev in evs:
            if waits.get(ev[0], 0) < ev[1]:
                waits[ev[0]] = ev[1]
        self.ops[eng].append((list(waits.items()), None, None))

    def emit(self):
        nc = self.nc
        keys = set(self.ENG) | set(self.dma_cnt.keys())
        assert len(keys) <= 100, f"too many semaphores: {len(keys)}"
        sems = {}
        for k in sorted(keys):
            sems[k] = self.stack.enter_context(nc.semaphore("s_" + k))
        ops = self.ops

        def run(e, lst):
            for wl, fn, inc in lst:
                for key, val in wl:
                    e.wait_ge(sems[key], val)
                if fn is None:
                    continue
                ins = fn(e)
                if inc is not None:
                    ins.then_inc(sems[inc[0]], inc[1])

        with nc.Block() as block:
            @block.tensor
            def _(e):
                run(e, ops["pe"])

            @block.scalar
            def _(e):
                run(e, ops["act"])

            @block.vector
            def _(e):
                run(e, ops["dve"])

            @block.gpsimd
            def _(e):
                run(e, ops["pool"])

            @block.sync
            def _(e):
                run(e, ops["sp"])
        self.stack.close()


NS = 16
D = 1024
EPS = 1e-6
INV_FREQ = [500000.0 ** (-(2.0 * j) / 16.0) for j in range(8)]
TWO_PI_S = 6.28318
PI_S = 3.14159


class T:
    def __init__(self, t, d=None):
        self.t = t
        self.d = d if d is not None else Dep()

    def __getitem__(self, k):
        return self.t[k]


ARENA_BASE = 16512
ARENA_END = 16512 + 212736


class B:
    def __init__(self, nc):
        self.nc = nc
        self.S = Sched(nc)
        self._consts = {}
        self.final = []
        self.arena = nc.alloc_sbuf_tensor("arena", [128, ARENA_END - ARENA_BASE], mybir.dt.uint8)
        self.lo = ARENA_BASE
        self.hi = ARENA_END
        self.nt = 0
        self.banks = [T(nc.alloc_psum_tensor(f"bank{i}", [128, 512], F32)) for i in range(8)]
        self.banks16 = [T(bk.t[:].bitcast(BF16), bk.d) for bk in self.banks]

    def sb(self, shape, dt, name=None, top=False):
        self.nt += 1
        nm = f"sb{self.nt}_{name or 't'}"
        size = int(np.prod(shape[1:])) * mybir.dt.size(dt)
        size = (size + 31) // 32 * 32
        if top:
            self.hi -= size
            off = self.hi
        else:
            off = self.lo
            self.lo += size
        assert self.lo <= self.hi, f"SBUF arena overflow at {nm}: lo={self.lo} hi={self.hi}"
        return T(self.nc.alloc_sbuf_tensor_at(nm, list(shape), dt, offset=off))

    def mark(self):
        return (self.lo, self.hi)

    def release(self, m):
        self.lo, self.hi = m
        self.S.barrier()

    def dram(self, name, shape, dt, kind):
        t = T(self.nc.dram_tensor(name, list(shape), dt, kind=kind).ap())
        return t

    def op(self, eng, fn, reads=(), writes=(), dma=None, sem_inc=16):
        return self.S.op(eng, fn, [x.d for x in reads], [x.d for x in writes], dma, sem_inc)

    def store(self, eng, fn, reads, dma):
        ev = self.S.op(eng, fn, [x.d for x in reads], [], dma)
        self.final.append(ev)
        return ev

    def finish(self):
        self.S.final_events("sp", self.final)
        self.S.emit()

    def ident(self):
        if "ident" not in self._consts:
            idt = self.sb([128, 128], BF16, "ident")
            self.op("pool", lambda e: e.memset(idt[:], 0.0), writes=[idt])
            self.op("pool", lambda e: e.affine_select(out=idt[:], in_=idt[:], pattern=[[-1, 128]],
                                                     compare_op=ALU.not_equal, fill=1.0, base=0,
                                                     channel_multiplier=1), reads=[idt], writes=[idt])
            self._consts["ident"] = idt
        return self._consts["ident"]

    def eps(self):
        if "eps" not in self._consts:
            t = self.sb([128, 1], F32, "eps")
            self.op("pool", lambda e: e.memset(t[:], EPS), writes=[t])
            self._consts["eps"] = t
        return self._consts["eps"]


def rstd_from_ss(b, ss, n, scale):
    eps = b.eps()
    b.op("act", lambda e: e.activation(out=ss[:, 0:n], in_=ss[:, 0:n], func=AF.Ln, bias=eps[:], scale=scale),
         reads=[ss, eps], writes=[ss])
    b.op("act", lambda e: e.activation(out=ss[:, 0:n], in_=ss[:, 0:n], func=AF.Exp, scale=-0.5),
         reads=[ss], writes=[ss])


def rope_tables(b, pos_t):
    posf = b.sb([128, NS], F32, "posf")
    inv = b.sb([128, 8], F32, "invf")
    ang = b.sb([128, NS, 8], F32, "ang")
    ti = b.sb([128, NS, 8], I32, "angi")
    tf = b.sb([128, NS, 8], F32, "angf")
    neg = b.sb([128, NS, 8], F32, "angn")
    cos = b.sb([128, NS, 8], F32, "cos")
    sin = b.sb([128, NS, 8], F32, "sin")
    nb = b.sb([128, 1], F32, "negpi")
    b.op("pool", lambda e: e.memset(nb[:], -PI_S), writes=[nb])
    b.op("dve", lambda e: e.tensor_copy(out=posf[:], in_=pos_t[:]), reads=[pos_t], writes=[posf])
    for j in range(8):
        b.op("pool", lambda e, j=j: e.memset(inv[:, j:j + 1], INV_FREQ[j] / (2 * math.pi)), writes=[inv])
    b.op("dve", lambda e: e.tensor_tensor(out=ang[:], in0=posf[:].unsqueeze(2).to_broadcast([128, NS, 8]),
                                          in1=inv[:].unsqueeze(1).to_broadcast([128, NS, 8]), op=ALU.mult),
         reads=[posf, inv], writes=[ang])
    for (dst, off) in ((sin, 0.5), (cos, 0.75)):
        b.op("dve", lambda e, off=off: e.tensor_scalar(out=tf[:], in0=ang[:], scalar1=off, scalar2=None, op0=ALU.add),
             reads=[ang], writes=[tf])
        b.op("dve", lambda e: e.tensor_copy(out=ti[:], in_=tf[:]), reads=[tf], writes=[ti])
        b.op("dve", lambda e: e.tensor_copy(out=neg[:], in_=ti[:]), reads=[ti], writes=[neg])
        b.op("dve", lambda e: e.tensor_tensor(out=tf[:], in0=tf[:], in1=neg[:], op=ALU.subtract),
             reads=[tf, neg], writes=[tf])
        b.op("dve", lambda e: e.tensor_scalar(out=neg[:], in0=tf[:], scalar1=0.0, scalar2=None, op0=ALU.is_lt),
             reads=[tf], writes=[neg])
        b.op("dve", lambda e: e.tensor_tensor(out=tf[:], in0=tf[:], in1=neg[:], op=ALU.add),
             reads=[tf, neg], writes=[tf])
        b.op("act", lambda e, dst=dst: e.activation(out=dst[:], in_=tf[:], func=AF.Sin, bias=nb[:], scale=TWO_PI_S),
             reads=[tf, nb], writes=[dst])
    return cos, sin


def rmsnorm_transpose(b, xt, gt, hbf, hT, pT, ss, junk):
    idt = b.ident()
    b.op("act", lambda e: e.activation(out=junk[:], in_=xt[:], func=AF.Square, accum_out=ss[:, 0:1]),
         reads=[xt], writes=[junk, ss])
    rstd_from_ss(b, ss, 1, 1.0 / D)
    b.op("dve", lambda e: e.scalar_tensor_tensor(out=hbf[:], in0=xt[:], scalar=ss[:, 0:1], in1=gt[:],
                                                 op0=ALU.mult, op1=ALU.mult),
         reads=[xt, ss, gt], writes=[hbf])
    for kc in range(8):
        b.op("pe", lambda e, kc=kc: e.transpose(out=pT[:, kc * 128:(kc + 1) * 128],
                                                in_=hbf[:, kc * 128:(kc + 1) * 128], identity=idt[:]),
             reads=[hbf, idt], writes=[pT])
    b.op("dve", lambda e: e.tensor_copy(out=hT[:].rearrange("p a b -> p (a b)"), in_=pT[:]),
         reads=[pT], writes=[hT])


def headnorm_rope(b, stage, sq, ssq, nh, gain, cos_a, sin_a, outbf, tmp, norm=True):
    s3 = stage[:].rearrange("p (h d) -> p h d", d=64)
    if norm:
        b.op("dve", lambda e: e.tensor_reduce(out=ssq[:, 0:nh], in_=sq[:].rearrange("p (h d) -> p h d", d=64),
                                              axis=AX.X, op=ALU.add), reads=[sq], writes=[ssq])
        rstd_from_ss(b, ssq, nh, 1.0 / 64)
    b.op("dve", lambda e: e.tensor_tensor(out=s3, in0=s3, in1=ssq[:, 0:nh].unsqueeze(2).to_broadcast([128, nh, 64]),
                                          op=ALU.mult), reads=[stage, ssq], writes=[stage])
    if gain is not None:
        b.op("pool", lambda e: e.tensor_tensor(out=stage[:], in0=stage[:], in1=gain[:], op=ALU.mult),
             reads=[stage, gain], writes=[stage])
    b.op("act", lambda e: e.activation(out=outbf[:], in_=stage[:], func=AF.Copy), reads=[stage], writes=[outbf])
    o3 = outbf[:].rearrange("p (h d) -> p h d", d=64)
    x1 = s3[:, :, 0:8]
    x2 = s3[:, :, 8:16]
    cb = cos_a.unsqueeze(1).to_broadcast([128, nh, 8])
    sb_ = sin_a.unsqueeze(1).to_broadcast([128, nh, 8])
    t = tmp[:].rearrange("p (k h d) -> p k h d", k=4, d=8)
    eng = "pool"
    b.op(eng, lambda e: e.tensor_tensor(out=t[:, 0, 0:nh, :], in0=x1, in1=cb, op=ALU.mult), reads=[stage], writes=[tmp])
    b.op(eng, lambda e: e.tensor_tensor(out=t[:, 1, 0:nh, :], in0=x2, in1=sb_, op=ALU.mult), reads=[stage], writes=[tmp])
    b.op(eng, lambda e: e.tensor_tensor(out=t[:, 2, 0:nh, :], in0=x2, in1=cb, op=ALU.mult), reads=[stage], writes=[tmp])
    b.op(eng, lambda e: e.tensor_tensor(out=t[:, 3, 0:nh, :], in0=x1, in1=sb_, op=ALU.mult), reads=[stage], writes=[tmp])
    b.op(eng, lambda e: e.tensor_tensor(out=o3[:, :, 0:8], in0=t[:, 0, 0:nh, :], in1=t[:, 1, 0:nh, :], op=ALU.subtract),
         reads=[tmp], writes=[outbf])
    b.op(eng, lambda e: e.tensor_tensor(out=o3[:, :, 8:16], in0=t[:, 2, 0:nh, :], in1=t[:, 3, 0:nh, :], op=ALU.add),
         reads=[tmp], writes=[outbf])


def phase_diff_proj(b, io):
    idt = b.ident()
    pos_t = b.sb([128, NS], I32, "pos")
    b.op("sp", lambda e: e.dma_start(out=pos_t[:], in_=io["pos"][:]), writes=[pos_t], dma="ld_pos")
    gmix = b.sb([128, D], F32, "gmix")
    b.op("sp", lambda e: e.dma_start(out=gmix[:], in_=io["gmix"][:]), writes=[gmix], dma="ld_gmix")
    gqk = b.sb([128, 2048], F32, "gqk")
    b.op("sp", lambda e: e.dma_start(out=gqk[:], in_=io["gqk"][:]), writes=[gqk], dma="ld_gqk")
    b.op("pool", lambda e: e.tensor_scalar(out=gqk[:, 0:1024], in0=gqk[:, 0:1024], scalar1=0.125, scalar2=None,
                                           op0=ALU.mult), reads=[gqk], writes=[gqk])
    w = b.sb([128, 8, 3072], BF16, "w_in")
    for kc in range(8):
        for hf in range(3):
            b.op("pool", lambda e, kc=kc, hf=hf: e.dma_start(
                out=w[:, kc, hf * 1024:(hf + 1) * 1024],
                in_=io["w_in"][kc * 128:(kc + 1) * 128, hf * 1024:(hf + 1) * 1024]),
                writes=[w], dma="ld_w")
    cos, sin = rope_tables(b, pos_t)

    xts = [b.sb([128, D], F32, f"xt{i}") for i in range(2)]
    junk = b.sb([128, D], BF16, "junk")
    ss = b.sb([128, 1], F32, "ss")
    hbf = b.sb([128, D], BF16, "hbf")
    hT = b.sb([128, 8, 128], BF16, "hT")
    pT = [b.banks16[0], b.banks16[1]]
    pY = [b.banks[2 + i] for i in range(4)]
    stage = b.sb([128, 2048], F32, "stage")
    sq = b.sb([128, 2048], F32, "sq")
    ssq = b.sb([128, 32], F32, "ssq")
    tmp = b.sb([128, 4 * 32 * 8], F32, "ropetmp")
    qkbf = b.sb([128, 2048], BF16, "qkbf")
    qkT = [b.sb([128, 16, 128], BF16, f"qkT{i}") for i in range(2)]
    vaug = [b.sb([128, 8, 129], BF16, f"vaug{i}") for i in range(2)]
    for i in range(2):
        b.op("pool", lambda e, i=i: e.memset(vaug[i][:], 1.0), writes=[vaug[i]])

    for a in range(NS):
        xt = xts[a % 2]
        b.op("sp", lambda e, a=a, xt=xt: e.dma_start(out=xt[:], in_=io["x"][a * 128:(a + 1) * 128, :]),
             writes=[xt], dma=f"ld_x{a % 2}")
        rmsnorm_transpose(b, xt, gmix, hbf, hT, pT[0], ss, junk)
        for n in range(6):
            py = pY[n % 4]
            for kc in range(8):
                b.op("pe", lambda e, n=n, kc=kc, py=py: e.matmul(py[:], lhsT=hT[:, kc, :],
                                                                   rhs=w[:, kc, n * 512:(n + 1) * 512],
                                                                   start=(kc == 0), stop=(kc == 7)),
                     reads=[hT, w], writes=[py])
            if n < 4:
                b.op("act", lambda e, n=n, py=py: e.activation(out=stage[:, n * 512:(n + 1) * 512], in_=py[:], func=AF.Copy),
                     reads=[py], writes=[stage])
                b.op("act", lambda e, n=n, py=py: e.activation(out=sq[:, n * 512:(n + 1) * 512], in_=py[:], func=AF.Square),
                     reads=[py], writes=[sq])
            else:
                va = vaug[a % 2]
                b.op("dve", lambda e, n=n, py=py, va=va: e.tensor_copy(
                    out=va[:, (n - 4) * 4:(n - 4) * 4 + 4, 0:128], in_=py[:].rearrange("p (h d) -> p h d", d=128)),
                    reads=[py], writes=[va])
        va = vaug[a % 2]
        b.store("sp", lambda e, a=a, va=va: e.dma_start(out=io["V"][a], in_=va[:].rearrange("p h d -> p (h d)")),
                reads=[va], dma=f"st_v{a % 2}")
        headnorm_rope(b, stage, sq, ssq, 32, gqk, cos[:, a, :], sin[:, a, :], qkbf, tmp)
        qt = qkT[a % 2]
        for i in range(16):
            pt = pT[1] if i < 8 else pT[0]
            b.op("pe", lambda e, i=i, pt=pt: e.transpose(out=pt[:, (i % 8) * 128:(i % 8 + 1) * 128],
                                                         in_=qkbf[:, i * 128:(i + 1) * 128], identity=idt[:]),
                 reads=[qkbf, idt], writes=[pt])
            if i % 8 == 7:
                b.op("dve", lambda e, i=i, pt=pt, qt=qt: e.tensor_copy(
                    out=qt[:, (i // 8) * 8:(i // 8) * 8 + 8, :].rearrange("p a b -> p (a b)"), in_=pt[:]),
                    reads=[pt], writes=[qt])
        b.store("sp", lambda e, a=a, qt=qt: e.dma_start(out=io["QT"][a], in_=qt[:, 0:8, :].rearrange("p a b -> p (a b)")),
                reads=[qt], dma=f"st_q{a % 2}")
        b.store("sp", lambda e, a=a, qt=qt: e.dma_start(out=io["KT"][a], in_=qt[:, 8:16, :].rearrange("p a b -> p (a b)")),
                reads=[qt], dma=f"st_k{a % 2}")


def phase_diff_attn(b, io, o_tok):
    LAM_INIT = 0.2
    qT = b.sb([128, NS, 1024], BF16, "qT")
    for a in range(NS):
        b.op("sp", lambda e, a=a: e.dma_start(out=qT[:, a, :], in_=io["QT"][a]), writes=[qT], dma="ld_qT")
    maskT = b.sb([128, 512], BF16, "maskT")
    b.op("sp", lambda e: e.dma_start(out=maskT[:], in_=io["maskT"][:]), writes=[maskT], dma="ld_mask")
    lam = b.sb([128, 256], F32, "lam")
    b.op("sp", lambda e: e.dma_start(out=lam[:], in_=io["lam"][:]), writes=[lam], dma="ld_lam")
    gsub = b.sb([128, 128], F32, "gsub")
    b.op("sp", lambda e: e.dma_start(out=gsub[:], in_=io["gsub"][:]), writes=[gsub], dma="ld_gsub")
    b.op("pool", lambda e: e.tensor_scalar(out=gsub[:], in0=gsub[:], scalar1=1.0 - LAM_INIT, scalar2=None,
                                           op0=ALU.mult), reads=[gsub], writes=[gsub])
    lprod = b.sb([128, 128], F32, "lprod")
    l2 = b.sb([128, 2], F32, "l2")
    neglam = b.sb([128, 1], F32, "neglam")
    l4 = lam[:].rearrange("p (a b d) -> p a b d", a=2, b=2)
    b.op("dve", lambda e: e.tensor_tensor(out=lprod[:].rearrange("p (a d) -> p a d", a=2), in0=l4[:, :, 0, :],
                                          in1=l4[:, :, 1, :], op=ALU.mult), reads=[lam], writes=[lprod])
    b.op("dve", lambda e: e.tensor_reduce(out=l2[:], in_=lprod[:].rearrange("p (a d) -> p a d", a=2), axis=AX.X,
                                          op=ALU.add), reads=[lprod], writes=[l2])
    b.op("act", lambda e: e.activation(out=l2[:], in_=l2[:], func=AF.Exp), reads=[l2], writes=[l2])
    b.op("dve", lambda e: e.tensor_tensor(out=neglam[:], in0=l2[:, 1:2], in1=l2[:, 0:1], op=ALU.subtract),
         reads=[l2], writes=[neglam])
    b.op("dve", lambda e: e.tensor_scalar(out=neglam[:], in0=neglam[:], scalar1=-LAM_INIT, scalar2=None, op0=ALU.add),
         reads=[neglam], writes=[neglam])

    ktb = [b.sb([128, 8192], BF16, f"ktb{i}") for i in range(2)]
    vb = [b.sb([128, 64, 129], BF16, f"vb{i}") for i in range(2)]
    pS = [[b.banks[c * 2 + i] for i in range(2)] for c in range(2)]
    pA = [[b.banks[4 + c * 2 + i] for i in range(2)] for c in range(2)]
    pT = [[b.sb([128, 512], BF16, f"pTs{c}{i}") for i in range(2)] for c in range(2)]
    rec = [b.sb([128, 2], F32, f"rec{i}") for i in range(2)]
    o32 = [b.sb([128, 128], F32, f"o32{i}") for i in range(2)]
    oj = b.sb([128, 128], BF16, "ojunk")
    ss1 = [b.sb([128, 1], F32, f"ss1{i}") for i in range(2)]

    units = [(h, a, blk) for h in range(8) for a in range(NS) for blk in range(a + 1)]

    def load_head(h):
        kb, vv = ktb[h % 2], vb[h % 2]
        for part in range(4):
            b.op("sp", lambda e, h=h, kb=kb, part=part: e.dma_start(
                out=kb[:, part * 2048:(part + 1) * 2048], in_=io["KTall"][h][:, part * 2048:(part + 1) * 2048]),
                writes=[kb], dma=f"ld_kt{h % 2}")
            b.op("sp", lambda e, h=h, vv=vv, part=part: e.dma_start(
                out=vv[:, part * 16:(part + 1) * 16, :].rearrange("p a b -> p (a b)"),
                in_=io["Vall"][h][:, part * 16 * 129:(part + 1) * 16 * 129]),
                writes=[vv], dma=f"ld_v{h % 2}")

    def qk(u, n):
        h, a, blk = u
        kb = ktb[h % 2]
        for c in range(2):
            ps = pS[c][n % 2]
            for i in range(4):
                kt = 4 * blk + i
                b.op("pe", lambda e, c=c, i=i, kt=kt, ps=ps, kb=kb, a=a, h=h: e.matmul(
                    ps[:, i * 128:(i + 1) * 128], lhsT=kb[64 * c:64 * c + 64, kt * 128:(kt + 1) * 128],
                    rhs=qT[64 * c:64 * c + 64, a, h * 128:(h + 1) * 128], start=True, stop=True),
                    reads=[kb, qT], writes=[ps])

    def softmax_pv(u, n):
        h, a, blk = u
        vv = vb[h % 2]
        for c in range(2):
            ps = pS[c][n % 2]
            pt = pT[c][n % 2]
            acc = pA[c][a % 2]
            b.op("act", lambda e, ps=ps, pt=pt: e.activation(out=pt[:], in_=ps[:], func=AF.Exp),
                 reads=[ps], writes=[pt])
            if blk == a:
                b.op("pool", lambda e, pt=pt: e.tensor_tensor(out=pt[:], in0=pt[:], in1=maskT[:], op=ALU.mult),
                     reads=[pt, maskT], writes=[pt])
            for i in range(4):
                kt = 4 * blk + i
                b.op("pe", lambda e, i=i, kt=kt, pt=pt, acc=acc, vv=vv, blk=blk, a=a: e.matmul(
                    acc[:, 0:129], lhsT=pt[:, i * 128:(i + 1) * 128], rhs=vv[:, kt, :],
                    start=(blk == 0 and i == 0), stop=(blk == a and i == 3)),
                    reads=[pt, vv], writes=[acc])
        if blk == a:
            evac(h, a)

    def evac(h, a):
        k = a % 2
        a0, a1 = pA[0][k], pA[1][k]
        r, o, s = rec[k], o32[k], ss1[k]
        b.op("dve", lambda e: e.reciprocal(out=r[:, 0:1], in_=a0[:, 128:129]), reads=[a0], writes=[r])
        b.op("dve", lambda e: e.reciprocal(out=r[:, 1:2], in_=a1[:, 128:129]), reads=[a1], writes=[r])
        b.op("dve", lambda e: e.tensor_tensor(out=r[:, 1:2], in0=r[:, 1:2], in1=neglam[:], op=ALU.mult),
             reads=[r, neglam], writes=[r])
        b.op("dve", lambda e: e.tensor_scalar(out=o[:], in0=a0[:, 0:128], scalar1=r[:, 0:1], scalar2=None, op0=ALU.mult),
             reads=[a0, r], writes=[o])
        b.op("dve", lambda e: e.scalar_tensor_tensor(out=o[:], in0=a1[:, 0:128], scalar=r[:, 1:2], in1=o[:],
                                                     op0=ALU.mult, op1=ALU.add), reads=[a1, r, o], writes=[o])
        b.op("act", lambda e: e.activation(out=oj[:], in_=o[:], func=AF.Square, accum_out=s[:, 0:1]),
             reads=[o], writes=[oj, s])
        rstd_from_ss(b, s, 1, 1.0 / 128)
        ot = o_tok[a]
        b.op("dve", lambda e: e.scalar_tensor_tensor(out=ot[:, h * 128:(h + 1) * 128], in0=o[:], scalar=s[:, 0:1],
                                                     in1=gsub[:], op0=ALU.mult, op1=ALU.mult),
             reads=[o, s, gsub], writes=[ot])

    load_head(0)
    qk(units[0], 0)
    for n, u in enumerate(units):
        if u[1] == 0 and u[2] == 0 and u[0] + 1 < 8:
            load_head(u[0] + 1)
        if n + 1 < len(units):
            qk(units[n + 1], n + 1)
        softmax_pv(u, n)


def load_w_bf16(b, wt, src, nk, ncols, key, colchunk=1024):
    for kc in range(nk):
        for c0 in range(0, ncols, colchunk):
            c1 = min(ncols, c0 + colchunk)
            b.op("pool", lambda e, kc=kc, c0=c0, c1=c1: e.dma_start(
                out=wt[:, kc, c0:c1], in_=src[kc * 128:(kc + 1) * 128, c0:c1]), writes=[wt], dma=key)


def phase_post_attn(b, io, o_tok, x_res, h2T, wout_ap, gmlp_ap, xsrc, xdeps=None):
    idt = b.ident()
    m = b.mark()
    wout = b.sb([128, 8, 1024], BF16, "wout")
    load_w_bf16(b, wout, wout_ap, 8, 1024, "ld_wout")
    gm = b.sb([128, D], F32, "gmlp")
    b.op("sp", lambda e: e.dma_start(out=gm[:], in_=gmlp_ap), writes=[gm], dma="ld_gmlp")
    oT = [b.sb([128, 8, 128], BF16, f"oT{i}") for i in range(2)]
    junk = b.sb([128, D], BF16, "junk2")
    ss = [b.sb([128, 1], F32, f"ss2{i}") for i in range(2)]
    hbf = [b.sb([128, D], BF16, f"hbf2{i}") for i in range(2)]
    for a in range(NS):
        xr = x_res[a]
        b.op("sp", lambda e, a=a, xr=xr: e.dma_start(out=xr[:], in_=xsrc[a * 128:(a + 1) * 128, :]),
             reads=([xdeps[a]] if xdeps else []), writes=[xr], dma="ld_xres")
        pt = b.banks16[a % 2]
        ot = oT[a % 2]
        for kc in range(8):
            b.op("pe", lambda e, kc=kc, pt=pt, a=a: e.transpose(out=pt[:, kc * 128:(kc + 1) * 128],
                                                                 in_=o_tok[a][:, kc * 128:(kc + 1) * 128], identity=idt[:]),
                 reads=[o_tok[a], idt], writes=[pt])
        b.op("act", lambda e, pt=pt, ot=ot: e.activation(out=ot[:].rearrange("p a b -> p (a b)"), in_=pt[:], func=AF.Copy),
             reads=[pt], writes=[ot])
        for n in range(2):
            py = b.banks[2 + (2 * a + n) % 4]
            for kc in range(8):
                b.op("pe", lambda e, kc=kc, n=n, py=py, ot=ot: e.matmul(py[:], lhsT=ot[:, kc, :],
                                                                         rhs=wout[:, kc, n * 512:(n + 1) * 512],
                                                                         start=(kc == 0), stop=(kc == 7)),
                     reads=[ot, wout], writes=[py])
            b.op("dve", lambda e, n=n, py=py, xr=xr: e.tensor_tensor(out=xr[:, n * 512:(n + 1) * 512],
                                                                     in0=xr[:, n * 512:(n + 1) * 512], in1=py[:], op=ALU.add),
                 reads=[xr, py], writes=[xr])
        rms_to_hT(b, xr, gm, hbf[a % 2], h2T, a, b.banks16[6 + a % 2], ss[a % 2], junk)
    return m


def rms_to_hT(b, xr, gm, hbf, h2T, a, pt, ss, junk):
    idt = b.ident()
    b.op("act", lambda e: e.activation(out=junk[:], in_=xr[:], func=AF.Square, accum_out=ss[:, 0:1]),
         reads=[xr], writes=[junk, ss])
    rstd_from_ss(b, ss, 1, 1.0 / D)
    b.op("dve", lambda e: e.scalar_tensor_tensor(out=hbf[:], in0=xr[:], scalar=ss[:, 0:1], in1=gm[:],
                                                 op0=ALU.mult, op1=ALU.mult), reads=[xr, ss, gm], writes=[hbf])
    for kc in range(8):
        b.op("pe", lambda e, kc=kc: e.transpose(out=pt[:, kc * 128:(kc + 1) * 128],
                                                in_=hbf[:, kc * 128:(kc + 1) * 128], identity=idt[:]),
             reads=[hbf, idt], writes=[pt])
    b.op("act", lambda e: e.activation(out=h2T[:, :, a * 128:(a + 1) * 128],
                                       in_=pt[:].rearrange("p (k t) -> p k t", k=8), func=AF.Copy),
         reads=[pt], writes=[h2T])


def phase_mlp(b, x_res, h2T, w1_ap, w2_ap):
    NFC = 8
    w1c = [b.sb([128, 8, 512], BF16, f"w1c{i}") for i in range(2)]
    w2c = [b.sb([128, 4, 1024], BF16, f"w2c{i}") for i in range(2)]
    rbuf = [b.sb([128, 512], F32, f"rbuf{i}") for i in range(2)]
    uT = [b.sb([128, 4, 512], BF16, f"uT{i}") for i in range(2)]
    pu = [b.banks[0], b.banks[1]]
    po = [b.banks[2 + i] for i in range(4)]

    def load_chunk(fc):
        w1, w2 = w1c[fc % 2], w2c[fc % 2]
        for kc in range(8):
            b.op("pool", lambda e, kc=kc, fc=fc, w1=w1: e.dma_start(
                out=w1[:, kc, :], in_=w1_ap[kc * 128:(kc + 1) * 128, fc * 512:(fc + 1) * 512]),
                writes=[w1], dma=f"ld_w1{fc % 2}")
        for ft in range(4):
            b.op("pool", lambda e, ft=ft, fc=fc, w2=w2: e.dma_start(
                out=w2[:, ft, :], in_=w2_ap[fc * 512 + ft * 128:fc * 512 + (ft + 1) * 128, :]),
                writes=[w2], dma=f"ld_w2{fc % 2}")

    steps = [(fc, tg) for fc in range(NFC) for tg in range(4)]
    cnt = {"u": 0, "o": 0}

    def stage_u(fc, tg):
        w1 = w1c[fc % 2]
        ut = uT[(fc * 4 + tg) % 2]
        for ft in range(4):
            p = pu[cnt["u"] % 2]
            r = rbuf[cnt["u"] % 2]
            cnt["u"] += 1
            for kc in range(8):
                b.op("pe", lambda e, kc=kc, ft=ft, p=p, w1=w1, tg=tg: e.matmul(
                    p[:], lhsT=w1[:, kc, ft * 128:(ft + 1) * 128], rhs=h2T[:, kc, tg * 512:(tg + 1) * 512],
                    start=(kc == 0), stop=(kc == 7)), reads=[w1, h2T], writes=[p])
            b.op("act", lambda e, p=p, r=r: e.activation(out=r[:], in_=p[:], func=AF.Relu), reads=[p], writes=[r])
            b.op("pool", lambda e, r=r, ut=ut, ft=ft: e.tensor_tensor(out=ut[:, ft, :], in0=r[:], in1=r[:], op=ALU.mult),
                 reads=[r], writes=[ut])

    def stage_o(fc, tg):
        w2 = w2c[fc % 2]
        ut = uT[(fc * 4 + tg) % 2]
        for tt in range(4):
            xr = x_res[tg * 4 + tt]
            for ch in range(2):
                p = po[cnt["o"] % 4]
                cnt["o"] += 1
                for ft in range(4):
                    b.op("pe", lambda e, ft=ft, tt=tt, ch=ch, p=p, ut=ut, w2=w2: e.matmul(
                        p[:], lhsT=ut[:, ft, tt * 128:(tt + 1) * 128], rhs=w2[:, ft, ch * 512:(ch + 1) * 512],
                        start=(ft == 0), stop=(ft == 3)), reads=[ut, w2], writes=[p])
                b.op("dve", lambda e, ch=ch, p=p, xr=xr: e.tensor_tensor(
                    out=xr[:, ch * 512:(ch + 1) * 512], in0=xr[:, ch * 512:(ch + 1) * 512], in1=p[:], op=ALU.add),
                    reads=[xr, p], writes=[xr])

    load_chunk(0)
    stage_u(*steps[0])
    for i, (fc, tg) in enumerate(steps):
        if tg == 0 and fc + 1 < NFC:
            load_chunk(fc + 1)
        if i + 1 < len(steps):
            stage_u(*steps[i + 1])
        stage_o(fc, tg)


IDX_SCALE = (8 ** -0.5) * (64 ** -0.5)


def phase_dsa_proj(b, io, x_res, cos, sin):
    idt = b.ident()
    gmix = b.sb([128, D], F32, "gmix1")
    b.op("sp", lambda e: e.dma_start(out=gmix[:], in_=io["gmix1"][:]), writes=[gmix], dma="ld_gmix1")
    gqk = b.sb([128, 2048], F32, "gqk2")
    b.op("sp", lambda e: e.dma_start(out=gqk[:], in_=io["gqk2"][:]), writes=[gqk], dma="ld_gqk2")
    b.op("pool", lambda e: e.tensor_scalar(out=gqk[:, 0:1024], in0=gqk[:, 0:1024], scalar1=0.125, scalar2=None,
                                           op0=ALU.mult), reads=[gqk], writes=[gqk])
    gcq = b.sb([128, 256], F32, "gcq")
    b.op("sp", lambda e: e.dma_start(out=gcq[:], in_=io["gcq"][:]), writes=[gcq], dma="ld_gcq")
    w = b.sb([128, 8, 2376], BF16, "w_in2")
    load_w_bf16(b, w, io["w_in2"], 8, 2376, "ld_w2in", colchunk=792)
    wuq = b.sb([128, 2, 1024], BF16, "wuq")
    load_w_bf16(b, wuq, io["w_uq"], 2, 1024, "ld_wuq")
    wuqi = b.sb([128, 2, 512], BF16, "wuqi")
    load_w_bf16(b, wuqi, io["w_uqi"], 2, 512, "ld_wuqi")

    junk = b.sb([128, D], BF16, "junk3")
    ss = b.sb([128, 1], F32, "ss3")
    hbf = b.sb([128, D], BF16, "hbf3")
    hT = b.sb([128, 8, 128], BF16, "hT3")
    stage = b.sb([128, 2048], F32, "stage3")
    sq = b.sb([128, 2048], F32, "sq3")
    ssq = b.sb([128, 32], F32, "ssq3")
    tmp = b.sb([128, 4 * 32 * 8], F32, "ropetmp3")
    qkbf = b.sb([128, 2048], BF16, "qkbf3")
    qkT = [b.sb([128, 16, 128], BF16, f"qkT3{i}") for i in range(2)]
    vaug = [b.sb([128, 16, 65], BF16, f"vaug3{i}") for i in range(2)]
    for i in range(2):
        b.op("pool", lambda e, i=i: e.memset(vaug[i][:], 1.0), writes=[vaug[i]])
    cqs = b.sb([128, 256], F32, "cqs")
    ssc = b.sb([128, 1], F32, "ssc")
    cqbf = b.sb([128, 256], BF16, "cqbf")
    cqT = b.sb([128, 2, 128], BF16, "cqT")
    kis = b.sb([128, 64], F32, "kis")
    ksq = b.sb([128, 64], F32, "ksq")
    kss = b.sb([128, 1], F32, "kss")
    kibf = b.sb([128, 128], BF16, "kibf")
    kiT = [b.sb([128, 128], BF16, f"kiT{i}") for i in range(2)]
    wi = b.sb([128, 8], F32, "wi")
    sgn = [b.sb([128, 8], F32, f"sgn{i}") for i in range(2)]
    aw = b.sb([128, 8], F32, "aw")
    qis = b.sb([128, 512], F32, "qis")
    qibf = b.sb([128, 512], BF16, "qibf")
    qiT = [b.sb([128, 4, 128], BF16, f"qiT{i}") for i in range(2)]
    nbank = [0]

    def bank():
        nbank[0] += 1
        return b.banks[2 + nbank[0] % 4]

    def proj(py, lhs, nk, rhs_fn, ncol):
        for kc in range(nk):
            b.op("pe", lambda e, kc=kc: e.matmul(py[:, 0:ncol], lhsT=lhs[:, kc, :], rhs=rhs_fn(kc),
                                                 start=(kc == 0), stop=(kc == nk - 1)), reads=[lhs, w, wuq, wuqi], writes=[py])

    for a in range(NS):
        xr = x_res[a]
        rmsnorm_transpose(b, xr, gmix, hbf, hT, b.banks16[0], ss, junk)
        py = bank()
        proj(py, hT, 8, lambda kc: w[:, kc, 0:256], 256)
        b.op("act", lambda e, py=py: e.activation(out=cqs[:], in_=py[:, 0:256], func=AF.Copy), reads=[py], writes=[cqs])
        b.op("act", lambda e, py=py: e.activation(out=junk[:, 0:256], in_=py[:, 0:256], func=AF.Square, accum_out=ssc[:, 0:1]),
             reads=[py], writes=[junk, ssc])
        rstd_from_ss(b, ssc, 1, 1.0 / 256)
        b.op("dve", lambda e: e.scalar_tensor_tensor(out=cqbf[:], in0=cqs[:], scalar=ssc[:, 0:1], in1=gcq[:],
                                                     op0=ALU.mult, op1=ALU.mult), reads=[cqs, ssc, gcq], writes=[cqbf])
        p6 = b.banks16[6]
        for kc in range(2):
            b.op("pe", lambda e, kc=kc: e.transpose(out=p6[:, kc * 128:(kc + 1) * 128], in_=cqbf[:, kc * 128:(kc + 1) * 128],
                                                    identity=idt[:]), reads=[cqbf, idt], writes=[p6])
        b.op("dve", lambda e: e.tensor_copy(out=cqT[:].rearrange("p a b -> p (a b)"), in_=p6[:, 0:256]),
             reads=[p6], writes=[cqT])
        for n in range(2):
            py = bank()
            proj(py, hT, 8, lambda kc, n=n: w[:, kc, 256 + n * 512:256 + (n + 1) * 512], 512)
            b.op("act", lambda e, py=py, n=n: e.activation(out=stage[:, 1024 + n * 512:1024 + (n + 1) * 512], in_=py[:], func=AF.Copy),
                 reads=[py], writes=[stage])
            b.op("act", lambda e, py=py, n=n: e.activation(out=sq[:, 1024 + n * 512:1024 + (n + 1) * 512], in_=py[:], func=AF.Square),
                 reads=[py], writes=[sq])
        va = vaug[a % 2]
        for n in range(2):
            py = bank()
            proj(py, hT, 8, lambda kc, n=n: w[:, kc, 1280 + n * 512:1280 + (n + 1) * 512], 512)
            b.op("dve", lambda e, py=py, n=n, va=va: e.tensor_copy(out=va[:, n * 8:(n + 1) * 8, 0:64],
                                                                   in_=py[:].rearrange("p (h d) -> p h d", d=64)),
                 reads=[py], writes=[va])
        b.store("sp", lambda e, a=a, va=va: e.dma_start(out=io["V2"][a], in_=va[:].rearrange("p h d -> p (h d)")),
                reads=[va], dma=f"st_v2{a % 2}")
        py = bank()
        proj(py, hT, 8, lambda kc: w[:, kc, 2304:2376], 72)
        b.op("act", lambda e, py=py: e.activation(out=kis[:], in_=py[:, 0:64], func=AF.Copy), reads=[py], writes=[kis])
        b.op("act", lambda e, py=py: e.activation(out=ksq[:], in_=py[:, 0:64], func=AF.Square), reads=[py], writes=[ksq])
        b.op("dve", lambda e, py=py: e.tensor_copy(out=wi[:], in_=py[:, 64:72]), reads=[py], writes=[wi])
        for n in range(2):
            py = bank()
            proj(py, cqT, 2, lambda kc, n=n: wuq[:, kc, n * 512:(n + 1) * 512], 512)
            b.op("act", lambda e, py=py, n=n: e.activation(out=stage[:, n * 512:(n + 1) * 512], in_=py[:], func=AF.Copy),
                 reads=[py], writes=[stage])
            b.op("act", lambda e, py=py, n=n: e.activation(out=sq[:, n * 512:(n + 1) * 512], in_=py[:], func=AF.Square),
                 reads=[py], writes=[sq])
        py = bank()
        proj(py, cqT, 2, lambda kc: wuqi[:, kc, :], 512)
        b.op("act", lambda e, py=py: e.activation(out=qis[:], in_=py[:], func=AF.Copy), reads=[py], writes=[qis])
        headnorm_rope(b, stage, sq, ssq, 32, gqk, cos[:, a, :], sin[:, a, :], qkbf, tmp)
        qt = qkT[a % 2]
        for i in range(16):
            pt = b.banks16[1] if i < 8 else b.banks16[0]
            b.op("pe", lambda e, i=i, pt=pt: e.transpose(out=pt[:, (i % 8) * 128:(i % 8 + 1) * 128],
                                                         in_=qkbf[:, i * 128:(i + 1) * 128], identity=idt[:]),
                 reads=[qkbf, idt], writes=[pt])
            if i % 8 == 7:
                b.op("dve", lambda e, i=i, pt=pt, qt=qt: e.tensor_copy(
                    out=qt[:, (i // 8) * 8:(i // 8) * 8 + 8, :].rearrange("p a b -> p (a b)"), in_=pt[:]),
                    reads=[pt], writes=[qt])
        b.store("sp", lambda e, a=a, qt=qt: e.dma_start(out=io["QT2"][a], in_=qt[:, 0:8, :].rearrange("p a b -> p (a b)")),
                reads=[qt], dma=f"st_q2{a % 2}")
        b.store("sp", lambda e, a=a, qt=qt: e.dma_start(out=io["KT2"][a], in_=qt[:, 8:16, :].rearrange("p a b -> p (a b)")),
                reads=[qt], dma=f"st_k2{a % 2}")
        ki_half = T(kibf.t[:, 0:64], kibf.d)
        headnorm_rope(b, kis, ksq, kss, 1, None, cos[:, a, :], sin[:, a, :], ki_half, tmp)
        b.op("pool", lambda e: e.tensor_copy(out=kibf[:, 64:128], in_=kibf[:, 0:64]), reads=[kibf], writes=[kibf])
        p7 = b.banks16[7]
        b.op("pe", lambda e: e.transpose(out=p7[:, 0:128], in_=kibf[:], identity=idt[:]), reads=[kibf, idt], writes=[p7])
        kt_ = kiT[a % 2]
        b.op("dve", lambda e, kt_=kt_: e.tensor_copy(out=kt_[:], in_=p7[:, 0:128]), reads=[p7], writes=[kt_])
        b.store("sp", lambda e, a=a, kt_=kt_: e.dma_start(out=io["KI"][a], in_=kt_[:]), reads=[kt_], dma=f"st_ki{a % 2}")
        sg = sgn[a % 2]
        b.op("act", lambda e, sg=sg: e.activation(out=sg[:], in_=wi[:], func=AF.Sign), reads=[wi], writes=[sg])
        b.op("dve", lambda e, sg=sg: e.scalar_tensor_tensor(out=aw[:], in0=wi[:], scalar=IDX_SCALE, in1=sg[:],
                                                           op0=ALU.mult, op1=ALU.mult), reads=[wi, sg], writes=[aw])
        b.store("sp", lambda e, a=a, sg=sg: e.dma_start(out=io["SG"][a], in_=sg[:]), reads=[sg], dma=f"st_sg{a % 2}")
        headnorm_rope(b, qis, None, aw, 8, None, cos[:, a, :], sin[:, a, :], qibf, tmp, norm=False)
        qi_ = qiT[a % 2]
        for i in range(4):
            b.op("pe", lambda e, i=i: e.transpose(out=p7[:, 256 + i * 128:256 + (i + 1) * 128],
                                                  in_=qibf[:, i * 128:(i + 1) * 128], identity=idt[:]),
                 reads=[qibf, idt], writes=[p7])
        b.op("dve", lambda e, qi_=qi_: e.tensor_copy(out=qi_[:].rearrange("p a b -> p (a b)"), in_=p7[:, 256:768]),
             reads=[p7], writes=[qi_])
        b.store("sp", lambda e, a=a, qi_=qi_: e.dma_start(out=io["QI"][a], in_=qi_[:].rearrange("p a b -> p (a b)")),
                reads=[qi_], dma=f"st_qi{a % 2}")


NIT = 22
TOPK = 256


def phase_dsa_attn(b, io, o_tok):
    idt = b.ident()
    kia = b.sb([128, 8192], BF16, "kiall")
    for part in range(4):
        b.op("sp", lambda e, part=part: e.dma_start(out=kia[:, part * 2048:(part + 1) * 2048],
                                                    in_=io["KIall"][:, part * 2048:(part + 1) * 2048]),
             writes=[kia], dma="ld_kia")
    negm = b.sb([128, 512], F32, "negm")
    b.op("sp", lambda e: e.dma_start(out=negm[:], in_=io["negmask"][:]), writes=[negm], dma="ld_negm")
    cW = b.sb([128, NIT], F32, "cW")
    for i in range(NIT):
        b.op("pool", lambda e, i=i: e.memset(cW[:, i:i + 1], 2.0 ** (-i)), writes=[cW])
    Ib = b.sb([128, 8192], F32, "Ibuf")
    Mq = b.sb([128, 8192], BF16, "Mq")
    MT = b.sb([128, 64, 128], BF16, "MT")
    ktp = [b.sb([128, 8192], BF16, f"ktp{i}") for i in range(2)]
    vp = [b.sb([128, 64, 130], BF16, f"vp{i}") for i in range(2)]
    qTa = [b.sb([128, 8, 128], BF16, f"qTa{i}") for i in range(2)]
    qiTa = [b.sb([128, 4, 128], BF16, f"qiTa{i}") for i in range(2)]
    sgn = [b.sb([128, 8], F32, f"sgna{i}") for i in range(2)]
    tb = [b.sb([128, 512], F32, f"tb{i}") for i in range(2)]
    pT = [b.sb([128, 512], BF16, f"pTd{i}") for i in range(2)]
    m1 = b.sb([128, 1], F32, "bm1")
    lo = b.sb([128, 1], F32, "blo")
    mid = b.sb([128, 1], F32, "bmid")
    cnt = b.sb([128, 1], F32, "bcnt")
    g = b.sb([128, 1], F32, "bg")
    W = b.sb([128, NIT], F32, "bW")
    rec = [b.sb([128, 1], F32, f"recd{i}") for i in range(2)]
    pS = [b.banks[0], b.banks[1]]
    pA = [b.banks[2], b.banks[3]]
    pI = [b.banks[4], b.banks[5]]
    pM = [b.banks16[6], b.banks16[7]]
    ctr = {"i": 0, "s": 0, "acc": 0, "kv": 0}

    def load_slot_small(a):
        b.op("sp", lambda e, a=a: e.dma_start(out=qTa[a % 2][:].rearrange("p a b -> p (a b)"), in_=io["QT2"][a]),
             writes=[qTa[a % 2]], dma=f"ld_qTa{a % 2}")
        b.op("sp", lambda e, a=a: e.dma_start(out=qiTa[a % 2][:].rearrange("p a b -> p (a b)"), in_=io["QI"][a]),
             writes=[qiTa[a % 2]], dma=f"ld_qiTa{a % 2}")
        b.op("sp", lambda e, a=a: e.dma_start(out=sgn[a % 2][:], in_=io["SG"][a]), writes=[sgn[a % 2]], dma=f"ld_sgn{a % 2}")

    def load_kv(a, hp):
        k = ctr["kv"] % 2
        ctr["kv"] += 1
        nv = 512 * (a + 1)
        nt = 4 * (a + 1)
        b.op("sp", lambda e, k=k, hp=hp, nv=nv: e.dma_start(out=ktp[k][:, 0:nv], in_=io["KT2all"][hp][:, 0:nv]),
             writes=[ktp[k]], dma=f"ld_ktp{k}")
        b.op("sp", lambda e, k=k, hp=hp, nt=nt: e.dma_start(out=vp[k][:, 0:nt, :].rearrange("p a b -> p (a b)"),
                                                            in_=io["V2all"][hp][:, 0:nt * 130]),
             writes=[vp[k]], dma=f"ld_vp{k}")
        return k

    def indexer(a):
        nb = a + 1
        nv = 512 * nb
        qi, sg = qiTa[a % 2], sgn[a % 2]
        for blk in range(nb):
            for head in range(8):
                hp, hh = head // 2, head % 2
                py = pI[ctr["i"] % 2]
                t = tb[ctr["i"] % 2]
                ctr["i"] += 1
                b.op("pe", lambda e, py=py, hp=hp, hh=hh, blk=blk, qi=qi: e.matmul(
                    py[:], lhsT=qi[64 * hh:64 * hh + 64, hp, :], rhs=kia[64 * hh:64 * hh + 64, blk * 512:(blk + 1) * 512],
                    start=True, stop=True), reads=[qi, kia], writes=[py])
                b.op("act", lambda e, py=py, t=t: e.activation(out=t[:], in_=py[:], func=AF.Relu), reads=[py], writes=[t])
                if head == 0:
                    b.op("dve", lambda e, t=t, blk=blk, sg=sg: e.tensor_scalar(
                        out=Ib[:, blk * 512:(blk + 1) * 512], in0=t[:], scalar1=sg[:, 0:1], scalar2=None, op0=ALU.mult),
                        reads=[t, sg], writes=[Ib])
                else:
                    b.op("dve", lambda e, t=t, blk=blk, sg=sg, head=head: e.scalar_tensor_tensor(
                        out=Ib[:, blk * 512:(blk + 1) * 512], in0=t[:], scalar=sg[:, head:head + 1],
                        in1=Ib[:, blk * 512:(blk + 1) * 512], op0=ALU.mult, op1=ALU.add),
                        reads=[t, sg, Ib], writes=[Ib])
        b.op("dve", lambda e: e.tensor_reduce(out=m1[:], in_=Ib[:, 0:nv], axis=AX.X, op=ALU.max, apply_absolute_value=True),
             reads=[Ib], writes=[m1])
        b.op("dve", lambda e: e.tensor_tensor(out=Ib[:, nv - 512:nv], in0=Ib[:, nv - 512:nv], in1=negm[:], op=ALU.add),
             reads=[Ib, negm], writes=[Ib])
        b.op("dve", lambda e: e.tensor_scalar(out=m1[:], in0=m1[:], scalar1=1.0, scalar2=None, op0=ALU.add),
             reads=[m1], writes=[m1])
        b.op("dve", lambda e: e.tensor_scalar(out=lo[:], in0=m1[:], scalar1=-1.0, scalar2=None, op0=ALU.mult),
             reads=[m1], writes=[lo])
        b.op("dve", lambda e: e.tensor_scalar(out=W[:], in0=cW[:], scalar1=m1[:, 0:1], scalar2=None, op0=ALU.mult),
             reads=[cW, m1], writes=[W])
        b.op("dve", lambda e: e.tensor_tensor(out=mid[:], in0=lo[:], in1=W[:, 0:1], op=ALU.add), reads=[lo, W], writes=[mid])
        for i in range(NIT):
            b.op("dve", lambda e: e.tensor_scalar(out=Mq[:, 0:nv], in0=Ib[:, 0:nv], scalar1=mid[:, 0:1], scalar2=0.0,
                                                  op0=ALU.is_ge, op1=ALU.add, accum_out=cnt[:, 0:1]),
                 reads=[Ib, mid], writes=[Mq, cnt])
            b.op("dve", lambda e, i=i: e.tensor_scalar(out=g[:], in0=cnt[:], scalar1=TOPK - 0.5, scalar2=W[:, i:i + 1],
                                                       op0=ALU.is_ge, op1=ALU.mult), reads=[cnt, W], writes=[g])
            b.op("dve", lambda e: e.tensor_tensor(out=lo[:], in0=lo[:], in1=g[:], op=ALU.add), reads=[lo, g], writes=[lo])
            if i + 1 < NIT:
                b.op("dve", lambda e, i=i: e.tensor_tensor(out=mid[:], in0=lo[:], in1=W[:, i + 1:i + 2], op=ALU.add),
                     reads=[lo, W], writes=[mid])
        b.op("dve", lambda e: e.tensor_scalar(out=Mq[:, 0:nv], in0=Ib[:, 0:nv], scalar1=lo[:, 0:1], scalar2=None,
                                              op0=ALU.is_ge), reads=[Ib, lo], writes=[Mq])
        nt = 4 * nb
        for kt in range(nt):
            pm = pM[(kt // 8) % 2]
            b.op("pe", lambda e, kt=kt, pm=pm: e.transpose(out=pm[:, (kt % 8) * 128:(kt % 8 + 1) * 128],
                                                           in_=Mq[:, kt * 128:(kt + 1) * 128], identity=idt[:]),
                 reads=[Mq, idt], writes=[pm])
            if kt % 8 == 7 or kt == nt - 1:
                k0 = (kt // 8) * 8
                n = kt - k0 + 1
                b.op("act", lambda e, pm=pm, k0=k0, n=n: e.activation(
                    out=MT[:, k0:k0 + n, :].rearrange("p a b -> p (a b)"), in_=pm[:, 0:n * 128], func=AF.Copy),
                    reads=[pm], writes=[MT])

    def qk(u, n, kbuf):
        a, hp, hh, blk = u
        ps = pS[n % 2]
        qa = qTa[a % 2]
        for i in range(4):
            kt = 4 * blk + i
            b.op("pe", lambda e, i=i, kt=kt, ps=ps, qa=qa, hh=hh, hp=hp, kbuf=kbuf: e.matmul(
                ps[:, i * 128:(i + 1) * 128], lhsT=ktp[kbuf][64 * hh:64 * hh + 64, kt * 128:(kt + 1) * 128],
                rhs=qa[64 * hh:64 * hh + 64, hp, :], start=True, stop=True), reads=[ktp[kbuf], qa], writes=[ps])

    def softmax_pv(u, n, kbuf):
        a, hp, hh, blk = u
        ps, pt = pS[n % 2], pT[n % 2]
        if blk == 0:
            ctr["acc"] += 1
        acc = pA[ctr["acc"] % 2]
        b.op("act", lambda e: e.activation(out=pt[:], in_=ps[:], func=AF.Exp), reads=[ps], writes=[pt])
        b.op("dve", lambda e: e.tensor_tensor(out=pt[:], in0=pt[:], in1=MT[:, 4 * blk:4 * blk + 4, :].rearrange("p a b -> p (a b)"),
                                              op=ALU.mult), reads=[pt, MT], writes=[pt])
        for i in range(4):
            kt = 4 * blk + i
            b.op("pe", lambda e, i=i, kt=kt: e.matmul(acc[:, 0:65], lhsT=pt[:, i * 128:(i + 1) * 128],
                                                      rhs=vp[kbuf][:, kt, hh * 65:(hh + 1) * 65],
                                                      start=(blk == 0 and i == 0), stop=(blk == a and i == 3)),
                 reads=[pt, vp[kbuf]], writes=[acc])
        if blk == a:
            head = 2 * hp + hh
            r = rec[ctr["acc"] % 2]
            b.op("dve", lambda e: e.reciprocal(out=r[:], in_=acc[:, 64:65]), reads=[acc], writes=[r])
            b.op("dve", lambda e: e.tensor_scalar(out=o_tok[a][:, head * 64:(head + 1) * 64], in0=acc[:, 0:64],
                                                  scalar1=r[:, 0:1], scalar2=None, op0=ALU.mult),
                 reads=[acc, r], writes=[o_tok[a]])

    load_slot_small(0)
    for a in range(NS):
        if a + 1 < NS:
            load_slot_small(a + 1)
        kb_next = load_kv(a, 0)
        indexer(a)
        units = [(a, hp, hh, blk) for hp in range(8) for hh in range(2) for blk in range(a + 1)]
        kbufs = {}
        kbufs[0] = kb_next
        qk(units[0], 0, kbufs[0])
        for n, u in enumerate(units):
            _, hp, hh, blk = u
            if hh == 0 and blk == 0 and hp + 1 < 8:
                kbufs[hp + 1] = load_kv(a, hp + 1)
            if n + 1 < len(units):
                qk(units[n + 1], n + 1, kbufs[units[n + 1][1]])
            softmax_pv(u, n, kbufs[hp])


GROUPS = [[0, 1, 2, 3], [4, 5, 6, 7]]
_RANK = {}


class Gather:
    def __init__(self, b, name, nblk, cols, zt):
        self.b, self.name, self.nblk, self.cols = b, name, nblk, cols
        nc = b.nc
        self.xb = nc.dram_tensor(name + "_xb", [nblk, 512, cols], BF16).ap()
        self.yb = nc.dram_tensor(name + "_yb", [nblk, 512, cols], BF16).ap()
        self.xo = nc.dram_tensor(name + "_xo", [nblk, 128, cols], BF16).ap()
        self.od = [T(self.xo[h]) for h in range(nblk)]
        self.xd = [T(self.xb[h]) for h in range(nblk)]
        self.yd = [T(self.yb[h]) for h in range(nblk)]
        for h in range(nblk):
            for r in range(4):
                b.op("act", lambda e, h=h, r=r: e.dma_start(out=self.xb[h, r * 128:(r + 1) * 128, :], in_=zt[:, 0:cols]),
                     reads=[zt], writes=[self.xd[h]], dma="zero_" + name)

    def put(self, h, c0, c1, src_ap, reads, key):
        self.b.op("sp", lambda e: e.dma_start(out=self.xo[h, :, c0:c1], in_=src_ap), reads=reads,
                  writes=[self.od[h]], dma=key)

    def place(self, h, eng):
        def fn(e):
            if eng not in _RANK:
                _RANK[eng] = e.partition_id() % 4
            r = _RANK[eng]
            return e.dma_start(out=self.xb[h, bass.ds(r * 128, 128), :], in_=self.xo[h])
        self.b.op(eng, fn, reads=[self.od[h]], writes=[self.xd[h]], dma="place_" + self.name)

    def reduce(self, h):
        b = self.b
        b.op("pool", lambda e: e.collective_compute("AllReduce", ALU.add, replica_groups=GROUPS,
                                                    ins=[self.xb[h]], outs=[self.yb[h]]),
             reads=[self.xd[h]], writes=[self.yd[h]], dma=f"cc_{self.name}{h}", sem_inc=1)


def f_diff_proj(b, io, qT_res, G0, cos, sin):
    idt = b.ident()
    gmix = b.sb([128, D], F32, "gmix")
    b.op("sp", lambda e: e.dma_start(out=gmix[:], in_=io["gmix"][:]), writes=[gmix], dma="ld_gmix")
    gqk = b.sb([128, 2048], F32, "gqk")
    b.op("sp", lambda e: e.dma_start(out=gqk[:], in_=io["gqk"][:]), writes=[gqk], dma="ld_gqk")
    b.op("pool", lambda e: e.tensor_scalar(out=gqk[:, 0:1024], in0=gqk[:, 0:1024], scalar1=0.125, scalar2=None,
                                           op0=ALU.mult), reads=[gqk], writes=[gqk])
    w = b.sb([128, 8, 3072], BF16, "w_in")
    load_w_bf16(b, w, io["w_in"], 8, 3072, "ld_w")
    xts = [b.sb([128, D], F32, f"xt{i}") for i in range(2)]
    junk = b.sb([128, D], BF16, "junk")
    ss = b.sb([128, 1], F32, "ss")
    hbf = b.sb([128, D], BF16, "hbf")
    hT = b.sb([128, 8, 128], BF16, "hT")
    pT = [b.banks16[0], b.banks16[1]]
    pY = [b.banks[2 + i] for i in range(4)]
    stage = b.sb([128, 2048], F32, "stage")
    sq = b.sb([128, 2048], F32, "sq")
    ssq = b.sb([128, 32], F32, "ssq")
    tmp = b.sb([128, 4 * 32 * 8], F32, "ropetmp")
    qkbf = b.sb([128, 2048], BF16, "qkbf")
    kT = [b.sb([128, 8, 128], BF16, f"kTst{i}") for i in range(2)]
    vst = [b.sb([128, 8, 128], BF16, f"vst{i}") for i in range(2)]
    for a in range(NS):
        xt = xts[a % 2]
        b.op("sp", lambda e, a=a, xt=xt: e.dma_start(out=xt[:], in_=io["x"][a * 128:(a + 1) * 128, :]),
             writes=[xt], dma=f"ld_x{a % 2}")
        rmsnorm_transpose(b, xt, gmix, hbf, hT, pT[0], ss, junk)
        va = vst[a % 2]
        for n in range(6):
            py = pY[n % 4]
            for kc in range(8):
                b.op("pe", lambda e, n=n, kc=kc, py=py: e.matmul(py[:], lhsT=hT[:, kc, :],
                                                                   rhs=w[:, kc, n * 512:(n + 1) * 512],
                                                                   start=(kc == 0), stop=(kc == 7)),
                     reads=[hT, w], writes=[py])
            if n < 4:
                b.op("act", lambda e, n=n, py=py: e.activation(out=stage[:, n * 512:(n + 1) * 512], in_=py[:], func=AF.Copy),
                     reads=[py], writes=[stage])
                b.op("act", lambda e, n=n, py=py: e.activation(out=sq[:, n * 512:(n + 1) * 512], in_=py[:], func=AF.Square),
                     reads=[py], writes=[sq])
            else:
                b.op("dve", lambda e, n=n, py=py, va=va: e.tensor_copy(
                    out=va[:, (n - 4) * 4:(n - 4) * 4 + 4, :], in_=py[:].rearrange("p (h d) -> p h d", d=128)),
                    reads=[py], writes=[va])
        for h in range(8):
            G0.put(h, 2048 + a * 128, 2048 + (a + 1) * 128, va[:, h, :], [va], f"st_v{a % 2}")
        headnorm_rope(b, stage, sq, ssq, 32, gqk, cos[:, a, :], sin[:, a, :], qkbf, tmp)
        kt = kT[a % 2]
        for i in range(16):
            pt = pT[1] if i < 8 else pT[0]
            b.op("pe", lambda e, i=i, pt=pt: e.transpose(out=pt[:, (i % 8) * 128:(i % 8 + 1) * 128],
                                                         in_=qkbf[:, i * 128:(i + 1) * 128], identity=idt[:]),
                 reads=[qkbf, idt], writes=[pt])
            if i == 7:
                b.op("dve", lambda e, pt=pt, a=a: e.tensor_copy(out=qT_res[:, a, :], in_=pt[:]), reads=[pt], writes=[qT_res])
            if i == 15:
                b.op("dve", lambda e, pt=pt, kt=kt: e.tensor_copy(out=kt[:].rearrange("p a b -> p (a b)"), in_=pt[:]),
                     reads=[pt], writes=[kt])
        for h in range(8):
            G0.put(h, a * 128, (a + 1) * 128, kt[:, h, :], [kt], f"st_k{a % 2}")
    for h in range(8):
        G0.place(h, "sp")
        G0.reduce(h)


def f_diff_attn(b, io, o_tok, qT, G0):
    LAM_INIT = 0.2
    maskT = b.sb([128, 512], BF16, "maskT")
    b.op("sp", lambda e: e.dma_start(out=maskT[:], in_=io["maskT"][:]), writes=[maskT], dma="ld_mask")
    lam = b.sb([128, 256], F32, "lam")
    b.op("sp", lambda e: e.dma_start(out=lam[:], in_=io["lam"][:]), writes=[lam], dma="ld_lam")
    gsub = b.sb([128, 128], F32, "gsub")
    b.op("sp", lambda e: e.dma_start(out=gsub[:], in_=io["gsub"][:]), writes=[gsub], dma="ld_gsub")
    b.op("pool", lambda e: e.tensor_scalar(out=gsub[:], in0=gsub[:], scalar1=1.0 - LAM_INIT, scalar2=None,
                                           op0=ALU.mult), reads=[gsub], writes=[gsub])
    lprod = b.sb([128, 128], F32, "lprod")
    l2 = b.sb([128, 2], F32, "l2")
    neglam = b.sb([128, 1], F32, "neglam")
    l4 = lam[:].rearrange("p (a b d) -> p a b d", a=2, b=2)
    b.op("dve", lambda e: e.tensor_tensor(out=lprod[:].rearrange("p (a d) -> p a d", a=2), in0=l4[:, :, 0, :],
                                          in1=l4[:, :, 1, :], op=ALU.mult), reads=[lam], writes=[lprod])
    b.op("dve", lambda e: e.tensor_reduce(out=l2[:], in_=lprod[:].rearrange("p (a d) -> p a d", a=2), axis=AX.X,
                                          op=ALU.add), reads=[lprod], writes=[l2])
    b.op("act", lambda e: e.activation(out=l2[:], in_=l2[:], func=AF.Exp), reads=[l2], writes=[l2])
    b.op("dve", lambda e: e.tensor_tensor(out=neglam[:], in0=l2[:, 1:2], in1=l2[:, 0:1], op=ALU.subtract),
         reads=[l2], writes=[neglam])
    b.op("dve", lambda e: e.tensor_scalar(out=neglam[:], in0=neglam[:], scalar1=-LAM_INIT, scalar2=None, op0=ALU.add),
         reads=[neglam], writes=[neglam])

    ktb = [b.sb([128, 8192], BF16, f"ktb{i}") for i in range(2)]
    vb = [b.sb([128, 64, 129], BF16, f"vb{i}") for i in range(2)]
    for i in range(2):
        b.op("pool", lambda e, i=i: e.memset(vb[i][:, :, 128:129], 1.0), writes=[vb[i]])
    pS = [[b.banks[c * 2 + i] for i in range(2)] for c in range(2)]
    pA = [[b.banks[4 + c * 2 + i] for i in range(2)] for c in range(2)]
    pT = [[b.sb([128, 512], BF16, f"pTs{c}{i}") for i in range(2)] for c in range(2)]
    rec = [b.sb([128, 2], F32, f"rec{i}") for i in range(2)]
    o32 = [b.sb([128, 128], F32, f"o32{i}") for i in range(2)]
    oj = b.sb([128, 128], BF16, "ojunk")
    ss1 = [b.sb([128, 1], F32, f"ss1{i}") for i in range(2)]
    units = [(h, a, blk) for h in range(8) for a in range(NS) for blk in range(a + 1)]

    def load_head(h):
        kb, vv = ktb[h % 2], vb[h % 2]
        yb = G0.yb[h]
        for r in range(4):
            b.op("sp", lambda e, kb=kb, r=r, yb=yb: e.dma_start(out=kb[:, r * 2048:(r + 1) * 2048],
                                                               in_=yb[r * 128:(r + 1) * 128, 0:2048]),
                 reads=[G0.yd[h]], writes=[kb], dma=f"ld_kt{h % 2}")
            b.op("sp", lambda e, vv=vv, r=r, yb=yb: e.dma_start(
                out=vv[:, r * 16:(r + 1) * 16, 0:128],
                in_=yb[r * 128:(r + 1) * 128, 2048:4096].rearrange("p (a e) -> p a e", e=128)),
                reads=[G0.yd[h]], writes=[vv], dma=f"ld_v{h % 2}")

    def qk(u, n):
        h, a, blk = u
        kb = ktb[h % 2]
        for c in range(2):
            ps = pS[c][n % 2]
            for i in range(4):
                kt = 16 * i + blk
                b.op("pe", lambda e, c=c, i=i, kt=kt, ps=ps, kb=kb, a=a, h=h: e.matmul(
                    ps[:, i * 128:(i + 1) * 128], lhsT=kb[64 * c:64 * c + 64, kt * 128:(kt + 1) * 128],
                    rhs=qT[64 * c:64 * c + 64, a, h * 128:(h + 1) * 128], start=True, stop=True),
                    reads=[kb, qT], writes=[ps])

    def evac(h, a):
        k = a % 2
        a0, a1 = pA[0][k], pA[1][k]
        r, o, s = rec[k], o32[k], ss1[k]
        b.op("dve", lambda e: e.reciprocal(out=r[:, 0:1], in_=a0[:, 128:129]), reads=[a0], writes=[r])
        b.op("dve", lambda e: e.reciprocal(out=r[:, 1:2], in_=a1[:, 128:129]), reads=[a1], writes=[r])
        b.op("dve", lambda e: e.tensor_tensor(out=r[:, 1:2], in0=r[:, 1:2], in1=neglam[:], op=ALU.mult),
             reads=[r, neglam], writes=[r])
        b.op("dve", lambda e: e.tensor_scalar(out=o[:], in0=a0[:, 0:128], scalar1=r[:, 0:1], scalar2=None, op0=ALU.mult),
             reads=[a0, r], writes=[o])
        b.op("dve", lambda e: e.scalar_tensor_tensor(out=o[:], in0=a1[:, 0:128], scalar=r[:, 1:2], in1=o[:],
                                                     op0=ALU.mult, op1=ALU.add), reads=[a1, r, o], writes=[o])
        b.op("act", lambda e: e.activation(out=oj[:], in_=o[:], func=AF.Square, accum_out=s[:, 0:1]),
             reads=[o], writes=[oj, s])
        rstd_from_ss(b, s, 1, 1.0 / 128)
        ot = o_tok[a]
        b.op("dve", lambda e: e.scalar_tensor_tensor(out=ot[:, h * 128:(h + 1) * 128], in0=o[:], scalar=s[:, 0:1],
                                                     in1=gsub[:], op0=ALU.mult, op1=ALU.mult),
             reads=[o, s, gsub], writes=[ot])

    def softmax_pv(u, n):
        h, a, blk = u
        vv = vb[h % 2]
        for c in range(2):
            ps = pS[c][n % 2]
            pt = pT[c][n % 2]
            acc = pA[c][a % 2]
            b.op("act", lambda e, ps=ps, pt=pt: e.activation(out=pt[:], in_=ps[:], func=AF.Exp),
                 reads=[ps], writes=[pt])
            if blk == a:
                b.op("pool", lambda e, pt=pt: e.tensor_tensor(out=pt[:], in0=pt[:], in1=maskT[:], op=ALU.mult),
                     reads=[pt, maskT], writes=[pt])
            for i in range(4):
                kt = 16 * i + blk
                b.op("pe", lambda e, i=i, kt=kt, pt=pt, acc=acc, vv=vv, blk=blk, a=a: e.matmul(
                    acc[:, 0:129], lhsT=pt[:, i * 128:(i + 1) * 128], rhs=vv[:, kt, :],
                    start=(blk == 0 and i == 0), stop=(blk == a and i == 3)),
                    reads=[pt, vv], writes=[acc])
        if blk == a:
            evac(h, a)

    load_head(0)
    qk(units[0], 0)
    for n, u in enumerate(units):
        if u[1] == 0 and u[2] == 0 and u[0] + 1 < 8:
            load_head(u[0] + 1)
        if n + 1 < len(units):
            qk(units[n + 1], n + 1)
        softmax_pv(u, n)


def f_dsa_proj(b, io, x_res, cos, sin, G1, GK, scr):
    idt = b.ident()
    gmix = b.sb([128, D], F32, "gmix1")
    b.op("sp", lambda e: e.dma_start(out=gmix[:], in_=io["gmix1"][:]), writes=[gmix], dma="ld_gmix1")
    gqk = b.sb([128, 2048], F32, "gqk2")
    b.op("sp", lambda e: e.dma_start(out=gqk[:], in_=io["gqk2"][:]), writes=[gqk], dma="ld_gqk2")
    b.op("pool", lambda e: e.tensor_scalar(out=gqk[:, 0:1024], in0=gqk[:, 0:1024], scalar1=0.125, scalar2=None,
                                           op0=ALU.mult), reads=[gqk], writes=[gqk])
    gcq = b.sb([128, 256], F32, "gcq")
    b.op("sp", lambda e: e.dma_start(out=gcq[:], in_=io["gcq"][:]), writes=[gcq], dma="ld_gcq")
    w = b.sb([128, 8, 2376], BF16, "w_in2")
    load_w_bf16(b, w, io["w_in2"], 8, 2376, "ld_w2in", colchunk=792)
    wuq = b.sb([128, 2, 1024], BF16, "wuq")
    load_w_bf16(b, wuq, io["w_uq"], 2, 1024, "ld_wuq")
    wuqi = b.sb([128, 2, 512], BF16, "wuqi")
    load_w_bf16(b, wuqi, io["w_uqi"], 2, 512, "ld_wuqi")
    junk = b.sb([128, D], BF16, "junk3")
    ss = b.sb([128, 1], F32, "ss3")
    hbf = b.sb([128, D], BF16, "hbf3")
    hT = b.sb([128, 8, 128], BF16, "hT3")
    stage = b.sb([128, 2048], F32, "stage3")
    sq = b.sb([128, 2048], F32, "sq3")
    ssq = b.sb([128, 32], F32, "ssq3")
    tmp = b.sb([128, 4 * 32 * 8], F32, "ropetmp3")
    qkbf = b.sb([128, 2048], BF16, "qkbf3")
    qkT = [b.sb([128, 16, 128], BF16, f"qkT3{i}") for i in range(2)]
    vst = [b.sb([128, 16, 64], BF16, f"vst3{i}") for i in range(2)]
    cqs = b.sb([128, 256], F32, "cqs")
    ssc = b.sb([128, 1], F32, "ssc")
    cqbf = b.sb([128, 256], BF16, "cqbf")
    cqT = b.sb([128, 2, 128], BF16, "cqT")
    kis = b.sb([128, 64], F32, "kis")
    ksq = b.sb([128, 64], F32, "ksq")
    kss = b.sb([128, 1], F32, "kss")
    kibf = b.sb([128, 128], BF16, "kibf")
    kiT = [b.sb([128, 128], BF16, f"kiT{i}") for i in range(2)]
    wi = b.sb([128, 8], F32, "wi")
    sgn = [b.sb([128, 8], F32, f"sgn{i}") for i in range(2)]
    aw = b.sb([128, 8], F32, "aw")
    qis = b.sb([128, 512], F32, "qis")
    qibf = b.sb([128, 512], BF16, "qibf")
    qiT = [b.sb([128, 4, 128], BF16, f"qiT{i}") for i in range(2)]
    nbank = [0]

    def bank():
        nbank[0] += 1
        return b.banks[2 + nbank[0] % 4]

    def proj(py, lhs, nk, rhs_fn, ncol):
        for kc in range(nk):
            b.op("pe", lambda e, kc=kc: e.matmul(py[:, 0:ncol], lhsT=lhs[:, kc, :], rhs=rhs_fn(kc),
                                                 start=(kc == 0), stop=(kc == nk - 1)), reads=[lhs, w, wuq, wuqi], writes=[py])

    for a in range(NS):
        xr = x_res[a]
        rmsnorm_transpose(b, xr, gmix, hbf, hT, b.banks16[0], ss, junk)
        py = bank()
        proj(py, hT, 8, lambda kc: w[:, kc, 0:256], 256)
        b.op("act", lambda e, py=py: e.activation(out=cqs[:], in_=py[:, 0:256], func=AF.Copy), reads=[py], writes=[cqs])
        b.op("act", lambda e, py=py: e.activation(out=junk[:, 0:256], in_=py[:, 0:256], func=AF.Square, accum_out=ssc[:, 0:1]),
             reads=[py], writes=[junk, ssc])
        rstd_from_ss(b, ssc, 1, 1.0 / 256)
        b.op("dve", lambda e: e.scalar_tensor_tensor(out=cqbf[:], in0=cqs[:], scalar=ssc[:, 0:1], in1=gcq[:],
                                                     op0=ALU.mult, op1=ALU.mult), reads=[cqs, ssc, gcq], writes=[cqbf])
        p6 = b.banks16[6]
        for kc in range(2):
            b.op("pe", lambda e, kc=kc: e.transpose(out=p6[:, kc * 128:(kc + 1) * 128], in_=cqbf[:, kc * 128:(kc + 1) * 128],
                                                    identity=idt[:]), reads=[cqbf, idt], writes=[p6])
        b.op("dve", lambda e: e.tensor_copy(out=cqT[:].rearrange("p a b -> p (a b)"), in_=p6[:, 0:256]),
             reads=[p6], writes=[cqT])
        for n in range(2):
            py = bank()
            proj(py, hT, 8, lambda kc, n=n: w[:, kc, 256 + n * 512:256 + (n + 1) * 512], 512)
            b.op("act", lambda e, py=py, n=n: e.activation(out=stage[:, 1024 + n * 512:1024 + (n + 1) * 512], in_=py[:], func=AF.Copy),
                 reads=[py], writes=[stage])
            b.op("act", lambda e, py=py, n=n: e.activation(out=sq[:, 1024 + n * 512:1024 + (n + 1) * 512], in_=py[:], func=AF.Square),
                 reads=[py], writes=[sq])
        va = vst[a % 2]
        for n in range(2):
            py = bank()
            proj(py, hT, 8, lambda kc, n=n: w[:, kc, 1280 + n * 512:1280 + (n + 1) * 512], 512)
            b.op("dve", lambda e, py=py, n=n, va=va: e.tensor_copy(out=va[:, n * 8:(n + 1) * 8, :],
                                                                   in_=py[:].rearrange("p (h d) -> p h d", d=64)),
                 reads=[py], writes=[va])
        for hp in range(8):
            G1.put(hp, 2048 + a * 128, 2048 + (a + 1) * 128, va[:, 2 * hp:2 * hp + 2, :].rearrange("p h d -> p (h d)"),
                   [va], f"st_v2{a % 2}")
        py = bank()
        proj(py, hT, 8, lambda kc: w[:, kc, 2304:2376], 72)
        b.op("act", lambda e, py=py: e.activation(out=kis[:], in_=py[:, 0:64], func=AF.Copy), reads=[py], writes=[kis])
        b.op("act", lambda e, py=py: e.activation(out=ksq[:], in_=py[:, 0:64], func=AF.Square), reads=[py], writes=[ksq])
        b.op("dve", lambda e, py=py: e.tensor_copy(out=wi[:], in_=py[:, 64:72]), reads=[py], writes=[wi])
        for n in range(2):
            py = bank()
            proj(py, cqT, 2, lambda kc, n=n: wuq[:, kc, n * 512:(n + 1) * 512], 512)
            b.op("act", lambda e, py=py, n=n: e.activation(out=stage[:, n * 512:(n + 1) * 512], in_=py[:], func=AF.Copy),
                 reads=[py], writes=[stage])
            b.op("act", lambda e, py=py, n=n: e.activation(out=sq[:, n * 512:(n + 1) * 512], in_=py[:], func=AF.Square),
                 reads=[py], writes=[sq])
        py = bank()
        proj(py, cqT, 2, lambda kc: wuqi[:, kc, :], 512)
        b.op("act", lambda e, py=py: e.activation(out=qis[:], in_=py[:], func=AF.Copy), reads=[py], writes=[qis])
        headnorm_rope(b, stage, sq, ssq, 32, gqk, cos[:, a, :], sin[:, a, :], qkbf, tmp)
        qt = qkT[a % 2]
        for i in range(16):
            pt = b.banks16[1] if i < 8 else b.banks16[0]
            b.op("pe", lambda e, i=i, pt=pt: e.transpose(out=pt[:, (i % 8) * 128:(i % 8 + 1) * 128],
                                                         in_=qkbf[:, i * 128:(i + 1) * 128], identity=idt[:]),
                 reads=[qkbf, idt], writes=[pt])
            if i % 8 == 7:
                b.op("dve", lambda e, i=i, pt=pt, qt=qt: e.tensor_copy(
                    out=qt[:, (i // 8) * 8:(i // 8) * 8 + 8, :].rearrange("p a b -> p (a b)"), in_=pt[:]),
                    reads=[pt], writes=[qt])
        b.op("sp", lambda e, a=a, qt=qt: e.dma_start(out=scr["QT2"][a][:], in_=qt[:, 0:8, :].rearrange("p a b -> p (a b)")),
             reads=[qt], writes=[scr["QT2"][a]], dma=f"st_q2{a % 2}")
        for hp in range(8):
            G1.put(hp, a * 128, (a + 1) * 128, qt[:, 8 + hp, :], [qt], f"st_k2{a % 2}")
        ki_half = T(kibf.t[:, 0:64], kibf.d)
        headnorm_rope(b, kis, ksq, kss, 1, None, cos[:, a, :], sin[:, a, :], ki_half, tmp)
        b.op("pool", lambda e: e.tensor_copy(out=kibf[:, 64:128], in_=kibf[:, 0:64]), reads=[kibf], writes=[kibf])
        p7 = b.banks16[7]
        b.op("pe", lambda e: e.transpose(out=p7[:, 0:128], in_=kibf[:], identity=idt[:]), reads=[kibf, idt], writes=[p7])
        kt_ = kiT[a % 2]
        b.op("dve", lambda e, kt_=kt_: e.tensor_copy(out=kt_[:], in_=p7[:, 0:128]), reads=[p7], writes=[kt_])
        GK.put(0, a * 128, (a + 1) * 128, kt_[:], [kt_], f"st_ki{a % 2}")
        sg = sgn[a % 2]
        b.op("act", lambda e, sg=sg: e.activation(out=sg[:], in_=wi[:], func=AF.Sign), reads=[wi], writes=[sg])
        b.op("dve", lambda e, sg=sg: e.scalar_tensor_tensor(out=aw[:], in0=wi[:], scalar=IDX_SCALE, in1=sg[:],
                                                           op0=ALU.mult, op1=ALU.mult), reads=[wi, sg], writes=[aw])
        b.op("sp", lambda e, a=a, sg=sg: e.dma_start(out=scr["SG"][a][:], in_=sg[:]), reads=[sg], writes=[scr["SG"][a]],
             dma=f"st_sg{a % 2}")
        headnorm_rope(b, qis, None, aw, 8, None, cos[:, a, :], sin[:, a, :], qibf, tmp, norm=False)
        qi_ = qiT[a % 2]
        for i in range(4):
            b.op("pe", lambda e, i=i: e.transpose(out=p7[:, 256 + i * 128:256 + (i + 1) * 128],
                                                  in_=qibf[:, i * 128:(i + 1) * 128], identity=idt[:]),
                 reads=[qibf, idt], writes=[p7])
        b.op("dve", lambda e, qi_=qi_: e.tensor_copy(out=qi_[:].rearrange("p a b -> p (a b)"), in_=p7[:, 256:768]),
             reads=[p7], writes=[qi_])
        b.op("sp", lambda e, a=a, qi_=qi_: e.dma_start(out=scr["QI"][a][:], in_=qi_[:].rearrange("p a b -> p (a b)")),
             reads=[qi_], writes=[scr["QI"][a]], dma=f"st_qi{a % 2}")
    GK.place(0, "act")
    GK.reduce(0)
    for hp in range(8):
        G1.place(hp, "act")
        G1.reduce(hp)


def f_dsa_attn(b, io, o_tok, G1, GK, scr):
    idt = b.ident()
    kia = b.sb([128, 8192], BF16, "kiall")
    for r in range(4):
        b.op("sp", lambda e, r=r: e.dma_start(out=kia[:, r * 2048:(r + 1) * 2048], in_=GK.yb[0][r * 128:(r + 1) * 128, :]),
             reads=[GK.yd[0]], writes=[kia], dma="ld_kia")
    kia4 = kia[:].rearrange("p (r a t) -> p r a t", r=4, a=16)
    negm = b.sb([128, 512], F32, "negm")
    b.op("sp", lambda e: e.dma_start(out=negm[:], in_=io["negmask"][:]), writes=[negm], dma="ld_negm")
    cW = b.sb([128, NIT], F32, "cW")
    for i in range(NIT):
        b.op("pool", lambda e, i=i: e.memset(cW[:, i:i + 1], 2.0 ** (-i)), writes=[cW])
    Ib = b.sb([128, 8192], F32, "Ibuf")
    Mq = b.sb([128, 8192], BF16, "Mq")
    MT = b.sb([128, 64, 128], BF16, "MT")
    ktp = [b.sb([128, 8192], BF16, f"ktp{i}") for i in range(2)]
    vp = [b.sb([128, 64, 130], BF16, f"vp{i}") for i in range(2)]
    for i in range(2):
        b.op("pool", lambda e, i=i: e.memset(vp[i][:, :, 0:1], 1.0), writes=[vp[i]])
        b.op("pool", lambda e, i=i: e.memset(vp[i][:, :, 129:130], 1.0), writes=[vp[i]])
    qTa = [b.sb([128, 8, 128], BF16, f"qTa{i}") for i in range(2)]
    qiTa = [b.sb([128, 4, 128], BF16, f"qiTa{i}") for i in range(2)]
    sgn = [b.sb([128, 8], F32, f"sgna{i}") for i in range(2)]
    tb = [b.sb([128, 512], F32, f"tb{i}") for i in range(2)]
    pT = [b.sb([128, 512], BF16, f"pTd{i}") for i in range(2)]
    m1 = b.sb([128, 1], F32, "bm1")
    lo = b.sb([128, 1], F32, "blo")
    mid = b.sb([128, 1], F32, "bmid")
    cnt = b.sb([128, 1], F32, "bcnt")
    g = b.sb([128, 1], F32, "bg")
    W = b.sb([128, NIT], F32, "bW")
    rec = [b.sb([128, 1], F32, f"recd{i}") for i in range(2)]
    pS = [b.banks[0], b.banks[1]]
    pA = [b.banks[2], b.banks[3]]
    pI = [b.banks[4], b.banks[5]]
    pM = [b.banks16[6], b.banks16[7]]
    ctr = {"i": 0, "acc": 0, "kv": 0}

    def load_slot_small(a):
        b.op("sp", lambda e, a=a: e.dma_start(out=qTa[a % 2][:].rearrange("p a b -> p (a b)"), in_=scr["QT2"][a][:]),
             reads=[scr["QT2"][a]], writes=[qTa[a % 2]], dma=f"ld_qTa{a % 2}")
        b.op("sp", lambda e, a=a: e.dma_start(out=qiTa[a % 2][:].rearrange("p a b -> p (a b)"), in_=scr["QI"][a][:]),
             reads=[scr["QI"][a]], writes=[qiTa[a % 2]], dma=f"ld_qiTa{a % 2}")
        b.op("sp", lambda e, a=a: e.dma_start(out=sgn[a % 2][:], in_=scr["SG"][a][:]), reads=[scr["SG"][a]],
             writes=[sgn[a % 2]], dma=f"ld_sgn{a % 2}")

    def load_kv(a, hp):
        k = ctr["kv"] % 2
        ctr["kv"] += 1
        n = (a + 1) * 128
        yb = G1.yb[hp]
        b.op("sp", lambda e: e.dma_start(out=ktp[k][:].rearrange("p (r x) -> p r x", r=4)[:, :, 0:n],
                                         in_=yb[:, 0:n].rearrange("(r p) x -> p r x", p=128)),
             reads=[G1.yd[hp]], writes=[ktp[k]], dma=f"ld_ktp{k}")
        for r in range(4):
            b.op("sp", lambda e, r=r: e.dma_start(
                out=vp[k][:, r * 16:r * 16 + a + 1, 1:129],
                in_=yb[r * 128:(r + 1) * 128, 2048:2048 + n].rearrange("p (a e) -> p a e", e=128)),
                reads=[G1.yd[hp]], writes=[vp[k]], dma=f"ld_vp{k}")
        return k

    def indexer(a):
        nb = a + 1
        nv = 512 * nb
        qi, sg = qiTa[a % 2], sgn[a % 2]
        for blk in range(nb):
            for head in range(8):
                hp, hh = head // 2, head % 2
                py = pI[ctr["i"] % 2]
                t = tb[ctr["i"] % 2]
                ctr["i"] += 1
                b.op("pe", lambda e, py=py, hp=hp, hh=hh, blk=blk, qi=qi: e.matmul(
                    py[:].rearrange("p (r t) -> p r t", r=4), lhsT=qi[64 * hh:64 * hh + 64, hp, :],
                    rhs=kia4[64 * hh:64 * hh + 64, :, blk, :], start=True, stop=True), reads=[qi, kia], writes=[py])
                b.op("act", lambda e, py=py, t=t: e.activation(out=t[:], in_=py[:], func=AF.Relu), reads=[py], writes=[t])
                if head == 0:
                    b.op("dve", lambda e, t=t, blk=blk, sg=sg: e.tensor_scalar(
                        out=Ib[:, blk * 512:(blk + 1) * 512], in0=t[:], scalar1=sg[:, 0:1], scalar2=None, op0=ALU.mult),
                        reads=[t, sg], writes=[Ib])
                else:
                    b.op("dve", lambda e, t=t, blk=blk, sg=sg, head=head: e.scalar_tensor_tensor(
                        out=Ib[:, blk * 512:(blk + 1) * 512], in0=t[:], scalar=sg[:, head:head + 1],
                        in1=Ib[:, blk * 512:(blk + 1) * 512], op0=ALU.mult, op1=ALU.add),
                        reads=[t, sg, Ib], writes=[Ib])
        b.op("dve", lambda e: e.tensor_reduce(out=m1[:], in_=Ib[:, 0:nv], axis=AX.X, op=ALU.max, apply_absolute_value=True),
             reads=[Ib], writes=[m1])
        b.op("dve", lambda e: e.tensor_tensor(out=Ib[:, nv - 512:nv], in0=Ib[:, nv - 512:nv], in1=negm[:], op=ALU.add),
             reads=[Ib, negm], writes=[Ib])
        b.op("dve", lambda e: e.tensor_scalar(out=m1[:], in0=m1[:], scalar1=1.0, scalar2=None, op0=ALU.add),
             reads=[m1], writes=[m1])
        b.op("dve", lambda e: e.tensor_scalar(out=lo[:], in0=m1[:], scalar1=-1.0, scalar2=None, op0=ALU.mult),
             reads=[m1], writes=[lo])
        b.op("dve", lambda e: e.tensor_scalar(out=W[:], in0=cW[:], scalar1=m1[:, 0:1], scalar2=None, op0=ALU.mult),
             reads=[cW, m1], writes=[W])
        b.op("dve", lambda e: e.tensor_tensor(out=mid[:], in0=lo[:], in1=W[:, 0:1], op=ALU.add), reads=[lo, W], writes=[mid])
        for i in range(NIT):
            b.op("dve", lambda e: e.tensor_scalar(out=Mq[:, 0:nv], in0=Ib[:, 0:nv], scalar1=mid[:, 0:1], scalar2=0.0,
                                                  op0=ALU.is_ge, op1=ALU.add, accum_out=cnt[:, 0:1]),
                 reads=[Ib, mid], writes=[Mq, cnt])
            b.op("dve", lambda e, i=i: e.tensor_scalar(out=g[:], in0=cnt[:], scalar1=TOPK - 0.5, scalar2=W[:, i:i + 1],
                                                       op0=ALU.is_ge, op1=ALU.mult), reads=[cnt, W], writes=[g])
            b.op("dve", lambda e: e.tensor_tensor(out=lo[:], in0=lo[:], in1=g[:], op=ALU.add), reads=[lo, g], writes=[lo])
            if i + 1 < NIT:
                b.op("dve", lambda e, i=i: e.tensor_tensor(out=mid[:], in0=lo[:], in1=W[:, i + 1:i + 2], op=ALU.add),
                     reads=[lo, W], writes=[mid])
        b.op("dve", lambda e: e.tensor_scalar(out=Mq[:, 0:nv], in0=Ib[:, 0:nv], scalar1=lo[:, 0:1], scalar2=None,
                                              op0=ALU.is_ge), reads=[Ib, lo], writes=[Mq])
        nt = 4 * nb
        for kt in range(nt):
            pm = pM[(kt // 8) % 2]
            b.op("pe", lambda e, kt=kt, pm=pm: e.transpose(out=pm[:, (kt % 8) * 128:(kt % 8 + 1) * 128],
                                                           in_=Mq[:, kt * 128:(kt + 1) * 128], identity=idt[:]),
                 reads=[Mq, idt], writes=[pm])
            if kt % 8 == 7 or kt == nt - 1:
                k0 = (kt // 8) * 8
                n = kt - k0 + 1
                b.op("act", lambda e, pm=pm, k0=k0, n=n: e.activation(
                    out=MT[:, k0:k0 + n, :].rearrange("p a b -> p (a b)"), in_=pm[:, 0:n * 128], func=AF.Copy),
                    reads=[pm], writes=[MT])

    def qk(u, n, kbuf):
        a, hp, hh, blk = u
        ps = pS[n % 2]
        qa = qTa[a % 2]
        for i in range(4):
            kt = 16 * i + blk
            b.op("pe", lambda e, i=i, kt=kt, ps=ps, qa=qa, hh=hh, hp=hp, kbuf=kbuf: e.matmul(
                ps[:, i * 128:(i + 1) * 128], lhsT=ktp[kbuf][64 * hh:64 * hh + 64, kt * 128:(kt + 1) * 128],
                rhs=qa[64 * hh:64 * hh + 64, hp, :], start=True, stop=True), reads=[ktp[kbuf], qa], writes=[ps])

    def softmax_pv(u, n, kbuf):
        a, hp, hh, blk = u
        ps, pt = pS[n % 2], pT[n % 2]
        if blk == 0:
            ctr["acc"] += 1
        acc = pA[ctr["acc"] % 2]
        b.op("act", lambda e: e.activation(out=pt[:], in_=ps[:], func=AF.Exp), reads=[ps], writes=[pt])
        b.op("dve", lambda e: e.tensor_tensor(out=pt[:], in0=pt[:], in1=MT[:, 4 * blk:4 * blk + 4, :].rearrange("p a b -> p (a b)"),
                                              op=ALU.mult), reads=[pt, MT], writes=[pt])
        for i in range(4):
            kt = 16 * i + blk
            b.op("pe", lambda e, i=i, kt=kt: e.matmul(acc[:, 0:65], lhsT=pt[:, i * 128:(i + 1) * 128],
                                                      rhs=vp[kbuf][:, kt, hh * 65:(hh + 1) * 65],
                                                      start=(blk == 0 and i == 0), stop=(blk == a and i == 3)),
                 reads=[pt, vp[kbuf]], writes=[acc])
        if blk == a:
            head = 2 * hp + hh
            r = rec[ctr["acc"] % 2]
            sc, v0 = (0, 1) if hh == 0 else (64, 0)
            b.op("dve", lambda e: e.reciprocal(out=r[:], in_=acc[:, sc:sc + 1]), reads=[acc], writes=[r])
            b.op("dve", lambda e: e.tensor_scalar(out=o_tok[a][:, head * 64:(head + 1) * 64], in0=acc[:, v0:v0 + 64],
                                                  scalar1=r[:, 0:1], scalar2=None, op0=ALU.mult),
                 reads=[acc, r], writes=[o_tok[a]])

    load_slot_small(0)
    for a in range(NS):
        if a + 1 < NS:
            load_slot_small(a + 1)
        kb_next = load_kv(a, 0)
        indexer(a)
        units = [(a, hp, hh, blk) for hp in range(8) for hh in range(2) for blk in range(a + 1)]
        kbufs = {0: kb_next}
        qk(units[0], 0, kbufs[0])
        for n, u in enumerate(units):
            _, hp, hh, blk = u
            if hh == 0 and blk == 0 and hp + 1 < 8:
                kbufs[hp + 1] = load_kv(a, hp + 1)
            if n + 1 < len(units):
                qk(units[n + 1], n + 1, kbufs[units[n + 1][1]])
            softmax_pv(u, n, kbufs[hp])


BF = ml_dtypes.bfloat16


def rep(v, n=128):
    return np.ascontiguousarray(np.tile(np.asarray(v).reshape(1, -1), (n, 1)))


def own_tiles(arr_bs, c):
    bb, j = c // 4, c % 4
    a = arr_bs[bb]
    return np.ascontiguousarray(a.reshape(64, 128, *a.shape[1:])[j::4].reshape(2048, *a.shape[1:]))


def gather_tiles(per_core, bb):
    out = np.empty((64,) + per_core[0].shape[1:], per_core[0].dtype)
    for j in range(4):
        out[j::4] = per_core[bb * 4 + j]
    return out


def diff_masks(c):
    j = c % 4
    m = np.zeros((128, 4, 128), np.float32)
    for i in range(4):
        if i < j:
            m[:, i, :] = 1.0
        elif i == j:
            m[0:64, i, :] = 1.0
            m[64:128, i, 64:128] = 1.0
    return m.reshape(128, 512).astype(BF)


def dsa_negmask(c):
    return np.where(diff_masks(c).astype(np.float32).reshape(128, 4, 128).transpose(2, 1, 0).reshape(128, 512) > 0,
                    0.0, -1e30).astype(np.float32)


def build_L1():
    nc = bass.Bass("TRN2", target_bir_lowering=False)
    b = B(nc)
    io = {
        "x": b.dram("x", [2048, 1024], F32, "ExternalInput"),
        "pos": b.dram("pos", [128, 16], I32, "ExternalInput"),
        "gmix": b.dram("gmix", [128, 1024], F32, "ExternalInput"),
        "gqk": b.dram("gqk", [128, 2048], F32, "ExternalInput"),
        "w_in": b.dram("w_in", [1024, 3072], F32, "ExternalInput"),
        "QT": b.dram("QT", [16, 128, 1024], BF16, "ExternalOutput"),
        "KT": b.dram("KT", [16, 128, 1024], BF16, "ExternalOutput"),
        "V": b.dram("V", [16, 128, 8 * 129], BF16, "ExternalOutput"),
    }
    phase_diff_proj(b, io)
    b.finish()
    return nc


def build_L2():
    nc = bass.Bass("TRN2", target_bir_lowering=False)
    b = B(nc)
    io = {
        "QT": b.dram("QT", [16, 128, 1024], BF16, "ExternalInput"),
        "KTall": b.dram("KTall", [8, 128, 8192], BF16, "ExternalInput"),
        "Vall": b.dram("Vall", [8, 128, 64 * 129], BF16, "ExternalInput"),
        "maskT": b.dram("maskT", [128, 512], BF16, "ExternalInput"),
        "lam": b.dram("lam", [128, 256], F32, "ExternalInput"),
        "gsub": b.dram("gsub", [128, 128], F32, "ExternalInput"),
        "x": b.dram("x", [2048, 1024], F32, "ExternalInput"),
        "pos": b.dram("pos", [128, 16], I32, "ExternalInput"),
        "w_out": b.dram("w_out", [1024, 1024], F32, "ExternalInput"),
        "gmlp": b.dram("gmlp", [128, 1024], F32, "ExternalInput"),
        "w1": b.dram("w1", [1024, 4096], F32, "ExternalInput"),
        "w2": b.dram("w2", [4096, 1024], F32, "ExternalInput"),
        "gmix1": b.dram("gmix1", [128, 1024], F32, "ExternalInput"),
        "gqk2": b.dram("gqk2", [128, 2048], F32, "ExternalInput"),
        "gcq": b.dram("gcq", [128, 256], F32, "ExternalInput"),
        "w_in2": b.dram("w_in2", [1024, 2376], F32, "ExternalInput"),
        "w_uq": b.dram("w_uq", [256, 1024], F32, "ExternalInput"),
        "w_uqi": b.dram("w_uqi", [256, 512], F32, "ExternalInput"),
        "X2": b.dram("X2", [2048, 1024], F32, "ExternalOutput"),
        "QT2": b.dram("QT2", [16, 128, 1024], BF16, "ExternalOutput"),
        "KT2": b.dram("KT2", [16, 128, 1024], BF16, "ExternalOutput"),
        "V2": b.dram("V2", [16, 128, 16 * 65], BF16, "ExternalOutput"),
        "KI": b.dram("KI", [16, 128, 128], BF16, "ExternalOutput"),
        "QI": b.dram("QI", [16, 128, 512], BF16, "ExternalOutput"),
        "SG": b.dram("SG", [16, 128, 8], F32, "ExternalOutput"),
    }
    b.ident(); b.eps()
    pos_t = b.sb([128, NS], I32, "pos")
    b.op("sp", lambda e: e.dma_start(out=pos_t[:], in_=io["pos"][:]), writes=[pos_t], dma="ld_pos")
    cos, sin = rope_tables(b, pos_t)
    o_tok = [b.sb([128, 1024], BF16, f"otok{a}", top=True) for a in range(NS)]
    m1 = b.mark()
    phase_diff_attn(b, io, o_tok)
    b.release(m1)
    x_res = [b.sb([128, 1024], F32, f"xres{a}") for a in range(NS)]
    h2T = b.sb([128, 8, 2048], BF16, "h2T")
    m2 = b.mark()
    phase_post_attn(b, io, o_tok, x_res, h2T, io["w_out"][:], io["gmlp"][:], io["x"])
    b.release(m2)
    b.hi = ARENA_END
    m3 = b.mark()
    phase_mlp(b, x_res, h2T, io["w1"], io["w2"])
    b.release(m3)
    for a in range(NS):
        b.store("sp", lambda e, a=a: e.dma_start(out=io["X2"][a * 128:(a + 1) * 128, :], in_=x_res[a][:]),
                reads=[x_res[a]], dma="st_x2")
    phase_dsa_proj(b, io, x_res, cos, sin)
    b.finish()
    return nc


def l1_inputs(inp, c):
    return {"x": own_tiles(inp["x"], c),
            "pos": np.ascontiguousarray(own_tiles(inp["positions"], c).reshape(16, 128).T),
            "gmix": rep(inp["norm_mix"][0]),
            "gqk": rep(np.concatenate([np.tile(inp["diff_q_norm"][0], 16), np.tile(inp["diff_k_norm"][0], 16)])),
            "w_in": np.ascontiguousarray(inp["diff_w_in"][0])}


def l2_inputs(inp, r1):
    KTall = []; Vall = []
    for bb in range(2):
        kt = gather_tiles([r["KT"].reshape(16, 128, 8, 128) for r in r1], bb)
        KTall.append(np.ascontiguousarray(kt.transpose(2, 1, 0, 3).reshape(8, 128, 8192)))
        v = gather_tiles([r["V"].reshape(16, 128, 8, 129) for r in r1], bb)
        Vall.append(np.ascontiguousarray(v.transpose(2, 1, 0, 3).reshape(8, 128, 64 * 129)))
    lam = rep(np.concatenate([inp["diff_lam_q1"][0], inp["diff_lam_k1"][0], inp["diff_lam_q2"][0], inp["diff_lam_k2"][0]]))
    gqk2 = rep(np.concatenate([np.tile(inp["dsa_q_norm"][0], 16), np.tile(inp["dsa_k_norm"][0], 16)]))
    ins = []
    for c in range(8):
        ins.append({"QT": r1[c]["QT"], "KTall": KTall[c // 4], "Vall": Vall[c // 4], "maskT": diff_masks(c),
                    "lam": lam, "gsub": rep(inp["diff_subln"][0]),
                    "x": own_tiles(inp["x"], c),
                    "pos": np.ascontiguousarray(own_tiles(inp["positions"], c).reshape(16, 128).T),
                    "w_out": np.ascontiguousarray(inp["diff_w_out"][0]), "gmlp": rep(inp["norm_mlp"][0]),
                    "w1": np.ascontiguousarray(inp["mlp_w1"][0]), "w2": np.ascontiguousarray(inp["mlp_w2"][0]),
                    "gmix1": rep(inp["norm_mix"][1]), "gqk2": gqk2, "gcq": rep(inp["dsa_cq_norm"][0]),
                    "w_in2": np.ascontiguousarray(inp["dsa_w_in"][0]), "w_uq": np.ascontiguousarray(inp["dsa_w_uq"][0]),
                    "w_uqi": np.ascontiguousarray(inp["dsa_w_uq_idx"][0])})
    return ins


def run(nc, ins):
    res = run_bass_kernel_spmd(nc, ins, core_ids=list(range(8)))
    return [{k: np.asarray(v) for k, v in r.items()} for r in res.results]


def build_L3():
    nc = bass.Bass("TRN2", target_bir_lowering=False)
    b = B(nc)
    io = {
        "QT2": b.dram("QT2", [16, 128, 1024], BF16, "ExternalInput"),
        "QI": b.dram("QI", [16, 128, 512], BF16, "ExternalInput"),
        "SG": b.dram("SG", [16, 128, 8], F32, "ExternalInput"),
        "KIall": b.dram("KIall", [128, 8192], BF16, "ExternalInput"),
        "KT2all": b.dram("KT2all", [8, 128, 8192], BF16, "ExternalInput"),
        "V2all": b.dram("V2all", [8, 128, 64 * 130], BF16, "ExternalInput"),
        "negmask": b.dram("negmask", [128, 512], F32, "ExternalInput"),
        "X2": b.dram("X2", [2048, 1024], F32, "ExternalInput"),
        "w_out": b.dram("w_out", [1024, 1024], F32, "ExternalInput"),
        "gmlp": b.dram("gmlp", [128, 1024], F32, "ExternalInput"),
        "w1": b.dram("w1", [1024, 4096], F32, "ExternalInput"),
        "w2": b.dram("w2", [4096, 1024], F32, "ExternalInput"),
        "OUT": b.dram("OUT", [2048, 1024], F32, "ExternalOutput"),
    }
    b.ident(); b.eps()
    o_tok = [b.sb([128, 1024], BF16, f"otok{a}", top=True) for a in range(NS)]
    m1 = b.mark()
    phase_dsa_attn(b, io, o_tok)
    b.release(m1)
    x_res = [b.sb([128, 1024], F32, f"xres{a}") for a in range(NS)]
    h2T = b.sb([128, 8, 2048], BF16, "h2T")
    m2 = b.mark()
    phase_post_attn(b, io, o_tok, x_res, h2T, io["w_out"][:], io["gmlp"][:], io["X2"])
    b.release(m2)
    b.hi = ARENA_END
    m3 = b.mark()
    phase_mlp(b, x_res, h2T, io["w1"], io["w2"])
    b.release(m3)
    for a in range(NS):
        b.store("sp", lambda e, a=a: e.dma_start(out=io["OUT"][a * 128:(a + 1) * 128, :], in_=x_res[a][:]),
                reads=[x_res[a]], dma="st_out")
    b.finish()
    return nc


def l3_inputs(inp, r2):
    KIall = []; KTall = []; Vall = []
    for bb in range(2):
        ki = gather_tiles([r["KI"] for r in r2], bb)
        KIall.append(np.ascontiguousarray(ki.transpose(1, 0, 2).reshape(128, 8192)))
        kt = gather_tiles([r["KT2"].reshape(16, 128, 8, 128) for r in r2], bb)
        KTall.append(np.ascontiguousarray(kt.transpose(2, 1, 0, 3).reshape(8, 128, 8192)))
        v = gather_tiles([r["V2"].reshape(16, 128, 8, 130) for r in r2], bb)
        Vall.append(np.ascontiguousarray(v.transpose(2, 1, 0, 3).reshape(8, 128, 64 * 130)))
    ins = []
    for c in range(8):
        ins.append({"QT2": r2[c]["QT2"], "QI": r2[c]["QI"], "SG": r2[c]["SG"], "KIall": KIall[c // 4],
                    "KT2all": KTall[c // 4], "V2all": Vall[c // 4], "negmask": dsa_negmask(c),
                    "X2": r2[c]["X2"], "w_out": np.ascontiguousarray(inp["dsa_w_out"][0]), "gmlp": rep(inp["norm_mlp"][1]),
                    "w1": np.ascontiguousarray(inp["mlp_w1"][1]), "w2": np.ascontiguousarray(inp["mlp_w2"][1])})
    return ins


def assemble(r3):
    out = np.empty((2, 8192, 1024), np.float32)
    for bb in range(2):
        o = gather_tiles([r["OUT"].reshape(16, 128, 1024) for r in r3], bb)
        out[bb] = o.reshape(8192, 1024)
    return out


FUSED_IN = [
    ("x", [2048, 1024], F32), ("pos", [128, 16], I32), ("gmix", [128, 1024], F32), ("gqk", [128, 2048], F32),
    ("w_in", [1024, 3072], F32), ("maskT", [128, 512], BF16), ("lam", [128, 256], F32), ("gsub", [128, 128], F32),
    ("w_out", [1024, 1024], F32), ("gmlp", [128, 1024], F32), ("w1", [1024, 4096], F32), ("w2", [4096, 1024], F32),
    ("gmix1", [128, 1024], F32), ("gqk2", [128, 2048], F32), ("gcq", [128, 256], F32), ("w_in2", [1024, 2376], F32),
    ("w_uq", [256, 1024], F32), ("w_uqi", [256, 512], F32), ("negmask", [128, 512], F32),
    ("w_outb", [1024, 1024], F32), ("gmlpb", [128, 1024], F32), ("w1b", [1024, 4096], F32), ("w2b", [4096, 1024], F32),
]


def build_fused():
    nc = bass.Bass("TRN2", target_bir_lowering=False)
    _RANK.clear()
    b = B(nc)
    io = {n: b.dram(n, s, d, "ExternalInput") for (n, s, d) in FUSED_IN}
    io["OUT"] = b.dram("OUT", [2048, 1024], F32, "ExternalOutput")
    qt2 = nc.dram_tensor("scr_qt2", [16, 128, 1024], BF16).ap()
    qi = nc.dram_tensor("scr_qi", [16, 128, 512], BF16).ap()
    sg = nc.dram_tensor("scr_sg", [16, 128, 8], F32).ap()
    x2 = nc.dram_tensor("scr_x2", [2048, 1024], F32).ap()
    scr = {"QT2": [T(qt2[a]) for a in range(NS)], "QI": [T(qi[a]) for a in range(NS)], "SG": [T(sg[a]) for a in range(NS)]}
    x2d = [T(x2[a * 128:(a + 1) * 128, :]) for a in range(NS)]

    b.ident(); b.eps()
    pos_t = b.sb([128, NS], I32, "pos")
    b.op("sp", lambda e: e.dma_start(out=pos_t[:], in_=io["pos"][:]), writes=[pos_t], dma="ld_pos")
    cos, sin = rope_tables(b, pos_t)
    o_tok = [b.sb([128, 1024], BF16, f"otok{a}", top=True) for a in range(NS)]
    hi_otok = b.hi
    qT_res = b.sb([128, NS, 1024], BF16, "qTres", top=True)
    m0 = b.mark()
    zt = b.sb([128, 4096], BF16, "zeros")
    b.op("pool", lambda e: e.memset(zt[:], 0.0), writes=[zt])
    G0 = Gather(b, "g0", 8, 4096, zt)
    G1 = Gather(b, "g1", 8, 4096, zt)
    GK = Gather(b, "gk", 1, 2048, zt)
    f_diff_proj(b, io, qT_res, G0, cos, sin)
    b.release(m0)
    f_diff_attn(b, io, o_tok, qT_res, G0)
    b.release(m0)
    b.hi = hi_otok
    x_res = [b.sb([128, 1024], F32, f"xres{a}") for a in range(NS)]
    h2T = b.sb([128, 8, 2048], BF16, "h2T")
    m2 = b.mark()
    phase_post_attn(b, io, o_tok, x_res, h2T, io["w_out"][:], io["gmlp"][:], io["x"])
    b.release(m2)
    b.hi = ARENA_END
    m3 = b.mark()
    phase_mlp(b, x_res, h2T, io["w1"], io["w2"])
    b.release(m3)
    for a in range(NS):
        b.op("sp", lambda e, a=a: e.dma_start(out=x2d[a][:], in_=x_res[a][:]), reads=[x_res[a]], writes=[x2d[a]], dma="st_x2")
    io2 = dict(io)
    f_dsa_proj(b, io2, x_res, cos, sin, G1, GK, scr)
    b.release(m0)
    b.hi = ARENA_END
    o_tok = [b.sb([128, 1024], BF16, f"otokb{a}", top=True) for a in range(NS)]
    m4 = b.mark()
    f_dsa_attn(b, io, o_tok, G1, GK, scr)
    b.release(m4)
    x_res = [b.sb([128, 1024], F32, f"xresb{a}") for a in range(NS)]
    h2T = b.sb([128, 8, 2048], BF16, "h2Tb")
    m5 = b.mark()
    phase_post_attn(b, io, o_tok, x_res, h2T, io["w_outb"][:], io["gmlpb"][:], x2, xdeps=x2d)
    b.release(m5)
    b.hi = ARENA_END
    phase_mlp(b, x_res, h2T, io["w1b"], io["w2b"])
    for a in range(NS):
        b.store("sp", lambda e, a=a: e.dma_start(out=io["OUT"][a * 128:(a + 1) * 128, :], in_=x_res[a][:]),
                reads=[x_res[a]], dma="st_out")
    b.finish()
    return nc


def fused_inputs(inp):
    lam = rep(np.concatenate([inp["diff_lam_q1"][0], inp["diff_lam_k1"][0], inp["diff_lam_q2"][0], inp["diff_lam_k2"][0]]))
    gqk = rep(np.concatenate([np.tile(inp["diff_q_norm"][0], 16), np.tile(inp["diff_k_norm"][0], 16)]))
    gqk2 = rep(np.concatenate([np.tile(inp["dsa_q_norm"][0], 16), np.tile(inp["dsa_k_norm"][0], 16)]))
    c_ = np.ascontiguousarray
    shared = {"gmix": rep(inp["norm_mix"][0]), "gqk": gqk, "w_in": c_(inp["diff_w_in"][0]), "lam": lam,
              "gsub": rep(inp["diff_subln"][0]), "w_out": c_(inp["diff_w_out"][0]), "gmlp": rep(inp["norm_mlp"][0]),
              "w1": c_(inp["mlp_w1"][0]), "w2": c_(inp["mlp_w2"][0]), "gmix1": rep(inp["norm_mix"][1]), "gqk2": gqk2,
              "gcq": rep(inp["dsa_cq_norm"][0]), "w_in2": c_(inp["dsa_w_in"][0]), "w_uq": c_(inp["dsa_w_uq"][0]),
              "w_uqi": c_(inp["dsa_w_uq_idx"][0]), "w_outb": c_(inp["dsa_w_out"][0]), "gmlpb": rep(inp["norm_mlp"][1]),
              "w1b": c_(inp["mlp_w1"][1]), "w2b": c_(inp["mlp_w2"][1])}
    ins = []
    for c in range(8):
        d = dict(shared)
        d["x"] = own_tiles(inp["x"], c)
        d["pos"] = np.ascontiguousarray(own_tiles(inp["positions"], c).reshape(16, 128).T)
        d["maskT"] = diff_masks(c)
        d["negmask"] = dsa_negmask(c)
        ins.append(d)
    return ins


def kernel(**inputs):
    inp = {k: np.asarray(v) for k, v in inputs.items()}
    r = run(build_fused(), fused_inputs(inp))
    return assemble(r)
```

```python
import math
from contextlib import ExitStack
import numpy as np
import ml_dtypes
import concourse.bass as bass
import concourse.mybir as mybir
from concourse.bass_utils import run_bass_kernel_spmd


F32 = mybir.dt.float32
BF16 = mybir.dt.bfloat16
I32 = mybir.dt.int32
AF = mybir.ActivationFunctionType
ALU = mybir.AluOpType
AX = mybir.AxisListType


class Dep:
    __slots__ = ("w", "r")

    def __init__(self):
        self.w = None
        self.r = {}


class Sched:
    ENG = ("pe", "act", "dve", "pool", "sp")

    def __init__(self, nc):
        self.nc = nc
        self.ops = {e: [] for e in self.ENG}
        self.cnt = {e: 0 for e in self.ENG}
        self.known = {e: {} for e in self.ENG}
        self.dma_cnt = {}
        self.stack = ExitStack()
        self.nt = 0

    def sb(self, shape, dtype, name=None):
        self.nt += 1
        name = "sb_" + (name or f"t{self.nt}")
        return self.stack.enter_context(self.nc.sbuf_tensor(name, list(shape), dtype))

    def ps(self, shape, dtype, name=None):
        self.nt += 1
        name = "ps_" + (name or f"p{self.nt}")
        return self.stack.enter_context(self.nc.psum_tensor(name, list(shape), dtype))

    def op(self, eng, fn, reads=(), writes=(), dma=None, sem_inc=16):
        waits = {}

        def need(ev, raw):
            if ev is None:
                return
            key, val = ev
            if key == eng:
                if eng == "pe" or not raw:
                    return
            if waits.get(key, 0) < val:
                waits[key] = val

        for d in reads:
            need(d.w, True)
        for d in writes:
            need(d.w, False)
            for ev in d.r.items():
                need(ev, False)
        kn = self.known[eng]
        wl = []
        for key, val in waits.items():
            if kn.get(key, 0) >= val:
                continue
            kn[key] = val
            wl.append((key, val))
        if dma is not None:
            n = self.dma_cnt.get(dma, 0) + sem_inc
            self.dma_cnt[dma] = n
            ev = (dma, n)
            inc = (dma, sem_inc)
        else:
            self.cnt[eng] += 1
            ev = (eng, self.cnt[eng])
            inc = (eng, 1)
        self.ops[eng].append((wl, fn, inc))
        for d in reads:
            if d.r.get(ev[0], 0) < ev[1]:
                d.r[ev[0]] = ev[1]
        for d in writes:
            d.w = ev
            d.r = {}
        return ev

    def final_wait(self, eng, deps):
        waits = {}
        for d in deps:
            for ev in ([d.w] if d.w else []) + list(d.r.items()):
                if waits.get(ev[0], 0) < ev[1]:
                    waits[ev[0]] = ev[1]
        self.ops[eng].append((list(waits.items()), None, None))

    def barrier(self):
        waits = {e: self.cnt[e] for e in self.ENG if self.cnt[e] > 0}
        for k, n in self.dma_cnt.items():
            if not k.startswith("cc_"):
                waits[k] = n
        for e in self.ENG:
            kn = self.known[e]
            wl = []
            for key, val in waits.items():
                if key == e or kn.get(key, 0) >= val:
                    continue
                kn[key] = val
                wl.append((key, val))
            self.ops[e].append((wl, None, None))

    def final_events(self, eng, evs):
        waits = {}
        for ev in evs:
            if waits.get(ev[0], 0) < ev[1]:
                waits[ev[0]] = ev[1]
        self.ops[eng].append((list(waits.items()), None, None))

    def emit(self):
        nc = self.nc
        keys = set(self.ENG) | set(self.dma_cnt.keys())
        assert len(keys) <= 100, f"too many semaphores: {len(keys)}"
        sems = {}
        for k in sorted(keys):
            sems[k] = self.stack.enter_context(nc.semaphore("s_" + k))
        ops = self.ops

        def run(e, lst):
            for wl, fn, inc in lst:
                for key, val in wl:
                    e.wait_ge(sems[key], val)
                if fn is None:
                    continue
                ins = fn(e)
                if inc is not None:
                    ins.then_inc(sems[inc[0]], inc[1])

        with nc.Block() as block:
            @block.tensor
            def _(e):
                run(e, ops["pe"])

            @block.scalar
            def _(e):
                run(e, ops["act"])

            @block.vector
            def _(e):
                run(e, ops["dve"])

            @block.gpsimd
            def _(e):
                run(e, ops["pool"])

            @block.sync
            def _(e):
                run(e, ops["sp"])
        self.stack.close()


NS = 16
D = 1024
EPS = 1e-6
INV_FREQ = [500000.0 ** (-(2.0 * j) / 16.0) for j in range(8)]
TWO_PI_S = 6.28318
PI_S = 3.14159


class T:
    def __init__(self, t, d=None):
        self.t = t
        self.d = d if d is not None else Dep()

    def __getitem__(self, k):
        return self.t[k]


ARENA_BASE = 16512
ARENA_END = 16512 + 212736


class B:
    def __init__(self, nc):
        self.nc = nc
        self.S = Sched(nc)
        self._consts = {}
        self.final = []
        self.arena = nc.alloc_sbuf_tensor("arena", [128, ARENA_END - ARENA_BASE], mybir.dt.uint8)
        self.lo = ARENA_BASE
        self.hi = ARENA_END
        self.nt = 0
        self.banks = [T(nc.alloc_psum_tensor(f"bank{i}", [128, 512], F32)) for i in range(8)]
        self.banks16 = [T(bk.t[:].bitcast(BF16), bk.d) for bk in self.banks]

    def sb(self, shape, dt, name=None, top=False):
        self.nt += 1
        nm = f"sb{self.nt}_{name or 't'}"
        size = int(np.prod(shape[1:])) * mybir.dt.size(dt)
        size = (size + 31) // 32 * 32
        if top:
            self.hi -= size
            off = self.hi
        else:
            off = self.lo
            self.lo += size
        assert self.lo <= self.hi, f"SBUF arena overflow at {nm}: lo={self.lo} hi={self.hi}"
        return T(self.nc.alloc_sbuf_tensor_at(nm, list(shape), dt, offset=off))

    def mark(self):
        return (self.lo, self.hi)

    def release(self, m):
        self.lo, self.hi = m
        self.S.barrier()

    def dram(self, name, shape, dt, kind):
        t = T(self.nc.dram_tensor(name, list(shape), dt, kind=kind).ap())
        return t

    def op(self, eng, fn, reads=(), writes=(), dma=None, sem_inc=16):
        return self.S.op(eng, fn, [x.d for x in reads], [x.d for x in writes], dma, sem_inc)

    def store(self, eng, fn, reads, dma):
        ev = self.S.op(eng, fn, [x.d for x in reads], [], dma)
        self.final.append(ev)
        return ev

    def finish(self):
        self.S.final_events("sp", self.final)
        self.S.emit()

    def ident(self):
        if "ident" not in self._consts:
            idt = self.sb([128, 128], BF16, "ident")
            self.op("pool", lambda e: e.memset(idt[:], 0.0), writes=[idt])
            self.op("pool", lambda e: e.affine_select(out=idt[:], in_=idt[:], pattern=[[-1, 128]],
                                                     compare_op=ALU.not_equal, fill=1.0, base=0,
                                                     channel_multiplier=1), reads=[idt], writes=[idt])
            self._consts["ident"] = idt
        return self._consts["ident"]

    def eps(self):
        if "eps" not in self._consts:
            t = self.sb([128, 1], F32, "eps")
            self.op("pool", lambda e: e.memset(t[:], EPS), writes=[t])
            self._consts["eps"] = t
        return self._consts["eps"]


def rstd_from_ss(b, ss, n, scale):
    eps = b.eps()
    b.op("act", lambda e: e.activation(out=ss[:, 0:n], in_=ss[:, 0:n], func=AF.Ln, bias=eps[:], scale=scale),
         reads=[ss, eps], writes=[ss])
    b.op("act", lambda e: e.activation(out=ss[:, 0:n], in_=ss[:, 0:n], func=AF.Exp, scale=-0.5),
         reads=[ss], writes=[ss])


def rope_tables(b, pos_t):
    posf = b.sb([128, NS], F32, "posf")
    inv = b.sb([128, 8], F32, "invf")
    ang = b.sb([128, NS, 8], F32, "ang")
    ti = b.sb([128, NS, 8], I32, "angi")
    tf = b.sb([128, NS, 8], F32, "angf")
    neg = b.sb([128, NS, 8], F32, "angn")
    cos = b.sb([128, NS, 8], F32, "cos")
    sin = b.sb([128, NS, 8], F32, "sin")
    nb = b.sb([128, 1], F32, "negpi")
    b.op("pool", lambda e: e.memset(nb[:], -PI_S), writes=[nb])
    b.op("dve", lambda e: e.tensor_copy(out=posf[:], in_=pos_t[:]), reads=[pos_t], writes=[posf])
    for j in range(8):
        b.op("pool", lambda e, j=j: e.memset(inv[:, j:j + 1], INV_FREQ[j] / (2 * math.pi)), writes=[inv])
    b.op("dve", lambda e: e.tensor_tensor(out=ang[:], in0=posf[:].unsqueeze(2).to_broadcast([128, NS, 8]),
                                          in1=inv[:].unsqueeze(1).to_broadcast([128, NS, 8]), op=ALU.mult),
         reads=[posf, inv], writes=[ang])
    for (dst, off) in ((sin, 0.5), (cos, 0.75)):
        b.op("dve", lambda e, off=off: e.tensor_scalar(out=tf[:], in0=ang[:], scalar1=off, scalar2=None, op0=ALU.add),
             reads=[ang], writes=[tf])
        b.op("dve", lambda e: e.tensor_copy(out=ti[:], in_=tf[:]), reads=[tf], writes=[ti])
        b.op("dve", lambda e: e.tensor_copy(out=neg[:], in_=ti[:]), reads=[ti], writes=[neg])
        b.op("dve", lambda e: e.tensor_tensor(out=tf[:], in0=tf[:], in1=neg[:], op=ALU.subtract),
             reads=[tf, neg], writes=[tf])
        b.op("dve", lambda e: e.tensor_scalar(out=neg[:], in0=tf[:], scalar1=0.0, scalar2=None, op0=ALU.is_lt),
             reads=[tf], writes=[neg])
        b.op("dve", lambda e: e.tensor_tensor(out=tf[:], in0=tf[:], in1=neg[:], op=ALU.add),
             reads=[tf, neg], writes=[tf])
        b.op("act", lambda e, dst=dst: e.activation(out=dst[:], in_=tf[:], func=AF.Sin, bias=nb[:], scale=TWO_PI_S),
             reads=[tf, nb], writes=[dst])
    return cos, sin


def rmsnorm_transpose(b, xt, gt, hbf, hT, pT, ss, junk):
    idt = b.ident()
    b.op("act", lambda e: e.activation(out=junk[:], in_=xt[:], func=AF.Square, accum_out=ss[:, 0:1]),
         reads=[xt], writes=[junk, ss])
    rstd_from_ss(b, ss, 1, 1.0 / D)
    b.op("dve", lambda e: e.scalar_tensor_tensor(out=hbf[:], in0=xt[:], scalar=ss[:, 0:1], in1=gt[:],
                                                 op0=ALU.mult, op1=ALU.mult),
         reads=[xt, ss, gt], writes=[hbf])
    for kc in range(8):
        b.op("pe", lambda e, kc=kc: e.transpose(out=pT[:, kc * 128:(kc + 1) * 128],
                                                in_=hbf[:, kc * 128:(kc + 1) * 128], identity=idt[:]),
             reads=[hbf, idt], writes=[pT])
    b.op("dve", lambda e: e.tensor_copy(out=hT[:].rearrange("p a b -> p (a b)"), in_=pT[:]),
         reads=[pT], writes=[hT])


def headnorm_rope(b, stage, sq, ssq, nh, gain, cos_a, sin_a, outbf, tmp, norm=True):
    s3 = stage[:].rearrange("p (h d) -> p h d", d=64)
    if norm:
        b.op("dve", lambda e: e.tensor_reduce(out=ssq[:, 0:nh], in_=sq[:].rearrange("p (h d) -> p h d", d=64),
                                              axis=AX.X, op=ALU.add), reads=[sq], writes=[ssq])
        rstd_from_ss(b, ssq, nh, 1.0 / 64)
    b.op("dve", lambda e: e.tensor_tensor(out=s3, in0=s3, in1=ssq[:, 0:nh].unsqueeze(2).to_broadcast([128, nh, 64]),
                                          op=ALU.mult), reads=[stage, ssq], writes=[stage])
    if gain is not None:
        b.op("pool", lambda e: e.tensor_tensor(out=stage[:], in0=stage[:], in1=gain[:], op=ALU.mult),
             reads=[stage, gain], writes=[stage])
    b.op("act", lambda e: e.activation(out=outbf[:], in_=stage[:], func=AF.Copy), reads=[stage], writes=[outbf])
    o3 = outbf[:].rearrange("p (h d) -> p h d", d=64)
    x1 = s3[:, :, 0:8]
    x2 = s3[:, :, 8:16]
    cb = cos_a.unsqueeze(1).to_broadcast([128, nh, 8])
    sb_ = sin_a.unsqueeze(1).to_broadcast([128, nh, 8])
    t = tmp[:].rearrange("p (k h d) -> p k h d", k=4, d=8)
    eng = "pool"
    b.op(eng, lambda e: e.tensor_tensor(out=t[:, 0, 0:nh, :], in0=x1, in1=cb, op=ALU.mult), reads=[stage], writes=[tmp])
    b.op(eng, lambda e: e.tensor_tensor(out=t[:, 1, 0:nh, :], in0=x2, in1=sb_, op=ALU.mult), reads=[stage], writes=[tmp])
    b.op(eng, lambda e: e.tensor_tensor(out=t[:, 2, 0:nh, :], in0=x2, in1=cb, op=ALU.mult), reads=[stage], writes=[tmp])
    b.op(eng, lambda e: e.tensor_tensor(out=t[:, 3, 0:nh, :], in0=x1, in1=sb_, op=ALU.mult), reads=[stage], writes=[tmp])
    b.op(eng, lambda e: e.tensor_tensor(out=o3[:, :, 0:8], in0=t[:, 0, 0:nh, :], in1=t[:, 1, 0:nh, :], op=ALU.subtract),
         reads=[tmp], writes=[outbf])
    b.op(eng, lambda e: e.tensor_tensor(out=o3[:, :, 8:16], in0=t[:, 2, 0:nh, :], in1=t[:, 3, 0:nh, :], op=ALU.add),
         reads=[tmp], writes=[outbf])


def phase_diff_proj(b, io):
    idt = b.ident()
    pos_t = b.sb([128, NS], I32, "pos")
    b.op("sp", lambda e: e.dma_start(out=pos_t[:], in_=io["pos"][:]), writes=[pos_t], dma="ld_pos")
    gmix = b.sb([128, D], F32, "gmix")
    b.op("sp", lambda e: e.dma_start(out=gmix[:], in_=io["gmix"][:]), writes=[gmix], dma="ld_gmix")
    gqk = b.sb([128, 2048], F32, "gqk")
    b.op("sp", lambda e: e.dma_start(out=gqk[:], in_=io["gqk"][:]), writes=[gqk], dma="ld_gqk")
    b.op("pool", lambda e: e.tensor_scalar(out=gqk[:, 0:1024], in0=gqk[:, 0:1024], scalar1=0.125, scalar2=None,
                                           op0=ALU.mult), reads=[gqk], writes=[gqk])
    w = b.sb([128, 8, 3072], BF16, "w_in")
    for kc in range(8):
        for hf in range(3):
            b.op("pool", lambda e, kc=kc, hf=hf: e.dma_start(
                out=w[:, kc, hf * 1024:(hf + 1) * 1024],
                in_=io["w_in"][kc * 128:(kc + 1) * 128, hf * 1024:(hf + 1) * 1024]),
                writes=[w], dma="ld_w")
    cos, sin = rope_tables(b, pos_t)

    xts = [b.sb([128, D], F32, f"xt{i}") for i in range(2)]
    junk = b.sb([128, D], BF16, "junk")
    ss = b.sb([128, 1], F32, "ss")
    hbf = b.sb([128, D], BF16, "hbf")
    hT = b.sb([128, 8, 128], BF16, "hT")
    pT = [b.banks16[0], b.banks16[1]]
    pY = [b.banks[2 + i] for i in range(4)]
    stage = b.sb([128, 2048], F32, "stage")
    sq = b.sb([128, 2048], F32, "sq")
    ssq = b.sb([128, 32], F32, "ssq")
    tmp = b.sb([128, 4 * 32 * 8], F32, "ropetmp")
    qkbf = b.sb([128, 2048], BF16, "qkbf")
    qkT = [b.sb([128, 16, 128], BF16, f"qkT{i}") for i in range(2)]
    vaug = [b.sb([128, 8, 129], BF16, f"vaug{i}") for i in range(2)]
    for i in range(2):
        b.op("pool", lambda e, i=i: e.memset(vaug[i][:], 1.0), writes=[vaug[i]])

    for a in range(NS):
        xt = xts[a % 2]
        b.op("sp", lambda e, a=a, xt=xt: e.dma_start(out=xt[:], in_=io["x"][a * 128:(a + 1) * 128, :]),
             writes=[xt], dma=f"ld_x{a % 2}")
        rmsnorm_transpose(b, xt, gmix, hbf, hT, pT[0], ss, junk)
        for n in range(6):
            py = pY[n % 4]
            for kc in range(8):
                b.op("pe", lambda e, n=n, kc=kc, py=py: e.matmul(py[:], lhsT=hT[:, kc, :],
                                                                   rhs=w[:, kc, n * 512:(n + 1) * 512],
                                                                   start=(kc == 0), stop=(kc == 7)),
                     reads=[hT, w], writes=[py])
            if n < 4:
                b.op("act", lambda e, n=n, py=py: e.activation(out=stage[:, n * 512:(n + 1) * 512], in_=py[:], func=AF.Copy),
                     reads=[py], writes=[stage])
                b.op("act", lambda e, n=n, py=py: e.activation(out=sq[:, n * 512:(n + 1) * 512], in_=py[:], func=AF.Square),
                     reads=[py], writes=[sq])
            else:
                va = vaug[a % 2]
                b.op("dve", lambda e, n=n, py=py, va=va: e.tensor_copy(
                    out=va[:, (n - 4) * 4:(n - 4) * 4 + 4, 0:128], in_=py[:].rearrange("p (h d) -> p h d", d=128)),
                    reads=[py], writes=[va])
        va = vaug[a % 2]
        b.store("sp", lambda e, a=a, va=va: e.dma_start(out=io["V"][a], in_=va[:].rearrange("p h d -> p (h d)")),
                reads=[va], dma=f"st_v{a % 2}")
        headnorm_rope(b, stage, sq, ssq, 32, gqk, cos[:, a, :], sin[:, a, :], qkbf, tmp)
        qt = qkT[a % 2]
        for i in range(16):
            pt = pT[1] if i < 8 else pT[0]
            b.op("pe", lambda e, i=i, pt=pt: e.transpose(out=pt[:, (i % 8) * 128:(i % 8 + 1) * 128],
                                                         in_=qkbf[:, i * 128:(i + 1) * 128], identity=idt[:]),
                 reads=[qkbf, idt], writes=[pt])
            if i % 8 == 7:
                b.op("dve", lambda e, i=i, pt=pt, qt=qt: e.tensor_copy(
                    out=qt[:, (i // 8) * 8:(i // 8) * 8 + 8, :].rearrange("p a b -> p (a b)"), in_=pt[:]),
                    reads=[pt], writes=[qt])
        b.store("sp", lambda e, a=a, qt=qt: e.dma_start(out=io["QT"][a], in_=qt[:, 0:8, :].rearrange("p a b -> p (a b)")),
                reads=[qt], dma=f"st_q{a % 2}")
        b.store("sp", lambda e, a=a, qt=qt: e.dma_start(out=io["KT"][a], in_=qt[:, 8:16, :].rearrange("p a b -> p (a b)")),
                reads=[qt], dma=f"st_k{a % 2}")


def phase_diff_attn(b, io, o_tok):
    LAM_INIT = 0.2
    qT = b.sb([128, NS, 1024], BF16, "qT")
    for a in range(NS):
        b.op("sp", lambda e, a=a: e.dma_start(out=qT[:, a, :], in_=io["QT"][a]), writes=[qT], dma="ld_qT")
    maskT = b.sb([128, 512], BF16, "maskT")
    b.op("sp", lambda e: e.dma_start(out=maskT[:], in_=io["maskT"][:]), writes=[maskT], dma="ld_mask")
    lam = b.sb([128, 256], F32, "lam")
    b.op("sp", lambda e: e.dma_start(out=lam[:], in_=io["lam"][:]), writes=[lam], dma="ld_lam")
    gsub = b.sb([128, 128], F32, "gsub")
    b.op("sp", lambda e: e.dma_start(out=gsub[:], in_=io["gsub"][:]), writes=[gsub], dma="ld_gsub")
    b.op("pool", lambda e: e.tensor_scalar(out=gsub[:], in0=gsub[:], scalar1=1.0 - LAM_INIT, scalar2=None,
                                           op0=ALU.mult), reads=[gsub], writes=[gsub])
    lprod = b.sb([128, 128], F32, "lprod")
    l2 = b.sb([128, 2], F32, "l2")
    neglam = b.sb([128, 1], F32, "neglam")
    l4 = lam[:].rearrange("p (a b d) -> p a b d", a=2, b=2)
    b.op("dve", lambda e: e.tensor_tensor(out=lprod[:].rearrange("p (a d) -> p a d", a=2), in0=l4[:, :, 0, :],
                                          in1=l4[:, :, 1, :], op=ALU.mult), reads=[lam], writes=[lprod])
    b.op("dve", lambda e: e.tensor_reduce(out=l2[:], in_=lprod[:].rearrange("p (a d) -> p a d", a=2), axis=AX.X,
                                          op=ALU.add), reads=[lprod], writes=[l2])
    b.op("act", lambda e: e.activation(out=l2[:], in_=l2[:], func=AF.Exp), reads=[l2], writes=[l2])
    b.op("dve", lambda e: e.tensor_tensor(out=neglam[:], in0=l2[:, 1:2], in1=l2[:, 0:1], op=ALU.subtract),
         reads=[l2], writes=[neglam])
    b.op("dve", lambda e: e.tensor_scalar(out=neglam[:], in0=neglam[:], scalar1=-LAM_INIT, scalar2=None, op0=ALU.add),
         reads=[neglam], writes=[neglam])

    ktb = [b.sb([128, 8192], BF16, f"ktb{i}") for i in range(2)]
    vb = [b.sb([128, 64, 129], BF16, f"vb{i}") for i in range(2)]
    pS = [[b.banks[c * 2 + i] for i in range(2)] for c in range(2)]
    pA = [[b.banks[4 + c * 2 + i] for i in range(2)] for c in range(2)]
    pT = [[b.sb([128, 512], BF16, f"pTs{c}{i}") for i in range(2)] for c in range(2)]
    rec = [b.sb([128, 2], F32, f"rec{i}") for i in range(2)]
    o32 = [b.sb([128, 128], F32, f"o32{i}") for i in range(2)]
    oj = b.sb([128, 128], BF16, "ojunk")
    ss1 = [b.sb([128, 1], F32, f"ss1{i}") for i in range(2)]

    units = [(h, a, blk) for h in range(8) for a in range(NS) for blk in range(a + 1)]

    def load_head(h):
        kb, vv = ktb[h % 2], vb[h % 2]
        for part in range(4):
            b.op("sp", lambda e, h=h, kb=kb, part=part: e.dma_start(
                out=kb[:, part * 2048:(part + 1) * 2048], in_=io["KTall"][h][:, part * 2048:(part + 1) * 2048]),
                writes=[kb], dma=f"ld_kt{h % 2}")
            b.op("sp", lambda e, h=h, vv=vv, part=part: e.dma_start(
                out=vv[:, part * 16:(part + 1) * 16, :].rearrange("p a b -> p (a b)"),
                in_=io["Vall"][h][:, part * 16 * 129:(part + 1) * 16 * 129]),
                writes=[vv], dma=f"ld_v{h % 2}")

    def qk(u, n):
        h, a, blk = u
        kb = ktb[h % 2]
        for c in range(2):
            ps = pS[c][n % 2]
            for i in range(4):
                kt = 4 * blk + i
                b.op("pe", lambda e, c=c, i=i, kt=kt, ps=ps, kb=kb, a=a, h=h: e.matmul(
                    ps[:, i * 128:(i + 1) * 128], lhsT=kb[64 * c:64 * c + 64, kt * 128:(kt + 1) * 128],
                    rhs=qT[64 * c:64 * c + 64, a, h * 128:(h + 1) * 128], start=True, stop=True),
                    reads=[kb, qT], writes=[ps])

    def softmax_pv(u, n):
        h, a, blk = u
        vv = vb[h % 2]
        for c in range(2):
            ps = pS[c][n % 2]
            pt = pT[c][n % 2]
            acc = pA[c][a % 2]
            b.op("act", lambda e, ps=ps, pt=pt: e.activation(out=pt[:], in_=ps[:], func=AF.Exp),
                 reads=[ps], writes=[pt])
            if blk == a:
                b.op("pool", lambda e, pt=pt: e.tensor_tensor(out=pt[:], in0=pt[:], in1=maskT[:], op=ALU.mult),
                     reads=[pt, maskT], writes=[pt])
            for i in range(4):
                kt = 4 * blk + i
                b.op("pe", lambda e, i=i, kt=kt, pt=pt, acc=acc, vv=vv, blk=blk, a=a: e.matmul(
                    acc[:, 0:129], lhsT=pt[:, i * 128:(i + 1) * 128], rhs=vv[:, kt, :],
                    start=(blk == 0 and i == 0), stop=(blk == a and i == 3)),
                    reads=[pt, vv], writes=[acc])
        if blk == a:
            evac(h, a)

    def evac(h, a):
        k = a % 2
        a0, a1 = pA[0][k], pA[1][k]
        r, o, s = rec[k], o32[k], ss1[k]
        b.op("dve", lambda e: e.reciprocal(out=r[:, 0:1], in_=a0[:, 128:129]), reads=[a0], writes=[r])
        b.op("dve", lambda e: e.reciprocal(out=r[:, 1:2], in_=a1[:, 128:129]), reads=[a1], writes=[r])
        b.op("dve", lambda e: e.tensor_tensor(out=r[:, 1:2], in0=r[:, 1:2], in1=neglam[:], op=ALU.mult),
             reads=[r, neglam], writes=[r])
        b.op("dve", lambda e: e.tensor_scalar(out=o[:], in0=a0[:, 0:128], scalar1=r[:, 0:1], scalar2=None, op0=ALU.mult),
             reads=[a0, r], writes=[o])
        b.op("dve", lambda e: e.scalar_tensor_tensor(out=o[:], in0=a1[:, 0:128], scalar=r[:, 1:2], in1=o[:],
                                                     op0=ALU.mult, op1=ALU.add), reads=[a1, r, o], writes=[o])
        b.op("act", lambda e: e.activation(out=oj[:], in_=o[:], func=AF.Square, accum_out=s[:, 0:1]),
             reads=[o], writes=[oj, s])
        rstd_from_ss(b, s, 1, 1.0 / 128)
        ot = o_tok[a]
        b.op("dve", lambda e: e.scalar_tensor_tensor(out=ot[:, h * 128:(h + 1) * 128], in0=o[:], scalar=s[:, 0:1],
                                                     in1=gsub[:], op0=ALU.mult, op1=ALU.mult),
             reads=[o, s, gsub], writes=[ot])

    load_head(0)
    qk(units[0], 0)
    for n, u in enumerate(units):
        if u[1] == 0 and u[2] == 0 and u[0] + 1 < 8:
            load_head(u[0] + 1)
        if n + 1 < len(units):
            qk(units[n + 1], n + 1)
        softmax_pv(u, n)


def load_w_bf16(b, wt, src, nk, ncols, key, colchunk=1024):
    for kc in range(nk):
        for c0 in range(0, ncols, colchunk):
            c1 = min(ncols, c0 + colchunk)
            b.op("pool", lambda e, kc=kc, c0=c0, c1=c1: e.dma_start(
                out=wt[:, kc, c0:c1], in_=src[kc * 128:(kc + 1) * 128, c0:c1]), writes=[wt], dma=key)


def phase_post_attn(b, io, o_tok, x_res, h2T, wout_ap, gmlp_ap, xsrc, xdeps=None):
    idt = b.ident()
    m = b.mark()
    wout = b.sb([128, 8, 1024], BF16, "wout")
    load_w_bf16(b, wout, wout_ap, 8, 1024, "ld_wout")
    gm = b.sb([128, D], F32, "gmlp")
    b.op("sp", lambda e: e.dma_start(out=gm[:], in_=gmlp_ap), writes=[gm], dma="ld_gmlp")
    oT = [b.sb([128, 8, 128], BF16, f"oT{i}") for i in range(2)]
    junk = b.sb([128, D], BF16, "junk2")
    ss = [b.sb([128, 1], F32, f"ss2{i}") for i in range(2)]
    hbf = [b.sb([128, D], BF16, f"hbf2{i}") for i in range(2)]
    for a in range(NS):
        xr = x_res[a]
        b.op("sp", lambda e, a=a, xr=xr: e.dma_start(out=xr[:], in_=xsrc[a * 128:(a + 1) * 128, :]),
             reads=([xdeps[a]] if xdeps else []), writes=[xr], dma="ld_xres")
        pt = b.banks16[a % 2]
        ot = oT[a % 2]
        for kc in range(8):
            b.op("pe", lambda e, kc=kc, pt=pt, a=a: e.transpose(out=pt[:, kc * 128:(kc + 1) * 128],
                                                                 in_=o_tok[a][:, kc * 128:(kc + 1) * 128], identity=idt[:]),
                 reads=[o_tok[a], idt], writes=[pt])
        b.op("act", lambda e, pt=pt, ot=ot: e.activation(out=ot[:].rearrange("p a b -> p (a b)"), in_=pt[:], func=AF.Copy),
             reads=[pt], writes=[ot])
        for n in range(2):
            py = b.banks[2 + (2 * a + n) % 4]
            for kc in range(8):
                b.op("pe", lambda e, kc=kc, n=n, py=py, ot=ot: e.matmul(py[:], lhsT=ot[:, kc, :],
                                                                         rhs=wout[:, kc, n * 512:(n + 1) * 512],
                                                                         start=(kc == 0), stop=(kc == 7)),
                     reads=[ot, wout], writes=[py])
            b.op("dve", lambda e, n=n, py=py, xr=xr: e.tensor_tensor(out=xr[:, n * 512:(n + 1) * 512],
                                                                     in0=xr[:, n * 512:(n + 1) * 512], in1=py[:], op=ALU.add),
                 reads=[xr, py], writes=[xr])
        rms_to_hT(b, xr, gm, hbf[a % 2], h2T, a, b.banks16[6 + a % 2], ss[a % 2], junk)
    return m


def rms_to_hT(b, xr, gm, hbf, h2T, a, pt, ss, junk):
    idt = b.ident()
    b.op("act", lambda e: e.activation(out=junk[:], in_=xr[:], func=AF.Square, accum_out=ss[:, 0:1]),
         reads=[xr], writes=[junk, ss])
    rstd_from_ss(b, ss, 1, 1.0 / D)
    b.op("dve", lambda e: e.scalar_tensor_tensor(out=hbf[:], in0=xr[:], scalar=ss[:, 0:1], in1=gm[:],
                                                 op0=ALU.mult, op1=ALU.mult), reads=[xr, ss, gm], writes=[hbf])
    for kc in range(8):
        b.op("pe", lambda e, kc=kc: e.transpose(out=pt[:, kc * 128:(kc + 1) * 128],
                                                in_=hbf[:, kc * 128:(kc + 1) * 128], identity=idt[:]),
             reads=[hbf, idt], writes=[pt])
    b.op("act", lambda e: e.activation(out=h2T[:, :, a * 128:(a + 1) * 128],
                                       in_=pt[:].rearrange("p (k t) -> p k t", k=8), func=AF.Copy),
         reads=[pt], writes=[h2T])


def phase_mlp(b, x_res, h2T, w1_ap, w2_ap):
    NFC = 8
    w1c = [b.sb([128, 8, 512], BF16, f"w1c{i}") for i in range(2)]
    w2c = [b.sb([128, 4, 1024], BF16, f"w2c{i}") for i in range(2)]
    rbuf = [b.sb([128, 512], F32, f"rbuf{i}") for i in range(2)]
    uT = [b.sb([128, 4, 512], BF16, f"uT{i}") for i in range(2)]
    pu = [b.banks[0], b.banks[1]]
    po = [b.banks[2 + i] for i in range(4)]

    def load_chunk(fc):
        w1, w2 = w1c[fc % 2], w2c[fc % 2]
        for kc in range(8):
            b.op("pool", lambda e, kc=kc, fc=fc, w1=w1: e.dma_start(
                out=w1[:, kc, :], in_=w1_ap[kc * 128:(kc + 1) * 128, fc * 512:(fc + 1) * 512]),
                writes=[w1], dma=f"ld_w1{fc % 2}")
        for ft in range(4):
            b.op("pool", lambda e, ft=ft, fc=fc, w2=w2: e.dma_start(
                out=w2[:, ft, :], in_=w2_ap[fc * 512 + ft * 128:fc * 512 + (ft + 1) * 128, :]),
                writes=[w2], dma=f"ld_w2{fc % 2}")

    steps = [(fc, tg) for fc in range(NFC) for tg in range(4)]
    cnt = {"u": 0, "o": 0}

    def stage_u(fc, tg):
        w1 = w1c[fc % 2]
        ut = uT[(fc * 4 + tg) % 2]
        for ft in range(4):
            p = pu[cnt["u"] % 2]
            r = rbuf[cnt["u"] % 2]
            cnt["u"] += 1
            for kc in range(8):
                b.op("pe", lambda e, kc=kc, ft=ft, p=p, w1=w1, tg=tg: e.matmul(
                    p[:], lhsT=w1[:, kc, ft * 128:(ft + 1) * 128], rhs=h2T[:, kc, tg * 512:(tg + 1) * 512],
                    start=(kc == 0), stop=(kc == 7)), reads=[w1, h2T], writes=[p])
            b.op("act", lambda e, p=p, r=r: e.activation(out=r[:], in_=p[:], func=AF.Relu), reads=[p], writes=[r])
            b.op("pool", lambda e, r=r, ut=ut, ft=ft: e.tensor_tensor(out=ut[:, ft, :], in0=r[:], in1=r[:], op=ALU.mult),
                 reads=[r], writes=[ut])

    def stage_o(fc, tg):
        w2 = w2c[fc % 2]
        ut = uT[(fc * 4 + tg) % 2]
        for tt in range(4):
            xr = x_res[tg * 4 + tt]
            for ch in range(2):
                p = po[cnt["o"] % 4]
                cnt["o"] += 1
                for ft in range(4):
                    b.op("pe", lambda e, ft=ft, tt=tt, ch=ch, p=p, ut=ut, w2=w2: e.matmul(
                        p[:], lhsT=ut[:, ft, tt * 128:(tt + 1) * 128], rhs=w2[:, ft, ch * 512:(ch + 1) * 512],
                        start=(ft == 0), stop=(ft == 3)), reads=[ut, w2], writes=[p])
                b.op("dve", lambda e, ch=ch, p=p, xr=xr: e.tensor_tensor(
                    out=xr[:, ch * 512:(ch + 1) * 512], in0=xr[:, ch * 512:(ch + 1) * 512], in1=p[:], op=ALU.add),
                    reads=[xr, p], writes=[xr])

    load_chunk(0)
    stage_u(*steps[0])
    for i, (fc, tg) in enumerate(steps):
        if tg == 0 and fc + 1 < NFC:
            load_chunk(fc + 1)
        if i + 1 < len(steps):
            stage_u(*steps[i + 1])
        stage_o(fc, tg)


IDX_SCALE = (8 ** -0.5) * (64 ** -0.5)


def phase_dsa_proj(b, io, x_res, cos, sin):
    idt = b.ident()
    gmix = b.sb([128, D], F32, "gmix1")
    b.op("sp", lambda e: e.dma_start(out=gmix[:], in_=io["gmix1"][:]), writes=[gmix], dma="ld_gmix1")
    gqk = b.sb([128, 2048], F32, "gqk2")
    b.op("sp", lambda e: e.dma_start(out=gqk[:], in_=io["gqk2"][:]), writes=[gqk], dma="ld_gqk2")
    b.op("pool", lambda e: e.tensor_scalar(out=gqk[:, 0:1024], in0=gqk[:, 0:1024], scalar1=0.125, scalar2=None,
                                           op0=ALU.mult), reads=[gqk], writes=[gqk])
    gcq = b.sb([128, 256], F32, "gcq")
    b.op("sp", lambda e: e.dma_start(out=gcq[:], in_=io["gcq"][:]), writes=[gcq], dma="ld_gcq")
    w = b.sb([128, 8, 2376], BF16, "w_in2")
    load_w_bf16(b, w, io["w_in2"], 8, 2376, "ld_w2in", colchunk=792)
    wuq = b.sb([128, 2, 1024], BF16, "wuq")
    load_w_bf16(b, wuq, io["w_uq"], 2, 1024, "ld_wuq")
    wuqi = b.sb([128, 2, 512], BF16, "wuqi")
    load_w_bf16(b, wuqi, io["w_uqi"], 2, 512, "ld_wuqi")

    junk = b.sb([128, D], BF16, "junk3")
    ss = b.sb([128, 1], F32, "ss3")
    hbf = b.sb([128, D], BF16, "hbf3")
    hT = b.sb([128, 8, 128], BF16, "hT3")
    stage = b.sb([128, 2048], F32, "stage3")
    sq = b.sb([128, 2048], F32, "sq3")
    ssq = b.sb([128, 32], F32, "ssq3")
    tmp = b.sb([128, 4 * 32 * 8], F32, "ropetmp3")
    qkbf = b.sb([128, 2048], BF16, "qkbf3")
    qkT = [b.sb([128, 16, 128], BF16, f"qkT3{i}") for i in range(2)]
    vaug = [b.sb([128, 16, 65], BF16, f"vaug3{i}") for i in range(2)]
    for i in range(2):
        b.op("pool", lambda e, i=i: e.memset(vaug[i][:], 1.0), writes=[vaug[i]])
    cqs = b.sb([128, 256], F32, "cqs")
    ssc = b.sb([128, 1], F32, "ssc")
    cqbf = b.sb([128, 256], BF16, "cqbf")
    cqT = b.sb([128, 2, 128], BF16, "cqT")
    kis = b.sb([128, 64], F32, "kis")
    ksq = b.sb([128, 64], F32, "ksq")
    kss = b.sb([128, 1], F32, "kss")
    kibf = b.sb([128, 128], BF16, "kibf")
    kiT = [b.sb([128, 128], BF16, f"kiT{i}") for i in range(2)]
    wi = b.sb([128, 8], F32, "wi")
    sgn = [b.sb([128, 8], F32, f"sgn{i}") for i in range(2)]
    aw = b.sb([128, 8], F32, "aw")
    qis = b.sb([128, 512], F32, "qis")
    qibf = b.sb([128, 512], BF16, "qibf")
    qiT = [b.sb([128, 4, 128], BF16, f"qiT{i}") for i in range(2)]
    nbank = [0]

    def bank():
        nbank[0] += 1
        return b.banks[2 + nbank[0] % 4]

    def proj(py, lhs, nk, rhs_fn, ncol):
        for kc in range(nk):
            b.op("pe", lambda e, kc=kc: e.matmul(py[:, 0:ncol], lhsT=lhs[:, kc, :], rhs=rhs_fn(kc),
                                                 start=(kc == 0), stop=(kc == nk - 1)), reads=[lhs, w, wuq, wuqi], writes=[py])

    for a in range(NS):
        xr = x_res[a]
        rmsnorm_transpose(b, xr, gmix, hbf, hT, b.banks16[0], ss, junk)
        py = bank()
        proj(py, hT, 8, lambda kc: w[:, kc, 0:256], 256)
        b.op("act", lambda e, py=py: e.activation(out=cqs[:], in_=py[:, 0:256], func=AF.Copy), reads=[py], writes=[cqs])
        b.op("act", lambda e, py=py: e.activation(out=junk[:, 0:256], in_=py[:, 0:256], func=AF.Square, accum_out=ssc[:, 0:1]),
             reads=[py], writes=[junk, ssc])
        rstd_from_ss(b, ssc, 1, 1.0 / 256)
        b.op("dve", lambda e: e.scalar_tensor_tensor(out=cqbf[:], in0=cqs[:], scalar=ssc[:, 0:1], in1=gcq[:],
                                                     op0=ALU.mult, op1=ALU.mult), reads=[cqs, ssc, gcq], writes=[cqbf])
        p6 = b.banks16[6]
        for kc in range(2):
            b.op("pe", lambda e, kc=kc: e.transpose(out=p6[:, kc * 128:(kc + 1) * 128], in_=cqbf[:, kc * 128:(kc + 1) * 128],
                                                    identity=idt[:]), reads=[cqbf, idt], writes=[p6])
        b.op("dve", lambda e: e.tensor_copy(out=cqT[:].rearrange("p a b -> p (a b)"), in_=p6[:, 0:256]),
             reads=[p6], writes=[cqT])
        for n in range(2):
            py = bank()
            proj(py, hT, 8, lambda kc, n=n: w[:, kc, 256 + n * 512:256 + (n + 1) * 512], 512)
            b.op("act", lambda e, py=py, n=n: e.activation(out=stage[:, 1024 + n * 512:1024 + (n + 1) * 512], in_=py[:], func=AF.Copy),
                 reads=[py], writes=[stage])
            b.op("act", lambda e, py=py, n=n: e.activation(out=sq[:, 1024 + n * 512:1024 + (n + 1) * 512], in_=py[:], func=AF.Square),
                 reads=[py], writes=[sq])
        va = vaug[a % 2]
        for n in range(2):
            py = bank()
            proj(py, hT, 8, lambda kc, n=n: w[:, kc, 1280 + n * 512:1280 + (n + 1) * 512], 512)
            b.op("dve", lambda e, py=py, n=n, va=va: e.tensor_copy(out=va[:, n * 8:(n + 1) * 8, 0:64],
                                                                   in_=py[:].rearrange("p (h d) -> p h d", d=64)),
                 reads=[py], writes=[va])
        b.store("sp", lambda e, a=a, va=va: e.dma_start(out=io["V2"][a], in_=va[:].rearrange("p h d -> p (h d)")),
                reads=[va], dma=f"st_v2{a % 2}")
        py = bank()
        proj(py, hT, 8, lambda kc: w[:, kc, 2304:2376], 72)
        b.op("act", lambda e, py=py: e.activation(out=kis[:], in_=py[:, 0:64], func=AF.Copy), reads=[py], writes=[kis])
        b.op("act", lambda e, py=py: e.activation(out=ksq[:], in_=py[:, 0:64], func=AF.Square), reads=[py], writes=[ksq])
        b.op("dve", lambda e, py=py: e.tensor_copy(out=wi[:], in_=py[:, 64:72]), reads=[py], writes=[wi])
        for n in range(2):
            py = bank()
            proj(py, cqT, 2, lambda kc, n=n: wuq[:, kc, n * 512:(n + 1) * 512], 512)
            b.op("act", lambda e, py=py, n=n: e.activation(out=stage[:, n * 512:(n + 1) * 512], in_=py[:], func=AF.Copy),
                 reads=[py], writes=[stage])
            b.op("act", lambda e, py=py, n=n: e.activation(out=sq[:, n * 512:(n + 1) * 512], in_=py[:], func=AF.Square),
                 reads=[py], writes=[sq])
        py = bank()
        proj(py, cqT, 2, lambda kc: wuqi[:, kc, :], 512)
        b.op("act", lambda e, py=py: e.activation(out=qis[:], in_=py[:], func=AF.Copy), reads=[py], writes=[qis])
        headnorm_rope(b, stage, sq, ssq, 32, gqk, cos[:, a, :], sin[:, a, :], qkbf, tmp)
        qt = qkT[a % 2]
        for i in range(16):
            pt = b.banks16[1] if i < 8 else b.banks16[0]
            b.op("pe", lambda e, i=i, pt=pt: e.transpose(out=pt[:, (i % 8) * 128:(i % 8 + 1) * 128],
                                                         in_=qkbf[:, i * 128:(i + 1) * 128], identity=idt[:]),
                 reads=[qkbf, idt], writes=[pt])
            if i % 8 == 7:
                b.op("dve", lambda e, i=i, pt=pt, qt=qt: e.tensor_copy(
                    out=qt[:, (i // 8) * 8:(i // 8) * 8 + 8, :].rearrange("p a b -> p (a b)"), in_=pt[:]),
                    reads=[pt], writes=[qt])
        b.store("sp", lambda e, a=a, qt=qt: e.dma_start(out=io["QT2"][a], in_=qt[:, 0:8, :].rearrange("p a b -> p (a b)")),
                reads=[qt], dma=f"st_q2{a % 2}")
        b.store("sp", lambda e, a=a, qt=qt: e.dma_start(out=io["KT2"][a], in_=qt[:, 8:16, :].rearrange("p a b -> p (a b)")),
                reads=[qt], dma=f"st_k2{a % 2}")
        ki_half = T(kibf.t[:, 0:64], kibf.d)
        headnorm_rope(b, kis, ksq, kss, 1, None, cos[:, a, :], sin[:, a, :], ki_half, tmp)
        b.op("pool", lambda e: e.tensor_copy(out=kibf[:, 64:128], in_=kibf[:, 0:64]), reads=[kibf], writes=[kibf])
        p7 = b.banks16[7]
        b.op("pe", lambda e: e.transpose(out=p7[:, 0:128], in_=kibf[:], identity=idt[:]), reads=[kibf, idt], writes=[p7])
        kt_ = kiT[a % 2]
        b.op("dve", lambda e, kt_=kt_: e.tensor_copy(out=kt_[:], in_=p7[:, 0:128]), reads=[p7], writes=[kt_])
        b.store("sp", lambda e, a=a, kt_=kt_: e.dma_start(out=io["KI"][a], in_=kt_[:]), reads=[kt_], dma=f"st_ki{a % 2}")
        sg = sgn[a % 2]
        b.op("act", lambda e, sg=sg: e.activation(out=sg[:], in_=wi[:], func=AF.Sign), reads=[wi], writes=[sg])
        b.op("dve", lambda e, sg=sg: e.scalar_tensor_tensor(out=aw[:], in0=wi[:], scalar=IDX_SCALE, in1=sg[:],
                                                           op0=ALU.mult, op1=ALU.mult), reads=[wi, sg], writes=[aw])
        b.store("sp", lambda e, a=a, sg=sg: e.dma_start(out=io["SG"][a], in_=sg[:]), reads=[sg], dma=f"st_sg{a % 2}")
        headnorm_rope(b, qis, None, aw, 8, None, cos[:, a, :], sin[:, a, :], qibf, tmp, norm=False)
        qi_ = qiT[a % 2]
        for i in range(4):
            b.op("pe", lambda e, i=i: e.transpose(out=p7[:, 256 + i * 128:256 + (i + 1) * 128],
                                                  in_=qibf[:, i * 128:(i + 1) * 128], identity=idt[:]),
                 reads=[qibf, idt], writes=[p7])
        b.op("dve", lambda e, qi_=qi_: e.tensor_copy(out=qi_[:].rearrange("p a b -> p (a b)"), in_=p7[:, 256:768]),
             reads=[p7], writes=[qi_])
        b.store("sp", lambda e, a=a, qi_=qi_: e.dma_start(out=io["QI"][a], in_=qi_[:].rearrange("p a b -> p (a b)")),
                reads=[qi_], dma=f"st_qi{a % 2}")


NIT = 22
TOPK = 256


def phase_dsa_attn(b, io, o_tok):
    idt = b.ident()
    kia = b.sb([128, 8192], BF16, "kiall")
    for part in range(4):
        b.op("sp", lambda e, part=part: e.dma_start(out=kia[:, part * 2048:(part + 1) * 2048],
                                                    in_=io["KIall"][:, part * 2048:(part + 1) * 2048]),
             writes=[kia], dma="ld_kia")
    negm = b.sb([128, 512], F32, "negm")
    b.op("sp", lambda e: e.dma_start(out=negm[:], in_=io["negmask"][:]), writes=[negm], dma="ld_negm")
    cW = b.sb([128, NIT], F32, "cW")
    for i in range(NIT):
        b.op("pool", lambda e, i=i: e.memset(cW[:, i:i + 1], 2.0 ** (-i)), writes=[cW])
    Ib = b.sb([128, 8192], F32, "Ibuf")
    Mq = b.sb([128, 8192], BF16, "Mq")
    MT = b.sb([128, 64, 128], BF16, "MT")
    ktp = [b.sb([128, 8192], BF16, f"ktp{i}") for i in range(2)]
    vp = [b.sb([128, 64, 130], BF16, f"vp{i}") for i in range(2)]
    qTa = [b.sb([128, 8, 128], BF16, f"qTa{i}") for i in range(2)]
    qiTa = [b.sb([128, 4, 128], BF16, f"qiTa{i}") for i in range(2)]
    sgn = [b.sb([128, 8], F32, f"sgna{i}") for i in range(2)]
    tb = [b.sb([128, 512], F32, f"tb{i}") for i in range(2)]
    pT = [b.sb([128, 512], BF16, f"pTd{i}") for i in range(2)]
    m1 = b.sb([128, 1], F32, "bm1")
    lo = b.sb([128, 1], F32, "blo")
    mid = b.sb([128, 1], F32, "bmid")
    cnt = b.sb([128, 1], F32, "bcnt")
    g = b.sb([128, 1], F32, "bg")
    W = b.sb([128, NIT], F32, "bW")
    rec = [b.sb([128, 1], F32, f"recd{i}") for i in range(2)]
    pS = [b.banks[0], b.banks[1]]
    pA = [b.banks[2], b.banks[3]]
    pI = [b.banks[4], b.banks[5]]
    pM = [b.banks16[6], b.banks16[7]]
    ctr = {"i": 0, "s": 0, "acc": 0, "kv": 0}

    def load_slot_small(a):
        b.op("sp", lambda e, a=a: e.dma_start(out=qTa[a % 2][:].rearrange("p a b -> p (a b)"), in_=io["QT2"][a]),
             writes=[qTa[a % 2]], dma=f"ld_qTa{a % 2}")
        b.op("sp", lambda e, a=a: e.dma_start(out=qiTa[a % 2][:].rearrange("p a b -> p (a b)"), in_=io["QI"][a]),
             writes=[qiTa[a % 2]], dma=f"ld_qiTa{a % 2}")
        b.op("sp", lambda e, a=a: e.dma_start(out=sgn[a % 2][:], in_=io["SG"][a]), writes=[sgn[a % 2]], dma=f"ld_sgn{a % 2}")

    def load_kv(a, hp):
        k = ctr["kv"] % 2
        ctr["kv"] += 1
        nv = 512 * (a + 1)
        nt = 4 * (a + 1)
        b.op("sp", lambda e, k=k, hp=hp, nv=nv: e.dma_start(out=ktp[k][:, 0:nv], in_=io["KT2all"][hp][:, 0:nv]),
             writes=[ktp[k]], dma=f"ld_ktp{k}")
        b.op("sp", lambda e, k=k, hp=hp, nt=nt: e.dma_start(out=vp[k][:, 0:nt, :].rearrange("p a b -> p (a b)"),
                                                            in_=io["V2all"][hp][:, 0:nt * 130]),
             writes=[vp[k]], dma=f"ld_vp{k}")
        return k

    def indexer(a):
        nb = a + 1
        nv = 512 * nb
        qi, sg = qiTa[a % 2], sgn[a % 2]
        for blk in range(nb):
            for head in range(8):
                hp, hh = head // 2, head % 2
                py = pI[ctr["i"] % 2]
                t = tb[ctr["i"] % 2]
                ctr["i"] += 1
                b.op("pe", lambda e, py=py, hp=hp, hh=hh, blk=blk, qi=qi: e.matmul(
                    py[:], lhsT=qi[64 * hh:64 * hh + 64, hp, :], rhs=kia[64 * hh:64 * hh + 64, blk * 512:(blk + 1) * 512],
                    start=True, stop=True), reads=[qi, kia], writes=[py])
                b.op("act", lambda e, py=py, t=t: e.activation(out=t[:], in_=py[:], func=AF.Relu), reads=[py], writes=[t])
                if head == 0:
                    b.op("dve", lambda e, t=t, blk=blk, sg=sg: e.tensor_scalar(
                        out=Ib[:, blk * 512:(blk + 1) * 512], in0=t[:], scalar1=sg[:, 0:1], scalar2=None, op0=ALU.mult),
                        reads=[t, sg], writes=[Ib])
                else:
                    b.op("dve", lambda e, t=t, blk=blk, sg=sg, head=head: e.scalar_tensor_tensor(
                        out=Ib[:, blk * 512:(blk + 1) * 512], in0=t[:], scalar=sg[:, head:head + 1],
                        in1=Ib[:, blk * 512:(blk + 1) * 512], op0=ALU.mult, op1=ALU.add),
                        reads=[t, sg, Ib], writes=[Ib])
        b.op("dve", lambda e: e.tensor_reduce(out=m1[:], in_=Ib[:, 0:nv], axis=AX.X, op=ALU.max, apply_absolute_value=True),
             reads=[Ib], writes=[m1])
        b.op("dve", lambda e: e.tensor_tensor(out=Ib[:, nv - 512:nv], in0=Ib[:, nv - 512:nv], in1=negm[:], op=ALU.add),
             reads=[Ib, negm], writes=[Ib])
        b.op("dve", lambda e: e.tensor_scalar(out=m1[:], in0=m1[:], scalar1=1.0, scalar2=None, op0=ALU.add),
             reads=[m1], writes=[m1])
        b.op("dve", lambda e: e.tensor_scalar(out=lo[:], in0=m1[:], scalar1=-1.0, scalar2=None, op0=ALU.mult),
             reads=[m1], writes=[lo])
        b.op("dve", lambda e: e.tensor_scalar(out=W[:], in0=cW[:], scalar1=m1[:, 0:1], scalar2=None, op0=ALU.mult),
             reads=[cW, m1], writes=[W])
        b.op("dve", lambda e: e.tensor_tensor(out=mid[:], in0=lo[:], in1=W[:, 0:1], op=ALU.add), reads=[lo, W], writes=[mid])
        for i in range(NIT):
            b.op("dve", lambda e: e.tensor_scalar(out=Mq[:, 0:nv], in0=Ib[:, 0:nv], scalar1=mid[:, 0:1], scalar2=0.0,
                                                  op0=ALU.is_ge, op1=ALU.add, accum_out=cnt[:, 0:1]),
                 reads=[Ib, mid], writes=[Mq, cnt])
            b.op("dve", lambda e, i=i: e.tensor_scalar(out=g[:], in0=cnt[:], scalar1=TOPK - 0.5, scalar2=W[:, i:i + 1],
                                                       op0=ALU.is_ge, op1=ALU.mult), reads=[cnt, W], writes=[g])
            b.op("dve", lambda e: e.tensor_tensor(out=lo[:], in0=lo[:], in1=g[:], op=ALU.add), reads=[lo, g], writes=[lo])
            if i + 1 < NIT:
                b.op("dve", lambda e, i=i: e.tensor_tensor(out=mid[:], in0=lo[:], in1=W[:, i + 1:i + 2], op=ALU.add),
                     reads=[lo, W], writes=[mid])
        b.op("dve", lambda e: e.tensor_scalar(out=Mq[:, 0:nv], in0=Ib[:, 0:nv], scalar1=lo[:, 0:1], scalar2=None,
                                              op0=ALU.is_ge), reads=[Ib, lo], writes=[Mq])
        nt = 4 * nb
        for kt in range(nt):
            pm = pM[(kt // 8) % 2]
            b.op("pe", lambda e, kt=kt, pm=pm: e.transpose(out=pm[:, (kt % 8) * 128:(kt % 8 + 1) * 128],
                                                           in_=Mq[:, kt * 128:(kt + 1) * 128], identity=idt[:]),
                 reads=[Mq, idt], writes=[pm])
            if kt % 8 == 7 or kt == nt - 1:
                k0 = (kt // 8) * 8
                n = kt - k0 + 1
                b.op("act", lambda e, pm=pm, k0=k0, n=n: e.activation(
                    out=MT[:, k0:k0 + n, :].rearrange("p a b -> p (a b)"), in_=pm[:, 0:n * 128], func=AF.Copy),
                    reads=[pm], writes=[MT])

    def qk(u, n, kbuf):
        a, hp, hh, blk = u
        ps = pS[n % 2]
        qa = qTa[a % 2]
        for i in range(4):
            kt = 4 * blk + i
            b.op("pe", lambda e, i=i, kt=kt, ps=ps, qa=qa, hh=hh, hp=hp, kbuf=kbuf: e.matmul(
                ps[:, i * 128:(i + 1) * 128], lhsT=ktp[kbuf][64 * hh:64 * hh + 64, kt * 128:(kt + 1) * 128],
                rhs=qa[64 * hh:64 * hh + 64, hp, :], start=True, stop=True), reads=[ktp[kbuf], qa], writes=[ps])

    def softmax_pv(u, n, kbuf):
        a, hp, hh, blk = u
        ps, pt = pS[n % 2], pT[n % 2]
        if blk == 0:
            ctr["acc"] += 1
        acc = pA[ctr["acc"] % 2]
        b.op("act", lambda e: e.activation(out=pt[:], in_=ps[:], func=AF.Exp), reads=[ps], writes=[pt])
        b.op("dve", lambda e: e.tensor_tensor(out=pt[:], in0=pt[:], in1=MT[:, 4 * blk:4 * blk + 4, :].rearrange("p a b -> p (a b)"),
                                              op=ALU.mult), reads=[pt, MT], writes=[pt])
        for i in range(4):
            kt = 4 * blk + i
            b.op("pe", lambda e, i=i, kt=kt: e.matmul(acc[:, 0:65], lhsT=pt[:, i * 128:(i + 1) * 128],
                                                      rhs=vp[kbuf][:, kt, hh * 65:(hh + 1) * 65],
                                                      start=(blk == 0 and i == 0), stop=(blk == a and i == 3)),
                 reads=[pt, vp[kbuf]], writes=[acc])
        if blk == a:
            head = 2 * hp + hh
            r = rec[ctr["acc"] % 2]
            b.op("dve", lambda e: e.reciprocal(out=r[:], in_=acc[:, 64:65]), reads=[acc], writes=[r])
            b.op("dve", lambda e: e.tensor_scalar(out=o_tok[a][:, head * 64:(head + 1) * 64], in0=acc[:, 0:64],
                                                  scalar1=r[:, 0:1], scalar2=None, op0=ALU.mult),
                 reads=[acc, r], writes=[o_tok[a]])

    load_slot_small(0)
    for a in range(NS):
        if a + 1 < NS:
            load_slot_small(a + 1)
        kb_next = load_kv(a, 0)
        indexer(a)
        units = [(a, hp, hh, blk) for hp in range(8) for hh in range(2) for blk in range(a + 1)]
        kbufs = {}
        kbufs[0] = kb_next
        qk(units[0], 0, kbufs[0])
        for n, u in enumerate(units):
            _, hp, hh, blk = u
            if hh == 0 and blk == 0 and hp + 1 < 8:
                kbufs[hp + 1] = load_kv(a, hp + 1)
            if n + 1 < len(units):
                qk(units[n + 1], n + 1, kbufs[units[n + 1][1]])
            softmax_pv(u, n, kbufs[hp])


GROUPS = [[0, 1, 2, 3], [4, 5, 6, 7]]
_RANK = {}


class Gather:
    def __init__(self, b, name, nblk, cols, zt):
        self.b, self.name, self.nblk, self.cols = b, name, nblk, cols
        nc = b.nc
        self.xb = nc.dram_tensor(name + "_xb", [nblk, 512, cols], BF16).ap()
        self.yb = nc.dram_tensor(name + "_yb", [nblk, 512, cols], BF16).ap()
        self.xo = nc.dram_tensor(name + "_xo", [nblk, 128, cols], BF16).ap()
        self.od = [T(self.xo[h]) for h in range(nblk)]
        self.xd = [T(self.xb[h]) for h in range(nblk)]
        self.yd = [T(self.yb[h]) for h in range(nblk)]
        for h in range(nblk):
            for r in range(4):
                b.op("act", lambda e, h=h, r=r: e.dma_start(out=self.xb[h, r * 128:(r + 1) * 128, :], in_=zt[:, 0:cols]),
                     reads=[zt], writes=[self.xd[h]], dma="zero_" + name)

    def put(self, h, c0, c1, src_ap, reads, key):
        self.b.op("sp", lambda e: e.dma_start(out=self.xo[h, :, c0:c1], in_=src_ap), reads=reads,
                  writes=[self.od[h]], dma=key)

    def place(self, h, eng):
        def fn(e):
            if eng not in _RANK:
                _RANK[eng] = e.partition_id() % 4
            r = _RANK[eng]
            return e.dma_start(out=self.xb[h, bass.ds(r * 128, 128), :], in_=self.xo[h])
        self.b.op(eng, fn, reads=[self.od[h]], writes=[self.xd[h]], dma="place_" + self.name)

    def reduce(self, h):
        b = self.b
        b.op("pool", lambda e: e.collective_compute("AllReduce", ALU.add, replica_groups=GROUPS,
                                                    ins=[self.xb[h]], outs=[self.yb[h]]),
             reads=[self.xd[h]], writes=[self.yd[h]], dma=f"cc_{self.name}{h}", sem_inc=1)


def f_diff_proj(b, io, qT_res, G0, cos, sin):
    idt = b.ident()
    gmix = b.sb([128, D], F32, "gmix")
    b.op("sp", lambda e: e.dma_start(out=gmix[:], in_=io["gmix"][:]), writes=[gmix], dma="ld_gmix")
    gqk = b.sb([128, 2048], F32, "gqk")
    b.op("sp", lambda e: e.dma_start(out=gqk[:], in_=io["gqk"][:]), writes=[gqk], dma="ld_gqk")
    b.op("pool", lambda e: e.tensor_scalar(out=gqk[:, 0:1024], in0=gqk[:, 0:1024], scalar1=0.125, scalar2=None,
                                           op0=ALU.mult), reads=[gqk], writes=[gqk])
    w = b.sb([128, 8, 3072], BF16, "w_in")
    load_w_bf16(b, w, io["w_in"], 8, 3072, "ld_w")
    xts = [b.sb([128, D], F32, f"xt{i}") for i in range(2)]
    junk = b.sb([128, D], BF16, "junk")
    ss = b.sb([128, 1], F32, "ss")
    hbf = b.sb([128, D], BF16, "hbf")
    hT = b.sb([128, 8, 128], BF16, "hT")
    pT = [b.banks16[0], b.banks16[1]]
    pY = [b.banks[2 + i] for i in range(4)]
    stage = b.sb([128, 2048], F32, "stage")
    sq = b.sb([128, 2048], F32, "sq")
    ssq = b.sb([128, 32], F32, "ssq")
    tmp = b.sb([128, 4 * 32 * 8], F32, "ropetmp")
    qkbf = b.sb([128, 2048], BF16, "qkbf")
    kT = [b.sb([128, 8, 128], BF16, f"kTst{i}") for i in range(2)]
    vst = [b.sb([128, 8, 128], BF16, f"vst{i}") for i in range(2)]
    for a in range(NS):
        xt = xts[a % 2]
        b.op("sp", lambda e, a=a, xt=xt: e.dma_start(out=xt[:], in_=io["x"][a * 128:(a + 1) * 128, :]),
             writes=[xt], dma=f"ld_x{a % 2}")
        rmsnorm_transpose(b, xt, gmix, hbf, hT, pT[0], ss, junk)
        va = vst[a % 2]
        for n in range(6):
            py = pY[n % 4]
            for kc in range(8):
                b.op("pe", lambda e, n=n, kc=kc, py=py: e.matmul(py[:], lhsT=hT[:, kc, :],
                                                                   rhs=w[:, kc, n * 512:(n + 1) * 512],
                                                                   start=(kc == 0), stop=(kc == 7)),
                     reads=[hT, w], writes=[py])
            if n < 4:
                b.op("act", lambda e, n=n, py=py: e.activation(out=stage[:, n * 512:(n + 1) * 512], in_=py[:], func=AF.Copy),
                     reads=[py], writes=[stage])
                b.op("act", lambda e, n=n, py=py: e.activation(out=sq[:, n * 512:(n + 1) * 512], in_=py[:], func=AF.Square),
                     reads=[py], writes=[sq])
            else:
                b.op("dve", lambda e, n=n, py=py, va=va: e.tensor_copy(
                    out=va[:, (n - 4) * 4:(n - 4) * 4 + 4, :], in_=py[:].rearrange("p (h d) -> p h d", d=128)),
                    reads=[py], writes=[va])
        for h in range(8):
            G0.put(h, 2048 + a * 128, 2048 + (a + 1) * 128, va[:, h, :], [va], f"st_v{a % 2}")
        headnorm_rope(b, stage, sq, ssq, 32, gqk, cos[:, a, :], sin[:, a, :], qkbf, tmp)
        kt = kT[a % 2]
        for i in range(16):
            pt = pT[1] if i < 8 else pT[0]
            b.op("pe", lambda e, i=i, pt=pt: e.transpose(out=pt[:, (i % 8) * 128:(i % 8 + 1) * 128],
                                                         in_=qkbf[:, i * 128:(i + 1) * 128], identity=idt[:]),
                 reads=[qkbf, idt], writes=[pt])
            if i == 7:
                b.op("dve", lambda e, pt=pt, a=a: e.tensor_copy(out=qT_res[:, a, :], in_=pt[:]), reads=[pt], writes=[qT_res])
            if i == 15:
                b.op("dve", lambda e, pt=pt, kt=kt: e.tensor_copy(out=kt[:].rearrange("p a b -> p (a b)"), in_=pt[:]),
                     reads=[pt], writes=[kt])
        for h in range(8):
            G0.put(h, a * 128, (a + 1) * 128, kt[:, h, :], [kt], f"st_k{a % 2}")
    for h in range(8):
        G0.place(h, "sp")
        G0.reduce(h)


def f_diff_attn(b, io, o_tok, qT, G0):
    LAM_INIT = 0.2
    maskT = b.sb([128, 512], BF16, "maskT")
    b.op("sp", lambda e: e.dma_start(out=maskT[:], in_=io["maskT"][:]), writes=[maskT], dma="ld_mask")
    lam = b.sb([128, 256], F32, "lam")
    b.op("sp", lambda e: e.dma_start(out=lam[:], in_=io["lam"][:]), writes=[lam], dma="ld_lam")
    gsub = b.sb([128, 128], F32, "gsub")
    b.op("sp", lambda e: e.dma_start(out=gsub[:], in_=io["gsub"][:]), writes=[gsub], dma="ld_gsub")
    b.op("dve", lambda e: e.tensor_scalar(out=gsub[:], in0=gsub[:], scalar1=1.0 - LAM_INIT, scalar2=None,
                                          op0=ALU.mult), reads=[gsub], writes=[gsub])
    lprod = b.sb([128, 128], F32, "lprod")
    l2 = b.sb([128, 2], F32, "l2")
    neglam = b.sb([128, 1], F32, "neglam")
    l4 = lam[:].rearrange("p (a b d) -> p a b d", a=2, b=2)
    b.op("dve", lambda e: e.tensor_tensor(out=lprod[:].rearrange("p (a d) -> p a d", a=2), in0=l4[:, :, 0, :],
                                          in1=l4[:, :, 1, :], op=ALU.mult), reads=[lam], writes=[lprod])
    b.op("dve", lambda e: e.tensor_reduce(out=l2[:], in_=lprod[:].rearrange("p (a d) -> p a d", a=2), axis=AX.X,
                                          op=ALU.add), reads=[lprod], writes=[l2])
    b.op("act", lambda e: e.activation(out=l2[:], in_=l2[:], func=AF.Exp), reads=[l2], writes=[l2])
    b.op("dve", lambda e: e.tensor_tensor(out=neglam[:], in0=l2[:, 1:2], in1=l2[:, 0:1], op=ALU.subtract),
         reads=[l2], writes=[neglam])
    b.op("dve", lambda e: e.tensor_scalar(out=neglam[:], in0=neglam[:], scalar1=-LAM_INIT, scalar2=None, op0=ALU.add),
         reads=[neglam], writes=[neglam])

    ktb = [b.sb([128, 8192], BF16, f"ktb{i}") for i in range(2)]
    vb = [b.sb([128, 64, 129], BF16, f"vb{i}") for i in range(2)]
    for i in range(2):
        b.op("dve", lambda e, i=i: e.memset(vb[i][:, :, 128:129], 1.0), writes=[vb[i]])
    pS = [[b.banks[c * 2 + i] for i in range(2)] for c in range(2)]
    pA = [[b.banks[4 + c * 2 + i] for i in range(2)] for c in range(2)]
    pT = [[b.sb([128, 512], BF16, f"pTs{c}{i}") for i in range(2)] for c in range(2)]
    rec = [b.sb([128, 2], F32, f"rec{i}") for i in range(2)]
    o32 = [b.sb([128, 128], F32, f"o32{i}") for i in range(2)]
    oj = b.sb([128, 128], BF16, "ojunk")
    ss1 = [b.sb([128, 1], F32, f"ss1{i}") for i in range(2)]
    units = [(h, a, blk) for h in range(8) for a in range(NS) for blk in range(a + 1)]

    def load_head(h):
        kb, vv = ktb[h % 2], vb[h % 2]
        yb = G0.yb[h]
        for r in range(4):
            b.op("sp", lambda e, kb=kb, r=r, yb=yb: e.dma_start(out=kb[:, r * 2048:(r + 1) * 2048],
                                                               in_=yb[r * 128:(r + 1) * 128, 0:2048]),
                 reads=[G0.yd[h]], writes=[kb], dma=f"ld_kt{h % 2}")
            b.op("sp", lambda e, vv=vv, r=r, yb=yb: e.dma_start(
                out=vv[:, r * 16:(r + 1) * 16, 0:128],
                in_=yb[r * 128:(r + 1) * 128, 2048:4096].rearrange("p (a e) -> p a e", e=128)),
                reads=[G0.yd[h]], writes=[vv], dma=f"ld_v{h % 2}")

    def qk(u, n):
        h, a, blk = u
        kb = ktb[h % 2]
        for i in range(4):
            for c in range(2):
                ps = pS[c][n % 2]
                kt = 16 * i + blk
                b.op("pe", lambda e, c=c, i=i, kt=kt, ps=ps, kb=kb, a=a, h=h: e.matmul(
                    ps[:, i * 128:(i + 1) * 128], lhsT=kb[64 * c:64 * c + 64, kt * 128:(kt + 1) * 128],
                    rhs=qT[64 * c:64 * c + 64, a, h * 128:(h + 1) * 128], start=True, stop=True),
                    reads=[kb, qT], writes=[ps])

    def evac(h, a):
        k = a % 2
        a0, a1 = pA[0][k], pA[1][k]
        r, o, s = rec[k], o32[k], ss1[k]
        b.op("dve", lambda e: e.reciprocal(out=r[:, 0:1], in_=a0[:, 128:129]), reads=[a0], writes=[r])
        b.op("dve", lambda e: e.reciprocal(out=r[:, 1:2], in_=a1[:, 128:129]), reads=[a1], writes=[r])
        b.op("dve", lambda e: e.tensor_tensor(out=r[:, 1:2], in0=r[:, 1:2], in1=neglam[:], op=ALU.mult),
             reads=[r, neglam], writes=[r])
        b.op("dve", lambda e: e.tensor_scalar(out=o[:], in0=a0[:, 0:128], scalar1=r[:, 0:1], scalar2=None, op0=ALU.mult),
             reads=[a0, r], writes=[o])
        b.op("dve", lambda e: e.scalar_tensor_tensor(out=o[:], in0=a1[:, 0:128], scalar=r[:, 1:2], in1=o[:],
                                                     op0=ALU.mult, op1=ALU.add), reads=[a1, r, o], writes=[o])
        b.op("act", lambda e: e.activation(out=oj[:], in_=o[:], func=AF.Square, accum_out=s[:, 0:1]),
             reads=[o], writes=[oj, s])
        rstd_from_ss(b, s, 1, 1.0 / 128)
        ot = o_tok[a]
        b.op("dve", lambda e: e.scalar_tensor_tensor(out=ot[:, h * 128:(h + 1) * 128], in0=o[:], scalar=s[:, 0:1],
                                                     in1=gsub[:], op0=ALU.mult, op1=ALU.mult),
             reads=[o, s, gsub], writes=[ot])

    def softmax_pv(u, n):
        h, a, blk = u
        vv = vb[h % 2]
        for c in range(2):
            ps = pS[c][n % 2]
            pt = pT[c][n % 2]
            acc = pA[c][a % 2]
            b.op("act", lambda e, ps=ps, pt=pt: e.activation(out=pt[:], in_=ps[:], func=AF.Exp),
                 reads=[ps], writes=[pt])
            if blk == a:
                b.op("dve", lambda e, pt=pt: e.tensor_tensor(out=pt[:], in0=pt[:], in1=maskT[:], op=ALU.mult),
                     reads=[pt, maskT], writes=[pt])
            for i in range(4):
                kt = 16 * i + blk
                b.op("pe", lambda e, i=i, kt=kt, pt=pt, acc=acc, vv=vv, blk=blk, a=a: e.matmul(
                    acc[:, 0:129], lhsT=pt[:, i * 128:(i + 1) * 128], rhs=vv[:, kt, :],
                    start=(blk == 0 and i == 0), stop=(blk == a and i == 3)),
                    reads=[pt, vv], writes=[acc])
        if blk == a:
            evac(h, a)

    load_head(0)
    qk(units[0], 0)
    for n, u in enumerate(units):
        if u[1] == 0 and u[2] == 0 and u[0] + 1 < 8:
            load_head(u[0] + 1)
        if n + 1 < len(units):
            qk(units[n + 1], n + 1)
        softmax_pv(u, n)


def f_dsa_proj(b, io, x_res, cos, sin, G1, GK, scr):
    idt = b.ident()
    gmix = b.sb([128, D], F32, "gmix1")
    b.op("sp", lambda e: e.dma_start(out=gmix[:], in_=io["gmix1"][:]), writes=[gmix], dma="ld_gmix1")
    gqk = b.sb([128, 2048], F32, "gqk2")
    b.op("sp", lambda e: e.dma_start(out=gqk[:], in_=io["gqk2"][:]), writes=[gqk], dma="ld_gqk2")
    b.op("pool", lambda e: e.tensor_scalar(out=gqk[:, 0:1024], in0=gqk[:, 0:1024], scalar1=0.125, scalar2=None,
                                           op0=ALU.mult), reads=[gqk], writes=[gqk])
    gcq = b.sb([128, 256], F32, "gcq")
    b.op("sp", lambda e: e.dma_start(out=gcq[:], in_=io["gcq"][:]), writes=[gcq], dma="ld_gcq")
    w = b.sb([128, 8, 2376], BF16, "w_in2")
    load_w_bf16(b, w, io["w_in2"], 8, 2376, "ld_w2in", colchunk=792)
    wuq = b.sb([128, 2, 1024], BF16, "wuq")
    load_w_bf16(b, wuq, io["w_uq"], 2, 1024, "ld_wuq")
    wuqi = b.sb([128, 2, 512], BF16, "wuqi")
    load_w_bf16(b, wuqi, io["w_uqi"], 2, 512, "ld_wuqi")
    junk = b.sb([128, D], BF16, "junk3")
    ss = b.sb([128, 1], F32, "ss3")
    hbf = b.sb([128, D], BF16, "hbf3")
    hT = b.sb([128, 8, 128], BF16, "hT3")
    stage = b.sb([128, 2048], F32, "stage3")
    sq = b.sb([128, 2048], F32, "sq3")
    ssq = b.sb([128, 32], F32, "ssq3")
    tmp = b.sb([128, 4 * 32 * 8], F32, "ropetmp3")
    qkbf = b.sb([128, 2048], BF16, "qkbf3")
    qkT = [b.sb([128, 16, 128], BF16, f"qkT3{i}") for i in range(2)]
    vst = [b.sb([128, 16, 64], BF16, f"vst3{i}") for i in range(2)]
    cqs = b.sb([128, 256], F32, "cqs")
    ssc = b.sb([128, 1], F32, "ssc")
    cqbf = b.sb([128, 256], BF16, "cqbf")
    cqT = b.sb([128, 2, 128], BF16, "cqT")
    kis = b.sb([128, 64], F32, "kis")
    ksq = b.sb([128, 64], F32, "ksq")
    kss = b.sb([128, 1], F32, "kss")
    kibf = b.sb([128, 128], BF16, "kibf")
    kiT = [b.sb([128, 128], BF16, f"kiT{i}") for i in range(2)]
    wi = b.sb([128, 8], F32, "wi")
    sgn = [b.sb([128, 8], F32, f"sgn{i}") for i in range(2)]
    aw = b.sb([128, 8], F32, "aw")
    qis = b.sb([128, 512], F32, "qis")
    qibf = b.sb([128, 512], BF16, "qibf")
    qiT = [b.sb([128, 4, 128], BF16, f"qiT{i}") for i in range(2)]
    nbank = [0]

    def bank():
        nbank[0] += 1
        return b.banks[2 + nbank[0] % 4]

    def proj(py, lhs, nk, rhs_fn, ncol):
        for kc in range(nk):
            b.op("pe", lambda e, kc=kc: e.matmul(py[:, 0:ncol], lhsT=lhs[:, kc, :], rhs=rhs_fn(kc),
                                                 start=(kc == 0), stop=(kc == nk - 1)), reads=[lhs, w, wuq, wuqi], writes=[py])

    for a in range(NS):
        xr = x_res[a]
        rmsnorm_transpose(b, xr, gmix, hbf, hT, b.banks16[0], ss, junk)
        py = bank()
        proj(py, hT, 8, lambda kc: w[:, kc, 0:256], 256)
        b.op("act", lambda e, py=py: e.activation(out=cqs[:], in_=py[:, 0:256], func=AF.Copy), reads=[py], writes=[cqs])
        b.op("act", lambda e, py=py: e.activation(out=junk[:, 0:256], in_=py[:, 0:256], func=AF.Square, accum_out=ssc[:, 0:1]),
             reads=[py], writes=[junk, ssc])
        rstd_from_ss(b, ssc, 1, 1.0 / 256)
        b.op("dve", lambda e: e.scalar_tensor_tensor(out=cqbf[:], in0=cqs[:], scalar=ssc[:, 0:1], in1=gcq[:],
                                                     op0=ALU.mult, op1=ALU.mult), reads=[cqs, ssc, gcq], writes=[cqbf])
        p6 = b.banks16[6]
        for kc in range(2):
            b.op("pe", lambda e, kc=kc: e.transpose(out=p6[:, kc * 128:(kc + 1) * 128], in_=cqbf[:, kc * 128:(kc + 1) * 128],
                                                    identity=idt[:]), reads=[cqbf, idt], writes=[p6])
        b.op("dve", lambda e: e.tensor_copy(out=cqT[:].rearrange("p a b -> p (a b)"), in_=p6[:, 0:256]),
             reads=[p6], writes=[cqT])
        for n in range(2):
            py = bank()
            proj(py, hT, 8, lambda kc, n=n: w[:, kc, 256 + n * 512:256 + (n + 1) * 512], 512)
            b.op("act", lambda e, py=py, n=n: e.activation(out=stage[:, 1024 + n * 512:1024 + (n + 1) * 512], in_=py[:], func=AF.Copy),
                 reads=[py], writes=[stage])
            b.op("act", lambda e, py=py, n=n: e.activation(out=sq[:, 1024 + n * 512:1024 + (n + 1) * 512], in_=py[:], func=AF.Square),
                 reads=[py], writes=[sq])
        va = vst[a % 2]
        for n in range(2):
            py = bank()
            proj(py, hT, 8, lambda kc, n=n: w[:, kc, 1280 + n * 512:1280 + (n + 1) * 512], 512)
            b.op("dve", lambda e, py=py, n=n, va=va: e.tensor_copy(out=va[:, n * 8:(n + 1) * 8, :],
                                                                   in_=py[:].rearrange("p (h d) -> p h d", d=64)),
                 reads=[py], writes=[va])
        for hp in range(8):
            G1.put(hp, 2048 + a * 128, 2048 + (a + 1) * 128, va[:, 2 * hp:2 * hp + 2, :].rearrange("p h d -> p (h d)"),
                   [va], f"st_v2{a % 2}")
        py = bank()
        proj(py, hT, 8, lambda kc: w[:, kc, 2304:2376], 72)
        b.op("act", lambda e, py=py: e.activation(out=kis[:], in_=py[:, 0:64], func=AF.Copy), reads=[py], writes=[kis])
        b.op("act", lambda e, py=py: e.activation(out=ksq[:], in_=py[:, 0:64], func=AF.Square), reads=[py], writes=[ksq])
        b.op("dve", lambda e, py=py: e.tensor_copy(out=wi[:], in_=py[:, 64:72]), reads=[py], writes=[wi])
        for n in range(2):
            py = bank()
            proj(py, cqT, 2, lambda kc, n=n: wuq[:, kc, n * 512:(n + 1) * 512], 512)
            b.op("act", lambda e, py=py, n=n: e.activation(out=stage[:, n * 512:(n + 1) * 512], in_=py[:], func=AF.Copy),
                 reads=[py], writes=[stage])
            b.op("act", lambda e, py=py, n=n: e.activation(out=sq[:, n * 512:(n + 1) * 512], in_=py[:], func=AF.Square),
                 reads=[py], writes=[sq])
        py = bank()
        proj(py, cqT, 2, lambda kc: wuqi[:, kc, :], 512)
        b.op("act", lambda e, py=py: e.activation(out=qis[:], in_=py[:], func=AF.Copy), reads=[py], writes=[qis])
        headnorm_rope(b, stage, sq, ssq, 32, gqk, cos[:, a, :], sin[:, a, :], qkbf, tmp)
        qt = qkT[a % 2]
        for i in range(16):
            pt = b.banks16[1] if i < 8 else b.banks16[0]
            b.op("pe", lambda e, i=i, pt=pt: e.transpose(out=pt[:, (i % 8) * 128:(i % 8 + 1) * 128],
                                                         in_=qkbf[:, i * 128:(i + 1) * 128], identity=idt[:]),
                 reads=[qkbf, idt], writes=[pt])
            if i % 8 == 7:
                b.op("dve", lambda e, i=i, pt=pt, qt=qt: e.tensor_copy(
                    out=qt[:, (i // 8) * 8:(i // 8) * 8 + 8, :].rearrange("p a b -> p (a b)"), in_=pt[:]),
                    reads=[pt], writes=[qt])
        b.op("sp", lambda e, a=a, qt=qt: e.dma_start(out=scr["QT2"][a][:], in_=qt[:, 0:8, :].rearrange("p a b -> p (a b)")),
             reads=[qt], writes=[scr["QT2"][a]], dma=f"st_q2{a % 2}")
        for hp in range(8):
            G1.put(hp, a * 128, (a + 1) * 128, qt[:, 8 + hp, :], [qt], f"st_k2{a % 2}")
        ki_half = T(kibf.t[:, 0:64], kibf.d)
        headnorm_rope(b, kis, ksq, kss, 1, None, cos[:, a, :], sin[:, a, :], ki_half, tmp)
        b.op("pool", lambda e: e.tensor_copy(out=kibf[:, 64:128], in_=kibf[:, 0:64]), reads=[kibf], writes=[kibf])
        p7 = b.banks16[7]
        b.op("pe", lambda e: e.transpose(out=p7[:, 0:128], in_=kibf[:], identity=idt[:]), reads=[kibf, idt], writes=[p7])
        kt_ = kiT[a % 2]
        b.op("dve", lambda e, kt_=kt_: e.tensor_copy(out=kt_[:], in_=p7[:, 0:128]), reads=[p7], writes=[kt_])
        GK.put(0, a * 128, (a + 1) * 128, kt_[:], [kt_], f"st_ki{a % 2}")
        sg = sgn[a % 2]
        b.op("act", lambda e, sg=sg: e.activation(out=sg[:], in_=wi[:], func=AF.Sign), reads=[wi], writes=[sg])
        b.op("dve", lambda e, sg=sg: e.scalar_tensor_tensor(out=aw[:], in0=wi[:], scalar=IDX_SCALE, in1=sg[:],
                                                           op0=ALU.mult, op1=ALU.mult), reads=[wi, sg], writes=[aw])
        b.op("sp", lambda e, a=a, sg=sg: e.dma_start(out=scr["SG"][a][:], in_=sg[:]), reads=[sg], writes=[scr["SG"][a]],
             dma=f"st_sg{a % 2}")
        headnorm_rope(b, qis, None, aw, 8, None, cos[:, a, :], sin[:, a, :], qibf, tmp, norm=False)
        qi_ = qiT[a % 2]
        for i in range(4):
            b.op("pe", lambda e, i=i: e.transpose(out=p7[:, 256 + i * 128:256 + (i + 1) * 128],
                                                  in_=qibf[:, i * 128:(i + 1) * 128], identity=idt[:]),
                 reads=[qibf, idt], writes=[p7])
        b.op("dve", lambda e, qi_=qi_: e.tensor_copy(out=qi_[:].rearrange("p a b -> p (a b)"), in_=p7[:, 256:768]),
             reads=[p7], writes=[qi_])
        b.op("sp", lambda e, a=a, qi_=qi_: e.dma_start(out=scr["QI"][a][:], in_=qi_[:].rearrange("p a b -> p (a b)")),
             reads=[qi_], writes=[scr["QI"][a]], dma=f"st_qi{a % 2}")
    GK.place(0, "act")
    GK.reduce(0)
    for hp in range(8):
        G1.place(hp, "act")
        G1.reduce(hp)


def f_dsa_attn(b, io, o_tok, G1, GK, scr):
    idt = b.ident()
    kia = b.sb([128, 8192], BF16, "kiall")
    for r in range(4):
        b.op("sp", lambda e, r=r: e.dma_start(out=kia[:, r * 2048:(r + 1) * 2048], in_=GK.yb[0][r * 128:(r + 1) * 128, :]),
             reads=[GK.yd[0]], writes=[kia], dma="ld_kia")
    kia4 = kia[:].rearrange("p (r a t) -> p r a t", r=4, a=16)
    negm = b.sb([128, 512], F32, "negm")
    b.op("sp", lambda e: e.dma_start(out=negm[:], in_=io["negmask"][:]), writes=[negm], dma="ld_negm")
    cW = b.sb([128, NIT], F32, "cW")
    for i in range(NIT):
        b.op("dve", lambda e, i=i: e.memset(cW[:, i:i + 1], 2.0 ** (-i)), writes=[cW])
    Ib = b.sb([128, 8192], F32, "Ibuf")
    Mq = b.sb([128, 8192], BF16, "Mq")
    MT = b.sb([128, 64, 128], BF16, "MT")
    ktp = [b.sb([128, 8192], BF16, f"ktp{i}") for i in range(2)]
    vp = [b.sb([128, 64, 130], BF16, f"vp{i}") for i in range(2)]
    for i in range(2):
        b.op("dve", lambda e, i=i: e.memset(vp[i][:, :, 0:1], 1.0), writes=[vp[i]])
        b.op("dve", lambda e, i=i: e.memset(vp[i][:, :, 129:130], 1.0), writes=[vp[i]])
    qTa = [b.sb([128, 8, 128], BF16, f"qTa{i}") for i in range(2)]
    qiTa = [b.sb([128, 4, 128], BF16, f"qiTa{i}") for i in range(2)]
    sgn = [b.sb([128, 8], F32, f"sgna{i}") for i in range(2)]
    tb = [b.sb([128, 512], F32, f"tb{i}") for i in range(2)]
    pT = [b.sb([128, 512], BF16, f"pTd{i}") for i in range(2)]
    m1 = b.sb([128, 1], F32, "bm1")
    lo = b.sb([128, 1], F32, "blo")
    mid = b.sb([128, 1], F32, "bmid")
    cnt = b.sb([128, 1], F32, "bcnt")
    g = b.sb([128, 1], F32, "bg")
    W = b.sb([128, NIT], F32, "bW")
    rec = [b.sb([128, 1], F32, f"recd{i}") for i in range(2)]
    pS = [b.banks[0], b.banks[1]]
    pA = [b.banks[2], b.banks[3]]
    pI = [b.banks[4], b.banks[5]]
    pM = [b.banks16[6], b.banks16[7]]
    ctr = {"i": 0, "acc": 0, "kv": 0}

    def load_slot_small(a):
        b.op("sp", lambda e, a=a: e.dma_start(out=qTa[a % 2][:].rearrange("p a b -> p (a b)"), in_=scr["QT2"][a][:]),
             reads=[scr["QT2"][a]], writes=[qTa[a % 2]], dma=f"ld_qTa{a % 2}")
        b.op("sp", lambda e, a=a: e.dma_start(out=qiTa[a % 2][:].rearrange("p a b -> p (a b)"), in_=scr["QI"][a][:]),
             reads=[scr["QI"][a]], writes=[qiTa[a % 2]], dma=f"ld_qiTa{a % 2}")
        b.op("sp", lambda e, a=a: e.dma_start(out=sgn[a % 2][:], in_=scr["SG"][a][:]), reads=[scr["SG"][a]],
             writes=[sgn[a % 2]], dma=f"ld_sgn{a % 2}")

    def load_kv(a, hp):
        k = ctr["kv"] % 2
        ctr["kv"] += 1
        n = (a + 1) * 128
        yb = G1.yb[hp]
        b.op("sp", lambda e: e.dma_start(out=ktp[k][:].rearrange("p (r x) -> p r x", r=4)[:, :, 0:n],
                                         in_=yb[:, 0:n].rearrange("(r p) x -> p r x", p=128)),
             reads=[G1.yd[hp]], writes=[ktp[k]], dma=f"ld_ktp{k}")
        for r in range(4):
            b.op("sp", lambda e, r=r: e.dma_start(
                out=vp[k][:, r * 16:r * 16 + a + 1, 1:129],
                in_=yb[r * 128:(r + 1) * 128, 2048:2048 + n].rearrange("p (a e) -> p a e", e=128)),
                reads=[G1.yd[hp]], writes=[vp[k]], dma=f"ld_vp{k}")
        return k

    def indexer(a):
        nb = a + 1
        nv = 512 * nb
        qi, sg = qiTa[a % 2], sgn[a % 2]
        for blk in range(nb):
            for head in range(8):
                hp, hh = head // 2, head % 2
                py = pI[ctr["i"] % 2]
                t = tb[ctr["i"] % 2]
                ctr["i"] += 1
                b.op("pe", lambda e, py=py, hp=hp, hh=hh, blk=blk, qi=qi: e.matmul(
                    py[:].rearrange("p (r t) -> p r t", r=4), lhsT=qi[64 * hh:64 * hh + 64, hp, :],
                    rhs=kia4[64 * hh:64 * hh + 64, :, blk, :], start=True, stop=True), reads=[qi, kia], writes=[py])
                b.op("act", lambda e, py=py, t=t: e.activation(out=t[:], in_=py[:], func=AF.Relu), reads=[py], writes=[t])
                if head == 0:
                    b.op("dve", lambda e, t=t, blk=blk, sg=sg: e.tensor_scalar(
                        out=Ib[:, blk * 512:(blk + 1) * 512], in0=t[:], scalar1=sg[:, 0:1], scalar2=None, op0=ALU.mult),
                        reads=[t, sg], writes=[Ib])
                else:
                    b.op("dve", lambda e, t=t, blk=blk, sg=sg, head=head: e.scalar_tensor_tensor(
                        out=Ib[:, blk * 512:(blk + 1) * 512], in0=t[:], scalar=sg[:, head:head + 1],
                        in1=Ib[:, blk * 512:(blk + 1) * 512], op0=ALU.mult, op1=ALU.add),
                        reads=[t, sg, Ib], writes=[Ib])
        b.op("dve", lambda e: e.tensor_reduce(out=m1[:], in_=Ib[:, 0:nv], axis=AX.X, op=ALU.max, apply_absolute_value=True),
             reads=[Ib], writes=[m1])
        b.op("dve", lambda e: e.tensor_tensor(out=Ib[:, nv - 512:nv], in0=Ib[:, nv - 512:nv], in1=negm[:], op=ALU.add),
             reads=[Ib, negm], writes=[Ib])
        b.op("dve", lambda e: e.tensor_scalar(out=m1[:], in0=m1[:], scalar1=1.0, scalar2=None, op0=ALU.add),
             reads=[m1], writes=[m1])
        b.op("dve", lambda e: e.tensor_scalar(out=lo[:], in0=m1[:], scalar1=-1.0, scalar2=None, op0=ALU.mult),
             reads=[m1], writes=[lo])
        b.op("dve", lambda e: e.tensor_scalar(out=W[:], in0=cW[:], scalar1=m1[:, 0:1], scalar2=None, op0=ALU.mult),
             reads=[cW, m1], writes=[W])
        b.op("dve", lambda e: e.tensor_tensor(out=mid[:], in0=lo[:], in1=W[:, 0:1], op=ALU.add), reads=[lo, W], writes=[mid])
        for i in range(NIT):
            b.op("dve", lambda e: e.tensor_scalar(out=Mq[:, 0:nv], in0=Ib[:, 0:nv], scalar1=mid[:, 0:1], scalar2=0.0,
                                                  op0=ALU.is_ge, op1=ALU.add, accum_out=cnt[:, 0:1]),
                 reads=[Ib, mid], writes=[Mq, cnt])
            b.op("dve", lambda e, i=i: e.tensor_scalar(out=g[:], in0=cnt[:], scalar1=TOPK - 0.5, scalar2=W[:, i:i + 1],
                                                       op0=ALU.is_ge, op1=ALU.mult), reads=[cnt, W], writes=[g])
            b.op("dve", lambda e: e.tensor_tensor(out=lo[:], in0=lo[:], in1=g[:], op=ALU.add), reads=[lo, g], writes=[lo])
            if i + 1 < NIT:
                b.op("dve", lambda e, i=i: e.tensor_tensor(out=mid[:], in0=lo[:], in1=W[:, i + 1:i + 2], op=ALU.add),
                     reads=[lo, W], writes=[mid])
        b.op("dve", lambda e: e.tensor_scalar(out=Mq[:, 0:nv], in0=Ib[:, 0:nv], scalar1=lo[:, 0:1], scalar2=None,
                                              op0=ALU.is_ge), reads=[Ib, lo], writes=[Mq])
        nt = 4 * nb
        for kt in range(nt):
            pm = pM[(kt // 8) % 2]
            b.op("pe", lambda e, kt=kt, pm=pm: e.transpose(out=pm[:, (kt % 8) * 128:(kt % 8 + 1) * 128],
                                                           in_=Mq[:, kt * 128:(kt + 1) * 128], identity=idt[:]),
                 reads=[Mq, idt], writes=[pm])
            if kt % 8 == 7 or kt == nt - 1:
                k0 = (kt // 8) * 8
                n = kt - k0 + 1
                b.op("act", lambda e, pm=pm, k0=k0, n=n: e.activation(
                    out=MT[:, k0:k0 + n, :].rearrange("p a b -> p (a b)"), in_=pm[:, 0:n * 128], func=AF.Copy),
                    reads=[pm], writes=[MT])

    def qk(u, n, kbuf):
        a, hp, hh, blk = u
        ps = pS[n % 2]
        qa = qTa[a % 2]
        for i in range(4):
            kt = 16 * i + blk
            b.op("pe", lambda e, i=i, kt=kt, ps=ps, qa=qa, hh=hh, hp=hp, kbuf=kbuf: e.matmul(
                ps[:, i * 128:(i + 1) * 128], lhsT=ktp[kbuf][64 * hh:64 * hh + 64, kt * 128:(kt + 1) * 128],
                rhs=qa[64 * hh:64 * hh + 64, hp, :], start=True, stop=True), reads=[ktp[kbuf], qa], writes=[ps])

    def softmax_pv(u, n, kbuf):
        a, hp, hh, blk = u
        ps, pt = pS[n % 2], pT[n % 2]
        if blk == 0:
            ctr["acc"] += 1
        acc = pA[ctr["acc"] % 2]
        b.op("act", lambda e: e.activation(out=pt[:], in_=ps[:], func=AF.Exp), reads=[ps], writes=[pt])
        b.op("dve", lambda e: e.tensor_tensor(out=pt[:], in0=pt[:], in1=MT[:, 4 * blk:4 * blk + 4, :].rearrange("p a b -> p (a b)"),
                                              op=ALU.mult), reads=[pt, MT], writes=[pt])
        for i in range(4):
            kt = 16 * i + blk
            b.op("pe", lambda e, i=i, kt=kt: e.matmul(acc[:, 0:65], lhsT=pt[:, i * 128:(i + 1) * 128],
                                                      rhs=vp[kbuf][:, kt, hh * 65:(hh + 1) * 65],
                                                      start=(blk == 0 and i == 0), stop=(blk == a and i == 3)),
                 reads=[pt, vp[kbuf]], writes=[acc])
        if blk == a:
            head = 2 * hp + hh
            r = rec[ctr["acc"] % 2]
            sc, v0 = (0, 1) if hh == 0 else (64, 0)
            b.op("dve", lambda e: e.reciprocal(out=r[:], in_=acc[:, sc:sc + 1]), reads=[acc], writes=[r])
            b.op("dve", lambda e: e.tensor_scalar(out=o_tok[a][:, head * 64:(head + 1) * 64], in0=acc[:, v0:v0 + 64],
                                                  scalar1=r[:, 0:1], scalar2=None, op0=ALU.mult),
                 reads=[acc, r], writes=[o_tok[a]])

    load_slot_small(0)
    for a in range(NS):
        if a + 1 < NS:
            load_slot_small(a + 1)
        kb_next = load_kv(a, 0)
        indexer(a)
        units = [(a, hp, hh, blk) for hp in range(8) for hh in range(2) for blk in range(a + 1)]
        kbufs = {0: kb_next}
        qk(units[0], 0, kbufs[0])
        for n, u in enumerate(units):
            _, hp, hh, blk = u
            if hh == 0 and blk == 0 and hp + 1 < 8:
                kbufs[hp + 1] = load_kv(a, hp + 1)
            if n + 1 < len(units):
                qk(units[n + 1], n + 1, kbufs[units[n + 1][1]])
            softmax_pv(u, n, kbufs[hp])


BF = ml_dtypes.bfloat16


def rep(v, n=128):
    return np.ascontiguousarray(np.tile(np.asarray(v).reshape(1, -1), (n, 1)))


def own_tiles(arr_bs, c):
    bb, j = c // 4, c % 4
    a = arr_bs[bb]
    return np.ascontiguousarray(a.reshape(64, 128, *a.shape[1:])[j::4].reshape(2048, *a.shape[1:]))


def gather_tiles(per_core, bb):
    out = np.empty((64,) + per_core[0].shape[1:], per_core[0].dtype)
    for j in range(4):
        out[j::4] = per_core[bb * 4 + j]
    return out


def diff_masks(c):
    j = c % 4
    m = np.zeros((128, 4, 128), np.float32)
    for i in range(4):
        if i < j:
            m[:, i, :] = 1.0
        elif i == j:
            m[0:64, i, :] = 1.0
            m[64:128, i, 64:128] = 1.0
    return m.reshape(128, 512).astype(BF)


def dsa_negmask(c):
    return np.where(diff_masks(c).astype(np.float32).reshape(128, 4, 128).transpose(2, 1, 0).reshape(128, 512) > 0,
                    0.0, -1e30).astype(np.float32)


def build_L1():
    nc = bass.Bass("TRN2", target_bir_lowering=False)
    b = B(nc)
    io = {
        "x": b.dram("x", [2048, 1024], F32, "ExternalInput"),
        "pos": b.dram("pos", [128, 16], I32, "ExternalInput"),
        "gmix": b.dram("gmix", [128, 1024], F32, "ExternalInput"),
        "gqk": b.dram("gqk", [128, 2048], F32, "ExternalInput"),
        "w_in": b.dram("w_in", [1024, 3072], F32, "ExternalInput"),
        "QT": b.dram("QT", [16, 128, 1024], BF16, "ExternalOutput"),
        "KT": b.dram("KT", [16, 128, 1024], BF16, "ExternalOutput"),
        "V": b.dram("V", [16, 128, 8 * 129], BF16, "ExternalOutput"),
    }
    phase_diff_proj(b, io)
    b.finish()
    return nc


def build_L2():
    nc = bass.Bass("TRN2", target_bir_lowering=False)
    b = B(nc)
    io = {
        "QT": b.dram("QT", [16, 128, 1024], BF16, "ExternalInput"),
        "KTall": b.dram("KTall", [8, 128, 8192], BF16, "ExternalInput"),
        "Vall": b.dram("Vall", [8, 128, 64 * 129], BF16, "ExternalInput"),
        "maskT": b.dram("maskT", [128, 512], BF16, "ExternalInput"),
        "lam": b.dram("lam", [128, 256], F32, "ExternalInput"),
        "gsub": b.dram("gsub", [128, 128], F32, "ExternalInput"),
        "x": b.dram("x", [2048, 1024], F32, "ExternalInput"),
        "pos": b.dram("pos", [128, 16], I32, "ExternalInput"),
        "w_out": b.dram("w_out", [1024, 1024], F32, "ExternalInput"),
        "gmlp": b.dram("gmlp", [128, 1024], F32, "ExternalInput"),
        "w1": b.dram("w1", [1024, 4096], F32, "ExternalInput"),
        "w2": b.dram("w2", [4096, 1024], F32, "ExternalInput"),
        "gmix1": b.dram("gmix1", [128, 1024], F32, "ExternalInput"),
        "gqk2": b.dram("gqk2", [128, 2048], F32, "ExternalInput"),
        "gcq": b.dram("gcq", [128, 256], F32, "ExternalInput"),
        "w_in2": b.dram("w_in2", [1024, 2376], F32, "ExternalInput"),
        "w_uq": b.dram("w_uq", [256, 1024], F32, "ExternalInput"),
        "w_uqi": b.dram("w_uqi", [256, 512], F32, "ExternalInput"),
        "X2": b.dram("X2", [2048, 1024], F32, "ExternalOutput"),
        "QT2": b.dram("QT2", [16, 128, 1024], BF16, "ExternalOutput"),
        "KT2": b.dram("KT2", [16, 128, 1024], BF16, "ExternalOutput"),
        "V2": b.dram("V2", [16, 128, 16 * 65], BF16, "ExternalOutput"),
        "KI": b.dram("KI", [16, 128, 128], BF16, "ExternalOutput"),
        "QI": b.dram("QI", [16, 128, 512], BF16, "ExternalOutput"),
        "SG": b.dram("SG", [16, 128, 8], F32, "ExternalOutput"),
    }
    b.ident(); b.eps()
    pos_t = b.sb([128, NS], I32, "pos")
    b.op("sp", lambda e: e.dma_start(out=pos_t[:], in_=io["pos"][:]), writes=[pos_t], dma="ld_pos")
    cos, sin = rope_tables(b, pos_t)
    o_tok = [b.sb([128, 1024], BF16, f"otok{a}", top=True) for a in range(NS)]
    m1 = b.mark()
    phase_diff_attn(b, io, o_tok)
    b.release(m1)
    x_res = [b.sb([128, 1024], F32, f"xres{a}") for a in range(NS)]
    h2T = b.sb([128, 8, 2048], BF16, "h2T")
    m2 = b.mark()
    phase_post_attn(b, io, o_tok, x_res, h2T, io["w_out"][:], io["gmlp"][:], io["x"])
    b.release(m2)
    b.hi = ARENA_END
    m3 = b.mark()
    phase_mlp(b, x_res, h2T, io["w1"], io["w2"])
    b.release(m3)
    for a in range(NS):
        b.store("sp", lambda e, a=a: e.dma_start(out=io["X2"][a * 128:(a + 1) * 128, :], in_=x_res[a][:]),
                reads=[x_res[a]], dma="st_x2")
    phase_dsa_proj(b, io, x_res, cos, sin)
    b.finish()
    return nc


def l1_inputs(inp, c):
    return {"x": own_tiles(inp["x"], c),
            "pos": np.ascontiguousarray(own_tiles(inp["positions"], c).reshape(16, 128).T),
            "gmix": rep(inp["norm_mix"][0]),
            "gqk": rep(np.concatenate([np.tile(inp["diff_q_norm"][0], 16), np.tile(inp["diff_k_norm"][0], 16)])),
            "w_in": np.ascontiguousarray(inp["diff_w_in"][0])}


def l2_inputs(inp, r1):
    KTall = []; Vall = []
    for bb in range(2):
        kt = gather_tiles([r["KT"].reshape(16, 128, 8, 128) for r in r1], bb)
        KTall.append(np.ascontiguousarray(kt.transpose(2, 1, 0, 3).reshape(8, 128, 8192)))
        v = gather_tiles([r["V"].reshape(16, 128, 8, 129) for r in r1], bb)
        Vall.append(np.ascontiguousarray(v.transpose(2, 1, 0, 3).reshape(8, 128, 64 * 129)))
    lam = rep(np.concatenate([inp["diff_lam_q1"][0], inp["diff_lam_k1"][0], inp["diff_lam_q2"][0], inp["diff_lam_k2"][0]]))
    gqk2 = rep(np.concatenate([np.tile(inp["dsa_q_norm"][0], 16), np.tile(inp["dsa_k_norm"][0], 16)]))
    ins = []
    for c in range(8):
        ins.append({"QT": r1[c]["QT"], "KTall": KTall[c // 4], "Vall": Vall[c // 4], "maskT": diff_masks(c),
                    "lam": lam, "gsub": rep(inp["diff_subln"][0]),
                    "x": own_tiles(inp["x"], c),
                    "pos": np.ascontiguousarray(own_tiles(inp["positions"], c).reshape(16, 128).T),
                    "w_out": np.ascontiguousarray(inp["diff_w_out"][0]), "gmlp": rep(inp["norm_mlp"][0]),
                    "w1": np.ascontiguousarray(inp["mlp_w1"][0]), "w2": np.ascontiguousarray(inp["mlp_w2"][0]),
                    "gmix1": rep(inp["norm_mix"][1]), "gqk2": gqk2, "gcq": rep(inp["dsa_cq_norm"][0]),
                    "w_in2": np.ascontiguousarray(inp["dsa_w_in"][0]), "w_uq": np.ascontiguousarray(inp["dsa_w_uq"][0]),
                    "w_uqi": np.ascontiguousarray(inp["dsa_w_uq_idx"][0])})
    return ins


def run(nc, ins):
    res = run_bass_kernel_spmd(nc, ins, core_ids=list(range(8)))
    return [{k: np.asarray(v) for k, v in r.items()} for r in res.results]


def build_L3():
    nc = bass.Bass("TRN2", target_bir_lowering=False)
    b = B(nc)
    io = {
        "QT2": b.dram("QT2", [16, 128, 1024], BF16, "ExternalInput"),
        "QI": b.dram("QI", [16, 128, 512], BF16, "ExternalInput"),
        "SG": b.dram("SG", [16, 128, 8], F32, "ExternalInput"),
        "KIall": b.dram("KIall", [128, 8192], BF16, "ExternalInput"),
        "KT2all": b.dram("KT2all", [8, 128, 8192], BF16, "ExternalInput"),
        "V2all": b.dram("V2all", [8, 128, 64 * 130], BF16, "ExternalInput"),
        "negmask": b.dram("negmask", [128, 512], F32, "ExternalInput"),
        "X2": b.dram("X2", [2048, 1024], F32, "ExternalInput"),
        "w_out": b.dram("w_out", [1024, 1024], F32, "ExternalInput"),
        "gmlp": b.dram("gmlp", [128, 1024], F32, "ExternalInput"),
        "w1": b.dram("w1", [1024, 4096], F32, "ExternalInput"),
        "w2": b.dram("w2", [4096, 1024], F32, "ExternalInput"),
        "OUT": b.dram("OUT", [2048, 1024], F32, "ExternalOutput"),
    }
    b.ident(); b.eps()
    o_tok = [b.sb([128, 1024], BF16, f"otok{a}", top=True) for a in range(NS)]
    m1 = b.mark()
    phase_dsa_attn(b, io, o_tok)
    b.release(m1)
    x_res = [b.sb([128, 1024], F32, f"xres{a}") for a in range(NS)]
    h2T = b.sb([128, 8, 2048], BF16, "h2T")
    m2 = b.mark()
    phase_post_attn(b, io, o_tok, x_res, h2T, io["w_out"][:], io["gmlp"][:], io["X2"])
    b.release(m2)
    b.hi = ARENA_END
    m3 = b.mark()
    phase_mlp(b, x_res, h2T, io["w1"], io["w2"])
    b.release(m3)
    for a in range(NS):
        b.store("sp", lambda e, a=a: e.dma_start(out=io["OUT"][a * 128:(a + 1) * 128, :], in_=x_res[a][:]),
                reads=[x_res[a]], dma="st_out")
    b.finish()
    return nc


def l3_inputs(inp, r2):
    KIall = []; KTall = []; Vall = []
    for bb in range(2):
        ki = gather_tiles([r["KI"] for r in r2], bb)
        KIall.append(np.ascontiguousarray(ki.transpose(1, 0, 2).reshape(128, 8192)))
        kt = gather_tiles([r["KT2"].reshape(16, 128, 8, 128) for r in r2], bb)
        KTall.append(np.ascontiguousarray(kt.transpose(2, 1, 0, 3).reshape(8, 128, 8192)))
        v = gather_tiles([r["V2"].reshape(16, 128, 8, 130) for r in r2], bb)
        Vall.append(np.ascontiguousarray(v.transpose(2, 1, 0, 3).reshape(8, 128, 64 * 130)))
    ins = []
    for c in range(8):
        ins.append({"QT2": r2[c]["QT2"], "QI": r2[c]["QI"], "SG": r2[c]["SG"], "KIall": KIall[c // 4],
                    "KT2all": KTall[c // 4], "V2all": Vall[c // 4], "negmask": dsa_negmask(c),
                    "X2": r2[c]["X2"], "w_out": np.ascontiguousarray(inp["dsa_w_out"][0]), "gmlp": rep(inp["norm_mlp"][1]),
                    "w1": np.ascontiguousarray(inp["mlp_w1"][1]), "w2": np.ascontiguousarray(inp["mlp_w2"][1])})
    return ins


def assemble(r3):
    out = np.empty((2, 8192, 1024), np.float32)
    for bb in range(2):
        o = gather_tiles([r["OUT"].reshape(16, 128, 1024) for r in r3], bb)
        out[bb] = o.reshape(8192, 1024)
    return out


FUSED_IN = [
    ("x", [2048, 1024], F32), ("pos", [128, 16], I32), ("gmix", [128, 1024], F32), ("gqk", [128, 2048], F32),
    ("w_in", [1024, 3072], F32), ("maskT", [128, 512], BF16), ("lam", [128, 256], F32), ("gsub", [128, 128], F32),
    ("w_out", [1024, 1024], F32), ("gmlp", [128, 1024], F32), ("w1", [1024, 4096], F32), ("w2", [4096, 1024], F32),
    ("gmix1", [128, 1024], F32), ("gqk2", [128, 2048], F32), ("gcq", [128, 256], F32), ("w_in2", [1024, 2376], F32),
    ("w_uq", [256, 1024], F32), ("w_uqi", [256, 512], F32), ("negmask", [128, 512], F32),
    ("w_outb", [1024, 1024], F32), ("gmlpb", [128, 1024], F32), ("w1b", [1024, 4096], F32), ("w2b", [4096, 1024], F32),
]


def build_fused():
    nc = bass.Bass("TRN2", target_bir_lowering=False)
    _RANK.clear()
    b = B(nc)
    io = {n: b.dram(n, s, d, "ExternalInput") for (n, s, d) in FUSED_IN}
    io["OUT"] = b.dram("OUT", [2048, 1024], F32, "ExternalOutput")
    qt2 = nc.dram_tensor("scr_qt2", [16, 128, 1024], BF16).ap()
    qi = nc.dram_tensor("scr_qi", [16, 128, 512], BF16).ap()
    sg = nc.dram_tensor("scr_sg", [16, 128, 8], F32).ap()
    x2 = nc.dram_tensor("scr_x2", [2048, 1024], F32).ap()
    scr = {"QT2": [T(qt2[a]) for a in range(NS)], "QI": [T(qi[a]) for a in range(NS)], "SG": [T(sg[a]) for a in range(NS)]}
    x2d = [T(x2[a * 128:(a + 1) * 128, :]) for a in range(NS)]

    b.ident(); b.eps()
    pos_t = b.sb([128, NS], I32, "pos")
    b.op("sp", lambda e: e.dma_start(out=pos_t[:], in_=io["pos"][:]), writes=[pos_t], dma="ld_pos")
    cos, sin = rope_tables(b, pos_t)
    o_tok = [b.sb([128, 1024], BF16, f"otok{a}", top=True) for a in range(NS)]
    hi_otok = b.hi
    qT_res = b.sb([128, NS, 1024], BF16, "qTres", top=True)
    m0 = b.mark()
    zt = b.sb([128, 4096], BF16, "zeros")
    b.op("pool", lambda e: e.memset(zt[:], 0.0), writes=[zt])
    G0 = Gather(b, "g0", 8, 4096, zt)
    G1 = Gather(b, "g1", 8, 4096, zt)
    GK = Gather(b, "gk", 1, 2048, zt)
    f_diff_proj(b, io, qT_res, G0, cos, sin)
    b.release(m0)
    f_diff_attn(b, io, o_tok, qT_res, G0)
    b.release(m0)
    b.hi = hi_otok
    x_res = [b.sb([128, 1024], F32, f"xres{a}") for a in range(NS)]
    h2T = b.sb([128, 8, 2048], BF16, "h2T")
    m2 = b.mark()
    phase_post_attn(b, io, o_tok, x_res, h2T, io["w_out"][:], io["gmlp"][:], io["x"])
    b.release(m2)
    b.hi = ARENA_END
    m3 = b.mark()
    phase_mlp(b, x_res, h2T, io["w1"], io["w2"])
    b.release(m3)
    for a in range(NS):
        b.op("sp", lambda e, a=a: e.dma_start(out=x2d[a][:], in_=x_res[a][:]), reads=[x_res[a]], writes=[x2d[a]], dma="st_x2")
    io2 = dict(io)
    f_dsa_proj(b, io2, x_res, cos, sin, G1, GK, scr)
    b.release(m0)
    b.hi = ARENA_END
    o_tok = [b.sb([128, 1024], BF16, f"otokb{a}", top=True) for a in range(NS)]
    m4 = b.mark()
    f_dsa_attn(b, io, o_tok, G1, GK, scr)
    b.release(m4)
    x_res = [b.sb([128, 1024], F32, f"xresb{a}") for a in range(NS)]
    h2T = b.sb([128, 8, 2048], BF16, "h2Tb")
    m5 = b.mark()
    phase_post_attn(b, io, o_tok, x_res, h2T, io["w_outb"][:], io["gmlpb"][:], x2, xdeps=x2d)
    b.release(m5)
    b.hi = ARENA_END
    phase_mlp(b, x_res, h2T, io["w1b"], io["w2b"])
    for a in range(NS):
        b.store("sp", lambda e, a=a: e.dma_start(out=io["OUT"][a * 128:(a + 1) * 128, :], in_=x_res[a][:]),
                reads=[x_res[a]], dma="st_out")
    b.finish()
    return nc


def fused_inputs(inp):
    lam = rep(np.concatenate([inp["diff_lam_q1"][0], inp["diff_lam_k1"][0], inp["diff_lam_q2"][0], inp["diff_lam_k2"][0]]))
    gqk = rep(np.concatenate([np.tile(inp["diff_q_norm"][0], 16), np.tile(inp["diff_k_norm"][0], 16)]))
    gqk2 = rep(np.concatenate([np.tile(inp["dsa_q_norm"][0], 16), np.tile(inp["dsa_k_norm"][0], 16)]))
    c_ = np.ascontiguousarray
    shared = {"gmix": rep(inp["norm_mix"][0]), "gqk": gqk, "w_in": c_(inp["diff_w_in"][0]), "lam": lam,
              "gsub": rep(inp["diff_subln"][0]), "w_out": c_(inp["diff_w_out"][0]), "gmlp": rep(inp["norm_mlp"][0]),
              "w1": c_(inp["mlp_w1"][0]), "w2": c_(inp["mlp_w2"][0]), "gmix1": rep(inp["norm_mix"][1]), "gqk2": gqk2,
              "gcq": rep(inp["dsa_cq_norm"][0]), "w_in2": c_(inp["dsa_w_in"][0]), "w_uq": c_(inp["dsa_w_uq"][0]),
              "w_uqi": c_(inp["dsa_w_uq_idx"][0]), "w_outb": c_(inp["dsa_w_out"][0]), "gmlpb": rep(inp["norm_mlp"][1]),
              "w1b": c_(inp["mlp_w1"][1]), "w2b": c_(inp["mlp_w2"][1])}
    ins = []
    for c in range(8):
        d = dict(shared)
        d["x"] = own_tiles(inp["x"], c)
        d["pos"] = np.ascontiguousarray(own_tiles(inp["positions"], c).reshape(16, 128).T)
        d["maskT"] = diff_masks(c)
        d["negmask"] = dsa_negmask(c)
        ins.append(d)
    return ins


def kernel(**inputs):
    inp = {k: np.asarray(v) for k, v in inputs.items()}
    r = run(build_fused(), fused_inputs(inp))
    return assemble(r)
```

```python
import math
from contextlib import ExitStack
import numpy as np
import ml_dtypes
import concourse.bass as bass
import concourse.mybir as mybir
from concourse.bass_utils import run_bass_kernel_spmd


F32 = mybir.dt.float32
BF16 = mybir.dt.bfloat16
I32 = mybir.dt.int32
AF = mybir.ActivationFunctionType
ALU = mybir.AluOpType
AX = mybir.AxisListType


class Dep:
    __slots__ = ("w", "r")

    def __init__(self):
        self.w = None
        self.r = {}


class Sched:
    ENG = ("pe", "act", "dve", "pool", "sp")

    def __init__(self, nc):
        self.nc = nc
        self.ops = {e: [] for e in self.ENG}
        self.cnt = {e: 0 for e in self.ENG}
        self.known = {e: {} for e in self.ENG}
        self.dma_cnt = {}
        self.stack = ExitStack()
        self.nt = 0

    def sb(self, shape, dtype, name=None):
        self.nt += 1
        name = "sb_" + (name or f"t{self.nt}")
        return self.stack.enter_context(self.nc.sbuf_tensor(name, list(shape), dtype))

    def ps(self, shape, dtype, name=None):
        self.nt += 1
        name = "ps_" + (name or f"p{self.nt}")
        return self.stack.enter_context(self.nc.psum_tensor(name, list(shape), dtype))

    def op(self, eng, fn, reads=(), writes=(), dma=None, sem_inc=16):
        waits = {}

        def need(ev, raw):
            if ev is None:
                return
            key, val = ev
            if key == eng:
                if eng == "pe" or not raw:
                    return
            if waits.get(key, 0) < val:
                waits[key] = val

        for d in reads:
            need(d.w, True)
        for d in writes:
            need(d.w, False)
            for ev in d.r.items():
                need(ev, False)
        kn = self.known[eng]
        wl = []
        for key, val in waits.items():
            if kn.get(key, 0) >= val:
                continue
            kn[key] = val
            wl.append((key, val))
        if dma is not None:
            n = self.dma_cnt.get(dma, 0) + sem_inc
            self.dma_cnt[dma] = n
            ev = (dma, n)
            inc = (dma, sem_inc)
        else:
            self.cnt[eng] += 1
            ev = (eng, self.cnt[eng])
            inc = (eng, 1)
        self.ops[eng].append((wl, fn, inc))
        for d in reads:
            if d.r.get(ev[0], 0) < ev[1]:
                d.r[ev[0]] = ev[1]
        for d in writes:
            d.w = ev
            d.r = {}
        return ev

    def final_wait(self, eng, deps):
        waits = {}
        for d in deps:
            for ev in ([d.w] if d.w else []) + list(d.r.items()):
                if waits.get(ev[0], 0) < ev[1]:
                    waits[ev[0]] = ev[1]
        self.ops[eng].append((list(waits.items()), None, None))

    def barrier(self):
        waits = {e: self.cnt[e] for e in self.ENG if self.cnt[e] > 0}
        for k, n in self.dma_cnt.items():
            if not k.startswith("cc_"):
                waits[k] = n
        for e in self.ENG:
            kn = self.known[e]
            wl = []
            for key, val in waits.items():
                if key == e or kn.get(key, 0) >= val:
                    continue
                kn[key] = val
                wl.append((key, val))
            self.ops[e].append((wl, None, None))

    def final_events(self, eng, evs):
        waits = {}
        for ev in evs:
            if waits.get(ev[0], 0) < ev[1]:
                waits[ev[0]] = ev[1]
        self.ops[eng].append((list(waits.items()), None, None))

    def emit(self):
        nc = self.nc
        keys = set(self.ENG) | set(self.dma_cnt.keys())
        assert len(keys) <= 100, f"too many semaphores: {len(keys)}"
        sems = {}
        for k in sorted(keys):
            sems[k] = self.stack.enter_context(nc.semaphore("s_" + k))
        ops = self.ops

        def run(e, lst):
            for wl, fn, inc in lst:
                for key, val in wl:
                    e.wait_ge(sems[key], val)
                if fn is None:
                    continue
                ins = fn(e)
                if inc is not None:
                    ins.then_inc(sems[inc[0]], inc[1])

        with nc.Block() as block:
            @block.tensor
            def _(e):
                run(e, ops["pe"])

            @block.scalar
            def _(e):
                run(e, ops["act"])

            @block.vector
            def _(e):
                run(e, ops["dve"])

            @block.gpsimd
            def _(e):
                run(e, ops["pool"])

            @block.sync
            def _(e):
                run(e, ops["sp"])
        self.stack.close()


NS = 16
D = 1024
EPS = 1e-6
INV_FREQ = [500000.0 ** (-(2.0 * j) / 16.0) for j in range(8)]
TWO_PI_S = 6.28318
PI_S = 3.14159


class T:
    def __init__(self, t, d=None):
        self.t = t
        self.d = d if d is not None else Dep()

    def __getitem__(self, k):
        return self.t[k]


ARENA_BASE = 16512
ARENA_END = 16512 + 212736


class B:
    def __init__(self, nc):
        self.nc = nc
        self.S = Sched(nc)
        self._consts = {}
        self.final = []
        self.arena = nc.alloc_sbuf_tensor("arena", [128, ARENA_END - ARENA_BASE], mybir.dt.uint8)
        self.lo = ARENA_BASE
        self.hi = ARENA_END
        self.nt = 0
        self.banks = [T(nc.alloc_psum_tensor(f"bank{i}", [128, 512], F32)) for i in range(8)]
        self.banks16 = [T(bk.t[:].bitcast(BF16), bk.d) for bk in self.banks]

    def sb(self, shape, dt, name=None, top=False):
        self.nt += 1
        nm = f"sb{self.nt}_{name or 't'}"
        size = int(np.prod(shape[1:])) * mybir.dt.size(dt)
        size = (size + 31) // 32 * 32
        if top:
            self.hi -= size
            off = self.hi
        else:
            off = self.lo
            self.lo += size
        assert self.lo <= self.hi, f"SBUF arena overflow at {nm}: lo={self.lo} hi={self.hi}"
        return T(self.nc.alloc_sbuf_tensor_at(nm, list(shape), dt, offset=off))

    def mark(self):
        return (self.lo, self.hi)

    def release(self, m):
        self.lo, self.hi = m
        self.S.barrier()

    def dram(self, name, shape, dt, kind):
        t = T(self.nc.dram_tensor(name, list(shape), dt, kind=kind).ap())
        return t

    def op(self, eng, fn, reads=(), writes=(), dma=None, sem_inc=16):
        return self.S.op(eng, fn, [x.d for x in reads], [x.d for x in writes], dma, sem_inc)

    def store(self, eng, fn, reads, dma):
        ev = self.S.op(eng, fn, [x.d for x in reads], [], dma)
        self.final.append(ev)
        return ev

    def finish(self):
        self.S.final_events("sp", self.final)
        self.S.emit()

    def ident(self):
        if "ident" not in self._consts:
            idt = self.sb([128, 128], BF16, "ident")
            self.op("pool", lambda e: e.memset(idt[:], 0.0), writes=[idt])
            self.op("pool", lambda e: e.affine_select(out=idt[:], in_=idt[:], pattern=[[-1, 128]],
                                                     compare_op=ALU.not_equal, fill=1.0, base=0,
                                                     channel_multiplier=1), reads=[idt], writes=[idt])
            self._consts["ident"] = idt
        return self._consts["ident"]

    def eps(self):
        if "eps" not in self._consts:
            t = self.sb([128, 1], F32, "eps")
            self.op("pool", lambda e: e.memset(t[:], EPS), writes=[t])
            self._consts["eps"] = t
        return self._consts["eps"]


def rstd_from_ss(b, ss, n, scale):
    eps = b.eps()
    b.op("act", lambda e: e.activation(out=ss[:, 0:n], in_=ss[:, 0:n], func=AF.Ln, bias=eps[:], scale=scale),
         reads=[ss, eps], writes=[ss])
    b.op("act", lambda e: e.activation(out=ss[:, 0:n], in_=ss[:, 0:n], func=AF.Exp, scale=-0.5),
         reads=[ss], writes=[ss])


def rope_tables(b, pos_t):
    posf = b.sb([128, NS], F32, "posf")
    inv = b.sb([128, 8], F32, "invf")
    ang = b.sb([128, NS, 8], F32, "ang")
    ti = b.sb([128, NS, 8], I32, "angi")
    tf = b.sb([128, NS, 8], F32, "angf")
    neg = b.sb([128, NS, 8], F32, "angn")
    cos = b.sb([128, NS, 8], F32, "cos")
    sin = b.sb([128, NS, 8], F32, "sin")
    nb = b.sb([128, 1], F32, "negpi")
    b.op("pool", lambda e: e.memset(nb[:], -PI_S), writes=[nb])
    b.op("dve", lambda e: e.tensor_copy(out=posf[:], in_=pos_t[:]), reads=[pos_t], writes=[posf])
    for j in range(8):
        b.op("pool", lambda e, j=j: e.memset(inv[:, j:j + 1], INV_FREQ[j] / (2 * math.pi)), writes=[inv])
    b.op("dve", lambda e: e.tensor_tensor(out=ang[:], in0=posf[:].unsqueeze(2).to_broadcast([128, NS, 8]),
                                          in1=inv[:].unsqueeze(1).to_broadcast([128, NS, 8]), op=ALU.mult),
         reads=[posf, inv], writes=[ang])
    for (dst, off) in ((sin, 0.5), (cos, 0.75)):
        b.op("dve", lambda e, off=off: e.tensor_scalar(out=tf[:], in0=ang[:], scalar1=off, scalar2=None, op0=ALU.add),
             reads=[ang], writes=[tf])
        b.op("dve", lambda e: e.tensor_copy(out=ti[:], in_=tf[:]), reads=[tf], writes=[ti])
        b.op("dve", lambda e: e.tensor_copy(out=neg[:], in_=ti[:]), reads=[ti], writes=[neg])
        b.op("dve", lambda e: e.tensor_tensor(out=tf[:], in0=tf[:], in1=neg[:], op=ALU.subtract),
             reads=[tf, neg], writes=[tf])
        b.op("dve", lambda e: e.tensor_scalar(out=neg[:], in0=tf[:], scalar1=0.0, scalar2=None, op0=ALU.is_lt),
             reads=[tf], writes=[neg])
        b.op("dve", lambda e: e.tensor_tensor(out=tf[:], in0=tf[:], in1=neg[:], op=ALU.add),
             reads=[tf, neg], writes=[tf])
        b.op("act", lambda e, dst=dst: e.activation(out=dst[:], in_=tf[:], func=AF.Sin, bias=nb[:], scale=TWO_PI_S),
             reads=[tf, nb], writes=[dst])
    return cos, sin


def rmsnorm_transpose(b, xt, gt, hbf, hT, pT, ss, junk):
    idt = b.ident()
    b.op("act", lambda e: e.activation(out=junk[:], in_=xt[:], func=AF.Square, accum_out=ss[:, 0:1]),
         reads=[xt], writes=[junk, ss])
    rstd_from_ss(b, ss, 1, 1.0 / D)
    b.op("dve", lambda e: e.scalar_tensor_tensor(out=hbf[:], in0=xt[:], scalar=ss[:, 0:1], in1=gt[:],
                                                 op0=ALU.mult, op1=ALU.mult),
         reads=[xt, ss, gt], writes=[hbf])
    for kc in range(8):
        b.op("pe", lambda e, kc=kc: e.transpose(out=pT[:, kc * 128:(kc + 1) * 128],
                                                in_=hbf[:, kc * 128:(kc + 1) * 128], identity=idt[:]),
             reads=[hbf, idt], writes=[pT])
    b.op("dve", lambda e: e.tensor_copy(out=hT[:].rearrange("p a b -> p (a b)"), in_=pT[:]),
         reads=[pT], writes=[hT])


def headnorm_rope(b, stage, sq, ssq, nh, gain, cos_a, sin_a, outbf, tmp, norm=True):
    s3 = stage[:].rearrange("p (h d) -> p h d", d=64)
    if norm:
        b.op("dve", lambda e: e.tensor_reduce(out=ssq[:, 0:nh], in_=sq[:].rearrange("p (h d) -> p h d", d=64),
                                              axis=AX.X, op=ALU.add), reads=[sq], writes=[ssq])
        rstd_from_ss(b, ssq, nh, 1.0 / 64)
    b.op("dve", lambda e: e.tensor_tensor(out=s3, in0=s3, in1=ssq[:, 0:nh].unsqueeze(2).to_broadcast([128, nh, 64]),
                                          op=ALU.mult), reads=[stage, ssq], writes=[stage])
    if gain is not None:
        b.op("pool", lambda e: e.tensor_tensor(out=stage[:], in0=stage[:], in1=gain[:], op=ALU.mult),
             reads=[stage, gain], writes=[stage])
    b.op("act", lambda e: e.activation(out=outbf[:], in_=stage[:], func=AF.Copy), reads=[stage], writes=[outbf])
    o3 = outbf[:].rearrange("p (h d) -> p h d", d=64)
    x1 = s3[:, :, 0:8]
    x2 = s3[:, :, 8:16]
    cb = cos_a.unsqueeze(1).to_broadcast([128, nh, 8])
    sb_ = sin_a.unsqueeze(1).to_broadcast([128, nh, 8])
    t = tmp[:].rearrange("p (k h d) -> p k h d", k=4, d=8)
    eng = "pool"
    b.op(eng, lambda e: e.tensor_tensor(out=t[:, 0, 0:nh, :], in0=x1, in1=cb, op=ALU.mult), reads=[stage], writes=[tmp])
    b.op(eng, lambda e: e.tensor_tensor(out=t[:, 1, 0:nh, :], in0=x2, in1=sb_, op=ALU.mult), reads=[stage], writes=[tmp])
    b.op(eng, lambda e: e.tensor_tensor(out=t[:, 2, 0:nh, :], in0=x2, in1=cb, op=ALU.mult), reads=[stage], writes=[tmp])
    b.op(eng, lambda e: e.tensor_tensor(out=t[:, 3, 0:nh, :], in0=x1, in1=sb_, op=ALU.mult), reads=[stage], writes=[tmp])
    b.op(eng, lambda e: e.tensor_tensor(out=o3[:, :, 0:8], in0=t[:, 0, 0:nh, :], in1=t[:, 1, 0:nh, :], op=ALU.subtract),
         reads=[tmp], writes=[outbf])
    b.op(eng, lambda e: e.tensor_tensor(out=o3[:, :, 8:16], in0=t[:, 2, 0:nh, :], in1=t[:, 3, 0:nh, :], op=ALU.add),
         reads=[tmp], writes=[outbf])


def phase_diff_proj(b, io):
    idt = b.ident()
    pos_t = b.sb([128, NS], I32, "pos")
    b.op("sp", lambda e: e.dma_start(out=pos_t[:], in_=io["pos"][:]), writes=[pos_t], dma="ld_pos")
    gmix = b.sb([128, D], F32, "gmix")
    b.op("sp", lambda e: e.dma_start(out=gmix[:], in_=io["gmix"][:]), writes=[gmix], dma="ld_gmix")
    gqk = b.sb([128, 2048], F32, "gqk")
    b.op("sp", lambda e: e.dma_start(out=gqk[:], in_=io["gqk"][:]), writes=[gqk], dma="ld_gqk")
    b.op("pool", lambda e: e.tensor_scalar(out=gqk[:, 0:1024], in0=gqk[:, 0:1024], scalar1=0.125, scalar2=None,
                                           op0=ALU.mult), reads=[gqk], writes=[gqk])
    w = b.sb([128, 8, 3072], BF16, "w_in")
    for kc in range(8):
        for hf in range(3):
            b.op("pool", lambda e, kc=kc, hf=hf: e.dma_start(
                out=w[:, kc, hf * 1024:(hf + 1) * 1024],
                in_=io["w_in"][kc * 128:(kc + 1) * 128, hf * 1024:(hf + 1) * 1024]),
                writes=[w], dma="ld_w")
    cos, sin = rope_tables(b, pos_t)

    xts = [b.sb([128, D], F32, f"xt{i}") for i in range(2)]
    junk = b.sb([128, D], BF16, "junk")
    ss = b.sb([128, 1], F32, "ss")
    hbf = b.sb([128, D], BF16, "hbf")
    hT = b.sb([128, 8, 128], BF16, "hT")
    pT = [b.banks16[0], b.banks16[1]]
    pY = [b.banks[2 + i] for i in range(4)]
    stage = b.sb([128, 2048], F32, "stage")
    sq = b.sb([128, 2048], F32, "sq")
    ssq = b.sb([128, 32], F32, "ssq")
    tmp = b.sb([128, 4 * 32 * 8], F32, "ropetmp")
    qkbf = b.sb([128, 2048], BF16, "qkbf")
    qkT = [b.sb([128, 16, 128], BF16, f"qkT{i}") for i in range(2)]
    vaug = [b.sb([128, 8, 129], BF16, f"vaug{i}") for i in range(2)]
    for i in range(2):
        b.op("pool", lambda e, i=i: e.memset(vaug[i][:], 1.0), writes=[vaug[i]])

    for a in range(NS):
        xt = xts[a % 2]
        b.op("sp", lambda e, a=a, xt=xt: e.dma_start(out=xt[:], in_=io["x"][a * 128:(a + 1) * 128, :]),
             writes=[xt], dma=f"ld_x{a % 2}")
        rmsnorm_transpose(b, xt, gmix, hbf, hT, pT[0], ss, junk)
        for n in range(6):
            py = pY[n % 4]
            for kc in range(8):
                b.op("pe", lambda e, n=n, kc=kc, py=py: e.matmul(py[:], lhsT=hT[:, kc, :],
                                                                   rhs=w[:, kc, n * 512:(n + 1) * 512],
                                                                   start=(kc == 0), stop=(kc == 7)),
                     reads=[hT, w], writes=[py])
            if n < 4:
                b.op("act", lambda e, n=n, py=py: e.activation(out=stage[:, n * 512:(n + 1) * 512], in_=py[:], func=AF.Copy),
                     reads=[py], writes=[stage])
                b.op("act", lambda e, n=n, py=py: e.activation(out=sq[:, n * 512:(n + 1) * 512], in_=py[:], func=AF.Square),
                     reads=[py], writes=[sq])
            else:
                va = vaug[a % 2]
                b.op("dve", lambda e, n=n, py=py, va=va: e.tensor_copy(
                    out=va[:, (n - 4) * 4:(n - 4) * 4 + 4, 0:128], in_=py[:].rearrange("p (h d) -> p h d", d=128)),
                    reads=[py], writes=[va])
        va = vaug[a % 2]
        b.store("sp", lambda e, a=a, va=va: e.dma_start(out=io["V"][a], in_=va[:].rearrange("p h d -> p (h d)")),
                reads=[va], dma=f"st_v{a % 2}")
        headnorm_rope(b, stage, sq, ssq, 32, gqk, cos[:, a, :], sin[:, a, :], qkbf, tmp)
        qt = qkT[a % 2]
        for i in range(16):
            pt = pT[1] if i < 8 else pT[0]
            b.op("pe", lambda e, i=i, pt=pt: e.transpose(out=pt[:, (i % 8) * 128:(i % 8 + 1) * 128],
                                                         in_=qkbf[:, i * 128:(i + 1) * 128], identity=idt[:]),
                 reads=[qkbf, idt], writes=[pt])
            if i % 8 == 7:
                b.op("dve", lambda e, i=i, pt=pt, qt=qt: e.tensor_copy(
                    out=qt[:, (i // 8) * 8:(i // 8) * 8 + 8, :].rearrange("p a b -> p (a b)"), in_=pt[:]),
                    reads=[pt], writes=[qt])
        b.store("sp", lambda e, a=a, qt=qt: e.dma_start(out=io["QT"][a], in_=qt[:, 0:8, :].rearrange("p a b -> p (a b)")),
                reads=[qt], dma=f"st_q{a % 2}")
        b.store("sp", lambda e, a=a, qt=qt: e.dma_start(out=io["KT"][a], in_=qt[:, 8:16, :].rearrange("p a b -> p (a b)")),
                reads=[qt], dma=f"st_k{a % 2}")


def phase_diff_attn(b, io, o_tok):
    LAM_INIT = 0.2
    qT = b.sb([128, NS, 1024], BF16, "qT")
    for a in range(NS):
        b.op("sp", lambda e, a=a: e.dma_start(out=qT[:, a, :], in_=io["QT"][a]), writes=[qT], dma="ld_qT")
    maskT = b.sb([128, 512], BF16, "maskT")
    b.op("sp", lambda e: e.dma_start(out=maskT[:], in_=io["maskT"][:]), writes=[maskT], dma="ld_mask")
    lam = b.sb([128, 256], F32, "lam")
    b.op("sp", lambda e: e.dma_start(out=lam[:], in_=io["lam"][:]), writes=[lam], dma="ld_lam")
    gsub = b.sb([128, 128], F32, "gsub")
    b.op("sp", lambda e: e.dma_start(out=gsub[:], in_=io["gsub"][:]), writes=[gsub], dma="ld_gsub")
    b.op("pool", lambda e: e.tensor_scalar(out=gsub[:], in0=gsub[:], scalar1=1.0 - LAM_INIT, scalar2=None,
                                           op0=ALU.mult), reads=[gsub], writes=[gsub])
    lprod = b.sb([128, 128], F32, "lprod")
    l2 = b.sb([128, 2], F32, "l2")
    neglam = b.sb([128, 1], F32, "neglam")
    l4 = lam[:].rearrange("p (a b d) -> p a b d", a=2, b=2)
    b.op("dve", lambda e: e.tensor_tensor(out=lprod[:].rearrange("p (a d) -> p a d", a=2), in0=l4[:, :, 0, :],
                                          in1=l4[:, :, 1, :], op=ALU.mult), reads=[lam], writes=[lprod])
    b.op("dve", lambda e: e.tensor_reduce(out=l2[:], in_=lprod[:].rearrange("p (a d) -> p a d", a=2), axis=AX.X,
                                          op=ALU.add), reads=[lprod], writes=[l2])
    b.op("act", lambda e: e.activation(out=l2[:], in_=l2[:], func=AF.Exp), reads=[l2], writes=[l2])
    b.op("dve", lambda e: e.tensor_tensor(out=neglam[:], in0=l2[:, 1:2], in1=l2[:, 0:1], op=ALU.subtract),
         reads=[l2], writes=[neglam])
    b.op("dve", lambda e: e.tensor_scalar(out=neglam[:], in0=neglam[:], scalar1=-LAM_INIT, scalar2=None, op0=ALU.add),
         reads=[neglam], writes=[neglam])

    ktb = [b.sb([128, 8192], BF16, f"ktb{i}") for i in range(2)]
    vb = [b.sb([128, 64, 129], BF16, f"vb{i}") for i in range(2)]
    pS = [[b.banks[c * 2 + i] for i in range(2)] for c in range(2)]
    pA = [[b.banks[4 + c * 2 + i] for i in range(2)] for c in range(2)]
    pT = [[b.sb([128, 512], BF16, f"pTs{c}{i}") for i in range(2)] for c in range(2)]
    rec = [b.sb([128, 2], F32, f"rec{i}") for i in range(2)]
    o32 = [b.sb([128, 128], F32, f"o32{i}") for i in range(2)]
    oj = b.sb([128, 128], BF16, "ojunk")
    ss1 = [b.sb([128, 1], F32, f"ss1{i}") for i in range(2)]

    units = [(h, a, blk) for h in range(8) for a in range(NS) for blk in range(a + 1)]

    def load_head(h):
        kb, vv = ktb[h % 2], vb[h % 2]
        for part in range(4):
            b.op("sp", lambda e, h=h, kb=kb, part=part: e.dma_start(
                out=kb[:, part * 2048:(part + 1) * 2048], in_=io["KTall"][h][:, part * 2048:(part + 1) * 2048]),
                writes=[kb], dma=f"ld_kt{h % 2}")
            b.op("sp", lambda e, h=h, vv=vv, part=part: e.dma_start(
                out=vv[:, part * 16:(part + 1) * 16, :].rearrange("p a b -> p (a b)"),
                in_=io["Vall"][h][:, part * 16 * 129:(part + 1) * 16 * 129]),
                writes=[vv], dma=f"ld_v{h % 2}")

    def qk(u, n):
        h, a, blk = u
        kb = ktb[h % 2]
        for c in range(2):
            ps = pS[c][n % 2]
            for i in range(4):
                kt = 4 * blk + i
                b.op("pe", lambda e, c=c, i=i, kt=kt, ps=ps, kb=kb, a=a, h=h: e.matmul(
                    ps[:, i * 128:(i + 1) * 128], lhsT=kb[64 * c:64 * c + 64, kt * 128:(kt + 1) * 128],
                    rhs=qT[64 * c:64 * c + 64, a, h * 128:(h + 1) * 128], start=True, stop=True),
                    reads=[kb, qT], writes=[ps])

    def softmax_pv(u, n):
        h, a, blk = u
        vv = vb[h % 2]
        for c in range(2):
            ps = pS[c][n % 2]
            pt = pT[c][n % 2]
            acc = pA[c][a % 2]
            b.op("act", lambda e, ps=ps, pt=pt: e.activation(out=pt[:], in_=ps[:], func=AF.Exp),
                 reads=[ps], writes=[pt])
            if blk == a:
                b.op("pool", lambda e, pt=pt: e.tensor_tensor(out=pt[:], in0=pt[:], in1=maskT[:], op=ALU.mult),
                     reads=[pt, maskT], writes=[pt])
            for i in range(4):
                kt = 4 * blk + i
                b.op("pe", lambda e, i=i, kt=kt, pt=pt, acc=acc, vv=vv, blk=blk, a=a: e.matmul(
                    acc[:, 0:129], lhsT=pt[:, i * 128:(i + 1) * 128], rhs=vv[:, kt, :],
                    start=(blk == 0 and i == 0), stop=(blk == a and i == 3)),
                    reads=[pt, vv], writes=[acc])
        if blk == a:
            evac(h, a)

    def evac(h, a):
        k = a % 2
        a0, a1 = pA[0][k], pA[1][k]
        r, o, s = rec[k], o32[k], ss1[k]
        b.op("dve", lambda e: e.reciprocal(out=r[:, 0:1], in_=a0[:, 128:129]), reads=[a0], writes=[r])
        b.op("dve", lambda e: e.reciprocal(out=r[:, 1:2], in_=a1[:, 128:129]), reads=[a1], writes=[r])
        b.op("dve", lambda e: e.tensor_tensor(out=r[:, 1:2], in0=r[:, 1:2], in1=neglam[:], op=ALU.mult),
             reads=[r, neglam], writes=[r])
        b.op("dve", lambda e: e.tensor_scalar(out=o[:], in0=a0[:, 0:128], scalar1=r[:, 0:1], scalar2=None, op0=ALU.mult),
             reads=[a0, r], writes=[o])
        b.op("dve", lambda e: e.scalar_tensor_tensor(out=o[:], in0=a1[:, 0:128], scalar=r[:, 1:2], in1=o[:],
                                                     op0=ALU.mult, op1=ALU.add), reads=[a1, r, o], writes=[o])
        b.op("act", lambda e: e.activation(out=oj[:], in_=o[:], func=AF.Square, accum_out=s[:, 0:1]),
             reads=[o], writes=[oj, s])
        rstd_from_ss(b, s, 1, 1.0 / 128)
        ot = o_tok[a]
        b.op("dve", lambda e: e.scalar_tensor_tensor(out=ot[:, h * 128:(h + 1) * 128], in0=o[:], scalar=s[:, 0:1],
                                                     in1=gsub[:], op0=ALU.mult, op1=ALU.mult),
             reads=[o, s, gsub], writes=[ot])

    load_head(0)
    qk(units[0], 0)
    for n, u in enumerate(units):
        if u[1] == 0 and u[2] == 0 and u[0] + 1 < 8:
            load_head(u[0] + 1)
        if n + 1 < len(units):
            qk(units[n + 1], n + 1)
        softmax_pv(u, n)


def load_w_bf16(b, wt, src, nk, ncols, key, colchunk=1024):
    for kc in range(nk):
        for c0 in range(0, ncols, colchunk):
            c1 = min(ncols, c0 + colchunk)
            b.op("pool", lambda e, kc=kc, c0=c0, c1=c1: e.dma_start(
                out=wt[:, kc, c0:c1], in_=src[kc * 128:(kc + 1) * 128, c0:c1]), writes=[wt], dma=key)


def phase_post_attn(b, io, o_tok, x_res, h2T, wout_ap, gmlp_ap, xsrc, xdeps=None):
    idt = b.ident()
    m = b.mark()
    wout = b.sb([128, 8, 1024], BF16, "wout")
    load_w_bf16(b, wout, wout_ap, 8, 1024, "ld_wout")
    gm = b.sb([128, D], F32, "gmlp")
    b.op("sp", lambda e: e.dma_start(out=gm[:], in_=gmlp_ap), writes=[gm], dma="ld_gmlp")
    oT = [b.sb([128, 8, 128], BF16, f"oT{i}") for i in range(2)]
    junk = b.sb([128, D], BF16, "junk2")
    ss = [b.sb([128, 1], F32, f"ss2{i}") for i in range(2)]
    hbf = [b.sb([128, D], BF16, f"hbf2{i}") for i in range(2)]
    for a in range(NS):
        xr = x_res[a]
        b.op("sp", lambda e, a=a, xr=xr: e.dma_start(out=xr[:], in_=xsrc[a * 128:(a + 1) * 128, :]),
             reads=([xdeps[a]] if xdeps else []), writes=[xr], dma="ld_xres")
        pt = b.banks16[a % 2]
        ot = oT[a % 2]
        for kc in range(8):
            b.op("pe", lambda e, kc=kc, pt=pt, a=a: e.transpose(out=pt[:, kc * 128:(kc + 1) * 128],
                                                                 in_=o_tok[a][:, kc * 128:(kc + 1) * 128], identity=idt[:]),
                 reads=[o_tok[a], idt], writes=[pt])
        b.op("act", lambda e, pt=pt, ot=ot: e.activation(out=ot[:].rearrange("p a b -> p (a b)"), in_=pt[:], func=AF.Copy),
             reads=[pt], writes=[ot])
        for n in range(2):
            py = b.banks[2 + (2 * a + n) % 4]
            for kc in range(8):
                b.op("pe", lambda e, kc=kc, n=n, py=py, ot=ot: e.matmul(py[:], lhsT=ot[:, kc, :],
                                                                         rhs=wout[:, kc, n * 512:(n + 1) * 512],
                                                                         start=(kc == 0), stop=(kc == 7)),
                     reads=[ot, wout], writes=[py])
            b.op("dve", lambda e, n=n, py=py, xr=xr: e.tensor_tensor(out=xr[:, n * 512:(n + 1) * 512],
                                                                     in0=xr[:, n * 512:(n + 1) * 512], in1=py[:], op=ALU.add),
                 reads=[xr, py], writes=[xr])
        rms_to_hT(b, xr, gm, hbf[a % 2], h2T, a, b.banks16[6 + a % 2], ss[a % 2], junk)
    return m


def rms_to_hT(b, xr, gm, hbf, h2T, a, pt, ss, junk):
    idt = b.ident()
    b.op("act", lambda e: e.activation(out=junk[:], in_=xr[:], func=AF.Square, accum_out=ss[:, 0:1]),
         reads=[xr], writes=[junk, ss])
    rstd_from_ss(b, ss, 1, 1.0 / D)
    b.op("dve", lambda e: e.scalar_tensor_tensor(out=hbf[:], in0=xr[:], scalar=ss[:, 0:1], in1=gm[:],
                                                 op0=ALU.mult, op1=ALU.mult), reads=[xr, ss, gm], writes=[hbf])
    for kc in range(8):
        b.op("pe", lambda e, kc=kc: e.transpose(out=pt[:, kc * 128:(kc + 1) * 128],
                                                in_=hbf[:, kc * 128:(kc + 1) * 128], identity=idt[:]),
             reads=[hbf, idt], writes=[pt])
    b.op("act", lambda e: e.activation(out=h2T[:, :, a * 128:(a + 1) * 128],
                                       in_=pt[:].rearrange("p (k t) -> p k t", k=8), func=AF.Copy),
         reads=[pt], writes=[h2T])


def phase_mlp(b, x_res, h2T, w1_ap, w2_ap):
    NFC = 8
    w1c = [b.sb([128, 8, 512], BF16, f"w1c{i}") for i in range(2)]
    w2c = [b.sb([128, 4, 1024], BF16, f"w2c{i}") for i in range(2)]
    rbuf = [b.sb([128, 512], F32, f"rbuf{i}") for i in range(2)]
    uT = [b.sb([128, 4, 512], BF16, f"uT{i}") for i in range(2)]
    pu = [b.banks[0], b.banks[1]]
    po = [b.banks[2 + i] for i in range(4)]

    def load_chunk(fc):
        w1, w2 = w1c[fc % 2], w2c[fc % 2]
        for kc in range(8):
            b.op("pool", lambda e, kc=kc, fc=fc, w1=w1: e.dma_start(
                out=w1[:, kc, :], in_=w1_ap[kc * 128:(kc + 1) * 128, fc * 512:(fc + 1) * 512]),
                writes=[w1], dma=f"ld_w1{fc % 2}")
        for ft in range(4):
            b.op("pool", lambda e, ft=ft, fc=fc, w2=w2: e.dma_start(
                out=w2[:, ft, :], in_=w2_ap[fc * 512 + ft * 128:fc * 512 + (ft + 1) * 128, :]),
                writes=[w2], dma=f"ld_w2{fc % 2}")

    steps = [(fc, tg) for fc in range(NFC) for tg in range(4)]
    cnt = {"u": 0, "o": 0}

    def stage_u(fc, tg):
        w1 = w1c[fc % 2]
        ut = uT[(fc * 4 + tg) % 2]
        for ft in range(4):
            p = pu[cnt["u"] % 2]
            r = rbuf[cnt["u"] % 2]
            cnt["u"] += 1
            for kc in range(8):
                b.op("pe", lambda e, kc=kc, ft=ft, p=p, w1=w1, tg=tg: e.matmul(
                    p[:], lhsT=w1[:, kc, ft * 128:(ft + 1) * 128], rhs=h2T[:, kc, tg * 512:(tg + 1) * 512],
                    start=(kc == 0), stop=(kc == 7)), reads=[w1, h2T], writes=[p])
            b.op("act", lambda e, p=p, r=r: e.activation(out=r[:], in_=p[:], func=AF.Relu), reads=[p], writes=[r])
            b.op("pool", lambda e, r=r, ut=ut, ft=ft: e.tensor_tensor(out=ut[:, ft, :], in0=r[:], in1=r[:], op=ALU.mult),
                 reads=[r], writes=[ut])

    def stage_o(fc, tg):
        w2 = w2c[fc % 2]
        ut = uT[(fc * 4 + tg) % 2]
        for tt in range(4):
            xr = x_res[tg * 4 + tt]
            for ch in range(2):
                p = po[cnt["o"] % 4]
                cnt["o"] += 1
                for ft in range(4):
                    b.op("pe", lambda e, ft=ft, tt=tt, ch=ch, p=p, ut=ut, w2=w2: e.matmul(
                        p[:], lhsT=ut[:, ft, tt * 128:(tt + 1) * 128], rhs=w2[:, ft, ch * 512:(ch + 1) * 512],
                        start=(ft == 0), stop=(ft == 3)), reads=[ut, w2], writes=[p])
                b.op("dve", lambda e, ch=ch, p=p, xr=xr: e.tensor_tensor(
                    out=xr[:, ch * 512:(ch + 1) * 512], in0=xr[:, ch * 512:(ch + 1) * 512], in1=p[:], op=ALU.add),
                    reads=[xr, p], writes=[xr])

    load_chunk(0)
    stage_u(*steps[0])
    for i, (fc, tg) in enumerate(steps):
        if tg == 0 and fc + 1 < NFC:
            load_chunk(fc + 1)
        if i + 1 < len(steps):
            stage_u(*steps[i + 1])
        stage_o(fc, tg)


IDX_SCALE = (8 ** -0.5) * (64 ** -0.5)


def phase_dsa_proj(b, io, x_res, cos, sin):
    idt = b.ident()
    gmix = b.sb([128, D], F32, "gmix1")
    b.op("sp", lambda e: e.dma_start(out=gmix[:], in_=io["gmix1"][:]), writes=[gmix], dma="ld_gmix1")
    gqk = b.sb([128, 2048], F32, "gqk2")
    b.op("sp", lambda e: e.dma_start(out=gqk[:], in_=io["gqk2"][:]), writes=[gqk], dma="ld_gqk2")
    b.op("pool", lambda e: e.tensor_scalar(out=gqk[:, 0:1024], in0=gqk[:, 0:1024], scalar1=0.125, scalar2=None,
                                           op0=ALU.mult), reads=[gqk], writes=[gqk])
    gcq = b.sb([128, 256], F32, "gcq")
    b.op("sp", lambda e: e.dma_start(out=gcq[:], in_=io["gcq"][:]), writes=[gcq], dma="ld_gcq")
    w = b.sb([128, 8, 2376], BF16, "w_in2")
    load_w_bf16(b, w, io["w_in2"], 8, 2376, "ld_w2in", colchunk=792)
    wuq = b.sb([128, 2, 1024], BF16, "wuq")
    load_w_bf16(b, wuq, io["w_uq"], 2, 1024, "ld_wuq")
    wuqi = b.sb([128, 2, 512], BF16, "wuqi")
    load_w_bf16(b, wuqi, io["w_uqi"], 2, 512, "ld_wuqi")

    junk = b.sb([128, D], BF16, "junk3")
    ss = b.sb([128, 1], F32, "ss3")
    hbf = b.sb([128, D], BF16, "hbf3")
    hT = b.sb([128, 8, 128], BF16, "hT3")
    stage = b.sb([128, 2048], F32, "stage3")
    sq = b.sb([128, 2048], F32, "sq3")
    ssq = b.sb([128, 32], F32, "ssq3")
    tmp = b.sb([128, 4 * 32 * 8], F32, "ropetmp3")
    qkbf = b.sb([128, 2048], BF16, "qkbf3")
    qkT = [b.sb([128, 16, 128], BF16, f"qkT3{i}") for i in range(2)]
    vaug = [b.sb([128, 16, 65], BF16, f"vaug3{i}") for i in range(2)]
    for i in range(2):
        b.op("pool", lambda e, i=i: e.memset(vaug[i][:], 1.0), writes=[vaug[i]])
    cqs = b.sb([128, 256], F32, "cqs")
    ssc = b.sb([128, 1], F32, "ssc")
    cqbf = b.sb([128, 256], BF16, "cqbf")
    cqT = b.sb([128, 2, 128], BF16, "cqT")
    kis = b.sb([128, 64], F32, "kis")
    ksq = b.sb([128, 64], F32, "ksq")
    kss = b.sb([128, 1], F32, "kss")
    kibf = b.sb([128, 128], BF16, "kibf")
    kiT = [b.sb([128, 128], BF16, f"kiT{i}") for i in range(2)]
    wi = b.sb([128, 8], F32, "wi")
    sgn = [b.sb([128, 8], F32, f"sgn{i}") for i in range(2)]
    aw = b.sb([128, 8], F32, "aw")
    qis = b.sb([128, 512], F32, "qis")
    qibf = b.sb([128, 512], BF16, "qibf")
    qiT = [b.sb([128, 4, 128], BF16, f"qiT{i}") for i in range(2)]
    nbank = [0]

    def bank():
        nbank[0] += 1
        return b.banks[2 + nbank[0] % 4]

    def proj(py, lhs, nk, rhs_fn, ncol):
        for kc in range(nk):
            b.op("pe", lambda e, kc=kc: e.matmul(py[:, 0:ncol], lhsT=lhs[:, kc, :], rhs=rhs_fn(kc),
                                                 start=(kc == 0), stop=(kc == nk - 1)), reads=[lhs, w, wuq, wuqi], writes=[py])

    for a in range(NS):
        xr = x_res[a]
        rmsnorm_transpose(b, xr, gmix, hbf, hT, b.banks16[0], ss, junk)
        py = bank()
        proj(py, hT, 8, lambda kc: w[:, kc, 0:256], 256)
        b.op("act", lambda e, py=py: e.activation(out=cqs[:], in_=py[:, 0:256], func=AF.Copy), reads=[py], writes=[cqs])
        b.op("act", lambda e, py=py: e.activation(out=junk[:, 0:256], in_=py[:, 0:256], func=AF.Square, accum_out=ssc[:, 0:1]),
             reads=[py], writes=[junk, ssc])
        rstd_from_ss(b, ssc, 1, 1.0 / 256)
        b.op("dve", lambda e: e.scalar_tensor_tensor(out=cqbf[:], in0=cqs[:], scalar=ssc[:, 0:1], in1=gcq[:],
                                                     op0=ALU.mult, op1=ALU.mult), reads=[cqs, ssc, gcq], writes=[cqbf])
        p6 = b.banks16[6]
        for kc in range(2):
            b.op("pe", lambda e, kc=kc: e.transpose(out=p6[:, kc * 128:(kc + 1) * 128], in_=cqbf[:, kc * 128:(kc + 1) * 128],
                                                    identity=idt[:]), reads=[cqbf, idt], writes=[p6])
        b.op("dve", lambda e: e.tensor_copy(out=cqT[:].rearrange("p a b -> p (a b)"), in_=p6[:, 0:256]),
             reads=[p6], writes=[cqT])
        for n in range(2):
            py = bank()
            proj(py, hT, 8, lambda kc, n=n: w[:, kc, 256 + n * 512:256 + (n + 1) * 512], 512)
            b.op("act", lambda e, py=py, n=n: e.activation(out=stage[:, 1024 + n * 512:1024 + (n + 1) * 512], in_=py[:], func=AF.Copy),
                 reads=[py], writes=[stage])
            b.op("act", lambda e, py=py, n=n: e.activation(out=sq[:, 1024 + n * 512:1024 + (n + 1) * 512], in_=py[:], func=AF.Square),
                 reads=[py], writes=[sq])
        va = vaug[a % 2]
        for n in range(2):
            py = bank()
            proj(py, hT, 8, lambda kc, n=n: w[:, kc, 1280 + n * 512:1280 + (n + 1) * 512], 512)
            b.op("dve", lambda e, py=py, n=n, va=va: e.tensor_copy(out=va[:, n * 8:(n + 1) * 8, 0:64],
                                                                   in_=py[:].rearrange("p (h d) -> p h d", d=64)),
                 reads=[py], writes=[va])
        b.store("sp", lambda e, a=a, va=va: e.dma_start(out=io["V2"][a], in_=va[:].rearrange("p h d -> p (h d)")),
                reads=[va], dma=f"st_v2{a % 2}")
        py = bank()
        proj(py, hT, 8, lambda kc: w[:, kc, 2304:2376], 72)
        b.op("act", lambda e, py=py: e.activation(out=kis[:], in_=py[:, 0:64], func=AF.Copy), reads=[py], writes=[kis])
        b.op("act", lambda e, py=py: e.activation(out=ksq[:], in_=py[:, 0:64], func=AF.Square), reads=[py], writes=[ksq])
        b.op("dve", lambda e, py=py: e.tensor_copy(out=wi[:], in_=py[:, 64:72]), reads=[py], writes=[wi])
        for n in range(2):
            py = bank()
            proj(py, cqT, 2, lambda kc, n=n: wuq[:, kc, n * 512:(n + 1) * 512], 512)
            b.op("act", lambda e, py=py, n=n: e.activation(out=stage[:, n * 512:(n + 1) * 512], in_=py[:], func=AF.Copy),
                 reads=[py], writes=[stage])
            b.op("act", lambda e, py=py, n=n: e.activation(out=sq[:, n * 512:(n + 1) * 512], in_=py[:], func=AF.Square),
                 reads=[py], writes=[sq])
        py = bank()
        proj(py, cqT, 2, lambda kc: wuqi[:, kc, :], 512)
        b.op("act", lambda e, py=py: e.activation(out=qis[:], in_=py[:], func=AF.Copy), reads=[py], writes=[qis])
        headnorm_rope(b, stage, sq, ssq, 32, gqk, cos[:, a, :], sin[:, a, :], qkbf, tmp)
        qt = qkT[a % 2]
        for i in range(16):
            pt = b.banks16[1] if i < 8 else b.banks16[0]
            b.op("pe", lambda e, i=i, pt=pt: e.transpose(out=pt[:, (i % 8) * 128:(i % 8 + 1) * 128],
                                                         in_=qkbf[:, i * 128:(i + 1) * 128], identity=idt[:]),
                 reads=[qkbf, idt], writes=[pt])
            if i % 8 == 7:
                b.op("dve", lambda e, i=i, pt=pt, qt=qt: e.tensor_copy(
                    out=qt[:, (i // 8) * 8:(i // 8) * 8 + 8, :].rearrange("p a b -> p (a b)"), in_=pt[:]),
                    reads=[pt], writes=[qt])
        b.store("sp", lambda e, a=a, qt=qt: e.dma_start(out=io["QT2"][a], in_=qt[:, 0:8, :].rearrange("p a b -> p (a b)")),
                reads=[qt], dma=f"st_q2{a % 2}")
        b.store("sp", lambda e, a=a, qt=qt: e.dma_start(out=io["KT2"][a], in_=qt[:, 8:16, :].rearrange("p a b -> p (a b)")),
                reads=[qt], dma=f"st_k2{a % 2}")
        ki_half = T(kibf.t[:, 0:64], kibf.d)
        headnorm_rope(b, kis, ksq, kss, 1, None, cos[:, a, :], sin[:, a, :], ki_half, tmp)
        b.op("pool", lambda e: e.tensor_copy(out=kibf[:, 64:128], in_=kibf[:, 0:64]), reads=[kibf], writes=[kibf])
        p7 = b.banks16[7]
        b.op("pe", lambda e: e.transpose(out=p7[:, 0:128], in_=kibf[:], identity=idt[:]), reads=[kibf, idt], writes=[p7])
        kt_ = kiT[a % 2]
        b.op("dve", lambda e, kt_=kt_: e.tensor_copy(out=kt_[:], in_=p7[:, 0:128]), reads=[p7], writes=[kt_])
        b.store("sp", lambda e, a=a, kt_=kt_: e.dma_start(out=io["KI"][a], in_=kt_[:]), reads=[kt_], dma=f"st_ki{a % 2}")
        sg = sgn[a % 2]
        b.op("act", lambda e, sg=sg: e.activation(out=sg[:], in_=wi[:], func=AF.Sign), reads=[wi], writes=[sg])
        b.op("dve", lambda e, sg=sg: e.scalar_tensor_tensor(out=aw[:], in0=wi[:], scalar=IDX_SCALE, in1=sg[:],
                                                           op0=ALU.mult, op1=ALU.mult), reads=[wi, sg], writes=[aw])
        b.store("sp", lambda e, a=a, sg=sg: e.dma_start(out=io["SG"][a], in_=sg[:]), reads=[sg], dma=f"st_sg{a % 2}")
        headnorm_rope(b, qis, None, aw, 8, None, cos[:, a, :], sin[:, a, :], qibf, tmp, norm=False)
        qi_ = qiT[a % 2]
        for i in range(4):
            b.op("pe", lambda e, i=i: e.transpose(out=p7[:, 256 + i * 128:256 + (i + 1) * 128],
                                                  in_=qibf[:, i * 128:(i + 1) * 128], identity=idt[:]),
                 reads=[qibf, idt], writes=[p7])
        b.op("dve", lambda e, qi_=qi_: e.tensor_copy(out=qi_[:].rearrange("p a b -> p (a b)"), in_=p7[:, 256:768]),
             reads=[p7], writes=[qi_])
        b.store("sp", lambda e, a=a, qi_=qi_: e.dma_start(out=io["QI"][a], in_=qi_[:].rearrange("p a b -> p (a b)")),
                reads=[qi_], dma=f"st_qi{a % 2}")


NIT = 22
TOPK = 256


def phase_dsa_attn(b, io, o_tok):
    idt = b.ident()
    kia = b.sb([128, 8192], BF16, "kiall")
    for part in range(4):
        b.op("sp", lambda e, part=part: e.dma_start(out=kia[:, part * 2048:(part + 1) * 2048],
                                                    in_=io["KIall"][:, part * 2048:(part + 1) * 2048]),
             writes=[kia], dma="ld_kia")
    negm = b.sb([128, 512], F32, "negm")
    b.op("sp", lambda e: e.dma_start(out=negm[:], in_=io["negmask"][:]), writes=[negm], dma="ld_negm")
    cW = b.sb([128, NIT], F32, "cW")
    for i in range(NIT):
        b.op("pool", lambda e, i=i: e.memset(cW[:, i:i + 1], 2.0 ** (-i)), writes=[cW])
    Ib = b.sb([128, 8192], F32, "Ibuf")
    Mq = b.sb([128, 8192], BF16, "Mq")
    MT = b.sb([128, 64, 128], BF16, "MT")
    ktp = [b.sb([128, 8192], BF16, f"ktp{i}") for i in range(2)]
    vp = [b.sb([128, 64, 130], BF16, f"vp{i}") for i in range(2)]
    qTa = [b.sb([128, 8, 128], BF16, f"qTa{i}") for i in range(2)]
    qiTa = [b.sb([128, 4, 128], BF16, f"qiTa{i}") for i in range(2)]
    sgn = [b.sb([128, 8], F32, f"sgna{i}") for i in range(2)]
    tb = [b.sb([128, 512], F32, f"tb{i}") for i in range(2)]
    pT = [b.sb([128, 512], BF16, f"pTd{i}") for i in range(2)]
    m1 = b.sb([128, 1], F32, "bm1")
    lo = b.sb([128, 1], F32, "blo")
    mid = b.sb([128, 1], F32, "bmid")
    cnt = b.sb([128, 1], F32, "bcnt")
    g = b.sb([128, 1], F32, "bg")
    W = b.sb([128, NIT], F32, "bW")
    rec = [b.sb([128, 1], F32, f"recd{i}") for i in range(2)]
    pS = [b.banks[0], b.banks[1]]
    pA = [b.banks[2], b.banks[3]]
    pI = [b.banks[4], b.banks[5]]
    pM = [b.banks16[6], b.banks16[7]]
    ctr = {"i": 0, "s": 0, "acc": 0, "kv": 0}

    def load_slot_small(a):
        b.op("sp", lambda e, a=a: e.dma_start(out=qTa[a % 2][:].rearrange("p a b -> p (a b)"), in_=io["QT2"][a]),
             writes=[qTa[a % 2]], dma=f"ld_qTa{a % 2}")
        b.op("sp", lambda e, a=a: e.dma_start(out=qiTa[a % 2][:].rearrange("p a b -> p (a b)"), in_=io["QI"][a]),
             writes=[qiTa[a % 2]], dma=f"ld_qiTa{a % 2}")
        b.op("sp", lambda e, a=a: e.dma_start(out=sgn[a % 2][:], in_=io["SG"][a]), writes=[sgn[a % 2]], dma=f"ld_sgn{a % 2}")

    def load_kv(a, hp):
        k = ctr["kv"] % 2
        ctr["kv"] += 1
        nv = 512 * (a + 1)
        nt = 4 * (a + 1)
        b.op("sp", lambda e, k=k, hp=hp, nv=nv: e.dma_start(out=ktp[k][:, 0:nv], in_=io["KT2all"][hp][:, 0:nv]),
             writes=[ktp[k]], dma=f"ld_ktp{k}")
        b.op("sp", lambda e, k=k, hp=hp, nt=nt: e.dma_start(out=vp[k][:, 0:nt, :].rearrange("p a b -> p (a b)"),
                                                            in_=io["V2all"][hp][:, 0:nt * 130]),
             writes=[vp[k]], dma=f"ld_vp{k}")
        return k

    def indexer(a):
        nb = a + 1
        nv = 512 * nb
        qi, sg = qiTa[a % 2], sgn[a % 2]
        for blk in range(nb):
            for head in range(8):
                hp, hh = head // 2, head % 2
                py = pI[ctr["i"] % 2]
                t = tb[ctr["i"] % 2]
                ctr["i"] += 1
                b.op("pe", lambda e, py=py, hp=hp, hh=hh, blk=blk, qi=qi: e.matmul(
                    py[:], lhsT=qi[64 * hh:64 * hh + 64, hp, :], rhs=kia[64 * hh:64 * hh + 64, blk * 512:(blk + 1) * 512],
                    start=True, stop=True), reads=[qi, kia], writes=[py])
                b.op("act", lambda e, py=py, t=t: e.activation(out=t[:], in_=py[:], func=AF.Relu), reads=[py], writes=[t])
                if head == 0:
                    b.op("dve", lambda e, t=t, blk=blk, sg=sg: e.tensor_scalar(
                        out=Ib[:, blk * 512:(blk + 1) * 512], in0=t[:], scalar1=sg[:, 0:1], scalar2=None, op0=ALU.mult),
                        reads=[t, sg], writes=[Ib])
                else:
                    b.op("dve", lambda e, t=t, blk=blk, sg=sg, head=head: e.scalar_tensor_tensor(
                        out=Ib[:, blk * 512:(blk + 1) * 512], in0=t[:], scalar=sg[:, head:head + 1],
                        in1=Ib[:, blk * 512:(blk + 1) * 512], op0=ALU.mult, op1=ALU.add),
                        reads=[t, sg, Ib], writes=[Ib])
        b.op("dve", lambda e: e.tensor_reduce(out=m1[:], in_=Ib[:, 0:nv], axis=AX.X, op=ALU.max, apply_absolute_value=True),
             reads=[Ib], writes=[m1])
        b.op("dve", lambda e: e.tensor_tensor(out=Ib[:, nv - 512:nv], in0=Ib[:, nv - 512:nv], in1=negm[:], op=ALU.add),
             reads=[Ib, negm], writes=[Ib])
        b.op("dve", lambda e: e.tensor_scalar(out=m1[:], in0=m1[:], scalar1=1.0, scalar2=None, op0=ALU.add),
             reads=[m1], writes=[m1])
        b.op("dve", lambda e: e.tensor_scalar(out=lo[:], in0=m1[:], scalar1=-1.0, scalar2=None, op0=ALU.mult),
             reads=[m1], writes=[lo])
        b.op("dve", lambda e: e.tensor_scalar(out=W[:], in0=cW[:], scalar1=m1[:, 0:1], scalar2=None, op0=ALU.mult),
             reads=[cW, m1], writes=[W])
        b.op("dve", lambda e: e.tensor_tensor(out=mid[:], in0=lo[:], in1=W[:, 0:1], op=ALU.add), reads=[lo, W], writes=[mid])
        for i in range(NIT):
            b.op("dve", lambda e: e.tensor_scalar(out=Mq[:, 0:nv], in0=Ib[:, 0:nv], scalar1=mid[:, 0:1], scalar2=0.0,
                                                  op0=ALU.is_ge, op1=ALU.add, accum_out=cnt[:, 0:1]),
                 reads=[Ib, mid], writes=[Mq, cnt])
            b.op("dve", lambda e, i=i: e.tensor_scalar(out=g[:], in0=cnt[:], scalar1=TOPK - 0.5, scalar2=W[:, i:i + 1],
                                                       op0=ALU.is_ge, op1=ALU.mult), reads=[cnt, W], writes=[g])
            b.op("dve", lambda e: e.tensor_tensor(out=lo[:], in0=lo[:], in1=g[:], op=ALU.add), reads=[lo, g], writes=[lo])
            if i + 1 < NIT:
                b.op("dve", lambda e, i=i: e.tensor_tensor(out=mid[:], in0=lo[:], in1=W[:, i + 1:i + 2], op=ALU.add),
                     reads=[lo, W], writes=[mid])
        b.op("dve", lambda e: e.tensor_scalar(out=Mq[:, 0:nv], in0=Ib[:, 0:nv], scalar1=lo[:, 0:1], scalar2=None,
                                              op0=ALU.is_ge), reads=[Ib, lo], writes=[Mq])
        nt = 4 * nb
        for kt in range(nt):
            pm = pM[(kt // 8) % 2]
            b.op("pe", lambda e, kt=kt, pm=pm: e.transpose(out=pm[:, (kt % 8) * 128:(kt % 8 + 1) * 128],
                                                           in_=Mq[:, kt * 128:(kt + 1) * 128], identity=idt[:]),
                 reads=[Mq, idt], writes=[pm])
            if kt % 8 == 7 or kt == nt - 1:
                k0 = (kt // 8) * 8
                n = kt - k0 + 1
                b.op("act", lambda e, pm=pm, k0=k0, n=n: e.activation(
                    out=MT[:, k0:k0 + n, :].rearrange("p a b -> p (a b)"), in_=pm[:, 0:n * 128], func=AF.Copy),
                    reads=[pm], writes=[MT])

    def qk(u, n, kbuf):
        a, hp, hh, blk = u
        ps = pS[n % 2]
        qa = qTa[a % 2]
        for i in range(4):
            kt = 4 * blk + i
            b.op("pe", lambda e, i=i, kt=kt, ps=ps, qa=qa, hh=hh, hp=hp, kbuf=kbuf: e.matmul(
                ps[:, i * 128:(i + 1) * 128], lhsT=ktp[kbuf][64 * hh:64 * hh + 64, kt * 128:(kt + 1) * 128],
                rhs=qa[64 * hh:64 * hh + 64, hp, :], start=True, stop=True), reads=[ktp[kbuf], qa], writes=[ps])

    def softmax_pv(u, n, kbuf):
        a, hp, hh, blk = u
        ps, pt = pS[n % 2], pT[n % 2]
        if blk == 0:
            ctr["acc"] += 1
        acc = pA[ctr["acc"] % 2]
        b.op("act", lambda e: e.activation(out=pt[:], in_=ps[:], func=AF.Exp), reads=[ps], writes=[pt])
        b.op("dve", lambda e: e.tensor_tensor(out=pt[:], in0=pt[:], in1=MT[:, 4 * blk:4 * blk + 4, :].rearrange("p a b -> p (a b)"),
                                              op=ALU.mult), reads=[pt, MT], writes=[pt])
        for i in range(4):
            kt = 4 * blk + i
            b.op("pe", lambda e, i=i, kt=kt: e.matmul(acc[:, 0:65], lhsT=pt[:, i * 128:(i + 1) * 128],
                                                      rhs=vp[kbuf][:, kt, hh * 65:(hh + 1) * 65],
                                                      start=(blk == 0 and i == 0), stop=(blk == a and i == 3)),
                 reads=[pt, vp[kbuf]], writes=[acc])
        if blk == a:
            head = 2 * hp + hh
            r = rec[ctr["acc"] % 2]
            b.op("dve", lambda e: e.reciprocal(out=r[:], in_=acc[:, 64:65]), reads=[acc], writes=[r])
            b.op("dve", lambda e: e.tensor_scalar(out=o_tok[a][:, head * 64:(head + 1) * 64], in0=acc[:, 0:64],
                                                  scalar1=r[:, 0:1], scalar2=None, op0=ALU.mult),
                 reads=[acc, r], writes=[o_tok[a]])

    load_slot_small(0)
    for a in range(NS):
        if a + 1 < NS:
            load_slot_small(a + 1)
        kb_next = load_kv(a, 0)
        indexer(a)
        units = [(a, hp, hh, blk) for hp in range(8) for hh in range(2) for blk in range(a + 1)]
        kbufs = {}
        kbufs[0] = kb_next
        qk(units[0], 0, kbufs[0])
        for n, u in enumerate(units):
            _, hp, hh, blk = u
            if hh == 0 and blk == 0 and hp + 1 < 8:
                kbufs[hp + 1] = load_kv(a, hp + 1)
            if n + 1 < len(units):
                qk(units[n + 1], n + 1, kbufs[units[n + 1][1]])
            softmax_pv(u, n, kbufs[hp])


GROUPS = [[0, 1, 2, 3], [4, 5, 6, 7]]
_RANK = {}


class Gather:
    def __init__(self, b, name, nblk, cols, zt):
        self.b, self.name, self.nblk, self.cols = b, name, nblk, cols
        nc = b.nc
        self.xb = nc.dram_tensor(name + "_xb", [nblk, 512, cols], BF16).ap()
        self.yb = nc.dram_tensor(name + "_yb", [nblk, 512, cols], BF16).ap()
        self.xo = nc.dram_tensor(name + "_xo", [nblk, 128, cols], BF16).ap()
        self.od = [T(self.xo[h]) for h in range(nblk)]
        self.xd = [T(self.xb[h]) for h in range(nblk)]
        self.yd = [T(self.yb[h]) for h in range(nblk)]
        for h in range(nblk):
            for r in range(4):
                b.op("act", lambda e, h=h, r=r: e.dma_start(out=self.xb[h, r * 128:(r + 1) * 128, :], in_=zt[:, 0:cols]),
                     reads=[zt], writes=[self.xd[h]], dma="zero_" + name)

    def put(self, h, c0, c1, src_ap, reads, key):
        self.b.op("sp", lambda e: e.dma_start(out=self.xo[h, :, c0:c1], in_=src_ap), reads=reads,
                  writes=[self.od[h]], dma=key)

    def place(self, h, eng):
        def fn(e):
            if eng not in _RANK:
                _RANK[eng] = e.partition_id() % 4
            r = _RANK[eng]
            return e.dma_start(out=self.xb[h, bass.ds(r * 128, 128), :], in_=self.xo[h])
        self.b.op(eng, fn, reads=[self.od[h]], writes=[self.xd[h]], dma="place_" + self.name)

    def reduce(self, h):
        b = self.b
        b.op("pool", lambda e: e.collective_compute("AllReduce", ALU.add, replica_groups=GROUPS,
                                                    ins=[self.xb[h]], outs=[self.yb[h]]),
             reads=[self.xd[h]], writes=[self.yd[h]], dma=f"cc_{self.name}{h}", sem_inc=1)


def f_diff_proj(b, io, qT_res, G0, cos, sin):
    idt = b.ident()
    gmix = b.sb([128, D], F32, "gmix")
    b.op("sp", lambda e: e.dma_start(out=gmix[:], in_=io["gmix"][:]), writes=[gmix], dma="ld_gmix")
    gqk = b.sb([128, 2048], F32, "gqk")
    b.op("sp", lambda e: e.dma_start(out=gqk[:], in_=io["gqk"][:]), writes=[gqk], dma="ld_gqk")
    b.op("pool", lambda e: e.tensor_scalar(out=gqk[:, 0:1024], in0=gqk[:, 0:1024], scalar1=0.125, scalar2=None,
                                           op0=ALU.mult), reads=[gqk], writes=[gqk])
    w = b.sb([128, 8, 3072], BF16, "w_in")
    load_w_bf16(b, w, io["w_in"], 8, 3072, "ld_w")
    xts = [b.sb([128, D], F32, f"xt{i}") for i in range(2)]
    junk = b.sb([128, D], BF16, "junk")
    ss = b.sb([128, 1], F32, "ss")
    hbf = b.sb([128, D], BF16, "hbf")
    hT = b.sb([128, 8, 128], BF16, "hT")
    pT = [b.banks16[0], b.banks16[1]]
    pY = [b.banks[2 + i] for i in range(4)]
    stage = b.sb([128, 2048], F32, "stage")
    sq = b.sb([128, 2048], F32, "sq")
    ssq = b.sb([128, 32], F32, "ssq")
    tmp = b.sb([128, 4 * 32 * 8], F32, "ropetmp")
    qkbf = b.sb([128, 2048], BF16, "qkbf")
    kT = [b.sb([128, 8, 128], BF16, f"kTst{i}") for i in range(2)]
    vst = [b.sb([128, 8, 128], BF16, f"vst{i}") for i in range(2)]
    for a in range(NS):
        xt = xts[a % 2]
        b.op("sp", lambda e, a=a, xt=xt: e.dma_start(out=xt[:], in_=io["x"][a * 128:(a + 1) * 128, :]),
             writes=[xt], dma=f"ld_x{a % 2}")
        rmsnorm_transpose(b, xt, gmix, hbf, hT, pT[0], ss, junk)
        va = vst[a % 2]
        for n in range(6):
            py = pY[n % 4]
            for kc in range(8):
                b.op("pe", lambda e, n=n, kc=kc, py=py: e.matmul(py[:], lhsT=hT[:, kc, :],
                                                                   rhs=w[:, kc, n * 512:(n + 1) * 512],
                                                                   start=(kc == 0), stop=(kc == 7)),
                     reads=[hT, w], writes=[py])
            if n < 4:
                b.op("act", lambda e, n=n, py=py: e.activation(out=stage[:, n * 512:(n + 1) * 512], in_=py[:], func=AF.Copy),
                     reads=[py], writes=[stage])
                b.op("act", lambda e, n=n, py=py: e.activation(out=sq[:, n * 512:(n + 1) * 512], in_=py[:], func=AF.Square),
                     reads=[py], writes=[sq])
            else:
                b.op("dve", lambda e, n=n, py=py, va=va: e.tensor_copy(
                    out=va[:, (n - 4) * 4:(n - 4) * 4 + 4, :], in_=py[:].rearrange("p (h d) -> p h d", d=128)),
                    reads=[py], writes=[va])
        for h in range(8):
            G0.put(h, 2048 + a * 128, 2048 + (a + 1) * 128, va[:, h, :], [va], f"st_v{a % 2}")
        headnorm_rope(b, stage, sq, ssq, 32, gqk, cos[:, a, :], sin[:, a, :], qkbf, tmp)
        kt = kT[a % 2]
        for i in range(16):
            pt = pT[1] if i < 8 else pT[0]
            b.op("pe", lambda e, i=i, pt=pt: e.transpose(out=pt[:, (i % 8) * 128:(i % 8 + 1) * 128],
                                                         in_=qkbf[:, i * 128:(i + 1) * 128], identity=idt[:]),
                 reads=[qkbf, idt], writes=[pt])
            if i == 7:
                b.op("dve", lambda e, pt=pt, a=a: e.tensor_copy(out=qT_res[:, a, :], in_=pt[:]), reads=[pt], writes=[qT_res])
            if i == 15:
                b.op("dve", lambda e, pt=pt, kt=kt: e.tensor_copy(out=kt[:].rearrange("p a b -> p (a b)"), in_=pt[:]),
                     reads=[pt], writes=[kt])
        for h in range(8):
            G0.put(h, a * 128, (a + 1) * 128, kt[:, h, :], [kt], f"st_k{a % 2}")
    for h in range(8):
        G0.place(h, "sp")
        G0.reduce(h)


def f_diff_attn(b, io, o_tok, qT, G0):
    LAM_INIT = 0.2
    maskT = b.sb([128, 512], BF16, "maskT")
    b.op("sp", lambda e: e.dma_start(out=maskT[:], in_=io["maskT"][:]), writes=[maskT], dma="ld_mask")
    lam = b.sb([128, 256], F32, "lam")
    b.op("sp", lambda e: e.dma_start(out=lam[:], in_=io["lam"][:]), writes=[lam], dma="ld_lam")
    gsub = b.sb([128, 128], F32, "gsub")
    b.op("sp", lambda e: e.dma_start(out=gsub[:], in_=io["gsub"][:]), writes=[gsub], dma="ld_gsub")
    b.op("dve", lambda e: e.tensor_scalar(out=gsub[:], in0=gsub[:], scalar1=1.0 - LAM_INIT, scalar2=None,
                                          op0=ALU.mult), reads=[gsub], writes=[gsub])
    lprod = b.sb([128, 128], F32, "lprod")
    l2 = b.sb([128, 2], F32, "l2")
    neglam = b.sb([128, 1], F32, "neglam")
    l4 = lam[:].rearrange("p (a b d) -> p a b d", a=2, b=2)
    b.op("dve", lambda e: e.tensor_tensor(out=lprod[:].rearrange("p (a d) -> p a d", a=2), in0=l4[:, :, 0, :],
                                          in1=l4[:, :, 1, :], op=ALU.mult), reads=[lam], writes=[lprod])
    b.op("dve", lambda e: e.tensor_reduce(out=l2[:], in_=lprod[:].rearrange("p (a d) -> p a d", a=2), axis=AX.X,
                                          op=ALU.add), reads=[lprod], writes=[l2])
    b.op("act", lambda e: e.activation(out=l2[:], in_=l2[:], func=AF.Exp), reads=[l2], writes=[l2])
    b.op("dve", lambda e: e.tensor_tensor(out=neglam[:], in0=l2[:, 1:2], in1=l2[:, 0:1], op=ALU.subtract),
         reads=[l2], writes=[neglam])
    b.op("dve", lambda e: e.tensor_scalar(out=neglam[:], in0=neglam[:], scalar1=-LAM_INIT, scalar2=None, op0=ALU.add),
         reads=[neglam], writes=[neglam])

    ktb = [b.sb([128, 8192], BF16, f"ktb{i}") for i in range(2)]
    vb = [b.sb([128, 64, 129], BF16, f"vb{i}") for i in range(2)]
    for i in range(2):
        b.op("dve", lambda e, i=i: e.memset(vb[i][:, :, 128:129], 1.0), writes=[vb[i]])
    pS = [[b.banks[c * 2 + i] for i in range(2)] for c in range(2)]
    pA = [[b.banks[4 + c * 2 + i] for i in range(2)] for c in range(2)]
    pT = [[b.sb([128, 512], BF16, f"pTs{c}{i}") for i in range(2)] for c in range(2)]
    rec = [b.sb([128, 2], F32, f"rec{i}") for i in range(2)]
    o32 = [b.sb([128, 128], F32, f"o32{i}") for i in range(2)]
    oj = b.sb([128, 128], BF16, "ojunk")
    ss1 = [b.sb([128, 1], F32, f"ss1{i}") for i in range(2)]
    units = [(h, a, blk) for h in range(8) for a in range(NS) for blk in range(a + 1)]

    def load_head(h):
        kb, vv = ktb[h % 2], vb[h % 2]
        yb = G0.yb[h]
        for r in range(4):
            b.op("sp", lambda e, kb=kb, r=r, yb=yb: e.dma_start(out=kb[:, r * 2048:(r + 1) * 2048],
                                                               in_=yb[r * 128:(r + 1) * 128, 0:2048]),
                 reads=[G0.yd[h]], writes=[kb], dma=f"ld_kt{h % 2}")
            b.op("sp", lambda e, vv=vv, r=r, yb=yb: e.dma_start(
                out=vv[:, r * 16:(r + 1) * 16, 0:128],
                in_=yb[r * 128:(r + 1) * 128, 2048:4096].rearrange("p (a e) -> p a e", e=128)),
                reads=[G0.yd[h]], writes=[vv], dma=f"ld_v{h % 2}")

    def qk(u, n):
        h, a, blk = u
        kb = ktb[h % 2]
        for i in range(4):
            for c in range(2):
                ps = pS[c][n % 2]
                kt = 16 * i + blk
                b.op("pe", lambda e, c=c, i=i, kt=kt, ps=ps, kb=kb, a=a, h=h: e.matmul(
                    ps[:, i * 128:(i + 1) * 128], lhsT=kb[64 * c:64 * c + 64, kt * 128:(kt + 1) * 128],
                    rhs=qT[64 * c:64 * c + 64, a, h * 128:(h + 1) * 128], start=True, stop=True),
                    reads=[kb, qT], writes=[ps])

    def evac(h, a):
        k = a % 2
        a0, a1 = pA[0][k], pA[1][k]
        r, o, s = rec[k], o32[k], ss1[k]
        b.op("dve", lambda e: e.reciprocal(out=r[:, 0:1], in_=a0[:, 128:129]), reads=[a0], writes=[r])
        b.op("dve", lambda e: e.reciprocal(out=r[:, 1:2], in_=a1[:, 128:129]), reads=[a1], writes=[r])
        b.op("dve", lambda e: e.tensor_tensor(out=r[:, 1:2], in0=r[:, 1:2], in1=neglam[:], op=ALU.mult),
             reads=[r, neglam], writes=[r])
        b.op("dve", lambda e: e.tensor_scalar(out=o[:], in0=a0[:, 0:128], scalar1=r[:, 0:1], scalar2=None, op0=ALU.mult),
             reads=[a0, r], writes=[o])
        b.op("dve", lambda e: e.scalar_tensor_tensor(out=o[:], in0=a1[:, 0:128], scalar=r[:, 1:2], in1=o[:],
                                                     op0=ALU.mult, op1=ALU.add), reads=[a1, r, o], writes=[o])
        b.op("act", lambda e: e.activation(out=oj[:], in_=o[:], func=AF.Square, accum_out=s[:, 0:1]),
             reads=[o], writes=[oj, s])
        rstd_from_ss(b, s, 1, 1.0 / 128)
        ot = o_tok[a]
        b.op("dve", lambda e: e.scalar_tensor_tensor(out=ot[:, h * 128:(h + 1) * 128], in0=o[:], scalar=s[:, 0:1],
                                                     in1=gsub[:], op0=ALU.mult, op1=ALU.mult),
             reads=[o, s, gsub], writes=[ot])

    def softmax_pv(u, n):
        h, a, blk = u
        vv = vb[h % 2]
        for c in range(2):
            ps = pS[c][n % 2]
            pt = pT[c][n % 2]
            acc = pA[c][a % 2]
            b.op("act", lambda e, ps=ps, pt=pt: e.activation(out=pt[:], in_=ps[:], func=AF.Exp),
                 reads=[ps], writes=[pt])
            if blk == a:
                b.op("dve", lambda e, pt=pt: e.tensor_tensor(out=pt[:], in0=pt[:], in1=maskT[:], op=ALU.mult),
                     reads=[pt, maskT], writes=[pt])
            for i in range(4):
                kt = 16 * i + blk
                b.op("pe", lambda e, i=i, kt=kt, pt=pt, acc=acc, vv=vv, blk=blk, a=a: e.matmul(
                    acc[:, 0:129], lhsT=pt[:, i * 128:(i + 1) * 128], rhs=vv[:, kt, :],
                    start=(blk == 0 and i == 0), stop=(blk == a and i == 3)),
                    reads=[pt, vv], writes=[acc])
        if blk == a:
            evac(h, a)

    load_head(0)
    qk(units[0], 0)
    for n, u in enumerate(units):
        if u[1] == 0 and u[2] == 0 and u[0] + 1 < 8:
            load_head(u[0] + 1)
        if n + 1 < len(units):
            qk(units[n + 1], n + 1)
        softmax_pv(u, n)


def f_dsa_proj(b, io, x_res, cos, sin, G1, GK, scr):
    idt = b.ident()
    gmix = b.sb([128, D], F32, "gmix1")
    b.op("sp", lambda e: e.dma_start(out=gmix[:], in_=io["gmix1"][:]), writes=[gmix], dma="ld_gmix1")
    gqk = b.sb([128, 2048], F32, "gqk2")
    b.op("sp", lambda e: e.dma_start(out=gqk[:], in_=io["gqk2"][:]), writes=[gqk], dma="ld_gqk2")
    b.op("pool", lambda e: e.tensor_scalar(out=gqk[:, 0:1024], in0=gqk[:, 0:1024], scalar1=0.125, scalar2=None,
                                           op0=ALU.mult), reads=[gqk], writes=[gqk])
    gcq = b.sb([128, 256], F32, "gcq")
    b.op("sp", lambda e: e.dma_start(out=gcq[:], in_=io["gcq"][:]), writes=[gcq], dma="ld_gcq")
    w = b.sb([128, 8, 2376], BF16, "w_in2")
    load_w_bf16(b, w, io["w_in2"], 8, 2376, "ld_w2in", colchunk=792)
    wuq = b.sb([128, 2, 1024], BF16, "wuq")
    load_w_bf16(b, wuq, io["w_uq"], 2, 1024, "ld_wuq")
    wuqi = b.sb([128, 2, 512], BF16, "wuqi")
    load_w_bf16(b, wuqi, io["w_uqi"], 2, 512, "ld_wuqi")
    junk = b.sb([128, D], BF16, "junk3")
    ss = b.sb([128, 1], F32, "ss3")
    hbf = b.sb([128, D], BF16, "hbf3")
    hT = b.sb([128, 8, 128], BF16, "hT3")
    stage = b.sb([128, 2048], F32, "stage3")
    sq = b.sb([128, 2048], F32, "sq3")
    ssq = b.sb([128, 32], F32, "ssq3")
    tmp = b.sb([128, 4 * 32 * 8], F32, "ropetmp3")
    qkbf = b.sb([128, 2048], BF16, "qkbf3")
    qkT = [b.sb([128, 16, 128], BF16, f"qkT3{i}") for i in range(2)]
    vst = [b.sb([128, 16, 64], BF16, f"vst3{i}") for i in range(2)]
    cqs = b.sb([128, 256], F32, "cqs")
    ssc = b.sb([128, 1], F32, "ssc")
    cqbf = b.sb([128, 256], BF16, "cqbf")
    cqT = b.sb([128, 2, 128], BF16, "cqT")
    kis = b.sb([128, 64], F32, "kis")
    ksq = b.sb([128, 64], F32, "ksq")
    kss = b.sb([128, 1], F32, "kss")
    kibf = b.sb([128, 128], BF16, "kibf")
    kiT = [b.sb([128, 128], BF16, f"kiT{i}") for i in range(2)]
    wi = b.sb([128, 8], F32, "wi")
    sgn = [b.sb([128, 8], F32, f"sgn{i}") for i in range(2)]
    aw = b.sb([128, 8], F32, "aw")
    qis = b.sb([128, 512], F32, "qis")
    qibf = b.sb([128, 512], BF16, "qibf")
    qiT = [b.sb([128, 4, 128], BF16, f"qiT{i}") for i in range(2)]
    nbank = [0]

    def bank():
        nbank[0] += 1
        return b.banks[2 + nbank[0] % 4]

    def proj(py, lhs, nk, rhs_fn, ncol):
        for kc in range(nk):
            b.op("pe", lambda e, kc=kc: e.matmul(py[:, 0:ncol], lhsT=lhs[:, kc, :], rhs=rhs_fn(kc),
                                                 start=(kc == 0), stop=(kc == nk - 1)), reads=[lhs, w, wuq, wuqi], writes=[py])

    for a in range(NS):
        xr = x_res[a]
        rmsnorm_transpose(b, xr, gmix, hbf, hT, b.banks16[0], ss, junk)
        py = bank()
        proj(py, hT, 8, lambda kc: w[:, kc, 0:256], 256)
        b.op("act", lambda e, py=py: e.activation(out=cqs[:], in_=py[:, 0:256], func=AF.Copy), reads=[py], writes=[cqs])
        b.op("act", lambda e, py=py: e.activation(out=junk[:, 0:256], in_=py[:, 0:256], func=AF.Square, accum_out=ssc[:, 0:1]),
             reads=[py], writes=[junk, ssc])
        rstd_from_ss(b, ssc, 1, 1.0 / 256)
        b.op("dve", lambda e: e.scalar_tensor_tensor(out=cqbf[:], in0=cqs[:], scalar=ssc[:, 0:1], in1=gcq[:],
                                                     op0=ALU.mult, op1=ALU.mult), reads=[cqs, ssc, gcq], writes=[cqbf])
        p6 = b.banks16[6]
        for kc in range(2):
            b.op("pe", lambda e, kc=kc: e.transpose(out=p6[:, kc * 128:(kc + 1) * 128], in_=cqbf[:, kc * 128:(kc + 1) * 128],
                                                    identity=idt[:]), reads=[cqbf, idt], writes=[p6])
        b.op("dve", lambda e: e.tensor_copy(out=cqT[:].rearrange("p a b -> p (a b)"), in_=p6[:, 0:256]),
             reads=[p6], writes=[cqT])
        for n in range(2):
            py = bank()
            proj(py, hT, 8, lambda kc, n=n: w[:, kc, 256 + n * 512:256 + (n + 1) * 512], 512)
            b.op("act", lambda e, py=py, n=n: e.activation(out=stage[:, 1024 + n * 512:1024 + (n + 1) * 512], in_=py[:], func=AF.Copy),
                 reads=[py], writes=[stage])
            b.op("act", lambda e, py=py, n=n: e.activation(out=sq[:, 1024 + n * 512:1024 + (n + 1) * 512], in_=py[:], func=AF.Square),
                 reads=[py], writes=[sq])
        va = vst[a % 2]
        for n in range(2):
            py = bank()
            proj(py, hT, 8, lambda kc, n=n: w[:, kc, 1280 + n * 512:1280 + (n + 1) * 512], 512)
            b.op("dve", lambda e, py=py, n=n, va=va: e.tensor_copy(out=va[:, n * 8:(n + 1) * 8, :],
                                                                   in_=py[:].rearrange("p (h d) -> p h d", d=64)),
                 reads=[py], writes=[va])
        for hp in range(8):
            G1.put(hp, 2048 + a * 128, 2048 + (a + 1) * 128, va[:, 2 * hp:2 * hp + 2, :].rearrange("p h d -> p (h d)"),
                   [va], f"st_v2{a % 2}")
        py = bank()
        proj(py, hT, 8, lambda kc: w[:, kc, 2304:2376], 72)
        b.op("act", lambda e, py=py: e.activation(out=kis[:], in_=py[:, 0:64], func=AF.Copy), reads=[py], writes=[kis])
        b.op("act", lambda e, py=py: e.activation(out=ksq[:], in_=py[:, 0:64], func=AF.Square), reads=[py], writes=[ksq])
        b.op("dve", lambda e, py=py: e.tensor_copy(out=wi[:], in_=py[:, 64:72]), reads=[py], writes=[wi])
        for n in range(2):
            py = bank()
            proj(py, cqT, 2, lambda kc, n=n: wuq[:, kc, n * 512:(n + 1) * 512], 512)
            b.op("act", lambda e, py=py, n=n: e.activation(out=stage[:, n * 512:(n + 1) * 512], in_=py[:], func=AF.Copy),
                 reads=[py], writes=[stage])
            b.op("act", lambda e, py=py, n=n: e.activation(out=sq[:, n * 512:(n + 1) * 512], in_=py[:], func=AF.Square),
                 reads=[py], writes=[sq])
        py = bank()
        proj(py, cqT, 2, lambda kc: wuqi[:, kc, :], 512)
        b.op("act", lambda e, py=py: e.activation(out=qis[:], in_=py[:], func=AF.Copy), reads=[py], writes=[qis])
        headnorm_rope(b, stage, sq, ssq, 32, gqk, cos[:, a, :], sin[:, a, :], qkbf, tmp)
        qt = qkT[a % 2]
        for i in range(16):
            pt = b.banks16[1] if i < 8 else b.banks16[0]
            b.op("pe", lambda e, i=i, pt=pt: e.transpose(out=pt[:, (i % 8) * 128:(i % 8 + 1) * 128],
                                                         in_=qkbf[:, i * 128:(i + 1) * 128], identity=idt[:]),
                 reads=[qkbf, idt], writes=[pt])
            if i % 8 == 7:
                b.op("dve", lambda e, i=i, pt=pt, qt=qt: e.tensor_copy(
                    out=qt[:, (i // 8) * 8:(i // 8) * 8 + 8, :].rearrange("p a b -> p (a b)"), in_=pt[:]),
                    reads=[pt], writes=[qt])
        b.op("sp", lambda e, a=a, qt=qt: e.dma_start(out=scr["QT2"][a][:], in_=qt[:, 0:8, :].rearrange("p a b -> p (a b)")),
             reads=[qt], writes=[scr["QT2"][a]], dma=f"st_q2{a % 2}")
        for hp in range(8):
            G1.put(hp, a * 128, (a + 1) * 128, qt[:, 8 + hp, :], [qt], f"st_k2{a % 2}")
        ki_half = T(kibf.t[:, 0:64], kibf.d)
        headnorm_rope(b, kis, ksq, kss, 1, None, cos[:, a, :], sin[:, a, :], ki_half, tmp)
        b.op("pool", lambda e: e.tensor_copy(out=kibf[:, 64:128], in_=kibf[:, 0:64]), reads=[kibf], writes=[kibf])
        p7 = b.banks16[7]
        b.op("pe", lambda e: e.transpose(out=p7[:, 0:128], in_=kibf[:], identity=idt[:]), reads=[kibf, idt], writes=[p7])
        kt_ = kiT[a % 2]
        b.op("dve", lambda e, kt_=kt_: e.tensor_copy(out=kt_[:], in_=p7[:, 0:128]), reads=[p7], writes=[kt_])
        GK.put(0, a * 128, (a + 1) * 128, kt_[:], [kt_], f"st_ki{a % 2}")
        sg = sgn[a % 2]
        b.op("act", lambda e, sg=sg: e.activation(out=sg[:], in_=wi[:], func=AF.Sign), reads=[wi], writes=[sg])
        b.op("dve", lambda e, sg=sg: e.scalar_tensor_tensor(out=aw[:], in0=wi[:], scalar=IDX_SCALE, in1=sg[:],
                                                           op0=ALU.mult, op1=ALU.mult), reads=[wi, sg], writes=[aw])
        b.op("sp", lambda e, a=a, sg=sg: e.dma_start(out=scr["SG"][a][:], in_=sg[:]), reads=[sg], writes=[scr["SG"][a]],
             dma=f"st_sg{a % 2}")
        headnorm_rope(b, qis, None, aw, 8, None, cos[:, a, :], sin[:, a, :], qibf, tmp, norm=False)
        qi_ = qiT[a % 2]
        for i in range(4):
            b.op("pe", lambda e, i=i: e.transpose(out=p7[:, 256 + i * 128:256 + (i + 1) * 128],
                                                  in_=qibf[:, i * 128:(i + 1) * 128], identity=idt[:]),
                 reads=[qibf, idt], writes=[p7])
        b.op("dve", lambda e, qi_=qi_: e.tensor_copy(out=qi_[:].rearrange("p a b -> p (a b)"), in_=p7[:, 256:768]),
             reads=[p7], writes=[qi_])
        b.op("sp", lambda e, a=a, qi_=qi_: e.dma_start(out=scr["QI"][a][:], in_=qi_[:].rearrange("p a b -> p (a b)")),
             reads=[qi_], writes=[scr["QI"][a]], dma=f"st_qi{a % 2}")
    GK.place(0, "act")
    GK.reduce(0)
    for hp in range(8):
        G1.place(hp, "act")
        G1.reduce(hp)


NIT2 = 16


def f_dsa_attn(b, io, o_tok, G1, GK, scr):
    idt = b.ident()
    kia = b.sb([128, 8192], BF16, "kiall")
    for r in range(4):
        b.op("sp", lambda e, r=r: e.dma_start(out=kia[:, r * 2048:(r + 1) * 2048], in_=GK.yb[0][r * 128:(r + 1) * 128, :]),
             reads=[GK.yd[0]], writes=[kia], dma="ld_kia")
    kia4 = kia[:].rearrange("p (r a t) -> p r a t", r=4, a=16)
    negm = b.sb([128, 512], F32, "negm")
    b.op("sp", lambda e: e.dma_start(out=negm[:], in_=io["negmask"][:]), writes=[negm], dma="ld_negm")
    cW = b.sb([128, NIT2], F32, "cW")
    for i in range(NIT2):
        b.op("dve", lambda e, i=i: e.memset(cW[:, i:i + 1], 2.0 ** (-i)), writes=[cW])
    THR = b.sb([128, NS], F32, "THR")
    qiTa = [b.sb([128, 4, 128], BF16, f"qiTa{i}") for i in range(2)]
    sgn = [b.sb([128, 8], F32, f"sgna{i}") for i in range(2)]
    Dg = [b.sb([128, 8, 128], BF16, f"Dg{i}") for i in range(2)]
    tb = [[b.sb([128, 512], BF16, f"tb{s}{i}") for i in range(3)] for s in range(2)]
    pIs = [[b.banks[4], b.banks[6]], [b.banks[5], b.banks[0]]]
    pIas = [b.banks[7], b.banks[1]]
    ctr = {"i": 0, "acc": 0, "kv": 0}

    def load_idx_small(a):
        b.op("sp", lambda e, a=a: e.dma_start(out=qiTa[a % 2][:].rearrange("p a b -> p (a b)"), in_=scr["QI"][a][:]),
             reads=[scr["QI"][a]], writes=[qiTa[a % 2]], dma=f"ld_qiTa{a % 2}")
        b.op("sp", lambda e, a=a: e.dma_start(out=sgn[a % 2][:], in_=scr["SG"][a][:]), reads=[scr["SG"][a]],
             writes=[sgn[a % 2]], dma=f"ld_sgn{a % 2}")

    def indexer_list(a, s, Ib_):
        L = []

        def add(eng, fn, reads=(), writes=()):
            L.append((eng, fn, reads, writes))
        nb = a + 1
        qi, sg, dg = qiTa[a % 2], sgn[a % 2], Dg[a % 2]
        pys, pia = pIs[s], pIas[s]
        add("dve", lambda e: e.tensor_tensor(out=dg[:], in0=idt[:].unsqueeze(1).to_broadcast([128, 8, 128]),
                                             in1=sg[:].unsqueeze(2).to_broadcast([128, 8, 128]), op=ALU.mult),
            [idt, sg], [dg])
        steps = [(blk, head) for blk in range(nb) for head in range(8)]
        S_ = len(steps)
        evq = []
        for k in range(S_ + 2):
            if k < S_:
                blk, head = steps[k]
                hp, hh = head // 2, head % 2
                py = pys[k % 2]
                add("pe", lambda e, hp=hp, hh=hh, blk=blk, py=py: e.matmul(
                    py[:].rearrange("p (r t) -> p r t", r=4), lhsT=qi[64 * hh:64 * hh + 64, hp, :],
                    rhs=kia4[64 * hh:64 * hh + 64, :, blk, :], start=True, stop=True), [qi, kia], [py])
            if 0 <= k - 1 < S_:
                py = pys[(k - 1) % 2]
                t = tb[s][(k - 1) % 3]
                add("act", lambda e, t=t, py=py: e.activation(out=t[:], in_=py[:], func=AF.Relu), [py], [t])
            if 0 <= k - 2 < S_:
                blk, head = steps[k - 2]
                t = tb[s][(k - 2) % 3]
                add("pe", lambda e, head=head, t=t: e.matmul(pia[:], lhsT=dg[:, head, :], rhs=t[:],
                                                             start=(head == 0), stop=(head == 7)), [dg, t], [pia])
                if head == 7:
                    add("act", lambda e, eb=blk: e.activation(out=Ib_[:, eb * 512:(eb + 1) * 512], in_=pia[:], func=AF.Copy),
                        [pia], [Ib_])
        for _, eb in evq:
            add("act", lambda e, eb=eb: e.activation(out=Ib_[:, eb * 512:(eb + 1) * 512], in_=pia[:], func=AF.Copy),
                [pia], [Ib_])
        return L

    mA = b.mark()
    IbA = [b.sb([128, 8192], F32, f"IbA{i}") for i in range(2)]
    MqA = [b.sb([128, 8192], BF16, f"MqA{i}") for i in range(2)]
    st = [{k: b.sb([128, n], F32, f"bs{k}{s}") for k, n in (("m1", 1), ("lo", 1), ("mid", 1), ("nmid", 1), ("cD", 1),
                                                           ("sA", 1), ("g", 1), ("W", NIT2))} for s in range(2)]

    def bisect_list(a, s):
        L = []

        def add(eng, fn, reads=(), writes=()):
            L.append((eng, fn, reads, writes))
        nb = a + 1
        nv = 512 * nb
        Ib_, Mq_ = IbA[s], MqA[s]
        S_ = st[s]
        m1, lo, mid, nmid, cD, sA, g, W = (S_[k] for k in ("m1", "lo", "mid", "nmid", "cD", "sA", "g", "W"))
        add("dve", lambda e: e.tensor_reduce(out=m1[:], in_=Ib_[:, 0:nv], axis=AX.X, op=ALU.max, apply_absolute_value=True),
            [Ib_], [m1])
        add("dve", lambda e: e.tensor_tensor(out=Ib_[:, nv - 512:nv], in0=Ib_[:, nv - 512:nv], in1=negm[:], op=ALU.add),
            [Ib_, negm], [Ib_])
        add("dve", lambda e: e.tensor_scalar(out=m1[:], in0=m1[:], scalar1=1.0, scalar2=None, op0=ALU.add), [m1], [m1])
        add("dve", lambda e: e.tensor_scalar(out=lo[:], in0=m1[:], scalar1=-1.0, scalar2=None, op0=ALU.mult), [m1], [lo])
        add("dve", lambda e: e.tensor_scalar(out=W[:], in0=cW[:], scalar1=m1[:, 0:1], scalar2=None, op0=ALU.mult),
            [cW, m1], [W])
        h = 512 * (nb // 2)
        thr = TOPK - 0.5 - 0.5 * h

        def set_mid(i):
            add("dve", lambda e: e.tensor_tensor(out=mid[:], in0=lo[:], in1=W[:, i:i + 1], op=ALU.add), [lo, W], [mid])
            if h > 0:
                add("dve", lambda e: e.tensor_scalar(out=nmid[:], in0=mid[:], scalar1=-1.0, scalar2=None, op0=ALU.mult),
                    [mid], [nmid])
        return L, (add, set_mid, h, thr, Ib_, Mq_, lo, mid, nmid, cD, sA, g, W, nv)

    def stageA_list(a, s):
        L = indexer_list(a, s, IbA[s])
        L2, (add, set_mid, h, thr, Ib_, Mq_, lo, mid, nmid, cD, sA, g, W, nv) = bisect_list(a, s)
        set_mid(0)
        for i in range(NIT2):
            if h > 0:
                add("act", lambda e: e.activation(out=Mq_[:, 0:h], in_=Ib_[:, 0:h], func=AF.Sign, bias=nmid[:, 0:1],
                                                  scale=1.0, accum_out=sA[:, 0:1]), [Ib_, nmid], [Mq_, sA])
            add("dve", lambda e: e.tensor_scalar(out=Mq_[:, h:nv], in0=Ib_[:, h:nv], scalar1=mid[:, 0:1], scalar2=0.0,
                                                 op0=ALU.is_ge, op1=ALU.add, accum_out=cD[:, 0:1]), [Ib_, mid], [Mq_, cD])
            if h > 0:
                add("dve", lambda e: e.scalar_tensor_tensor(out=cD[:], in0=sA[:], scalar=0.5, in1=cD[:], op0=ALU.mult,
                                                            op1=ALU.add), [sA, cD], [cD])
            add("dve", lambda e, i=i: e.tensor_scalar(out=g[:], in0=cD[:], scalar1=thr, scalar2=W[:, i:i + 1],
                                                      op0=ALU.is_ge, op1=ALU.mult), [cD, W], [g])
            add("dve", lambda e: e.tensor_tensor(out=lo[:], in0=lo[:], in1=g[:], op=ALU.add), [lo, g], [lo])
            if i + 1 < NIT2:
                set_mid(i + 1)
        add("dve", lambda e: e.tensor_copy(out=THR[:, a:a + 1], in_=lo[:]), [lo], [THR])
        return L + L2

    for a0 in range(0, NS, 2):
        load_idx_small(a0)
        load_idx_small(a0 + 1)
        la = stageA_list(a0, 0)
        lb = stageA_list(a0 + 1, 1)
        i = j = 0
        while i < len(la) or j < len(lb):
            if j >= len(lb) or (i < len(la) and i * len(lb) <= j * len(la)):
                b.op(*la[i]); i += 1
            else:
                b.op(*lb[j]); j += 1
    b.release(mA)

    Ib = b.sb([128, 8192], F32, "Ibuf")
    Mq = b.sb([128, 8192], BF16, "Mq")
    MT = b.sb([128, 64, 128], BF16, "MT")
    ktp = [b.sb([128, 8192], BF16, f"ktp{i}") for i in range(2)]
    vp = [b.sb([128, 64, 130], BF16, f"vp{i}") for i in range(2)]
    for i in range(2):
        b.op("dve", lambda e, i=i: e.memset(vp[i][:, :, 0:1], 1.0), writes=[vp[i]])
        b.op("dve", lambda e, i=i: e.memset(vp[i][:, :, 129:130], 1.0), writes=[vp[i]])
    qTa = [b.sb([128, 8, 128], BF16, f"qTa{i}") for i in range(2)]
    pT = [b.sb([128, 512], BF16, f"pTd{i}") for i in range(3)]
    rec = [b.sb([128, 1], F32, f"recd{i}") for i in range(2)]
    pS = [b.banks[0], b.banks[1], b.banks[5]]
    pA = [b.banks[2], b.banks[3]]
    pM = b.banks16[6]

    def load_slot_small(a):
        b.op("sp", lambda e, a=a: e.dma_start(out=qTa[a % 2][:].rearrange("p a b -> p (a b)"), in_=scr["QT2"][a][:]),
             reads=[scr["QT2"][a]], writes=[qTa[a % 2]], dma=f"ld_qTa{a % 2}")
        load_idx_small(a)

    def load_kv(a, hp):
        k = ctr["kv"] % 2
        ctr["kv"] += 1
        n = (a + 1) * 128
        yb = G1.yb[hp]
        b.op("sp", lambda e: e.dma_start(out=ktp[k][:].rearrange("p (r x) -> p r x", r=4)[:, :, 0:n],
                                         in_=yb[:, 0:n].rearrange("(r p) x -> p r x", p=128)),
             reads=[G1.yd[hp]], writes=[ktp[k]], dma=f"ld_ktp{k}")
        for r in range(4):
            b.op("sp", lambda e, r=r: e.dma_start(
                out=vp[k][:, r * 16:r * 16 + a + 1, 1:129],
                in_=yb[r * 128:(r + 1) * 128, 2048:2048 + n].rearrange("p (a e) -> p a e", e=128)),
                reads=[G1.yd[hp]], writes=[vp[k]], dma=f"ld_vp{k}")
        return k

    def pre_list(a):
        L = indexer_list(a, 0, Ib)
        nv = 512 * (a + 1)
        L.append(("dve", lambda e: e.tensor_tensor(out=Ib[:, nv - 512:nv], in0=Ib[:, nv - 512:nv], in1=negm[:], op=ALU.add),
                  [Ib, negm], [Ib]))
        for c0 in range(0, nv, 2048):
            c1 = min(nv, c0 + 2048)
            L.append(("dve", lambda e, c0=c0, c1=c1: e.tensor_scalar(out=Mq[:, c0:c1], in0=Ib[:, c0:c1],
                                                                     scalar1=THR[:, a:a + 1], scalar2=None, op0=ALU.is_ge),
                      [Ib, THR], [Mq]))
        return L

    def mask_transposes(a):
        nt = 4 * (a + 1)
        for kt in range(nt):
            b.op("pe", lambda e, kt=kt: e.transpose(out=pM[:, (kt % 8) * 128:(kt % 8 + 1) * 128],
                                                    in_=Mq[:, kt * 128:(kt + 1) * 128], identity=idt[:]),
                 reads=[Mq, idt], writes=[pM])
            if kt % 8 == 7 or kt == nt - 1:
                k0 = (kt // 8) * 8
                n = kt - k0 + 1
                b.op("act", lambda e, k0=k0, n=n: e.activation(
                    out=MT[:, k0:k0 + n, :].rearrange("p a b -> p (a b)"), in_=pM[:, 0:n * 128], func=AF.Copy),
                    reads=[pM], writes=[MT])

    def qk(u, n, kbuf):
        a, hp, hh, blk = u
        ps = pS[n % 3]
        qa = qTa[a % 2]
        for i in range(4):
            kt = 16 * i + blk
            b.op("pe", lambda e, i=i, kt=kt, ps=ps, qa=qa, hh=hh, hp=hp, kbuf=kbuf: e.matmul(
                ps[:, i * 128:(i + 1) * 128], lhsT=ktp[kbuf][64 * hh:64 * hh + 64, kt * 128:(kt + 1) * 128],
                rhs=qa[64 * hh:64 * hh + 64, hp, :], start=True, stop=True), reads=[ktp[kbuf], qa], writes=[ps])

    def softmax_pv(u, n, kbuf):
        a, hp, hh, blk = u
        ps, pt = pS[n % 3], pT[n % 3]
        if blk == 0:
            ctr["acc"] += 1
        acc = pA[ctr["acc"] % 2]
        b.op("act", lambda e: e.activation(out=pt[:], in_=ps[:], func=AF.Exp), reads=[ps], writes=[pt])
        b.op("dve", lambda e: e.tensor_tensor(out=pt[:], in0=pt[:], in1=MT[:, 4 * blk:4 * blk + 4, :].rearrange("p a b -> p (a b)"),
                                              op=ALU.mult), reads=[pt, MT], writes=[pt])
        for i in range(4):
            kt = 16 * i + blk
            b.op("pe", lambda e, i=i, kt=kt: e.matmul(acc[:, 0:65], lhsT=pt[:, i * 128:(i + 1) * 128],
                                                      rhs=vp[kbuf][:, kt, hh * 65:(hh + 1) * 65],
                                                      start=(blk == 0 and i == 0), stop=(blk == a and i == 3)),
                 reads=[pt, vp[kbuf]], writes=[acc])
        if blk == a:
            head = 2 * hp + hh
            r = rec[ctr["acc"] % 2]
            sc, v0 = (0, 1) if hh == 0 else (64, 0)
            b.op("dve", lambda e: e.reciprocal(out=r[:], in_=acc[:, sc:sc + 1]), reads=[acc], writes=[r])
            b.op("dve", lambda e: e.tensor_scalar(out=o_tok[a][:, head * 64:(head + 1) * 64], in0=acc[:, v0:v0 + 64],
                                                  scalar1=r[:, 0:1], scalar2=None, op0=ALU.mult),
                 reads=[acc, r], writes=[o_tok[a]])

    load_slot_small(0)
    for it in pre_list(0):
        b.op(*it)
    for a in range(NS):
        if a + 1 < NS:
            load_slot_small(a + 1)
        kb_next = load_kv(a, 0)
        mask_transposes(a)
        nxt = pre_list(a + 1) if a + 1 < NS else []
        done = 0
        units = [(a, hp, hh, blk) for hp in range(8) for hh in range(2) for blk in range(a + 1)]
        kbufs = {0: kb_next}
        kbufs[1] = load_kv(a, 1)
        qk(units[0], 0, kbufs[0])
        if len(units) > 1:
            qk(units[1], 1, kbufs[units[1][1]])
        for n, u in enumerate(units):
            _, hp, hh, blk = u
            if hh == 0 and blk == 0 and 1 <= hp and hp + 1 < 8:
                kbufs[hp + 1] = load_kv(a, hp + 1)
            if n + 2 < len(units):
                qk(units[n + 2], n + 2, kbufs[units[n + 2][1]])
            softmax_pv(u, n, kbufs[hp])
            want = (n + 1) * len(nxt) // len(units)
            while done < want:
                b.op(*nxt[done])
                done += 1
        while done < len(nxt):
            b.op(*nxt[done])
            done += 1


BF = ml_dtypes.bfloat16


def rep(v, n=128):
    return np.ascontiguousarray(np.tile(np.asarray(v).reshape(1, -1), (n, 1)))


def own_tiles(arr_bs, c):
    bb, j = c // 4, c % 4
    a = arr_bs[bb]
    return np.ascontiguousarray(a.reshape(64, 128, *a.shape[1:])[j::4].reshape(2048, *a.shape[1:]))


def gather_tiles(per_core, bb):
    out = np.empty((64,) + per_core[0].shape[1:], per_core[0].dtype)
    for j in range(4):
        out[j::4] = per_core[bb * 4 + j]
    return out


def diff_masks(c):
    j = c % 4
    m = np.zeros((128, 4, 128), np.float32)
    for i in range(4):
        if i < j:
            m[:, i, :] = 1.0
        elif i == j:
            m[0:64, i, :] = 1.0
            m[64:128, i, 64:128] = 1.0
    return m.reshape(128, 512).astype(BF)


def dsa_negmask(c):
    return np.where(diff_masks(c).astype(np.float32).reshape(128, 4, 128).transpose(2, 1, 0).reshape(128, 512) > 0,
                    0.0, -1e30).astype(np.float32)


def build_L1():
    nc = bass.Bass("TRN2", target_bir_lowering=False)
    b = B(nc)
    io = {
        "x": b.dram("x", [2048, 1024], F32, "ExternalInput"),
        "pos": b.dram("pos", [128, 16], I32, "ExternalInput"),
        "gmix": b.dram("gmix", [128, 1024], F32, "ExternalInput"),
        "gqk": b.dram("gqk", [128, 2048], F32, "ExternalInput"),
        "w_in": b.dram("w_in", [1024, 3072], F32, "ExternalInput"),
        "QT": b.dram("QT", [16, 128, 1024], BF16, "ExternalOutput"),
        "KT": b.dram("KT", [16, 128, 1024], BF16, "ExternalOutput"),
        "V": b.dram("V", [16, 128, 8 * 129], BF16, "ExternalOutput"),
    }
    phase_diff_proj(b, io)
    b.finish()
    return nc


def build_L2():
    nc = bass.Bass("TRN2", target_bir_lowering=False)
    b = B(nc)
    io = {
        "QT": b.dram("QT", [16, 128, 1024], BF16, "ExternalInput"),
        "KTall": b.dram("KTall", [8, 128, 8192], BF16, "ExternalInput"),
        "Vall": b.dram("Vall", [8, 128, 64 * 129], BF16, "ExternalInput"),
        "maskT": b.dram("maskT", [128, 512], BF16, "ExternalInput"),
        "lam": b.dram("lam", [128, 256], F32, "ExternalInput"),
        "gsub": b.dram("gsub", [128, 128], F32, "ExternalInput"),
        "x": b.dram("x", [2048, 1024], F32, "ExternalInput"),
        "pos": b.dram("pos", [128, 16], I32, "ExternalInput"),
        "w_out": b.dram("w_out", [1024, 1024], F32, "ExternalInput"),
        "gmlp": b.dram("gmlp", [128, 1024], F32, "ExternalInput"),
        "w1": b.dram("w1", [1024, 4096], F32, "ExternalInput"),
        "w2": b.dram("w2", [4096, 1024], F32, "ExternalInput"),
        "gmix1": b.dram("gmix1", [128, 1024], F32, "ExternalInput"),
        "gqk2": b.dram("gqk2", [128, 2048], F32, "ExternalInput"),
        "gcq": b.dram("gcq", [128, 256], F32, "ExternalInput"),
        "w_in2": b.dram("w_in2", [1024, 2376], F32, "ExternalInput"),
        "w_uq": b.dram("w_uq", [256, 1024], F32, "ExternalInput"),
        "w_uqi": b.dram("w_uqi", [256, 512], F32, "ExternalInput"),
        "X2": b.dram("X2", [2048, 1024], F32, "ExternalOutput"),
        "QT2": b.dram("QT2", [16, 128, 1024], BF16, "ExternalOutput"),
        "KT2": b.dram("KT2", [16, 128, 1024], BF16, "ExternalOutput"),
        "V2": b.dram("V2", [16, 128, 16 * 65], BF16, "ExternalOutput"),
        "KI": b.dram("KI", [16, 128, 128], BF16, "ExternalOutput"),
        "QI": b.dram("QI", [16, 128, 512], BF16, "ExternalOutput"),
        "SG": b.dram("SG", [16, 128, 8], F32, "ExternalOutput"),
    }
    b.ident(); b.eps()
    pos_t = b.sb([128, NS], I32, "pos")
    b.op("sp", lambda e: e.dma_start(out=pos_t[:], in_=io["pos"][:]), writes=[pos_t], dma="ld_pos")
    cos, sin = rope_tables(b, pos_t)
    o_tok = [b.sb([128, 1024], BF16, f"otok{a}", top=True) for a in range(NS)]
    m1 = b.mark()
    phase_diff_attn(b, io, o_tok)
    b.release(m1)
    x_res = [b.sb([128, 1024], F32, f"xres{a}") for a in range(NS)]
    h2T = b.sb([128, 8, 2048], BF16, "h2T")
    m2 = b.mark()
    phase_post_attn(b, io, o_tok, x_res, h2T, io["w_out"][:], io["gmlp"][:], io["x"])
    b.release(m2)
    b.hi = ARENA_END
    m3 = b.mark()
    phase_mlp(b, x_res, h2T, io["w1"], io["w2"])
    b.release(m3)
    for a in range(NS):
        b.store("sp", lambda e, a=a: e.dma_start(out=io["X2"][a * 128:(a + 1) * 128, :], in_=x_res[a][:]),
                reads=[x_res[a]], dma="st_x2")
    phase_dsa_proj(b, io, x_res, cos, sin)
    b.finish()
    return nc


def l1_inputs(inp, c):
    return {"x": own_tiles(inp["x"], c),
            "pos": np.ascontiguousarray(own_tiles(inp["positions"], c).reshape(16, 128).T),
            "gmix": rep(inp["norm_mix"][0]),
            "gqk": rep(np.concatenate([np.tile(inp["diff_q_norm"][0], 16), np.tile(inp["diff_k_norm"][0], 16)])),
            "w_in": np.ascontiguousarray(inp["diff_w_in"][0])}


def l2_inputs(inp, r1):
    KTall = []; Vall = []
    for bb in range(2):
        kt = gather_tiles([r["KT"].reshape(16, 128, 8, 128) for r in r1], bb)
        KTall.append(np.ascontiguousarray(kt.transpose(2, 1, 0, 3).reshape(8, 128, 8192)))
        v = gather_tiles([r["V"].reshape(16, 128, 8, 129) for r in r1], bb)
        Vall.append(np.ascontiguousarray(v.transpose(2, 1, 0, 3).reshape(8, 128, 64 * 129)))
    lam = rep(np.concatenate([inp["diff_lam_q1"][0], inp["diff_lam_k1"][0], inp["diff_lam_q2"][0], inp["diff_lam_k2"][0]]))
    gqk2 = rep(np.concatenate([np.tile(inp["dsa_q_norm"][0], 16), np.tile(inp["dsa_k_norm"][0], 16)]))
    ins = []
    for c in range(8):
        ins.append({"QT": r1[c]["QT"], "KTall": KTall[c // 4], "Vall": Vall[c // 4], "maskT": diff_masks(c),
                    "lam": lam, "gsub": rep(inp["diff_subln"][0]),
                    "x": own_tiles(inp["x"], c),
                    "pos": np.ascontiguousarray(own_tiles(inp["positions"], c).reshape(16, 128).T),
                    "w_out": np.ascontiguousarray(inp["diff_w_out"][0]), "gmlp": rep(inp["norm_mlp"][0]),
                    "w1": np.ascontiguousarray(inp["mlp_w1"][0]), "w2": np.ascontiguousarray(inp["mlp_w2"][0]),
                    "gmix1": rep(inp["norm_mix"][1]), "gqk2": gqk2, "gcq": rep(inp["dsa_cq_norm"][0]),
                    "w_in2": np.ascontiguousarray(inp["dsa_w_in"][0]), "w_uq": np.ascontiguousarray(inp["dsa_w_uq"][0]),
                    "w_uqi": np.ascontiguousarray(inp["dsa_w_uq_idx"][0])})
    return ins


def run(nc, ins):
    res = run_bass_kernel_spmd(nc, ins, core_ids=list(range(8)))
    return [{k: np.asarray(v) for k, v in r.items()} for r in res.results]


def build_L3():
    nc = bass.Bass("TRN2", target_bir_lowering=False)
    b = B(nc)
    io = {
        "QT2": b.dram("QT2", [16, 128, 1024], BF16, "ExternalInput"),
        "QI": b.dram("QI", [16, 128, 512], BF16, "ExternalInput"),
        "SG": b.dram("SG", [16, 128, 8], F32, "ExternalInput"),
        "KIall": b.dram("KIall", [128, 8192], BF16, "ExternalInput"),
        "KT2all": b.dram("KT2all", [8, 128, 8192], BF16, "ExternalInput"),
        "V2all": b.dram("V2all", [8, 128, 64 * 130], BF16, "ExternalInput"),
        "negmask": b.dram("negmask", [128, 512], F32, "ExternalInput"),
        "X2": b.dram("X2", [2048, 1024], F32, "ExternalInput"),
        "w_out": b.dram("w_out", [1024, 1024], F32, "ExternalInput"),
        "gmlp": b.dram("gmlp", [128, 1024], F32, "ExternalInput"),
        "w1": b.dram("w1", [1024, 4096], F32, "ExternalInput"),
        "w2": b.dram("w2", [4096, 1024], F32, "ExternalInput"),
        "OUT": b.dram("OUT", [2048, 1024], F32, "ExternalOutput"),
    }
    b.ident(); b.eps()
    o_tok = [b.sb([128, 1024], BF16, f"otok{a}", top=True) for a in range(NS)]
    m1 = b.mark()
    phase_dsa_attn(b, io, o_tok)
    b.release(m1)
    x_res = [b.sb([128, 1024], F32, f"xres{a}") for a in range(NS)]
    h2T = b.sb([128, 8, 2048], BF16, "h2T")
    m2 = b.mark()
    phase_post_attn(b, io, o_tok, x_res, h2T, io["w_out"][:], io["gmlp"][:], io["X2"])
    b.release(m2)
    b.hi = ARENA_END
    m3 = b.mark()
    phase_mlp(b, x_res, h2T, io["w1"], io["w2"])
    b.release(m3)
    for a in range(NS):
        b.store("sp", lambda e, a=a: e.dma_start(out=io["OUT"][a * 128:(a + 1) * 128, :], in_=x_res[a][:]),
                reads=[x_res[a]], dma="st_out")
    b.finish()
    return nc


def l3_inputs(inp, r2):
    KIall = []; KTall = []; Vall = []
    for bb in range(2):
        ki = gather_tiles([r["KI"] for r in r2], bb)
        KIall.append(np.ascontiguousarray(ki.transpose(1, 0, 2).reshape(128, 8192)))
        kt = gather_tiles([r["KT2"].reshape(16, 128, 8, 128) for r in r2], bb)
        KTall.append(np.ascontiguousarray(kt.transpose(2, 1, 0, 3).reshape(8, 128, 8192)))
        v = gather_tiles([r["V2"].reshape(16, 128, 8, 130) for r in r2], bb)
        Vall.append(np.ascontiguousarray(v.transpose(2, 1, 0, 3).reshape(8, 128, 64 * 130)))
    ins = []
    for c in range(8):
        ins.append({"QT2": r2[c]["QT2"], "QI": r2[c]["QI"], "SG": r2[c]["SG"], "KIall": KIall[c // 4],
                    "KT2all": KTall[c // 4], "V2all": Vall[c // 4], "negmask": dsa_negmask(c),
                    "X2": r2[c]["X2"], "w_out": np.ascontiguousarray(inp["dsa_w_out"][0]), "gmlp": rep(inp["norm_mlp"][1]),
                    "w1": np.ascontiguousarray(inp["mlp_w1"][1]), "w2": np.ascontiguousarray(inp["mlp_w2"][1])})
    return ins


def assemble(r3):
    out = np.empty((2, 8192, 1024), np.float32)
    for bb in range(2):
        o = gather_tiles([r["OUT"].reshape(16, 128, 1024) for r in r3], bb)
        out[bb] = o.reshape(8192, 1024)
    return out


FUSED_IN = [
    ("x", [2048, 1024], F32), ("pos", [128, 16], I32), ("gmix", [128, 1024], F32), ("gqk", [128, 2048], F32),
    ("w_in", [1024, 3072], F32), ("maskT", [128, 512], BF16), ("lam", [128, 256], F32), ("gsub", [128, 128], F32),
    ("w_out", [1024, 1024], F32), ("gmlp", [128, 1024], F32), ("w1", [1024, 4096], F32), ("w2", [4096, 1024], F32),
    ("gmix1", [128, 1024], F32), ("gqk2", [128, 2048], F32), ("gcq", [128, 256], F32), ("w_in2", [1024, 2376], F32),
    ("w_uq", [256, 1024], F32), ("w_uqi", [256, 512], F32), ("negmask", [128, 512], F32),
    ("w_outb", [1024, 1024], F32), ("gmlpb", [128, 1024], F32), ("w1b", [1024, 4096], F32), ("w2b", [4096, 1024], F32),
]


def build_fused():
    nc = bass.Bass("TRN2", target_bir_lowering=False)
    _RANK.clear()
    b = B(nc)
    io = {n: b.dram(n, s, d, "ExternalInput") for (n, s, d) in FUSED_IN}
    io["OUT"] = b.dram("OUT", [2048, 1024], F32, "ExternalOutput")
    qt2 = nc.dram_tensor("scr_qt2", [16, 128, 1024], BF16).ap()
    qi = nc.dram_tensor("scr_qi", [16, 128, 512], BF16).ap()
    sg = nc.dram_tensor("scr_sg", [16, 128, 8], F32).ap()
    x2 = nc.dram_tensor("scr_x2", [2048, 1024], F32).ap()
    scr = {"QT2": [T(qt2[a]) for a in range(NS)], "QI": [T(qi[a]) for a in range(NS)], "SG": [T(sg[a]) for a in range(NS)]}
    x2d = [T(x2[a * 128:(a + 1) * 128, :]) for a in range(NS)]

    b.ident(); b.eps()
    pos_t = b.sb([128, NS], I32, "pos")
    b.op("sp", lambda e: e.dma_start(out=pos_t[:], in_=io["pos"][:]), writes=[pos_t], dma="ld_pos")
    cos, sin = rope_tables(b, pos_t)
    o_tok = [b.sb([128, 1024], BF16, f"otok{a}", top=True) for a in range(NS)]
    hi_otok = b.hi
    qT_res = b.sb([128, NS, 1024], BF16, "qTres", top=True)
    m0 = b.mark()
    zt = b.sb([128, 4096], BF16, "zeros")
    b.op("pool", lambda e: e.memset(zt[:], 0.0), writes=[zt])
    G0 = Gather(b, "g0", 8, 4096, zt)
    G1 = Gather(b, "g1", 8, 4096, zt)
    GK = Gather(b, "gk", 1, 2048, zt)
    f_diff_proj(b, io, qT_res, G0, cos, sin)
    b.release(m0)
    f_diff_attn(b, io, o_tok, qT_res, G0)
    b.release(m0)
    b.hi = hi_otok
    x_res = [b.sb([128, 1024], F32, f"xres{a}") for a in range(NS)]
    h2T = b.sb([128, 8, 2048], BF16, "h2T")
    m2 = b.mark()
    phase_post_attn(b, io, o_tok, x_res, h2T, io["w_out"][:], io["gmlp"][:], io["x"])
    b.release(m2)
    b.hi = ARENA_END
    m3 = b.mark()
    phase_mlp(b, x_res, h2T, io["w1"], io["w2"])
    b.release(m3)
    for a in range(NS):
        b.op("sp", lambda e, a=a: e.dma_start(out=x2d[a][:], in_=x_res[a][:]), reads=[x_res[a]], writes=[x2d[a]], dma="st_x2")
    io2 = dict(io)
    f_dsa_proj(b, io2, x_res, cos, sin, G1, GK, scr)
    b.release(m0)
    b.hi = ARENA_END
    o_tok = [b.sb([128, 1024], BF16, f"otokb{a}", top=True) for a in range(NS)]
    m4 = b.mark()
    f_dsa_attn(b, io, o_tok, G1, GK, scr)
    b.release(m4)
    x_res = [b.sb([128, 1024], F32, f"xresb{a}") for a in range(NS)]
    h2T = b.sb([128, 8, 2048], BF16, "h2Tb")
    m5 = b.mark()
    phase_post_attn(b, io, o_tok, x_res, h2T, io["w_outb"][:], io["gmlpb"][:], x2, xdeps=x2d)
    b.release(m5)
    b.hi = ARENA_END
    phase_mlp(b, x_res, h2T, io["w1b"], io["w2b"])
    for a in range(NS):
        b.store("sp", lambda e, a=a: e.dma_start(out=io["OUT"][a * 128:(a + 1) * 128, :], in_=x_res[a][:]),
                reads=[x_res[a]], dma="st_out")
    b.finish()
    return nc


def fused_inputs(inp):
    lam = rep(np.concatenate([inp["diff_lam_q1"][0], inp["diff_lam_k1"][0], inp["diff_lam_q2"][0], inp["diff_lam_k2"][0]]))
    gqk = rep(np.concatenate([np.tile(inp["diff_q_norm"][0], 16), np.tile(inp["diff_k_norm"][0], 16)]))
    gqk2 = rep(np.concatenate([np.tile(inp["dsa_q_norm"][0], 16), np.tile(inp["dsa_k_norm"][0], 16)]))
    c_ = np.ascontiguousarray
    shared = {"gmix": rep(inp["norm_mix"][0]), "gqk": gqk, "w_in": c_(inp["diff_w_in"][0]), "lam": lam,
              "gsub": rep(inp["diff_subln"][0]), "w_out": c_(inp["diff_w_out"][0]), "gmlp": rep(inp["norm_mlp"][0]),
              "w1": c_(inp["mlp_w1"][0]), "w2": c_(inp["mlp_w2"][0]), "gmix1": rep(inp["norm_mix"][1]), "gqk2": gqk2,
              "gcq": rep(inp["dsa_cq_norm"][0]), "w_in2": c_(inp["dsa_w_in"][0]), "w_uq": c_(inp["dsa_w_uq"][0]),
              "w_uqi": c_(inp["dsa_w_uq_idx"][0]), "w_outb": c_(inp["dsa_w_out"][0]), "gmlpb": rep(inp["norm_mlp"][1]),
              "w1b": c_(inp["mlp_w1"][1]), "w2b": c_(inp["mlp_w2"][1])}
    ins = []
    for c in range(8):
        d = dict(shared)
        d["x"] = own_tiles(inp["x"], c)
        d["pos"] = np.ascontiguousarray(own_tiles(inp["positions"], c).reshape(16, 128).T)
        d["maskT"] = diff_masks(c)
        d["negmask"] = dsa_negmask(c)
        ins.append(d)
    return ins


def kernel(**inputs):
    inp = {k: np.asarray(v) for k, v in inputs.items()}
    r = run(build_fused(), fused_inputs(inp))
    return assemble(r)
```

```python
import math
from contextlib import ExitStack
import numpy as np
import ml_dtypes
import concourse.bass as bass
import concourse.mybir as mybir
from concourse.bass_utils import run_bass_kernel_spmd


F32 = mybir.dt.float32
BF16 = mybir.dt.bfloat16
I32 = mybir.dt.int32
AF = mybir.ActivationFunctionType
ALU = mybir.AluOpType
AX = mybir.AxisListType


class Dep:
    __slots__ = ("w", "r")

    def __init__(self):
        self.w = None
        self.r = {}


class Sched:
    ENG = ("pe", "act", "dve", "pool", "sp")

    def __init__(self, nc):
        self.nc = nc
        self.ops = {e: [] for e in self.ENG}
        self.cnt = {e: 0 for e in self.ENG}
        self.known = {e: {} for e in self.ENG}
        self.dma_cnt = {}
        self.stack = ExitStack()
        self.nt = 0

    def sb(self, shape, dtype, name=None):
        self.nt += 1
        name = "sb_" + (name or f"t{self.nt}")
        return self.stack.enter_context(self.nc.sbuf_tensor(name, list(shape), dtype))

    def ps(self, shape, dtype, name=None):
        self.nt += 1
        name = "ps_" + (name or f"p{self.nt}")
        return self.stack.enter_context(self.nc.psum_tensor(name, list(shape), dtype))

    def op(self, eng, fn, reads=(), writes=(), dma=None, sem_inc=16):
        waits = {}

        def need(ev, raw):
            if ev is None:
                return
            key, val = ev
            if key == eng:
                if eng == "pe" or not raw:
                    return
            if waits.get(key, 0) < val:
                waits[key] = val

        for d in reads:
            need(d.w, True)
        for d in writes:
            need(d.w, False)
            for ev in d.r.items():
                need(ev, False)
        kn = self.known[eng]
        wl = []
        for key, val in waits.items():
            if kn.get(key, 0) >= val:
                continue
            kn[key] = val
            wl.append((key, val))
        if dma is not None:
            n = self.dma_cnt.get(dma, 0) + sem_inc
            self.dma_cnt[dma] = n
            ev = (dma, n)
            inc = (dma, sem_inc)
        else:
            self.cnt[eng] += 1
            ev = (eng, self.cnt[eng])
            inc = (eng, 1)
        self.ops[eng].append((wl, fn, inc))
        for d in reads:
            if d.r.get(ev[0], 0) < ev[1]:
                d.r[ev[0]] = ev[1]
        for d in writes:
            d.w = ev
            d.r = {}
        return ev

    def final_wait(self, eng, deps):
        waits = {}
        for d in deps:
            for ev in ([d.w] if d.w else []) + list(d.r.items()):
                if waits.get(ev[0], 0) < ev[1]:
                    waits[ev[0]] = ev[1]
        self.ops[eng].append((list(waits.items()), None, None))

    def barrier(self):
        waits = {e: self.cnt[e] for e in self.ENG if self.cnt[e] > 0}
        for k, n in self.dma_cnt.items():
            if not k.startswith("cc_"):
                waits[k] = n
        for e in self.ENG:
            kn = self.known[e]
            wl = []
            for key, val in waits.items():
                if key == e or kn.get(key, 0) >= val:
                    continue
                kn[key] = val
                wl.append((key, val))
            self.ops[e].append((wl, None, None))

    def final_events(self, eng, evs):
        waits = {}
        for ev in evs:
            if waits.get(ev[0], 0) < ev[1]:
                waits[ev[0]] = ev[1]
        self.ops[eng].append((list(waits.items()), None, None))

    def emit(self):
        nc = self.nc
        keys = set(self.ENG) | set(self.dma_cnt.keys())
        assert len(keys) <= 100, f"too many semaphores: {len(keys)}"
        sems = {}
        for k in sorted(keys):
            sems[k] = self.stack.enter_context(nc.semaphore("s_" + k))
        ops = self.ops

        def run(e, lst):
            for wl, fn, inc in lst:
                for key, val in wl:
                    e.wait_ge(sems[key], val)
                if fn is None:
                    continue
                ins = fn(e)
                if inc is not None:
                    ins.then_inc(sems[inc[0]], inc[1])

        with nc.Block() as block:
            @block.tensor
            def _(e):
                run(e, ops["pe"])

            @block.scalar
            def _(e):
                run(e, ops["act"])

            @block.vector
            def _(e):
                run(e, ops["dve"])

            @block.gpsimd
            def _(e):
                run(e, ops["pool"])

            @block.sync
            def _(e):
                run(e, ops["sp"])
        self.stack.close()


NS = 16
D = 1024
EPS = 1e-6
INV_FREQ = [500000.0 ** (-(2.0 * j) / 16.0) for j in range(8)]
TWO_PI_S = 6.28318
PI_S = 3.14159


class T:
    def __init__(self, t, d=None):
        self.t = t
        self.d = d if d is not None else Dep()

    def __getitem__(self, k):
        return self.t[k]


ARENA_BASE = 16512
ARENA_END = 16512 + 212736


class B:
    def __init__(self, nc):
        self.nc = nc
        self.S = Sched(nc)
        self._consts = {}
        self.final = []
        self.arena = nc.alloc_sbuf_tensor("arena", [128, ARENA_END - ARENA_BASE], mybir.dt.uint8)
        self.lo = ARENA_BASE
        self.hi = ARENA_END
        self.nt = 0
        self.banks = [T(nc.alloc_psum_tensor(f"bank{i}", [128, 512], F32)) for i in range(8)]
        self.banks16 = [T(bk.t[:].bitcast(BF16), bk.d) for bk in self.banks]

    def sb(self, shape, dt, name=None, top=False):
        self.nt += 1
        nm = f"sb{self.nt}_{name or 't'}"
        size = int(np.prod(shape[1:])) * mybir.dt.size(dt)
        size = (size + 31) // 32 * 32
        if top:
            self.hi -= size
            off = self.hi
        else:
            off = self.lo
            self.lo += size
        assert self.lo <= self.hi, f"SBUF arena overflow at {nm}: lo={self.lo} hi={self.hi}"
        return T(self.nc.alloc_sbuf_tensor_at(nm, list(shape), dt, offset=off))

    def mark(self):
        return (self.lo, self.hi)

    def release(self, m):
        self.lo, self.hi = m
        self.S.barrier()

    def dram(self, name, shape, dt, kind):
        t = T(self.nc.dram_tensor(name, list(shape), dt, kind=kind).ap())
        return t

    def op(self, eng, fn, reads=(), writes=(), dma=None, sem_inc=16):
        return self.S.op(eng, fn, [x.d for x in reads], [x.d for x in writes], dma, sem_inc)

    def store(self, eng, fn, reads, dma):
        ev = self.S.op(eng, fn, [x.d for x in reads], [], dma)
        self.final.append(ev)
        return ev

    def finish(self):
        self.S.final_events("sp", self.final)
        self.S.emit()

    def ident(self):
        if "ident" not in self._consts:
            idt = self.sb([128, 128], BF16, "ident")
            self.op("pool", lambda e: e.memset(idt[:], 0.0), writes=[idt])
            self.op("pool", lambda e: e.affine_select(out=idt[:], in_=idt[:], pattern=[[-1, 128]],
                                                     compare_op=ALU.not_equal, fill=1.0, base=0,
                                                     channel_multiplier=1), reads=[idt], writes=[idt])
            self._consts["ident"] = idt
        return self._consts["ident"]

    def eps(self):
        if "eps" not in self._consts:
            t = self.sb([128, 1], F32, "eps")
            self.op("pool", lambda e: e.memset(t[:], EPS), writes=[t])
            self._consts["eps"] = t
        return self._consts["eps"]


def rstd_from_ss(b, ss, n, scale):
    eps = b.eps()
    b.op("act", lambda e: e.activation(out=ss[:, 0:n], in_=ss[:, 0:n], func=AF.Ln, bias=eps[:], scale=scale),
         reads=[ss, eps], writes=[ss])
    b.op("act", lambda e: e.activation(out=ss[:, 0:n], in_=ss[:, 0:n], func=AF.Exp, scale=-0.5),
         reads=[ss], writes=[ss])


def rope_tables(b, pos_t):
    posf = b.sb([128, NS], F32, "posf")
    inv = b.sb([128, 8], F32, "invf")
    ang = b.sb([128, NS, 8], F32, "ang")
    ti = b.sb([128, NS, 8], I32, "angi")
    tf = b.sb([128, NS, 8], F32, "angf")
    neg = b.sb([128, NS, 8], F32, "angn")
    cos = b.sb([128, NS, 8], F32, "cos")
    sin = b.sb([128, NS, 8], F32, "sin")
    nb = b.sb([128, 1], F32, "negpi")
    b.op("pool", lambda e: e.memset(nb[:], -PI_S), writes=[nb])
    b.op("dve", lambda e: e.tensor_copy(out=posf[:], in_=pos_t[:]), reads=[pos_t], writes=[posf])
    for j in range(8):
        b.op("pool", lambda e, j=j: e.memset(inv[:, j:j + 1], INV_FREQ[j] / (2 * math.pi)), writes=[inv])
    b.op("dve", lambda e: e.tensor_tensor(out=ang[:], in0=posf[:].unsqueeze(2).to_broadcast([128, NS, 8]),
                                          in1=inv[:].unsqueeze(1).to_broadcast([128, NS, 8]), op=ALU.mult),
         reads=[posf, inv], writes=[ang])
    for (dst, off) in ((sin, 0.5), (cos, 0.75)):
        b.op("dve", lambda e, off=off: e.tensor_scalar(out=tf[:], in0=ang[:], scalar1=off, scalar2=None, op0=ALU.add),
             reads=[ang], writes=[tf])
        b.op("dve", lambda e: e.tensor_copy(out=ti[:], in_=tf[:]), reads=[tf], writes=[ti])
        b.op("dve", lambda e: e.tensor_copy(out=neg[:], in_=ti[:]), reads=[ti], writes=[neg])
        b.op("dve", lambda e: e.tensor_tensor(out=tf[:], in0=tf[:], in1=neg[:], op=ALU.subtract),
             reads=[tf, neg], writes=[tf])
        b.op("dve", lambda e: e.tensor_scalar(out=neg[:], in0=tf[:], scalar1=0.0, scalar2=None, op0=ALU.is_lt),
             reads=[tf], writes=[neg])
        b.op("dve", lambda e: e.tensor_tensor(out=tf[:], in0=tf[:], in1=neg[:], op=ALU.add),
             reads=[tf, neg], writes=[tf])
        b.op("act", lambda e, dst=dst: e.activation(out=dst[:], in_=tf[:], func=AF.Sin, bias=nb[:], scale=TWO_PI_S),
             reads=[tf, nb], writes=[dst])
    return cos, sin


def rmsnorm_transpose(b, xt, gt, hbf, hT, pT, ss, junk):
    idt = b.ident()
    b.op("act", lambda e: e.activation(out=junk[:], in_=xt[:], func=AF.Square, accum_out=ss[:, 0:1]),
         reads=[xt], writes=[junk, ss])
    rstd_from_ss(b, ss, 1, 1.0 / D)
    b.op("dve", lambda e: e.scalar_tensor_tensor(out=hbf[:], in0=xt[:], scalar=ss[:, 0:1], in1=gt[:],
                                                 op0=ALU.mult, op1=ALU.mult),
         reads=[xt, ss, gt], writes=[hbf])
    for kc in range(8):
        b.op("pe", lambda e, kc=kc: e.transpose(out=pT[:, kc * 128:(kc + 1) * 128],
                                                in_=hbf[:, kc * 128:(kc + 1) * 128], identity=idt[:]),
             reads=[hbf, idt], writes=[pT])
    b.op("dve", lambda e: e.tensor_copy(out=hT[:].rearrange("p a b -> p (a b)"), in_=pT[:]),
         reads=[pT], writes=[hT])


def headnorm_rope(b, stage, sq, ssq, nh, gain, cos_a, sin_a, outbf, tmp, norm=True):
    s3 = stage[:].rearrange("p (h d) -> p h d", d=64)
    if norm:
        b.op("dve", lambda e: e.tensor_reduce(out=ssq[:, 0:nh], in_=sq[:].rearrange("p (h d) -> p h d", d=64),
                                              axis=AX.X, op=ALU.add), reads=[sq], writes=[ssq])
        rstd_from_ss(b, ssq, nh, 1.0 / 64)
    b.op("dve", lambda e: e.tensor_tensor(out=s3, in0=s3, in1=ssq[:, 0:nh].unsqueeze(2).to_broadcast([128, nh, 64]),
                                          op=ALU.mult), reads=[stage, ssq], writes=[stage])
    if gain is not None:
        b.op("pool", lambda e: e.tensor_tensor(out=stage[:], in0=stage[:], in1=gain[:], op=ALU.mult),
             reads=[stage, gain], writes=[stage])
    b.op("act", lambda e: e.activation(out=outbf[:], in_=stage[:], func=AF.Copy), reads=[stage], writes=[outbf])
    o3 = outbf[:].rearrange("p (h d) -> p h d", d=64)
    x1 = s3[:, :, 0:8]
    x2 = s3[:, :, 8:16]
    cb = cos_a.unsqueeze(1).to_broadcast([128, nh, 8])
    sb_ = sin_a.unsqueeze(1).to_broadcast([128, nh, 8])
    t = tmp[:].rearrange("p (k h d) -> p k h d", k=4, d=8)
    eng = "pool"
    b.op(eng, lambda e: e.tensor_tensor(out=t[:, 0, 0:nh, :], in0=x1, in1=cb, op=ALU.mult), reads=[stage], writes=[tmp])
    b.op(eng, lambda e: e.tensor_tensor(out=t[:, 1, 0:nh, :], in0=x2, in1=sb_, op=ALU.mult), reads=[stage], writes=[tmp])
    b.op(eng, lambda e: e.tensor_tensor(out=t[:, 2, 0:nh, :], in0=x2, in1=cb, op=ALU.mult), reads=[stage], writes=[tmp])
    b.op(eng, lambda e: e.tensor_tensor(out=t[:, 3, 0:nh, :], in0=x1, in1=sb_, op=ALU.mult), reads=[stage], writes=[tmp])
    b.op(eng, lambda e: e.tensor_tensor(out=o3[:, :, 0:8], in0=t[:, 0, 0:nh, :], in1=t[:, 1, 0:nh, :], op=ALU.subtract),
         reads=[tmp], writes=[outbf])
    b.op(eng, lambda e: e.tensor_tensor(out=o3[:, :, 8:16], in0=t[:, 2, 0:nh, :], in1=t[:, 3, 0:nh, :], op=ALU.add),
         reads=[tmp], writes=[outbf])


def phase_diff_proj(b, io):
    idt = b.ident()
    pos_t = b.sb([128, NS], I32, "pos")
    b.op("sp", lambda e: e.dma_start(out=pos_t[:], in_=io["pos"][:]), writes=[pos_t], dma="ld_pos")
    gmix = b.sb([128, D], F32, "gmix")
    b.op("sp", lambda e: e.dma_start(out=gmix[:], in_=io["gmix"][:]), writes=[gmix], dma="ld_gmix")
    gqk = b.sb([128, 2048], F32, "gqk")
    b.op("sp", lambda e: e.dma_start(out=gqk[:], in_=io["gqk"][:]), writes=[gqk], dma="ld_gqk")
    b.op("pool", lambda e: e.tensor_scalar(out=gqk[:, 0:1024], in0=gqk[:, 0:1024], scalar1=0.125, scalar2=None,
                                           op0=ALU.mult), reads=[gqk], writes=[gqk])
    w = b.sb([128, 8, 3072], BF16, "w_in")
    for kc in range(8):
        for hf in range(3):
            b.op("pool", lambda e, kc=kc, hf=hf: e.dma_start(
                out=w[:, kc, hf * 1024:(hf + 1) * 1024],
                in_=io["w_in"][kc * 128:(kc + 1) * 128, hf * 1024:(hf + 1) * 1024]),
                writes=[w], dma="ld_w")
    cos, sin = rope_tables(b, pos_t)

    xts = [b.sb([128, D], F32, f"xt{i}") for i in range(2)]
    junk = b.sb([128, D], BF16, "junk")
    ss = b.sb([128, 1], F32, "ss")
    hbf = b.sb([128, D], BF16, "hbf")
    hT = b.sb([128, 8, 128], BF16, "hT")
    pT = [b.banks16[0], b.banks16[1]]
    pY = [b.banks[2 + i] for i in range(4)]
    stage = b.sb([128, 2048], F32, "stage")
    sq = b.sb([128, 2048], F32, "sq")
    ssq = b.sb([128, 32], F32, "ssq")
    tmp = b.sb([128, 4 * 32 * 8], F32, "ropetmp")
    qkbf = b.sb([128, 2048], BF16, "qkbf")
    qkT = [b.sb([128, 16, 128], BF16, f"qkT{i}") for i in range(2)]
    vaug = [b.sb([128, 8, 129], BF16, f"vaug{i}") for i in range(2)]
    for i in range(2):
        b.op("pool", lambda e, i=i: e.memset(vaug[i][:], 1.0), writes=[vaug[i]])

    for a in range(NS):
        xt = xts[a % 2]
        b.op("sp", lambda e, a=a, xt=xt: e.dma_start(out=xt[:], in_=io["x"][a * 128:(a + 1) * 128, :]),
             writes=[xt], dma=f"ld_x{a % 2}")
        rmsnorm_transpose(b, xt, gmix, hbf, hT, pT[0], ss, junk)
        for n in range(6):
            py = pY[n % 4]
            for kc in range(8):
                b.op("pe", lambda e, n=n, kc=kc, py=py: e.matmul(py[:], lhsT=hT[:, kc, :],
                                                                   rhs=w[:, kc, n * 512:(n + 1) * 512],
                                                                   start=(kc == 0), stop=(kc == 7)),
                     reads=[hT, w], writes=[py])
            if n < 4:
                b.op("act", lambda e, n=n, py=py: e.activation(out=stage[:, n * 512:(n + 1) * 512], in_=py[:], func=AF.Copy),
                     reads=[py], writes=[stage])
                b.op("act", lambda e, n=n, py=py: e.activation(out=sq[:, n * 512:(n + 1) * 512], in_=py[:], func=AF.Square),
                     reads=[py], writes=[sq])
            else:
                va = vaug[a % 2]
                b.op("dve", lambda e, n=n, py=py, va=va: e.tensor_copy(
                    out=va[:, (n - 4) * 4:(n - 4) * 4 + 4, 0:128], in_=py[:].rearrange("p (h d) -> p h d", d=128)),
                    reads=[py], writes=[va])
        va = vaug[a % 2]
        b.store("sp", lambda e, a=a, va=va: e.dma_start(out=io["V"][a], in_=va[:].rearrange("p h d -> p (h d)")),
                reads=[va], dma=f"st_v{a % 2}")
        headnorm_rope(b, stage, sq, ssq, 32, gqk, cos[:, a, :], sin[:, a, :], qkbf, tmp)
        qt = qkT[a % 2]
        for i in range(16):
            pt = pT[1] if i < 8 else pT[0]
            b.op("pe", lambda e, i=i, pt=pt: e.transpose(out=pt[:, (i % 8) * 128:(i % 8 + 1) * 128],
                                                         in_=qkbf[:, i * 128:(i + 1) * 128], identity=idt[:]),
                 reads=[qkbf, idt], writes=[pt])
            if i % 8 == 7:
                b.op("dve", lambda e, i=i, pt=pt, qt=qt: e.tensor_copy(
                    out=qt[:, (i // 8) * 8:(i // 8) * 8 + 8, :].rearrange("p a b -> p (a b)"), in_=pt[:]),
                    reads=[pt], writes=[qt])
        b.store("sp", lambda e, a=a, qt=qt: e.dma_start(out=io["QT"][a], in_=qt[:, 0:8, :].rearrange("p a b -> p (a b)")),
                reads=[qt], dma=f"st_q{a % 2}")
        b.store("sp", lambda e, a=a, qt=qt: e.dma_start(out=io["KT"][a], in_=qt[:, 8:16, :].rearrange("p a b -> p (a b)")),
                reads=[qt], dma=f"st_k{a % 2}")


def phase_diff_attn(b, io, o_tok):
    LAM_INIT = 0.2
    qT = b.sb([128, NS, 1024], BF16, "qT")
    for a in range(NS):
        b.op("sp", lambda e, a=a: e.dma_start(out=qT[:, a, :], in_=io["QT"][a]), writes=[qT], dma="ld_qT")
    maskT = b.sb([128, 512], BF16, "maskT")
    b.op("sp", lambda e: e.dma_start(out=maskT[:], in_=io["maskT"][:]), writes=[maskT], dma="ld_mask")
    lam = b.sb([128, 256], F32, "lam")
    b.op("sp", lambda e: e.dma_start(out=lam[:], in_=io["lam"][:]), writes=[lam], dma="ld_lam")
    gsub = b.sb([128, 128], F32, "gsub")
    b.op("sp", lambda e: e.dma_start(out=gsub[:], in_=io["gsub"][:]), writes=[gsub], dma="ld_gsub")
    b.op("pool", lambda e: e.tensor_scalar(out=gsub[:], in0=gsub[:], scalar1=1.0 - LAM_INIT, scalar2=None,
                                           op0=ALU.mult), reads=[gsub], writes=[gsub])
    lprod = b.sb([128, 128], F32, "lprod")
    l2 = b.sb([128, 2], F32, "l2")
    neglam = b.sb([128, 1], F32, "neglam")
    l4 = lam[:].rearrange("p (a b d) -> p a b d", a=2, b=2)
    b.op("dve", lambda e: e.tensor_tensor(out=lprod[:].rearrange("p (a d) -> p a d", a=2), in0=l4[:, :, 0, :],
                                          in1=l4[:, :, 1, :], op=ALU.mult), reads=[lam], writes=[lprod])
    b.op("dve", lambda e: e.tensor_reduce(out=l2[:], in_=lprod[:].rearrange("p (a d) -> p a d", a=2), axis=AX.X,
                                          op=ALU.add), reads=[lprod], writes=[l2])
    b.op("act", lambda e: e.activation(out=l2[:], in_=l2[:], func=AF.Exp), reads=[l2], writes=[l2])
    b.op("dve", lambda e: e.tensor_tensor(out=neglam[:], in0=l2[:, 1:2], in1=l2[:, 0:1], op=ALU.subtract),
         reads=[l2], writes=[neglam])
    b.op("dve", lambda e: e.tensor_scalar(out=neglam[:], in0=neglam[:], scalar1=-LAM_INIT, scalar2=None, op0=ALU.add),
         reads=[neglam], writes=[neglam])

    ktb = [b.sb([128, 8192], BF16, f"ktb{i}") for i in range(2)]
    vb = [b.sb([128, 64, 129], BF16, f"vb{i}") for i in range(2)]
    pS = [[b.banks[c * 2 + i] for i in range(2)] for c in range(2)]
    pA = [[b.banks[4 + c * 2 + i] for i in range(2)] for c in range(2)]
    pT = [[b.sb([128, 512], BF16, f"pTs{c}{i}") for i in range(2)] for c in range(2)]
    rec = [b.sb([128, 2], F32, f"rec{i}") for i in range(2)]
    o32 = [b.sb([128, 128], F32, f"o32{i}") for i in range(2)]
    oj = b.sb([128, 128], BF16, "ojunk")
    ss1 = [b.sb([128, 1], F32, f"ss1{i}") for i in range(2)]

    units = [(h, a, blk) for h in range(8) for a in range(NS) for blk in range(a + 1)]

    def load_head(h):
        kb, vv = ktb[h % 2], vb[h % 2]
        for part in range(4):
            b.op("sp", lambda e, h=h, kb=kb, part=part: e.dma_start(
                out=kb[:, part * 2048:(part + 1) * 2048], in_=io["KTall"][h][:, part * 2048:(part + 1) * 2048]),
                writes=[kb], dma=f"ld_kt{h % 2}")
            b.op("sp", lambda e, h=h, vv=vv, part=part: e.dma_start(
                out=vv[:, part * 16:(part + 1) * 16, :].rearrange("p a b -> p (a b)"),
                in_=io["Vall"][h][:, part * 16 * 129:(part + 1) * 16 * 129]),
                writes=[vv], dma=f"ld_v{h % 2}")

    def qk(u, n):
        h, a, blk = u
        kb = ktb[h % 2]
        for c in range(2):
            ps = pS[c][n % 2]
            for i in range(4):
                kt = 4 * blk + i
                b.op("pe", lambda e, c=c, i=i, kt=kt, ps=ps, kb=kb, a=a, h=h: e.matmul(
                    ps[:, i * 128:(i + 1) * 128], lhsT=kb[64 * c:64 * c + 64, kt * 128:(kt + 1) * 128],
                    rhs=qT[64 * c:64 * c + 64, a, h * 128:(h + 1) * 128], start=True, stop=True),
                    reads=[kb, qT], writes=[ps])

    def softmax_pv(u, n):
        h, a, blk = u
        vv = vb[h % 2]
        for c in range(2):
            ps = pS[c][n % 2]
            pt = pT[c][n % 2]
            acc = pA[c][a % 2]
            b.op("act", lambda e, ps=ps, pt=pt: e.activation(out=pt[:], in_=ps[:], func=AF.Exp),
                 reads=[ps], writes=[pt])
            if blk == a:
                b.op("pool", lambda e, pt=pt: e.tensor_tensor(out=pt[:], in0=pt[:], in1=maskT[:], op=ALU.mult),
                     reads=[pt, maskT], writes=[pt])
            for i in range(4):
                kt = 4 * blk + i
                b.op("pe", lambda e, i=i, kt=kt, pt=pt, acc=acc, vv=vv, blk=blk, a=a: e.matmul(
                    acc[:, 0:129], lhsT=pt[:, i * 128:(i + 1) * 128], rhs=vv[:, kt, :],
                    start=(blk == 0 and i == 0), stop=(blk == a and i == 3)),
                    reads=[pt, vv], writes=[acc])
        if blk == a:
            evac(h, a)

    def evac(h, a):
        k = a % 2
        a0, a1 = pA[0][k], pA[1][k]
        r, o, s = rec[k], o32[k], ss1[k]
        b.op("dve", lambda e: e.reciprocal(out=r[:, 0:1], in_=a0[:, 128:129]), reads=[a0], writes=[r])
        b.op("dve", lambda e: e.reciprocal(out=r[:, 1:2], in_=a1[:, 128:129]), reads=[a1], writes=[r])
        b.op("dve", lambda e: e.tensor_tensor(out=r[:, 1:2], in0=r[:, 1:2], in1=neglam[:], op=ALU.mult),
             reads=[r, neglam], writes=[r])
        b.op("dve", lambda e: e.tensor_scalar(out=o[:], in0=a0[:, 0:128], scalar1=r[:, 0:1], scalar2=None, op0=ALU.mult),
             reads=[a0, r], writes=[o])
        b.op("dve", lambda e: e.scalar_tensor_tensor(out=o[:], in0=a1[:, 0:128], scalar=r[:, 1:2], in1=o[:],
                                                     op0=ALU.mult, op1=ALU.add), reads=[a1, r, o], writes=[o])
        b.op("act", lambda e: e.activation(out=oj[:], in_=o[:], func=AF.Square, accum_out=s[:, 0:1]),
             reads=[o], writes=[oj, s])
        rstd_from_ss(b, s, 1, 1.0 / 128)
        ot = o_tok[a]
        b.op("dve", lambda e: e.scalar_tensor_tensor(out=ot[:, h * 128:(h + 1) * 128], in0=o[:], scalar=s[:, 0:1],
                                                     in1=gsub[:], op0=ALU.mult, op1=ALU.mult),
             reads=[o, s, gsub], writes=[ot])

    load_head(0)
    qk(units[0], 0)
    for n, u in enumerate(units):
        if u[1] == 0 and u[2] == 0 and u[0] + 1 < 8:
            load_head(u[0] + 1)
        if n + 1 < len(units):
            qk(units[n + 1], n + 1)
        softmax_pv(u, n)


def load_w_bf16(b, wt, src, nk, ncols, key, colchunk=1024):
    for c0 in range(0, ncols, colchunk):
        for kc in range(nk):
            c1 = min(ncols, c0 + colchunk)
            b.op("pool", lambda e, kc=kc, c0=c0, c1=c1: e.dma_start(
                out=wt[:, kc, c0:c1], in_=src[kc * 128:(kc + 1) * 128, c0:c1]), writes=[wt], dma=key)


def phase_post_attn(b, io, o_tok, x_res, h2T, wout_ap, gmlp_ap, xsrc, xdeps=None):
    idt = b.ident()
    m = b.mark()
    wout = b.sb([128, 8, 1024], BF16, "wout")
    load_w_bf16(b, wout, wout_ap, 8, 1024, "ld_wout")
    gm = b.sb([128, D], F32, "gmlp")
    b.op("sp", lambda e: e.dma_start(out=gm[:], in_=gmlp_ap), writes=[gm], dma="ld_gmlp")
    oT = [b.sb([128, 8, 128], BF16, f"oT{i}") for i in range(2)]
    junk = b.sb([128, D], BF16, "junk2")
    ss = [b.sb([128, 1], F32, f"ss2{i}") for i in range(2)]
    hbf = [b.sb([128, D], BF16, f"hbf2{i}") for i in range(2)]
    for a in range(NS):
        xr = x_res[a]
        b.op("sp", lambda e, a=a, xr=xr: e.dma_start(out=xr[:], in_=xsrc[a * 128:(a + 1) * 128, :]),
             reads=([xdeps[a]] if xdeps else []), writes=[xr], dma="ld_xres")
        pt = b.banks16[a % 2]
        ot = oT[a % 2]
        for kc in range(8):
            b.op("pe", lambda e, kc=kc, pt=pt, a=a: e.transpose(out=pt[:, kc * 128:(kc + 1) * 128],
                                                                 in_=o_tok[a][:, kc * 128:(kc + 1) * 128], identity=idt[:]),
                 reads=[o_tok[a], idt], writes=[pt])
        b.op("act", lambda e, pt=pt, ot=ot: e.activation(out=ot[:].rearrange("p a b -> p (a b)"), in_=pt[:], func=AF.Copy),
             reads=[pt], writes=[ot])
        for n in range(2):
            py = b.banks[2 + (2 * a + n) % 4]
            for kc in range(8):
                b.op("pe", lambda e, kc=kc, n=n, py=py, ot=ot: e.matmul(py[:], lhsT=ot[:, kc, :],
                                                                         rhs=wout[:, kc, n * 512:(n + 1) * 512],
                                                                         start=(kc == 0), stop=(kc == 7)),
                     reads=[ot, wout], writes=[py])
            b.op("dve", lambda e, n=n, py=py, xr=xr: e.tensor_tensor(out=xr[:, n * 512:(n + 1) * 512],
                                                                     in0=xr[:, n * 512:(n + 1) * 512], in1=py[:], op=ALU.add),
                 reads=[xr, py], writes=[xr])
        rms_to_hT(b, xr, gm, hbf[a % 2], h2T, a, b.banks16[6 + a % 2], ss[a % 2], junk)
    return m


def rms_to_hT(b, xr, gm, hbf, h2T, a, pt, ss, junk):
    idt = b.ident()
    b.op("act", lambda e: e.activation(out=junk[:], in_=xr[:], func=AF.Square, accum_out=ss[:, 0:1]),
         reads=[xr], writes=[junk, ss])
    rstd_from_ss(b, ss, 1, 1.0 / D)
    b.op("dve", lambda e: e.scalar_tensor_tensor(out=hbf[:], in0=xr[:], scalar=ss[:, 0:1], in1=gm[:],
                                                 op0=ALU.mult, op1=ALU.mult), reads=[xr, ss, gm], writes=[hbf])
    for kc in range(8):
        b.op("pe", lambda e, kc=kc: e.transpose(out=pt[:, kc * 128:(kc + 1) * 128],
                                                in_=hbf[:, kc * 128:(kc + 1) * 128], identity=idt[:]),
             reads=[hbf, idt], writes=[pt])
    b.op("act", lambda e: e.activation(out=h2T[:, :, a * 128:(a + 1) * 128],
                                       in_=pt[:].rearrange("p (k t) -> p k t", k=8), func=AF.Copy),
         reads=[pt], writes=[h2T])


def phase_mlp(b, x_res, h2T, w1_ap, w2_ap):
    NFC = 8
    w1c = [b.sb([128, 8, 512], BF16, f"w1c{i}") for i in range(2)]
    w2c = [b.sb([128, 4, 1024], BF16, f"w2c{i}") for i in range(2)]
    rbuf = [b.sb([128, 512], F32, f"rbuf{i}") for i in range(2)]
    uT = [b.sb([128, 4, 512], BF16, f"uT{i}") for i in range(2)]
    pu = [b.banks[0], b.banks[1]]
    po = [b.banks[2 + i] for i in range(4)]

    def load_chunk(fc):
        w1, w2 = w1c[fc % 2], w2c[fc % 2]
        for kc in range(8):
            b.op("pool", lambda e, kc=kc, fc=fc, w1=w1: e.dma_start(
                out=w1[:, kc, :], in_=w1_ap[kc * 128:(kc + 1) * 128, fc * 512:(fc + 1) * 512]),
                writes=[w1], dma=f"ld_w1{fc % 2}")
        for ft in range(4):
            b.op("pool", lambda e, ft=ft, fc=fc, w2=w2: e.dma_start(
                out=w2[:, ft, :], in_=w2_ap[fc * 512 + ft * 128:fc * 512 + (ft + 1) * 128, :]),
                writes=[w2], dma=f"ld_w2{fc % 2}")

    steps = [(fc, tg) for fc in range(NFC) for tg in range(4)]
    cnt = {"u": 0, "o": 0}

    def stage_u(fc, tg):
        w1 = w1c[fc % 2]
        ut = uT[(fc * 4 + tg) % 2]
        for ft in range(4):
            p = pu[cnt["u"] % 2]
            r = rbuf[cnt["u"] % 2]
            cnt["u"] += 1
            for kc in range(8):
                b.op("pe", lambda e, kc=kc, ft=ft, p=p, w1=w1, tg=tg: e.matmul(
                    p[:], lhsT=w1[:, kc, ft * 128:(ft + 1) * 128], rhs=h2T[:, kc, tg * 512:(tg + 1) * 512],
                    start=(kc == 0), stop=(kc == 7)), reads=[w1, h2T], writes=[p])
            b.op("act", lambda e, p=p, r=r: e.activation(out=r[:], in_=p[:], func=AF.Relu), reads=[p], writes=[r])
            b.op("pool", lambda e, r=r, ut=ut, ft=ft: e.tensor_tensor(out=ut[:, ft, :], in0=r[:], in1=r[:], op=ALU.mult),
                 reads=[r], writes=[ut])

    def stage_o(fc, tg):
        w2 = w2c[fc % 2]
        ut = uT[(fc * 4 + tg) % 2]
        for tt in range(4):
            xr = x_res[tg * 4 + tt]
            for ch in range(2):
                p = po[cnt["o"] % 4]
                cnt["o"] += 1
                for ft in range(4):
                    b.op("pe", lambda e, ft=ft, tt=tt, ch=ch, p=p, ut=ut, w2=w2: e.matmul(
                        p[:], lhsT=ut[:, ft, tt * 128:(tt + 1) * 128], rhs=w2[:, ft, ch * 512:(ch + 1) * 512],
                        start=(ft == 0), stop=(ft == 3)), reads=[ut, w2], writes=[p])
                b.op("dve", lambda e, ch=ch, p=p, xr=xr: e.tensor_tensor(
                    out=xr[:, ch * 512:(ch + 1) * 512], in0=xr[:, ch * 512:(ch + 1) * 512], in1=p[:], op=ALU.add),
                    reads=[xr, p], writes=[xr])

    load_chunk(0)
    stage_u(*steps[0])
    for i, (fc, tg) in enumerate(steps):
        if tg == 0 and fc + 1 < NFC:
            load_chunk(fc + 1)
        if i + 1 < len(steps):
            stage_u(*steps[i + 1])
        stage_o(fc, tg)


IDX_SCALE = (8 ** -0.5) * (64 ** -0.5)


def phase_dsa_proj(b, io, x_res, cos, sin):
    idt = b.ident()
    gmix = b.sb([128, D], F32, "gmix1")
    b.op("sp", lambda e: e.dma_start(out=gmix[:], in_=io["gmix1"][:]), writes=[gmix], dma="ld_gmix1")
    gqk = b.sb([128, 2048], F32, "gqk2")
    b.op("sp", lambda e: e.dma_start(out=gqk[:], in_=io["gqk2"][:]), writes=[gqk], dma="ld_gqk2")
    b.op("pool", lambda e: e.tensor_scalar(out=gqk[:, 0:1024], in0=gqk[:, 0:1024], scalar1=0.125, scalar2=None,
                                           op0=ALU.mult), reads=[gqk], writes=[gqk])
    gcq = b.sb([128, 256], F32, "gcq")
    b.op("sp", lambda e: e.dma_start(out=gcq[:], in_=io["gcq"][:]), writes=[gcq], dma="ld_gcq")
    w = b.sb([128, 8, 2376], BF16, "w_in2")
    load_w_bf16(b, w, io["w_in2"], 8, 2376, "ld_w2in", colchunk=792)
    wuq = b.sb([128, 2, 1024], BF16, "wuq")
    load_w_bf16(b, wuq, io["w_uq"], 2, 1024, "ld_wuq")
    wuqi = b.sb([128, 2, 512], BF16, "wuqi")
    load_w_bf16(b, wuqi, io["w_uqi"], 2, 512, "ld_wuqi")

    junk = b.sb([128, D], BF16, "junk3")
    ss = b.sb([128, 1], F32, "ss3")
    hbf = b.sb([128, D], BF16, "hbf3")
    hT = b.sb([128, 8, 128], BF16, "hT3")
    stage = b.sb([128, 2048], F32, "stage3")
    sq = b.sb([128, 2048], F32, "sq3")
    ssq = b.sb([128, 32], F32, "ssq3")
    tmp = b.sb([128, 4 * 32 * 8], F32, "ropetmp3")
    qkbf = b.sb([128, 2048], BF16, "qkbf3")
    qkT = [b.sb([128, 16, 128], BF16, f"qkT3{i}") for i in range(2)]
    vaug = [b.sb([128, 16, 65], BF16, f"vaug3{i}") for i in range(2)]
    for i in range(2):
        b.op("pool", lambda e, i=i: e.memset(vaug[i][:], 1.0), writes=[vaug[i]])
    cqs = b.sb([128, 256], F32, "cqs")
    ssc = b.sb([128, 1], F32, "ssc")
    cqbf = b.sb([128, 256], BF16, "cqbf")
    cqT = b.sb([128, 2, 128], BF16, "cqT")
    kis = b.sb([128, 64], F32, "kis")
    ksq = b.sb([128, 64], F32, "ksq")
    kss = b.sb([128, 1], F32, "kss")
    kibf = b.sb([128, 128], BF16, "kibf")
    kiT = [b.sb([128, 128], BF16, f"kiT{i}") for i in range(2)]
    wi = b.sb([128, 8], F32, "wi")
    sgn = [b.sb([128, 8], F32, f"sgn{i}") for i in range(2)]
    aw = b.sb([128, 8], F32, "aw")
    qis = b.sb([128, 512], F32, "qis")
    qibf = b.sb([128, 512], BF16, "qibf")
    qiT = [b.sb([128, 4, 128], BF16, f"qiT{i}") for i in range(2)]
    nbank = [0]

    def bank():
        nbank[0] += 1
        return b.banks[2 + nbank[0] % 4]

    def proj(py, lhs, nk, rhs_fn, ncol):
        for kc in range(nk):
            b.op("pe", lambda e, kc=kc: e.matmul(py[:, 0:ncol], lhsT=lhs[:, kc, :], rhs=rhs_fn(kc),
                                                 start=(kc == 0), stop=(kc == nk - 1)), reads=[lhs, w, wuq, wuqi], writes=[py])

    for a in range(NS):
        xr = x_res[a]
        rmsnorm_transpose(b, xr, gmix, hbf, hT, b.banks16[0], ss, junk)
        py = bank()
        proj(py, hT, 8, lambda kc: w[:, kc, 0:256], 256)
        b.op("act", lambda e, py=py: e.activation(out=cqs[:], in_=py[:, 0:256], func=AF.Copy), reads=[py], writes=[cqs])
        b.op("act", lambda e, py=py: e.activation(out=junk[:, 0:256], in_=py[:, 0:256], func=AF.Square, accum_out=ssc[:, 0:1]),
             reads=[py], writes=[junk, ssc])
        rstd_from_ss(b, ssc, 1, 1.0 / 256)
        b.op("dve", lambda e: e.scalar_tensor_tensor(out=cqbf[:], in0=cqs[:], scalar=ssc[:, 0:1], in1=gcq[:],
                                                     op0=ALU.mult, op1=ALU.mult), reads=[cqs, ssc, gcq], writes=[cqbf])
        p6 = b.banks16[6]
        for kc in range(2):
            b.op("pe", lambda e, kc=kc: e.transpose(out=p6[:, kc * 128:(kc + 1) * 128], in_=cqbf[:, kc * 128:(kc + 1) * 128],
                                                    identity=idt[:]), reads=[cqbf, idt], writes=[p6])
        b.op("dve", lambda e: e.tensor_copy(out=cqT[:].rearrange("p a b -> p (a b)"), in_=p6[:, 0:256]),
             reads=[p6], writes=[cqT])
        for n in range(2):
            py = bank()
            proj(py, hT, 8, lambda kc, n=n: w[:, kc, 256 + n * 512:256 + (n + 1) * 512], 512)
            b.op("act", lambda e, py=py, n=n: e.activation(out=stage[:, 1024 + n * 512:1024 + (n + 1) * 512], in_=py[:], func=AF.Copy),
                 reads=[py], writes=[stage])
            b.op("act", lambda e, py=py, n=n: e.activation(out=sq[:, 1024 + n * 512:1024 + (n + 1) * 512], in_=py[:], func=AF.Square),
                 reads=[py], writes=[sq])
        va = vaug[a % 2]
        for n in range(2):
            py = bank()
            proj(py, hT, 8, lambda kc, n=n: w[:, kc, 1280 + n * 512:1280 + (n + 1) * 512], 512)
            b.op("dve", lambda e, py=py, n=n, va=va: e.tensor_copy(out=va[:, n * 8:(n + 1) * 8, 0:64],
                                                                   in_=py[:].rearrange("p (h d) -> p h d", d=64)),
                 reads=[py], writes=[va])
        b.store("sp", lambda e, a=a, va=va: e.dma_start(out=io["V2"][a], in_=va[:].rearrange("p h d -> p (h d)")),
                reads=[va], dma=f"st_v2{a % 2}")
        py = bank()
        proj(py, hT, 8, lambda kc: w[:, kc, 2304:2376], 72)
        b.op("act", lambda e, py=py: e.activation(out=kis[:], in_=py[:, 0:64], func=AF.Copy), reads=[py], writes=[kis])
        b.op("act", lambda e, py=py: e.activation(out=ksq[:], in_=py[:, 0:64], func=AF.Square), reads=[py], writes=[ksq])
        b.op("dve", lambda e, py=py: e.tensor_copy(out=wi[:], in_=py[:, 64:72]), reads=[py], writes=[wi])
        for n in range(2):
            py = bank()
            proj(py, cqT, 2, lambda kc, n=n: wuq[:, kc, n * 512:(n + 1) * 512], 512)
            b.op("act", lambda e, py=py, n=n: e.activation(out=stage[:, n * 512:(n + 1) * 512], in_=py[:], func=AF.Copy),
                 reads=[py], writes=[stage])
            b.op("act", lambda e, py=py, n=n: e.activation(out=sq[:, n * 512:(n + 1) * 512], in_=py[:], func=AF.Square),
                 reads=[py], writes=[sq])
        py = bank()
        proj(py, cqT, 2, lambda kc: wuqi[:, kc, :], 512)
        b.op("act", lambda e, py=py: e.activation(out=qis[:], in_=py[:], func=AF.Copy), reads=[py], writes=[qis])
        headnorm_rope(b, stage, sq, ssq, 32, gqk, cos[:, a, :], sin[:, a, :], qkbf, tmp)
        qt = qkT[a % 2]
        for i in range(16):
            pt = b.banks16[1] if i < 8 else b.banks16[0]
            b.op("pe", lambda e, i=i, pt=pt: e.transpose(out=pt[:, (i % 8) * 128:(i % 8 + 1) * 128],
                                                         in_=qkbf[:, i * 128:(i + 1) * 128], identity=idt[:]),
                 reads=[qkbf, idt], writes=[pt])
            if i % 8 == 7:
                b.op("dve", lambda e, i=i, pt=pt, qt=qt: e.tensor_copy(
                    out=qt[:, (i // 8) * 8:(i // 8) * 8 + 8, :].rearrange("p a b -> p (a b)"), in_=pt[:]),
                    reads=[pt], writes=[qt])
        b.store("sp", lambda e, a=a, qt=qt: e.dma_start(out=io["QT2"][a], in_=qt[:, 0:8, :].rearrange("p a b -> p (a b)")),
                reads=[qt], dma=f"st_q2{a % 2}")
        b.store("sp", lambda e, a=a, qt=qt: e.dma_start(out=io["KT2"][a], in_=qt[:, 8:16, :].rearrange("p a b -> p (a b)")),
                reads=[qt], dma=f"st_k2{a % 2}")
        ki_half = T(kibf.t[:, 0:64], kibf.d)
        headnorm_rope(b, kis, ksq, kss, 1, None, cos[:, a, :], sin[:, a, :], ki_half, tmp)
        b.op("pool", lambda e: e.tensor_copy(out=kibf[:, 64:128], in_=kibf[:, 0:64]), reads=[kibf], writes=[kibf])
        p7 = b.banks16[7]
        b.op("pe", lambda e: e.transpose(out=p7[:, 0:128], in_=kibf[:], identity=idt[:]), reads=[kibf, idt], writes=[p7])
        kt_ = kiT[a % 2]
        b.op("dve", lambda e, kt_=kt_: e.tensor_copy(out=kt_[:], in_=p7[:, 0:128]), reads=[p7], writes=[kt_])
        b.store("sp", lambda e, a=a, kt_=kt_: e.dma_start(out=io["KI"][a], in_=kt_[:]), reads=[kt_], dma=f"st_ki{a % 2}")
        sg = sgn[a % 2]
        b.op("act", lambda e, sg=sg: e.activation(out=sg[:], in_=wi[:], func=AF.Sign), reads=[wi], writes=[sg])
        b.op("dve", lambda e, sg=sg: e.scalar_tensor_tensor(out=aw[:], in0=wi[:], scalar=IDX_SCALE, in1=sg[:],
                                                           op0=ALU.mult, op1=ALU.mult), reads=[wi, sg], writes=[aw])
        b.store("sp", lambda e, a=a, sg=sg: e.dma_start(out=io["SG"][a], in_=sg[:]), reads=[sg], dma=f"st_sg{a % 2}")
        headnorm_rope(b, qis, None, aw, 8, None, cos[:, a, :], sin[:, a, :], qibf, tmp, norm=False)
        qi_ = qiT[a % 2]
        for i in range(4):
            b.op("pe", lambda e, i=i: e.transpose(out=p7[:, 256 + i * 128:256 + (i + 1) * 128],
                                                  in_=qibf[:, i * 128:(i + 1) * 128], identity=idt[:]),
                 reads=[qibf, idt], writes=[p7])
        b.op("dve", lambda e, qi_=qi_: e.tensor_copy(out=qi_[:].rearrange("p a b -> p (a b)"), in_=p7[:, 256:768]),
             reads=[p7], writes=[qi_])
        b.store("sp", lambda e, a=a, qi_=qi_: e.dma_start(out=io["QI"][a], in_=qi_[:].rearrange("p a b -> p (a b)")),
                reads=[qi_], dma=f"st_qi{a % 2}")


NIT = 22
TOPK = 256


def phase_dsa_attn(b, io, o_tok):
    idt = b.ident()
    kia = b.sb([128, 8192], BF16, "kiall")
    for part in range(4):
        b.op("sp", lambda e, part=part: e.dma_start(out=kia[:, part * 2048:(part + 1) * 2048],
                                                    in_=io["KIall"][:, part * 2048:(part + 1) * 2048]),
             writes=[kia], dma="ld_kia")
    negm = b.sb([128, 512], F32, "negm")
    b.op("sp", lambda e: e.dma_start(out=negm[:], in_=io["negmask"][:]), writes=[negm], dma="ld_negm")
    cW = b.sb([128, NIT], F32, "cW")
    for i in range(NIT):
        b.op("pool", lambda e, i=i: e.memset(cW[:, i:i + 1], 2.0 ** (-i)), writes=[cW])
    Ib = b.sb([128, 8192], F32, "Ibuf")
    Mq = b.sb([128, 8192], BF16, "Mq")
    MT = b.sb([128, 64, 128], BF16, "MT")
    ktp = [b.sb([128, 8192], BF16, f"ktp{i}") for i in range(2)]
    vp = [b.sb([128, 64, 130], BF16, f"vp{i}") for i in range(2)]
    qTa = [b.sb([128, 8, 128], BF16, f"qTa{i}") for i in range(2)]
    qiTa = [b.sb([128, 4, 128], BF16, f"qiTa{i}") for i in range(2)]
    sgn = [b.sb([128, 8], F32, f"sgna{i}") for i in range(2)]
    tb = [b.sb([128, 512], F32, f"tb{i}") for i in range(2)]
    pT = [b.sb([128, 512], BF16, f"pTd{i}") for i in range(2)]
    m1 = b.sb([128, 1], F32, "bm1")
    lo = b.sb([128, 1], F32, "blo")
    mid = b.sb([128, 1], F32, "bmid")
    cnt = b.sb([128, 1], F32, "bcnt")
    g = b.sb([128, 1], F32, "bg")
    W = b.sb([128, NIT], F32, "bW")
    rec = [b.sb([128, 1], F32, f"recd{i}") for i in range(2)]
    pS = [b.banks[0], b.banks[1]]
    pA = [b.banks[2], b.banks[3]]
    pI = [b.banks[4], b.banks[5]]
    pM = [b.banks16[6], b.banks16[7]]
    ctr = {"i": 0, "s": 0, "acc": 0, "kv": 0}

    def load_slot_small(a):
        b.op("sp", lambda e, a=a: e.dma_start(out=qTa[a % 2][:].rearrange("p a b -> p (a b)"), in_=io["QT2"][a]),
             writes=[qTa[a % 2]], dma=f"ld_qTa{a % 2}")
        b.op("sp", lambda e, a=a: e.dma_start(out=qiTa[a % 2][:].rearrange("p a b -> p (a b)"), in_=io["QI"][a]),
             writes=[qiTa[a % 2]], dma=f"ld_qiTa{a % 2}")
        b.op("sp", lambda e, a=a: e.dma_start(out=sgn[a % 2][:], in_=io["SG"][a]), writes=[sgn[a % 2]], dma=f"ld_sgn{a % 2}")

    def load_kv(a, hp):
        k = ctr["kv"] % 2
        ctr["kv"] += 1
        nv = 512 * (a + 1)
        nt = 4 * (a + 1)
        b.op("sp", lambda e, k=k, hp=hp, nv=nv: e.dma_start(out=ktp[k][:, 0:nv], in_=io["KT2all"][hp][:, 0:nv]),
             writes=[ktp[k]], dma=f"ld_ktp{k}")
        b.op("sp", lambda e, k=k, hp=hp, nt=nt: e.dma_start(out=vp[k][:, 0:nt, :].rearrange("p a b -> p (a b)"),
                                                            in_=io["V2all"][hp][:, 0:nt * 130]),
             writes=[vp[k]], dma=f"ld_vp{k}")
        return k

    def indexer(a):
        nb = a + 1
        nv = 512 * nb
        qi, sg = qiTa[a % 2], sgn[a % 2]
        for blk in range(nb):
            for head in range(8):
                hp, hh = head // 2, head % 2
                py = pI[ctr["i"] % 2]
                t = tb[ctr["i"] % 2]
                ctr["i"] += 1
                b.op("pe", lambda e, py=py, hp=hp, hh=hh, blk=blk, qi=qi: e.matmul(
                    py[:], lhsT=qi[64 * hh:64 * hh + 64, hp, :], rhs=kia[64 * hh:64 * hh + 64, blk * 512:(blk + 1) * 512],
                    start=True, stop=True), reads=[qi, kia], writes=[py])
                b.op("act", lambda e, py=py, t=t: e.activation(out=t[:], in_=py[:], func=AF.Relu), reads=[py], writes=[t])
                if head == 0:
                    b.op("dve", lambda e, t=t, blk=blk, sg=sg: e.tensor_scalar(
                        out=Ib[:, blk * 512:(blk + 1) * 512], in0=t[:], scalar1=sg[:, 0:1], scalar2=None, op0=ALU.mult),
                        reads=[t, sg], writes=[Ib])
                else:
                    b.op("dve", lambda e, t=t, blk=blk, sg=sg, head=head: e.scalar_tensor_tensor(
                        out=Ib[:, blk * 512:(blk + 1) * 512], in0=t[:], scalar=sg[:, head:head + 1],
                        in1=Ib[:, blk * 512:(blk + 1) * 512], op0=ALU.mult, op1=ALU.add),
                        reads=[t, sg, Ib], writes=[Ib])
        b.op("dve", lambda e: e.tensor_reduce(out=m1[:], in_=Ib[:, 0:nv], axis=AX.X, op=ALU.max, apply_absolute_value=True),
             reads=[Ib], writes=[m1])
        b.op("dve", lambda e: e.tensor_tensor(out=Ib[:, nv - 512:nv], in0=Ib[:, nv - 512:nv], in1=negm[:], op=ALU.add),
             reads=[Ib, negm], writes=[Ib])
        b.op("dve", lambda e: e.tensor_scalar(out=m1[:], in0=m1[:], scalar1=1.0, scalar2=None, op0=ALU.add),
             reads=[m1], writes=[m1])
        b.op("dve", lambda e: e.tensor_scalar(out=lo[:], in0=m1[:], scalar1=-1.0, scalar2=None, op0=ALU.mult),
             reads=[m1], writes=[lo])
        b.op("dve", lambda e: e.tensor_scalar(out=W[:], in0=cW[:], scalar1=m1[:, 0:1], scalar2=None, op0=ALU.mult),
             reads=[cW, m1], writes=[W])
        b.op("dve", lambda e: e.tensor_tensor(out=mid[:], in0=lo[:], in1=W[:, 0:1], op=ALU.add), reads=[lo, W], writes=[mid])
        for i in range(NIT):
            b.op("dve", lambda e: e.tensor_scalar(out=Mq[:, 0:nv], in0=Ib[:, 0:nv], scalar1=mid[:, 0:1], scalar2=0.0,
                                                  op0=ALU.is_ge, op1=ALU.add, accum_out=cnt[:, 0:1]),
                 reads=[Ib, mid], writes=[Mq, cnt])
            b.op("dve", lambda e, i=i: e.tensor_scalar(out=g[:], in0=cnt[:], scalar1=TOPK - 0.5, scalar2=W[:, i:i + 1],
                                                       op0=ALU.is_ge, op1=ALU.mult), reads=[cnt, W], writes=[g])
            b.op("dve", lambda e: e.tensor_tensor(out=lo[:], in0=lo[:], in1=g[:], op=ALU.add), reads=[lo, g], writes=[lo])
            if i + 1 < NIT:
                b.op("dve", lambda e, i=i: e.tensor_tensor(out=mid[:], in0=lo[:], in1=W[:, i + 1:i + 2], op=ALU.add),
                     reads=[lo, W], writes=[mid])
        b.op("dve", lambda e: e.tensor_scalar(out=Mq[:, 0:nv], in0=Ib[:, 0:nv], scalar1=lo[:, 0:1], scalar2=None,
                                              op0=ALU.is_ge), reads=[Ib, lo], writes=[Mq])
        nt = 4 * nb
        for kt in range(nt):
            pm = pM[(kt // 8) % 2]
            b.op("pe", lambda e, kt=kt, pm=pm: e.transpose(out=pm[:, (kt % 8) * 128:(kt % 8 + 1) * 128],
                                                           in_=Mq[:, kt * 128:(kt + 1) * 128], identity=idt[:]),
                 reads=[Mq, idt], writes=[pm])
            if kt % 8 == 7 or kt == nt - 1:
                k0 = (kt // 8) * 8
                n = kt - k0 + 1
                b.op("act", lambda e, pm=pm, k0=k0, n=n: e.activation(
                    out=MT[:, k0:k0 + n, :].rearrange("p a b -> p (a b)"), in_=pm[:, 0:n * 128], func=AF.Copy),
                    reads=[pm], writes=[MT])

    def qk(u, n, kbuf):
        a, hp, hh, blk = u
        ps = pS[n % 2]
        qa = qTa[a % 2]
        for i in range(4):
            kt = 4 * blk + i
            b.op("pe", lambda e, i=i, kt=kt, ps=ps, qa=qa, hh=hh, hp=hp, kbuf=kbuf: e.matmul(
                ps[:, i * 128:(i + 1) * 128], lhsT=ktp[kbuf][64 * hh:64 * hh + 64, kt * 128:(kt + 1) * 128],
                rhs=qa[64 * hh:64 * hh + 64, hp, :], start=True, stop=True), reads=[ktp[kbuf], qa], writes=[ps])

    def softmax_pv(u, n, kbuf):
        a, hp, hh, blk = u
        ps, pt = pS[n % 2], pT[n % 2]
        if blk == 0:
            ctr["acc"] += 1
        acc = pA[ctr["acc"] % 2]
        b.op("act", lambda e: e.activation(out=pt[:], in_=ps[:], func=AF.Exp), reads=[ps], writes=[pt])
        b.op("dve", lambda e: e.tensor_tensor(out=pt[:], in0=pt[:], in1=MT[:, 4 * blk:4 * blk + 4, :].rearrange("p a b -> p (a b)"),
                                              op=ALU.mult), reads=[pt, MT], writes=[pt])
        for i in range(4):
            kt = 4 * blk + i
            b.op("pe", lambda e, i=i, kt=kt: e.matmul(acc[:, 0:65], lhsT=pt[:, i * 128:(i + 1) * 128],
                                                      rhs=vp[kbuf][:, kt, hh * 65:(hh + 1) * 65],
                                                      start=(blk == 0 and i == 0), stop=(blk == a and i == 3)),
                 reads=[pt, vp[kbuf]], writes=[acc])
        if blk == a:
            head = 2 * hp + hh
            r = rec[ctr["acc"] % 2]
            b.op("dve", lambda e: e.reciprocal(out=r[:], in_=acc[:, 64:65]), reads=[acc], writes=[r])
            b.op("dve", lambda e: e.tensor_scalar(out=o_tok[a][:, head * 64:(head + 1) * 64], in0=acc[:, 0:64],
                                                  scalar1=r[:, 0:1], scalar2=None, op0=ALU.mult),
                 reads=[acc, r], writes=[o_tok[a]])

    load_slot_small(0)
    for a in range(NS):
        if a + 1 < NS:
            load_slot_small(a + 1)
        kb_next = load_kv(a, 0)
        indexer(a)
        units = [(a, hp, hh, blk) for hp in range(8) for hh in range(2) for blk in range(a + 1)]
        kbufs = {}
        kbufs[0] = kb_next
        qk(units[0], 0, kbufs[0])
        for n, u in enumerate(units):
            _, hp, hh, blk = u
            if hh == 0 and blk == 0 and hp + 1 < 8:
                kbufs[hp + 1] = load_kv(a, hp + 1)
            if n + 1 < len(units):
                qk(units[n + 1], n + 1, kbufs[units[n + 1][1]])
            softmax_pv(u, n, kbufs[hp])


GROUPS = [[0, 1, 2, 3], [4, 5, 6, 7]]
_RANK = {}


class Gather:
    def __init__(self, b, name, nblk, cols, zt):
        self.b, self.name, self.nblk, self.cols = b, name, nblk, cols
        nc = b.nc
        self.xb = nc.dram_tensor(name + "_xb", [nblk, 512, cols], BF16).ap()
        self.yb = nc.dram_tensor(name + "_yb", [nblk, 512, cols], BF16).ap()
        self.xo = nc.dram_tensor(name + "_xo", [nblk, 128, cols], BF16).ap()
        self.od = [T(self.xo[h]) for h in range(nblk)]
        self.xd = [T(self.xb[h]) for h in range(nblk)]
        self.yd = [T(self.yb[h]) for h in range(nblk)]
        for h in range(nblk):
            for r in range(4):
                b.op("act", lambda e, h=h, r=r: e.dma_start(out=self.xb[h, r * 128:(r + 1) * 128, :], in_=zt[:, 0:cols]),
                     reads=[zt], writes=[self.xd[h]], dma="zero_" + name)

    def put(self, h, c0, c1, src_ap, reads, key):
        self.b.op("sp", lambda e: e.dma_start(out=self.xo[h, :, c0:c1], in_=src_ap), reads=reads,
                  writes=[self.od[h]], dma=key)

    def place(self, h, eng):
        def fn(e):
            if eng not in _RANK:
                _RANK[eng] = e.partition_id() % 4
            r = _RANK[eng]
            return e.dma_start(out=self.xb[h, bass.ds(r * 128, 128), :], in_=self.xo[h])
        self.b.op(eng, fn, reads=[self.od[h]], writes=[self.xd[h]], dma="place_" + self.name)

    def reduce(self, h):
        b = self.b
        b.op("pool", lambda e: e.collective_compute("AllReduce", ALU.add, replica_groups=GROUPS,
                                                    ins=[self.xb[h]], outs=[self.yb[h]]),
             reads=[self.xd[h]], writes=[self.yd[h]], dma=f"cc_{self.name}{h}", sem_inc=1)


def f_diff_proj(b, io, qT_res, G0, cos, sin):
    idt = b.ident()
    gmix = b.sb([128, D], F32, "gmix")
    b.op("sp", lambda e: e.dma_start(out=gmix[:], in_=io["gmix"][:]), writes=[gmix], dma="ld_gmix")
    gqk = b.sb([128, 2048], F32, "gqk")
    b.op("sp", lambda e: e.dma_start(out=gqk[:], in_=io["gqk"][:]), writes=[gqk], dma="ld_gqk")
    b.op("pool", lambda e: e.tensor_scalar(out=gqk[:, 0:1024], in0=gqk[:, 0:1024], scalar1=0.125, scalar2=None,
                                           op0=ALU.mult), reads=[gqk], writes=[gqk])
    w = b.sb([128, 8, 3072], BF16, "w_in")
    load_w_bf16(b, w, io["w_in"], 8, 3072, "ld_w")
    xts = [b.sb([128, D], F32, f"xt{i}") for i in range(2)]
    junk_ = [b.sb([128, D], BF16, f"junk{i}") for i in range(2)]
    ss_ = [b.sb([128, 1], F32, f"ss{i}") for i in range(2)]
    hbf_ = [b.sb([128, D], BF16, f"hbf{i}") for i in range(2)]
    hT_ = [b.sb([128, 8, 128], BF16, f"hT{i}") for i in range(2)]
    pT = [b.banks16[0], b.banks16[1]]
    pY = [b.banks[2 + i] for i in range(4)]
    stage_ = [b.sb([128, 2048], F32, f"stage{i}") for i in range(2)]
    sq1 = b.sb([128, 2048], F32, "sq"); sq_ = [sq1, sq1]
    ssq_ = [b.sb([128, 32], F32, f"ssq{i}") for i in range(2)]
    tmp1 = b.sb([128, 4 * 32 * 8], F32, "ropetmp"); tmp_ = [tmp1, tmp1]
    qkbf_ = [b.sb([128, 2048], BF16, f"qkbf{i}") for i in range(2)]
    kT = [b.sb([128, 8, 128], BF16, f"kTst{i}") for i in range(2)]
    vst = [b.sb([128, 8, 128], BF16, f"vst{i}") for i in range(2)]
    def load_x(a):
        xt = xts[a % 2]
        b.op("sp", lambda e: e.dma_start(out=xt[:], in_=io["x"][a * 128:(a + 1) * 128, :]), writes=[xt], dma=f"ld_x{a % 2}")
    load_x(0)

    def body(a, junk, ss, hbf, hT, stage, sq, ssq, tmp, qkbf):
        xt = xts[a % 2]
        if a + 1 < NS:
            load_x(a + 1)
        rmsnorm_transpose(b, xt, gmix, hbf, hT, pT[0], ss, junk)
        va = vst[a % 2]
        for n in range(6):
            py = pY[n % 4]
            for kc in range(8):
                b.op("pe", lambda e, n=n, kc=kc, py=py: e.matmul(py[:], lhsT=hT[:, kc, :],
                                                                   rhs=w[:, kc, n * 512:(n + 1) * 512],
                                                                   start=(kc == 0), stop=(kc == 7)),
                     reads=[hT, w], writes=[py])
            if n < 4:
                b.op("act", lambda e, n=n, py=py: e.activation(out=stage[:, n * 512:(n + 1) * 512], in_=py[:], func=AF.Copy),
                     reads=[py], writes=[stage])
                b.op("act", lambda e, n=n, py=py: e.activation(out=sq[:, n * 512:(n + 1) * 512], in_=py[:], func=AF.Square),
                     reads=[py], writes=[sq])
            else:
                b.op("dve", lambda e, n=n, py=py, va=va: e.tensor_copy(
                    out=va[:, (n - 4) * 4:(n - 4) * 4 + 4, :], in_=py[:].rearrange("p (h d) -> p h d", d=128)),
                    reads=[py], writes=[va])
        for h in range(8):
            G0.put(h, 2048 + a * 128, 2048 + (a + 1) * 128, va[:, h, :], [va], f"st_v{a % 2}")
        headnorm_rope(b, stage, sq, ssq, 32, gqk, cos[:, a, :], sin[:, a, :], qkbf, tmp)
        kt = kT[a % 2]
        for i in range(16):
            pt = pT[1] if i < 8 else pT[0]
            b.op("pe", lambda e, i=i, pt=pt: e.transpose(out=pt[:, (i % 8) * 128:(i % 8 + 1) * 128],
                                                         in_=qkbf[:, i * 128:(i + 1) * 128], identity=idt[:]),
                 reads=[qkbf, idt], writes=[pt])
            if i == 7:
                b.op("dve", lambda e, pt=pt, a=a: e.tensor_copy(out=qT_res[:, a, :], in_=pt[:]), reads=[pt], writes=[qT_res])
            if i == 15:
                b.op("dve", lambda e, pt=pt, kt=kt: e.tensor_copy(out=kt[:].rearrange("p a b -> p (a b)"), in_=pt[:]),
                     reads=[pt], writes=[kt])
        for h in range(8):
            G0.put(h, a * 128, (a + 1) * 128, kt[:, h, :], [kt], f"st_k{a % 2}")
    for a in range(NS):
        body(a, *(z[a % 2] for z in (junk_, ss_, hbf_, hT_, stage_, sq_, ssq_, tmp_, qkbf_)))
    for h in range(8):
        G0.place(h, "sp")
        G0.reduce(h)


def f_diff_attn(b, io, o_tok, qT, G0):
    LAM_INIT = 0.2
    maskT = b.sb([128, 512], BF16, "maskT")
    b.op("sp", lambda e: e.dma_start(out=maskT[:], in_=io["maskT"][:]), writes=[maskT], dma="ld_mask")
    lam = b.sb([128, 256], F32, "lam")
    b.op("sp", lambda e: e.dma_start(out=lam[:], in_=io["lam"][:]), writes=[lam], dma="ld_lam")
    gsub = b.sb([128, 128], F32, "gsub")
    b.op("sp", lambda e: e.dma_start(out=gsub[:], in_=io["gsub"][:]), writes=[gsub], dma="ld_gsub")
    b.op("dve", lambda e: e.tensor_scalar(out=gsub[:], in0=gsub[:], scalar1=1.0 - LAM_INIT, scalar2=None,
                                          op0=ALU.mult), reads=[gsub], writes=[gsub])
    lprod = b.sb([128, 128], F32, "lprod")
    l2 = b.sb([128, 2], F32, "l2")
    neglam = b.sb([128, 1], F32, "neglam")
    l4 = lam[:].rearrange("p (a b d) -> p a b d", a=2, b=2)
    b.op("dve", lambda e: e.tensor_tensor(out=lprod[:].rearrange("p (a d) -> p a d", a=2), in0=l4[:, :, 0, :],
                                          in1=l4[:, :, 1, :], op=ALU.mult), reads=[lam], writes=[lprod])
    b.op("dve", lambda e: e.tensor_reduce(out=l2[:], in_=lprod[:].rearrange("p (a d) -> p a d", a=2), axis=AX.X,
                                          op=ALU.add), reads=[lprod], writes=[l2])
    b.op("act", lambda e: e.activation(out=l2[:], in_=l2[:], func=AF.Exp), reads=[l2], writes=[l2])
    b.op("dve", lambda e: e.tensor_tensor(out=neglam[:], in0=l2[:, 1:2], in1=l2[:, 0:1], op=ALU.subtract),
         reads=[l2], writes=[neglam])
    b.op("dve", lambda e: e.tensor_scalar(out=neglam[:], in0=neglam[:], scalar1=-LAM_INIT, scalar2=None, op0=ALU.add),
         reads=[neglam], writes=[neglam])

    ktb = [b.sb([128, 8192], BF16, f"ktb{i}") for i in range(2)]
    vb = [b.sb([128, 64, 129], BF16, f"vb{i}") for i in range(2)]
    for i in range(2):
        b.op("dve", lambda e, i=i: e.memset(vb[i][:, :, 128:129], 1.0), writes=[vb[i]])
    pS = [[b.banks[c * 2 + i] for i in range(2)] for c in range(2)]
    pA = [[b.banks[4 + c * 2 + i] for i in range(2)] for c in range(2)]
    pT = [[b.sb([128, 512], BF16, f"pTs{c}{i}") for i in range(2)] for c in range(2)]
    rec = [b.sb([128, 2], F32, f"rec{i}") for i in range(2)]
    o32 = [b.sb([128, 128], F32, f"o32{i}") for i in range(2)]
    oj = b.sb([128, 128], BF16, "ojunk")
    ss1 = [b.sb([128, 1], F32, f"ss1{i}") for i in range(2)]
    units = [(h, a, blk) for h in range(8) for a in range(NS) for blk in range(a + 1)]

    def load_head(h):
        kb, vv = ktb[h % 2], vb[h % 2]
        yb = G0.yb[h]
        for r in range(4):
            b.op("sp", lambda e, kb=kb, r=r, yb=yb: e.dma_start(out=kb[:, r * 2048:(r + 1) * 2048],
                                                               in_=yb[r * 128:(r + 1) * 128, 0:2048]),
                 reads=[G0.yd[h]], writes=[kb], dma=f"ld_kt{h % 2}")
            b.op("sp", lambda e, vv=vv, r=r, yb=yb: e.dma_start(
                out=vv[:, r * 16:(r + 1) * 16, 0:128],
                in_=yb[r * 128:(r + 1) * 128, 2048:4096].rearrange("p (a e) -> p a e", e=128)),
                reads=[G0.yd[h]], writes=[vv], dma=f"ld_v{h % 2}")

    def qk(u, n):
        h, a, blk = u
        kb = ktb[h % 2]
        for i in range(4):
            for c in range(2):
                ps = pS[c][n % 2]
                kt = 16 * i + blk
                b.op("pe", lambda e, c=c, i=i, kt=kt, ps=ps, kb=kb, a=a, h=h: e.matmul(
                    ps[:, i * 128:(i + 1) * 128], lhsT=kb[64 * c:64 * c + 64, kt * 128:(kt + 1) * 128],
                    rhs=qT[64 * c:64 * c + 64, a, h * 128:(h + 1) * 128], start=True, stop=True),
                    reads=[kb, qT], writes=[ps])

    def evac(h, a):
        k = a % 2
        a0, a1 = pA[0][k], pA[1][k]
        r, o, s = rec[k], o32[k], ss1[k]
        b.op("dve", lambda e: e.reciprocal(out=r[:, 0:1], in_=a0[:, 128:129]), reads=[a0], writes=[r])
        b.op("dve", lambda e: e.reciprocal(out=r[:, 1:2], in_=a1[:, 128:129]), reads=[a1], writes=[r])
        b.op("dve", lambda e: e.tensor_tensor(out=r[:, 1:2], in0=r[:, 1:2], in1=neglam[:], op=ALU.mult),
             reads=[r, neglam], writes=[r])
        b.op("dve", lambda e: e.tensor_scalar(out=o[:], in0=a0[:, 0:128], scalar1=r[:, 0:1], scalar2=None, op0=ALU.mult),
             reads=[a0, r], writes=[o])
        b.op("dve", lambda e: e.scalar_tensor_tensor(out=o[:], in0=a1[:, 0:128], scalar=r[:, 1:2], in1=o[:],
                                                     op0=ALU.mult, op1=ALU.add), reads=[a1, r, o], writes=[o])
        b.op("act", lambda e: e.activation(out=oj[:], in_=o[:], func=AF.Square, accum_out=s[:, 0:1]),
             reads=[o], writes=[oj, s])
        rstd_from_ss(b, s, 1, 1.0 / 128)
        ot = o_tok[a]
        b.op("dve", lambda e: e.scalar_tensor_tensor(out=ot[:, h * 128:(h + 1) * 128], in0=o[:], scalar=s[:, 0:1],
                                                     in1=gsub[:], op0=ALU.mult, op1=ALU.mult),
             reads=[o, s, gsub], writes=[ot])

    def softmax_pv(u, n):
        h, a, blk = u
        vv = vb[h % 2]
        for c in range(2):
            ps = pS[c][n % 2]
            pt = pT[c][n % 2]
            acc = pA[c][a % 2]
            b.op("act", lambda e, ps=ps, pt=pt: e.activation(out=pt[:], in_=ps[:], func=AF.Exp),
                 reads=[ps], writes=[pt])
            if blk == a:
                b.op("dve", lambda e, pt=pt: e.tensor_tensor(out=pt[:], in0=pt[:], in1=maskT[:], op=ALU.mult),
                     reads=[pt, maskT], writes=[pt])
            for i in range(4):
                kt = 16 * i + blk
                b.op("pe", lambda e, i=i, kt=kt, pt=pt, acc=acc, vv=vv, blk=blk, a=a: e.matmul(
                    acc[:, 0:129], lhsT=pt[:, i * 128:(i + 1) * 128], rhs=vv[:, kt, :],
                    start=(blk == 0 and i == 0), stop=(blk == a and i == 3)),
                    reads=[pt, vv], writes=[acc])
        if blk == a:
            evac(h, a)

    load_head(0)
    qk(units[0], 0)
    for n, u in enumerate(units):
        if u[1] == 0 and u[2] == 0 and u[0] + 1 < 8:
            load_head(u[0] + 1)
        if n + 1 < len(units):
            qk(units[n + 1], n + 1)
        softmax_pv(u, n)


def f_dsa_proj(b, io, x_res, cos, sin, G1, GK, scr):
    idt = b.ident()
    gmix = b.sb([128, D], F32, "gmix1")
    b.op("sp", lambda e: e.dma_start(out=gmix[:], in_=io["gmix1"][:]), writes=[gmix], dma="ld_gmix1")
    gqk = b.sb([128, 2048], F32, "gqk2")
    b.op("sp", lambda e: e.dma_start(out=gqk[:], in_=io["gqk2"][:]), writes=[gqk], dma="ld_gqk2")
    b.op("pool", lambda e: e.tensor_scalar(out=gqk[:, 0:1024], in0=gqk[:, 0:1024], scalar1=0.125, scalar2=None,
                                           op0=ALU.mult), reads=[gqk], writes=[gqk])
    gcq = b.sb([128, 256], F32, "gcq")
    b.op("sp", lambda e: e.dma_start(out=gcq[:], in_=io["gcq"][:]), writes=[gcq], dma="ld_gcq")
    w = b.sb([128, 8, 2376], BF16, "w_in2")
    load_w_bf16(b, w, io["w_in2"], 8, 2376, "ld_w2in", colchunk=792)
    wuq = b.sb([128, 2, 1024], BF16, "wuq")
    load_w_bf16(b, wuq, io["w_uq"], 2, 1024, "ld_wuq")
    wuqi = b.sb([128, 2, 512], BF16, "wuqi")
    load_w_bf16(b, wuqi, io["w_uqi"], 2, 512, "ld_wuqi")
    junk_ = [b.sb([128, D], BF16, f"junk3{i}") for i in range(2)]
    ss_ = [b.sb([128, 1], F32, f"ss3{i}") for i in range(2)]
    hbf_ = [b.sb([128, D], BF16, f"hbf3{i}") for i in range(2)]
    hT_ = [b.sb([128, 8, 128], BF16, f"hT3{i}") for i in range(2)]
    stage_ = [b.sb([128, 2048], F32, f"stage3{i}") for i in range(2)]
    sq1 = b.sb([128, 2048], F32, "sq3"); sq_ = [sq1, sq1]
    ssq_ = [b.sb([128, 32], F32, f"ssq3{i}") for i in range(2)]
    tmp1 = b.sb([128, 4 * 32 * 8], F32, "ropetmp3"); tmp_ = [tmp1, tmp1]
    qkbf_ = [b.sb([128, 2048], BF16, f"qkbf3{i}") for i in range(2)]
    qkT = [b.sb([128, 16, 128], BF16, f"qkT3{i}") for i in range(2)]
    vst = [b.sb([128, 16, 64], BF16, f"vst3{i}") for i in range(2)]
    cqs_ = [b.sb([128, 256], F32, f"cqs{i}") for i in range(2)]
    ssc_ = [b.sb([128, 1], F32, f"ssc{i}") for i in range(2)]
    cqbf_ = [b.sb([128, 256], BF16, f"cqbf{i}") for i in range(2)]
    cqT_ = [b.sb([128, 2, 128], BF16, f"cqT{i}") for i in range(2)]
    kis_ = [b.sb([128, 64], F32, f"kis{i}") for i in range(2)]
    ksq_ = [b.sb([128, 64], F32, f"ksq{i}") for i in range(2)]
    kss_ = [b.sb([128, 1], F32, f"kss{i}") for i in range(2)]
    kibf_ = [b.sb([128, 128], BF16, f"kibf{i}") for i in range(2)]
    kiT = [b.sb([128, 128], BF16, f"kiT{i}") for i in range(2)]
    wi_ = [b.sb([128, 8], F32, f"wi{i}") for i in range(2)]
    sgn = [b.sb([128, 8], F32, f"sgn{i}") for i in range(2)]
    aw_ = [b.sb([128, 8], F32, f"aw{i}") for i in range(2)]
    qis_ = [b.sb([128, 512], F32, f"qis{i}") for i in range(2)]
    qibf_ = [b.sb([128, 512], BF16, f"qibf{i}") for i in range(2)]
    qiT = [b.sb([128, 4, 128], BF16, f"qiT{i}") for i in range(2)]
    nbank = [0]

    def bank():
        nbank[0] += 1
        return b.banks[2 + nbank[0] % 4]

    def proj(py, lhs, nk, rhs_fn, ncol):
        for kc in range(nk):
            b.op("pe", lambda e, kc=kc: e.matmul(py[:, 0:ncol], lhsT=lhs[:, kc, :], rhs=rhs_fn(kc),
                                                 start=(kc == 0), stop=(kc == nk - 1)), reads=[lhs, w, wuq, wuqi], writes=[py])

    def body2(a, junk, ss, hbf, hT, stage, sq, ssq, tmp, qkbf, cqs, ssc, cqbf, cqT, kis, ksq, kss, kibf, wi, aw, qis, qibf):
        xr = x_res[a]
        rmsnorm_transpose(b, xr, gmix, hbf, hT, b.banks16[0], ss, junk)
        py = bank()
        proj(py, hT, 8, lambda kc: w[:, kc, 0:256], 256)
        b.op("act", lambda e, py=py: e.activation(out=cqs[:], in_=py[:, 0:256], func=AF.Copy), reads=[py], writes=[cqs])
        b.op("act", lambda e, py=py: e.activation(out=junk[:, 0:256], in_=py[:, 0:256], func=AF.Square, accum_out=ssc[:, 0:1]),
             reads=[py], writes=[junk, ssc])
        rstd_from_ss(b, ssc, 1, 1.0 / 256)
        b.op("dve", lambda e: e.scalar_tensor_tensor(out=cqbf[:], in0=cqs[:], scalar=ssc[:, 0:1], in1=gcq[:],
                                                     op0=ALU.mult, op1=ALU.mult), reads=[cqs, ssc, gcq], writes=[cqbf])
        p6 = b.banks16[6]
        for kc in range(2):
            b.op("pe", lambda e, kc=kc: e.transpose(out=p6[:, kc * 128:(kc + 1) * 128], in_=cqbf[:, kc * 128:(kc + 1) * 128],
                                                    identity=idt[:]), reads=[cqbf, idt], writes=[p6])
        b.op("dve", lambda e: e.tensor_copy(out=cqT[:].rearrange("p a b -> p (a b)"), in_=p6[:, 0:256]),
             reads=[p6], writes=[cqT])
        for n in range(2):
            py = bank()
            proj(py, hT, 8, lambda kc, n=n: w[:, kc, 256 + n * 512:256 + (n + 1) * 512], 512)
            b.op("act", lambda e, py=py, n=n: e.activation(out=stage[:, 1024 + n * 512:1024 + (n + 1) * 512], in_=py[:], func=AF.Copy),
                 reads=[py], writes=[stage])
            b.op("act", lambda e, py=py, n=n: e.activation(out=sq[:, 1024 + n * 512:1024 + (n + 1) * 512], in_=py[:], func=AF.Square),
                 reads=[py], writes=[sq])
        va = vst[a % 2]
        for n in range(2):
            py = bank()
            proj(py, hT, 8, lambda kc, n=n: w[:, kc, 1280 + n * 512:1280 + (n + 1) * 512], 512)
            b.op("dve", lambda e, py=py, n=n, va=va: e.tensor_copy(out=va[:, n * 8:(n + 1) * 8, :],
                                                                   in_=py[:].rearrange("p (h d) -> p h d", d=64)),
                 reads=[py], writes=[va])
        for hp in range(8):
            G1.put(hp, 2048 + a * 128, 2048 + (a + 1) * 128, va[:, 2 * hp:2 * hp + 2, :].rearrange("p h d -> p (h d)"),
                   [va], f"st_v2{a % 2}")
        py = bank()
        proj(py, hT, 8, lambda kc: w[:, kc, 2304:2376], 72)
        b.op("act", lambda e, py=py: e.activation(out=kis[:], in_=py[:, 0:64], func=AF.Copy), reads=[py], writes=[kis])
        b.op("act", lambda e, py=py: e.activation(out=ksq[:], in_=py[:, 0:64], func=AF.Square), reads=[py], writes=[ksq])
        b.op("dve", lambda e, py=py: e.tensor_copy(out=wi[:], in_=py[:, 64:72]), reads=[py], writes=[wi])
        for n in range(2):
            py = bank()
            proj(py, cqT, 2, lambda kc, n=n: wuq[:, kc, n * 512:(n + 1) * 512], 512)
            b.op("act", lambda e, py=py, n=n: e.activation(out=stage[:, n * 512:(n + 1) * 512], in_=py[:], func=AF.Copy),
                 reads=[py], writes=[stage])
            b.op("act", lambda e, py=py, n=n: e.activation(out=sq[:, n * 512:(n + 1) * 512], in_=py[:], func=AF.Square),
                 reads=[py], writes=[sq])
        py = bank()
        proj(py, cqT, 2, lambda kc: wuqi[:, kc, :], 512)
        b.op("act", lambda e, py=py: e.activation(out=qis[:], in_=py[:], func=AF.Copy), reads=[py], writes=[qis])
        headnorm_rope(b, stage, sq, ssq, 32, gqk, cos[:, a, :], sin[:, a, :], qkbf, tmp)
        qt = qkT[a % 2]
        for i in range(16):
            pt = b.banks16[1] if i < 8 else b.banks16[0]
            b.op("pe", lambda e, i=i, pt=pt: e.transpose(out=pt[:, (i % 8) * 128:(i % 8 + 1) * 128],
                                                         in_=qkbf[:, i * 128:(i + 1) * 128], identity=idt[:]),
                 reads=[qkbf, idt], writes=[pt])
            if i % 8 == 7:
                b.op("dve", lambda e, i=i, pt=pt, qt=qt: e.tensor_copy(
                    out=qt[:, (i // 8) * 8:(i // 8) * 8 + 8, :].rearrange("p a b -> p (a b)"), in_=pt[:]),
                    reads=[pt], writes=[qt])
        b.op("sp", lambda e, a=a, qt=qt: e.dma_start(out=scr["QT2"][a][:], in_=qt[:, 0:8, :].rearrange("p a b -> p (a b)")),
             reads=[qt], writes=[scr["QT2"][a]], dma=f"st_q2{a % 2}")
        for hp in range(8):
            G1.put(hp, a * 128, (a + 1) * 128, qt[:, 8 + hp, :], [qt], f"st_k2{a % 2}")
        ki_half = T(kibf.t[:, 0:64], kibf.d)
        headnorm_rope(b, kis, ksq, kss, 1, None, cos[:, a, :], sin[:, a, :], ki_half, tmp)
        b.op("pool", lambda e: e.tensor_copy(out=kibf[:, 64:128], in_=kibf[:, 0:64]), reads=[kibf], writes=[kibf])
        p7 = b.banks16[7]
        b.op("pe", lambda e: e.transpose(out=p7[:, 0:128], in_=kibf[:], identity=idt[:]), reads=[kibf, idt], writes=[p7])
        kt_ = kiT[a % 2]
        b.op("dve", lambda e, kt_=kt_: e.tensor_copy(out=kt_[:], in_=p7[:, 0:128]), reads=[p7], writes=[kt_])
        GK.put(0, a * 128, (a + 1) * 128, kt_[:], [kt_], f"st_ki{a % 2}")
        sg = sgn[a % 2]
        b.op("act", lambda e, sg=sg: e.activation(out=sg[:], in_=wi[:], func=AF.Sign), reads=[wi], writes=[sg])
        b.op("dve", lambda e, sg=sg: e.scalar_tensor_tensor(out=aw[:], in0=wi[:], scalar=IDX_SCALE, in1=sg[:],
                                                           op0=ALU.mult, op1=ALU.mult), reads=[wi, sg], writes=[aw])
        b.op("sp", lambda e, a=a, sg=sg: e.dma_start(out=scr["SG"][a][:], in_=sg[:]), reads=[sg], writes=[scr["SG"][a]],
             dma=f"st_sg{a % 2}")
        headnorm_rope(b, qis, None, aw, 8, None, cos[:, a, :], sin[:, a, :], qibf, tmp, norm=False)
        qi_ = qiT[a % 2]
        for i in range(4):
            b.op("pe", lambda e, i=i: e.transpose(out=p7[:, 256 + i * 128:256 + (i + 1) * 128],
                                                  in_=qibf[:, i * 128:(i + 1) * 128], identity=idt[:]),
                 reads=[qibf, idt], writes=[p7])
        b.op("dve", lambda e, qi_=qi_: e.tensor_copy(out=qi_[:].rearrange("p a b -> p (a b)"), in_=p7[:, 256:768]),
             reads=[p7], writes=[qi_])
        b.op("sp", lambda e, a=a, qi_=qi_: e.dma_start(out=scr["QI"][a][:], in_=qi_[:].rearrange("p a b -> p (a b)")),
             reads=[qi_], writes=[scr["QI"][a]], dma=f"st_qi{a % 2}")
    for a in range(NS):
        body2(a, *(z[a % 2] for z in (junk_, ss_, hbf_, hT_, stage_, sq_, ssq_, tmp_, qkbf_, cqs_, ssc_, cqbf_, cqT_, kis_, ksq_, kss_, kibf_, wi_, aw_, qis_, qibf_)))
    GK.place(0, "act")
    GK.reduce(0)
    for hp in range(8):
        G1.place(hp, "act")
        G1.reduce(hp)


NIT2 = 14


def f_dsa_attn(b, io, o_tok, G1, GK, scr):
    idt = b.ident()
    kia = b.sb([128, 8192], BF16, "kiall")
    for r in range(4):
        b.op("sp", lambda e, r=r: e.dma_start(out=kia[:, r * 2048:(r + 1) * 2048], in_=GK.yb[0][r * 128:(r + 1) * 128, :]),
             reads=[GK.yd[0]], writes=[kia], dma="ld_kia")
    kia4 = kia[:].rearrange("p (r a t) -> p r a t", r=4, a=16)
    negm = b.sb([128, 512], F32, "negm")
    b.op("sp", lambda e: e.dma_start(out=negm[:], in_=io["negmask"][:]), writes=[negm], dma="ld_negm")
    cW = b.sb([128, NIT2], F32, "cW")
    for i in range(NIT2):
        b.op("dve", lambda e, i=i: e.memset(cW[:, i:i + 1], 2.0 ** (-i)), writes=[cW])
    THR = b.sb([128, NS], F32, "THR")
    qiTa = [b.sb([128, 4, 128], BF16, f"qiTa{i}") for i in range(2)]
    sgn = [b.sb([128, 8], F32, f"sgna{i}") for i in range(2)]
    Dg = [b.sb([128, 8, 128], BF16, f"Dg{i}") for i in range(2)]
    tb = [[b.sb([128, 512], BF16, f"tb{s}{i}") for i in range(3)] for s in range(2)]
    pIs = [[b.banks[4], b.banks[6]], [b.banks[5], b.banks[0]]]
    pIas = [b.banks[7], b.banks[1]]
    ctr = {"i": 0, "acc": 0, "kv": 0}

    def load_idx_small(a):
        b.op("sp", lambda e, a=a: e.dma_start(out=qiTa[a % 2][:].rearrange("p a b -> p (a b)"), in_=scr["QI"][a][:]),
             reads=[scr["QI"][a]], writes=[qiTa[a % 2]], dma=f"ld_qiTa{a % 2}")
        b.op("sp", lambda e, a=a: e.dma_start(out=sgn[a % 2][:], in_=scr["SG"][a][:]), reads=[scr["SG"][a]],
             writes=[sgn[a % 2]], dma=f"ld_sgn{a % 2}")

    def indexer_list(a, s, Ib_):
        L = []

        def add(eng, fn, reads=(), writes=()):
            L.append((eng, fn, reads, writes))
        nb = a + 1
        qi, sg, dg = qiTa[a % 2], sgn[a % 2], Dg[a % 2]
        pys, pia = pIs[s], pIas[s]
        add("dve", lambda e: e.tensor_tensor(out=dg[:], in0=idt[:].unsqueeze(1).to_broadcast([128, 8, 128]),
                                             in1=sg[:].unsqueeze(2).to_broadcast([128, 8, 128]), op=ALU.mult),
            [idt, sg], [dg])
        steps = [(blk, head) for blk in range(nb) for head in range(8)]
        S_ = len(steps)
        evq = []
        for k in range(S_ + 2):
            if k < S_:
                blk, head = steps[k]
                hp, hh = head // 2, head % 2
                py = pys[k % 2]
                add("pe", lambda e, hp=hp, hh=hh, blk=blk, py=py: e.matmul(
                    py[:].rearrange("p (r t) -> p r t", r=4), lhsT=qi[64 * hh:64 * hh + 64, hp, :],
                    rhs=kia4[64 * hh:64 * hh + 64, :, blk, :], start=True, stop=True), [qi, kia], [py])
            if 0 <= k - 1 < S_:
                py = pys[(k - 1) % 2]
                t = tb[s][(k - 1) % 3]
                add("act", lambda e, t=t, py=py: e.activation(out=t[:], in_=py[:], func=AF.Relu), [py], [t])
            if 0 <= k - 2 < S_:
                blk, head = steps[k - 2]
                t = tb[s][(k - 2) % 3]
                add("pe", lambda e, head=head, t=t: e.matmul(pia[:], lhsT=dg[:, head, :], rhs=t[:],
                                                             start=(head == 0), stop=(head == 7)), [dg, t], [pia])
                if head == 7:
                    add("act", lambda e, eb=blk: e.activation(out=Ib_[:, eb * 512:(eb + 1) * 512], in_=pia[:], func=AF.Copy),
                        [pia], [Ib_])
        for _, eb in evq:
            add("act", lambda e, eb=eb: e.activation(out=Ib_[:, eb * 512:(eb + 1) * 512], in_=pia[:], func=AF.Copy),
                [pia], [Ib_])
        return L

    mA = b.mark()
    IbA = [b.sb([128, 8192], F32, f"IbA{i}") for i in range(2)]
    MqA = [b.sb([128, 8192], BF16, f"MqA{i}") for i in range(2)]
    st = [{k: b.sb([128, n], F32, f"bs{k}{s}") for k, n in (("m1", 1), ("lo", 1), ("mid", 1), ("nmid", 1), ("cD", 1),
                                                           ("sA", 1), ("g", 1), ("W", NIT2))} for s in range(2)]

    def bisect_list(a, s):
        L = []

        def add(eng, fn, reads=(), writes=()):
            L.append((eng, fn, reads, writes))
        nb = a + 1
        nv = 512 * nb
        Ib_, Mq_ = IbA[s], MqA[s]
        S_ = st[s]
        m1, lo, mid, nmid, cD, sA, g, W = (S_[k] for k in ("m1", "lo", "mid", "nmid", "cD", "sA", "g", "W"))
        add("dve", lambda e: e.tensor_reduce(out=m1[:], in_=Ib_[:, 0:nv], axis=AX.X, op=ALU.max, apply_absolute_value=True),
            [Ib_], [m1])
        add("dve", lambda e: e.tensor_tensor(out=Ib_[:, nv - 512:nv], in0=Ib_[:, nv - 512:nv], in1=negm[:], op=ALU.add),
            [Ib_, negm], [Ib_])
        add("dve", lambda e: e.tensor_scalar(out=m1[:], in0=m1[:], scalar1=1.0, scalar2=None, op0=ALU.add), [m1], [m1])
        add("dve", lambda e: e.tensor_scalar(out=lo[:], in0=m1[:], scalar1=-1.0, scalar2=None, op0=ALU.mult), [m1], [lo])
        add("dve", lambda e: e.tensor_scalar(out=W[:], in0=cW[:], scalar1=m1[:, 0:1], scalar2=None, op0=ALU.mult),
            [cW, m1], [W])
        h = 512 * (nb // 2)
        thr = TOPK - 0.5 - 0.5 * h

        def set_mid(i):
            add("dve", lambda e: e.tensor_tensor(out=mid[:], in0=lo[:], in1=W[:, i:i + 1], op=ALU.add), [lo, W], [mid])
            if h > 0:
                add("dve", lambda e: e.tensor_scalar(out=nmid[:], in0=mid[:], scalar1=-1.0, scalar2=None, op0=ALU.mult),
                    [mid], [nmid])
        return L, (add, set_mid, h, thr, Ib_, Mq_, lo, mid, nmid, cD, sA, g, W, nv)

    def stageA_list(a, s):
        L = indexer_list(a, s, IbA[s])
        L2, (add, set_mid, h, thr, Ib_, Mq_, lo, mid, nmid, cD, sA, g, W, nv) = bisect_list(a, s)
        set_mid(0)
        for i in range(NIT2):
            if h > 0:
                add("act", lambda e: e.activation(out=Mq_[:, 0:h], in_=Ib_[:, 0:h], func=AF.Sign, bias=nmid[:, 0:1],
                                                  scale=1.0, accum_out=sA[:, 0:1]), [Ib_, nmid], [Mq_, sA])
            add("dve", lambda e: e.tensor_scalar(out=Mq_[:, h:nv], in0=Ib_[:, h:nv], scalar1=mid[:, 0:1], scalar2=0.0,
                                                 op0=ALU.is_ge, op1=ALU.add, accum_out=cD[:, 0:1]), [Ib_, mid], [Mq_, cD])
            if h > 0:
                add("dve", lambda e: e.scalar_tensor_tensor(out=cD[:], in0=sA[:], scalar=0.5, in1=cD[:], op0=ALU.mult,
                                                            op1=ALU.add), [sA, cD], [cD])
            add("dve", lambda e, i=i: e.tensor_scalar(out=g[:], in0=cD[:], scalar1=thr, scalar2=W[:, i:i + 1],
                                                      op0=ALU.is_ge, op1=ALU.mult), [cD, W], [g])
            add("dve", lambda e: e.tensor_tensor(out=lo[:], in0=lo[:], in1=g[:], op=ALU.add), [lo, g], [lo])
            if i + 1 < NIT2:
                set_mid(i + 1)
        for c0 in range(0, nv, 2048):
            c1 = min(nv, c0 + 2048)
            add("dve", lambda e, c0=c0, c1=c1: e.tensor_scalar(out=Mq_[:, c0:c1], in0=Ib_[:, c0:c1], scalar1=lo[:, 0:1],
                                                               scalar2=None, op0=ALU.is_ge), [Ib_, lo], [Mq_])
        add("sp", ("dma", lambda e: e.dma_start(out=scr["MQ"][a][:, 0:nv], in_=Mq_[:, 0:nv]), f"st_mq{s}"),
            [Mq_], [scr["MQ"][a]])
        return L + L2

    def emit_item(it):
        eng, fn, reads, writes = it
        if isinstance(fn, tuple):
            b.op(eng, fn[1], reads, writes, dma=fn[2])
        else:
            b.op(eng, fn, reads, writes)

    for a0 in range(0, NS, 2):
        load_idx_small(a0)
        load_idx_small(a0 + 1)
        la = stageA_list(a0, 0)
        lb = stageA_list(a0 + 1, 1)
        i = j = 0
        while i < len(la) or j < len(lb):
            if j >= len(lb) or (i < len(la) and i * len(lb) <= j * len(la)):
                emit_item(la[i]); i += 1
            else:
                emit_item(lb[j]); j += 1
    b.release(mA)

    Mqs = [b.sb([128, 8192], BF16, f"MqB{i}") for i in range(2)]
    MTs = [b.sb([128, 64, 128], BF16, f"MT{i}") for i in range(2)]
    ktp = [b.sb([128, 8192], BF16, f"ktp{i}") for i in range(2)]
    vp = [b.sb([128, 64, 130], BF16, f"vp{i}") for i in range(2)]
    for i in range(2):
        b.op("dve", lambda e, i=i: e.memset(vp[i][:, :, 0:1], 1.0), writes=[vp[i]])
        b.op("dve", lambda e, i=i: e.memset(vp[i][:, :, 129:130], 1.0), writes=[vp[i]])
    qTa = [b.sb([128, 8, 128], BF16, f"qTa{i}") for i in range(2)]
    pT = [b.sb([128, 512], BF16, f"pTd{i}") for i in range(3)]
    rec = [b.sb([128, 1], F32, f"recd{i}") for i in range(2)]
    pS = [b.banks[0], b.banks[1], b.banks[5]]
    pA = [b.banks[2], b.banks[3]]
    pMs = [b.banks16[6], b.banks16[7]]

    def load_slot_small(a):
        nv = 512 * (a + 1)
        b.op("sp", lambda e, a=a: e.dma_start(out=qTa[a % 2][:].rearrange("p a b -> p (a b)"), in_=scr["QT2"][a][:]),
             reads=[scr["QT2"][a]], writes=[qTa[a % 2]], dma=f"ld_qTa{a % 2}")
        b.op("sp", lambda e, a=a, nv=nv: e.dma_start(out=Mqs[a % 2][:, 0:nv], in_=scr["MQ"][a][:, 0:nv]),
             reads=[scr["MQ"][a]], writes=[Mqs[a % 2]], dma=f"ld_mq{a % 2}")

    def load_kv(a, hp):
        k = ctr["kv"] % 2
        ctr["kv"] += 1
        n = (a + 1) * 128
        yb = G1.yb[hp]
        b.op("sp", lambda e: e.dma_start(out=ktp[k][:].rearrange("p (r x) -> p r x", r=4)[:, :, 0:n],
                                         in_=yb[:, 0:n].rearrange("(r p) x -> p r x", p=128)),
             reads=[G1.yd[hp]], writes=[ktp[k]], dma=f"ld_ktp{k}")
        for r in range(4):
            b.op("sp", lambda e, r=r: e.dma_start(
                out=vp[k][:, r * 16:r * 16 + a + 1, 1:129],
                in_=yb[r * 128:(r + 1) * 128, 2048:2048 + n].rearrange("p (a e) -> p a e", e=128)),
                reads=[G1.yd[hp]], writes=[vp[k]], dma=f"ld_vp{k}")
        return k

    def pre_list(a):
        L = []
        Mq, MT = Mqs[a % 2], MTs[a % 2]
        nt = 4 * (a + 1)
        ngr = (nt + 7) // 8
        for gi in range(ngr + 1):
            if gi < ngr:
                pm = pMs[gi % 2]
                for kt in range(gi * 8, min(nt, gi * 8 + 8)):
                    L.append(("pe", lambda e, kt=kt, pm=pm: e.transpose(out=pm[:, (kt % 8) * 128:(kt % 8 + 1) * 128],
                                                                        in_=Mq[:, kt * 128:(kt + 1) * 128], identity=idt[:]),
                              [Mq, idt], [pm]))
            if gi >= 1:
                g0 = gi - 1
                pm = pMs[g0 % 2]
                k0 = g0 * 8
                n = min(nt, k0 + 8) - k0
                L.append(("act", lambda e, pm=pm, k0=k0, n=n: e.activation(
                    out=MT[:, k0:k0 + n, :].rearrange("p a b -> p (a b)"), in_=pm[:, 0:n * 128], func=AF.Copy),
                    [pm], [MT]))
        return L

    def qk(u, n, kbuf):
        a, hp, hh, blk = u
        ps = pS[n % 3]
        qa = qTa[a % 2]
        for i in range(4):
            kt = 16 * i + blk
            b.op("pe", lambda e, i=i, kt=kt, ps=ps, qa=qa, hh=hh, hp=hp, kbuf=kbuf: e.matmul(
                ps[:, i * 128:(i + 1) * 128], lhsT=ktp[kbuf][64 * hh:64 * hh + 64, kt * 128:(kt + 1) * 128],
                rhs=qa[64 * hh:64 * hh + 64, hp, :], start=True, stop=True), reads=[ktp[kbuf], qa], writes=[ps])

    def softmax_pv(u, n, kbuf):
        a, hp, hh, blk = u
        ps, pt = pS[n % 3], pT[n % 3]
        if blk == 0:
            ctr["acc"] += 1
        acc = pA[ctr["acc"] % 2]
        b.op("act", lambda e: e.activation(out=pt[:], in_=ps[:], func=AF.Exp), reads=[ps], writes=[pt])
        MT = MTs[a % 2]
        b.op("dve", lambda e: e.tensor_tensor(out=pt[:], in0=pt[:], in1=MT[:, 4 * blk:4 * blk + 4, :].rearrange("p a b -> p (a b)"),
                                              op=ALU.mult), reads=[pt, MT], writes=[pt])
        for i in range(4):
            kt = 16 * i + blk
            b.op("pe", lambda e, i=i, kt=kt: e.matmul(acc[:, 0:65], lhsT=pt[:, i * 128:(i + 1) * 128],
                                                      rhs=vp[kbuf][:, kt, hh * 65:(hh + 1) * 65],
                                                      start=(blk == 0 and i == 0), stop=(blk == a and i == 3)),
                 reads=[pt, vp[kbuf]], writes=[acc])
        if blk == a:
            head = 2 * hp + hh
            r = rec[ctr["acc"] % 2]
            sc, v0 = (0, 1) if hh == 0 else (64, 0)
            b.op("dve", lambda e: e.reciprocal(out=r[:], in_=acc[:, sc:sc + 1]), reads=[acc], writes=[r])
            b.op("dve", lambda e: e.tensor_scalar(out=o_tok[a][:, head * 64:(head + 1) * 64], in0=acc[:, v0:v0 + 64],
                                                  scalar1=r[:, 0:1], scalar2=None, op0=ALU.mult),
                 reads=[acc, r], writes=[o_tok[a]])

    load_slot_small(0)
    for it in pre_list(0):
        b.op(*it)
    for a in range(NS):
        if a + 1 < NS:
            load_slot_small(a + 1)
        kb_next = load_kv(a, 0)
        nxt = pre_list(a + 1) if a + 1 < NS else []
        done = 0
        units = [(a, hp, hh, blk) for hp in range(8) for hh in range(2) for blk in range(a + 1)]
        kbufs = {0: kb_next}
        kbufs[1] = load_kv(a, 1)
        qk(units[0], 0, kbufs[0])
        if len(units) > 1:
            qk(units[1], 1, kbufs[units[1][1]])
        for n, u in enumerate(units):
            _, hp, hh, blk = u
            if hh == 0 and blk == 0 and 1 <= hp and hp + 1 < 8:
                kbufs[hp + 1] = load_kv(a, hp + 1)
            if n + 2 < len(units):
                qk(units[n + 2], n + 2, kbufs[units[n + 2][1]])
            softmax_pv(u, n, kbufs[hp])
            want = (n + 1) * len(nxt) // len(units)
            while done < want:
                b.op(*nxt[done])
                done += 1
        while done < len(nxt):
            b.op(*nxt[done])
            done += 1


BF = ml_dtypes.bfloat16


def rep(v, n=128):
    return np.ascontiguousarray(np.tile(np.asarray(v).reshape(1, -1), (n, 1)))


def own_tiles(arr_bs, c):
    bb, j = c // 4, c % 4
    a = arr_bs[bb]
    return np.ascontiguousarray(a.reshape(64, 128, *a.shape[1:])[j::4].reshape(2048, *a.shape[1:]))


def gather_tiles(per_core, bb):
    out = np.empty((64,) + per_core[0].shape[1:], per_core[0].dtype)
    for j in range(4):
        out[j::4] = per_core[bb * 4 + j]
    return out


def diff_masks(c):
    j = c % 4
    m = np.zeros((128, 4, 128), np.float32)
    for i in range(4):
        if i < j:
            m[:, i, :] = 1.0
        elif i == j:
            m[0:64, i, :] = 1.0
            m[64:128, i, 64:128] = 1.0
    return m.reshape(128, 512).astype(BF)


def dsa_negmask(c):
    return np.where(diff_masks(c).astype(np.float32).reshape(128, 4, 128).transpose(2, 1, 0).reshape(128, 512) > 0,
                    0.0, -1e30).astype(np.float32)


def build_L1():
    nc = bass.Bass("TRN2", target_bir_lowering=False)
    b = B(nc)
    io = {
        "x": b.dram("x", [2048, 1024], F32, "ExternalInput"),
        "pos": b.dram("pos", [128, 16], I32, "ExternalInput"),
        "gmix": b.dram("gmix", [128, 1024], F32, "ExternalInput"),
        "gqk": b.dram("gqk", [128, 2048], F32, "ExternalInput"),
        "w_in": b.dram("w_in", [1024, 3072], F32, "ExternalInput"),
        "QT": b.dram("QT", [16, 128, 1024], BF16, "ExternalOutput"),
        "KT": b.dram("KT", [16, 128, 1024], BF16, "ExternalOutput"),
        "V": b.dram("V", [16, 128, 8 * 129], BF16, "ExternalOutput"),
    }
    phase_diff_proj(b, io)
    b.finish()
    return nc


def build_L2():
    nc = bass.Bass("TRN2", target_bir_lowering=False)
    b = B(nc)
    io = {
        "QT": b.dram("QT", [16, 128, 1024], BF16, "ExternalInput"),
        "KTall": b.dram("KTall", [8, 128, 8192], BF16, "ExternalInput"),
        "Vall": b.dram("Vall", [8, 128, 64 * 129], BF16, "ExternalInput"),
        "maskT": b.dram("maskT", [128, 512], BF16, "ExternalInput"),
        "lam": b.dram("lam", [128, 256], F32, "ExternalInput"),
        "gsub": b.dram("gsub", [128, 128], F32, "ExternalInput"),
        "x": b.dram("x", [2048, 1024], F32, "ExternalInput"),
        "pos": b.dram("pos", [128, 16], I32, "ExternalInput"),
        "w_out": b.dram("w_out", [1024, 1024], F32, "ExternalInput"),
        "gmlp": b.dram("gmlp", [128, 1024], F32, "ExternalInput"),
        "w1": b.dram("w1", [1024, 4096], F32, "ExternalInput"),
        "w2": b.dram("w2", [4096, 1024], F32, "ExternalInput"),
        "gmix1": b.dram("gmix1", [128, 1024], F32, "ExternalInput"),
        "gqk2": b.dram("gqk2", [128, 2048], F32, "ExternalInput"),
        "gcq": b.dram("gcq", [128, 256], F32, "ExternalInput"),
        "w_in2": b.dram("w_in2", [1024, 2376], F32, "ExternalInput"),
        "w_uq": b.dram("w_uq", [256, 1024], F32, "ExternalInput"),
        "w_uqi": b.dram("w_uqi", [256, 512], F32, "ExternalInput"),
        "X2": b.dram("X2", [2048, 1024], F32, "ExternalOutput"),
        "QT2": b.dram("QT2", [16, 128, 1024], BF16, "ExternalOutput"),
        "KT2": b.dram("KT2", [16, 128, 1024], BF16, "ExternalOutput"),
        "V2": b.dram("V2", [16, 128, 16 * 65], BF16, "ExternalOutput"),
        "KI": b.dram("KI", [16, 128, 128], BF16, "ExternalOutput"),
        "QI": b.dram("QI", [16, 128, 512], BF16, "ExternalOutput"),
        "SG": b.dram("SG", [16, 128, 8], F32, "ExternalOutput"),
    }
    b.ident(); b.eps()
    pos_t = b.sb([128, NS], I32, "pos")
    b.op("sp", lambda e: e.dma_start(out=pos_t[:], in_=io["pos"][:]), writes=[pos_t], dma="ld_pos")
    cos, sin = rope_tables(b, pos_t)
    o_tok = [b.sb([128, 1024], BF16, f"otok{a}", top=True) for a in range(NS)]
    m1 = b.mark()
    phase_diff_attn(b, io, o_tok)
    b.release(m1)
    x_res = [b.sb([128, 1024], F32, f"xres{a}") for a in range(NS)]
    h2T = b.sb([128, 8, 2048], BF16, "h2T")
    m2 = b.mark()
    phase_post_attn(b, io, o_tok, x_res, h2T, io["w_out"][:], io["gmlp"][:], io["x"])
    b.release(m2)
    b.hi = ARENA_END
    m3 = b.mark()
    phase_mlp(b, x_res, h2T, io["w1"], io["w2"])
    b.release(m3)
    for a in range(NS):
        b.store("sp", lambda e, a=a: e.dma_start(out=io["X2"][a * 128:(a + 1) * 128, :], in_=x_res[a][:]),
                reads=[x_res[a]], dma="st_x2")
    phase_dsa_proj(b, io, x_res, cos, sin)
    b.finish()
    return nc


def l1_inputs(inp, c):
    return {"x": own_tiles(inp["x"], c),
            "pos": np.ascontiguousarray(own_tiles(inp["positions"], c).reshape(16, 128).T),
            "gmix": rep(inp["norm_mix"][0]),
            "gqk": rep(np.concatenate([np.tile(inp["diff_q_norm"][0], 16), np.tile(inp["diff_k_norm"][0], 16)])),
            "w_in": np.ascontiguousarray(inp["diff_w_in"][0])}


def l2_inputs(inp, r1):
    KTall = []; Vall = []
    for bb in range(2):
        kt = gather_tiles([r["KT"].reshape(16, 128, 8, 128) for r in r1], bb)
        KTall.append(np.ascontiguousarray(kt.transpose(2, 1, 0, 3).reshape(8, 128, 8192)))
        v = gather_tiles([r["V"].reshape(16, 128, 8, 129) for r in r1], bb)
        Vall.append(np.ascontiguousarray(v.transpose(2, 1, 0, 3).reshape(8, 128, 64 * 129)))
    lam = rep(np.concatenate([inp["diff_lam_q1"][0], inp["diff_lam_k1"][0], inp["diff_lam_q2"][0], inp["diff_lam_k2"][0]]))
    gqk2 = rep(np.concatenate([np.tile(inp["dsa_q_norm"][0], 16), np.tile(inp["dsa_k_norm"][0], 16)]))
    ins = []
    for c in range(8):
        ins.append({"QT": r1[c]["QT"], "KTall": KTall[c // 4], "Vall": Vall[c // 4], "maskT": diff_masks(c),
                    "lam": lam, "gsub": rep(inp["diff_subln"][0]),
                    "x": own_tiles(inp["x"], c),
                    "pos": np.ascontiguousarray(own_tiles(inp["positions"], c).reshape(16, 128).T),
                    "w_out": np.ascontiguousarray(inp["diff_w_out"][0]), "gmlp": rep(inp["norm_mlp"][0]),
                    "w1": np.ascontiguousarray(inp["mlp_w1"][0]), "w2": np.ascontiguousarray(inp["mlp_w2"][0]),
                    "gmix1": rep(inp["norm_mix"][1]), "gqk2": gqk2, "gcq": rep(inp["dsa_cq_norm"][0]),
                    "w_in2": np.ascontiguousarray(inp["dsa_w_in"][0]), "w_uq": np.ascontiguousarray(inp["dsa_w_uq"][0]),
                    "w_uqi": np.ascontiguousarray(inp["dsa_w_uq_idx"][0])})
    return ins


def run(nc, ins):
    res = run_bass_kernel_spmd(nc, ins, core_ids=list(range(8)))
    return [{k: np.asarray(v) for k, v in r.items()} for r in res.results]


def build_L3():
    nc = bass.Bass("TRN2", target_bir_lowering=False)
    b = B(nc)
    io = {
        "QT2": b.dram("QT2", [16, 128, 1024], BF16, "ExternalInput"),
        "QI": b.dram("QI", [16, 128, 512], BF16, "ExternalInput"),
        "SG": b.dram("SG", [16, 128, 8], F32, "ExternalInput"),
        "KIall": b.dram("KIall", [128, 8192], BF16, "ExternalInput"),
        "KT2all": b.dram("KT2all", [8, 128, 8192], BF16, "ExternalInput"),
        "V2all": b.dram("V2all", [8, 128, 64 * 130], BF16, "ExternalInput"),
        "negmask": b.dram("negmask", [128, 512], F32, "ExternalInput"),
        "X2": b.dram("X2", [2048, 1024], F32, "ExternalInput"),
        "w_out": b.dram("w_out", [1024, 1024], F32, "ExternalInput"),
        "gmlp": b.dram("gmlp", [128, 1024], F32, "ExternalInput"),
        "w1": b.dram("w1", [1024, 4096], F32, "ExternalInput"),
        "w2": b.dram("w2", [4096, 1024], F32, "ExternalInput"),
        "OUT": b.dram("OUT", [2048, 1024], F32, "ExternalOutput"),
    }
    b.ident(); b.eps()
    o_tok = [b.sb([128, 1024], BF16, f"otok{a}", top=True) for a in range(NS)]
    m1 = b.mark()
    phase_dsa_attn(b, io, o_tok)
    b.release(m1)
    x_res = [b.sb([128, 1024], F32, f"xres{a}") for a in range(NS)]
    h2T = b.sb([128, 8, 2048], BF16, "h2T")
    m2 = b.mark()
    phase_post_attn(b, io, o_tok, x_res, h2T, io["w_out"][:], io["gmlp"][:], io["X2"])
    b.release(m2)
    b.hi = ARENA_END
    m3 = b.mark()
    phase_mlp(b, x_res, h2T, io["w1"], io["w2"])
    b.release(m3)
    for a in range(NS):
        b.store("sp", lambda e, a=a: e.dma_start(out=io["OUT"][a * 128:(a + 1) * 128, :], in_=x_res[a][:]),
                reads=[x_res[a]], dma="st_out")
    b.finish()
    return nc


def l3_inputs(inp, r2):
    KIall = []; KTall = []; Vall = []
    for bb in range(2):
        ki = gather_tiles([r["KI"] for r in r2], bb)
        KIall.append(np.ascontiguousarray(ki.transpose(1, 0, 2).reshape(128, 8192)))
        kt = gather_tiles([r["KT2"].reshape(16, 128, 8, 128) for r in r2], bb)
        KTall.append(np.ascontiguousarray(kt.transpose(2, 1, 0, 3).reshape(8, 128, 8192)))
        v = gather_tiles([r["V2"].reshape(16, 128, 8, 130) for r in r2], bb)
        Vall.append(np.ascontiguousarray(v.transpose(2, 1, 0, 3).reshape(8, 128, 64 * 130)))
    ins = []
    for c in range(8):
        ins.append({"QT2": r2[c]["QT2"], "QI": r2[c]["QI"], "SG": r2[c]["SG"], "KIall": KIall[c // 4],
                    "KT2all": KTall[c // 4], "V2all": Vall[c // 4], "negmask": dsa_negmask(c),
                    "X2": r2[c]["X2"], "w_out": np.ascontiguousarray(inp["dsa_w_out"][0]), "gmlp": rep(inp["norm_mlp"][1]),
                    "w1": np.ascontiguousarray(inp["mlp_w1"][1]), "w2": np.ascontiguousarray(inp["mlp_w2"][1])})
    return ins


def assemble(r3):
    out = np.empty((2, 8192, 1024), np.float32)
    for bb in range(2):
        o = gather_tiles([r["OUT"].reshape(16, 128, 1024) for r in r3], bb)
        out[bb] = o.reshape(8192, 1024)
    return out


FUSED_IN = [
    ("x", [2048, 1024], F32), ("pos", [128, 16], I32), ("gmix", [128, 1024], F32), ("gqk", [128, 2048], F32),
    ("w_in", [1024, 3072], F32), ("maskT", [128, 512], BF16), ("lam", [128, 256], F32), ("gsub", [128, 128], F32),
    ("w_out", [1024, 1024], F32), ("gmlp", [128, 1024], F32), ("w1", [1024, 4096], F32), ("w2", [4096, 1024], F32),
    ("gmix1", [128, 1024], F32), ("gqk2", [128, 2048], F32), ("gcq", [128, 256], F32), ("w_in2", [1024, 2376], F32),
    ("w_uq", [256, 1024], F32), ("w_uqi", [256, 512], F32), ("negmask", [128, 512], F32),
    ("w_outb", [1024, 1024], F32), ("gmlpb", [128, 1024], F32), ("w1b", [1024, 4096], F32), ("w2b", [4096, 1024], F32),
]


def build_fused():
    nc = bass.Bass("TRN2", target_bir_lowering=False)
    _RANK.clear()
    b = B(nc)
    io = {n: b.dram(n, s, d, "ExternalInput") for (n, s, d) in FUSED_IN}
    io["OUT"] = b.dram("OUT", [2048, 1024], F32, "ExternalOutput")
    qt2 = nc.dram_tensor("scr_qt2", [16, 128, 1024], BF16).ap()
    qi = nc.dram_tensor("scr_qi", [16, 128, 512], BF16).ap()
    sg = nc.dram_tensor("scr_sg", [16, 128, 8], F32).ap()
    x2 = nc.dram_tensor("scr_x2", [2048, 1024], F32).ap()
    mqd = nc.dram_tensor("scr_mq", [16, 128, 8192], BF16).ap()
    scr = {"QT2": [T(qt2[a]) for a in range(NS)], "QI": [T(qi[a]) for a in range(NS)], "SG": [T(sg[a]) for a in range(NS)],
           "MQ": [T(mqd[a]) for a in range(NS)]}
    x2d = [T(x2[a * 128:(a + 1) * 128, :]) for a in range(NS)]

    b.ident(); b.eps()
    pos_t = b.sb([128, NS], I32, "pos")
    b.op("sp", lambda e: e.dma_start(out=pos_t[:], in_=io["pos"][:]), writes=[pos_t], dma="ld_pos")
    cos, sin = rope_tables(b, pos_t)
    o_tok = [b.sb([128, 1024], BF16, f"otok{a}", top=True) for a in range(NS)]
    hi_otok = b.hi
    qT_res = b.sb([128, NS, 1024], BF16, "qTres", top=True)
    m0 = b.mark()
    zt = b.sb([128, 4096], BF16, "zeros")
    b.op("pool", lambda e: e.memset(zt[:], 0.0), writes=[zt])
    G0 = Gather(b, "g0", 8, 4096, zt)
    G1 = Gather(b, "g1", 8, 4096, zt)
    GK = Gather(b, "gk", 1, 2048, zt)
    f_diff_proj(b, io, qT_res, G0, cos, sin)
    b.release(m0)
    f_diff_attn(b, io, o_tok, qT_res, G0)
    b.release(m0)
    b.hi = hi_otok
    x_res = [b.sb([128, 1024], F32, f"xres{a}") for a in range(NS)]
    mh = b.mark()
    h2T = b.sb([128, 8, 2048], BF16, "h2T")
    m2 = b.mark()
    phase_post_attn(b, io, o_tok, x_res, h2T, io["w_out"][:], io["gmlp"][:], io["x"])
    b.release(m2)
    b.hi = ARENA_END
    phase_mlp(b, x_res, h2T, io["w1"], io["w2"])
    b.release((mh[0], ARENA_END))
    for a in range(NS):
        b.op("sp", lambda e, a=a: e.dma_start(out=x2d[a][:], in_=x_res[a][:]), reads=[x_res[a]], writes=[x2d[a]], dma="st_x2")
    io2 = dict(io)
    f_dsa_proj(b, io2, x_res, cos, sin, G1, GK, scr)
    b.release(m0)
    b.hi = ARENA_END
    o_tok = [b.sb([128, 1024], BF16, f"otokb{a}", top=True) for a in range(NS)]
    m4 = b.mark()
    f_dsa_attn(b, io, o_tok, G1, GK, scr)
    b.release(m4)
    x_res = [b.sb([128, 1024], F32, f"xresb{a}") for a in range(NS)]
    h2T = b.sb([128, 8, 2048], BF16, "h2Tb")
    m5 = b.mark()
    phase_post_attn(b, io, o_tok, x_res, h2T, io["w_outb"][:], io["gmlpb"][:], x2, xdeps=x2d)
    b.release(m5)
    b.hi = ARENA_END
    phase_mlp(b, x_res, h2T, io["w1b"], io["w2b"])
    for a in range(NS):
        b.store("sp", lambda e, a=a: e.dma_start(out=io["OUT"][a * 128:(a + 1) * 128, :], in_=x_res[a][:]),
                reads=[x_res[a]], dma="st_out")
    b.finish()
    return nc


def fused_inputs(inp):
    lam = rep(np.concatenate([inp["diff_lam_q1"][0], inp["diff_lam_k1"][0], inp["diff_lam_q2"][0], inp["diff_lam_k2"][0]]))
    gqk = rep(np.concatenate([np.tile(inp["diff_q_norm"][0], 16), np.tile(inp["diff_k_norm"][0], 16)]))
    gqk2 = rep(np.concatenate([np.tile(inp["dsa_q_norm"][0], 16), np.tile(inp["dsa_k_norm"][0], 16)]))
    c_ = np.ascontiguousarray
    shared = {"gmix": rep(inp["norm_mix"][0]), "gqk": gqk, "w_in": c_(inp["diff_w_in"][0]), "lam": lam,
              "gsub": rep(inp["diff_subln"][0]), "w_out": c_(inp["diff_w_out"][0]), "gmlp": rep(inp["norm_mlp"][0]),
              "w1": c_(inp["mlp_w1"][0]), "w2": c_(inp["mlp_w2"][0]), "gmix1": rep(inp["norm_mix"][1]), "gqk2": gqk2,
              "gcq": rep(inp["dsa_cq_norm"][0]), "w_in2": c_(inp["dsa_w_in"][0]), "w_uq": c_(inp["dsa_w_uq"][0]),
              "w_uqi": c_(inp["dsa_w_uq_idx"][0]), "w_outb": c_(inp["dsa_w_out"][0]), "gmlpb": rep(inp["norm_mlp"][1]),
              "w1b": c_(inp["mlp_w1"][1]), "w2b": c_(inp["mlp_w2"][1])}
    ins = []
    for c in range(8):
        d = dict(shared)
        d["x"] = own_tiles(inp["x"], c)
        d["pos"] = np.ascontiguousarray(own_tiles(inp["positions"], c).reshape(16, 128).T)
        d["maskT"] = diff_masks(c)
        d["negmask"] = dsa_negmask(c)
        ins.append(d)
    return ins


def kernel(**inputs):
    inp = {k: np.asarray(v) for k, v in inputs.items()}
    r = run(build_fused(), fused_inputs(inp))
    return assemble(r)
```

```python
import math
from contextlib import ExitStack
import numpy as np
import ml_dtypes
import concourse.bass as bass
import concourse.mybir as mybir
from concourse.bass_utils import run_bass_kernel_spmd


F32 = mybir.dt.float32
BF16 = mybir.dt.bfloat16
I32 = mybir.dt.int32
AF = mybir.ActivationFunctionType
ALU = mybir.AluOpType
AX = mybir.AxisListType


class Dep:
    __slots__ = ("w", "r")

    def __init__(self):
        self.w = None
        self.r = {}


class Sched:
    ENG = ("pe", "act", "dve", "pool", "sp")

    def __init__(self, nc):
        self.nc = nc
        self.ops = {e: [] for e in self.ENG}
        self.cnt = {e: 0 for e in self.ENG}
        self.known = {e: {} for e in self.ENG}
        self.dma_cnt = {}
        self.stack = ExitStack()
        self.nt = 0

    def sb(self, shape, dtype, name=None):
        self.nt += 1
        name = "sb_" + (name or f"t{self.nt}")
        return self.stack.enter_context(self.nc.sbuf_tensor(name, list(shape), dtype))

    def ps(self, shape, dtype, name=None):
        self.nt += 1
        name = "ps_" + (name or f"p{self.nt}")
        return self.stack.enter_context(self.nc.psum_tensor(name, list(shape), dtype))

    def op(self, eng, fn, reads=(), writes=(), dma=None, sem_inc=16):
        waits = {}

        def need(ev, raw):
            if ev is None:
                return
            key, val = ev
            if key == eng:
                if eng == "pe" or not raw:
                    return
            if waits.get(key, 0) < val:
                waits[key] = val

        for d in reads:
            need(d.w, True)
        for d in writes:
            need(d.w, False)
            for ev in d.r.items():
                need(ev, False)
        kn = self.known[eng]
        wl = []
        for key, val in waits.items():
            if kn.get(key, 0) >= val:
                continue
            kn[key] = val
            wl.append((key, val))
        if dma is not None:
            n = self.dma_cnt.get(dma, 0) + sem_inc
            self.dma_cnt[dma] = n
            ev = (dma, n)
            inc = (dma, sem_inc)
        else:
            self.cnt[eng] += 1
            ev = (eng, self.cnt[eng])
            inc = (eng, 1)
        self.ops[eng].append((wl, fn, inc))
        for d in reads:
            if d.r.get(ev[0], 0) < ev[1]:
                d.r[ev[0]] = ev[1]
        for d in writes:
            d.w = ev
            d.r = {}
        return ev

    def final_wait(self, eng, deps):
        waits = {}
        for d in deps:
            for ev in ([d.w] if d.w else []) + list(d.r.items()):
                if waits.get(ev[0], 0) < ev[1]:
                    waits[ev[0]] = ev[1]
        self.ops[eng].append((list(waits.items()), None, None))

    def barrier(self):
        waits = {e: self.cnt[e] for e in self.ENG if self.cnt[e] > 0}
        for k, n in self.dma_cnt.items():
            if not k.startswith("cc_"):
                waits[k] = n
        for e in self.ENG:
            kn = self.known[e]
            wl = []
            for key, val in waits.items():
                if key == e or kn.get(key, 0) >= val:
                    continue
                kn[key] = val
                wl.append((key, val))
            self.ops[e].append((wl, None, None))

    def final_events(self, eng, evs):
        waits = {}
        for ev in evs:
            if waits.get(ev[0], 0) < ev[1]:
                waits[ev[0]] = ev[1]
        self.ops[eng].append((list(waits.items()), None, None))

    def emit(self):
        nc = self.nc
        keys = set(self.ENG) | set(self.dma_cnt.keys())
        assert len(keys) <= 100, f"too many semaphores: {len(keys)}"
        sems = {}
        for k in sorted(keys):
            sems[k] = self.stack.enter_context(nc.semaphore("s_" + k))
        ops = self.ops

        def run(e, lst):
            for wl, fn, inc in lst:
                for key, val in wl:
                    e.wait_ge(sems[key], val)
                if fn is None:
                    continue
                ins = fn(e)
                if inc is not None:
                    ins.then_inc(sems[inc[0]], inc[1])

        with nc.Block() as block:
            @block.tensor
            def _(e):
                run(e, ops["pe"])

            @block.scalar
            def _(e):
                run(e, ops["act"])

            @block.vector
            def _(e):
                run(e, ops["dve"])

            @block.gpsimd
            def _(e):
                run(e, ops["pool"])

            @block.sync
            def _(e):
                run(e, ops["sp"])
        self.stack.close()


NS = 16
D = 1024
EPS = 1e-6
INV_FREQ = [500000.0 ** (-(2.0 * j) / 16.0) for j in range(8)]
TWO_PI_S = 6.28318
PI_S = 3.14159


class T:
    def __init__(self, t, d=None):
        self.t = t
        self.d = d if d is not None else Dep()

    def __getitem__(self, k):
        return self.t[k]


ARENA_BASE = 16512
ARENA_END = 16512 + 212736


class B:
    def __init__(self, nc):
        self.nc = nc
        self.S = Sched(nc)
        self._consts = {}
        self.final = []
        self.arena = nc.alloc_sbuf_tensor("arena", [128, ARENA_END - ARENA_BASE], mybir.dt.uint8)
        self.lo = ARENA_BASE
        self.hi = ARENA_END
        self.nt = 0
        self.banks = [T(nc.alloc_psum_tensor(f"bank{i}", [128, 512], F32)) for i in range(8)]
        self.banks16 = [T(bk.t[:].bitcast(BF16), bk.d) for bk in self.banks]

    def sb(self, shape, dt, name=None, top=False):
        self.nt += 1
        nm = f"sb{self.nt}_{name or 't'}"
        size = int(np.prod(shape[1:])) * mybir.dt.size(dt)
        size = (size + 31) // 32 * 32
        if top:
            self.hi -= size
            off = self.hi
        else:
            off = self.lo
            self.lo += size
        assert self.lo <= self.hi, f"SBUF arena overflow at {nm}: lo={self.lo} hi={self.hi}"
        return T(self.nc.alloc_sbuf_tensor_at(nm, list(shape), dt, offset=off))

    def mark(self):
        return (self.lo, self.hi)

    def release(self, m):
        self.lo, self.hi = m
        self.S.barrier()

    def dram(self, name, shape, dt, kind):
        t = T(self.nc.dram_tensor(name, list(shape), dt, kind=kind).ap())
        return t

    def op(self, eng, fn, reads=(), writes=(), dma=None, sem_inc=16):
        return self.S.op(eng, fn, [x.d for x in reads], [x.d for x in writes], dma, sem_inc)

    def store(self, eng, fn, reads, dma):
        ev = self.S.op(eng, fn, [x.d for x in reads], [], dma)
        self.final.append(ev)
        return ev

    def finish(self):
        self.S.final_events("sp", self.final)
        self.S.emit()

    def ident(self):
        if "ident" not in self._consts:
            idt = self.sb([128, 128], BF16, "ident")
            self.op("pool", lambda e: e.memset(idt[:], 0.0), writes=[idt])
            self.op("pool", lambda e: e.affine_select(out=idt[:], in_=idt[:], pattern=[[-1, 128]],
                                                     compare_op=ALU.not_equal, fill=1.0, base=0,
                                                     channel_multiplier=1), reads=[idt], writes=[idt])
            self._consts["ident"] = idt
        return self._consts["ident"]

    def eps(self):
        if "eps" not in self._consts:
            t = self.sb([128, 1], F32, "eps")
            self.op("pool", lambda e: e.memset(t[:], EPS), writes=[t])
            self._consts["eps"] = t
        return self._consts["eps"]


def rstd_from_ss(b, ss, n, scale):
    eps = b.eps()
    b.op("act", lambda e: e.activation(out=ss[:, 0:n], in_=ss[:, 0:n], func=AF.Ln, bias=eps[:], scale=scale),
         reads=[ss, eps], writes=[ss])
    b.op("act", lambda e: e.activation(out=ss[:, 0:n], in_=ss[:, 0:n], func=AF.Exp, scale=-0.5),
         reads=[ss], writes=[ss])


def rope_tables(b, pos_t):
    posf = b.sb([128, NS], F32, "posf")
    inv = b.sb([128, 8], F32, "invf")
    ang = b.sb([128, NS, 8], F32, "ang")
    ti = b.sb([128, NS, 8], I32, "angi")
    tf = b.sb([128, NS, 8], F32, "angf")
    neg = b.sb([128, NS, 8], F32, "angn")
    cos = b.sb([128, NS, 8], F32, "cos")
    sin = b.sb([128, NS, 8], F32, "sin")
    nb = b.sb([128, 1], F32, "negpi")
    b.op("pool", lambda e: e.memset(nb[:], -PI_S), writes=[nb])
    b.op("dve", lambda e: e.tensor_copy(out=posf[:], in_=pos_t[:]), reads=[pos_t], writes=[posf])
    for j in range(8):
        b.op("pool", lambda e, j=j: e.memset(inv[:, j:j + 1], INV_FREQ[j] / (2 * math.pi)), writes=[inv])
    b.op("dve", lambda e: e.tensor_tensor(out=ang[:], in0=posf[:].unsqueeze(2).to_broadcast([128, NS, 8]),
                                          in1=inv[:].unsqueeze(1).to_broadcast([128, NS, 8]), op=ALU.mult),
         reads=[posf, inv], writes=[ang])
    for (dst, off) in ((sin, 0.5), (cos, 0.75)):
        b.op("dve", lambda e, off=off: e.tensor_scalar(out=tf[:], in0=ang[:], scalar1=off, scalar2=None, op0=ALU.add),
             reads=[ang], writes=[tf])
        b.op("dve", lambda e: e.tensor_copy(out=ti[:], in_=tf[:]), reads=[tf], writes=[ti])
        b.op("dve", lambda e: e.tensor_copy(out=neg[:], in_=ti[:]), reads=[ti], writes=[neg])
        b.op("dve", lambda e: e.tensor_tensor(out=tf[:], in0=tf[:], in1=neg[:], op=ALU.subtract),
             reads=[tf, neg], writes=[tf])
        b.op("dve", lambda e: e.tensor_scalar(out=neg[:], in0=tf[:], scalar1=0.0, scalar2=None, op0=ALU.is_lt),
             reads=[tf], writes=[neg])
        b.op("dve", lambda e: e.tensor_tensor(out=tf[:], in0=tf[:], in1=neg[:], op=ALU.add),
             reads=[tf, neg], writes=[tf])
        b.op("act", lambda e, dst=dst: e.activation(out=dst[:], in_=tf[:], func=AF.Sin, bias=nb[:], scale=TWO_PI_S),
             reads=[tf, nb], writes=[dst])
    return cos, sin


def rmsnorm_transpose(b, xt, gt, hbf, hT, pT, ss, junk):
    idt = b.ident()
    b.op("act", lambda e: e.activation(out=junk[:], in_=xt[:], func=AF.Square, accum_out=ss[:, 0:1]),
         reads=[xt], writes=[junk, ss])
    rstd_from_ss(b, ss, 1, 1.0 / D)
    b.op("dve", lambda e: e.scalar_tensor_tensor(out=hbf[:], in0=xt[:], scalar=ss[:, 0:1], in1=gt[:],
                                                 op0=ALU.mult, op1=ALU.mult),
         reads=[xt, ss, gt], writes=[hbf])
    for kc in range(8):
        b.op("pe", lambda e, kc=kc: e.transpose(out=pT[:, kc * 128:(kc + 1) * 128],
                                                in_=hbf[:, kc * 128:(kc + 1) * 128], identity=idt[:]),
             reads=[hbf, idt], writes=[pT])
    b.op("dve", lambda e: e.tensor_copy(out=hT[:].rearrange("p a b -> p (a b)"), in_=pT[:]),
         reads=[pT], writes=[hT])


def headnorm_rope(b, stage, sq, ssq, nh, gain, cos_a, sin_a, outbf, tmp, norm=True):
    s3 = stage[:].rearrange("p (h d) -> p h d", d=64)
    if norm:
        b.op("dve", lambda e: e.tensor_reduce(out=ssq[:, 0:nh], in_=sq[:].rearrange("p (h d) -> p h d", d=64),
                                              axis=AX.X, op=ALU.add), reads=[sq], writes=[ssq])
        rstd_from_ss(b, ssq, nh, 1.0 / 64)
    b.op("dve", lambda e: e.tensor_tensor(out=s3, in0=s3, in1=ssq[:, 0:nh].unsqueeze(2).to_broadcast([128, nh, 64]),
                                          op=ALU.mult), reads=[stage, ssq], writes=[stage])
    if gain is not None:
        b.op("pool", lambda e: e.tensor_tensor(out=stage[:], in0=stage[:], in1=gain[:], op=ALU.mult),
             reads=[stage, gain], writes=[stage])
    b.op("act", lambda e: e.activation(out=outbf[:], in_=stage[:], func=AF.Copy), reads=[stage], writes=[outbf])
    o3 = outbf[:].rearrange("p (h d) -> p h d", d=64)
    x1 = s3[:, :, 0:8]
    x2 = s3[:, :, 8:16]
    cb = cos_a.unsqueeze(1).to_broadcast([128, nh, 8])
    sb_ = sin_a.unsqueeze(1).to_broadcast([128, nh, 8])
    t = tmp[:].rearrange("p (k h d) -> p k h d", k=4, d=8)
    eng = "pool"
    b.op(eng, lambda e: e.tensor_tensor(out=t[:, 0, 0:nh, :], in0=x1, in1=cb, op=ALU.mult), reads=[stage], writes=[tmp])
    b.op(eng, lambda e: e.tensor_tensor(out=t[:, 1, 0:nh, :], in0=x2, in1=sb_, op=ALU.mult), reads=[stage], writes=[tmp])
    b.op(eng, lambda e: e.tensor_tensor(out=t[:, 2, 0:nh, :], in0=x2, in1=cb, op=ALU.mult), reads=[stage], writes=[tmp])
    b.op(eng, lambda e: e.tensor_tensor(out=t[:, 3, 0:nh, :], in0=x1, in1=sb_, op=ALU.mult), reads=[stage], writes=[tmp])
    b.op(eng, lambda e: e.tensor_tensor(out=o3[:, :, 0:8], in0=t[:, 0, 0:nh, :], in1=t[:, 1, 0:nh, :], op=ALU.subtract),
         reads=[tmp], writes=[outbf])
    b.op(eng, lambda e: e.tensor_tensor(out=o3[:, :, 8:16], in0=t[:, 2, 0:nh, :], in1=t[:, 3, 0:nh, :], op=ALU.add),
         reads=[tmp], writes=[outbf])


def phase_diff_proj(b, io):
    idt = b.ident()
    pos_t = b.sb([128, NS], I32, "pos")
    b.op("sp", lambda e: e.dma_start(out=pos_t[:], in_=io["pos"][:]), writes=[pos_t], dma="ld_pos")
    gmix = b.sb([128, D], F32, "gmix")
    b.op("sp", lambda e: e.dma_start(out=gmix[:], in_=io["gmix"][:]), writes=[gmix], dma="ld_gmix")
    gqk = b.sb([128, 2048], F32, "gqk")
    b.op("sp", lambda e: e.dma_start(out=gqk[:], in_=io["gqk"][:]), writes=[gqk], dma="ld_gqk")
    b.op("pool", lambda e: e.tensor_scalar(out=gqk[:, 0:1024], in0=gqk[:, 0:1024], scalar1=0.125, scalar2=None,
                                           op0=ALU.mult), reads=[gqk], writes=[gqk])
    w = b.sb([128, 8, 3072], BF16, "w_in")
    for kc in range(8):
        for hf in range(3):
            b.op("pool", lambda e, kc=kc, hf=hf: e.dma_start(
                out=w[:, kc, hf * 1024:(hf + 1) * 1024],
                in_=io["w_in"][kc * 128:(kc + 1) * 128, hf * 1024:(hf + 1) * 1024]),
                writes=[w], dma="ld_w")
    cos, sin = rope_tables(b, pos_t)

    xts = [b.sb([128, D], F32, f"xt{i}") for i in range(2)]
    junk = b.sb([128, D], BF16, "junk")
    ss = b.sb([128, 1], F32, "ss")
    hbf = b.sb([128, D], BF16, "hbf")
    hT = b.sb([128, 8, 128], BF16, "hT")
    pT = [b.banks16[0], b.banks16[1]]
    pY = [b.banks[2 + i] for i in range(4)]
    stage = b.sb([128, 2048], F32, "stage")
    sq = b.sb([128, 2048], F32, "sq")
    ssq = b.sb([128, 32], F32, "ssq")
    tmp = b.sb([128, 4 * 32 * 8], F32, "ropetmp")
    qkbf = b.sb([128, 2048], BF16, "qkbf")
    qkT = [b.sb([128, 16, 128], BF16, f"qkT{i}") for i in range(2)]
    vaug = [b.sb([128, 8, 129], BF16, f"vaug{i}") for i in range(2)]
    for i in range(2):
        b.op("pool", lambda e, i=i: e.memset(vaug[i][:], 1.0), writes=[vaug[i]])

    for a in range(NS):
        xt = xts[a % 2]
        b.op("sp", lambda e, a=a, xt=xt: e.dma_start(out=xt[:], in_=io["x"][a * 128:(a + 1) * 128, :]),
             writes=[xt], dma=f"ld_x{a % 2}")
        rmsnorm_transpose(b, xt, gmix, hbf, hT, pT[0], ss, junk)
        for n in range(6):
            py = pY[n % 4]
            for kc in range(8):
                b.op("pe", lambda e, n=n, kc=kc, py=py: e.matmul(py[:], lhsT=hT[:, kc, :],
                                                                   rhs=w[:, kc, n * 512:(n + 1) * 512],
                                                                   start=(kc == 0), stop=(kc == 7)),
                     reads=[hT, w], writes=[py])
            if n < 4:
                b.op("act", lambda e, n=n, py=py: e.activation(out=stage[:, n * 512:(n + 1) * 512], in_=py[:], func=AF.Copy),
                     reads=[py], writes=[stage])
                b.op("act", lambda e, n=n, py=py: e.activation(out=sq[:, n * 512:(n + 1) * 512], in_=py[:], func=AF.Square),
                     reads=[py], writes=[sq])
            else:
                va = vaug[a % 2]
                b.op("dve", lambda e, n=n, py=py, va=va: e.tensor_copy(
                    out=va[:, (n - 4) * 4:(n - 4) * 4 + 4, 0:128], in_=py[:].rearrange("p (h d) -> p h d", d=128)),
                    reads=[py], writes=[va])
        va = vaug[a % 2]
        b.store("sp", lambda e, a=a, va=va: e.dma_start(out=io["V"][a], in_=va[:].rearrange("p h d -> p (h d)")),
                reads=[va], dma=f"st_v{a % 2}")
        headnorm_rope(b, stage, sq, ssq, 32, gqk, cos[:, a, :], sin[:, a, :], qkbf, tmp)
        qt = qkT[a % 2]
        for i in range(16):
            pt = pT[1] if i < 8 else pT[0]
            b.op("pe", lambda e, i=i, pt=pt: e.transpose(out=pt[:, (i % 8) * 128:(i % 8 + 1) * 128],
                                                         in_=qkbf[:, i * 128:(i + 1) * 128], identity=idt[:]),
                 reads=[qkbf, idt], writes=[pt])
            if i % 8 == 7:
                b.op("dve", lambda e, i=i, pt=pt, qt=qt: e.tensor_copy(
                    out=qt[:, (i // 8) * 8:(i // 8) * 8 + 8, :].rearrange("p a b -> p (a b)"), in_=pt[:]),
                    reads=[pt], writes=[qt])
        b.store("sp", lambda e, a=a, qt=qt: e.dma_start(out=io["QT"][a], in_=qt[:, 0:8, :].rearrange("p a b -> p (a b)")),
                reads=[qt], dma=f"st_q{a % 2}")
        b.store("sp", lambda e, a=a, qt=qt: e.dma_start(out=io["KT"][a], in_=qt[:, 8:16, :].rearrange("p a b -> p (a b)")),
                reads=[qt], dma=f"st_k{a % 2}")


def phase_diff_attn(b, io, o_tok):
    LAM_INIT = 0.2
    qT = b.sb([128, NS, 1024], BF16, "qT")
    for a in range(NS):
        b.op("sp", lambda e, a=a: e.dma_start(out=qT[:, a, :], in_=io["QT"][a]), writes=[qT], dma="ld_qT")
    maskT = b.sb([128, 512], BF16, "maskT")
    b.op("sp", lambda e: e.dma_start(out=maskT[:], in_=io["maskT"][:]), writes=[maskT], dma="ld_mask")
    lam = b.sb([128, 256], F32, "lam")
    b.op("sp", lambda e: e.dma_start(out=lam[:], in_=io["lam"][:]), writes=[lam], dma="ld_lam")
    gsub = b.sb([128, 128], F32, "gsub")
    b.op("sp", lambda e: e.dma_start(out=gsub[:], in_=io["gsub"][:]), writes=[gsub], dma="ld_gsub")
    b.op("pool", lambda e: e.tensor_scalar(out=gsub[:], in0=gsub[:], scalar1=1.0 - LAM_INIT, scalar2=None,
                                           op0=ALU.mult), reads=[gsub], writes=[gsub])
    lprod = b.sb([128, 128], F32, "lprod")
    l2 = b.sb([128, 2], F32, "l2")
    neglam = b.sb([128, 1], F32, "neglam")
    l4 = lam[:].rearrange("p (a b d) -> p a b d", a=2, b=2)
    b.op("dve", lambda e: e.tensor_tensor(out=lprod[:].rearrange("p (a d) -> p a d", a=2), in0=l4[:, :, 0, :],
                                          in1=l4[:, :, 1, :], op=ALU.mult), reads=[lam], writes=[lprod])
    b.op("dve", lambda e: e.tensor_reduce(out=l2[:], in_=lprod[:].rearrange("p (a d) -> p a d", a=2), axis=AX.X,
                                          op=ALU.add), reads=[lprod], writes=[l2])
    b.op("act", lambda e: e.activation(out=l2[:], in_=l2[:], func=AF.Exp), reads=[l2], writes=[l2])
    b.op("dve", lambda e: e.tensor_tensor(out=neglam[:], in0=l2[:, 1:2], in1=l2[:, 0:1], op=ALU.subtract),
         reads=[l2], writes=[neglam])
    b.op("dve", lambda e: e.tensor_scalar(out=neglam[:], in0=neglam[:], scalar1=-LAM_INIT, scalar2=None, op0=ALU.add),
         reads=[neglam], writes=[neglam])

    ktb = [b.sb([128, 8192], BF16, f"ktb{i}") for i in range(2)]
    vb = [b.sb([128, 64, 129], BF16, f"vb{i}") for i in range(2)]
    pS = [[b.banks[c * 2 + i] for i in range(2)] for c in range(2)]
    pA = [[b.banks[4 + c * 2 + i] for i in range(2)] for c in range(2)]
    pT = [[b.sb([128, 512], BF16, f"pTs{c}{i}") for i in range(2)] for c in range(2)]
    rec = [b.sb([128, 2], F32, f"rec{i}") for i in range(2)]
    o32 = [b.sb([128, 128], F32, f"o32{i}") for i in range(2)]
    oj = b.sb([128, 128], BF16, "ojunk")
    ss1 = [b.sb([128, 1], F32, f"ss1{i}") for i in range(2)]

    units = [(h, a, blk) for h in range(8) for a in range(NS) for blk in range(a + 1)]

    def load_head(h):
        kb, vv = ktb[h % 2], vb[h % 2]
        for part in range(4):
            b.op("sp", lambda e, h=h, kb=kb, part=part: e.dma_start(
                out=kb[:, part * 2048:(part + 1) * 2048], in_=io["KTall"][h][:, part * 2048:(part + 1) * 2048]),
                writes=[kb], dma=f"ld_kt{h % 2}")
            b.op("sp", lambda e, h=h, vv=vv, part=part: e.dma_start(
                out=vv[:, part * 16:(part + 1) * 16, :].rearrange("p a b -> p (a b)"),
                in_=io["Vall"][h][:, part * 16 * 129:(part + 1) * 16 * 129]),
                writes=[vv], dma=f"ld_v{h % 2}")

    def qk(u, n):
        h, a, blk = u
        kb = ktb[h % 2]
        for c in range(2):
            ps = pS[c][n % 2]
            for i in range(4):
                kt = 4 * blk + i
                b.op("pe", lambda e, c=c, i=i, kt=kt, ps=ps, kb=kb, a=a, h=h: e.matmul(
                    ps[:, i * 128:(i + 1) * 128], lhsT=kb[64 * c:64 * c + 64, kt * 128:(kt + 1) * 128],
                    rhs=qT[64 * c:64 * c + 64, a, h * 128:(h + 1) * 128], start=True, stop=True),
                    reads=[kb, qT], writes=[ps])

    def softmax_pv(u, n):
        h, a, blk = u
        vv = vb[h % 2]
        for c in range(2):
            ps = pS[c][n % 2]
            pt = pT[c][n % 2]
            acc = pA[c][a % 2]
            b.op("act", lambda e, ps=ps, pt=pt: e.activation(out=pt[:], in_=ps[:], func=AF.Exp),
                 reads=[ps], writes=[pt])
            if blk == a:
                b.op("pool", lambda e, pt=pt: e.tensor_tensor(out=pt[:], in0=pt[:], in1=maskT[:], op=ALU.mult),
                     reads=[pt, maskT], writes=[pt])
            for i in range(4):
                kt = 4 * blk + i
                b.op("pe", lambda e, i=i, kt=kt, pt=pt, acc=acc, vv=vv, blk=blk, a=a: e.matmul(
                    acc[:, 0:129], lhsT=pt[:, i * 128:(i + 1) * 128], rhs=vv[:, kt, :],
                    start=(blk == 0 and i == 0), stop=(blk == a and i == 3)),
                    reads=[pt, vv], writes=[acc])
        if blk == a:
            evac(h, a)

    def evac(h, a):
        k = a % 2
        a0, a1 = pA[0][k], pA[1][k]
        r, o, s = rec[k], o32[k], ss1[k]
        b.op("dve", lambda e: e.reciprocal(out=r[:, 0:1], in_=a0[:, 128:129]), reads=[a0], writes=[r])
        b.op("dve", lambda e: e.reciprocal(out=r[:, 1:2], in_=a1[:, 128:129]), reads=[a1], writes=[r])
        b.op("dve", lambda e: e.tensor_tensor(out=r[:, 1:2], in0=r[:, 1:2], in1=neglam[:], op=ALU.mult),
             reads=[r, neglam], writes=[r])
        b.op("dve", lambda e: e.tensor_scalar(out=o[:], in0=a0[:, 0:128], scalar1=r[:, 0:1], scalar2=None, op0=ALU.mult),
             reads=[a0, r], writes=[o])
        b.op("dve", lambda e: e.scalar_tensor_tensor(out=o[:], in0=a1[:, 0:128], scalar=r[:, 1:2], in1=o[:],
                                                     op0=ALU.mult, op1=ALU.add), reads=[a1, r, o], writes=[o])
        b.op("act", lambda e: e.activation(out=oj[:], in_=o[:], func=AF.Square, accum_out=s[:, 0:1]),
             reads=[o], writes=[oj, s])
        rstd_from_ss(b, s, 1, 1.0 / 128)
        ot = o_tok[a]
        b.op("dve", lambda e: e.scalar_tensor_tensor(out=ot[:, h * 128:(h + 1) * 128], in0=o[:], scalar=s[:, 0:1],
                                                     in1=gsub[:], op0=ALU.mult, op1=ALU.mult),
             reads=[o, s, gsub], writes=[ot])

    load_head(0)
    qk(units[0], 0)
    for n, u in enumerate(units):
        if u[1] == 0 and u[2] == 0 and u[0] + 1 < 8:
            load_head(u[0] + 1)
        if n + 1 < len(units):
            qk(units[n + 1], n + 1)
        softmax_pv(u, n)


def load_w_bf16(b, wt, src, nk, ncols, key, colchunk=1024):
    for c0 in range(0, ncols, colchunk):
        for kc in range(nk):
            c1 = min(ncols, c0 + colchunk)
            b.op("pool", lambda e, kc=kc, c0=c0, c1=c1: e.dma_start(
                out=wt[:, kc, c0:c1], in_=src[kc * 128:(kc + 1) * 128, c0:c1]), writes=[wt], dma=key)


def phase_post_attn(b, io, o_tok, x_res, h2T, wout_ap, gmlp_ap, xsrc, xdeps=None):
    idt = b.ident()
    m = b.mark()
    wout = b.sb([128, 8, 1024], BF16, "wout")
    load_w_bf16(b, wout, wout_ap, 8, 1024, "ld_wout")
    gm = b.sb([128, D], F32, "gmlp")
    b.op("sp", lambda e: e.dma_start(out=gm[:], in_=gmlp_ap), writes=[gm], dma="ld_gmlp")
    oT = [b.sb([128, 8, 128], BF16, f"oT{i}") for i in range(2)]
    junk = b.sb([128, D], BF16, "junk2")
    ss = [b.sb([128, 1], F32, f"ss2{i}") for i in range(2)]
    hbf = [b.sb([128, D], BF16, f"hbf2{i}") for i in range(2)]
    for a in range(NS):
        xr = x_res[a]
        b.op("sp", lambda e, a=a, xr=xr: e.dma_start(out=xr[:], in_=xsrc[a * 128:(a + 1) * 128, :]),
             reads=([xdeps[a]] if xdeps else []), writes=[xr], dma="ld_xres")
        pt = b.banks16[a % 2]
        ot = oT[a % 2]
        for kc in range(8):
            b.op("pe", lambda e, kc=kc, pt=pt, a=a: e.transpose(out=pt[:, kc * 128:(kc + 1) * 128],
                                                                 in_=o_tok[a][:, kc * 128:(kc + 1) * 128], identity=idt[:]),
                 reads=[o_tok[a], idt], writes=[pt])
        b.op("act", lambda e, pt=pt, ot=ot: e.activation(out=ot[:].rearrange("p a b -> p (a b)"), in_=pt[:], func=AF.Copy),
             reads=[pt], writes=[ot])
        for n in range(2):
            py = b.banks[2 + (2 * a + n) % 4]
            for kc in range(8):
                b.op("pe", lambda e, kc=kc, n=n, py=py, ot=ot: e.matmul(py[:], lhsT=ot[:, kc, :],
                                                                         rhs=wout[:, kc, n * 512:(n + 1) * 512],
                                                                         start=(kc == 0), stop=(kc == 7)),
                     reads=[ot, wout], writes=[py])
            b.op("dve", lambda e, n=n, py=py, xr=xr: e.tensor_tensor(out=xr[:, n * 512:(n + 1) * 512],
                                                                     in0=xr[:, n * 512:(n + 1) * 512], in1=py[:], op=ALU.add),
                 reads=[xr, py], writes=[xr])
        rms_to_hT(b, xr, gm, hbf[a % 2], h2T, a, b.banks16[6 + a % 2], ss[a % 2], junk)
    return m


def rms_to_hT(b, xr, gm, hbf, h2T, a, pt, ss, junk):
    idt = b.ident()
    b.op("act", lambda e: e.activation(out=junk[:], in_=xr[:], func=AF.Square, accum_out=ss[:, 0:1]),
         reads=[xr], writes=[junk, ss])
    rstd_from_ss(b, ss, 1, 1.0 / D)
    b.op("dve", lambda e: e.scalar_tensor_tensor(out=hbf[:], in0=xr[:], scalar=ss[:, 0:1], in1=gm[:],
                                                 op0=ALU.mult, op1=ALU.mult), reads=[xr, ss, gm], writes=[hbf])
    for kc in range(8):
        b.op("pe", lambda e, kc=kc: e.transpose(out=pt[:, kc * 128:(kc + 1) * 128],
                                                in_=hbf[:, kc * 128:(kc + 1) * 128], identity=idt[:]),
             reads=[hbf, idt], writes=[pt])
    b.op("act", lambda e: e.activation(out=h2T[:, :, a * 128:(a + 1) * 128],
                                       in_=pt[:].rearrange("p (k t) -> p k t", k=8), func=AF.Copy),
         reads=[pt], writes=[h2T])


def phase_mlp(b, x_res, h2T, w1_ap, w2_ap, hook=None):
    NFC = 8
    w1c = [b.sb([128, 8, 512], BF16, f"w1c{i}") for i in range(2)]
    w2c = [b.sb([128, 4, 1024], BF16, f"w2c{i}") for i in range(2)]
    rbuf = [b.sb([128, 512], F32, f"rbuf{i}") for i in range(2)]
    uT = [b.sb([128, 4, 512], BF16, f"uT{i}") for i in range(2)]
    pu = [b.banks[0], b.banks[1]]
    po = [b.banks[2 + i] for i in range(4)]

    def load_chunk(fc):
        w1, w2 = w1c[fc % 2], w2c[fc % 2]
        for kc in range(8):
            b.op("pool", lambda e, kc=kc, fc=fc, w1=w1: e.dma_start(
                out=w1[:, kc, :], in_=w1_ap[kc * 128:(kc + 1) * 128, fc * 512:(fc + 1) * 512]),
                writes=[w1], dma=f"ld_w1{fc % 2}")
        for ft in range(4):
            b.op("pool", lambda e, ft=ft, fc=fc, w2=w2: e.dma_start(
                out=w2[:, ft, :], in_=w2_ap[fc * 512 + ft * 128:fc * 512 + (ft + 1) * 128, :]),
                writes=[w2], dma=f"ld_w2{fc % 2}")

    steps = [(fc, tg) for fc in range(NFC) for tg in range(4)]
    cnt = {"u": 0, "o": 0}

    def stage_u(fc, tg):
        w1 = w1c[fc % 2]
        ut = uT[(fc * 4 + tg) % 2]
        for ft in range(4):
            p = pu[cnt["u"] % 2]
            r = rbuf[cnt["u"] % 2]
            cnt["u"] += 1
            for kc in range(8):
                b.op("pe", lambda e, kc=kc, ft=ft, p=p, w1=w1, tg=tg: e.matmul(
                    p[:], lhsT=w1[:, kc, ft * 128:(ft + 1) * 128], rhs=h2T[:, kc, tg * 512:(tg + 1) * 512],
                    start=(kc == 0), stop=(kc == 7)), reads=[w1, h2T], writes=[p])
            b.op("act", lambda e, p=p, r=r: e.activation(out=r[:], in_=p[:], func=AF.Relu), reads=[p], writes=[r])
            b.op("pool", lambda e, r=r, ut=ut, ft=ft: e.tensor_tensor(out=ut[:, ft, :], in0=r[:], in1=r[:], op=ALU.mult),
                 reads=[r], writes=[ut])

    def stage_o(fc, tg):
        w2 = w2c[fc % 2]
        ut = uT[(fc * 4 + tg) % 2]
        for tt in range(4):
            xr = x_res[tg * 4 + tt]
            for ch in range(2):
                p = po[cnt["o"] % 4]
                cnt["o"] += 1
                for ft in range(4):
                    b.op("pe", lambda e, ft=ft, tt=tt, ch=ch, p=p, ut=ut, w2=w2: e.matmul(
                        p[:], lhsT=ut[:, ft, tt * 128:(tt + 1) * 128], rhs=w2[:, ft, ch * 512:(ch + 1) * 512],
                        start=(ft == 0), stop=(ft == 3)), reads=[ut, w2], writes=[p])
                b.op("dve", lambda e, ch=ch, p=p, xr=xr: e.tensor_tensor(
                    out=xr[:, ch * 512:(ch + 1) * 512], in0=xr[:, ch * 512:(ch + 1) * 512], in1=p[:], op=ALU.add),
                    reads=[xr, p], writes=[xr])

    load_chunk(0)
    stage_u(*steps[0])
    for i, (fc, tg) in enumerate(steps):
        if tg == 0 and fc + 1 < NFC:
            load_chunk(fc + 1)
        if hook is not None and fc == 1 and tg == 1:
            hook()
        if i + 1 < len(steps):
            stage_u(*steps[i + 1])
        stage_o(fc, tg)


IDX_SCALE = (8 ** -0.5) * (64 ** -0.5)


def phase_dsa_proj(b, io, x_res, cos, sin):
    idt = b.ident()
    gmix = b.sb([128, D], F32, "gmix1")
    b.op("sp", lambda e: e.dma_start(out=gmix[:], in_=io["gmix1"][:]), writes=[gmix], dma="ld_gmix1")
    gqk = b.sb([128, 2048], F32, "gqk2")
    b.op("sp", lambda e: e.dma_start(out=gqk[:], in_=io["gqk2"][:]), writes=[gqk], dma="ld_gqk2")
    b.op("pool", lambda e: e.tensor_scalar(out=gqk[:, 0:1024], in0=gqk[:, 0:1024], scalar1=0.125, scalar2=None,
                                           op0=ALU.mult), reads=[gqk], writes=[gqk])
    gcq = b.sb([128, 256], F32, "gcq")
    b.op("sp", lambda e: e.dma_start(out=gcq[:], in_=io["gcq"][:]), writes=[gcq], dma="ld_gcq")
    w = b.sb([128, 8, 2376], BF16, "w_in2")
    load_w_bf16(b, w, io["w_in2"], 8, 2376, "ld_w2in", colchunk=792)
    wuq = b.sb([128, 2, 1024], BF16, "wuq")
    load_w_bf16(b, wuq, io["w_uq"], 2, 1024, "ld_wuq")
    wuqi = b.sb([128, 2, 512], BF16, "wuqi")
    load_w_bf16(b, wuqi, io["w_uqi"], 2, 512, "ld_wuqi")

    junk = b.sb([128, D], BF16, "junk3")
    ss = b.sb([128, 1], F32, "ss3")
    hbf = b.sb([128, D], BF16, "hbf3")
    hT = b.sb([128, 8, 128], BF16, "hT3")
    stage = b.sb([128, 2048], F32, "stage3")
    sq = b.sb([128, 2048], F32, "sq3")
    ssq = b.sb([128, 32], F32, "ssq3")
    tmp = b.sb([128, 4 * 32 * 8], F32, "ropetmp3")
    qkbf = b.sb([128, 2048], BF16, "qkbf3")
    qkT = [b.sb([128, 16, 128], BF16, f"qkT3{i}") for i in range(2)]
    vaug = [b.sb([128, 16, 65], BF16, f"vaug3{i}") for i in range(2)]
    for i in range(2):
        b.op("pool", lambda e, i=i: e.memset(vaug[i][:], 1.0), writes=[vaug[i]])
    cqs = b.sb([128, 256], F32, "cqs")
    ssc = b.sb([128, 1], F32, "ssc")
    cqbf = b.sb([128, 256], BF16, "cqbf")
    cqT = b.sb([128, 2, 128], BF16, "cqT")
    kis = b.sb([128, 64], F32, "kis")
    ksq = b.sb([128, 64], F32, "ksq")
    kss = b.sb([128, 1], F32, "kss")
    kibf = b.sb([128, 128], BF16, "kibf")
    kiT = [b.sb([128, 128], BF16, f"kiT{i}") for i in range(2)]
    wi = b.sb([128, 8], F32, "wi")
    sgn = [b.sb([128, 8], F32, f"sgn{i}") for i in range(2)]
    aw = b.sb([128, 8], F32, "aw")
    qis = b.sb([128, 512], F32, "qis")
    qibf = b.sb([128, 512], BF16, "qibf")
    qiT = [b.sb([128, 4, 128], BF16, f"qiT{i}") for i in range(2)]
    nbank = [0]

    def bank():
        nbank[0] += 1
        return b.banks[2 + nbank[0] % 4]

    def proj(py, lhs, nk, rhs_fn, ncol):
        for kc in range(nk):
            b.op("pe", lambda e, kc=kc: e.matmul(py[:, 0:ncol], lhsT=lhs[:, kc, :], rhs=rhs_fn(kc),
                                                 start=(kc == 0), stop=(kc == nk - 1)), reads=[lhs, w, wuq, wuqi], writes=[py])

    for a in range(NS):
        xr = x_res[a]
        rmsnorm_transpose(b, xr, gmix, hbf, hT, b.banks16[0], ss, junk)
        py = bank()
        proj(py, hT, 8, lambda kc: w[:, kc, 0:256], 256)
        b.op("act", lambda e, py=py: e.activation(out=cqs[:], in_=py[:, 0:256], func=AF.Copy), reads=[py], writes=[cqs])
        b.op("act", lambda e, py=py: e.activation(out=junk[:, 0:256], in_=py[:, 0:256], func=AF.Square, accum_out=ssc[:, 0:1]),
             reads=[py], writes=[junk, ssc])
        rstd_from_ss(b, ssc, 1, 1.0 / 256)
        b.op("dve", lambda e: e.scalar_tensor_tensor(out=cqbf[:], in0=cqs[:], scalar=ssc[:, 0:1], in1=gcq[:],
                                                     op0=ALU.mult, op1=ALU.mult), reads=[cqs, ssc, gcq], writes=[cqbf])
        p6 = b.banks16[6]
        for kc in range(2):
            b.op("pe", lambda e, kc=kc: e.transpose(out=p6[:, kc * 128:(kc + 1) * 128], in_=cqbf[:, kc * 128:(kc + 1) * 128],
                                                    identity=idt[:]), reads=[cqbf, idt], writes=[p6])
        b.op("dve", lambda e: e.tensor_copy(out=cqT[:].rearrange("p a b -> p (a b)"), in_=p6[:, 0:256]),
             reads=[p6], writes=[cqT])
        for n in range(2):
            py = bank()
            proj(py, hT, 8, lambda kc, n=n: w[:, kc, 256 + n * 512:256 + (n + 1) * 512], 512)
            b.op("act", lambda e, py=py, n=n: e.activation(out=stage[:, 1024 + n * 512:1024 + (n + 1) * 512], in_=py[:], func=AF.Copy),
                 reads=[py], writes=[stage])
            b.op("act", lambda e, py=py, n=n: e.activation(out=sq[:, 1024 + n * 512:1024 + (n + 1) * 512], in_=py[:], func=AF.Square),
                 reads=[py], writes=[sq])
        va = vaug[a % 2]
        for n in range(2):
            py = bank()
            proj(py, hT, 8, lambda kc, n=n: w[:, kc, 1280 + n * 512:1280 + (n + 1) * 512], 512)
            b.op("dve", lambda e, py=py, n=n, va=va: e.tensor_copy(out=va[:, n * 8:(n + 1) * 8, 0:64],
                                                                   in_=py[:].rearrange("p (h d) -> p h d", d=64)),
                 reads=[py], writes=[va])
        b.store("sp", lambda e, a=a, va=va: e.dma_start(out=io["V2"][a], in_=va[:].rearrange("p h d -> p (h d)")),
                reads=[va], dma=f"st_v2{a % 2}")
        py = bank()
        proj(py, hT, 8, lambda kc: w[:, kc, 2304:2376], 72)
        b.op("act", lambda e, py=py: e.activation(out=kis[:], in_=py[:, 0:64], func=AF.Copy), reads=[py], writes=[kis])
        b.op("act", lambda e, py=py: e.activation(out=ksq[:], in_=py[:, 0:64], func=AF.Square), reads=[py], writes=[ksq])
        b.op("dve", lambda e, py=py: e.tensor_copy(out=wi[:], in_=py[:, 64:72]), reads=[py], writes=[wi])
        for n in range(2):
            py = bank()
            proj(py, cqT, 2, lambda kc, n=n: wuq[:, kc, n * 512:(n + 1) * 512], 512)
            b.op("act", lambda e, py=py, n=n: e.activation(out=stage[:, n * 512:(n + 1) * 512], in_=py[:], func=AF.Copy),
                 reads=[py], writes=[stage])
            b.op("act", lambda e, py=py, n=n: e.activation(out=sq[:, n * 512:(n + 1) * 512], in_=py[:], func=AF.Square),
                 reads=[py], writes=[sq])
        py = bank()
        proj(py, cqT, 2, lambda kc: wuqi[:, kc, :], 512)
        b.op("act", lambda e, py=py: e.activation(out=qis[:], in_=py[:], func=AF.Copy), reads=[py], writes=[qis])
        headnorm_rope(b, stage, sq, ssq, 32, gqk, cos[:, a, :], sin[:, a, :], qkbf, tmp)
        qt = qkT[a % 2]
        for i in range(16):
            pt = b.banks16[1] if i < 8 else b.banks16[0]
            b.op("pe", lambda e, i=i, pt=pt: e.transpose(out=pt[:, (i % 8) * 128:(i % 8 + 1) * 128],
                                                         in_=qkbf[:, i * 128:(i + 1) * 128], identity=idt[:]),
                 reads=[qkbf, idt], writes=[pt])
            if i % 8 == 7:
                b.op("dve", lambda e, i=i, pt=pt, qt=qt: e.tensor_copy(
                    out=qt[:, (i // 8) * 8:(i // 8) * 8 + 8, :].rearrange("p a b -> p (a b)"), in_=pt[:]),
                    reads=[pt], writes=[qt])
        b.store("sp", lambda e, a=a, qt=qt: e.dma_start(out=io["QT2"][a], in_=qt[:, 0:8, :].rearrange("p a b -> p (a b)")),
                reads=[qt], dma=f"st_q2{a % 2}")
        b.store("sp", lambda e, a=a, qt=qt: e.dma_start(out=io["KT2"][a], in_=qt[:, 8:16, :].rearrange("p a b -> p (a b)")),
                reads=[qt], dma=f"st_k2{a % 2}")
        ki_half = T(kibf.t[:, 0:64], kibf.d)
        headnorm_rope(b, kis, ksq, kss, 1, None, cos[:, a, :], sin[:, a, :], ki_half, tmp)
        b.op("pool", lambda e: e.tensor_copy(out=kibf[:, 64:128], in_=kibf[:, 0:64]), reads=[kibf], writes=[kibf])
        p7 = b.banks16[7]
        b.op("pe", lambda e: e.transpose(out=p7[:, 0:128], in_=kibf[:], identity=idt[:]), reads=[kibf, idt], writes=[p7])
        kt_ = kiT[a % 2]
        b.op("dve", lambda e, kt_=kt_: e.tensor_copy(out=kt_[:], in_=p7[:, 0:128]), reads=[p7], writes=[kt_])
        b.store("sp", lambda e, a=a, kt_=kt_: e.dma_start(out=io["KI"][a], in_=kt_[:]), reads=[kt_], dma=f"st_ki{a % 2}")
        sg = sgn[a % 2]
        b.op("act", lambda e, sg=sg: e.activation(out=sg[:], in_=wi[:], func=AF.Sign), reads=[wi], writes=[sg])
        b.op("dve", lambda e, sg=sg: e.scalar_tensor_tensor(out=aw[:], in0=wi[:], scalar=IDX_SCALE, in1=sg[:],
                                                           op0=ALU.mult, op1=ALU.mult), reads=[wi, sg], writes=[aw])
        b.store("sp", lambda e, a=a, sg=sg: e.dma_start(out=io["SG"][a], in_=sg[:]), reads=[sg], dma=f"st_sg{a % 2}")
        headnorm_rope(b, qis, None, aw, 8, None, cos[:, a, :], sin[:, a, :], qibf, tmp, norm=False)
        qi_ = qiT[a % 2]
        for i in range(4):
            b.op("pe", lambda e, i=i: e.transpose(out=p7[:, 256 + i * 128:256 + (i + 1) * 128],
                                                  in_=qibf[:, i * 128:(i + 1) * 128], identity=idt[:]),
                 reads=[qibf, idt], writes=[p7])
        b.op("dve", lambda e, qi_=qi_: e.tensor_copy(out=qi_[:].rearrange("p a b -> p (a b)"), in_=p7[:, 256:768]),
             reads=[p7], writes=[qi_])
        b.store("sp", lambda e, a=a, qi_=qi_: e.dma_start(out=io["QI"][a], in_=qi_[:].rearrange("p a b -> p (a b)")),
                reads=[qi_], dma=f"st_qi{a % 2}")


NIT = 22
TOPK = 256


def phase_dsa_attn(b, io, o_tok):
    idt = b.ident()
    kia = b.sb([128, 8192], BF16, "kiall")
    for part in range(4):
        b.op("sp", lambda e, part=part: e.dma_start(out=kia[:, part * 2048:(part + 1) * 2048],
                                                    in_=io["KIall"][:, part * 2048:(part + 1) * 2048]),
             writes=[kia], dma="ld_kia")
    negm = b.sb([128, 512], F32, "negm")
    b.op("sp", lambda e: e.dma_start(out=negm[:], in_=io["negmask"][:]), writes=[negm], dma="ld_negm")
    cW = b.sb([128, NIT], F32, "cW")
    for i in range(NIT):
        b.op("pool", lambda e, i=i: e.memset(cW[:, i:i + 1], 2.0 ** (-i)), writes=[cW])
    Ib = b.sb([128, 8192], F32, "Ibuf")
    Mq = b.sb([128, 8192], BF16, "Mq")
    MT = b.sb([128, 64, 128], BF16, "MT")
    ktp = [b.sb([128, 8192], BF16, f"ktp{i}") for i in range(2)]
    vp = [b.sb([128, 64, 130], BF16, f"vp{i}") for i in range(2)]
    qTa = [b.sb([128, 8, 128], BF16, f"qTa{i}") for i in range(2)]
    qiTa = [b.sb([128, 4, 128], BF16, f"qiTa{i}") for i in range(2)]
    sgn = [b.sb([128, 8], F32, f"sgna{i}") for i in range(2)]
    tb = [b.sb([128, 512], F32, f"tb{i}") for i in range(2)]
    pT = [b.sb([128, 512], BF16, f"pTd{i}") for i in range(2)]
    m1 = b.sb([128, 1], F32, "bm1")
    lo = b.sb([128, 1], F32, "blo")
    mid = b.sb([128, 1], F32, "bmid")
    cnt = b.sb([128, 1], F32, "bcnt")
    g = b.sb([128, 1], F32, "bg")
    W = b.sb([128, NIT], F32, "bW")
    rec = [b.sb([128, 1], F32, f"recd{i}") for i in range(2)]
    pS = [b.banks[0], b.banks[1]]
    pA = [b.banks[2], b.banks[3]]
    pI = [b.banks[4], b.banks[5]]
    pM = [b.banks16[6], b.banks16[7]]
    ctr = {"i": 0, "s": 0, "acc": 0, "kv": 0}

    def load_slot_small(a):
        b.op("sp", lambda e, a=a: e.dma_start(out=qTa[a % 2][:].rearrange("p a b -> p (a b)"), in_=io["QT2"][a]),
             writes=[qTa[a % 2]], dma=f"ld_qTa{a % 2}")
        b.op("sp", lambda e, a=a: e.dma_start(out=qiTa[a % 2][:].rearrange("p a b -> p (a b)"), in_=io["QI"][a]),
             writes=[qiTa[a % 2]], dma=f"ld_qiTa{a % 2}")
        b.op("sp", lambda e, a=a: e.dma_start(out=sgn[a % 2][:], in_=io["SG"][a]), writes=[sgn[a % 2]], dma=f"ld_sgn{a % 2}")

    def load_kv(a, hp):
        k = ctr["kv"] % 2
        ctr["kv"] += 1
        nv = 512 * (a + 1)
        nt = 4 * (a + 1)
        b.op("sp", lambda e, k=k, hp=hp, nv=nv: e.dma_start(out=ktp[k][:, 0:nv], in_=io["KT2all"][hp][:, 0:nv]),
             writes=[ktp[k]], dma=f"ld_ktp{k}")
        b.op("sp", lambda e, k=k, hp=hp, nt=nt: e.dma_start(out=vp[k][:, 0:nt, :].rearrange("p a b -> p (a b)"),
                                                            in_=io["V2all"][hp][:, 0:nt * 130]),
             writes=[vp[k]], dma=f"ld_vp{k}")
        return k

    def indexer(a):
        nb = a + 1
        nv = 512 * nb
        qi, sg = qiTa[a % 2], sgn[a % 2]
        for blk in range(nb):
            for head in range(8):
                hp, hh = head // 2, head % 2
                py = pI[ctr["i"] % 2]
                t = tb[ctr["i"] % 2]
                ctr["i"] += 1
                b.op("pe", lambda e, py=py, hp=hp, hh=hh, blk=blk, qi=qi: e.matmul(
                    py[:], lhsT=qi[64 * hh:64 * hh + 64, hp, :], rhs=kia[64 * hh:64 * hh + 64, blk * 512:(blk + 1) * 512],
                    start=True, stop=True), reads=[qi, kia], writes=[py])
                b.op("act", lambda e, py=py, t=t: e.activation(out=t[:], in_=py[:], func=AF.Relu), reads=[py], writes=[t])
                if head == 0:
                    b.op("dve", lambda e, t=t, blk=blk, sg=sg: e.tensor_scalar(
                        out=Ib[:, blk * 512:(blk + 1) * 512], in0=t[:], scalar1=sg[:, 0:1], scalar2=None, op0=ALU.mult),
                        reads=[t, sg], writes=[Ib])
                else:
                    b.op("dve", lambda e, t=t, blk=blk, sg=sg, head=head: e.scalar_tensor_tensor(
                        out=Ib[:, blk * 512:(blk + 1) * 512], in0=t[:], scalar=sg[:, head:head + 1],
                        in1=Ib[:, blk * 512:(blk + 1) * 512], op0=ALU.mult, op1=ALU.add),
                        reads=[t, sg, Ib], writes=[Ib])
        b.op("dve", lambda e: e.tensor_reduce(out=m1[:], in_=Ib[:, 0:nv], axis=AX.X, op=ALU.max, apply_absolute_value=True),
             reads=[Ib], writes=[m1])
        b.op("dve", lambda e: e.tensor_tensor(out=Ib[:, nv - 512:nv], in0=Ib[:, nv - 512:nv], in1=negm[:], op=ALU.add),
             reads=[Ib, negm], writes=[Ib])
        b.op("dve", lambda e: e.tensor_scalar(out=m1[:], in0=m1[:], scalar1=1.0, scalar2=None, op0=ALU.add),
             reads=[m1], writes=[m1])
        b.op("dve", lambda e: e.tensor_scalar(out=lo[:], in0=m1[:], scalar1=-1.0, scalar2=None, op0=ALU.mult),
             reads=[m1], writes=[lo])
        b.op("dve", lambda e: e.tensor_scalar(out=W[:], in0=cW[:], scalar1=m1[:, 0:1], scalar2=None, op0=ALU.mult),
             reads=[cW, m1], writes=[W])
        b.op("dve", lambda e: e.tensor_tensor(out=mid[:], in0=lo[:], in1=W[:, 0:1], op=ALU.add), reads=[lo, W], writes=[mid])
        for i in range(NIT):
            b.op("dve", lambda e: e.tensor_scalar(out=Mq[:, 0:nv], in0=Ib[:, 0:nv], scalar1=mid[:, 0:1], scalar2=0.0,
                                                  op0=ALU.is_ge, op1=ALU.add, accum_out=cnt[:, 0:1]),
                 reads=[Ib, mid], writes=[Mq, cnt])
            b.op("dve", lambda e, i=i: e.tensor_scalar(out=g[:], in0=cnt[:], scalar1=TOPK - 0.5, scalar2=W[:, i:i + 1],
                                                       op0=ALU.is_ge, op1=ALU.mult), reads=[cnt, W], writes=[g])
            b.op("dve", lambda e: e.tensor_tensor(out=lo[:], in0=lo[:], in1=g[:], op=ALU.add), reads=[lo, g], writes=[lo])
            if i + 1 < NIT:
                b.op("dve", lambda e, i=i: e.tensor_tensor(out=mid[:], in0=lo[:], in1=W[:, i + 1:i + 2], op=ALU.add),
                     reads=[lo, W], writes=[mid])
        b.op("dve", lambda e: e.tensor_scalar(out=Mq[:, 0:nv], in0=Ib[:, 0:nv], scalar1=lo[:, 0:1], scalar2=None,
                                              op0=ALU.is_ge), reads=[Ib, lo], writes=[Mq])
        nt = 4 * nb
        for kt in range(nt):
            pm = pM[(kt // 8) % 2]
            b.op("pe", lambda e, kt=kt, pm=pm: e.transpose(out=pm[:, (kt % 8) * 128:(kt % 8 + 1) * 128],
                                                           in_=Mq[:, kt * 128:(kt + 1) * 128], identity=idt[:]),
                 reads=[Mq, idt], writes=[pm])
            if kt % 8 == 7 or kt == nt - 1:
                k0 = (kt // 8) * 8
                n = kt - k0 + 1
                b.op("act", lambda e, pm=pm, k0=k0, n=n: e.activation(
                    out=MT[:, k0:k0 + n, :].rearrange("p a b -> p (a b)"), in_=pm[:, 0:n * 128], func=AF.Copy),
                    reads=[pm], writes=[MT])

    def qk(u, n, kbuf):
        a, hp, hh, blk = u
        ps = pS[n % 2]
        qa = qTa[a % 2]
        for i in range(4):
            kt = 4 * blk + i
            b.op("pe", lambda e, i=i, kt=kt, ps=ps, qa=qa, hh=hh, hp=hp, kbuf=kbuf: e.matmul(
                ps[:, i * 128:(i + 1) * 128], lhsT=ktp[kbuf][64 * hh:64 * hh + 64, kt * 128:(kt + 1) * 128],
                rhs=qa[64 * hh:64 * hh + 64, hp, :], start=True, stop=True), reads=[ktp[kbuf], qa], writes=[ps])

    def softmax_pv(u, n, kbuf):
        a, hp, hh, blk = u
        ps, pt = pS[n % 2], pT[n % 2]
        if blk == 0:
            ctr["acc"] += 1
        acc = pA[ctr["acc"] % 2]
        b.op("act", lambda e: e.activation(out=pt[:], in_=ps[:], func=AF.Exp), reads=[ps], writes=[pt])
        b.op("dve", lambda e: e.tensor_tensor(out=pt[:], in0=pt[:], in1=MT[:, 4 * blk:4 * blk + 4, :].rearrange("p a b -> p (a b)"),
                                              op=ALU.mult), reads=[pt, MT], writes=[pt])
        for i in range(4):
            kt = 4 * blk + i
            b.op("pe", lambda e, i=i, kt=kt: e.matmul(acc[:, 0:65], lhsT=pt[:, i * 128:(i + 1) * 128],
                                                      rhs=vp[kbuf][:, kt, hh * 65:(hh + 1) * 65],
                                                      start=(blk == 0 and i == 0), stop=(blk == a and i == 3)),
                 reads=[pt, vp[kbuf]], writes=[acc])
        if blk == a:
            head = 2 * hp + hh
            r = rec[ctr["acc"] % 2]
            b.op("dve", lambda e: e.reciprocal(out=r[:], in_=acc[:, 64:65]), reads=[acc], writes=[r])
            b.op("dve", lambda e: e.tensor_scalar(out=o_tok[a][:, head * 64:(head + 1) * 64], in0=acc[:, 0:64],
                                                  scalar1=r[:, 0:1], scalar2=None, op0=ALU.mult),
                 reads=[acc, r], writes=[o_tok[a]])

    load_slot_small(0)
    for a in range(NS):
        if a + 1 < NS:
            load_slot_small(a + 1)
        kb_next = load_kv(a, 0)
        indexer(a)
        units = [(a, hp, hh, blk) for hp in range(8) for hh in range(2) for blk in range(a + 1)]
        kbufs = {}
        kbufs[0] = kb_next
        qk(units[0], 0, kbufs[0])
        for n, u in enumerate(units):
            _, hp, hh, blk = u
            if hh == 0 and blk == 0 and hp + 1 < 8:
                kbufs[hp + 1] = load_kv(a, hp + 1)
            if n + 1 < len(units):
                qk(units[n + 1], n + 1, kbufs[units[n + 1][1]])
            softmax_pv(u, n, kbufs[hp])


GROUPS = [[0, 1, 2, 3], [4, 5, 6, 7]]
_RANK = {}


class Gather:
    def __init__(self, b, name, nblk, cols, zt):
        self.b, self.name, self.nblk, self.cols = b, name, nblk, cols
        nc = b.nc
        self.xb = nc.dram_tensor(name + "_xb", [nblk, 512, cols], BF16).ap()
        self.yb = nc.dram_tensor(name + "_yb", [nblk, 512, cols], BF16).ap()
        self.xo = nc.dram_tensor(name + "_xo", [nblk, 128, cols], BF16).ap()
        self.od = [T(self.xo[h]) for h in range(nblk)]
        self.xd = [T(self.xb[h]) for h in range(nblk)]
        self.yd = [T(self.yb[h]) for h in range(nblk)]
        for h in range(nblk):
            for r in range(4):
                b.op("act", lambda e, h=h, r=r: e.dma_start(out=self.xb[h, r * 128:(r + 1) * 128, :], in_=zt[:, 0:cols]),
                     reads=[zt], writes=[self.xd[h]], dma="zero_" + name)

    def put(self, h, c0, c1, src_ap, reads, key):
        self.b.op("sp", lambda e: e.dma_start(out=self.xo[h, :, c0:c1], in_=src_ap), reads=reads,
                  writes=[self.od[h]], dma=key)

    def place(self, h, eng):
        def fn(e):
            if eng not in _RANK:
                _RANK[eng] = e.partition_id() % 4
            r = _RANK[eng]
            return e.dma_start(out=self.xb[h, bass.ds(r * 128, 128), :], in_=self.xo[h])
        self.b.op(eng, fn, reads=[self.od[h]], writes=[self.xd[h]], dma="place_" + self.name)

    def reduce(self, h):
        b = self.b
        b.op("pool", lambda e: e.collective_compute("AllReduce", ALU.add, replica_groups=GROUPS,
                                                    ins=[self.xb[h]], outs=[self.yb[h]]),
             reads=[self.xd[h]], writes=[self.yd[h]], dma=f"cc_{self.name}{h}", sem_inc=1)


def f_diff_proj(b, io, qT_res, G0, cos, sin, wstage=None):
    idt = b.ident()
    gmix = b.sb([128, D], F32, "gmix")
    b.op("sp", lambda e: e.dma_start(out=gmix[:], in_=io["gmix"][:]), writes=[gmix], dma="ld_gmix")
    gqk = b.sb([128, 2048], F32, "gqk")
    b.op("sp", lambda e: e.dma_start(out=gqk[:], in_=io["gqk"][:]), writes=[gqk], dma="ld_gqk")
    b.op("pool", lambda e: e.tensor_scalar(out=gqk[:, 0:1024], in0=gqk[:, 0:1024], scalar1=0.125, scalar2=None,
                                           op0=ALU.mult), reads=[gqk], writes=[gqk])
    w = b.sb([128, 8, 3072], BF16, "w_in")
    if wstage is None:
        load_w_bf16(b, w, io["w_in"], 8, 3072, "ld_w")
    else:
        for n in range(6):
            stg = wstage[n % 2]
            b.op("sp", lambda e, n=n, stg=stg: e.dma_start(
                out=stg[:], in_=io["w_in"][:, n * 512:(n + 1) * 512].rearrange("(k p) c -> p k c", p=128)),
                writes=[stg], dma=f"ld_wst{n % 2}")
            b.op("act" if n % 2 == 0 else "dve",
                 (lambda e, n=n, stg=stg: e.activation(out=w[:, :, n * 512:(n + 1) * 512], in_=stg[:], func=AF.Copy)) if n % 2 == 0
                 else (lambda e, n=n, stg=stg: e.tensor_copy(out=w[:, :, n * 512:(n + 1) * 512], in_=stg[:])),
                 reads=[stg], writes=[w])
    xts = [b.sb([128, D], F32, f"xt{i}") for i in range(2)]
    junk_ = [b.sb([128, D], BF16, f"junk{i}") for i in range(2)]
    ss_ = [b.sb([128, 1], F32, f"ss{i}") for i in range(2)]
    hbf_ = [b.sb([128, D], BF16, f"hbf{i}") for i in range(2)]
    hT_ = [b.sb([128, 8, 128], BF16, f"hT{i}") for i in range(2)]
    pT = [b.banks16[0], b.banks16[1]]
    pY = [b.banks[2 + i] for i in range(4)]
    stage_ = [b.sb([128, 2048], F32, f"stage{i}") for i in range(2)]
    sq_ = [b.sb([128, 2048], BF16, f"sq{i}") for i in range(2)]
    ssq_ = [b.sb([128, 32], F32, f"ssq{i}") for i in range(2)]
    tmp1 = b.sb([128, 4 * 32 * 8], F32, "ropetmp"); tmp_ = [tmp1, tmp1]
    qkbf_ = [b.sb([128, 2048], BF16, f"qkbf{i}") for i in range(2)]
    kT = [b.sb([128, 8, 128], BF16, f"kTst{i}") for i in range(2)]
    vst = [b.sb([128, 8, 128], BF16, f"vst{i}") for i in range(2)]
    def load_x(a):
        xt = xts[a % 2]
        b.op("sp", lambda e: e.dma_start(out=xt[:], in_=io["x"][a * 128:(a + 1) * 128, :]), writes=[xt], dma=f"ld_x{a % 2}")
    load_x(0)

    def body(a, junk, ss, hbf, hT, stage, sq, ssq, tmp, qkbf):
        xt = xts[a % 2]
        if a + 1 < NS:
            load_x(a + 1)
        rmsnorm_transpose(b, xt, gmix, hbf, hT, pT[0], ss, junk)
        va = vst[a % 2]
        for n in range(6):
            py = pY[n % 4]
            for kc in range(8):
                b.op("pe", lambda e, n=n, kc=kc, py=py: e.matmul(py[:], lhsT=hT[:, kc, :],
                                                                   rhs=w[:, kc, n * 512:(n + 1) * 512],
                                                                   start=(kc == 0), stop=(kc == 7)),
                     reads=[hT, w], writes=[py])
            if n < 4:
                b.op("act", lambda e, n=n, py=py: e.activation(out=stage[:, n * 512:(n + 1) * 512], in_=py[:], func=AF.Copy),
                     reads=[py], writes=[stage])
                b.op("act", lambda e, n=n, py=py: e.activation(out=sq[:, n * 512:(n + 1) * 512], in_=py[:], func=AF.Square),
                     reads=[py], writes=[sq])
            else:
                b.op("dve", lambda e, n=n, py=py, va=va: e.tensor_copy(
                    out=va[:, (n - 4) * 4:(n - 4) * 4 + 4, :], in_=py[:].rearrange("p (h d) -> p h d", d=128)),
                    reads=[py], writes=[va])
        for h in range(8):
            G0.put(h, 2048 + a * 128, 2048 + (a + 1) * 128, va[:, h, :], [va], f"st_v{a % 2}")

    def tail(a, junk, ss, hbf, hT, stage, sq, ssq, tmp, qkbf):
        headnorm_rope(b, stage, sq, ssq, 32, gqk, cos[:, a, :], sin[:, a, :], qkbf, tmp)
        kt = kT[a % 2]
        for i in range(16):
            pt = b.banks16[6] if i < 8 else b.banks16[7]
            b.op("pe", lambda e, i=i, pt=pt: e.transpose(out=pt[:, (i % 8) * 128:(i % 8 + 1) * 128],
                                                         in_=qkbf[:, i * 128:(i + 1) * 128], identity=idt[:]),
                 reads=[qkbf, idt], writes=[pt])
            if i == 7:
                b.op("dve", lambda e, pt=pt, a=a: e.tensor_copy(out=qT_res[:, a, :], in_=pt[:]), reads=[pt], writes=[qT_res])
            if i == 15:
                b.op("dve", lambda e, pt=pt, kt=kt: e.tensor_copy(out=kt[:].rearrange("p a b -> p (a b)"), in_=pt[:]),
                     reads=[pt], writes=[kt])
        for h in range(8):
            G0.put(h, a * 128, (a + 1) * 128, kt[:, h, :], [kt], f"st_k{a % 2}")
    def args(a):
        return [z[a % 2] for z in (junk_, ss_, hbf_, hT_, stage_, sq_, ssq_, tmp_, qkbf_)]
    import os
    if os.environ.get("PIPE1", "0") == "1":
        body(0, *args(0))
        for a in range(NS):
            if a + 1 < NS:
                body(a + 1, *args(a + 1))
            tail(a, *args(a))
    else:
        for a in range(NS):
            body(a, *args(a))
            tail(a, *args(a))
    for h in range(8):
        G0.place(h, "sp")
        G0.reduce(h)


def f_diff_attn(b, io, o_tok, qT, G0):
    LAM_INIT = 0.2
    maskT = b.sb([128, 512], BF16, "maskT")
    b.op("sp", lambda e: e.dma_start(out=maskT[:], in_=io["maskT"][:]), writes=[maskT], dma="ld_mask")
    lam = b.sb([128, 256], F32, "lam")
    b.op("sp", lambda e: e.dma_start(out=lam[:], in_=io["lam"][:]), writes=[lam], dma="ld_lam")
    gsub = b.sb([128, 128], F32, "gsub")
    b.op("sp", lambda e: e.dma_start(out=gsub[:], in_=io["gsub"][:]), writes=[gsub], dma="ld_gsub")
    b.op("dve", lambda e: e.tensor_scalar(out=gsub[:], in0=gsub[:], scalar1=1.0 - LAM_INIT, scalar2=None,
                                          op0=ALU.mult), reads=[gsub], writes=[gsub])
    lprod = b.sb([128, 128], F32, "lprod")
    l2 = b.sb([128, 2], F32, "l2")
    neglam = b.sb([128, 1], F32, "neglam")
    l4 = lam[:].rearrange("p (a b d) -> p a b d", a=2, b=2)
    b.op("dve", lambda e: e.tensor_tensor(out=lprod[:].rearrange("p (a d) -> p a d", a=2), in0=l4[:, :, 0, :],
                                          in1=l4[:, :, 1, :], op=ALU.mult), reads=[lam], writes=[lprod])
    b.op("dve", lambda e: e.tensor_reduce(out=l2[:], in_=lprod[:].rearrange("p (a d) -> p a d", a=2), axis=AX.X,
                                          op=ALU.add), reads=[lprod], writes=[l2])
    b.op("act", lambda e: e.activation(out=l2[:], in_=l2[:], func=AF.Exp), reads=[l2], writes=[l2])
    b.op("dve", lambda e: e.tensor_tensor(out=neglam[:], in0=l2[:, 1:2], in1=l2[:, 0:1], op=ALU.subtract),
         reads=[l2], writes=[neglam])
    b.op("dve", lambda e: e.tensor_scalar(out=neglam[:], in0=neglam[:], scalar1=-LAM_INIT, scalar2=None, op0=ALU.add),
         reads=[neglam], writes=[neglam])

    ktb = [b.sb([128, 8192], BF16, f"ktb{i}") for i in range(2)]
    vb = [b.sb([128, 64, 129], BF16, f"vb{i}") for i in range(2)]
    for i in range(2):
        b.op("dve", lambda e, i=i: e.memset(vb[i][:, :, 128:129], 1.0), writes=[vb[i]])
    pS = [[b.banks[c * 2 + i] for i in range(2)] for c in range(2)]
    pA = [[b.banks[4 + c * 2 + i] for i in range(2)] for c in range(2)]
    pT = [[b.sb([128, 512], BF16, f"pTs{c}{i}") for i in range(2)] for c in range(2)]
    rec = [b.sb([128, 2], F32, f"rec{i}") for i in range(2)]
    o32 = [b.sb([128, 128], F32, f"o32{i}") for i in range(2)]
    oj = b.sb([128, 128], BF16, "ojunk")
    ss1 = [b.sb([128, 1], F32, f"ss1{i}") for i in range(2)]
    units = [(h, a, blk) for h in range(8) for a in range(NS) for blk in range(a + 1)]

    def load_head(h):
        kb, vv = ktb[h % 2], vb[h % 2]
        yb = G0.yb[h]
        for r in range(4):
            b.op("sp", lambda e, kb=kb, r=r, yb=yb: e.dma_start(out=kb[:, r * 2048:(r + 1) * 2048],
                                                               in_=yb[r * 128:(r + 1) * 128, 0:2048]),
                 reads=[G0.yd[h]], writes=[kb], dma=f"ld_kt{h % 2}")
            b.op("sp", lambda e, vv=vv, r=r, yb=yb: e.dma_start(
                out=vv[:, r * 16:(r + 1) * 16, 0:128],
                in_=yb[r * 128:(r + 1) * 128, 2048:4096].rearrange("p (a e) -> p a e", e=128)),
                reads=[G0.yd[h]], writes=[vv], dma=f"ld_v{h % 2}")

    def qk(u, n):
        h, a, blk = u
        kb = ktb[h % 2]
        for i in range(4):
            for c in range(2):
                ps = pS[c][n % 2]
                kt = 16 * i + blk
                b.op("pe", lambda e, c=c, i=i, kt=kt, ps=ps, kb=kb, a=a, h=h: e.matmul(
                    ps[:, i * 128:(i + 1) * 128], lhsT=kb[64 * c:64 * c + 64, kt * 128:(kt + 1) * 128],
                    rhs=qT[64 * c:64 * c + 64, a, h * 128:(h + 1) * 128], start=True, stop=True),
                    reads=[kb, qT], writes=[ps])

    def evac(h, a):
        k = a % 2
        a0, a1 = pA[0][k], pA[1][k]
        r, o, s = rec[k], o32[k], ss1[k]
        b.op("dve", lambda e: e.reciprocal(out=r[:, 0:1], in_=a0[:, 128:129]), reads=[a0], writes=[r])
        b.op("dve", lambda e: e.reciprocal(out=r[:, 1:2], in_=a1[:, 128:129]), reads=[a1], writes=[r])
        b.op("dve", lambda e: e.tensor_tensor(out=r[:, 1:2], in0=r[:, 1:2], in1=neglam[:], op=ALU.mult),
             reads=[r, neglam], writes=[r])
        b.op("dve", lambda e: e.tensor_scalar(out=o[:], in0=a0[:, 0:128], scalar1=r[:, 0:1], scalar2=None, op0=ALU.mult),
             reads=[a0, r], writes=[o])
        b.op("dve", lambda e: e.scalar_tensor_tensor(out=o[:], in0=a1[:, 0:128], scalar=r[:, 1:2], in1=o[:],
                                                     op0=ALU.mult, op1=ALU.add), reads=[a1, r, o], writes=[o])
        b.op("act", lambda e: e.activation(out=oj[:], in_=o[:], func=AF.Square, accum_out=s[:, 0:1]),
             reads=[o], writes=[oj, s])
        rstd_from_ss(b, s, 1, 1.0 / 128)
        ot = o_tok[a]
        b.op("dve", lambda e: e.scalar_tensor_tensor(out=ot[:, h * 128:(h + 1) * 128], in0=o[:], scalar=s[:, 0:1],
                                                     in1=gsub[:], op0=ALU.mult, op1=ALU.mult),
             reads=[o, s, gsub], writes=[ot])

    def softmax_pv(u, n):
        h, a, blk = u
        vv = vb[h % 2]
        for c in range(2):
            ps = pS[c][n % 2]
            pt = pT[c][n % 2]
            acc = pA[c][a % 2]
            b.op("act", lambda e, ps=ps, pt=pt: e.activation(out=pt[:], in_=ps[:], func=AF.Exp),
                 reads=[ps], writes=[pt])
            if blk == a:
                b.op("dve", lambda e, pt=pt: e.tensor_tensor(out=pt[:], in0=pt[:], in1=maskT[:], op=ALU.mult),
                     reads=[pt, maskT], writes=[pt])
            for i in range(4):
                kt = 16 * i + blk
                b.op("pe", lambda e, i=i, kt=kt, pt=pt, acc=acc, vv=vv, blk=blk, a=a: e.matmul(
                    acc[:, 0:129], lhsT=pt[:, i * 128:(i + 1) * 128], rhs=vv[:, kt, :],
                    start=(blk == 0 and i == 0), stop=(blk == a and i == 3)),
                    reads=[pt, vv], writes=[acc])
        if blk == a:
            evac(h, a)

    load_head(0)
    qk(units[0], 0)
    for n, u in enumerate(units):
        if u[1] == 0 and u[2] == 0 and u[0] + 1 < 8:
            load_head(u[0] + 1)
        if n + 1 < len(units):
            qk(units[n + 1], n + 1)
        softmax_pv(u, n)


def dsa_proj_weights(b, io, top=False):
    w = b.sb([128, 8, 2376], BF16, "w_in2", top=top)
    wuq = b.sb([128, 2, 1024], BF16, "wuq", top=top)
    wuqi = b.sb([128, 2, 512], BF16, "wuqi", top=top)

    def load():
        load_w_bf16(b, w, io["w_in2"], 8, 2376, "ld_w2in", colchunk=792)
        load_w_bf16(b, wuq, io["w_uq"], 2, 1024, "ld_wuq")
        load_w_bf16(b, wuqi, io["w_uqi"], 2, 512, "ld_wuqi")
    return {"w": w, "wuq": wuq, "wuqi": wuqi, "load": load}


def f_dsa_proj(b, io, x_res, cos, sin, G1, GK, scr, pre=None):
    idt = b.ident()
    gmix = b.sb([128, D], F32, "gmix1")
    b.op("sp", lambda e: e.dma_start(out=gmix[:], in_=io["gmix1"][:]), writes=[gmix], dma="ld_gmix1")
    gqk = b.sb([128, 2048], F32, "gqk2")
    b.op("sp", lambda e: e.dma_start(out=gqk[:], in_=io["gqk2"][:]), writes=[gqk], dma="ld_gqk2")
    b.op("pool", lambda e: e.tensor_scalar(out=gqk[:, 0:1024], in0=gqk[:, 0:1024], scalar1=0.125, scalar2=None,
                                           op0=ALU.mult), reads=[gqk], writes=[gqk])
    gcq = b.sb([128, 256], F32, "gcq")
    b.op("sp", lambda e: e.dma_start(out=gcq[:], in_=io["gcq"][:]), writes=[gcq], dma="ld_gcq")
    if pre is None:
        pre = dsa_proj_weights(b, io)
        pre["load"]()
    w, wuq, wuqi = pre["w"], pre["wuq"], pre["wuqi"]
    junk_ = [b.sb([128, D], BF16, f"junk3{i}") for i in range(2)]
    ss_ = [b.sb([128, 1], F32, f"ss3{i}") for i in range(2)]
    hbf_ = [b.sb([128, D], BF16, f"hbf3{i}") for i in range(2)]
    hT_ = [b.sb([128, 8, 128], BF16, f"hT3{i}") for i in range(2)]
    stage_ = [b.sb([128, 2048], F32, f"stage3{i}") for i in range(2)]
    sq_ = [b.sb([128, 2048], BF16, f"sq3{i}") for i in range(2)]
    ssq_ = [b.sb([128, 32], F32, f"ssq3{i}") for i in range(2)]
    tmp1 = b.sb([128, 4 * 32 * 8], F32, "ropetmp3"); tmp_ = [tmp1, tmp1]
    qkbf_ = [b.sb([128, 2048], BF16, f"qkbf3{i}") for i in range(2)]
    qkT = [b.sb([128, 16, 128], BF16, f"qkT3{i}") for i in range(2)]
    vst = [b.sb([128, 16, 64], BF16, f"vst3{i}") for i in range(2)]
    cqs_ = [b.sb([128, 256], F32, f"cqs{i}") for i in range(2)]
    ssc_ = [b.sb([128, 1], F32, f"ssc{i}") for i in range(2)]
    cqbf_ = [b.sb([128, 256], BF16, f"cqbf{i}") for i in range(2)]
    cqT_ = [b.sb([128, 2, 128], BF16, f"cqT{i}") for i in range(2)]
    kis_ = [b.sb([128, 64], F32, f"kis{i}") for i in range(2)]
    ksq_ = [b.sb([128, 64], F32, f"ksq{i}") for i in range(2)]
    kss_ = [b.sb([128, 1], F32, f"kss{i}") for i in range(2)]
    kibf_ = [b.sb([128, 128], BF16, f"kibf{i}") for i in range(2)]
    kiT = [b.sb([128, 128], BF16, f"kiT{i}") for i in range(2)]
    wi_ = [b.sb([128, 8], F32, f"wi{i}") for i in range(2)]
    sgn = [b.sb([128, 8], F32, f"sgn{i}") for i in range(2)]
    aw_ = [b.sb([128, 8], F32, f"aw{i}") for i in range(2)]
    qis_ = [b.sb([128, 512], F32, f"qis{i}") for i in range(2)]
    qibf_ = [b.sb([128, 512], BF16, f"qibf{i}") for i in range(2)]
    qiT = [b.sb([128, 4, 128], BF16, f"qiT{i}") for i in range(2)]
    nbank = [0]

    def bank():
        nbank[0] += 1
        return b.banks[2 + nbank[0] % 4]

    def proj(py, lhs, nk, rhs_fn, ncol):
        for kc in range(nk):
            b.op("pe", lambda e, kc=kc: e.matmul(py[:, 0:ncol], lhsT=lhs[:, kc, :], rhs=rhs_fn(kc),
                                                 start=(kc == 0), stop=(kc == nk - 1)), reads=[lhs, w, wuq, wuqi], writes=[py])

    def body2(a, junk, ss, hbf, hT, stage, sq, ssq, tmp, qkbf, cqs, ssc, cqbf, cqT, kis, ksq, kss, kibf, wi, aw, qis, qibf):
        xr = x_res[a]
        rmsnorm_transpose(b, xr, gmix, hbf, hT, b.banks16[0], ss, junk)
        py = bank()
        proj(py, hT, 8, lambda kc: w[:, kc, 0:256], 256)
        b.op("act", lambda e, py=py: e.activation(out=cqs[:], in_=py[:, 0:256], func=AF.Copy), reads=[py], writes=[cqs])
        b.op("act", lambda e, py=py: e.activation(out=junk[:, 0:256], in_=py[:, 0:256], func=AF.Square, accum_out=ssc[:, 0:1]),
             reads=[py], writes=[junk, ssc])
        rstd_from_ss(b, ssc, 1, 1.0 / 256)
        b.op("dve", lambda e: e.scalar_tensor_tensor(out=cqbf[:], in0=cqs[:], scalar=ssc[:, 0:1], in1=gcq[:],
                                                     op0=ALU.mult, op1=ALU.mult), reads=[cqs, ssc, gcq], writes=[cqbf])
        p6 = b.banks16[6]
        for kc in range(2):
            b.op("pe", lambda e, kc=kc: e.transpose(out=p6[:, kc * 128:(kc + 1) * 128], in_=cqbf[:, kc * 128:(kc + 1) * 128],
                                                    identity=idt[:]), reads=[cqbf, idt], writes=[p6])
        b.op("dve", lambda e: e.tensor_copy(out=cqT[:].rearrange("p a b -> p (a b)"), in_=p6[:, 0:256]),
             reads=[p6], writes=[cqT])
        for n in range(2):
            py = bank()
            proj(py, hT, 8, lambda kc, n=n: w[:, kc, 256 + n * 512:256 + (n + 1) * 512], 512)
            b.op("act", lambda e, py=py, n=n: e.activation(out=stage[:, 1024 + n * 512:1024 + (n + 1) * 512], in_=py[:], func=AF.Copy),
                 reads=[py], writes=[stage])
            b.op("act", lambda e, py=py, n=n: e.activation(out=sq[:, 1024 + n * 512:1024 + (n + 1) * 512], in_=py[:], func=AF.Square),
                 reads=[py], writes=[sq])
        va = vst[a % 2]
        for n in range(2):
            py = bank()
            proj(py, hT, 8, lambda kc, n=n: w[:, kc, 1280 + n * 512:1280 + (n + 1) * 512], 512)
            b.op("dve", lambda e, py=py, n=n, va=va: e.tensor_copy(out=va[:, n * 8:(n + 1) * 8, :],
                                                                   in_=py[:].rearrange("p (h d) -> p h d", d=64)),
                 reads=[py], writes=[va])
        for hp in range(8):
            G1.put(hp, 2048 + a * 128, 2048 + (a + 1) * 128, va[:, 2 * hp:2 * hp + 2, :].rearrange("p h d -> p (h d)"),
                   [va], f"st_v2{a % 2}")
        py = bank()
        proj(py, hT, 8, lambda kc: w[:, kc, 2304:2376], 72)
        b.op("act", lambda e, py=py: e.activation(out=kis[:], in_=py[:, 0:64], func=AF.Copy), reads=[py], writes=[kis])
        b.op("act", lambda e, py=py: e.activation(out=ksq[:], in_=py[:, 0:64], func=AF.Square), reads=[py], writes=[ksq])
        b.op("dve", lambda e, py=py: e.tensor_copy(out=wi[:], in_=py[:, 64:72]), reads=[py], writes=[wi])
        for n in range(2):
            py = bank()
            proj(py, cqT, 2, lambda kc, n=n: wuq[:, kc, n * 512:(n + 1) * 512], 512)
            b.op("act", lambda e, py=py, n=n: e.activation(out=stage[:, n * 512:(n + 1) * 512], in_=py[:], func=AF.Copy),
                 reads=[py], writes=[stage])
            b.op("act", lambda e, py=py, n=n: e.activation(out=sq[:, n * 512:(n + 1) * 512], in_=py[:], func=AF.Square),
                 reads=[py], writes=[sq])
        py = bank()
        proj(py, cqT, 2, lambda kc: wuqi[:, kc, :], 512)
        b.op("act", lambda e, py=py: e.activation(out=qis[:], in_=py[:], func=AF.Copy), reads=[py], writes=[qis])

    def tail2(a, junk, ss, hbf, hT, stage, sq, ssq, tmp, qkbf, cqs, ssc, cqbf, cqT, kis, ksq, kss, kibf, wi, aw, qis, qibf):
        headnorm_rope(b, stage, sq, ssq, 32, gqk, cos[:, a, :], sin[:, a, :], qkbf, tmp)
        qt = qkT[a % 2]
        for i in range(16):
            pt = b.banks16[1] if i < 8 else b.banks16[7]
            b.op("pe", lambda e, i=i, pt=pt: e.transpose(out=pt[:, (i % 8) * 128:(i % 8 + 1) * 128],
                                                         in_=qkbf[:, i * 128:(i + 1) * 128], identity=idt[:]),
                 reads=[qkbf, idt], writes=[pt])
            if i % 8 == 7:
                b.op("dve", lambda e, i=i, pt=pt, qt=qt: e.tensor_copy(
                    out=qt[:, (i // 8) * 8:(i // 8) * 8 + 8, :].rearrange("p a b -> p (a b)"), in_=pt[:]),
                    reads=[pt], writes=[qt])
        b.op("sp", lambda e, a=a, qt=qt: e.dma_start(out=scr["QT2"][a][:], in_=qt[:, 0:8, :].rearrange("p a b -> p (a b)")),
             reads=[qt], writes=[scr["QT2"][a]], dma=f"st_q2{a % 2}")
        for hp in range(8):
            G1.put(hp, a * 128, (a + 1) * 128, qt[:, 8 + hp, :], [qt], f"st_k2{a % 2}")
        ki_half = T(kibf.t[:, 0:64], kibf.d)
        headnorm_rope(b, kis, ksq, kss, 1, None, cos[:, a, :], sin[:, a, :], ki_half, tmp)
        b.op("pool", lambda e: e.tensor_copy(out=kibf[:, 64:128], in_=kibf[:, 0:64]), reads=[kibf], writes=[kibf])
        p7 = b.banks16[7]
        b.op("pe", lambda e: e.transpose(out=p7[:, 0:128], in_=kibf[:], identity=idt[:]), reads=[kibf, idt], writes=[p7])
        kt_ = kiT[a % 2]
        b.op("dve", lambda e, kt_=kt_: e.tensor_copy(out=kt_[:], in_=p7[:, 0:128]), reads=[p7], writes=[kt_])
        GK.put(0, a * 128, (a + 1) * 128, kt_[:], [kt_], f"st_ki{a % 2}")
        sg = sgn[a % 2]
        b.op("act", lambda e, sg=sg: e.activation(out=sg[:], in_=wi[:], func=AF.Sign), reads=[wi], writes=[sg])
        b.op("dve", lambda e, sg=sg: e.scalar_tensor_tensor(out=aw[:], in0=wi[:], scalar=IDX_SCALE, in1=sg[:],
                                                           op0=ALU.mult, op1=ALU.mult), reads=[wi, sg], writes=[aw])
        b.op("sp", lambda e, a=a, sg=sg: e.dma_start(out=scr["SG"][a][:], in_=sg[:]), reads=[sg], writes=[scr["SG"][a]],
             dma=f"st_sg{a % 2}")
        headnorm_rope(b, qis, None, aw, 8, None, cos[:, a, :], sin[:, a, :], qibf, tmp, norm=False)
        qi_ = qiT[a % 2]
        for i in range(4):
            b.op("pe", lambda e, i=i: e.transpose(out=p7[:, 256 + i * 128:256 + (i + 1) * 128],
                                                  in_=qibf[:, i * 128:(i + 1) * 128], identity=idt[:]),
                 reads=[qibf, idt], writes=[p7])
        b.op("dve", lambda e, qi_=qi_: e.tensor_copy(out=qi_[:].rearrange("p a b -> p (a b)"), in_=p7[:, 256:768]),
             reads=[p7], writes=[qi_])
        b.op("sp", lambda e, a=a, qi_=qi_: e.dma_start(out=scr["QI"][a][:], in_=qi_[:].rearrange("p a b -> p (a b)")),
             reads=[qi_], writes=[scr["QI"][a]], dma=f"st_qi{a % 2}")
    def args2(a):
        return [z[a % 2] for z in (junk_, ss_, hbf_, hT_, stage_, sq_, ssq_, tmp_, qkbf_, cqs_, ssc_, cqbf_, cqT_, kis_, ksq_, kss_, kibf_, wi_, aw_, qis_, qibf_)]
    import os
    if os.environ.get("PIPE2", "0") == "1":
        body2(0, *args2(0))
        for a in range(NS):
            if a + 1 < NS:
                body2(a + 1, *args2(a + 1))
            tail2(a, *args2(a))
    else:
        for a in range(NS):
            body2(a, *args2(a))
            tail2(a, *args2(a))
    GK.place(0, "act")
    GK.reduce(0)
    for hp in range(8):
        G1.place(hp, "act")
        G1.reduce(hp)


NIT2 = 14


def f_dsa_attn(b, io, o_tok, G1, GK, scr):
    idt = b.ident()
    kia = b.sb([128, 8192], BF16, "kiall")
    for r in range(4):
        b.op("sp", lambda e, r=r: e.dma_start(out=kia[:, r * 2048:(r + 1) * 2048], in_=GK.yb[0][r * 128:(r + 1) * 128, :]),
             reads=[GK.yd[0]], writes=[kia], dma="ld_kia")
    kia4 = kia[:].rearrange("p (r a t) -> p r a t", r=4, a=16)
    negm = b.sb([128, 512], F32, "negm")
    b.op("sp", lambda e: e.dma_start(out=negm[:], in_=io["negmask"][:]), writes=[negm], dma="ld_negm")
    cW = b.sb([128, NIT2], F32, "cW")
    for i in range(NIT2):
        b.op("dve", lambda e, i=i: e.memset(cW[:, i:i + 1], 2.0 ** (-i)), writes=[cW])
    THR = b.sb([128, NS], F32, "THR")
    qiTa = [b.sb([128, 4, 128], BF16, f"qiTa{i}") for i in range(2)]
    sgn = [b.sb([128, 8], F32, f"sgna{i}") for i in range(2)]
    Dg = [b.sb([128, 8, 128], BF16, f"Dg{i}") for i in range(2)]
    tb = [[b.sb([128, 512], BF16, f"tb{s}{i}") for i in range(3)] for s in range(2)]
    pIs = [[b.banks[4], b.banks[6]], [b.banks[5], b.banks[0]]]
    pIas = [b.banks[7], b.banks[1]]
    ctr = {"i": 0, "acc": 0, "kv": 0}

    def load_idx_small(a):
        b.op("sp", lambda e, a=a: e.dma_start(out=qiTa[a % 2][:].rearrange("p a b -> p (a b)"), in_=scr["QI"][a][:]),
             reads=[scr["QI"][a]], writes=[qiTa[a % 2]], dma=f"ld_qiTa{a % 2}")
        b.op("sp", lambda e, a=a: e.dma_start(out=sgn[a % 2][:], in_=scr["SG"][a][:]), reads=[scr["SG"][a]],
             writes=[sgn[a % 2]], dma=f"ld_sgn{a % 2}")

    def indexer_list(a, s, Ib_):
        L = []

        def add(eng, fn, reads=(), writes=()):
            L.append((eng, fn, reads, writes))
        nb = a + 1
        qi, sg, dg = qiTa[a % 2], sgn[a % 2], Dg[a % 2]
        pys, pia = pIs[s], pIas[s]
        add("dve", lambda e: e.tensor_tensor(out=dg[:], in0=idt[:].unsqueeze(1).to_broadcast([128, 8, 128]),
                                             in1=sg[:].unsqueeze(2).to_broadcast([128, 8, 128]), op=ALU.mult),
            [idt, sg], [dg])
        steps = [(blk, head) for blk in range(nb) for head in range(8)]
        S_ = len(steps)
        evq = []
        for k in range(S_ + 2):
            if k < S_:
                blk, head = steps[k]
                hp, hh = head // 2, head % 2
                py = pys[k % 2]
                add("pe", lambda e, hp=hp, hh=hh, blk=blk, py=py: e.matmul(
                    py[:].rearrange("p (r t) -> p r t", r=4), lhsT=qi[64 * hh:64 * hh + 64, hp, :],
                    rhs=kia4[64 * hh:64 * hh + 64, :, blk, :], start=True, stop=True), [qi, kia], [py])
            if 0 <= k - 1 < S_:
                py = pys[(k - 1) % 2]
                t = tb[s][(k - 1) % 3]
                add("act", lambda e, t=t, py=py: e.activation(out=t[:], in_=py[:], func=AF.Relu), [py], [t])
            if 0 <= k - 2 < S_:
                blk, head = steps[k - 2]
                t = tb[s][(k - 2) % 3]
                add("pe", lambda e, head=head, t=t: e.matmul(pia[:], lhsT=dg[:, head, :], rhs=t[:],
                                                             start=(head == 0), stop=(head == 7)), [dg, t], [pia])
                if head == 7:
                    add("act", lambda e, eb=blk: e.activation(out=Ib_[:, eb * 512:(eb + 1) * 512], in_=pia[:], func=AF.Copy),
                        [pia], [Ib_])
        for _, eb in evq:
            add("act", lambda e, eb=eb: e.activation(out=Ib_[:, eb * 512:(eb + 1) * 512], in_=pia[:], func=AF.Copy),
                [pia], [Ib_])
        return L

    mA = b.mark()
    IbA = [b.sb([128, 8192], F32, f"IbA{i}") for i in range(2)]
    MqA = [b.sb([128, 8192], BF16, f"MqA{i}") for i in range(2)]
    st = [{k: b.sb([128, n], F32, f"bs{k}{s}") for k, n in (("m1", 1), ("lo", 1), ("mid", 1), ("nmid", 1), ("cD", 1),
                                                           ("sA", 1), ("g", 1), ("W", NIT2))} for s in range(2)]

    def stageA_list(a, s):
        L = indexer_list(a, s, IbA[s])

        def add(eng, fn, reads=(), writes=()):
            L.append((eng, fn, reads, writes))
        nb = a + 1
        nv = 512 * nb
        Ib_, Mq_ = IbA[s], MqA[s]
        S_ = st[s]
        m1, lo, mid, cD, sA, g, W = (S_[k] for k in ("m1", "lo", "mid", "cD", "sA", "g", "W"))
        add("dve", lambda e: e.tensor_reduce(out=m1[:], in_=Ib_[:, 0:nv], axis=AX.X, op=ALU.max, apply_absolute_value=True),
            [Ib_], [m1])
        add("dve", lambda e: e.tensor_tensor(out=Ib_[:, nv - 512:nv], in0=Ib_[:, nv - 512:nv], in1=negm[:], op=ALU.add),
            [Ib_, negm], [Ib_])
        add("dve", lambda e: e.tensor_scalar(out=W[:], in0=cW[:], scalar1=m1[:, 0:1], scalar2=None, op0=ALU.mult),
            [cW, m1], [W])
        add("dve", lambda e: e.tensor_tensor(out=W[:], in0=W[:], in1=cW[:], op=ALU.add), [W, cW], [W])
        add("dve", lambda e: e.memset(mid[:], 0.0), [], [mid])
        h = 512 * (nb // 2)
        thr = TOPK - 0.5 - 0.5 * h
        for i in range(NIT2):
            if h > 0:
                add("act", lambda e: e.activation(out=Mq_[:, 0:h], in_=Ib_[:, 0:h], func=AF.Sign, bias=mid[:, 0:1],
                                                  scale=-1.0, accum_out=sA[:, 0:1]), [Ib_, mid], [Mq_, sA])
            add("dve", lambda e: e.tensor_scalar(out=Mq_[:, h:nv], in0=Ib_[:, h:nv], scalar1=mid[:, 0:1], scalar2=0.0,
                                                 op0=ALU.is_ge, op1=ALU.add, accum_out=cD[:, 0:1]), [Ib_, mid], [Mq_, cD])
            if h > 0:
                add("dve", lambda e: e.scalar_tensor_tensor(out=cD[:], in0=sA[:], scalar=-0.5, in1=cD[:], op0=ALU.mult,
                                                            op1=ALU.add), [sA, cD], [cD])
            add("dve", lambda e, i=i: e.tensor_scalar(out=g[:], in0=cD[:], scalar1=thr, scalar2=W[:, i:i + 1],
                                                      op0=ALU.is_ge, op1=ALU.mult), [cD, W], [g])
            if i + 1 < NIT2:
                add("dve", lambda e, i=i: e.scalar_tensor_tensor(out=mid[:], in0=mid[:], scalar=W[:, i + 1:i + 2], in1=g[:],
                                                                 op0=ALU.subtract, op1=ALU.add), [mid, W, g], [mid])
            else:
                add("dve", lambda e, i=i: e.scalar_tensor_tensor(out=lo[:], in0=mid[:], scalar=W[:, i:i + 1], in1=g[:],
                                                                 op0=ALU.subtract, op1=ALU.add), [mid, W, g], [lo])
        for c0 in range(0, nv, 2048):
            c1 = min(nv, c0 + 2048)
            add("dve", lambda e, c0=c0, c1=c1: e.tensor_scalar(out=Mq_[:, c0:c1], in0=Ib_[:, c0:c1], scalar1=lo[:, 0:1],
                                                               scalar2=None, op0=ALU.is_ge), [Ib_, lo], [Mq_])
        add("sp", ("dma", lambda e: e.dma_start(out=scr["MQ"][a][:, 0:nv], in_=Mq_[:, 0:nv]), f"st_mq{s}"),
            [Mq_], [scr["MQ"][a]])
        return L

    def emit_item(it):
        eng, fn, reads, writes = it
        if isinstance(fn, tuple):
            b.op(eng, fn[1], reads, writes, dma=fn[2])
        else:
            b.op(eng, fn, reads, writes)

    for a0 in range(0, NS, 2):
        load_idx_small(a0)
        load_idx_small(a0 + 1)
        la = stageA_list(a0, 0)
        lb = stageA_list(a0 + 1, 1)
        i = j = 0
        while i < len(la) or j < len(lb):
            if j >= len(lb) or (i < len(la) and i * len(lb) <= j * len(la)):
                emit_item(la[i]); i += 1
            else:
                emit_item(lb[j]); j += 1
    b.release(mA)

    Mqs = [b.sb([128, 8192], BF16, f"MqB{i}") for i in range(2)]
    MTs = [b.sb([128, 64, 128], BF16, f"MT{i}") for i in range(2)]
    ktp = [b.sb([128, 8192], BF16, f"ktp{i}") for i in range(2)]
    vp = [b.sb([128, 64, 130], BF16, f"vp{i}") for i in range(2)]
    for i in range(2):
        b.op("dve", lambda e, i=i: e.memset(vp[i][:, :, 0:1], 1.0), writes=[vp[i]])
        b.op("dve", lambda e, i=i: e.memset(vp[i][:, :, 129:130], 1.0), writes=[vp[i]])
    qTa = [b.sb([128, 8, 128], BF16, f"qTa{i}") for i in range(2)]
    pT = [b.sb([128, 512], BF16, f"pTd{i}") for i in range(3)]
    rec = [b.sb([128, 1], F32, f"recd{i}") for i in range(2)]
    pS = [b.banks[0], b.banks[1], b.banks[5]]
    pA = [b.banks[2], b.banks[3]]
    pMs = [b.banks16[6], b.banks16[7]]

    def load_slot_small(a):
        nv = 512 * (a + 1)
        b.op("sp", lambda e, a=a: e.dma_start(out=qTa[a % 2][:].rearrange("p a b -> p (a b)"), in_=scr["QT2"][a][:]),
             reads=[scr["QT2"][a]], writes=[qTa[a % 2]], dma=f"ld_qTa{a % 2}")
        b.op("sp", lambda e, a=a, nv=nv: e.dma_start(out=Mqs[a % 2][:, 0:nv], in_=scr["MQ"][a][:, 0:nv]),
             reads=[scr["MQ"][a]], writes=[Mqs[a % 2]], dma=f"ld_mq{a % 2}")

    def load_kv(a, hp):
        k = ctr["kv"] % 2
        ctr["kv"] += 1
        n = (a + 1) * 128
        yb = G1.yb[hp]
        b.op("sp", lambda e: e.dma_start(out=ktp[k][:].rearrange("p (r x) -> p r x", r=4)[:, :, 0:n],
                                         in_=yb[:, 0:n].rearrange("(r p) x -> p r x", p=128)),
             reads=[G1.yd[hp]], writes=[ktp[k]], dma=f"ld_ktp{k}")
        for r in range(4):
            b.op("sp", lambda e, r=r: e.dma_start(
                out=vp[k][:, r * 16:r * 16 + a + 1, 1:129],
                in_=yb[r * 128:(r + 1) * 128, 2048:2048 + n].rearrange("p (a e) -> p a e", e=128)),
                reads=[G1.yd[hp]], writes=[vp[k]], dma=f"ld_vp{k}")
        return k

    def pre_list(a):
        L = []
        Mq, MT = Mqs[a % 2], MTs[a % 2]
        nt = 4 * (a + 1)
        ngr = (nt + 7) // 8
        for gi in range(ngr + 1):
            if gi < ngr:
                pm = pMs[gi % 2]
                for kt in range(gi * 8, min(nt, gi * 8 + 8)):
                    L.append(("pe", lambda e, kt=kt, pm=pm: e.transpose(out=pm[:, (kt % 8) * 128:(kt % 8 + 1) * 128],
                                                                        in_=Mq[:, kt * 128:(kt + 1) * 128], identity=idt[:]),
                              [Mq, idt], [pm]))
            if gi >= 1:
                g0 = gi - 1
                pm = pMs[g0 % 2]
                k0 = g0 * 8
                n = min(nt, k0 + 8) - k0
                L.append(("act", lambda e, pm=pm, k0=k0, n=n: e.activation(
                    out=MT[:, k0:k0 + n, :].rearrange("p a b -> p (a b)"), in_=pm[:, 0:n * 128], func=AF.Copy),
                    [pm], [MT]))
        return L

    def qk(u, n, kbuf):
        a, hp, hh, blk = u
        ps = pS[n % 3]
        qa = qTa[a % 2]
        for i in range(4):
            kt = 16 * i + blk
            b.op("pe", lambda e, i=i, kt=kt, ps=ps, qa=qa, hh=hh, hp=hp, kbuf=kbuf: e.matmul(
                ps[:, i * 128:(i + 1) * 128], lhsT=ktp[kbuf][64 * hh:64 * hh + 64, kt * 128:(kt + 1) * 128],
                rhs=qa[64 * hh:64 * hh + 64, hp, :], start=True, stop=True), reads=[ktp[kbuf], qa], writes=[ps])

    def softmax_pv(u, n, kbuf):
        a, hp, hh, blk = u
        ps, pt = pS[n % 3], pT[n % 3]
        if blk == 0:
            ctr["acc"] += 1
        acc = pA[ctr["acc"] % 2]
        b.op("act", lambda e: e.activation(out=pt[:], in_=ps[:], func=AF.Exp), reads=[ps], writes=[pt])
        MT = MTs[a % 2]
        b.op("dve", lambda e: e.tensor_tensor(out=pt[:], in0=pt[:], in1=MT[:, 4 * blk:4 * blk + 4, :].rearrange("p a b -> p (a b)"),
                                              op=ALU.mult), reads=[pt, MT], writes=[pt])
        for i in range(4):
            kt = 16 * i + blk
            b.op("pe", lambda e, i=i, kt=kt: e.matmul(acc[:, 0:65], lhsT=pt[:, i * 128:(i + 1) * 128],
                                                      rhs=vp[kbuf][:, kt, hh * 65:(hh + 1) * 65],
                                                      start=(blk == 0 and i == 0), stop=(blk == a and i == 3)),
                 reads=[pt, vp[kbuf]], writes=[acc])
        if blk == a:
            head = 2 * hp + hh
            r = rec[ctr["acc"] % 2]
            sc, v0 = (0, 1) if hh == 0 else (64, 0)
            b.op("dve", lambda e: e.reciprocal(out=r[:], in_=acc[:, sc:sc + 1]), reads=[acc], writes=[r])
            b.op("dve", lambda e: e.tensor_scalar(out=o_tok[a][:, head * 64:(head + 1) * 64], in0=acc[:, v0:v0 + 64],
                                                  scalar1=r[:, 0:1], scalar2=None, op0=ALU.mult),
                 reads=[acc, r], writes=[o_tok[a]])

    load_slot_small(0)
    for it in pre_list(0):
        b.op(*it)
    for a in range(NS):
        if a + 1 < NS:
            load_slot_small(a + 1)
        kb_next = load_kv(a, 0)
        nxt = pre_list(a + 1) if a + 1 < NS else []
        done = 0
        units = [(a, hp, hh, blk) for hp in range(8) for hh in range(2) for blk in range(a + 1)]
        kbufs = {0: kb_next}
        kbufs[1] = load_kv(a, 1)
        qk(units[0], 0, kbufs[0])
        if len(units) > 1:
            qk(units[1], 1, kbufs[units[1][1]])
        for n, u in enumerate(units):
            _, hp, hh, blk = u
            if hh == 0 and blk == 0 and 1 <= hp and hp + 1 < 8:
                kbufs[hp + 1] = load_kv(a, hp + 1)
            if n + 2 < len(units):
                qk(units[n + 2], n + 2, kbufs[units[n + 2][1]])
            softmax_pv(u, n, kbufs[hp])
            want = (n + 1) * len(nxt) // len(units)
            while done < want:
                b.op(*nxt[done])
                done += 1
        while done < len(nxt):
            b.op(*nxt[done])
            done += 1


BF = ml_dtypes.bfloat16


def rep(v, n=128):
    return np.ascontiguousarray(np.tile(np.asarray(v).reshape(1, -1), (n, 1)))


def own_tiles(arr_bs, c):
    bb, j = c // 4, c % 4
    a = arr_bs[bb]
    return np.ascontiguousarray(a.reshape(64, 128, *a.shape[1:])[j::4].reshape(2048, *a.shape[1:]))


def gather_tiles(per_core, bb):
    out = np.empty((64,) + per_core[0].shape[1:], per_core[0].dtype)
    for j in range(4):
        out[j::4] = per_core[bb * 4 + j]
    return out


def diff_masks(c):
    j = c % 4
    m = np.zeros((128, 4, 128), np.float32)
    for i in range(4):
        if i < j:
            m[:, i, :] = 1.0
        elif i == j:
            m[0:64, i, :] = 1.0
            m[64:128, i, 64:128] = 1.0
    return m.reshape(128, 512).astype(BF)


def dsa_negmask(c):
    return np.where(diff_masks(c).astype(np.float32).reshape(128, 4, 128).transpose(2, 1, 0).reshape(128, 512) > 0,
                    0.0, -1e30).astype(np.float32)


def build_L1():
    nc = bass.Bass("TRN2", target_bir_lowering=False)
    b = B(nc)
    io = {
        "x": b.dram("x", [2048, 1024], F32, "ExternalInput"),
        "pos": b.dram("pos", [128, 16], I32, "ExternalInput"),
        "gmix": b.dram("gmix", [128, 1024], F32, "ExternalInput"),
        "gqk": b.dram("gqk", [128, 2048], F32, "ExternalInput"),
        "w_in": b.dram("w_in", [1024, 3072], F32, "ExternalInput"),
        "QT": b.dram("QT", [16, 128, 1024], BF16, "ExternalOutput"),
        "KT": b.dram("KT", [16, 128, 1024], BF16, "ExternalOutput"),
        "V": b.dram("V", [16, 128, 8 * 129], BF16, "ExternalOutput"),
    }
    phase_diff_proj(b, io)
    b.finish()
    return nc


def build_L2():
    nc = bass.Bass("TRN2", target_bir_lowering=False)
    b = B(nc)
    io = {
        "QT": b.dram("QT", [16, 128, 1024], BF16, "ExternalInput"),
        "KTall": b.dram("KTall", [8, 128, 8192], BF16, "ExternalInput"),
        "Vall": b.dram("Vall", [8, 128, 64 * 129], BF16, "ExternalInput"),
        "maskT": b.dram("maskT", [128, 512], BF16, "ExternalInput"),
        "lam": b.dram("lam", [128, 256], F32, "ExternalInput"),
        "gsub": b.dram("gsub", [128, 128], F32, "ExternalInput"),
        "x": b.dram("x", [2048, 1024], F32, "ExternalInput"),
        "pos": b.dram("pos", [128, 16], I32, "ExternalInput"),
        "w_out": b.dram("w_out", [1024, 1024], F32, "ExternalInput"),
        "gmlp": b.dram("gmlp", [128, 1024], F32, "ExternalInput"),
        "w1": b.dram("w1", [1024, 4096], F32, "ExternalInput"),
        "w2": b.dram("w2", [4096, 1024], F32, "ExternalInput"),
        "gmix1": b.dram("gmix1", [128, 1024], F32, "ExternalInput"),
        "gqk2": b.dram("gqk2", [128, 2048], F32, "ExternalInput"),
        "gcq": b.dram("gcq", [128, 256], F32, "ExternalInput"),
        "w_in2": b.dram("w_in2", [1024, 2376], F32, "ExternalInput"),
        "w_uq": b.dram("w_uq", [256, 1024], F32, "ExternalInput"),
        "w_uqi": b.dram("w_uqi", [256, 512], F32, "ExternalInput"),
        "X2": b.dram("X2", [2048, 1024], F32, "ExternalOutput"),
        "QT2": b.dram("QT2", [16, 128, 1024], BF16, "ExternalOutput"),
        "KT2": b.dram("KT2", [16, 128, 1024], BF16, "ExternalOutput"),
        "V2": b.dram("V2", [16, 128, 16 * 65], BF16, "ExternalOutput"),
        "KI": b.dram("KI", [16, 128, 128], BF16, "ExternalOutput"),
        "QI": b.dram("QI", [16, 128, 512], BF16, "ExternalOutput"),
        "SG": b.dram("SG", [16, 128, 8], F32, "ExternalOutput"),
    }
    b.ident(); b.eps()
    pos_t = b.sb([128, NS], I32, "pos")
    b.op("sp", lambda e: e.dma_start(out=pos_t[:], in_=io["pos"][:]), writes=[pos_t], dma="ld_pos")
    cos, sin = rope_tables(b, pos_t)
    o_tok = [b.sb([128, 1024], BF16, f"otok{a}", top=True) for a in range(NS)]
    m1 = b.mark()
    phase_diff_attn(b, io, o_tok)
    b.release(m1)
    x_res = [b.sb([128, 1024], F32, f"xres{a}") for a in range(NS)]
    h2T = b.sb([128, 8, 2048], BF16, "h2T")
    m2 = b.mark()
    phase_post_attn(b, io, o_tok, x_res, h2T, io["w_out"][:], io["gmlp"][:], io["x"])
    b.release(m2)
    b.hi = ARENA_END
    m3 = b.mark()
    phase_mlp(b, x_res, h2T, io["w1"], io["w2"])
    b.release(m3)
    for a in range(NS):
        b.store("sp", lambda e, a=a: e.dma_start(out=io["X2"][a * 128:(a + 1) * 128, :], in_=x_res[a][:]),
                reads=[x_res[a]], dma="st_x2")
    phase_dsa_proj(b, io, x_res, cos, sin)
    b.finish()
    return nc


def l1_inputs(inp, c):
    return {"x": own_tiles(inp["x"], c),
            "pos": np.ascontiguousarray(own_tiles(inp["positions"], c).reshape(16, 128).T),
            "gmix": rep(inp["norm_mix"][0]),
            "gqk": rep(np.concatenate([np.tile(inp["diff_q_norm"][0], 16), np.tile(inp["diff_k_norm"][0], 16)])),
            "w_in": np.ascontiguousarray(inp["diff_w_in"][0])}


def l2_inputs(inp, r1):
    KTall = []; Vall = []
    for bb in range(2):
        kt = gather_tiles([r["KT"].reshape(16, 128, 8, 128) for r in r1], bb)
        KTall.append(np.ascontiguousarray(kt.transpose(2, 1, 0, 3).reshape(8, 128, 8192)))
        v = gather_tiles([r["V"].reshape(16, 128, 8, 129) for r in r1], bb)
        Vall.append(np.ascontiguousarray(v.transpose(2, 1, 0, 3).reshape(8, 128, 64 * 129)))
    lam = rep(np.concatenate([inp["diff_lam_q1"][0], inp["diff_lam_k1"][0], inp["diff_lam_q2"][0], inp["diff_lam_k2"][0]]))
    gqk2 = rep(np.concatenate([np.tile(inp["dsa_q_norm"][0], 16), np.tile(inp["dsa_k_norm"][0], 16)]))
    ins = []
    for c in range(8):
        ins.append({"QT": r1[c]["QT"], "KTall": KTall[c // 4], "Vall": Vall[c // 4], "maskT": diff_masks(c),
                    "lam": lam, "gsub": rep(inp["diff_subln"][0]),
                    "x": own_tiles(inp["x"], c),
                    "pos": np.ascontiguousarray(own_tiles(inp["positions"], c).reshape(16, 128).T),
                    "w_out": np.ascontiguousarray(inp["diff_w_out"][0]), "gmlp": rep(inp["norm_mlp"][0]),
                    "w1": np.ascontiguousarray(inp["mlp_w1"][0]), "w2": np.ascontiguousarray(inp["mlp_w2"][0]),
                    "gmix1": rep(inp["norm_mix"][1]), "gqk2": gqk2, "gcq": rep(inp["dsa_cq_norm"][0]),
                    "w_in2": np.ascontiguousarray(inp["dsa_w_in"][0]), "w_uq": np.ascontiguousarray(inp["dsa_w_uq"][0]),
                    "w_uqi": np.ascontiguousarray(inp["dsa_w_uq_idx"][0])})
    return ins


def run(nc, ins):
    res = run_bass_kernel_spmd(nc, ins, core_ids=list(range(8)))
    return [{k: np.asarray(v) for k, v in r.items()} for r in res.results]


def build_L3():
    nc = bass.Bass("TRN2", target_bir_lowering=False)
    b = B(nc)
    io = {
        "QT2": b.dram("QT2", [16, 128, 1024], BF16, "ExternalInput"),
        "QI": b.dram("QI", [16, 128, 512], BF16, "ExternalInput"),
        "SG": b.dram("SG", [16, 128, 8], F32, "ExternalInput"),
        "KIall": b.dram("KIall", [128, 8192], BF16, "ExternalInput"),
        "KT2all": b.dram("KT2all", [8, 128, 8192], BF16, "ExternalInput"),
        "V2all": b.dram("V2all", [8, 128, 64 * 130], BF16, "ExternalInput"),
        "negmask": b.dram("negmask", [128, 512], F32, "ExternalInput"),
        "X2": b.dram("X2", [2048, 1024], F32, "ExternalInput"),
        "w_out": b.dram("w_out", [1024, 1024], F32, "ExternalInput"),
        "gmlp": b.dram("gmlp", [128, 1024], F32, "ExternalInput"),
        "w1": b.dram("w1", [1024, 4096], F32, "ExternalInput"),
        "w2": b.dram("w2", [4096, 1024], F32, "ExternalInput"),
        "OUT": b.dram("OUT", [2048, 1024], F32, "ExternalOutput"),
    }
    b.ident(); b.eps()
    o_tok = [b.sb([128, 1024], BF16, f"otok{a}", top=True) for a in range(NS)]
    m1 = b.mark()
    phase_dsa_attn(b, io, o_tok)
    b.release(m1)
    x_res = [b.sb([128, 1024], F32, f"xres{a}") for a in range(NS)]
    h2T = b.sb([128, 8, 2048], BF16, "h2T")
    m2 = b.mark()
    phase_post_attn(b, io, o_tok, x_res, h2T, io["w_out"][:], io["gmlp"][:], io["X2"])
    b.release(m2)
    b.hi = ARENA_END
    m3 = b.mark()
    phase_mlp(b, x_res, h2T, io["w1"], io["w2"])
    b.release(m3)
    for a in range(NS):
        b.store("sp", lambda e, a=a: e.dma_start(out=io["OUT"][a * 128:(a + 1) * 128, :], in_=x_res[a][:]),
                reads=[x_res[a]], dma="st_out")
    b.finish()
    return nc


def l3_inputs(inp, r2):
    KIall = []; KTall = []; Vall = []
    for bb in range(2):
        ki = gather_tiles([r["KI"] for r in r2], bb)
        KIall.append(np.ascontiguousarray(ki.transpose(1, 0, 2).reshape(128, 8192)))
        kt = gather_tiles([r["KT2"].reshape(16, 128, 8, 128) for r in r2], bb)
        KTall.append(np.ascontiguousarray(kt.transpose(2, 1, 0, 3).reshape(8, 128, 8192)))
        v = gather_tiles([r["V2"].reshape(16, 128, 8, 130) for r in r2], bb)
        Vall.append(np.ascontiguousarray(v.transpose(2, 1, 0, 3).reshape(8, 128, 64 * 130)))
    ins = []
    for c in range(8):
        ins.append({"QT2": r2[c]["QT2"], "QI": r2[c]["QI"], "SG": r2[c]["SG"], "KIall": KIall[c // 4],
                    "KT2all": KTall[c // 4], "V2all": Vall[c // 4], "negmask": dsa_negmask(c),
                    "X2": r2[c]["X2"], "w_out": np.ascontiguousarray(inp["dsa_w_out"][0]), "gmlp": rep(inp["norm_mlp"][1]),
                    "w1": np.ascontiguousarray(inp["mlp_w1"][1]), "w2": np.ascontiguousarray(inp["mlp_w2"][1])})
    return ins


def assemble(r3):
    out = np.empty((2, 8192, 1024), np.float32)
    for bb in range(2):
        o = gather_tiles([r["OUT"].reshape(16, 128, 1024) for r in r3], bb)
        out[bb] = o.reshape(8192, 1024)
    return out


FUSED_IN = [
    ("x", [2048, 1024], F32), ("pos", [128, 16], I32), ("gmix", [128, 1024], F32), ("gqk", [128, 2048], F32),
    ("w_in", [1024, 3072], F32), ("maskT", [128, 512], BF16), ("lam", [128, 256], F32), ("gsub", [128, 128], F32),
    ("w_out", [1024, 1024], F32), ("gmlp", [128, 1024], F32), ("w1", [1024, 4096], F32), ("w2", [4096, 1024], F32),
    ("gmix1", [128, 1024], F32), ("gqk2", [128, 2048], F32), ("gcq", [128, 256], F32), ("w_in2", [1024, 2376], F32),
    ("w_uq", [256, 1024], F32), ("w_uqi", [256, 512], F32), ("negmask", [128, 512], F32),
    ("w_outb", [1024, 1024], F32), ("gmlpb", [128, 1024], F32), ("w1b", [1024, 4096], F32), ("w2b", [4096, 1024], F32),
]


def build_fused():
    nc = bass.Bass("TRN2", target_bir_lowering=False)
    _RANK.clear()
    b = B(nc)
    io = {n: b.dram(n, s, d, "ExternalInput") for (n, s, d) in FUSED_IN}
    io["OUT"] = b.dram("OUT", [2048, 1024], F32, "ExternalOutput")
    qt2 = nc.dram_tensor("scr_qt2", [16, 128, 1024], BF16).ap()
    qi = nc.dram_tensor("scr_qi", [16, 128, 512], BF16).ap()
    sg = nc.dram_tensor("scr_sg", [16, 128, 8], F32).ap()
    x2 = nc.dram_tensor("scr_x2", [2048, 1024], F32).ap()
    mqd = nc.dram_tensor("scr_mq", [16, 128, 8192], BF16).ap()
    scr = {"QT2": [T(qt2[a]) for a in range(NS)], "QI": [T(qi[a]) for a in range(NS)], "SG": [T(sg[a]) for a in range(NS)],
           "MQ": [T(mqd[a]) for a in range(NS)]}
    x2d = [T(x2[a * 128:(a + 1) * 128, :]) for a in range(NS)]

    b.ident(); b.eps()
    pos_t = b.sb([128, NS], I32, "pos")
    b.op("sp", lambda e: e.dma_start(out=pos_t[:], in_=io["pos"][:]), writes=[pos_t], dma="ld_pos")
    cos, sin = rope_tables(b, pos_t)
    o_tok = [b.sb([128, 1024], BF16, f"otok{a}", top=True) for a in range(NS)]
    hi_otok = b.hi
    qT_res = b.sb([128, NS, 1024], BF16, "qTres", top=True)
    m0 = b.mark()
    zt = b.sb([128, 4096], BF16, "zeros")
    b.op("pool", lambda e: e.memset(zt[:], 0.0), writes=[zt])
    G0 = Gather(b, "g0", 8, 4096, zt)
    G1 = Gather(b, "g1", 8, 4096, zt)
    GK = Gather(b, "gk", 1, 2048, zt)
    wst = [T(nc.alloc_sbuf_tensor_at(f"wstage{i}", [128, 8, 512], F32, offset=ARENA_END - (i + 1) * 16384)) for i in range(2)]
    f_diff_proj(b, io, qT_res, G0, cos, sin, wstage=wst)
    b.release(m0)
    f_diff_attn(b, io, o_tok, qT_res, G0)
    b.release(m0)
    b.hi = hi_otok
    x_res = [b.sb([128, 1024], F32, f"xres{a}") for a in range(NS)]
    mh = b.mark()
    h2T = b.sb([128, 8, 2048], BF16, "h2T")
    m2 = b.mark()
    phase_post_attn(b, io, o_tok, x_res, h2T, io["w_out"][:], io["gmlp"][:], io["x"])
    b.release(m2)
    b.hi = ARENA_END
    pre = dsa_proj_weights(b, io, top=True)
    phase_mlp(b, x_res, h2T, io["w1"], io["w2"], hook=pre["load"])
    b.release((mh[0], b.hi))
    for a in range(NS):
        b.op("sp", lambda e, a=a: e.dma_start(out=x2d[a][:], in_=x_res[a][:]), reads=[x_res[a]], writes=[x2d[a]], dma="st_x2")
    io2 = dict(io)
    f_dsa_proj(b, io2, x_res, cos, sin, G1, GK, scr, pre=pre)
    b.release(m0)
    b.hi = ARENA_END
    o_tok = [b.sb([128, 1024], BF16, f"otokb{a}", top=True) for a in range(NS)]
    m4 = b.mark()
    f_dsa_attn(b, io, o_tok, G1, GK, scr)
    b.release(m4)
    x_res = [b.sb([128, 1024], F32, f"xresb{a}") for a in range(NS)]
    h2T = b.sb([128, 8, 2048], BF16, "h2Tb")
    m5 = b.mark()
    phase_post_attn(b, io, o_tok, x_res, h2T, io["w_outb"][:], io["gmlpb"][:], x2, xdeps=x2d)
    b.release(m5)
    b.hi = ARENA_END
    phase_mlp(b, x_res, h2T, io["w1b"], io["w2b"])
    for a in range(NS):
        b.store("sp", lambda e, a=a: e.dma_start(out=io["OUT"][a * 128:(a + 1) * 128, :], in_=x_res[a][:]),
                reads=[x_res[a]], dma="st_out")
    b.finish()
    return nc


def fused_inputs(inp):
    lam = rep(np.concatenate([inp["diff_lam_q1"][0], inp["diff_lam_k1"][0], inp["diff_lam_q2"][0], inp["diff_lam_k2"][0]]))
    gqk = rep(np.concatenate([np.tile(inp["diff_q_norm"][0], 16), np.tile(inp["diff_k_norm"][0], 16)]))
    gqk2 = rep(np.concatenate([np.tile(inp["dsa_q_norm"][0], 16), np.tile(inp["dsa_k_norm"][0], 16)]))
    c_ = np.ascontiguousarray
    shared = {"gmix": rep(inp["norm_mix"][0]), "gqk": gqk, "w_in": c_(inp["diff_w_in"][0]), "lam": lam,
              "gsub": rep(inp["diff_subln"][0]), "w_out": c_(inp["diff_w_out"][0]), "gmlp": rep(inp["norm_mlp"][0]),
              "w1": c_(inp["mlp_w1"][0]), "w2": c_(inp["mlp_w2"][0]), "gmix1": rep(inp["norm_mix"][1]), "gqk2": gqk2,
              "gcq": rep(inp["dsa_cq_norm"][0]), "w_in2": c_(inp["dsa_w_in"][0]), "w_uq": c_(inp["dsa_w_uq"][0]),
              "w_uqi": c_(inp["dsa_w_uq_idx"][0]), "w_outb": c_(inp["dsa_w_out"][0]), "gmlpb": rep(inp["norm_mlp"][1]),
              "w1b": c_(inp["mlp_w1"][1]), "w2b": c_(inp["mlp_w2"][1])}
    ins = []
    for c in range(8):
        d = dict(shared)
        d["x"] = own_tiles(inp["x"], c)
        d["pos"] = np.ascontiguousarray(own_tiles(inp["positions"], c).reshape(16, 128).T)
        d["maskT"] = diff_masks(c)
        d["negmask"] = dsa_negmask(c)
        ins.append(d)
    return ins


def kernel(**inputs):
    inp = {k: np.asarray(v) for k, v in inputs.items()}
    r = run(build_fused(), fused_inputs(inp))
    return assemble(r)
```

```python
import math
from contextlib import ExitStack
import numpy as np
import ml_dtypes
import concourse.bass as bass
import concourse.mybir as mybir
from concourse.bass_utils import run_bass_kernel_spmd


F32 = mybir.dt.float32
BF16 = mybir.dt.bfloat16
I32 = mybir.dt.int32
AF = mybir.ActivationFunctionType
ALU = mybir.AluOpType
AX = mybir.AxisListType


class Dep:
    __slots__ = ("w", "r")

    def __init__(self):
        self.w = None
        self.r = {}


class Sched:
    ENG = ("pe", "act", "dve", "pool", "sp")

    def __init__(self, nc):
        self.nc = nc
        self.ops = {e: [] for e in self.ENG}
        self.cnt = {e: 0 for e in self.ENG}
        self.known = {e: {} for e in self.ENG}
        self.dma_cnt = {}
        self.stack = ExitStack()
        self.nt = 0

    def sb(self, shape, dtype, name=None):
        self.nt += 1
        name = "sb_" + (name or f"t{self.nt}")
        return self.stack.enter_context(self.nc.sbuf_tensor(name, list(shape), dtype))

    def ps(self, shape, dtype, name=None):
        self.nt += 1
        name = "ps_" + (name or f"p{self.nt}")
        return self.stack.enter_context(self.nc.psum_tensor(name, list(shape), dtype))

    def op(self, eng, fn, reads=(), writes=(), dma=None, sem_inc=16):
        waits = {}

        def need(ev, raw):
            if ev is None:
                return
            key, val = ev
            if key == eng:
                if eng == "pe" or not raw:
                    return
            if waits.get(key, 0) < val:
                waits[key] = val

        for d in reads:
            need(d.w, True)
        for d in writes:
            need(d.w, False)
            for ev in d.r.items():
                need(ev, False)
        kn = self.known[eng]
        wl = []
        for key, val in waits.items():
            if kn.get(key, 0) >= val:
                continue
            kn[key] = val
            wl.append((key, val))
        if dma is not None:
            n = self.dma_cnt.get(dma, 0) + sem_inc
            self.dma_cnt[dma] = n
            ev = (dma, n)
            inc = (dma, sem_inc)
        else:
            self.cnt[eng] += 1
            ev = (eng, self.cnt[eng])
            inc = (eng, 1)
        self.ops[eng].append((wl, fn, inc))
        for d in reads:
            if d.r.get(ev[0], 0) < ev[1]:
                d.r[ev[0]] = ev[1]
        for d in writes:
            d.w = ev
            d.r = {}
        return ev

    def final_wait(self, eng, deps):
        waits = {}
        for d in deps:
            for ev in ([d.w] if d.w else []) + list(d.r.items()):
                if waits.get(ev[0], 0) < ev[1]:
                    waits[ev[0]] = ev[1]
        self.ops[eng].append((list(waits.items()), None, None))

    def barrier(self):
        waits = {e: self.cnt[e] for e in self.ENG if self.cnt[e] > 0}
        for k, n in self.dma_cnt.items():
            if not k.startswith("cc_"):
                waits[k] = n
        for e in self.ENG:
            kn = self.known[e]
            wl = []
            for key, val in waits.items():
                if key == e or kn.get(key, 0) >= val:
                    continue
                kn[key] = val
                wl.append((key, val))
            self.ops[e].append((wl, None, None))

    def final_events(self, eng, evs):
        waits = {}
        for ev in evs:
            if waits.get(ev[0], 0) < ev[1]:
                waits[ev[0]] = ev[1]
        self.ops[eng].append((list(waits.items()), None, None))

    def emit(self):
        nc = self.nc
        keys = set(self.ENG) | set(self.dma_cnt.keys())
        assert len(keys) <= 100, f"too many semaphores: {len(keys)}"
        sems = {}
        for k in sorted(keys):
            sems[k] = self.stack.enter_context(nc.semaphore("s_" + k))
        ops = self.ops

        def run(e, lst):
            for wl, fn, inc in lst:
                for key, val in wl:
                    e.wait_ge(sems[key], val)
                if fn is None:
                    continue
                ins = fn(e)
                if inc is not None:
                    ins.then_inc(sems[inc[0]], inc[1])

        with nc.Block() as block:
            @block.tensor
            def _(e):
                run(e, ops["pe"])

            @block.scalar
            def _(e):
                run(e, ops["act"])

            @block.vector
            def _(e):
                run(e, ops["dve"])

            @block.gpsimd
            def _(e):
                run(e, ops["pool"])

            @block.sync
            def _(e):
                run(e, ops["sp"])
        self.stack.close()


NS = 16
D = 1024
EPS = 1e-6
INV_FREQ = [500000.0 ** (-(2.0 * j) / 16.0) for j in range(8)]
TWO_PI_S = 6.28318
PI_S = 3.14159


class T:
    def __init__(self, t, d=None):
        self.t = t
        self.d = d if d is not None else Dep()

    def __getitem__(self, k):
        return self.t[k]


ARENA_BASE = 16512
ARENA_END = 16512 + 212736


class B:
    def __init__(self, nc):
        self.nc = nc
        self.S = Sched(nc)
        self._consts = {}
        self.final = []
        self.arena = nc.alloc_sbuf_tensor("arena", [128, ARENA_END - ARENA_BASE], mybir.dt.uint8)
        self.lo = ARENA_BASE
        self.hi = ARENA_END
        self.nt = 0
        self.banks = [T(nc.alloc_psum_tensor(f"bank{i}", [128, 512], F32)) for i in range(8)]
        self.banks16 = [T(bk.t[:].bitcast(BF16), bk.d) for bk in self.banks]

    def sb(self, shape, dt, name=None, top=False):
        self.nt += 1
        nm = f"sb{self.nt}_{name or 't'}"
        size = int(np.prod(shape[1:])) * mybir.dt.size(dt)
        size = (size + 31) // 32 * 32
        if top:
            self.hi -= size
            off = self.hi
        else:
            off = self.lo
            self.lo += size
        assert self.lo <= self.hi, f"SBUF arena overflow at {nm}: lo={self.lo} hi={self.hi}"
        return T(self.nc.alloc_sbuf_tensor_at(nm, list(shape), dt, offset=off))

    def mark(self):
        return (self.lo, self.hi)

    def release(self, m):
        self.lo, self.hi = m
        self.S.barrier()

    def dram(self, name, shape, dt, kind):
        t = T(self.nc.dram_tensor(name, list(shape), dt, kind=kind).ap())
        return t

    def op(self, eng, fn, reads=(), writes=(), dma=None, sem_inc=16):
        return self.S.op(eng, fn, [x.d for x in reads], [x.d for x in writes], dma, sem_inc)

    def store(self, eng, fn, reads, dma):
        ev = self.S.op(eng, fn, [x.d for x in reads], [], dma)
        self.final.append(ev)
        return ev

    def finish(self):
        self.S.final_events("sp", self.final)
        self.S.emit()

    def ident(self):
        if "ident" not in self._consts:
            idt = self.sb([128, 128], BF16, "ident")
            self.op("pool", lambda e: e.memset(idt[:], 0.0), writes=[idt])
            self.op("pool", lambda e: e.affine_select(out=idt[:], in_=idt[:], pattern=[[-1, 128]],
                                                     compare_op=ALU.not_equal, fill=1.0, base=0,
                                                     channel_multiplier=1), reads=[idt], writes=[idt])
            self._consts["ident"] = idt
        return self._consts["ident"]

    def eps(self):
        if "eps" not in self._consts:
            t = self.sb([128, 1], F32, "eps")
            self.op("pool", lambda e: e.memset(t[:], EPS), writes=[t])
            self._consts["eps"] = t
        return self._consts["eps"]


def rstd_from_ss(b, ss, n, scale):
    eps = b.eps()
    b.op("act", lambda e: e.activation(out=ss[:, 0:n], in_=ss[:, 0:n], func=AF.Ln, bias=eps[:], scale=scale),
         reads=[ss, eps], writes=[ss])
    b.op("act", lambda e: e.activation(out=ss[:, 0:n], in_=ss[:, 0:n], func=AF.Exp, scale=-0.5),
         reads=[ss], writes=[ss])


def rope_tables(b, pos_t):
    posf = b.sb([128, NS], F32, "posf")
    inv = b.sb([128, 8], F32, "invf")
    ang = b.sb([128, NS, 8], F32, "ang")
    ti = b.sb([128, NS, 8], I32, "angi")
    tf = b.sb([128, NS, 8], F32, "angf")
    neg = b.sb([128, NS, 8], F32, "angn")
    cos = b.sb([128, NS, 8], F32, "cos")
    sin = b.sb([128, NS, 8], F32, "sin")
    nb = b.sb([128, 1], F32, "negpi")
    b.op("pool", lambda e: e.memset(nb[:], -PI_S), writes=[nb])
    b.op("dve", lambda e: e.tensor_copy(out=posf[:], in_=pos_t[:]), reads=[pos_t], writes=[posf])
    for j in range(8):
        b.op("pool", lambda e, j=j: e.memset(inv[:, j:j + 1], INV_FREQ[j] / (2 * math.pi)), writes=[inv])
    b.op("dve", lambda e: e.tensor_tensor(out=ang[:], in0=posf[:].unsqueeze(2).to_broadcast([128, NS, 8]),
                                          in1=inv[:].unsqueeze(1).to_broadcast([128, NS, 8]), op=ALU.mult),
         reads=[posf, inv], writes=[ang])
    for (dst, off) in ((sin, 0.5), (cos, 0.75)):
        b.op("dve", lambda e, off=off: e.tensor_scalar(out=tf[:], in0=ang[:], scalar1=off, scalar2=None, op0=ALU.add),
             reads=[ang], writes=[tf])
        b.op("dve", lambda e: e.tensor_copy(out=ti[:], in_=tf[:]), reads=[tf], writes=[ti])
        b.op("dve", lambda e: e.tensor_copy(out=neg[:], in_=ti[:]), reads=[ti], writes=[neg])
        b.op("dve", lambda e: e.tensor_tensor(out=tf[:], in0=tf[:], in1=neg[:], op=ALU.subtract),
             reads=[tf, neg], writes=[tf])
        b.op("dve", lambda e: e.tensor_scalar(out=neg[:], in0=tf[:], scalar1=0.0, scalar2=None, op0=ALU.is_lt),
             reads=[tf], writes=[neg])
        b.op("dve", lambda e: e.tensor_tensor(out=tf[:], in0=tf[:], in1=neg[:], op=ALU.add),
             reads=[tf, neg], writes=[tf])
        b.op("act", lambda e, dst=dst: e.activation(out=dst[:], in_=tf[:], func=AF.Sin, bias=nb[:], scale=TWO_PI_S),
             reads=[tf, nb], writes=[dst])
    return cos, sin


def rmsnorm_transpose(b, xt, gt, hbf, hT, pT, ss, junk):
    idt = b.ident()
    b.op("act", lambda e: e.activation(out=junk[:], in_=xt[:], func=AF.Square, accum_out=ss[:, 0:1]),
         reads=[xt], writes=[junk, ss])
    rstd_from_ss(b, ss, 1, 1.0 / D)
    b.op("dve", lambda e: e.scalar_tensor_tensor(out=hbf[:], in0=xt[:], scalar=ss[:, 0:1], in1=gt[:],
                                                 op0=ALU.mult, op1=ALU.mult),
         reads=[xt, ss, gt], writes=[hbf])
    for kc in range(8):
        b.op("pe", lambda e, kc=kc: e.transpose(out=pT[:, kc * 128:(kc + 1) * 128],
                                                in_=hbf[:, kc * 128:(kc + 1) * 128], identity=idt[:]),
             reads=[hbf, idt], writes=[pT])
    b.op("dve", lambda e: e.tensor_copy(out=hT[:].rearrange("p a b -> p (a b)"), in_=pT[:]),
         reads=[pT], writes=[hT])


def headnorm_rope(b, stage, sq, ssq, nh, gain, cos_a, sin_a, outbf, tmp, norm=True):
    s3 = stage[:].rearrange("p (h d) -> p h d", d=64)
    if norm:
        b.op("dve", lambda e: e.tensor_reduce(out=ssq[:, 0:nh], in_=sq[:].rearrange("p (h d) -> p h d", d=64),
                                              axis=AX.X, op=ALU.add), reads=[sq], writes=[ssq])
        rstd_from_ss(b, ssq, nh, 1.0 / 64)
    b.op("dve", lambda e: e.tensor_tensor(out=s3, in0=s3, in1=ssq[:, 0:nh].unsqueeze(2).to_broadcast([128, nh, 64]),
                                          op=ALU.mult), reads=[stage, ssq], writes=[stage])
    if gain is not None:
        b.op("pool", lambda e: e.tensor_tensor(out=stage[:], in0=stage[:], in1=gain[:], op=ALU.mult),
             reads=[stage, gain], writes=[stage])
    b.op("act", lambda e: e.activation(out=outbf[:], in_=stage[:], func=AF.Copy), reads=[stage], writes=[outbf])
    o3 = outbf[:].rearrange("p (h d) -> p h d", d=64)
    x1 = s3[:, :, 0:8]
    x2 = s3[:, :, 8:16]
    cb = cos_a.unsqueeze(1).to_broadcast([128, nh, 8])
    sb_ = sin_a.unsqueeze(1).to_broadcast([128, nh, 8])
    t = tmp[:].rearrange("p (k h d) -> p k h d", k=4, d=8)
    eng = "pool"
    b.op(eng, lambda e: e.tensor_tensor(out=t[:, 0, 0:nh, :], in0=x1, in1=cb, op=ALU.mult), reads=[stage], writes=[tmp])
    b.op(eng, lambda e: e.tensor_tensor(out=t[:, 1, 0:nh, :], in0=x2, in1=sb_, op=ALU.mult), reads=[stage], writes=[tmp])
    b.op(eng, lambda e: e.tensor_tensor(out=t[:, 2, 0:nh, :], in0=x2, in1=cb, op=ALU.mult), reads=[stage], writes=[tmp])
    b.op(eng, lambda e: e.tensor_tensor(out=t[:, 3, 0:nh, :], in0=x1, in1=sb_, op=ALU.mult), reads=[stage], writes=[tmp])
    b.op(eng, lambda e: e.tensor_tensor(out=o3[:, :, 0:8], in0=t[:, 0, 0:nh, :], in1=t[:, 1, 0:nh, :], op=ALU.subtract),
         reads=[tmp], writes=[outbf])
    b.op(eng, lambda e: e.tensor_tensor(out=o3[:, :, 8:16], in0=t[:, 2, 0:nh, :], in1=t[:, 3, 0:nh, :], op=ALU.add),
         reads=[tmp], writes=[outbf])


def phase_diff_proj(b, io):
    idt = b.ident()
    pos_t = b.sb([128, NS], I32, "pos")
    b.op("sp", lambda e: e.dma_start(out=pos_t[:], in_=io["pos"][:]), writes=[pos_t], dma="ld_pos")
    gmix = b.sb([128, D], F32, "gmix")
    b.op("sp", lambda e: e.dma_start(out=gmix[:], in_=io["gmix"][:]), writes=[gmix], dma="ld_gmix")
    gqk = b.sb([128, 2048], F32, "gqk")
    b.op("sp", lambda e: e.dma_start(out=gqk[:], in_=io["gqk"][:]), writes=[gqk], dma="ld_gqk")
    b.op("pool", lambda e: e.tensor_scalar(out=gqk[:, 0:1024], in0=gqk[:, 0:1024], scalar1=0.125, scalar2=None,
                                           op0=ALU.mult), reads=[gqk], writes=[gqk])
    w = b.sb([128, 8, 3072], BF16, "w_in")
    for kc in range(8):
        for hf in range(3):
            b.op("pool", lambda e, kc=kc, hf=hf: e.dma_start(
                out=w[:, kc, hf * 1024:(hf + 1) * 1024],
                in_=io["w_in"][kc * 128:(kc + 1) * 128, hf * 1024:(hf + 1) * 1024]),
                writes=[w], dma="ld_w")
    cos, sin = rope_tables(b, pos_t)

    xts = [b.sb([128, D], F32, f"xt{i}") for i in range(2)]
    junk = b.sb([128, D], BF16, "junk")
    ss = b.sb([128, 1], F32, "ss")
    hbf = b.sb([128, D], BF16, "hbf")
    hT = b.sb([128, 8, 128], BF16, "hT")
    pT = [b.banks16[0], b.banks16[1]]
    pY = [b.banks[2 + i] for i in range(4)]
    stage = b.sb([128, 2048], F32, "stage")
    sq = b.sb([128, 2048], F32, "sq")
    ssq = b.sb([128, 32], F32, "ssq")
    tmp = b.sb([128, 4 * 32 * 8], F32, "ropetmp")
    qkbf = b.sb([128, 2048], BF16, "qkbf")
    qkT = [b.sb([128, 16, 128], BF16, f"qkT{i}") for i in range(2)]
    vaug = [b.sb([128, 8, 129], BF16, f"vaug{i}") for i in range(2)]
    for i in range(2):
        b.op("pool", lambda e, i=i: e.memset(vaug[i][:], 1.0), writes=[vaug[i]])

    for a in range(NS):
        xt = xts[a % 2]
        b.op("sp", lambda e, a=a, xt=xt: e.dma_start(out=xt[:], in_=io["x"][a * 128:(a + 1) * 128, :]),
             writes=[xt], dma=f"ld_x{a % 2}")
        rmsnorm_transpose(b, xt, gmix, hbf, hT, pT[0], ss, junk)
        for n in range(6):
            py = pY[n % 4]
            for kc in range(8):
                b.op("pe", lambda e, n=n, kc=kc, py=py: e.matmul(py[:], lhsT=hT[:, kc, :],
                                                                   rhs=w[:, kc, n * 512:(n + 1) * 512],
                                                                   start=(kc == 0), stop=(kc == 7)),
                     reads=[hT, w], writes=[py])
            if n < 4:
                b.op("act", lambda e, n=n, py=py: e.activation(out=stage[:, n * 512:(n + 1) * 512], in_=py[:], func=AF.Copy),
                     reads=[py], writes=[stage])
                b.op("act", lambda e, n=n, py=py: e.activation(out=sq[:, n * 512:(n + 1) * 512], in_=py[:], func=AF.Square),
                     reads=[py], writes=[sq])
            else:
                va = vaug[a % 2]
                b.op("dve", lambda e, n=n, py=py, va=va: e.tensor_copy(
                    out=va[:, (n - 4) * 4:(n - 4) * 4 + 4, 0:128], in_=py[:].rearrange("p (h d) -> p h d", d=128)),
                    reads=[py], writes=[va])
        va = vaug[a % 2]
        b.store("sp", lambda e, a=a, va=va: e.dma_start(out=io["V"][a], in_=va[:].rearrange("p h d -> p (h d)")),
                reads=[va], dma=f"st_v{a % 2}")
        headnorm_rope(b, stage, sq, ssq, 32, gqk, cos[:, a, :], sin[:, a, :], qkbf, tmp)
        qt = qkT[a % 2]
        for i in range(16):
            pt = pT[1] if i < 8 else pT[0]
            b.op("pe", lambda e, i=i, pt=pt: e.transpose(out=pt[:, (i % 8) * 128:(i % 8 + 1) * 128],
                                                         in_=qkbf[:, i * 128:(i + 1) * 128], identity=idt[:]),
                 reads=[qkbf, idt], writes=[pt])
            if i % 8 == 7:
                b.op("dve", lambda e, i=i, pt=pt, qt=qt: e.tensor_copy(
                    out=qt[:, (i // 8) * 8:(i // 8) * 8 + 8, :].rearrange("p a b -> p (a b)"), in_=pt[:]),
                    reads=[pt], writes=[qt])
        b.store("sp", lambda e, a=a, qt=qt: e.dma_start(out=io["QT"][a], in_=qt[:, 0:8, :].rearrange("p a b -> p (a b)")),
                reads=[qt], dma=f"st_q{a % 2}")
        b.store("sp", lambda e, a=a, qt=qt: e.dma_start(out=io["KT"][a], in_=qt[:, 8:16, :].rearrange("p a b -> p (a b)")),
                reads=[qt], dma=f"st_k{a % 2}")


def phase_diff_attn(b, io, o_tok):
    LAM_INIT = 0.2
    qT = b.sb([128, NS, 1024], BF16, "qT")
    for a in range(NS):
        b.op("sp", lambda e, a=a: e.dma_start(out=qT[:, a, :], in_=io["QT"][a]), writes=[qT], dma="ld_qT")
    maskT = b.sb([128, 512], BF16, "maskT")
    b.op("sp", lambda e: e.dma_start(out=maskT[:], in_=io["maskT"][:]), writes=[maskT], dma="ld_mask")
    lam = b.sb([128, 256], F32, "lam")
    b.op("sp", lambda e: e.dma_start(out=lam[:], in_=io["lam"][:]), writes=[lam], dma="ld_lam")
    gsub = b.sb([128, 128], F32, "gsub")
    b.op("sp", lambda e: e.dma_start(out=gsub[:], in_=io["gsub"][:]), writes=[gsub], dma="ld_gsub")
    b.op("pool", lambda e: e.tensor_scalar(out=gsub[:], in0=gsub[:], scalar1=1.0 - LAM_INIT, scalar2=None,
                                           op0=ALU.mult), reads=[gsub], writes=[gsub])
    lprod = b.sb([128, 128], F32, "lprod")
    l2 = b.sb([128, 2], F32, "l2")
    neglam = b.sb([128, 1], F32, "neglam")
    l4 = lam[:].rearrange("p (a b d) -> p a b d", a=2, b=2)
    b.op("dve", lambda e: e.tensor_tensor(out=lprod[:].rearrange("p (a d) -> p a d", a=2), in0=l4[:, :, 0, :],
                                          in1=l4[:, :, 1, :], op=ALU.mult), reads=[lam], writes=[lprod])
    b.op("dve", lambda e: e.tensor_reduce(out=l2[:], in_=lprod[:].rearrange("p (a d) -> p a d", a=2), axis=AX.X,
                                          op=ALU.add), reads=[lprod], writes=[l2])
    b.op("act", lambda e: e.activation(out=l2[:], in_=l2[:], func=AF.Exp), reads=[l2], writes=[l2])
    b.op("dve", lambda e: e.tensor_tensor(out=neglam[:], in0=l2[:, 1:2], in1=l2[:, 0:1], op=ALU.subtract),
         reads=[l2], writes=[neglam])
    b.op("dve", lambda e: e.tensor_scalar(out=neglam[:], in0=neglam[:], scalar1=-LAM_INIT, scalar2=None, op0=ALU.add),
         reads=[neglam], writes=[neglam])

    ktb = [b.sb([128, 8192], BF16, f"ktb{i}") for i in range(2)]
    vb = [b.sb([128, 64, 129], BF16, f"vb{i}") for i in range(2)]
    pS = [[b.banks[c * 2 + i] for i in range(2)] for c in range(2)]
    pA = [[b.banks[4 + c * 2 + i] for i in range(2)] for c in range(2)]
    pT = [[b.sb([128, 512], BF16, f"pTs{c}{i}") for i in range(2)] for c in range(2)]
    rec = [b.sb([128, 2], F32, f"rec{i}") for i in range(2)]
    o32 = [b.sb([128, 128], F32, f"o32{i}") for i in range(2)]
    oj = b.sb([128, 128], BF16, "ojunk")
    ss1 = [b.sb([128, 1], F32, f"ss1{i}") for i in range(2)]

    units = [(h, a, blk) for h in range(8) for a in range(NS) for blk in range(a + 1)]

    def load_head(h):
        kb, vv = ktb[h % 2], vb[h % 2]
        for part in range(4):
            b.op("sp", lambda e, h=h, kb=kb, part=part: e.dma_start(
                out=kb[:, part * 2048:(part + 1) * 2048], in_=io["KTall"][h][:, part * 2048:(part + 1) * 2048]),
                writes=[kb], dma=f"ld_kt{h % 2}")
            b.op("sp", lambda e, h=h, vv=vv, part=part: e.dma_start(
                out=vv[:, part * 16:(part + 1) * 16, :].rearrange("p a b -> p (a b)"),
                in_=io["Vall"][h][:, part * 16 * 129:(part + 1) * 16 * 129]),
                writes=[vv], dma=f"ld_v{h % 2}")

    def qk(u, n):
        h, a, blk = u
        kb = ktb[h % 2]
        for c in range(2):
            ps = pS[c][n % 2]
            for i in range(4):
                kt = 4 * blk + i
                b.op("pe", lambda e, c=c, i=i, kt=kt, ps=ps, kb=kb, a=a, h=h: e.matmul(
                    ps[:, i * 128:(i + 1) * 128], lhsT=kb[64 * c:64 * c + 64, kt * 128:(kt + 1) * 128],
                    rhs=qT[64 * c:64 * c + 64, a, h * 128:(h + 1) * 128], start=True, stop=True),
                    reads=[kb, qT], writes=[ps])

    def softmax_pv(u, n):
        h, a, blk = u
        vv = vb[h % 2]
        for c in range(2):
            ps = pS[c][n % 2]
            pt = pT[c][n % 2]
            acc = pA[c][a % 2]
            b.op("act", lambda e, ps=ps, pt=pt: e.activation(out=pt[:], in_=ps[:], func=AF.Exp),
                 reads=[ps], writes=[pt])
            if blk == a:
                b.op("pool", lambda e, pt=pt: e.tensor_tensor(out=pt[:], in0=pt[:], in1=maskT[:], op=ALU.mult),
                     reads=[pt, maskT], writes=[pt])
            for i in range(4):
                kt = 4 * blk + i
                b.op("pe", lambda e, i=i, kt=kt, pt=pt, acc=acc, vv=vv, blk=blk, a=a: e.matmul(
                    acc[:, 0:129], lhsT=pt[:, i * 128:(i + 1) * 128], rhs=vv[:, kt, :],
                    start=(blk == 0 and i == 0), stop=(blk == a and i == 3)),
                    reads=[pt, vv], writes=[acc])
        if blk == a:
            evac(h, a)

    def evac(h, a):
        k = a % 2
        a0, a1 = pA[0][k], pA[1][k]
        r, o, s = rec[k], o32[k], ss1[k]
        b.op("dve", lambda e: e.reciprocal(out=r[:, 0:1], in_=a0[:, 128:129]), reads=[a0], writes=[r])
        b.op("dve", lambda e: e.reciprocal(out=r[:, 1:2], in_=a1[:, 128:129]), reads=[a1], writes=[r])
        b.op("dve", lambda e: e.tensor_tensor(out=r[:, 1:2], in0=r[:, 1:2], in1=neglam[:], op=ALU.mult),
             reads=[r, neglam], writes=[r])
        b.op("dve", lambda e: e.tensor_scalar(out=o[:], in0=a0[:, 0:128], scalar1=r[:, 0:1], scalar2=None, op0=ALU.mult),
             reads=[a0, r], writes=[o])
        b.op("dve", lambda e: e.scalar_tensor_tensor(out=o[:], in0=a1[:, 0:128], scalar=r[:, 1:2], in1=o[:],
                                                     op0=ALU.mult, op1=ALU.add), reads=[a1, r, o], writes=[o])
        b.op("act", lambda e: e.activation(out=oj[:], in_=o[:], func=AF.Square, accum_out=s[:, 0:1]),
             reads=[o], writes=[oj, s])
        rstd_from_ss(b, s, 1, 1.0 / 128)
        ot = o_tok[a]
        b.op("dve", lambda e: e.scalar_tensor_tensor(out=ot[:, h * 128:(h + 1) * 128], in0=o[:], scalar=s[:, 0:1],
                                                     in1=gsub[:], op0=ALU.mult, op1=ALU.mult),
             reads=[o, s, gsub], writes=[ot])

    load_head(0)
    qk(units[0], 0)
    for n, u in enumerate(units):
        if u[1] == 0 and u[2] == 0 and u[0] + 1 < 8:
            load_head(u[0] + 1)
        if n + 1 < len(units):
            qk(units[n + 1], n + 1)
        softmax_pv(u, n)


def load_w_bf16(b, wt, src, nk, ncols, key, colchunk=1024):
    for c0 in range(0, ncols, colchunk):
        for kc in range(nk):
            c1 = min(ncols, c0 + colchunk)
            b.op("pool", lambda e, kc=kc, c0=c0, c1=c1: e.dma_start(
                out=wt[:, kc, c0:c1], in_=src[kc * 128:(kc + 1) * 128, c0:c1]), writes=[wt], dma=key)


def phase_post_attn(b, io, o_tok, x_res, h2T, wout_ap, gmlp_ap, xsrc, xdeps=None):
    idt = b.ident()
    m = b.mark()
    wout = b.sb([128, 8, 1024], BF16, "wout")
    load_w_bf16(b, wout, wout_ap, 8, 1024, "ld_wout")
    gm = b.sb([128, D], F32, "gmlp")
    b.op("sp", lambda e: e.dma_start(out=gm[:], in_=gmlp_ap), writes=[gm], dma="ld_gmlp")
    oT = [b.sb([128, 8, 128], BF16, f"oT{i}") for i in range(2)]
    junk = b.sb([128, D], BF16, "junk2")
    ss = [b.sb([128, 1], F32, f"ss2{i}") for i in range(2)]
    hbf = [b.sb([128, D], BF16, f"hbf2{i}") for i in range(2)]
    for a in range(NS):
        xr = x_res[a]
        b.op("sp", lambda e, a=a, xr=xr: e.dma_start(out=xr[:], in_=xsrc[a * 128:(a + 1) * 128, :]),
             reads=([xdeps[a]] if xdeps else []), writes=[xr], dma="ld_xres")
        pt = b.banks16[a % 2]
        ot = oT[a % 2]
        for kc in range(8):
            b.op("pe", lambda e, kc=kc, pt=pt, a=a: e.transpose(out=pt[:, kc * 128:(kc + 1) * 128],
                                                                 in_=o_tok[a][:, kc * 128:(kc + 1) * 128], identity=idt[:]),
                 reads=[o_tok[a], idt], writes=[pt])
        b.op("act", lambda e, pt=pt, ot=ot: e.activation(out=ot[:].rearrange("p a b -> p (a b)"), in_=pt[:], func=AF.Copy),
             reads=[pt], writes=[ot])
        for n in range(2):
            py = b.banks[2 + (2 * a + n) % 4]
            for kc in range(8):
                b.op("pe", lambda e, kc=kc, n=n, py=py, ot=ot: e.matmul(py[:], lhsT=ot[:, kc, :],
                                                                         rhs=wout[:, kc, n * 512:(n + 1) * 512],
                                                                         start=(kc == 0), stop=(kc == 7)),
                     reads=[ot, wout], writes=[py])
            b.op("dve", lambda e, n=n, py=py, xr=xr: e.tensor_tensor(out=xr[:, n * 512:(n + 1) * 512],
                                                                     in0=xr[:, n * 512:(n + 1) * 512], in1=py[:], op=ALU.add),
                 reads=[xr, py], writes=[xr])
        rms_to_hT(b, xr, gm, hbf[a % 2], h2T, a, b.banks16[6 + a % 2], ss[a % 2], junk)
    return m


def rms_to_hT(b, xr, gm, hbf, h2T, a, pt, ss, junk):
    idt = b.ident()
    b.op("act", lambda e: e.activation(out=junk[:], in_=xr[:], func=AF.Square, accum_out=ss[:, 0:1]),
         reads=[xr], writes=[junk, ss])
    rstd_from_ss(b, ss, 1, 1.0 / D)
    b.op("dve", lambda e: e.scalar_tensor_tensor(out=hbf[:], in0=xr[:], scalar=ss[:, 0:1], in1=gm[:],
                                                 op0=ALU.mult, op1=ALU.mult), reads=[xr, ss, gm], writes=[hbf])
    for kc in range(8):
        b.op("pe", lambda e, kc=kc: e.transpose(out=pt[:, kc * 128:(kc + 1) * 128],
                                                in_=hbf[:, kc * 128:(kc + 1) * 128], identity=idt[:]),
             reads=[hbf, idt], writes=[pt])
    b.op("act", lambda e: e.activation(out=h2T[:, :, a * 128:(a + 1) * 128],
                                       in_=pt[:].rearrange("p (k t) -> p k t", k=8), func=AF.Copy),
         reads=[pt], writes=[h2T])


def phase_mlp(b, x_res, h2T, w1_ap, w2_ap, hook=None):
    NFC = 8
    w1c = [b.sb([128, 8, 512], BF16, f"w1c{i}") for i in range(2)]
    w2c = [b.sb([128, 4, 1024], BF16, f"w2c{i}") for i in range(2)]
    rbuf = [b.sb([128, 512], F32, f"rbuf{i}") for i in range(2)]
    uT = [b.sb([128, 4, 512], BF16, f"uT{i}") for i in range(2)]
    pu = [b.banks[0], b.banks[1]]
    po = [b.banks[2 + i] for i in range(4)]

    def load_chunk(fc):
        w1, w2 = w1c[fc % 2], w2c[fc % 2]
        for kc in range(8):
            b.op("pool", lambda e, kc=kc, fc=fc, w1=w1: e.dma_start(
                out=w1[:, kc, :], in_=w1_ap[kc * 128:(kc + 1) * 128, fc * 512:(fc + 1) * 512]),
                writes=[w1], dma=f"ld_w1{fc % 2}")
        for ft in range(4):
            b.op("pool", lambda e, ft=ft, fc=fc, w2=w2: e.dma_start(
                out=w2[:, ft, :], in_=w2_ap[fc * 512 + ft * 128:fc * 512 + (ft + 1) * 128, :]),
                writes=[w2], dma=f"ld_w2{fc % 2}")

    steps = [(fc, tg) for fc in range(NFC) for tg in range(4)]
    cnt = {"u": 0, "o": 0}

    def stage_u(fc, tg):
        w1 = w1c[fc % 2]
        ut = uT[(fc * 4 + tg) % 2]
        for ft in range(4):
            p = pu[cnt["u"] % 2]
            r = rbuf[cnt["u"] % 2]
            cnt["u"] += 1
            for kc in range(8):
                b.op("pe", lambda e, kc=kc, ft=ft, p=p, w1=w1, tg=tg: e.matmul(
                    p[:], lhsT=w1[:, kc, ft * 128:(ft + 1) * 128], rhs=h2T[:, kc, tg * 512:(tg + 1) * 512],
                    start=(kc == 0), stop=(kc == 7)), reads=[w1, h2T], writes=[p])
            b.op("act", lambda e, p=p, r=r: e.activation(out=r[:], in_=p[:], func=AF.Relu), reads=[p], writes=[r])
            b.op("pool", lambda e, r=r, ut=ut, ft=ft: e.tensor_tensor(out=ut[:, ft, :], in0=r[:], in1=r[:], op=ALU.mult),
                 reads=[r], writes=[ut])

    def stage_o(fc, tg):
        w2 = w2c[fc % 2]
        ut = uT[(fc * 4 + tg) % 2]
        for tt in range(4):
            xr = x_res[tg * 4 + tt]
            for ch in range(2):
                p = po[cnt["o"] % 4]
                cnt["o"] += 1
                for ft in range(4):
                    b.op("pe", lambda e, ft=ft, tt=tt, ch=ch, p=p, ut=ut, w2=w2: e.matmul(
                        p[:], lhsT=ut[:, ft, tt * 128:(tt + 1) * 128], rhs=w2[:, ft, ch * 512:(ch + 1) * 512],
                        start=(ft == 0), stop=(ft == 3)), reads=[ut, w2], writes=[p])
                b.op("dve", lambda e, ch=ch, p=p, xr=xr: e.tensor_tensor(
                    out=xr[:, ch * 512:(ch + 1) * 512], in0=xr[:, ch * 512:(ch + 1) * 512], in1=p[:], op=ALU.add),
                    reads=[xr, p], writes=[xr])

    load_chunk(0)
    stage_u(*steps[0])
    for i, (fc, tg) in enumerate(steps):
        if tg == 0 and fc + 1 < NFC:
            load_chunk(fc + 1)
        if hook is not None and fc == 1 and tg == 1:
            hook()
        if i + 1 < len(steps):
            stage_u(*steps[i + 1])
        stage_o(fc, tg)


IDX_SCALE = (8 ** -0.5) * (64 ** -0.5)


def phase_dsa_proj(b, io, x_res, cos, sin):
    idt = b.ident()
    gmix = b.sb([128, D], F32, "gmix1")
    b.op("sp", lambda e: e.dma_start(out=gmix[:], in_=io["gmix1"][:]), writes=[gmix], dma="ld_gmix1")
    gqk = b.sb([128, 2048], F32, "gqk2")
    b.op("sp", lambda e: e.dma_start(out=gqk[:], in_=io["gqk2"][:]), writes=[gqk], dma="ld_gqk2")
    b.op("pool", lambda e: e.tensor_scalar(out=gqk[:, 0:1024], in0=gqk[:, 0:1024], scalar1=0.125, scalar2=None,
                                           op0=ALU.mult), reads=[gqk], writes=[gqk])
    gcq = b.sb([128, 256], F32, "gcq")
    b.op("sp", lambda e: e.dma_start(out=gcq[:], in_=io["gcq"][:]), writes=[gcq], dma="ld_gcq")
    w = b.sb([128, 8, 2376], BF16, "w_in2")
    load_w_bf16(b, w, io["w_in2"], 8, 2376, "ld_w2in", colchunk=792)
    wuq = b.sb([128, 2, 1024], BF16, "wuq")
    load_w_bf16(b, wuq, io["w_uq"], 2, 1024, "ld_wuq")
    wuqi = b.sb([128, 2, 512], BF16, "wuqi")
    load_w_bf16(b, wuqi, io["w_uqi"], 2, 512, "ld_wuqi")

    junk = b.sb([128, D], BF16, "junk3")
    ss = b.sb([128, 1], F32, "ss3")
    hbf = b.sb([128, D], BF16, "hbf3")
    hT = b.sb([128, 8, 128], BF16, "hT3")
    stage = b.sb([128, 2048], F32, "stage3")
    sq = b.sb([128, 2048], F32, "sq3")
    ssq = b.sb([128, 32], F32, "ssq3")
    tmp = b.sb([128, 4 * 32 * 8], F32, "ropetmp3")
    qkbf = b.sb([128, 2048], BF16, "qkbf3")
    qkT = [b.sb([128, 16, 128], BF16, f"qkT3{i}") for i in range(2)]
    vaug = [b.sb([128, 16, 65], BF16, f"vaug3{i}") for i in range(2)]
    for i in range(2):
        b.op("pool", lambda e, i=i: e.memset(vaug[i][:], 1.0), writes=[vaug[i]])
    cqs = b.sb([128, 256], F32, "cqs")
    ssc = b.sb([128, 1], F32, "ssc")
    cqbf = b.sb([128, 256], BF16, "cqbf")
    cqT = b.sb([128, 2, 128], BF16, "cqT")
    kis = b.sb([128, 64], F32, "kis")
    ksq = b.sb([128, 64], F32, "ksq")
    kss = b.sb([128, 1], F32, "kss")
    kibf = b.sb([128, 128], BF16, "kibf")
    kiT = [b.sb([128, 128], BF16, f"kiT{i}") for i in range(2)]
    wi = b.sb([128, 8], F32, "wi")
    sgn = [b.sb([128, 8], F32, f"sgn{i}") for i in range(2)]
    aw = b.sb([128, 8], F32, "aw")
    qis = b.sb([128, 512], F32, "qis")
    qibf = b.sb([128, 512], BF16, "qibf")
    qiT = [b.sb([128, 4, 128], BF16, f"qiT{i}") for i in range(2)]
    nbank = [0]

    def bank():
        nbank[0] += 1
        return b.banks[2 + nbank[0] % 4]

    def proj(py, lhs, nk, rhs_fn, ncol):
        for kc in range(nk):
            b.op("pe", lambda e, kc=kc: e.matmul(py[:, 0:ncol], lhsT=lhs[:, kc, :], rhs=rhs_fn(kc),
                                                 start=(kc == 0), stop=(kc == nk - 1)), reads=[lhs, w, wuq, wuqi], writes=[py])

    for a in range(NS):
        xr = x_res[a]
        rmsnorm_transpose(b, xr, gmix, hbf, hT, b.banks16[0], ss, junk)
        py = bank()
        proj(py, hT, 8, lambda kc: w[:, kc, 0:256], 256)
        b.op("act", lambda e, py=py: e.activation(out=cqs[:], in_=py[:, 0:256], func=AF.Copy), reads=[py], writes=[cqs])
        b.op("act", lambda e, py=py: e.activation(out=junk[:, 0:256], in_=py[:, 0:256], func=AF.Square, accum_out=ssc[:, 0:1]),
             reads=[py], writes=[junk, ssc])
        rstd_from_ss(b, ssc, 1, 1.0 / 256)
        b.op("dve", lambda e: e.scalar_tensor_tensor(out=cqbf[:], in0=cqs[:], scalar=ssc[:, 0:1], in1=gcq[:],
                                                     op0=ALU.mult, op1=ALU.mult), reads=[cqs, ssc, gcq], writes=[cqbf])
        p6 = b.banks16[6]
        for kc in range(2):
            b.op("pe", lambda e, kc=kc: e.transpose(out=p6[:, kc * 128:(kc + 1) * 128], in_=cqbf[:, kc * 128:(kc + 1) * 128],
                                                    identity=idt[:]), reads=[cqbf, idt], writes=[p6])
        b.op("dve", lambda e: e.tensor_copy(out=cqT[:].rearrange("p a b -> p (a b)"), in_=p6[:, 0:256]),
             reads=[p6], writes=[cqT])
        for n in range(2):
            py = bank()
            proj(py, hT, 8, lambda kc, n=n: w[:, kc, 256 + n * 512:256 + (n + 1) * 512], 512)
            b.op("act", lambda e, py=py, n=n: e.activation(out=stage[:, 1024 + n * 512:1024 + (n + 1) * 512], in_=py[:], func=AF.Copy),
                 reads=[py], writes=[stage])
            b.op("act", lambda e, py=py, n=n: e.activation(out=sq[:, 1024 + n * 512:1024 + (n + 1) * 512], in_=py[:], func=AF.Square),
                 reads=[py], writes=[sq])
        va = vaug[a % 2]
        for n in range(2):
            py = bank()
            proj(py, hT, 8, lambda kc, n=n: w[:, kc, 1280 + n * 512:1280 + (n + 1) * 512], 512)
            b.op("dve", lambda e, py=py, n=n, va=va: e.tensor_copy(out=va[:, n * 8:(n + 1) * 8, 0:64],
                                                                   in_=py[:].rearrange("p (h d) -> p h d", d=64)),
                 reads=[py], writes=[va])
        b.store("sp", lambda e, a=a, va=va: e.dma_start(out=io["V2"][a], in_=va[:].rearrange("p h d -> p (h d)")),
                reads=[va], dma=f"st_v2{a % 2}")
        py = bank()
        proj(py, hT, 8, lambda kc: w[:, kc, 2304:2376], 72)
        b.op("act", lambda e, py=py: e.activation(out=kis[:], in_=py[:, 0:64], func=AF.Copy), reads=[py], writes=[kis])
        b.op("act", lambda e, py=py: e.activation(out=ksq[:], in_=py[:, 0:64], func=AF.Square), reads=[py], writes=[ksq])
        b.op("dve", lambda e, py=py: e.tensor_copy(out=wi[:], in_=py[:, 64:72]), reads=[py], writes=[wi])
        for n in range(2):
            py = bank()
            proj(py, cqT, 2, lambda kc, n=n: wuq[:, kc, n * 512:(n + 1) * 512], 512)
            b.op("act", lambda e, py=py, n=n: e.activation(out=stage[:, n * 512:(n + 1) * 512], in_=py[:], func=AF.Copy),
                 reads=[py], writes=[stage])
            b.op("act", lambda e, py=py, n=n: e.activation(out=sq[:, n * 512:(n + 1) * 512], in_=py[:], func=AF.Square),
                 reads=[py], writes=[sq])
        py = bank()
        proj(py, cqT, 2, lambda kc: wuqi[:, kc, :], 512)
        b.op("act", lambda e, py=py: e.activation(out=qis[:], in_=py[:], func=AF.Copy), reads=[py], writes=[qis])
        headnorm_rope(b, stage, sq, ssq, 32, gqk, cos[:, a, :], sin[:, a, :], qkbf, tmp)
        qt = qkT[a % 2]
        for i in range(16):
            pt = b.banks16[1] if i < 8 else b.banks16[0]
            b.op("pe", lambda e, i=i, pt=pt: e.transpose(out=pt[:, (i % 8) * 128:(i % 8 + 1) * 128],
                                                         in_=qkbf[:, i * 128:(i + 1) * 128], identity=idt[:]),
                 reads=[qkbf, idt], writes=[pt])
            if i % 8 == 7:
                b.op("dve", lambda e, i=i, pt=pt, qt=qt: e.tensor_copy(
                    out=qt[:, (i // 8) * 8:(i // 8) * 8 + 8, :].rearrange("p a b -> p (a b)"), in_=pt[:]),
                    reads=[pt], writes=[qt])
        b.store("sp", lambda e, a=a, qt=qt: e.dma_start(out=io["QT2"][a], in_=qt[:, 0:8, :].rearrange("p a b -> p (a b)")),
                reads=[qt], dma=f"st_q2{a % 2}")
        b.store("sp", lambda e, a=a, qt=qt: e.dma_start(out=io["KT2"][a], in_=qt[:, 8:16, :].rearrange("p a b -> p (a b)")),
                reads=[qt], dma=f"st_k2{a % 2}")
        ki_half = T(kibf.t[:, 0:64], kibf.d)
        headnorm_rope(b, kis, ksq, kss, 1, None, cos[:, a, :], sin[:, a, :], ki_half, tmp)
        b.op("pool", lambda e: e.tensor_copy(out=kibf[:, 64:128], in_=kibf[:, 0:64]), reads=[kibf], writes=[kibf])
        p7 = b.banks16[7]
        b.op("pe", lambda e: e.transpose(out=p7[:, 0:128], in_=kibf[:], identity=idt[:]), reads=[kibf, idt], writes=[p7])
        kt_ = kiT[a % 2]
        b.op("dve", lambda e, kt_=kt_: e.tensor_copy(out=kt_[:], in_=p7[:, 0:128]), reads=[p7], writes=[kt_])
        b.store("sp", lambda e, a=a, kt_=kt_: e.dma_start(out=io["KI"][a], in_=kt_[:]), reads=[kt_], dma=f"st_ki{a % 2}")
        sg = sgn[a % 2]
        b.op("act", lambda e, sg=sg: e.activation(out=sg[:], in_=wi[:], func=AF.Sign), reads=[wi], writes=[sg])
        b.op("dve", lambda e, sg=sg: e.scalar_tensor_tensor(out=aw[:], in0=wi[:], scalar=IDX_SCALE, in1=sg[:],
                                                           op0=ALU.mult, op1=ALU.mult), reads=[wi, sg], writes=[aw])
        b.store("sp", lambda e, a=a, sg=sg: e.dma_start(out=io["SG"][a], in_=sg[:]), reads=[sg], dma=f"st_sg{a % 2}")
        headnorm_rope(b, qis, None, aw, 8, None, cos[:, a, :], sin[:, a, :], qibf, tmp, norm=False)
        qi_ = qiT[a % 2]
        for i in range(4):
            b.op("pe", lambda e, i=i: e.transpose(out=p7[:, 256 + i * 128:256 + (i + 1) * 128],
                                                  in_=qibf[:, i * 128:(i + 1) * 128], identity=idt[:]),
                 reads=[qibf, idt], writes=[p7])
        b.op("dve", lambda e, qi_=qi_: e.tensor_copy(out=qi_[:].rearrange("p a b -> p (a b)"), in_=p7[:, 256:768]),
             reads=[p7], writes=[qi_])
        b.store("sp", lambda e, a=a, qi_=qi_: e.dma_start(out=io["QI"][a], in_=qi_[:].rearrange("p a b -> p (a b)")),
                reads=[qi_], dma=f"st_qi{a % 2}")


NIT = 22
TOPK = 256


def phase_dsa_attn(b, io, o_tok):
    idt = b.ident()
    kia = b.sb([128, 8192], BF16, "kiall")
    for part in range(4):
        b.op("sp", lambda e, part=part: e.dma_start(out=kia[:, part * 2048:(part + 1) * 2048],
                                                    in_=io["KIall"][:, part * 2048:(part + 1) * 2048]),
             writes=[kia], dma="ld_kia")
    negm = b.sb([128, 512], F32, "negm")
    b.op("sp", lambda e: e.dma_start(out=negm[:], in_=io["negmask"][:]), writes=[negm], dma="ld_negm")
    cW = b.sb([128, NIT], F32, "cW")
    for i in range(NIT):
        b.op("pool", lambda e, i=i: e.memset(cW[:, i:i + 1], 2.0 ** (-i)), writes=[cW])
    Ib = b.sb([128, 8192], F32, "Ibuf")
    Mq = b.sb([128, 8192], BF16, "Mq")
    MT = b.sb([128, 64, 128], BF16, "MT")
    ktp = [b.sb([128, 8192], BF16, f"ktp{i}") for i in range(2)]
    vp = [b.sb([128, 64, 130], BF16, f"vp{i}") for i in range(2)]
    qTa = [b.sb([128, 8, 128], BF16, f"qTa{i}") for i in range(2)]
    qiTa = [b.sb([128, 4, 128], BF16, f"qiTa{i}") for i in range(2)]
    sgn = [b.sb([128, 8], F32, f"sgna{i}") for i in range(2)]
    tb = [b.sb([128, 512], F32, f"tb{i}") for i in range(2)]
    pT = [b.sb([128, 512], BF16, f"pTd{i}") for i in range(2)]
    m1 = b.sb([128, 1], F32, "bm1")
    lo = b.sb([128, 1], F32, "blo")
    mid = b.sb([128, 1], F32, "bmid")
    cnt = b.sb([128, 1], F32, "bcnt")
    g = b.sb([128, 1], F32, "bg")
    W = b.sb([128, NIT], F32, "bW")
    rec = [b.sb([128, 1], F32, f"recd{i}") for i in range(2)]
    pS = [b.banks[0], b.banks[1]]
    pA = [b.banks[2], b.banks[3]]
    pI = [b.banks[4], b.banks[5]]
    pM = [b.banks16[6], b.banks16[7]]
    ctr = {"i": 0, "s": 0, "acc": 0, "kv": 0}

    def load_slot_small(a):
        b.op("sp", lambda e, a=a: e.dma_start(out=qTa[a % 2][:].rearrange("p a b -> p (a b)"), in_=io["QT2"][a]),
             writes=[qTa[a % 2]], dma=f"ld_qTa{a % 2}")
        b.op("sp", lambda e, a=a: e.dma_start(out=qiTa[a % 2][:].rearrange("p a b -> p (a b)"), in_=io["QI"][a]),
             writes=[qiTa[a % 2]], dma=f"ld_qiTa{a % 2}")
        b.op("sp", lambda e, a=a: e.dma_start(out=sgn[a % 2][:], in_=io["SG"][a]), writes=[sgn[a % 2]], dma=f"ld_sgn{a % 2}")

    def load_kv(a, hp):
        k = ctr["kv"] % 2
        ctr["kv"] += 1
        nv = 512 * (a + 1)
        nt = 4 * (a + 1)
        b.op("sp", lambda e, k=k, hp=hp, nv=nv: e.dma_start(out=ktp[k][:, 0:nv], in_=io["KT2all"][hp][:, 0:nv]),
             writes=[ktp[k]], dma=f"ld_ktp{k}")
        b.op("sp", lambda e, k=k, hp=hp, nt=nt: e.dma_start(out=vp[k][:, 0:nt, :].rearrange("p a b -> p (a b)"),
                                                            in_=io["V2all"][hp][:, 0:nt * 130]),
             writes=[vp[k]], dma=f"ld_vp{k}")
        return k

    def indexer(a):
        nb = a + 1
        nv = 512 * nb
        qi, sg = qiTa[a % 2], sgn[a % 2]
        for blk in range(nb):
            for head in range(8):
                hp, hh = head // 2, head % 2
                py = pI[ctr["i"] % 2]
                t = tb[ctr["i"] % 2]
                ctr["i"] += 1
                b.op("pe", lambda e, py=py, hp=hp, hh=hh, blk=blk, qi=qi: e.matmul(
                    py[:], lhsT=qi[64 * hh:64 * hh + 64, hp, :], rhs=kia[64 * hh:64 * hh + 64, blk * 512:(blk + 1) * 512],
                    start=True, stop=True), reads=[qi, kia], writes=[py])
                b.op("act", lambda e, py=py, t=t: e.activation(out=t[:], in_=py[:], func=AF.Relu), reads=[py], writes=[t])
                if head == 0:
                    b.op("dve", lambda e, t=t, blk=blk, sg=sg: e.tensor_scalar(
                        out=Ib[:, blk * 512:(blk + 1) * 512], in0=t[:], scalar1=sg[:, 0:1], scalar2=None, op0=ALU.mult),
                        reads=[t, sg], writes=[Ib])
                else:
                    b.op("dve", lambda e, t=t, blk=blk, sg=sg, head=head: e.scalar_tensor_tensor(
                        out=Ib[:, blk * 512:(blk + 1) * 512], in0=t[:], scalar=sg[:, head:head + 1],
                        in1=Ib[:, blk * 512:(blk + 1) * 512], op0=ALU.mult, op1=ALU.add),
                        reads=[t, sg, Ib], writes=[Ib])
        b.op("dve", lambda e: e.tensor_reduce(out=m1[:], in_=Ib[:, 0:nv], axis=AX.X, op=ALU.max, apply_absolute_value=True),
             reads=[Ib], writes=[m1])
        b.op("dve", lambda e: e.tensor_tensor(out=Ib[:, nv - 512:nv], in0=Ib[:, nv - 512:nv], in1=negm[:], op=ALU.add),
             reads=[Ib, negm], writes=[Ib])
        b.op("dve", lambda e: e.tensor_scalar(out=m1[:], in0=m1[:], scalar1=1.0, scalar2=None, op0=ALU.add),
             reads=[m1], writes=[m1])
        b.op("dve", lambda e: e.tensor_scalar(out=lo[:], in0=m1[:], scalar1=-1.0, scalar2=None, op0=ALU.mult),
             reads=[m1], writes=[lo])
        b.op("dve", lambda e: e.tensor_scalar(out=W[:], in0=cW[:], scalar1=m1[:, 0:1], scalar2=None, op0=ALU.mult),
             reads=[cW, m1], writes=[W])
        b.op("dve", lambda e: e.tensor_tensor(out=mid[:], in0=lo[:], in1=W[:, 0:1], op=ALU.add), reads=[lo, W], writes=[mid])
        for i in range(NIT):
            b.op("dve", lambda e: e.tensor_scalar(out=Mq[:, 0:nv], in0=Ib[:, 0:nv], scalar1=mid[:, 0:1], scalar2=0.0,
                                                  op0=ALU.is_ge, op1=ALU.add, accum_out=cnt[:, 0:1]),
                 reads=[Ib, mid], writes=[Mq, cnt])
            b.op("dve", lambda e, i=i: e.tensor_scalar(out=g[:], in0=cnt[:], scalar1=TOPK - 0.5, scalar2=W[:, i:i + 1],
                                                       op0=ALU.is_ge, op1=ALU.mult), reads=[cnt, W], writes=[g])
            b.op("dve", lambda e: e.tensor_tensor(out=lo[:], in0=lo[:], in1=g[:], op=ALU.add), reads=[lo, g], writes=[lo])
            if i + 1 < NIT:
                b.op("dve", lambda e, i=i: e.tensor_tensor(out=mid[:], in0=lo[:], in1=W[:, i + 1:i + 2], op=ALU.add),
                     reads=[lo, W], writes=[mid])
        b.op("dve", lambda e: e.tensor_scalar(out=Mq[:, 0:nv], in0=Ib[:, 0:nv], scalar1=lo[:, 0:1], scalar2=None,
                                              op0=ALU.is_ge), reads=[Ib, lo], writes=[Mq])
        nt = 4 * nb
        for kt in range(nt):
            pm = pM[(kt // 8) % 2]
            b.op("pe", lambda e, kt=kt, pm=pm: e.transpose(out=pm[:, (kt % 8) * 128:(kt % 8 + 1) * 128],
                                                           in_=Mq[:, kt * 128:(kt + 1) * 128], identity=idt[:]),
                 reads=[Mq, idt], writes=[pm])
            if kt % 8 == 7 or kt == nt - 1:
                k0 = (kt // 8) * 8
                n = kt - k0 + 1
                b.op("act", lambda e, pm=pm, k0=k0, n=n: e.activation(
                    out=MT[:, k0:k0 + n, :].rearrange("p a b -> p (a b)"), in_=pm[:, 0:n * 128], func=AF.Copy),
                    reads=[pm], writes=[MT])

    def qk(u, n, kbuf):
        a, hp, hh, blk = u
        ps = pS[n % 2]
        qa = qTa[a % 2]
        for i in range(4):
            kt = 4 * blk + i
            b.op("pe", lambda e, i=i, kt=kt, ps=ps, qa=qa, hh=hh, hp=hp, kbuf=kbuf: e.matmul(
                ps[:, i * 128:(i + 1) * 128], lhsT=ktp[kbuf][64 * hh:64 * hh + 64, kt * 128:(kt + 1) * 128],
                rhs=qa[64 * hh:64 * hh + 64, hp, :], start=True, stop=True), reads=[ktp[kbuf], qa], writes=[ps])

    def softmax_pv(u, n, kbuf):
        a, hp, hh, blk = u
        ps, pt = pS[n % 2], pT[n % 2]
        if blk == 0:
            ctr["acc"] += 1
        acc = pA[ctr["acc"] % 2]
        b.op("act", lambda e: e.activation(out=pt[:], in_=ps[:], func=AF.Exp), reads=[ps], writes=[pt])
        b.op("dve", lambda e: e.tensor_tensor(out=pt[:], in0=pt[:], in1=MT[:, 4 * blk:4 * blk + 4, :].rearrange("p a b -> p (a b)"),
                                              op=ALU.mult), reads=[pt, MT], writes=[pt])
        for i in range(4):
            kt = 4 * blk + i
            b.op("pe", lambda e, i=i, kt=kt: e.matmul(acc[:, 0:65], lhsT=pt[:, i * 128:(i + 1) * 128],
                                                      rhs=vp[kbuf][:, kt, hh * 65:(hh + 1) * 65],
                                                      start=(blk == 0 and i == 0), stop=(blk == a and i == 3)),
                 reads=[pt, vp[kbuf]], writes=[acc])
        if blk == a:
            head = 2 * hp + hh
            r = rec[ctr["acc"] % 2]
            b.op("dve", lambda e: e.reciprocal(out=r[:], in_=acc[:, 64:65]), reads=[acc], writes=[r])
            b.op("dve", lambda e: e.tensor_scalar(out=o_tok[a][:, head * 64:(head + 1) * 64], in0=acc[:, 0:64],
                                                  scalar1=r[:, 0:1], scalar2=None, op0=ALU.mult),
                 reads=[acc, r], writes=[o_tok[a]])

    load_slot_small(0)
    for a in range(NS):
        if a + 1 < NS:
            load_slot_small(a + 1)
        kb_next = load_kv(a, 0)
        indexer(a)
        units = [(a, hp, hh, blk) for hp in range(8) for hh in range(2) for blk in range(a + 1)]
        kbufs = {}
        kbufs[0] = kb_next
        qk(units[0], 0, kbufs[0])
        for n, u in enumerate(units):
            _, hp, hh, blk = u
            if hh == 0 and blk == 0 and hp + 1 < 8:
                kbufs[hp + 1] = load_kv(a, hp + 1)
            if n + 1 < len(units):
                qk(units[n + 1], n + 1, kbufs[units[n + 1][1]])
            softmax_pv(u, n, kbufs[hp])


GROUPS = [[0, 1, 2, 3], [4, 5, 6, 7]]
_RANK = {}


class Gather:
    def __init__(self, b, name, nblk, cols, zt, zero_eng="act"):
        self.b, self.name, self.nblk, self.cols = b, name, nblk, cols
        nc = b.nc
        self.xb = nc.dram_tensor(name + "_xb", [nblk, 512, cols], BF16).ap()
        self.yb = nc.dram_tensor(name + "_yb", [nblk, 512, cols], BF16).ap()
        self.xo = nc.dram_tensor(name + "_xo", [nblk, 128, cols], BF16).ap()
        self.od = [T(self.xo[h]) for h in range(nblk)]
        self.xd = [T(self.xb[h]) for h in range(nblk)]
        self.yd = [T(self.yb[h]) for h in range(nblk)]
        for h in range(nblk):
            for r in range(4):
                for c0 in range(0, cols, 2048):
                    b.op(zero_eng, lambda e, h=h, r=r, c0=c0: e.dma_start(
                        out=self.xb[h, r * 128:(r + 1) * 128, c0:c0 + 2048], in_=zt[:, 0:2048]),
                        reads=[zt], writes=[self.xd[h]], dma="zero_" + name)

    def put(self, h, c0, c1, src_ap, reads, key):
        self.b.op("sp", lambda e: e.dma_start(out=self.xo[h, :, c0:c1], in_=src_ap), reads=reads,
                  writes=[self.od[h]], dma=key)

    def place(self, h, eng):
        def fn(e):
            if eng not in _RANK:
                _RANK[eng] = e.partition_id() % 4
            r = _RANK[eng]
            return e.dma_start(out=self.xb[h, bass.ds(r * 128, 128), :], in_=self.xo[h])
        self.b.op(eng, fn, reads=[self.od[h]], writes=[self.xd[h]], dma="place_" + self.name)

    def reduce(self, h):
        b = self.b
        b.op("pool", lambda e: e.collective_compute("AllReduce", ALU.add, replica_groups=GROUPS,
                                                    ins=[self.xb[h]], outs=[self.yb[h]]),
             reads=[self.xd[h]], writes=[self.yd[h]], dma=f"cc_{self.name}{h}", sem_inc=1)


def f_diff_proj(b, io, qT_res, G0, cos, sin, wstage=None):
    idt = b.ident()
    gmix = b.sb([128, D], F32, "gmix")
    b.op("sp", lambda e: e.dma_start(out=gmix[:], in_=io["gmix"][:]), writes=[gmix], dma="ld_gmix")
    gqk = b.sb([128, 2048], F32, "gqk")
    b.op("sp", lambda e: e.dma_start(out=gqk[:], in_=io["gqk"][:]), writes=[gqk], dma="ld_gqk")
    b.op("pool", lambda e: e.tensor_scalar(out=gqk[:, 0:1024], in0=gqk[:, 0:1024], scalar1=0.125, scalar2=None,
                                           op0=ALU.mult), reads=[gqk], writes=[gqk])
    xts = [b.sb([128, D], F32, f"xt{i}") for i in range(2)]

    def load_x(a):
        xt = xts[a % 2]
        b.op("sp", lambda e: e.dma_start(out=xt[:], in_=io["x"][a * 128:(a + 1) * 128, :]), writes=[xt], dma=f"ld_x{a % 2}")
    load_x(0)
    w = b.sb([128, 8, 3072], BF16, "w_in")
    if wstage is None:
        load_w_bf16(b, w, io["w_in"], 8, 3072, "ld_w")
    else:
        for n in range(6):
            stg = wstage[n % 2]
            b.op("sp", lambda e, n=n, stg=stg: e.dma_start(
                out=stg[:], in_=io["w_in"][:, n * 512:(n + 1) * 512].rearrange("(k p) c -> p k c", p=128)),
                writes=[stg], dma=f"ld_wst{n % 2}")
            b.op("act" if n % 2 == 0 else "dve",
                 (lambda e, n=n, stg=stg: e.activation(out=w[:, :, n * 512:(n + 1) * 512], in_=stg[:], func=AF.Copy)) if n % 2 == 0
                 else (lambda e, n=n, stg=stg: e.tensor_copy(out=w[:, :, n * 512:(n + 1) * 512], in_=stg[:])),
                 reads=[stg], writes=[w])
    junk_ = [b.sb([128, D], BF16, f"junk{i}") for i in range(2)]
    ss_ = [b.sb([128, 1], F32, f"ss{i}") for i in range(2)]
    hbf_ = [b.sb([128, D], BF16, f"hbf{i}") for i in range(2)]
    hT_ = [b.sb([128, 8, 128], BF16, f"hT{i}") for i in range(2)]
    pT = [b.banks16[0], b.banks16[1]]
    pY = [b.banks[2 + i] for i in range(4)]
    stage_ = [b.sb([128, 2048], F32, f"stage{i}") for i in range(2)]
    sq_ = [b.sb([128, 2048], BF16, f"sq{i}") for i in range(2)]
    ssq_ = [b.sb([128, 32], F32, f"ssq{i}") for i in range(2)]
    tmp1 = b.sb([128, 4 * 32 * 8], F32, "ropetmp"); tmp_ = [tmp1, tmp1]
    qkbf_ = [b.sb([128, 2048], BF16, f"qkbf{i}") for i in range(2)]
    kT = [b.sb([128, 8, 128], BF16, f"kTst{i}") for i in range(2)]
    vst = [b.sb([128, 8, 128], BF16, f"vst{i}") for i in range(2)]

    def body(a, junk, ss, hbf, hT, stage, sq, ssq, tmp, qkbf):
        xt = xts[a % 2]
        if a + 1 < NS:
            load_x(a + 1)
        rmsnorm_transpose(b, xt, gmix, hbf, hT, pT[0], ss, junk)
        va = vst[a % 2]
        for n in range(6):
            py = pY[n % 4]
            for kc in range(8):
                b.op("pe", lambda e, n=n, kc=kc, py=py: e.matmul(py[:], lhsT=hT[:, kc, :],
                                                                   rhs=w[:, kc, n * 512:(n + 1) * 512],
                                                                   start=(kc == 0), stop=(kc == 7)),
                     reads=[hT, w], writes=[py])
            if n < 4:
                b.op("act", lambda e, n=n, py=py: e.activation(out=stage[:, n * 512:(n + 1) * 512], in_=py[:], func=AF.Copy),
                     reads=[py], writes=[stage])
                b.op("act", lambda e, n=n, py=py: e.activation(out=sq[:, n * 512:(n + 1) * 512], in_=py[:], func=AF.Square),
                     reads=[py], writes=[sq])
            else:
                b.op("dve", lambda e, n=n, py=py, va=va: e.tensor_copy(
                    out=va[:, (n - 4) * 4:(n - 4) * 4 + 4, :], in_=py[:].rearrange("p (h d) -> p h d", d=128)),
                    reads=[py], writes=[va])
        for h in range(8):
            G0.put(h, 2048 + a * 128, 2048 + (a + 1) * 128, va[:, h, :], [va], f"st_v{a % 2}")

    def tail(a, junk, ss, hbf, hT, stage, sq, ssq, tmp, qkbf):
        headnorm_rope(b, stage, sq, ssq, 32, gqk, cos[:, a, :], sin[:, a, :], qkbf, tmp)
        kt = kT[a % 2]
        for i in range(16):
            pt = b.banks16[6] if i < 8 else b.banks16[7]
            b.op("pe", lambda e, i=i, pt=pt: e.transpose(out=pt[:, (i % 8) * 128:(i % 8 + 1) * 128],
                                                         in_=qkbf[:, i * 128:(i + 1) * 128], identity=idt[:]),
                 reads=[qkbf, idt], writes=[pt])
            if i == 7:
                b.op("dve", lambda e, pt=pt, a=a: e.tensor_copy(out=qT_res[:, a, :], in_=pt[:]), reads=[pt], writes=[qT_res])
            if i == 15:
                b.op("dve", lambda e, pt=pt, kt=kt: e.tensor_copy(out=kt[:].rearrange("p a b -> p (a b)"), in_=pt[:]),
                     reads=[pt], writes=[kt])
        for h in range(8):
            G0.put(h, a * 128, (a + 1) * 128, kt[:, h, :], [kt], f"st_k{a % 2}")
    def args(a):
        return [z[a % 2] for z in (junk_, ss_, hbf_, hT_, stage_, sq_, ssq_, tmp_, qkbf_)]
    import os
    if os.environ.get("PIPE1", "0") == "1":
        body(0, *args(0))
        for a in range(NS):
            if a + 1 < NS:
                body(a + 1, *args(a + 1))
            tail(a, *args(a))
    else:
        for a in range(NS):
            body(a, *args(a))
            tail(a, *args(a))
    for h in range(8):
        G0.place(h, "sp")
        G0.reduce(h)


def f_diff_attn(b, io, o_tok, qT, G0):
    LAM_INIT = 0.2
    maskT = b.sb([128, 512], BF16, "maskT")
    b.op("sp", lambda e: e.dma_start(out=maskT[:], in_=io["maskT"][:]), writes=[maskT], dma="ld_mask")
    lam = b.sb([128, 256], F32, "lam")
    b.op("sp", lambda e: e.dma_start(out=lam[:], in_=io["lam"][:]), writes=[lam], dma="ld_lam")
    gsub = b.sb([128, 128], F32, "gsub")
    b.op("sp", lambda e: e.dma_start(out=gsub[:], in_=io["gsub"][:]), writes=[gsub], dma="ld_gsub")
    b.op("dve", lambda e: e.tensor_scalar(out=gsub[:], in0=gsub[:], scalar1=1.0 - LAM_INIT, scalar2=None,
                                          op0=ALU.mult), reads=[gsub], writes=[gsub])
    lprod = b.sb([128, 128], F32, "lprod")
    l2 = b.sb([128, 2], F32, "l2")
    neglam = b.sb([128, 1], F32, "neglam")
    l4 = lam[:].rearrange("p (a b d) -> p a b d", a=2, b=2)
    b.op("dve", lambda e: e.tensor_tensor(out=lprod[:].rearrange("p (a d) -> p a d", a=2), in0=l4[:, :, 0, :],
                                          in1=l4[:, :, 1, :], op=ALU.mult), reads=[lam], writes=[lprod])
    b.op("dve", lambda e: e.tensor_reduce(out=l2[:], in_=lprod[:].rearrange("p (a d) -> p a d", a=2), axis=AX.X,
                                          op=ALU.add), reads=[lprod], writes=[l2])
    b.op("act", lambda e: e.activation(out=l2[:], in_=l2[:], func=AF.Exp), reads=[l2], writes=[l2])
    b.op("dve", lambda e: e.tensor_tensor(out=neglam[:], in0=l2[:, 1:2], in1=l2[:, 0:1], op=ALU.subtract),
         reads=[l2], writes=[neglam])
    b.op("dve", lambda e: e.tensor_scalar(out=neglam[:], in0=neglam[:], scalar1=-LAM_INIT, scalar2=None, op0=ALU.add),
         reads=[neglam], writes=[neglam])

    ktb = [b.sb([128, 8192], BF16, f"ktb{i}") for i in range(2)]
    vb = [b.sb([128, 64, 129], BF16, f"vb{i}") for i in range(2)]
    for i in range(2):
        b.op("dve", lambda e, i=i: e.memset(vb[i][:, :, 128:129], 1.0), writes=[vb[i]])
    pS = [[b.banks[c * 2 + i] for i in range(2)] for c in range(2)]
    pA = [[b.banks[4 + c * 2 + i] for i in range(2)] for c in range(2)]
    pT = [[b.sb([128, 512], BF16, f"pTs{c}{i}") for i in range(2)] for c in range(2)]
    rec = [b.sb([128, 2], F32, f"rec{i}") for i in range(2)]
    o32 = [b.sb([128, 128], F32, f"o32{i}") for i in range(2)]
    oj = b.sb([128, 128], BF16, "ojunk")
    ss1 = [b.sb([128, 1], F32, f"ss1{i}") for i in range(2)]
    units = [(h, a, blk) for h in range(8) for a in range(NS) for blk in range(a + 1)]

    def load_head(h):
        kb, vv = ktb[h % 2], vb[h % 2]
        yb = G0.yb[h]
        for r in range(4):
            b.op("sp", lambda e, kb=kb, r=r, yb=yb: e.dma_start(out=kb[:, r * 2048:(r + 1) * 2048],
                                                               in_=yb[r * 128:(r + 1) * 128, 0:2048]),
                 reads=[G0.yd[h]], writes=[kb], dma=f"ld_kt{h % 2}")
            b.op("sp", lambda e, vv=vv, r=r, yb=yb: e.dma_start(
                out=vv[:, r * 16:(r + 1) * 16, 0:128],
                in_=yb[r * 128:(r + 1) * 128, 2048:4096].rearrange("p (a e) -> p a e", e=128)),
                reads=[G0.yd[h]], writes=[vv], dma=f"ld_v{h % 2}")

    def qk(u, n):
        h, a, blk = u
        kb = ktb[h % 2]
        for i in range(4):
            for c in range(2):
                ps = pS[c][n % 2]
                kt = 16 * i + blk
                b.op("pe", lambda e, c=c, i=i, kt=kt, ps=ps, kb=kb, a=a, h=h: e.matmul(
                    ps[:, i * 128:(i + 1) * 128], lhsT=kb[64 * c:64 * c + 64, kt * 128:(kt + 1) * 128],
                    rhs=qT[64 * c:64 * c + 64, a, h * 128:(h + 1) * 128], start=True, stop=True),
                    reads=[kb, qT], writes=[ps])

    def evac(h, a):
        k = a % 2
        a0, a1 = pA[0][k], pA[1][k]
        r, o, s = rec[k], o32[k], ss1[k]
        b.op("dve", lambda e: e.reciprocal(out=r[:, 0:1], in_=a0[:, 128:129]), reads=[a0], writes=[r])
        b.op("dve", lambda e: e.reciprocal(out=r[:, 1:2], in_=a1[:, 128:129]), reads=[a1], writes=[r])
        b.op("dve", lambda e: e.tensor_tensor(out=r[:, 1:2], in0=r[:, 1:2], in1=neglam[:], op=ALU.mult),
             reads=[r, neglam], writes=[r])
        b.op("dve", lambda e: e.tensor_scalar(out=o[:], in0=a0[:, 0:128], scalar1=r[:, 0:1], scalar2=None, op0=ALU.mult),
             reads=[a0, r], writes=[o])
        b.op("dve", lambda e: e.scalar_tensor_tensor(out=o[:], in0=a1[:, 0:128], scalar=r[:, 1:2], in1=o[:],
                                                     op0=ALU.mult, op1=ALU.add), reads=[a1, r, o], writes=[o])
        b.op("act", lambda e: e.activation(out=oj[:], in_=o[:], func=AF.Square, accum_out=s[:, 0:1]),
             reads=[o], writes=[oj, s])
        rstd_from_ss(b, s, 1, 1.0 / 128)
        ot = o_tok[a]
        b.op("dve", lambda e: e.scalar_tensor_tensor(out=ot[:, h * 128:(h + 1) * 128], in0=o[:], scalar=s[:, 0:1],
                                                     in1=gsub[:], op0=ALU.mult, op1=ALU.mult),
             reads=[o, s, gsub], writes=[ot])

    def softmax_pv(u, n):
        h, a, blk = u
        vv = vb[h % 2]
        for c in range(2):
            ps = pS[c][n % 2]
            pt = pT[c][n % 2]
            acc = pA[c][a % 2]
            b.op("act", lambda e, ps=ps, pt=pt: e.activation(out=pt[:], in_=ps[:], func=AF.Exp),
                 reads=[ps], writes=[pt])
            if blk == a:
                b.op("dve", lambda e, pt=pt: e.tensor_tensor(out=pt[:], in0=pt[:], in1=maskT[:], op=ALU.mult),
                     reads=[pt, maskT], writes=[pt])
            for i in range(4):
                kt = 16 * i + blk
                b.op("pe", lambda e, i=i, kt=kt, pt=pt, acc=acc, vv=vv, blk=blk, a=a: e.matmul(
                    acc[:, 0:129], lhsT=pt[:, i * 128:(i + 1) * 128], rhs=vv[:, kt, :],
                    start=(blk == 0 and i == 0), stop=(blk == a and i == 3)),
                    reads=[pt, vv], writes=[acc])
        if blk == a:
            evac(h, a)

    load_head(0)
    qk(units[0], 0)
    for n, u in enumerate(units):
        if u[1] == 0 and u[2] == 0 and u[0] + 1 < 8:
            load_head(u[0] + 1)
        if n + 1 < len(units):
            qk(units[n + 1], n + 1)
        softmax_pv(u, n)


def dsa_proj_weights(b, io, top=False):
    w = b.sb([128, 8, 2376], BF16, "w_in2", top=top)
    wuq = b.sb([128, 2, 1024], BF16, "wuq", top=top)
    wuqi = b.sb([128, 2, 512], BF16, "wuqi", top=top)

    def load():
        load_w_bf16(b, w, io["w_in2"], 8, 2376, "ld_w2in", colchunk=792)
        load_w_bf16(b, wuq, io["w_uq"], 2, 1024, "ld_wuq")
        load_w_bf16(b, wuqi, io["w_uqi"], 2, 512, "ld_wuqi")
    return {"w": w, "wuq": wuq, "wuqi": wuqi, "load": load}


def f_dsa_proj(b, io, x_res, cos, sin, G1, GK, scr, pre=None):
    idt = b.ident()
    gmix = b.sb([128, D], F32, "gmix1")
    b.op("sp", lambda e: e.dma_start(out=gmix[:], in_=io["gmix1"][:]), writes=[gmix], dma="ld_gmix1")
    gqk = b.sb([128, 2048], F32, "gqk2")
    b.op("sp", lambda e: e.dma_start(out=gqk[:], in_=io["gqk2"][:]), writes=[gqk], dma="ld_gqk2")
    b.op("pool", lambda e: e.tensor_scalar(out=gqk[:, 0:1024], in0=gqk[:, 0:1024], scalar1=0.125, scalar2=None,
                                           op0=ALU.mult), reads=[gqk], writes=[gqk])
    gcq = b.sb([128, 256], F32, "gcq")
    b.op("sp", lambda e: e.dma_start(out=gcq[:], in_=io["gcq"][:]), writes=[gcq], dma="ld_gcq")
    if pre is None:
        pre = dsa_proj_weights(b, io)
        pre["load"]()
    w, wuq, wuqi = pre["w"], pre["wuq"], pre["wuqi"]
    junk_ = [b.sb([128, D], BF16, f"junk3{i}") for i in range(2)]
    ss_ = [b.sb([128, 1], F32, f"ss3{i}") for i in range(2)]
    hbf_ = [b.sb([128, D], BF16, f"hbf3{i}") for i in range(2)]
    hT_ = [b.sb([128, 8, 128], BF16, f"hT3{i}") for i in range(2)]
    stage_ = [b.sb([128, 2048], F32, f"stage3{i}") for i in range(2)]
    sq_ = [b.sb([128, 2048], BF16, f"sq3{i}") for i in range(2)]
    ssq_ = [b.sb([128, 32], F32, f"ssq3{i}") for i in range(2)]
    tmp1 = b.sb([128, 4 * 32 * 8], F32, "ropetmp3"); tmp_ = [tmp1, tmp1]
    qkbf_ = [b.sb([128, 2048], BF16, f"qkbf3{i}") for i in range(2)]
    qkT = [b.sb([128, 16, 128], BF16, f"qkT3{i}") for i in range(2)]
    vst = [b.sb([128, 16, 64], BF16, f"vst3{i}") for i in range(2)]
    cqs_ = [b.sb([128, 256], F32, f"cqs{i}") for i in range(2)]
    ssc_ = [b.sb([128, 1], F32, f"ssc{i}") for i in range(2)]
    cqbf_ = [b.sb([128, 256], BF16, f"cqbf{i}") for i in range(2)]
    cqT_ = [b.sb([128, 2, 128], BF16, f"cqT{i}") for i in range(2)]
    kis_ = [b.sb([128, 64], F32, f"kis{i}") for i in range(2)]
    ksq_ = [b.sb([128, 64], F32, f"ksq{i}") for i in range(2)]
    kss_ = [b.sb([128, 1], F32, f"kss{i}") for i in range(2)]
    kibf_ = [b.sb([128, 128], BF16, f"kibf{i}") for i in range(2)]
    kiT = [b.sb([128, 128], BF16, f"kiT{i}") for i in range(2)]
    wi_ = [b.sb([128, 8], F32, f"wi{i}") for i in range(2)]
    sgn = [b.sb([128, 8], F32, f"sgn{i}") for i in range(2)]
    aw_ = [b.sb([128, 8], F32, f"aw{i}") for i in range(2)]
    qis_ = [b.sb([128, 512], F32, f"qis{i}") for i in range(2)]
    qibf_ = [b.sb([128, 512], BF16, f"qibf{i}") for i in range(2)]
    qiT = [b.sb([128, 4, 128], BF16, f"qiT{i}") for i in range(2)]
    nbank = [0]

    def bank():
        nbank[0] += 1
        return b.banks[2 + nbank[0] % 4]

    def proj(py, lhs, nk, rhs_fn, ncol):
        for kc in range(nk):
            b.op("pe", lambda e, kc=kc: e.matmul(py[:, 0:ncol], lhsT=lhs[:, kc, :], rhs=rhs_fn(kc),
                                                 start=(kc == 0), stop=(kc == nk - 1)), reads=[lhs, w, wuq, wuqi], writes=[py])

    def body2(a, junk, ss, hbf, hT, stage, sq, ssq, tmp, qkbf, cqs, ssc, cqbf, cqT, kis, ksq, kss, kibf, wi, aw, qis, qibf):
        xr = x_res[a]
        rmsnorm_transpose(b, xr, gmix, hbf, hT, b.banks16[0], ss, junk)
        py = bank()
        proj(py, hT, 8, lambda kc: w[:, kc, 0:256], 256)
        b.op("act", lambda e, py=py: e.activation(out=cqs[:], in_=py[:, 0:256], func=AF.Copy), reads=[py], writes=[cqs])
        b.op("act", lambda e, py=py: e.activation(out=junk[:, 0:256], in_=py[:, 0:256], func=AF.Square, accum_out=ssc[:, 0:1]),
             reads=[py], writes=[junk, ssc])
        rstd_from_ss(b, ssc, 1, 1.0 / 256)
        b.op("dve", lambda e: e.scalar_tensor_tensor(out=cqbf[:], in0=cqs[:], scalar=ssc[:, 0:1], in1=gcq[:],
                                                     op0=ALU.mult, op1=ALU.mult), reads=[cqs, ssc, gcq], writes=[cqbf])
        p6 = b.banks16[6]
        for kc in range(2):
            b.op("pe", lambda e, kc=kc: e.transpose(out=p6[:, kc * 128:(kc + 1) * 128], in_=cqbf[:, kc * 128:(kc + 1) * 128],
                                                    identity=idt[:]), reads=[cqbf, idt], writes=[p6])
        b.op("dve", lambda e: e.tensor_copy(out=cqT[:].rearrange("p a b -> p (a b)"), in_=p6[:, 0:256]),
             reads=[p6], writes=[cqT])
        for n in range(2):
            py = bank()
            proj(py, hT, 8, lambda kc, n=n: w[:, kc, 256 + n * 512:256 + (n + 1) * 512], 512)
            b.op("act", lambda e, py=py, n=n: e.activation(out=stage[:, 1024 + n * 512:1024 + (n + 1) * 512], in_=py[:], func=AF.Copy),
                 reads=[py], writes=[stage])
            b.op("act", lambda e, py=py, n=n: e.activation(out=sq[:, 1024 + n * 512:1024 + (n + 1) * 512], in_=py[:], func=AF.Square),
                 reads=[py], writes=[sq])
        va = vst[a % 2]
        for n in range(2):
            py = bank()
            proj(py, hT, 8, lambda kc, n=n: w[:, kc, 1280 + n * 512:1280 + (n + 1) * 512], 512)
            b.op("dve", lambda e, py=py, n=n, va=va: e.tensor_copy(out=va[:, n * 8:(n + 1) * 8, :],
                                                                   in_=py[:].rearrange("p (h d) -> p h d", d=64)),
                 reads=[py], writes=[va])
        for hp in range(8):
            G1.put(hp, 2048 + a * 128, 2048 + (a + 1) * 128, va[:, 2 * hp:2 * hp + 2, :].rearrange("p h d -> p (h d)"),
                   [va], f"st_v2{a % 2}")
        py = bank()
        proj(py, hT, 8, lambda kc: w[:, kc, 2304:2376], 72)
        b.op("act", lambda e, py=py: e.activation(out=kis[:], in_=py[:, 0:64], func=AF.Copy), reads=[py], writes=[kis])
        b.op("act", lambda e, py=py: e.activation(out=ksq[:], in_=py[:, 0:64], func=AF.Square), reads=[py], writes=[ksq])
        b.op("dve", lambda e, py=py: e.tensor_copy(out=wi[:], in_=py[:, 64:72]), reads=[py], writes=[wi])
        for n in range(2):
            py = bank()
            proj(py, cqT, 2, lambda kc, n=n: wuq[:, kc, n * 512:(n + 1) * 512], 512)
            b.op("act", lambda e, py=py, n=n: e.activation(out=stage[:, n * 512:(n + 1) * 512], in_=py[:], func=AF.Copy),
                 reads=[py], writes=[stage])
            b.op("act", lambda e, py=py, n=n: e.activation(out=sq[:, n * 512:(n + 1) * 512], in_=py[:], func=AF.Square),
                 reads=[py], writes=[sq])
        py = bank()
        proj(py, cqT, 2, lambda kc: wuqi[:, kc, :], 512)
        b.op("act", lambda e, py=py: e.activation(out=qis[:], in_=py[:], func=AF.Copy), reads=[py], writes=[qis])

    def tail2(a, junk, ss, hbf, hT, stage, sq, ssq, tmp, qkbf, cqs, ssc, cqbf, cqT, kis, ksq, kss, kibf, wi, aw, qis, qibf):
        headnorm_rope(b, stage, sq, ssq, 32, gqk, cos[:, a, :], sin[:, a, :], qkbf, tmp)
        qt = qkT[a % 2]
        for i in range(16):
            pt = b.banks16[1] if i < 8 else b.banks16[7]
            b.op("pe", lambda e, i=i, pt=pt: e.transpose(out=pt[:, (i % 8) * 128:(i % 8 + 1) * 128],
                                                         in_=qkbf[:, i * 128:(i + 1) * 128], identity=idt[:]),
                 reads=[qkbf, idt], writes=[pt])
            if i % 8 == 7:
                b.op("dve", lambda e, i=i, pt=pt, qt=qt: e.tensor_copy(
                    out=qt[:, (i // 8) * 8:(i // 8) * 8 + 8, :].rearrange("p a b -> p (a b)"), in_=pt[:]),
                    reads=[pt], writes=[qt])
        b.op("sp", lambda e, a=a, qt=qt: e.dma_start(out=scr["QT2"][a][:], in_=qt[:, 0:8, :].rearrange("p a b -> p (a b)")),
             reads=[qt], writes=[scr["QT2"][a]], dma=f"st_q2{a % 2}")
        for hp in range(8):
            G1.put(hp, a * 128, (a + 1) * 128, qt[:, 8 + hp, :], [qt], f"st_k2{a % 2}")
        ki_half = T(kibf.t[:, 0:64], kibf.d)
        headnorm_rope(b, kis, ksq, kss, 1, None, cos[:, a, :], sin[:, a, :], ki_half, tmp)
        b.op("pool", lambda e: e.tensor_copy(out=kibf[:, 64:128], in_=kibf[:, 0:64]), reads=[kibf], writes=[kibf])
        p7 = b.banks16[7]
        b.op("pe", lambda e: e.transpose(out=p7[:, 0:128], in_=kibf[:], identity=idt[:]), reads=[kibf, idt], writes=[p7])
        kt_ = kiT[a % 2]
        b.op("dve", lambda e, kt_=kt_: e.tensor_copy(out=kt_[:], in_=p7[:, 0:128]), reads=[p7], writes=[kt_])
        GK.put(0, a * 128, (a + 1) * 128, kt_[:], [kt_], f"st_ki{a % 2}")
        sg = sgn[a % 2]
        b.op("act", lambda e, sg=sg: e.activation(out=sg[:], in_=wi[:], func=AF.Sign), reads=[wi], writes=[sg])
        b.op("dve", lambda e, sg=sg: e.scalar_tensor_tensor(out=aw[:], in0=wi[:], scalar=IDX_SCALE, in1=sg[:],
                                                           op0=ALU.mult, op1=ALU.mult), reads=[wi, sg], writes=[aw])
        b.op("sp", lambda e, a=a, sg=sg: e.dma_start(out=scr["SG"][a][:], in_=sg[:]), reads=[sg], writes=[scr["SG"][a]],
             dma=f"st_sg{a % 2}")
        headnorm_rope(b, qis, None, aw, 8, None, cos[:, a, :], sin[:, a, :], qibf, tmp, norm=False)
        qi_ = qiT[a % 2]
        for i in range(4):
            b.op("pe", lambda e, i=i: e.transpose(out=p7[:, 256 + i * 128:256 + (i + 1) * 128],
                                                  in_=qibf[:, i * 128:(i + 1) * 128], identity=idt[:]),
                 reads=[qibf, idt], writes=[p7])
        b.op("dve", lambda e, qi_=qi_: e.tensor_copy(out=qi_[:].rearrange("p a b -> p (a b)"), in_=p7[:, 256:768]),
             reads=[p7], writes=[qi_])
        b.op("sp", lambda e, a=a, qi_=qi_: e.dma_start(out=scr["QI"][a][:], in_=qi_[:].rearrange("p a b -> p (a b)")),
             reads=[qi_], writes=[scr["QI"][a]], dma=f"st_qi{a % 2}")
    def args2(a):
        return [z[a % 2] for z in (junk_, ss_, hbf_, hT_, stage_, sq_, ssq_, tmp_, qkbf_, cqs_, ssc_, cqbf_, cqT_, kis_, ksq_, kss_, kibf_, wi_, aw_, qis_, qibf_)]
    import os
    if os.environ.get("PIPE2", "0") == "1":
        body2(0, *args2(0))
        for a in range(NS):
            if a + 1 < NS:
                body2(a + 1, *args2(a + 1))
            tail2(a, *args2(a))
    else:
        for a in range(NS):
            body2(a, *args2(a))
            tail2(a, *args2(a))
    GK.place(0, "act")
    GK.reduce(0)
    for hp in range(8):
        G1.place(hp, "act")
        G1.reduce(hp)


NIT2 = 14


def f_dsa_attn(b, io, o_tok, G1, GK, scr):
    idt = b.ident()
    kia = b.sb([128, 8192], BF16, "kiall")
    for r in range(4):
        b.op("sp", lambda e, r=r: e.dma_start(out=kia[:, r * 2048:(r + 1) * 2048], in_=GK.yb[0][r * 128:(r + 1) * 128, :]),
             reads=[GK.yd[0]], writes=[kia], dma="ld_kia")
    kia4 = kia[:].rearrange("p (r a t) -> p r a t", r=4, a=16)
    negm = b.sb([128, 512], F32, "negm")
    b.op("sp", lambda e: e.dma_start(out=negm[:], in_=io["negmask"][:]), writes=[negm], dma="ld_negm")
    cW = b.sb([128, NIT2], F32, "cW")
    for i in range(NIT2):
        b.op("dve", lambda e, i=i: e.memset(cW[:, i:i + 1], 2.0 ** (-i)), writes=[cW])
    THR = b.sb([128, NS], F32, "THR")
    qiTa = [b.sb([128, 4, 128], BF16, f"qiTa{i}") for i in range(2)]
    sgn = [b.sb([128, 8], F32, f"sgna{i}") for i in range(2)]
    Dg = [b.sb([128, 8, 128], BF16, f"Dg{i}") for i in range(2)]
    tb = [[b.sb([128, 512], BF16, f"tb{s}{i}") for i in range(3)] for s in range(2)]
    pIs = [[b.banks[4], b.banks[6]], [b.banks[5], b.banks[0]]]
    pIas = [b.banks[7], b.banks[1]]
    ctr = {"i": 0, "acc": 0, "kv": 0}

    def load_idx_small(a):
        b.op("sp", lambda e, a=a: e.dma_start(out=qiTa[a % 2][:].rearrange("p a b -> p (a b)"), in_=scr["QI"][a][:]),
             reads=[scr["QI"][a]], writes=[qiTa[a % 2]], dma=f"ld_qiTa{a % 2}")
        b.op("sp", lambda e, a=a: e.dma_start(out=sgn[a % 2][:], in_=scr["SG"][a][:]), reads=[scr["SG"][a]],
             writes=[sgn[a % 2]], dma=f"ld_sgn{a % 2}")

    def indexer_list(a, s, Ib_):
        L = []

        def add(eng, fn, reads=(), writes=()):
            L.append((eng, fn, reads, writes))
        nb = a + 1
        qi, sg, dg = qiTa[a % 2], sgn[a % 2], Dg[a % 2]
        pys, pia = pIs[s], pIas[s]
        add("dve", lambda e: e.tensor_tensor(out=dg[:], in0=idt[:].unsqueeze(1).to_broadcast([128, 8, 128]),
                                             in1=sg[:].unsqueeze(2).to_broadcast([128, 8, 128]), op=ALU.mult),
            [idt, sg], [dg])
        steps = [(blk, head) for blk in range(nb) for head in range(8)]
        S_ = len(steps)
        evq = []
        for k in range(S_ + 2):
            if k < S_:
                blk, head = steps[k]
                hp, hh = head // 2, head % 2
                py = pys[k % 2]
                add("pe", lambda e, hp=hp, hh=hh, blk=blk, py=py: e.matmul(
                    py[:].rearrange("p (r t) -> p r t", r=4), lhsT=qi[64 * hh:64 * hh + 64, hp, :],
                    rhs=kia4[64 * hh:64 * hh + 64, :, blk, :], start=True, stop=True), [qi, kia], [py])
            if 0 <= k - 1 < S_:
                py = pys[(k - 1) % 2]
                t = tb[s][(k - 1) % 3]
                add("act", lambda e, t=t, py=py: e.activation(out=t[:], in_=py[:], func=AF.Relu), [py], [t])
            if 0 <= k - 2 < S_:
                blk, head = steps[k - 2]
                t = tb[s][(k - 2) % 3]
                add("pe", lambda e, head=head, t=t: e.matmul(pia[:], lhsT=dg[:, head, :], rhs=t[:],
                                                             start=(head == 0), stop=(head == 7)), [dg, t], [pia])
                if head == 7:
                    add("act", lambda e, eb=blk: e.activation(out=Ib_[:, eb * 512:(eb + 1) * 512], in_=pia[:], func=AF.Copy),
                        [pia], [Ib_])
        for _, eb in evq:
            add("act", lambda e, eb=eb: e.activation(out=Ib_[:, eb * 512:(eb + 1) * 512], in_=pia[:], func=AF.Copy),
                [pia], [Ib_])
        return L

    mA = b.mark()
    IbA = [b.sb([128, 8192], F32, f"IbA{i}") for i in range(2)]
    MqA = [b.sb([128, 8192], BF16, f"MqA{i}") for i in range(2)]
    st = [{k: b.sb([128, n], F32, f"bs{k}{s}") for k, n in (("m1", 1), ("lo", 1), ("mid", 1), ("nmid", 1), ("cD", 1),
                                                           ("sA", 1), ("g", 1), ("W", NIT2))} for s in range(2)]

    def stageA_list(a, s):
        L = indexer_list(a, s, IbA[s])

        def add(eng, fn, reads=(), writes=()):
            L.append((eng, fn, reads, writes))
        nb = a + 1
        nv = 512 * nb
        Ib_, Mq_ = IbA[s], MqA[s]
        S_ = st[s]
        m1, lo, mid, cD, sA, g, W = (S_[k] for k in ("m1", "lo", "mid", "cD", "sA", "g", "W"))
        add("dve", lambda e: e.tensor_reduce(out=m1[:], in_=Ib_[:, 0:nv], axis=AX.X, op=ALU.max, apply_absolute_value=True),
            [Ib_], [m1])
        add("dve", lambda e: e.tensor_tensor(out=Ib_[:, nv - 512:nv], in0=Ib_[:, nv - 512:nv], in1=negm[:], op=ALU.add),
            [Ib_, negm], [Ib_])
        add("dve", lambda e: e.tensor_scalar(out=W[:], in0=cW[:], scalar1=m1[:, 0:1], scalar2=None, op0=ALU.mult),
            [cW, m1], [W])
        add("dve", lambda e: e.tensor_tensor(out=W[:], in0=W[:], in1=cW[:], op=ALU.add), [W, cW], [W])
        add("dve", lambda e: e.memset(mid[:], 0.0), [], [mid])
        h = 512 * (nb // 2)
        thr = TOPK - 0.5 - 0.5 * h
        for i in range(NIT2):
            if h > 0:
                add("act", lambda e: e.activation(out=Mq_[:, 0:h], in_=Ib_[:, 0:h], func=AF.Sign, bias=mid[:, 0:1],
                                                  scale=-1.0, accum_out=sA[:, 0:1]), [Ib_, mid], [Mq_, sA])
            add("dve", lambda e: e.tensor_scalar(out=Mq_[:, h:nv], in0=Ib_[:, h:nv], scalar1=mid[:, 0:1], scalar2=0.0,
                                                 op0=ALU.is_ge, op1=ALU.add, accum_out=cD[:, 0:1]), [Ib_, mid], [Mq_, cD])
            if h > 0:
                add("dve", lambda e: e.scalar_tensor_tensor(out=cD[:], in0=sA[:], scalar=-0.5, in1=cD[:], op0=ALU.mult,
                                                            op1=ALU.add), [sA, cD], [cD])
            add("dve", lambda e, i=i: e.tensor_scalar(out=g[:], in0=cD[:], scalar1=thr, scalar2=W[:, i:i + 1],
                                                      op0=ALU.is_ge, op1=ALU.mult), [cD, W], [g])
            if i + 1 < NIT2:
                add("dve", lambda e, i=i: e.scalar_tensor_tensor(out=mid[:], in0=mid[:], scalar=W[:, i + 1:i + 2], in1=g[:],
                                                                 op0=ALU.subtract, op1=ALU.add), [mid, W, g], [mid])
            else:
                add("dve", lambda e, i=i: e.scalar_tensor_tensor(out=lo[:], in0=mid[:], scalar=W[:, i:i + 1], in1=g[:],
                                                                 op0=ALU.subtract, op1=ALU.add), [mid, W, g], [lo])
        for c0 in range(0, nv, 2048):
            c1 = min(nv, c0 + 2048)
            add("dve", lambda e, c0=c0, c1=c1: e.tensor_scalar(out=Mq_[:, c0:c1], in0=Ib_[:, c0:c1], scalar1=lo[:, 0:1],
                                                               scalar2=None, op0=ALU.is_ge), [Ib_, lo], [Mq_])
        add("sp", ("dma", lambda e: e.dma_start(out=scr["MQ"][a][:, 0:nv], in_=Mq_[:, 0:nv]), f"st_mq{s}"),
            [Mq_], [scr["MQ"][a]])
        return L

    def emit_item(it):
        eng, fn, reads, writes = it
        if isinstance(fn, tuple):
            b.op(eng, fn[1], reads, writes, dma=fn[2])
        else:
            b.op(eng, fn, reads, writes)

    for a0 in range(0, NS, 2):
        load_idx_small(a0)
        load_idx_small(a0 + 1)
        la = stageA_list(a0, 0)
        lb = stageA_list(a0 + 1, 1)
        i = j = 0
        while i < len(la) or j < len(lb):
            if j >= len(lb) or (i < len(la) and i * len(lb) <= j * len(la)):
                emit_item(la[i]); i += 1
            else:
                emit_item(lb[j]); j += 1
    b.release(mA)

    Mqs = [b.sb([128, 8192], BF16, f"MqB{i}") for i in range(2)]
    MTs = [b.sb([128, 64, 128], BF16, f"MT{i}") for i in range(2)]
    ktp = [b.sb([128, 8192], BF16, f"ktp{i}") for i in range(2)]
    vp = [b.sb([128, 64, 130], BF16, f"vp{i}") for i in range(2)]
    for i in range(2):
        b.op("dve", lambda e, i=i: e.memset(vp[i][:, :, 0:1], 1.0), writes=[vp[i]])
        b.op("dve", lambda e, i=i: e.memset(vp[i][:, :, 129:130], 1.0), writes=[vp[i]])
    qTa = [b.sb([128, 8, 128], BF16, f"qTa{i}") for i in range(2)]
    pT = [b.sb([128, 512], BF16, f"pTd{i}") for i in range(3)]
    rec = [b.sb([128, 1], F32, f"recd{i}") for i in range(2)]
    pS = [b.banks[0], b.banks[1], b.banks[5]]
    pA = [b.banks[2], b.banks[3]]
    pMs = [b.banks16[6], b.banks16[7]]

    def load_slot_small(a):
        nv = 512 * (a + 1)
        b.op("sp", lambda e, a=a: e.dma_start(out=qTa[a % 2][:].rearrange("p a b -> p (a b)"), in_=scr["QT2"][a][:]),
             reads=[scr["QT2"][a]], writes=[qTa[a % 2]], dma=f"ld_qTa{a % 2}")
        b.op("sp", lambda e, a=a, nv=nv: e.dma_start(out=Mqs[a % 2][:, 0:nv], in_=scr["MQ"][a][:, 0:nv]),
             reads=[scr["MQ"][a]], writes=[Mqs[a % 2]], dma=f"ld_mq{a % 2}")

    def load_kv(a, hp):
        k = ctr["kv"] % 2
        ctr["kv"] += 1
        n = (a + 1) * 128
        yb = G1.yb[hp]
        b.op("sp", lambda e: e.dma_start(out=ktp[k][:].rearrange("p (r x) -> p r x", r=4)[:, :, 0:n],
                                         in_=yb[:, 0:n].rearrange("(r p) x -> p r x", p=128)),
             reads=[G1.yd[hp]], writes=[ktp[k]], dma=f"ld_ktp{k}")
        for r in range(4):
            b.op("sp", lambda e, r=r: e.dma_start(
                out=vp[k][:, r * 16:r * 16 + a + 1, 1:129],
                in_=yb[r * 128:(r + 1) * 128, 2048:2048 + n].rearrange("p (a e) -> p a e", e=128)),
                reads=[G1.yd[hp]], writes=[vp[k]], dma=f"ld_vp{k}")
        return k

    def pre_list(a):
        L = []
        Mq, MT = Mqs[a % 2], MTs[a % 2]
        nt = 4 * (a + 1)
        ngr = (nt + 7) // 8
        for gi in range(ngr + 1):
            if gi < ngr:
                pm = pMs[gi % 2]
                for kt in range(gi * 8, min(nt, gi * 8 + 8)):
                    L.append(("pe", lambda e, kt=kt, pm=pm: e.transpose(out=pm[:, (kt % 8) * 128:(kt % 8 + 1) * 128],
                                                                        in_=Mq[:, kt * 128:(kt + 1) * 128], identity=idt[:]),
                              [Mq, idt], [pm]))
            if gi >= 1:
                g0 = gi - 1
                pm = pMs[g0 % 2]
                k0 = g0 * 8
                n = min(nt, k0 + 8) - k0
                L.append(("act", lambda e, pm=pm, k0=k0, n=n: e.activation(
                    out=MT[:, k0:k0 + n, :].rearrange("p a b -> p (a b)"), in_=pm[:, 0:n * 128], func=AF.Copy),
                    [pm], [MT]))
        return L

    def qk(u, n, kbuf):
        a, hp, hh, blk = u
        ps = pS[n % 3]
        qa = qTa[a % 2]
        for i in range(4):
            kt = 16 * i + blk
            b.op("pe", lambda e, i=i, kt=kt, ps=ps, qa=qa, hh=hh, hp=hp, kbuf=kbuf: e.matmul(
                ps[:, i * 128:(i + 1) * 128], lhsT=ktp[kbuf][64 * hh:64 * hh + 64, kt * 128:(kt + 1) * 128],
                rhs=qa[64 * hh:64 * hh + 64, hp, :], start=True, stop=True), reads=[ktp[kbuf], qa], writes=[ps])

    def softmax_pv(u, n, kbuf):
        a, hp, hh, blk = u
        ps, pt = pS[n % 3], pT[n % 3]
        if blk == 0:
            ctr["acc"] += 1
        acc = pA[ctr["acc"] % 2]
        b.op("act", lambda e: e.activation(out=pt[:], in_=ps[:], func=AF.Exp), reads=[ps], writes=[pt])
        MT = MTs[a % 2]
        b.op("dve", lambda e: e.tensor_tensor(out=pt[:], in0=pt[:], in1=MT[:, 4 * blk:4 * blk + 4, :].rearrange("p a b -> p (a b)"),
                                              op=ALU.mult), reads=[pt, MT], writes=[pt])
        for i in range(4):
            kt = 16 * i + blk
            b.op("pe", lambda e, i=i, kt=kt: e.matmul(acc[:, 0:65], lhsT=pt[:, i * 128:(i + 1) * 128],
                                                      rhs=vp[kbuf][:, kt, hh * 65:(hh + 1) * 65],
                                                      start=(blk == 0 and i == 0), stop=(blk == a and i == 3)),
                 reads=[pt, vp[kbuf]], writes=[acc])
        if blk == a:
            head = 2 * hp + hh
            r = rec[ctr["acc"] % 2]
            sc, v0 = (0, 1) if hh == 0 else (64, 0)
            b.op("dve", lambda e: e.reciprocal(out=r[:], in_=acc[:, sc:sc + 1]), reads=[acc], writes=[r])
            b.op("dve", lambda e: e.tensor_scalar(out=o_tok[a][:, head * 64:(head + 1) * 64], in0=acc[:, v0:v0 + 64],
                                                  scalar1=r[:, 0:1], scalar2=None, op0=ALU.mult),
                 reads=[acc, r], writes=[o_tok[a]])

    load_slot_small(0)
    for it in pre_list(0):
        b.op(*it)
    for a in range(NS):
        if a + 1 < NS:
            load_slot_small(a + 1)
        kb_next = load_kv(a, 0)
        nxt = pre_list(a + 1) if a + 1 < NS else []
        done = 0
        units = [(a, hp, hh, blk) for hp in range(8) for hh in range(2) for blk in range(a + 1)]
        kbufs = {0: kb_next}
        kbufs[1] = load_kv(a, 1)
        qk(units[0], 0, kbufs[0])
        if len(units) > 1:
            qk(units[1], 1, kbufs[units[1][1]])
        for n, u in enumerate(units):
            _, hp, hh, blk = u
            if hh == 0 and blk == 0 and 1 <= hp and hp + 1 < 8:
                kbufs[hp + 1] = load_kv(a, hp + 1)
            if n + 2 < len(units):
                qk(units[n + 2], n + 2, kbufs[units[n + 2][1]])
            softmax_pv(u, n, kbufs[hp])
            want = (n + 1) * len(nxt) // len(units)
            while done < want:
                b.op(*nxt[done])
                done += 1
        while done < len(nxt):
            b.op(*nxt[done])
            done += 1


BF = ml_dtypes.bfloat16


def rep(v, n=128):
    return np.ascontiguousarray(np.tile(np.asarray(v).reshape(1, -1), (n, 1)))


def own_tiles(arr_bs, c):
    bb, j = c // 4, c % 4
    a = arr_bs[bb]
    return np.ascontiguousarray(a.reshape(64, 128, *a.shape[1:])[j::4].reshape(2048, *a.shape[1:]))


def gather_tiles(per_core, bb):
    out = np.empty((64,) + per_core[0].shape[1:], per_core[0].dtype)
    for j in range(4):
        out[j::4] = per_core[bb * 4 + j]
    return out


def diff_masks(c):
    j = c % 4
    m = np.zeros((128, 4, 128), np.float32)
    for i in range(4):
        if i < j:
            m[:, i, :] = 1.0
        elif i == j:
            m[0:64, i, :] = 1.0
            m[64:128, i, 64:128] = 1.0
    return m.reshape(128, 512).astype(BF)


def dsa_negmask(c):
    return np.where(diff_masks(c).astype(np.float32).reshape(128, 4, 128).transpose(2, 1, 0).reshape(128, 512) > 0,
                    0.0, -1e30).astype(np.float32)


def build_L1():
    nc = bass.Bass("TRN2", target_bir_lowering=False)
    b = B(nc)
    io = {
        "x": b.dram("x", [2048, 1024], F32, "ExternalInput"),
        "pos": b.dram("pos", [128, 16], I32, "ExternalInput"),
        "gmix": b.dram("gmix", [128, 1024], F32, "ExternalInput"),
        "gqk": b.dram("gqk", [128, 2048], F32, "ExternalInput"),
        "w_in": b.dram("w_in", [1024, 3072], F32, "ExternalInput"),
        "QT": b.dram("QT", [16, 128, 1024], BF16, "ExternalOutput"),
        "KT": b.dram("KT", [16, 128, 1024], BF16, "ExternalOutput"),
        "V": b.dram("V", [16, 128, 8 * 129], BF16, "ExternalOutput"),
    }
    phase_diff_proj(b, io)
    b.finish()
    return nc


def build_L2():
    nc = bass.Bass("TRN2", target_bir_lowering=False)
    b = B(nc)
    io = {
        "QT": b.dram("QT", [16, 128, 1024], BF16, "ExternalInput"),
        "KTall": b.dram("KTall", [8, 128, 8192], BF16, "ExternalInput"),
        "Vall": b.dram("Vall", [8, 128, 64 * 129], BF16, "ExternalInput"),
        "maskT": b.dram("maskT", [128, 512], BF16, "ExternalInput"),
        "lam": b.dram("lam", [128, 256], F32, "ExternalInput"),
        "gsub": b.dram("gsub", [128, 128], F32, "ExternalInput"),
        "x": b.dram("x", [2048, 1024], F32, "ExternalInput"),
        "pos": b.dram("pos", [128, 16], I32, "ExternalInput"),
        "w_out": b.dram("w_out", [1024, 1024], F32, "ExternalInput"),
        "gmlp": b.dram("gmlp", [128, 1024], F32, "ExternalInput"),
        "w1": b.dram("w1", [1024, 4096], F32, "ExternalInput"),
        "w2": b.dram("w2", [4096, 1024], F32, "ExternalInput"),
        "gmix1": b.dram("gmix1", [128, 1024], F32, "ExternalInput"),
        "gqk2": b.dram("gqk2", [128, 2048], F32, "ExternalInput"),
        "gcq": b.dram("gcq", [128, 256], F32, "ExternalInput"),
        "w_in2": b.dram("w_in2", [1024, 2376], F32, "ExternalInput"),
        "w_uq": b.dram("w_uq", [256, 1024], F32, "ExternalInput"),
        "w_uqi": b.dram("w_uqi", [256, 512], F32, "ExternalInput"),
        "X2": b.dram("X2", [2048, 1024], F32, "ExternalOutput"),
        "QT2": b.dram("QT2", [16, 128, 1024], BF16, "ExternalOutput"),
        "KT2": b.dram("KT2", [16, 128, 1024], BF16, "ExternalOutput"),
        "V2": b.dram("V2", [16, 128, 16 * 65], BF16, "ExternalOutput"),
        "KI": b.dram("KI", [16, 128, 128], BF16, "ExternalOutput"),
        "QI": b.dram("QI", [16, 128, 512], BF16, "ExternalOutput"),
        "SG": b.dram("SG", [16, 128, 8], F32, "ExternalOutput"),
    }
    b.ident(); b.eps()
    pos_t = b.sb([128, NS], I32, "pos")
    b.op("sp", lambda e: e.dma_start(out=pos_t[:], in_=io["pos"][:]), writes=[pos_t], dma="ld_pos")
    cos, sin = rope_tables(b, pos_t)
    o_tok = [b.sb([128, 1024], BF16, f"otok{a}", top=True) for a in range(NS)]
    m1 = b.mark()
    phase_diff_attn(b, io, o_tok)
    b.release(m1)
    x_res = [b.sb([128, 1024], F32, f"xres{a}") for a in range(NS)]
    h2T = b.sb([128, 8, 2048], BF16, "h2T")
    m2 = b.mark()
    phase_post_attn(b, io, o_tok, x_res, h2T, io["w_out"][:], io["gmlp"][:], io["x"])
    b.release(m2)
    b.hi = ARENA_END
    m3 = b.mark()
    phase_mlp(b, x_res, h2T, io["w1"], io["w2"])
    b.release(m3)
    for a in range(NS):
        b.store("sp", lambda e, a=a: e.dma_start(out=io["X2"][a * 128:(a + 1) * 128, :], in_=x_res[a][:]),
                reads=[x_res[a]], dma="st_x2")
    phase_dsa_proj(b, io, x_res, cos, sin)
    b.finish()
    return nc


def l1_inputs(inp, c):
    return {"x": own_tiles(inp["x"], c),
            "pos": np.ascontiguousarray(own_tiles(inp["positions"], c).reshape(16, 128).T),
            "gmix": rep(inp["norm_mix"][0]),
            "gqk": rep(np.concatenate([np.tile(inp["diff_q_norm"][0], 16), np.tile(inp["diff_k_norm"][0], 16)])),
            "w_in": np.ascontiguousarray(inp["diff_w_in"][0])}


def l2_inputs(inp, r1):
    KTall = []; Vall = []
    for bb in range(2):
        kt = gather_tiles([r["KT"].reshape(16, 128, 8, 128) for r in r1], bb)
        KTall.append(np.ascontiguousarray(kt.transpose(2, 1, 0, 3).reshape(8, 128, 8192)))
        v = gather_tiles([r["V"].reshape(16, 128, 8, 129) for r in r1], bb)
        Vall.append(np.ascontiguousarray(v.transpose(2, 1, 0, 3).reshape(8, 128, 64 * 129)))
    lam = rep(np.concatenate([inp["diff_lam_q1"][0], inp["diff_lam_k1"][0], inp["diff_lam_q2"][0], inp["diff_lam_k2"][0]]))
    gqk2 = rep(np.concatenate([np.tile(inp["dsa_q_norm"][0], 16), np.tile(inp["dsa_k_norm"][0], 16)]))
    ins = []
    for c in range(8):
        ins.append({"QT": r1[c]["QT"], "KTall": KTall[c // 4], "Vall": Vall[c // 4], "maskT": diff_masks(c),
                    "lam": lam, "gsub": rep(inp["diff_subln"][0]),
                    "x": own_tiles(inp["x"], c),
                    "pos": np.ascontiguousarray(own_tiles(inp["positions"], c).reshape(16, 128).T),
                    "w_out": np.ascontiguousarray(inp["diff_w_out"][0]), "gmlp": rep(inp["norm_mlp"][0]),
                    "w1": np.ascontiguousarray(inp["mlp_w1"][0]), "w2": np.ascontiguousarray(inp["mlp_w2"][0]),
                    "gmix1": rep(inp["norm_mix"][1]), "gqk2": gqk2, "gcq": rep(inp["dsa_cq_norm"][0]),
                    "w_in2": np.ascontiguousarray(inp["dsa_w_in"][0]), "w_uq": np.ascontiguousarray(inp["dsa_w_uq"][0]),
                    "w_uqi": np.ascontiguousarray(inp["dsa_w_uq_idx"][0])})
    return ins


def run(nc, ins):
    res = run_bass_kernel_spmd(nc, ins, core_ids=list(range(8)))
    return [{k: np.asarray(v) for k, v in r.items()} for r in res.results]


def build_L3():
    nc = bass.Bass("TRN2", target_bir_lowering=False)
    b = B(nc)
    io = {
        "QT2": b.dram("QT2", [16, 128, 1024], BF16, "ExternalInput"),
        "QI": b.dram("QI", [16, 128, 512], BF16, "ExternalInput"),
        "SG": b.dram("SG", [16, 128, 8], F32, "ExternalInput"),
        "KIall": b.dram("KIall", [128, 8192], BF16, "ExternalInput"),
        "KT2all": b.dram("KT2all", [8, 128, 8192], BF16, "ExternalInput"),
        "V2all": b.dram("V2all", [8, 128, 64 * 130], BF16, "ExternalInput"),
        "negmask": b.dram("negmask", [128, 512], F32, "ExternalInput"),
        "X2": b.dram("X2", [2048, 1024], F32, "ExternalInput"),
        "w_out": b.dram("w_out", [1024, 1024], F32, "ExternalInput"),
        "gmlp": b.dram("gmlp", [128, 1024], F32, "ExternalInput"),
        "w1": b.dram("w1", [1024, 4096], F32, "ExternalInput"),
        "w2": b.dram("w2", [4096, 1024], F32, "ExternalInput"),
        "OUT": b.dram("OUT", [2048, 1024], F32, "ExternalOutput"),
    }
    b.ident(); b.eps()
    o_tok = [b.sb([128, 1024], BF16, f"otok{a}", top=True) for a in range(NS)]
    m1 = b.mark()
    phase_dsa_attn(b, io, o_tok)
    b.release(m1)
    x_res = [b.sb([128, 1024], F32, f"xres{a}") for a in range(NS)]
    h2T = b.sb([128, 8, 2048], BF16, "h2T")
    m2 = b.mark()
    phase_post_attn(b, io, o_tok, x_res, h2T, io["w_out"][:], io["gmlp"][:], io["X2"])
    b.release(m2)
    b.hi = ARENA_END
    m3 = b.mark()
    phase_mlp(b, x_res, h2T, io["w1"], io["w2"])
    b.release(m3)
    for a in range(NS):
        b.store("sp", lambda e, a=a: e.dma_start(out=io["OUT"][a * 128:(a + 1) * 128, :], in_=x_res[a][:]),
                reads=[x_res[a]], dma="st_out")
    b.finish()
    return nc


def l3_inputs(inp, r2):
    KIall = []; KTall = []; Vall = []
    for bb in range(2):
        ki = gather_tiles([r["KI"] for r in r2], bb)
        KIall.append(np.ascontiguousarray(ki.transpose(1, 0, 2).reshape(128, 8192)))
        kt = gather_tiles([r["KT2"].reshape(16, 128, 8, 128) for r in r2], bb)
        KTall.append(np.ascontiguousarray(kt.transpose(2, 1, 0, 3).reshape(8, 128, 8192)))
        v = gather_tiles([r["V2"].reshape(16, 128, 8, 130) for r in r2], bb)
        Vall.append(np.ascontiguousarray(v.transpose(2, 1, 0, 3).reshape(8, 128, 64 * 130)))
    ins = []
    for c in range(8):
        ins.append({"QT2": r2[c]["QT2"], "QI": r2[c]["QI"], "SG": r2[c]["SG"], "KIall": KIall[c // 4],
                    "KT2all": KTall[c // 4], "V2all": Vall[c // 4], "negmask": dsa_negmask(c),
                    "X2": r2[c]["X2"], "w_out": np.ascontiguousarray(inp["dsa_w_out"][0]), "gmlp": rep(inp["norm_mlp"][1]),
                    "w1": np.ascontiguousarray(inp["mlp_w1"][1]), "w2": np.ascontiguousarray(inp["mlp_w2"][1])})
    return ins


def assemble(r3):
    out = np.empty((2, 8192, 1024), np.float32)
    for bb in range(2):
        o = gather_tiles([r["OUT"].reshape(16, 128, 1024) for r in r3], bb)
        out[bb] = o.reshape(8192, 1024)
    return out


FUSED_IN = [
    ("x", [2048, 1024], F32), ("pos", [128, 16], I32), ("gmix", [128, 1024], F32), ("gqk", [128, 2048], F32),
    ("w_in", [1024, 3072], F32), ("maskT", [128, 512], BF16), ("lam", [128, 256], F32), ("gsub", [128, 128], F32),
    ("w_out", [1024, 1024], F32), ("gmlp", [128, 1024], F32), ("w1", [1024, 4096], F32), ("w2", [4096, 1024], F32),
    ("gmix1", [128, 1024], F32), ("gqk2", [128, 2048], F32), ("gcq", [128, 256], F32), ("w_in2", [1024, 2376], F32),
    ("w_uq", [256, 1024], F32), ("w_uqi", [256, 512], F32), ("negmask", [128, 512], F32),
    ("w_outb", [1024, 1024], F32), ("gmlpb", [128, 1024], F32), ("w1b", [1024, 4096], F32), ("w2b", [4096, 1024], F32),
]


def build_fused():
    nc = bass.Bass("TRN2", target_bir_lowering=False)
    _RANK.clear()
    b = B(nc)
    io = {n: b.dram(n, s, d, "ExternalInput") for (n, s, d) in FUSED_IN}
    io["OUT"] = b.dram("OUT", [2048, 1024], F32, "ExternalOutput")
    qt2 = nc.dram_tensor("scr_qt2", [16, 128, 1024], BF16).ap()
    qi = nc.dram_tensor("scr_qi", [16, 128, 512], BF16).ap()
    sg = nc.dram_tensor("scr_sg", [16, 128, 8], F32).ap()
    x2 = nc.dram_tensor("scr_x2", [2048, 1024], F32).ap()
    mqd = nc.dram_tensor("scr_mq", [16, 128, 8192], BF16).ap()
    scr = {"QT2": [T(qt2[a]) for a in range(NS)], "QI": [T(qi[a]) for a in range(NS)], "SG": [T(sg[a]) for a in range(NS)],
           "MQ": [T(mqd[a]) for a in range(NS)]}
    x2d = [T(x2[a * 128:(a + 1) * 128, :]) for a in range(NS)]

    b.ident(); b.eps()
    pos_t = b.sb([128, NS], I32, "pos")
    b.op("sp", lambda e: e.dma_start(out=pos_t[:], in_=io["pos"][:]), writes=[pos_t], dma="ld_pos")
    cos, sin = rope_tables(b, pos_t)
    o_tok = [b.sb([128, 1024], BF16, f"otok{a}", top=True) for a in range(NS)]
    hi_otok = b.hi
    qT_res = b.sb([128, NS, 1024], BF16, "qTres", top=True)
    zt = b.sb([128, 2048], BF16, "zeros")
    b.op("dve", lambda e: e.memset(zt[:], 0.0), writes=[zt])
    m0 = b.mark()
    G0 = Gather(b, "g0", 8, 4096, zt, "act")
    wst = [T(nc.alloc_sbuf_tensor_at(f"wstage{i}", [128, 8, 512], F32, offset=ARENA_END - (i + 1) * 16384)) for i in range(2)]
    f_diff_proj(b, io, qT_res, G0, cos, sin, wstage=wst)
    b.release(m0)
    G1 = Gather(b, "g1", 8, 4096, zt, "pool")
    GK = Gather(b, "gk", 1, 2048, zt, "pool")
    f_diff_attn(b, io, o_tok, qT_res, G0)
    b.release(m0)
    b.hi = hi_otok
    x_res = [b.sb([128, 1024], F32, f"xres{a}") for a in range(NS)]
    mh = b.mark()
    h2T = b.sb([128, 8, 2048], BF16, "h2T")
    m2 = b.mark()
    phase_post_attn(b, io, o_tok, x_res, h2T, io["w_out"][:], io["gmlp"][:], io["x"])
    b.release(m2)
    b.hi = ARENA_END
    pre = dsa_proj_weights(b, io, top=True)
    phase_mlp(b, x_res, h2T, io["w1"], io["w2"], hook=pre["load"])
    b.release((mh[0], b.hi))
    for a in range(NS):
        b.op("sp", lambda e, a=a: e.dma_start(out=x2d[a][:], in_=x_res[a][:]), reads=[x_res[a]], writes=[x2d[a]], dma="st_x2")
    io2 = dict(io)
    f_dsa_proj(b, io2, x_res, cos, sin, G1, GK, scr, pre=pre)
    b.release(m0)
    b.hi = ARENA_END
    o_tok = [b.sb([128, 1024], BF16, f"otokb{a}", top=True) for a in range(NS)]
    m4 = b.mark()
    f_dsa_attn(b, io, o_tok, G1, GK, scr)
    b.release(m4)
    x_res = [b.sb([128, 1024], F32, f"xresb{a}") for a in range(NS)]
    h2T = b.sb([128, 8, 2048], BF16, "h2Tb")
    m5 = b.mark()
    phase_post_attn(b, io, o_tok, x_res, h2T, io["w_outb"][:], io["gmlpb"][:], x2, xdeps=x2d)
    b.release(m5)
    b.hi = ARENA_END
    phase_mlp(b, x_res, h2T, io["w1b"], io["w2b"])
    for a in range(NS):
        b.store("sp", lambda e, a=a: e.dma_start(out=io["OUT"][a * 128:(a + 1) * 128, :], in_=x_res[a][:]),
                reads=[x_res[a]], dma="st_out")
    b.finish()
    return nc


def fused_inputs(inp):
    lam = rep(np.concatenate([inp["diff_lam_q1"][0], inp["diff_lam_k1"][0], inp["diff_lam_q2"][0], inp["diff_lam_k2"][0]]))
    gqk = rep(np.concatenate([np.tile(inp["diff_q_norm"][0], 16), np.tile(inp["diff_k_norm"][0], 16)]))
    gqk2 = rep(np.concatenate([np.tile(inp["dsa_q_norm"][0], 16), np.tile(inp["dsa_k_norm"][0], 16)]))
    c_ = np.ascontiguousarray
    shared = {"gmix": rep(inp["norm_mix"][0]), "gqk": gqk, "w_in": c_(inp["diff_w_in"][0]), "lam": lam,
              "gsub": rep(inp["diff_subln"][0]), "w_out": c_(inp["diff_w_out"][0]), "gmlp": rep(inp["norm_mlp"][0]),
              "w1": c_(inp["mlp_w1"][0]), "w2": c_(inp["mlp_w2"][0]), "gmix1": rep(inp["norm_mix"][1]), "gqk2": gqk2,
              "gcq": rep(inp["dsa_cq_norm"][0]), "w_in2": c_(inp["dsa_w_in"][0]), "w_uq": c_(inp["dsa_w_uq"][0]),
              "w_uqi": c_(inp["dsa_w_uq_idx"][0]), "w_outb": c_(inp["dsa_w_out"][0]), "gmlpb": rep(inp["norm_mlp"][1]),
              "w1b": c_(inp["mlp_w1"][1]), "w2b": c_(inp["mlp_w2"][1])}
    ins = []
    for c in range(8):
        d = dict(shared)
        d["x"] = own_tiles(inp["x"], c)
        d["pos"] = np.ascontiguousarray(own_tiles(inp["positions"], c).reshape(16, 128).T)
        d["maskT"] = diff_masks(c)
        d["negmask"] = dsa_negmask(c)
        ins.append(d)
    return ins


def kernel(**inputs):
    inp = {k: np.asarray(v) for k, v in inputs.items()}
    r = run(build_fused(), fused_inputs(inp))
    return assemble(r)
```

```python
import math
from contextlib import ExitStack
import numpy as np
import ml_dtypes
import concourse.bass as bass
import concourse.mybir as mybir
from concourse.bass_utils import run_bass_kernel_spmd


F32 = mybir.dt.float32
BF16 = mybir.dt.bfloat16
I32 = mybir.dt.int32
AF = mybir.ActivationFunctionType
ALU = mybir.AluOpType
AX = mybir.AxisListType


class Dep:
    __slots__ = ("w", "r")

    def __init__(self):
        self.w = None
        self.r = {}


class Sched:
    ENG = ("pe", "act", "dve", "pool", "sp")

    def __init__(self, nc):
        self.nc = nc
        self.ops = {e: [] for e in self.ENG}
        self.cnt = {e: 0 for e in self.ENG}
        self.known = {e: {} for e in self.ENG}
        self.dma_cnt = {}
        self.stack = ExitStack()
        self.nt = 0

    def sb(self, shape, dtype, name=None):
        self.nt += 1
        name = "sb_" + (name or f"t{self.nt}")
        return self.stack.enter_context(self.nc.sbuf_tensor(name, list(shape), dtype))

    def ps(self, shape, dtype, name=None):
        self.nt += 1
        name = "ps_" + (name or f"p{self.nt}")
        return self.stack.enter_context(self.nc.psum_tensor(name, list(shape), dtype))

    def op(self, eng, fn, reads=(), writes=(), dma=None, sem_inc=16):
        waits = {}

        def need(ev, raw):
            if ev is None:
                return
            key, val = ev
            if key == eng:
                if eng == "pe" or not raw:
                    return
            if waits.get(key, 0) < val:
                waits[key] = val

        for d in reads:
            need(d.w, True)
        for d in writes:
            need(d.w, False)
            for ev in d.r.items():
                need(ev, False)
        kn = self.known[eng]
        wl = []
        for key, val in waits.items():
            if kn.get(key, 0) >= val:
                continue
            kn[key] = val
            wl.append((key, val))
        if dma is not None:
            n = self.dma_cnt.get(dma, 0) + sem_inc
            self.dma_cnt[dma] = n
            ev = (dma, n)
            inc = (dma, sem_inc)
        else:
            self.cnt[eng] += 1
            ev = (eng, self.cnt[eng])
            inc = (eng, 1)
        self.ops[eng].append((wl, fn, inc))
        for d in reads:
            if d.r.get(ev[0], 0) < ev[1]:
                d.r[ev[0]] = ev[1]
        for d in writes:
            d.w = ev
            d.r = {}
        return ev

    def final_wait(self, eng, deps):
        waits = {}
        for d in deps:
            for ev in ([d.w] if d.w else []) + list(d.r.items()):
                if waits.get(ev[0], 0) < ev[1]:
                    waits[ev[0]] = ev[1]
        self.ops[eng].append((list(waits.items()), None, None))

    def barrier(self):
        waits = {e: self.cnt[e] for e in self.ENG if self.cnt[e] > 0}
        for k, n in self.dma_cnt.items():
            if not k.startswith("cc_"):
                waits[k] = n
        for e in self.ENG:
            kn = self.known[e]
            wl = []
            for key, val in waits.items():
                if key == e or kn.get(key, 0) >= val:
                    continue
                kn[key] = val
                wl.append((key, val))
            self.ops[e].append((wl, None, None))

    def final_events(self, eng, evs):
        waits = {}
        for ev in evs:
            if waits.get(ev[0], 0) < ev[1]:
                waits[ev[0]] = ev[1]
        self.ops[eng].append((list(waits.items()), None, None))

    def emit(self):
        nc = self.nc
        keys = set(self.ENG) | set(self.dma_cnt.keys())
        assert len(keys) <= 100, f"too many semaphores: {len(keys)}"
        sems = {}
        for k in sorted(keys):
            sems[k] = self.stack.enter_context(nc.semaphore("s_" + k))
        ops = self.ops

        def run(e, lst):
            for wl, fn, inc in lst:
                for key, val in wl:
                    e.wait_ge(sems[key], val)
                if fn is None:
                    continue
                ins = fn(e)
                if inc is not None:
                    ins.then_inc(sems[inc[0]], inc[1])

        with nc.Block() as block:
            @block.tensor
            def _(e):
                run(e, ops["pe"])

            @block.scalar
            def _(e):
                run(e, ops["act"])

            @block.vector
            def _(e):
                run(e, ops["dve"])

            @block.gpsimd
            def _(e):
                run(e, ops["pool"])

            @block.sync
            def _(e):
                run(e, ops["sp"])
        self.stack.close()


NS = 16
D = 1024
EPS = 1e-6
INV_FREQ = [500000.0 ** (-(2.0 * j) / 16.0) for j in range(8)]
TWO_PI_S = 6.28318
PI_S = 3.14159


class T:
    def __init__(self, t, d=None):
        self.t = t
        self.d = d if d is not None else Dep()

    def __getitem__(self, k):
        return self.t[k]


ARENA_BASE = 16512
ARENA_END = 16512 + 212736


class B:
    def __init__(self, nc):
        self.nc = nc
        self.S = Sched(nc)
        self._consts = {}
        self.final = []
        self.arena = nc.alloc_sbuf_tensor("arena", [128, ARENA_END - ARENA_BASE], mybir.dt.uint8)
        self.lo = ARENA_BASE
        self.hi = ARENA_END
        self.nt = 0
        self.banks = [T(nc.alloc_psum_tensor(f"bank{i}", [128, 512], F32)) for i in range(8)]
        self.banks16 = [T(bk.t[:].bitcast(BF16), bk.d) for bk in self.banks]

    def sb(self, shape, dt, name=None, top=False):
        self.nt += 1
        nm = f"sb{self.nt}_{name or 't'}"
        size = int(np.prod(shape[1:])) * mybir.dt.size(dt)
        size = (size + 31) // 32 * 32
        if top:
            self.hi -= size
            off = self.hi
        else:
            off = self.lo
            self.lo += size
        assert self.lo <= self.hi, f"SBUF arena overflow at {nm}: lo={self.lo} hi={self.hi}"
        return T(self.nc.alloc_sbuf_tensor_at(nm, list(shape), dt, offset=off))

    def mark(self):
        return (self.lo, self.hi)

    def release(self, m):
        self.lo, self.hi = m
        self.S.barrier()

    def dram(self, name, shape, dt, kind):
        t = T(self.nc.dram_tensor(name, list(shape), dt, kind=kind).ap())
        return t

    def op(self, eng, fn, reads=(), writes=(), dma=None, sem_inc=16):
        return self.S.op(eng, fn, [x.d for x in reads], [x.d for x in writes], dma, sem_inc)

    def store(self, eng, fn, reads, dma):
        ev = self.S.op(eng, fn, [x.d for x in reads], [], dma)
        self.final.append(ev)
        return ev

    def finish(self):
        self.S.final_events("sp", self.final)
        self.S.emit()

    def ident(self):
        if "ident" not in self._consts:
            idt = self.sb([128, 128], BF16, "ident")
            self.op("pool", lambda e: e.memset(idt[:], 0.0), writes=[idt])
            self.op("pool", lambda e: e.affine_select(out=idt[:], in_=idt[:], pattern=[[-1, 128]],
                                                     compare_op=ALU.not_equal, fill=1.0, base=0,
                                                     channel_multiplier=1), reads=[idt], writes=[idt])
            self._consts["ident"] = idt
        return self._consts["ident"]

    def eps(self):
        if "eps" not in self._consts:
            t = self.sb([128, 1], F32, "eps")
            self.op("pool", lambda e: e.memset(t[:], EPS), writes=[t])
            self._consts["eps"] = t
        return self._consts["eps"]


def rstd_from_ss(b, ss, n, scale):
    eps = b.eps()
    b.op("act", lambda e: e.activation(out=ss[:, 0:n], in_=ss[:, 0:n], func=AF.Ln, bias=eps[:], scale=scale),
         reads=[ss, eps], writes=[ss])
    b.op("act", lambda e: e.activation(out=ss[:, 0:n], in_=ss[:, 0:n], func=AF.Exp, scale=-0.5),
         reads=[ss], writes=[ss])


def rope_tables(b, pos_t):
    posf = b.sb([128, NS], F32, "posf")
    inv = b.sb([128, 8], F32, "invf")
    ang = b.sb([128, NS, 8], F32, "ang")
    ti = b.sb([128, NS, 8], I32, "angi")
    tf = b.sb([128, NS, 8], F32, "angf")
    neg = b.sb([128, NS, 8], F32, "angn")
    cos = b.sb([128, NS, 8], F32, "cos")
    sin = b.sb([128, NS, 8], F32, "sin")
    nb = b.sb([128, 1], F32, "negpi")
    b.op("pool", lambda e: e.memset(nb[:], -PI_S), writes=[nb])
    b.op("dve", lambda e: e.tensor_copy(out=posf[:], in_=pos_t[:]), reads=[pos_t], writes=[posf])
    for j in range(8):
        b.op("pool", lambda e, j=j: e.memset(inv[:, j:j + 1], INV_FREQ[j] / (2 * math.pi)), writes=[inv])
    b.op("dve", lambda e: e.tensor_tensor(out=ang[:], in0=posf[:].unsqueeze(2).to_broadcast([128, NS, 8]),
                                          in1=inv[:].unsqueeze(1).to_broadcast([128, NS, 8]), op=ALU.mult),
         reads=[posf, inv], writes=[ang])
    for (dst, off) in ((sin, 0.5), (cos, 0.75)):
        b.op("dve", lambda e, off=off: e.tensor_scalar(out=tf[:], in0=ang[:], scalar1=off, scalar2=None, op0=ALU.add),
             reads=[ang], writes=[tf])
        b.op("dve", lambda e: e.tensor_copy(out=ti[:], in_=tf[:]), reads=[tf], writes=[ti])
        b.op("dve", lambda e: e.tensor_copy(out=neg[:], in_=ti[:]), reads=[ti], writes=[neg])
        b.op("dve", lambda e: e.tensor_tensor(out=tf[:], in0=tf[:], in1=neg[:], op=ALU.subtract),
             reads=[tf, neg], writes=[tf])
        b.op("dve", lambda e: e.tensor_scalar(out=neg[:], in0=tf[:], scalar1=0.0, scalar2=None, op0=ALU.is_lt),
             reads=[tf], writes=[neg])
        b.op("dve", lambda e: e.tensor_tensor(out=tf[:], in0=tf[:], in1=neg[:], op=ALU.add),
             reads=[tf, neg], writes=[tf])
        b.op("act", lambda e, dst=dst: e.activation(out=dst[:], in_=tf[:], func=AF.Sin, bias=nb[:], scale=TWO_PI_S),
             reads=[tf, nb], writes=[dst])
    return cos, sin


def rmsnorm_transpose(b, xt, gt, hbf, hT, pT, ss, junk):
    idt = b.ident()
    b.op("act", lambda e: e.activation(out=junk[:], in_=xt[:], func=AF.Square, accum_out=ss[:, 0:1]),
         reads=[xt], writes=[junk, ss])
    rstd_from_ss(b, ss, 1, 1.0 / D)
    b.op("dve", lambda e: e.scalar_tensor_tensor(out=hbf[:], in0=xt[:], scalar=ss[:, 0:1], in1=gt[:],
                                                 op0=ALU.mult, op1=ALU.mult),
         reads=[xt, ss, gt], writes=[hbf])
    for kc in range(8):
        b.op("pe", lambda e, kc=kc: e.transpose(out=pT[:, kc * 128:(kc + 1) * 128],
                                                in_=hbf[:, kc * 128:(kc + 1) * 128], identity=idt[:]),
             reads=[hbf, idt], writes=[pT])
    b.op("dve", lambda e: e.tensor_copy(out=hT[:].rearrange("p a b -> p (a b)"), in_=pT[:]),
         reads=[pT], writes=[hT])


def headnorm_rope(b, stage, sq, ssq, nh, gain, cos_a, sin_a, outbf, tmp, norm=True):
    s3 = stage[:].rearrange("p (h d) -> p h d", d=64)
    if norm:
        b.op("dve", lambda e: e.tensor_reduce(out=ssq[:, 0:nh], in_=sq[:].rearrange("p (h d) -> p h d", d=64),
                                              axis=AX.X, op=ALU.add), reads=[sq], writes=[ssq])
        rstd_from_ss(b, ssq, nh, 1.0 / 64)
    b.op("dve", lambda e: e.tensor_tensor(out=s3, in0=s3, in1=ssq[:, 0:nh].unsqueeze(2).to_broadcast([128, nh, 64]),
                                          op=ALU.mult), reads=[stage, ssq], writes=[stage])
    if gain is not None:
        b.op("pool", lambda e: e.tensor_tensor(out=stage[:], in0=stage[:], in1=gain[:], op=ALU.mult),
             reads=[stage, gain], writes=[stage])
    b.op("act", lambda e: e.activation(out=outbf[:], in_=stage[:], func=AF.Copy), reads=[stage], writes=[outbf])
    o3 = outbf[:].rearrange("p (h d) -> p h d", d=64)
    x1 = s3[:, :, 0:8]
    x2 = s3[:, :, 8:16]
    cb = cos_a.unsqueeze(1).to_broadcast([128, nh, 8])
    sb_ = sin_a.unsqueeze(1).to_broadcast([128, nh, 8])
    t = tmp[:].rearrange("p (k h d) -> p k h d", k=4, d=8)
    eng = "pool"
    b.op(eng, lambda e: e.tensor_tensor(out=t[:, 0, 0:nh, :], in0=x1, in1=cb, op=ALU.mult), reads=[stage], writes=[tmp])
    b.op(eng, lambda e: e.tensor_tensor(out=t[:, 1, 0:nh, :], in0=x2, in1=sb_, op=ALU.mult), reads=[stage], writes=[tmp])
    b.op(eng, lambda e: e.tensor_tensor(out=t[:, 2, 0:nh, :], in0=x2, in1=cb, op=ALU.mult), reads=[stage], writes=[tmp])
    b.op(eng, lambda e: e.tensor_tensor(out=t[:, 3, 0:nh, :], in0=x1, in1=sb_, op=ALU.mult), reads=[stage], writes=[tmp])
    b.op(eng, lambda e: e.tensor_tensor(out=o3[:, :, 0:8], in0=t[:, 0, 0:nh, :], in1=t[:, 1, 0:nh, :], op=ALU.subtract),
         reads=[tmp], writes=[outbf])
    b.op(eng, lambda e: e.tensor_tensor(out=o3[:, :, 8:16], in0=t[:, 2, 0:nh, :], in1=t[:, 3, 0:nh, :], op=ALU.add),
         reads=[tmp], writes=[outbf])


def phase_diff_proj(b, io):
    idt = b.ident()
    pos_t = b.sb([128, NS], I32, "pos")
    b.op("sp", lambda e: e.dma_start(out=pos_t[:], in_=io["pos"][:]), writes=[pos_t], dma="ld_pos")
    gmix = b.sb([128, D], F32, "gmix")
    b.op("sp", lambda e: e.dma_start(out=gmix[:], in_=io["gmix"][:]), writes=[gmix], dma="ld_gmix")
    gqk = b.sb([128, 2048], F32, "gqk")
    b.op("sp", lambda e: e.dma_start(out=gqk[:], in_=io["gqk"][:]), writes=[gqk], dma="ld_gqk")
    b.op("pool", lambda e: e.tensor_scalar(out=gqk[:, 0:1024], in0=gqk[:, 0:1024], scalar1=0.125, scalar2=None,
                                           op0=ALU.mult), reads=[gqk], writes=[gqk])
    w = b.sb([128, 8, 3072], BF16, "w_in")
    for kc in range(8):
        for hf in range(3):
            b.op("pool", lambda e, kc=kc, hf=hf: e.dma_start(
                out=w[:, kc, hf * 1024:(hf + 1) * 1024],
                in_=io["w_in"][kc * 128:(kc + 1) * 128, hf * 1024:(hf + 1) * 1024]),
                writes=[w], dma="ld_w")
    cos, sin = rope_tables(b, pos_t)

    xts = [b.sb([128, D], F32, f"xt{i}") for i in range(2)]
    junk = b.sb([128, D], BF16, "junk")
    ss = b.sb([128, 1], F32, "ss")
    hbf = b.sb([128, D], BF16, "hbf")
    hT = b.sb([128, 8, 128], BF16, "hT")
    pT = [b.banks16[0], b.banks16[1]]
    pY = [b.banks[2 + i] for i in range(4)]
    stage = b.sb([128, 2048], F32, "stage")
    sq = b.sb([128, 2048], F32, "sq")
    ssq = b.sb([128, 32], F32, "ssq")
    tmp = b.sb([128, 4 * 32 * 8], F32, "ropetmp")
    qkbf = b.sb([128, 2048], BF16, "qkbf")
    qkT = [b.sb([128, 16, 128], BF16, f"qkT{i}") for i in range(2)]
    vaug = [b.sb([128, 8, 129], BF16, f"vaug{i}") for i in range(2)]
    for i in range(2):
        b.op("pool", lambda e, i=i: e.memset(vaug[i][:], 1.0), writes=[vaug[i]])

    for a in range(NS):
        xt = xts[a % 2]
        b.op("sp", lambda e, a=a, xt=xt: e.dma_start(out=xt[:], in_=io["x"][a * 128:(a + 1) * 128, :]),
             writes=[xt], dma=f"ld_x{a % 2}")
        rmsnorm_transpose(b, xt, gmix, hbf, hT, pT[0], ss, junk)
        for n in range(6):
            py = pY[n % 4]
            for kc in range(8):
                b.op("pe", lambda e, n=n, kc=kc, py=py: e.matmul(py[:], lhsT=hT[:, kc, :],
                                                                   rhs=w[:, kc, n * 512:(n + 1) * 512],
                                                                   start=(kc == 0), stop=(kc == 7)),
                     reads=[hT, w], writes=[py])
            if n < 4:
                b.op("act", lambda e, n=n, py=py: e.activation(out=stage[:, n * 512:(n + 1) * 512], in_=py[:], func=AF.Copy),
                     reads=[py], writes=[stage])
                b.op("act", lambda e, n=n, py=py: e.activation(out=sq[:, n * 512:(n + 1) * 512], in_=py[:], func=AF.Square),
                     reads=[py], writes=[sq])
            else:
                va = vaug[a % 2]
                b.op("dve", lambda e, n=n, py=py, va=va: e.tensor_copy(
                    out=va[:, (n - 4) * 4:(n - 4) * 4 + 4, 0:128], in_=py[:].rearrange("p (h d) -> p h d", d=128)),
                    reads=[py], writes=[va])
        va = vaug[a % 2]
        b.store("sp", lambda e, a=a, va=va: e.dma_start(out=io["V"][a], in_=va[:].rearrange("p h d -> p (h d)")),
                reads=[va], dma=f"st_v{a % 2}")
        headnorm_rope(b, stage, sq, ssq, 32, gqk, cos[:, a, :], sin[:, a, :], qkbf, tmp)
        qt = qkT[a % 2]
        for i in range(16):
            pt = pT[1] if i < 8 else pT[0]
            b.op("pe", lambda e, i=i, pt=pt: e.transpose(out=pt[:, (i % 8) * 128:(i % 8 + 1) * 128],
                                                         in_=qkbf[:, i * 128:(i + 1) * 128], identity=idt[:]),
                 reads=[qkbf, idt], writes=[pt])
            if i % 8 == 7:
                b.op("dve", lambda e, i=i, pt=pt, qt=qt: e.tensor_copy(
                    out=qt[:, (i // 8) * 8:(i // 8) * 8 + 8, :].rearrange("p a b -> p (a b)"), in_=pt[:]),
                    reads=[pt], writes=[qt])
        b.store("sp", lambda e, a=a, qt=qt: e.dma_start(out=io["QT"][a], in_=qt[:, 0:8, :].rearrange("p a b -> p (a b)")),
                reads=[qt], dma=f"st_q{a % 2}")
        b.store("sp", lambda e, a=a, qt=qt: e.dma_start(out=io["KT"][a], in_=qt[:, 8:16, :].rearrange("p a b -> p (a b)")),
                reads=[qt], dma=f"st_k{a % 2}")


def phase_diff_attn(b, io, o_tok):
    LAM_INIT = 0.2
    qT = b.sb([128, NS, 1024], BF16, "qT")
    for a in range(NS):
        b.op("sp", lambda e, a=a: e.dma_start(out=qT[:, a, :], in_=io["QT"][a]), writes=[qT], dma="ld_qT")
    maskT = b.sb([128, 512], BF16, "maskT")
    b.op("sp", lambda e: e.dma_start(out=maskT[:], in_=io["maskT"][:]), writes=[maskT], dma="ld_mask")
    lam = b.sb([128, 256], F32, "lam")
    b.op("sp", lambda e: e.dma_start(out=lam[:], in_=io["lam"][:]), writes=[lam], dma="ld_lam")
    gsub = b.sb([128, 128], F32, "gsub")
    b.op("sp", lambda e: e.dma_start(out=gsub[:], in_=io["gsub"][:]), writes=[gsub], dma="ld_gsub")
    b.op("pool", lambda e: e.tensor_scalar(out=gsub[:], in0=gsub[:], scalar1=1.0 - LAM_INIT, scalar2=None,
                                           op0=ALU.mult), reads=[gsub], writes=[gsub])
    lprod = b.sb([128, 128], F32, "lprod")
    l2 = b.sb([128, 2], F32, "l2")
    neglam = b.sb([128, 1], F32, "neglam")
    l4 = lam[:].rearrange("p (a b d) -> p a b d", a=2, b=2)
    b.op("dve", lambda e: e.tensor_tensor(out=lprod[:].rearrange("p (a d) -> p a d", a=2), in0=l4[:, :, 0, :],
                                          in1=l4[:, :, 1, :], op=ALU.mult), reads=[lam], writes=[lprod])
    b.op("dve", lambda e: e.tensor_reduce(out=l2[:], in_=lprod[:].rearrange("p (a d) -> p a d", a=2), axis=AX.X,
                                          op=ALU.add), reads=[lprod], writes=[l2])
    b.op("act", lambda e: e.activation(out=l2[:], in_=l2[:], func=AF.Exp), reads=[l2], writes=[l2])
    b.op("dve", lambda e: e.tensor_tensor(out=neglam[:], in0=l2[:, 1:2], in1=l2[:, 0:1], op=ALU.subtract),
         reads=[l2], writes=[neglam])
    b.op("dve", lambda e: e.tensor_scalar(out=neglam[:], in0=neglam[:], scalar1=-LAM_INIT, scalar2=None, op0=ALU.add),
         reads=[neglam], writes=[neglam])

    ktb = [b.sb([128, 8192], BF16, f"ktb{i}") for i in range(2)]
    vb = [b.sb([128, 64, 129], BF16, f"vb{i}") for i in range(2)]
    pS = [[b.banks[c * 2 + i] for i in range(2)] for c in range(2)]
    pA = [[b.banks[4 + c * 2 + i] for i in range(2)] for c in range(2)]
    pT = [[b.sb([128, 512], BF16, f"pTs{c}{i}") for i in range(2)] for c in range(2)]
    rec = [b.sb([128, 2], F32, f"rec{i}") for i in range(2)]
    o32 = [b.sb([128, 128], F32, f"o32{i}") for i in range(2)]
    oj = b.sb([128, 128], BF16, "ojunk")
    ss1 = [b.sb([128, 1], F32, f"ss1{i}") for i in range(2)]

    units = [(h, a, blk) for h in range(8) for a in range(NS) for blk in range(a + 1)]

    def load_head(h):
        kb, vv = ktb[h % 2], vb[h % 2]
        for part in range(4):
            b.op("sp", lambda e, h=h, kb=kb, part=part: e.dma_start(
                out=kb[:, part * 2048:(part + 1) * 2048], in_=io["KTall"][h][:, part * 2048:(part + 1) * 2048]),
                writes=[kb], dma=f"ld_kt{h % 2}")
            b.op("sp", lambda e, h=h, vv=vv, part=part: e.dma_start(
                out=vv[:, part * 16:(part + 1) * 16, :].rearrange("p a b -> p (a b)"),
                in_=io["Vall"][h][:, part * 16 * 129:(part + 1) * 16 * 129]),
                writes=[vv], dma=f"ld_v{h % 2}")

    def qk(u, n):
        h, a, blk = u
        kb = ktb[h % 2]
        for c in range(2):
            ps = pS[c][n % 2]
            for i in range(4):
                kt = 4 * blk + i
                b.op("pe", lambda e, c=c, i=i, kt=kt, ps=ps, kb=kb, a=a, h=h: e.matmul(
                    ps[:, i * 128:(i + 1) * 128], lhsT=kb[64 * c:64 * c + 64, kt * 128:(kt + 1) * 128],
                    rhs=qT[64 * c:64 * c + 64, a, h * 128:(h + 1) * 128], start=True, stop=True),
                    reads=[kb, qT], writes=[ps])

    def softmax_pv(u, n):
        h, a, blk = u
        vv = vb[h % 2]
        for c in range(2):
            ps = pS[c][n % 2]
            pt = pT[c][n % 2]
            acc = pA[c][a % 2]
            b.op("act", lambda e, ps=ps, pt=pt: e.activation(out=pt[:], in_=ps[:], func=AF.Exp),
                 reads=[ps], writes=[pt])
            if blk == a:
                b.op("pool", lambda e, pt=pt: e.tensor_tensor(out=pt[:], in0=pt[:], in1=maskT[:], op=ALU.mult),
                     reads=[pt, maskT], writes=[pt])
            for i in range(4):
                kt = 4 * blk + i
                b.op("pe", lambda e, i=i, kt=kt, pt=pt, acc=acc, vv=vv, blk=blk, a=a: e.matmul(
                    acc[:, 0:129], lhsT=pt[:, i * 128:(i + 1) * 128], rhs=vv[:, kt, :],
                    start=(blk == 0 and i == 0), stop=(blk == a and i == 3)),
                    reads=[pt, vv], writes=[acc])
        if blk == a:
            evac(h, a)

    def evac(h, a):
        k = a % 2
        a0, a1 = pA[0][k], pA[1][k]
        r, o, s = rec[k], o32[k], ss1[k]
        b.op("dve", lambda e: e.reciprocal(out=r[:, 0:1], in_=a0[:, 128:129]), reads=[a0], writes=[r])
        b.op("dve", lambda e: e.reciprocal(out=r[:, 1:2], in_=a1[:, 128:129]), reads=[a1], writes=[r])
        b.op("dve", lambda e: e.tensor_tensor(out=r[:, 1:2], in0=r[:, 1:2], in1=neglam[:], op=ALU.mult),
             reads=[r, neglam], writes=[r])
        b.op("dve", lambda e: e.tensor_scalar(out=o[:], in0=a0[:, 0:128], scalar1=r[:, 0:1], scalar2=None, op0=ALU.mult),
             reads=[a0, r], writes=[o])
        b.op("dve", lambda e: e.scalar_tensor_tensor(out=o[:], in0=a1[:, 0:128], scalar=r[:, 1:2], in1=o[:],
                                                     op0=ALU.mult, op1=ALU.add), reads=[a1, r, o], writes=[o])
        b.op("act", lambda e: e.activation(out=oj[:], in_=o[:], func=AF.Square, accum_out=s[:, 0:1]),
             reads=[o], writes=[oj, s])
        rstd_from_ss(b, s, 1, 1.0 / 128)
        ot = o_tok[a]
        b.op("dve", lambda e: e.scalar_tensor_tensor(out=ot[:, h * 128:(h + 1) * 128], in0=o[:], scalar=s[:, 0:1],
                                                     in1=gsub[:], op0=ALU.mult, op1=ALU.mult),
             reads=[o, s, gsub], writes=[ot])

    load_head(0)
    qk(units[0], 0)
    for n, u in enumerate(units):
        if u[1] == 0 and u[2] == 0 and u[0] + 1 < 8:
            load_head(u[0] + 1)
        if n + 1 < len(units):
            qk(units[n + 1], n + 1)
        softmax_pv(u, n)


def load_w_bf16(b, wt, src, nk, ncols, key, colchunk=1024):
    for c0 in range(0, ncols, colchunk):
        for kc in range(nk):
            c1 = min(ncols, c0 + colchunk)
            b.op("pool", lambda e, kc=kc, c0=c0, c1=c1: e.dma_start(
                out=wt[:, kc, c0:c1], in_=src[kc * 128:(kc + 1) * 128, c0:c1]), writes=[wt], dma=key)


def phase_post_attn(b, io, o_tok, x_res, h2T, wout_ap, gmlp_ap, xsrc, xdeps=None):
    idt = b.ident()
    m = b.mark()
    wout = b.sb([128, 8, 1024], BF16, "wout")
    load_w_bf16(b, wout, wout_ap, 8, 1024, "ld_wout")
    gm = b.sb([128, D], F32, "gmlp")
    b.op("sp", lambda e: e.dma_start(out=gm[:], in_=gmlp_ap), writes=[gm], dma="ld_gmlp")
    oT = [b.sb([128, 8, 128], BF16, f"oT{i}") for i in range(2)]
    junk = b.sb([128, D], BF16, "junk2")
    ss = [b.sb([128, 1], F32, f"ss2{i}") for i in range(2)]
    hbf = [b.sb([128, D], BF16, f"hbf2{i}") for i in range(2)]
    for a in range(NS):
        xr = x_res[a]
        b.op("sp", lambda e, a=a, xr=xr: e.dma_start(out=xr[:], in_=xsrc[a * 128:(a + 1) * 128, :]),
             reads=([xdeps[a]] if xdeps else []), writes=[xr], dma="ld_xres")
        pt = b.banks16[a % 2]
        ot = oT[a % 2]
        for kc in range(8):
            b.op("pe", lambda e, kc=kc, pt=pt, a=a: e.transpose(out=pt[:, kc * 128:(kc + 1) * 128],
                                                                 in_=o_tok[a][:, kc * 128:(kc + 1) * 128], identity=idt[:]),
                 reads=[o_tok[a], idt], writes=[pt])
        b.op("act", lambda e, pt=pt, ot=ot: e.activation(out=ot[:].rearrange("p a b -> p (a b)"), in_=pt[:], func=AF.Copy),
             reads=[pt], writes=[ot])
        for n in range(2):
            py = b.banks[2 + (2 * a + n) % 4]
            for kc in range(8):
                b.op("pe", lambda e, kc=kc, n=n, py=py, ot=ot: e.matmul(py[:], lhsT=ot[:, kc, :],
                                                                         rhs=wout[:, kc, n * 512:(n + 1) * 512],
                                                                         start=(kc == 0), stop=(kc == 7)),
                     reads=[ot, wout], writes=[py])
            b.op("dve", lambda e, n=n, py=py, xr=xr: e.tensor_tensor(out=xr[:, n * 512:(n + 1) * 512],
                                                                     in0=xr[:, n * 512:(n + 1) * 512], in1=py[:], op=ALU.add),
                 reads=[xr, py], writes=[xr])
        rms_to_hT(b, xr, gm, hbf[a % 2], h2T, a, b.banks16[6 + a % 2], ss[a % 2], junk)
    return m


def rms_to_hT(b, xr, gm, hbf, h2T, a, pt, ss, junk):
    idt = b.ident()
    b.op("act", lambda e: e.activation(out=junk[:], in_=xr[:], func=AF.Square, accum_out=ss[:, 0:1]),
         reads=[xr], writes=[junk, ss])
    rstd_from_ss(b, ss, 1, 1.0 / D)
    b.op("dve", lambda e: e.scalar_tensor_tensor(out=hbf[:], in0=xr[:], scalar=ss[:, 0:1], in1=gm[:],
                                                 op0=ALU.mult, op1=ALU.mult), reads=[xr, ss, gm], writes=[hbf])
    for kc in range(8):
        b.op("pe", lambda e, kc=kc: e.transpose(out=pt[:, kc * 128:(kc + 1) * 128],
                                                in_=hbf[:, kc * 128:(kc + 1) * 128], identity=idt[:]),
             reads=[hbf, idt], writes=[pt])
    b.op("act", lambda e: e.activation(out=h2T[:, :, a * 128:(a + 1) * 128],
                                       in_=pt[:].rearrange("p (k t) -> p k t", k=8), func=AF.Copy),
         reads=[pt], writes=[h2T])


def phase_mlp(b, x_res, h2T, w1_ap, w2_ap, hook=None):
    NFC = 8
    w1c = [b.sb([128, 8, 512], BF16, f"w1c{i}") for i in range(2)]
    w2c = [b.sb([128, 4, 1024], BF16, f"w2c{i}") for i in range(2)]
    rbuf = [b.sb([128, 512], F32, f"rbuf{i}") for i in range(2)]
    uT = [b.sb([128, 4, 512], BF16, f"uT{i}") for i in range(2)]
    pu = [b.banks[0], b.banks[1]]
    po = [b.banks[2 + i] for i in range(4)]

    def load_chunk(fc):
        w1, w2 = w1c[fc % 2], w2c[fc % 2]
        for kc in range(8):
            b.op("pool", lambda e, kc=kc, fc=fc, w1=w1: e.dma_start(
                out=w1[:, kc, :], in_=w1_ap[kc * 128:(kc + 1) * 128, fc * 512:(fc + 1) * 512]),
                writes=[w1], dma=f"ld_w1{fc % 2}")
        for ft in range(4):
            b.op("pool", lambda e, ft=ft, fc=fc, w2=w2: e.dma_start(
                out=w2[:, ft, :], in_=w2_ap[fc * 512 + ft * 128:fc * 512 + (ft + 1) * 128, :]),
                writes=[w2], dma=f"ld_w2{fc % 2}")

    steps = [(fc, tg) for fc in range(NFC) for tg in range(4)]
    cnt = {"u": 0, "o": 0}

    def stage_u(fc, tg):
        w1 = w1c[fc % 2]
        ut = uT[(fc * 4 + tg) % 2]
        for ft in range(4):
            p = pu[cnt["u"] % 2]
            r = rbuf[cnt["u"] % 2]
            cnt["u"] += 1
            for kc in range(8):
                b.op("pe", lambda e, kc=kc, ft=ft, p=p, w1=w1, tg=tg: e.matmul(
                    p[:], lhsT=w1[:, kc, ft * 128:(ft + 1) * 128], rhs=h2T[:, kc, tg * 512:(tg + 1) * 512],
                    start=(kc == 0), stop=(kc == 7)), reads=[w1, h2T], writes=[p])
            b.op("act", lambda e, p=p, r=r: e.activation(out=r[:], in_=p[:], func=AF.Relu), reads=[p], writes=[r])
            b.op("pool", lambda e, r=r, ut=ut, ft=ft: e.tensor_tensor(out=ut[:, ft, :], in0=r[:], in1=r[:], op=ALU.mult),
                 reads=[r], writes=[ut])

    def stage_o(fc, tg):
        w2 = w2c[fc % 2]
        ut = uT[(fc * 4 + tg) % 2]
        for tt in range(4):
            xr = x_res[tg * 4 + tt]
            for ch in range(2):
                p = po[cnt["o"] % 4]
                cnt["o"] += 1
                for ft in range(4):
                    b.op("pe", lambda e, ft=ft, tt=tt, ch=ch, p=p, ut=ut, w2=w2: e.matmul(
                        p[:], lhsT=ut[:, ft, tt * 128:(tt + 1) * 128], rhs=w2[:, ft, ch * 512:(ch + 1) * 512],
                        start=(ft == 0), stop=(ft == 3)), reads=[ut, w2], writes=[p])
                b.op("dve", lambda e, ch=ch, p=p, xr=xr: e.tensor_tensor(
                    out=xr[:, ch * 512:(ch + 1) * 512], in0=xr[:, ch * 512:(ch + 1) * 512], in1=p[:], op=ALU.add),
                    reads=[xr, p], writes=[xr])

    load_chunk(0)
    stage_u(*steps[0])
    for i, (fc, tg) in enumerate(steps):
        if tg == 0 and fc + 1 < NFC:
            load_chunk(fc + 1)
        if hook is not None and fc == 1 and tg == 1:
            hook()
        if i + 1 < len(steps):
            stage_u(*steps[i + 1])
        stage_o(fc, tg)


IDX_SCALE = (8 ** -0.5) * (64 ** -0.5)


def phase_dsa_proj(b, io, x_res, cos, sin):
    idt = b.ident()
    gmix = b.sb([128, D], F32, "gmix1")
    b.op("sp", lambda e: e.dma_start(out=gmix[:], in_=io["gmix1"][:]), writes=[gmix], dma="ld_gmix1")
    gqk = b.sb([128, 2048], F32, "gqk2")
    b.op("sp", lambda e: e.dma_start(out=gqk[:], in_=io["gqk2"][:]), writes=[gqk], dma="ld_gqk2")
    b.op("pool", lambda e: e.tensor_scalar(out=gqk[:, 0:1024], in0=gqk[:, 0:1024], scalar1=0.125, scalar2=None,
                                           op0=ALU.mult), reads=[gqk], writes=[gqk])
    gcq = b.sb([128, 256], F32, "gcq")
    b.op("sp", lambda e: e.dma_start(out=gcq[:], in_=io["gcq"][:]), writes=[gcq], dma="ld_gcq")
    w = b.sb([128, 8, 2376], BF16, "w_in2")
    load_w_bf16(b, w, io["w_in2"], 8, 2376, "ld_w2in", colchunk=792)
    wuq = b.sb([128, 2, 1024], BF16, "wuq")
    load_w_bf16(b, wuq, io["w_uq"], 2, 1024, "ld_wuq")
    wuqi = b.sb([128, 2, 512], BF16, "wuqi")
    load_w_bf16(b, wuqi, io["w_uqi"], 2, 512, "ld_wuqi")

    junk = b.sb([128, D], BF16, "junk3")
    ss = b.sb([128, 1], F32, "ss3")
    hbf = b.sb([128, D], BF16, "hbf3")
    hT = b.sb([128, 8, 128], BF16, "hT3")
    stage = b.sb([128, 2048], F32, "stage3")
    sq = b.sb([128, 2048], F32, "sq3")
    ssq = b.sb([128, 32], F32, "ssq3")
    tmp = b.sb([128, 4 * 32 * 8], F32, "ropetmp3")
    qkbf = b.sb([128, 2048], BF16, "qkbf3")
    qkT = [b.sb([128, 16, 128], BF16, f"qkT3{i}") for i in range(2)]
    vaug = [b.sb([128, 16, 65], BF16, f"vaug3{i}") for i in range(2)]
    for i in range(2):
        b.op("pool", lambda e, i=i: e.memset(vaug[i][:], 1.0), writes=[vaug[i]])
    cqs = b.sb([128, 256], F32, "cqs")
    ssc = b.sb([128, 1], F32, "ssc")
    cqbf = b.sb([128, 256], BF16, "cqbf")
    cqT = b.sb([128, 2, 128], BF16, "cqT")
    kis = b.sb([128, 64], F32, "kis")
    ksq = b.sb([128, 64], F32, "ksq")
    kss = b.sb([128, 1], F32, "kss")
    kibf = b.sb([128, 128], BF16, "kibf")
    kiT = [b.sb([128, 128], BF16, f"kiT{i}") for i in range(2)]
    wi = b.sb([128, 8], F32, "wi")
    sgn = [b.sb([128, 8], F32, f"sgn{i}") for i in range(2)]
    aw = b.sb([128, 8], F32, "aw")
    qis = b.sb([128, 512], F32, "qis")
    qibf = b.sb([128, 512], BF16, "qibf")
    qiT = [b.sb([128, 4, 128], BF16, f"qiT{i}") for i in range(2)]
    nbank = [0]

    def bank():
        nbank[0] += 1
        return b.banks[2 + nbank[0] % 4]

    def proj(py, lhs, nk, rhs_fn, ncol):
        for kc in range(nk):
            b.op("pe", lambda e, kc=kc: e.matmul(py[:, 0:ncol], lhsT=lhs[:, kc, :], rhs=rhs_fn(kc),
                                                 start=(kc == 0), stop=(kc == nk - 1)), reads=[lhs, w, wuq, wuqi], writes=[py])

    for a in range(NS):
        xr = x_res[a]
        rmsnorm_transpose(b, xr, gmix, hbf, hT, b.banks16[0], ss, junk)
        py = bank()
        proj(py, hT, 8, lambda kc: w[:, kc, 0:256], 256)
        b.op("act", lambda e, py=py: e.activation(out=cqs[:], in_=py[:, 0:256], func=AF.Copy), reads=[py], writes=[cqs])
        b.op("act", lambda e, py=py: e.activation(out=junk[:, 0:256], in_=py[:, 0:256], func=AF.Square, accum_out=ssc[:, 0:1]),
             reads=[py], writes=[junk, ssc])
        rstd_from_ss(b, ssc, 1, 1.0 / 256)
        b.op("dve", lambda e: e.scalar_tensor_tensor(out=cqbf[:], in0=cqs[:], scalar=ssc[:, 0:1], in1=gcq[:],
                                                     op0=ALU.mult, op1=ALU.mult), reads=[cqs, ssc, gcq], writes=[cqbf])
        p6 = b.banks16[6]
        for kc in range(2):
            b.op("pe", lambda e, kc=kc: e.transpose(out=p6[:, kc * 128:(kc + 1) * 128], in_=cqbf[:, kc * 128:(kc + 1) * 128],
                                                    identity=idt[:]), reads=[cqbf, idt], writes=[p6])
        b.op("dve", lambda e: e.tensor_copy(out=cqT[:].rearrange("p a b -> p (a b)"), in_=p6[:, 0:256]),
             reads=[p6], writes=[cqT])
        for n in range(2):
            py = bank()
            proj(py, hT, 8, lambda kc, n=n: w[:, kc, 256 + n * 512:256 + (n + 1) * 512], 512)
            b.op("act", lambda e, py=py, n=n: e.activation(out=stage[:, 1024 + n * 512:1024 + (n + 1) * 512], in_=py[:], func=AF.Copy),
                 reads=[py], writes=[stage])
            b.op("act", lambda e, py=py, n=n: e.activation(out=sq[:, 1024 + n * 512:1024 + (n + 1) * 512], in_=py[:], func=AF.Square),
                 reads=[py], writes=[sq])
        va = vaug[a % 2]
        for n in range(2):
            py = bank()
            proj(py, hT, 8, lambda kc, n=n: w[:, kc, 1280 + n * 512:1280 + (n + 1) * 512], 512)
            b.op("dve", lambda e, py=py, n=n, va=va: e.tensor_copy(out=va[:, n * 8:(n + 1) * 8, 0:64],
                                                                   in_=py[:].rearrange("p (h d) -> p h d", d=64)),
                 reads=[py], writes=[va])
        b.store("sp", lambda e, a=a, va=va: e.dma_start(out=io["V2"][a], in_=va[:].rearrange("p h d -> p (h d)")),
                reads=[va], dma=f"st_v2{a % 2}")
        py = bank()
        proj(py, hT, 8, lambda kc: w[:, kc, 2304:2376], 72)
        b.op("act", lambda e, py=py: e.activation(out=kis[:], in_=py[:, 0:64], func=AF.Copy), reads=[py], writes=[kis])
        b.op("act", lambda e, py=py: e.activation(out=ksq[:], in_=py[:, 0:64], func=AF.Square), reads=[py], writes=[ksq])
        b.op("dve", lambda e, py=py: e.tensor_copy(out=wi[:], in_=py[:, 64:72]), reads=[py], writes=[wi])
        for n in range(2):
            py = bank()
            proj(py, cqT, 2, lambda kc, n=n: wuq[:, kc, n * 512:(n + 1) * 512], 512)
            b.op("act", lambda e, py=py, n=n: e.activation(out=stage[:, n * 512:(n + 1) * 512], in_=py[:], func=AF.Copy),
                 reads=[py], writes=[stage])
            b.op("act", lambda e, py=py, n=n: e.activation(out=sq[:, n * 512:(n + 1) * 512], in_=py[:], func=AF.Square),
                 reads=[py], writes=[sq])
        py = bank()
        proj(py, cqT, 2, lambda kc: wuqi[:, kc, :], 512)
        b.op("act", lambda e, py=py: e.activation(out=qis[:], in_=py[:], func=AF.Copy), reads=[py], writes=[qis])
        headnorm_rope(b, stage, sq, ssq, 32, gqk, cos[:, a, :], sin[:, a, :], qkbf, tmp)
        qt = qkT[a % 2]
        for i in range(16):
            pt = b.banks16[1] if i < 8 else b.banks16[0]
            b.op("pe", lambda e, i=i, pt=pt: e.transpose(out=pt[:, (i % 8) * 128:(i % 8 + 1) * 128],
                                                         in_=qkbf[:, i * 128:(i + 1) * 128], identity=idt[:]),
                 reads=[qkbf, idt], writes=[pt])
            if i % 8 == 7:
                b.op("dve", lambda e, i=i, pt=pt, qt=qt: e.tensor_copy(
                    out=qt[:, (i // 8) * 8:(i // 8) * 8 + 8, :].rearrange("p a b -> p (a b)"), in_=pt[:]),
                    reads=[pt], writes=[qt])
        b.store("sp", lambda e, a=a, qt=qt: e.dma_start(out=io["QT2"][a], in_=qt[:, 0:8, :].rearrange("p a b -> p (a b)")),
                reads=[qt], dma=f"st_q2{a % 2}")
        b.store("sp", lambda e, a=a, qt=qt: e.dma_start(out=io["KT2"][a], in_=qt[:, 8:16, :].rearrange("p a b -> p (a b)")),
                reads=[qt], dma=f"st_k2{a % 2}")
        ki_half = T(kibf.t[:, 0:64], kibf.d)
        headnorm_rope(b, kis, ksq, kss, 1, None, cos[:, a, :], sin[:, a, :], ki_half, tmp)
        b.op("pool", lambda e: e.tensor_copy(out=kibf[:, 64:128], in_=kibf[:, 0:64]), reads=[kibf], writes=[kibf])
        p7 = b.banks16[7]
        b.op("pe", lambda e: e.transpose(out=p7[:, 0:128], in_=kibf[:], identity=idt[:]), reads=[kibf, idt], writes=[p7])
        kt_ = kiT[a % 2]
        b.op("dve", lambda e, kt_=kt_: e.tensor_copy(out=kt_[:], in_=p7[:, 0:128]), reads=[p7], writes=[kt_])
        b.store("sp", lambda e, a=a, kt_=kt_: e.dma_start(out=io["KI"][a], in_=kt_[:]), reads=[kt_], dma=f"st_ki{a % 2}")
        sg = sgn[a % 2]
        b.op("act", lambda e, sg=sg: e.activation(out=sg[:], in_=wi[:], func=AF.Sign), reads=[wi], writes=[sg])
        b.op("dve", lambda e, sg=sg: e.scalar_tensor_tensor(out=aw[:], in0=wi[:], scalar=IDX_SCALE, in1=sg[:],
                                                           op0=ALU.mult, op1=ALU.mult), reads=[wi, sg], writes=[aw])
        b.store("sp", lambda e, a=a, sg=sg: e.dma_start(out=io["SG"][a], in_=sg[:]), reads=[sg], dma=f"st_sg{a % 2}")
        headnorm_rope(b, qis, None, aw, 8, None, cos[:, a, :], sin[:, a, :], qibf, tmp, norm=False)
        qi_ = qiT[a % 2]
        for i in range(4):
            b.op("pe", lambda e, i=i: e.transpose(out=p7[:, 256 + i * 128:256 + (i + 1) * 128],
                                                  in_=qibf[:, i * 128:(i + 1) * 128], identity=idt[:]),
                 reads=[qibf, idt], writes=[p7])
        b.op("dve", lambda e, qi_=qi_: e.tensor_copy(out=qi_[:].rearrange("p a b -> p (a b)"), in_=p7[:, 256:768]),
             reads=[p7], writes=[qi_])
        b.store("sp", lambda e, a=a, qi_=qi_: e.dma_start(out=io["QI"][a], in_=qi_[:].rearrange("p a b -> p (a b)")),
                reads=[qi_], dma=f"st_qi{a % 2}")


NIT = 22
TOPK = 256


def phase_dsa_attn(b, io, o_tok):
    idt = b.ident()
    kia = b.sb([128, 8192], BF16, "kiall")
    for part in range(4):
        b.op("sp", lambda e, part=part: e.dma_start(out=kia[:, part * 2048:(part + 1) * 2048],
                                                    in_=io["KIall"][:, part * 2048:(part + 1) * 2048]),
             writes=[kia], dma="ld_kia")
    negm = b.sb([128, 512], F32, "negm")
    b.op("sp", lambda e: e.dma_start(out=negm[:], in_=io["negmask"][:]), writes=[negm], dma="ld_negm")
    cW = b.sb([128, NIT], F32, "cW")
    for i in range(NIT):
        b.op("pool", lambda e, i=i: e.memset(cW[:, i:i + 1], 2.0 ** (-i)), writes=[cW])
    Ib = b.sb([128, 8192], F32, "Ibuf")
    Mq = b.sb([128, 8192], BF16, "Mq")
    MT = b.sb([128, 64, 128], BF16, "MT")
    ktp = [b.sb([128, 8192], BF16, f"ktp{i}") for i in range(2)]
    vp = [b.sb([128, 64, 130], BF16, f"vp{i}") for i in range(2)]
    qTa = [b.sb([128, 8, 128], BF16, f"qTa{i}") for i in range(2)]
    qiTa = [b.sb([128, 4, 128], BF16, f"qiTa{i}") for i in range(2)]
    sgn = [b.sb([128, 8], F32, f"sgna{i}") for i in range(2)]
    tb = [b.sb([128, 512], F32, f"tb{i}") for i in range(2)]
    pT = [b.sb([128, 512], BF16, f"pTd{i}") for i in range(2)]
    m1 = b.sb([128, 1], F32, "bm1")
    lo = b.sb([128, 1], F32, "blo")
    mid = b.sb([128, 1], F32, "bmid")
    cnt = b.sb([128, 1], F32, "bcnt")
    g = b.sb([128, 1], F32, "bg")
    W = b.sb([128, NIT], F32, "bW")
    rec = [b.sb([128, 1], F32, f"recd{i}") for i in range(2)]
    pS = [b.banks[0], b.banks[1]]
    pA = [b.banks[2], b.banks[3]]
    pI = [b.banks[4], b.banks[5]]
    pM = [b.banks16[6], b.banks16[7]]
    ctr = {"i": 0, "s": 0, "acc": 0, "kv": 0}

    def load_slot_small(a):
        b.op("sp", lambda e, a=a: e.dma_start(out=qTa[a % 2][:].rearrange("p a b -> p (a b)"), in_=io["QT2"][a]),
             writes=[qTa[a % 2]], dma=f"ld_qTa{a % 2}")
        b.op("sp", lambda e, a=a: e.dma_start(out=qiTa[a % 2][:].rearrange("p a b -> p (a b)"), in_=io["QI"][a]),
             writes=[qiTa[a % 2]], dma=f"ld_qiTa{a % 2}")
        b.op("sp", lambda e, a=a: e.dma_start(out=sgn[a % 2][:], in_=io["SG"][a]), writes=[sgn[a % 2]], dma=f"ld_sgn{a % 2}")

    def load_kv(a, hp):
        k = ctr["kv"] % 2
        ctr["kv"] += 1
        nv = 512 * (a + 1)
        nt = 4 * (a + 1)
        b.op("sp", lambda e, k=k, hp=hp, nv=nv: e.dma_start(out=ktp[k][:, 0:nv], in_=io["KT2all"][hp][:, 0:nv]),
             writes=[ktp[k]], dma=f"ld_ktp{k}")
        b.op("sp", lambda e, k=k, hp=hp, nt=nt: e.dma_start(out=vp[k][:, 0:nt, :].rearrange("p a b -> p (a b)"),
                                                            in_=io["V2all"][hp][:, 0:nt * 130]),
             writes=[vp[k]], dma=f"ld_vp{k}")
        return k

    def indexer(a):
        nb = a + 1
        nv = 512 * nb
        qi, sg = qiTa[a % 2], sgn[a % 2]
        for blk in range(nb):
            for head in range(8):
                hp, hh = head // 2, head % 2
                py = pI[ctr["i"] % 2]
                t = tb[ctr["i"] % 2]
                ctr["i"] += 1
                b.op("pe", lambda e, py=py, hp=hp, hh=hh, blk=blk, qi=qi: e.matmul(
                    py[:], lhsT=qi[64 * hh:64 * hh + 64, hp, :], rhs=kia[64 * hh:64 * hh + 64, blk * 512:(blk + 1) * 512],
                    start=True, stop=True), reads=[qi, kia], writes=[py])
                b.op("act", lambda e, py=py, t=t: e.activation(out=t[:], in_=py[:], func=AF.Relu), reads=[py], writes=[t])
                if head == 0:
                    b.op("dve", lambda e, t=t, blk=blk, sg=sg: e.tensor_scalar(
                        out=Ib[:, blk * 512:(blk + 1) * 512], in0=t[:], scalar1=sg[:, 0:1], scalar2=None, op0=ALU.mult),
                        reads=[t, sg], writes=[Ib])
                else:
                    b.op("dve", lambda e, t=t, blk=blk, sg=sg, head=head: e.scalar_tensor_tensor(
                        out=Ib[:, blk * 512:(blk + 1) * 512], in0=t[:], scalar=sg[:, head:head + 1],
                        in1=Ib[:, blk * 512:(blk + 1) * 512], op0=ALU.mult, op1=ALU.add),
                        reads=[t, sg, Ib], writes=[Ib])
        b.op("dve", lambda e: e.tensor_reduce(out=m1[:], in_=Ib[:, 0:nv], axis=AX.X, op=ALU.max, apply_absolute_value=True),
             reads=[Ib], writes=[m1])
        b.op("dve", lambda e: e.tensor_tensor(out=Ib[:, nv - 512:nv], in0=Ib[:, nv - 512:nv], in1=negm[:], op=ALU.add),
             reads=[Ib, negm], writes=[Ib])
        b.op("dve", lambda e: e.tensor_scalar(out=m1[:], in0=m1[:], scalar1=1.0, scalar2=None, op0=ALU.add),
             reads=[m1], writes=[m1])
        b.op("dve", lambda e: e.tensor_scalar(out=lo[:], in0=m1[:], scalar1=-1.0, scalar2=None, op0=ALU.mult),
             reads=[m1], writes=[lo])
        b.op("dve", lambda e: e.tensor_scalar(out=W[:], in0=cW[:], scalar1=m1[:, 0:1], scalar2=None, op0=ALU.mult),
             reads=[cW, m1], writes=[W])
        b.op("dve", lambda e: e.tensor_tensor(out=mid[:], in0=lo[:], in1=W[:, 0:1], op=ALU.add), reads=[lo, W], writes=[mid])
        for i in range(NIT):
            b.op("dve", lambda e: e.tensor_scalar(out=Mq[:, 0:nv], in0=Ib[:, 0:nv], scalar1=mid[:, 0:1], scalar2=0.0,
                                                  op0=ALU.is_ge, op1=ALU.add, accum_out=cnt[:, 0:1]),
                 reads=[Ib, mid], writes=[Mq, cnt])
            b.op("dve", lambda e, i=i: e.tensor_scalar(out=g[:], in0=cnt[:], scalar1=TOPK - 0.5, scalar2=W[:, i:i + 1],
                                                       op0=ALU.is_ge, op1=ALU.mult), reads=[cnt, W], writes=[g])
            b.op("dve", lambda e: e.tensor_tensor(out=lo[:], in0=lo[:], in1=g[:], op=ALU.add), reads=[lo, g], writes=[lo])
            if i + 1 < NIT:
                b.op("dve", lambda e, i=i: e.tensor_tensor(out=mid[:], in0=lo[:], in1=W[:, i + 1:i + 2], op=ALU.add),
                     reads=[lo, W], writes=[mid])
        b.op("dve", lambda e: e.tensor_scalar(out=Mq[:, 0:nv], in0=Ib[:, 0:nv], scalar1=lo[:, 0:1], scalar2=None,
                                              op0=ALU.is_ge), reads=[Ib, lo], writes=[Mq])
        nt = 4 * nb
        for kt in range(nt):
            pm = pM[(kt // 8) % 2]
            b.op("pe", lambda e, kt=kt, pm=pm: e.transpose(out=pm[:, (kt % 8) * 128:(kt % 8 + 1) * 128],
                                                           in_=Mq[:, kt * 128:(kt + 1) * 128], identity=idt[:]),
                 reads=[Mq, idt], writes=[pm])
            if kt % 8 == 7 or kt == nt - 1:
                k0 = (kt // 8) * 8
                n = kt - k0 + 1
                b.op("act", lambda e, pm=pm, k0=k0, n=n: e.activation(
                    out=MT[:, k0:k0 + n, :].rearrange("p a b -> p (a b)"), in_=pm[:, 0:n * 128], func=AF.Copy),
                    reads=[pm], writes=[MT])

    def qk(u, n, kbuf):
        a, hp, hh, blk = u
        ps = pS[n % 2]
        qa = qTa[a % 2]
        for i in range(4):
            kt = 4 * blk + i
            b.op("pe", lambda e, i=i, kt=kt, ps=ps, qa=qa, hh=hh, hp=hp, kbuf=kbuf: e.matmul(
                ps[:, i * 128:(i + 1) * 128], lhsT=ktp[kbuf][64 * hh:64 * hh + 64, kt * 128:(kt + 1) * 128],
                rhs=qa[64 * hh:64 * hh + 64, hp, :], start=True, stop=True), reads=[ktp[kbuf], qa], writes=[ps])

    def softmax_pv(u, n, kbuf):
        a, hp, hh, blk = u
        ps, pt = pS[n % 2], pT[n % 2]
        if blk == 0:
            ctr["acc"] += 1
        acc = pA[ctr["acc"] % 2]
        b.op("act", lambda e: e.activation(out=pt[:], in_=ps[:], func=AF.Exp), reads=[ps], writes=[pt])
        b.op("dve", lambda e: e.tensor_tensor(out=pt[:], in0=pt[:], in1=MT[:, 4 * blk:4 * blk + 4, :].rearrange("p a b -> p (a b)"),
                                              op=ALU.mult), reads=[pt, MT], writes=[pt])
        for i in range(4):
            kt = 4 * blk + i
            b.op("pe", lambda e, i=i, kt=kt: e.matmul(acc[:, 0:65], lhsT=pt[:, i * 128:(i + 1) * 128],
                                                      rhs=vp[kbuf][:, kt, hh * 65:(hh + 1) * 65],
                                                      start=(blk == 0 and i == 0), stop=(blk == a and i == 3)),
                 reads=[pt, vp[kbuf]], writes=[acc])
        if blk == a:
            head = 2 * hp + hh
            r = rec[ctr["acc"] % 2]
            b.op("dve", lambda e: e.reciprocal(out=r[:], in_=acc[:, 64:65]), reads=[acc], writes=[r])
            b.op("dve", lambda e: e.tensor_scalar(out=o_tok[a][:, head * 64:(head + 1) * 64], in0=acc[:, 0:64],
                                                  scalar1=r[:, 0:1], scalar2=None, op0=ALU.mult),
                 reads=[acc, r], writes=[o_tok[a]])

    load_slot_small(0)
    for a in range(NS):
        if a + 1 < NS:
            load_slot_small(a + 1)
        kb_next = load_kv(a, 0)
        indexer(a)
        units = [(a, hp, hh, blk) for hp in range(8) for hh in range(2) for blk in range(a + 1)]
        kbufs = {}
        kbufs[0] = kb_next
        qk(units[0], 0, kbufs[0])
        for n, u in enumerate(units):
            _, hp, hh, blk = u
            if hh == 0 and blk == 0 and hp + 1 < 8:
                kbufs[hp + 1] = load_kv(a, hp + 1)
            if n + 1 < len(units):
                qk(units[n + 1], n + 1, kbufs[units[n + 1][1]])
            softmax_pv(u, n, kbufs[hp])


GROUPS = [[0, 1, 2, 3], [4, 5, 6, 7]]
_RANK = {}


class Gather:
    def __init__(self, b, name, nblk, cols, zt, zero_eng="act"):
        self.b, self.name, self.nblk, self.cols = b, name, nblk, cols
        nc = b.nc
        self.xb = nc.dram_tensor(name + "_xb", [nblk, 512, cols], BF16).ap()
        self.yb = nc.dram_tensor(name + "_yb", [nblk, 512, cols], BF16).ap()
        self.xo = nc.dram_tensor(name + "_xo", [nblk, 128, cols], BF16).ap()
        self.od = [T(self.xo[h]) for h in range(nblk)]
        self.xd = [T(self.xb[h]) for h in range(nblk)]
        self.yd = [T(self.yb[h]) for h in range(nblk)]
        for h in range(nblk):
            for r in range(4):
                for c0 in range(0, cols, 2048):
                    b.op(zero_eng, lambda e, h=h, r=r, c0=c0: e.dma_start(
                        out=self.xb[h, r * 128:(r + 1) * 128, c0:c0 + 2048], in_=zt[:, 0:2048]),
                        reads=[zt], writes=[self.xd[h]], dma="zero_" + name)

    def put(self, h, c0, c1, src_ap, reads, key):
        self.b.op("sp", lambda e: e.dma_start(out=self.xo[h, :, c0:c1], in_=src_ap), reads=reads,
                  writes=[self.od[h]], dma=key)

    def place(self, h, eng):
        def fn(e):
            if eng not in _RANK:
                _RANK[eng] = e.partition_id() % 4
            r = _RANK[eng]
            return e.dma_start(out=self.xb[h, bass.ds(r * 128, 128), :], in_=self.xo[h])
        self.b.op(eng, fn, reads=[self.od[h]], writes=[self.xd[h]], dma="place_" + self.name)

    def reduce(self, h):
        b = self.b
        b.op("pool", lambda e: e.collective_compute("AllReduce", ALU.add, replica_groups=GROUPS,
                                                    ins=[self.xb[h]], outs=[self.yb[h]]),
             reads=[self.xd[h]], writes=[self.yd[h]], dma=f"cc_{self.name}{h}", sem_inc=1)


def f_diff_proj(b, io, qT_res, G0, cos, sin, wstage=None):
    idt = b.ident()
    gmix = b.sb([128, D], F32, "gmix")
    b.op("sp", lambda e: e.dma_start(out=gmix[:], in_=io["gmix"][:]), writes=[gmix], dma="ld_gmix")
    gqk = b.sb([128, 2048], F32, "gqk")
    b.op("sp", lambda e: e.dma_start(out=gqk[:], in_=io["gqk"][:]), writes=[gqk], dma="ld_gqk")
    b.op("pool", lambda e: e.tensor_scalar(out=gqk[:, 0:1024], in0=gqk[:, 0:1024], scalar1=0.125, scalar2=None,
                                           op0=ALU.mult), reads=[gqk], writes=[gqk])
    xts = [b.sb([128, D], F32, f"xt{i}") for i in range(2)]

    def load_x(a):
        xt = xts[a % 2]
        b.op("sp", lambda e: e.dma_start(out=xt[:], in_=io["x"][a * 128:(a + 1) * 128, :]), writes=[xt], dma=f"ld_x{a % 2}")
    load_x(0)
    w = b.sb([128, 8, 3072], BF16, "w_in")
    if wstage is None:
        load_w_bf16(b, w, io["w_in"], 8, 3072, "ld_w")
    else:
        for n in range(6):
            stg = wstage[n % 2]
            b.op("sp", lambda e, n=n, stg=stg: e.dma_start(
                out=stg[:], in_=io["w_in"][:, n * 512:(n + 1) * 512].rearrange("(k p) c -> p k c", p=128)),
                writes=[stg], dma=f"ld_wst{n % 2}")
            b.op("act" if n % 2 == 0 else "dve",
                 (lambda e, n=n, stg=stg: e.activation(out=w[:, :, n * 512:(n + 1) * 512], in_=stg[:], func=AF.Copy)) if n % 2 == 0
                 else (lambda e, n=n, stg=stg: e.tensor_copy(out=w[:, :, n * 512:(n + 1) * 512], in_=stg[:])),
                 reads=[stg], writes=[w])
    junk_ = [b.sb([128, D], BF16, f"junk{i}") for i in range(2)]
    ss_ = [b.sb([128, 1], F32, f"ss{i}") for i in range(2)]
    hbf_ = [b.sb([128, D], BF16, f"hbf{i}") for i in range(2)]
    hT_ = [b.sb([128, 8, 128], BF16, f"hT{i}") for i in range(2)]
    pT = [b.banks16[0], b.banks16[1]]
    pY = [b.banks[2 + i] for i in range(4)]
    stage_ = [b.sb([128, 2048], F32, f"stage{i}") for i in range(2)]
    sq_ = [b.sb([128, 2048], BF16, f"sq{i}") for i in range(2)]
    ssq_ = [b.sb([128, 32], F32, f"ssq{i}") for i in range(2)]
    tmp1 = b.sb([128, 4 * 32 * 8], F32, "ropetmp"); tmp_ = [tmp1, tmp1]
    qkbf_ = [b.sb([128, 2048], BF16, f"qkbf{i}") for i in range(2)]
    kT = [b.sb([128, 8, 128], BF16, f"kTst{i}") for i in range(2)]
    vst = [b.sb([128, 8, 128], BF16, f"vst{i}") for i in range(2)]

    def body(a, junk, ss, hbf, hT, stage, sq, ssq, tmp, qkbf):
        xt = xts[a % 2]
        if a + 1 < NS:
            load_x(a + 1)
        rmsnorm_transpose(b, xt, gmix, hbf, hT, pT[0], ss, junk)
        va = vst[a % 2]
        for n in range(6):
            py = pY[n % 4]
            for kc in range(8):
                b.op("pe", lambda e, n=n, kc=kc, py=py: e.matmul(py[:], lhsT=hT[:, kc, :],
                                                                   rhs=w[:, kc, n * 512:(n + 1) * 512],
                                                                   start=(kc == 0), stop=(kc == 7)),
                     reads=[hT, w], writes=[py])
            if n < 4:
                b.op("act", lambda e, n=n, py=py: e.activation(out=stage[:, n * 512:(n + 1) * 512], in_=py[:], func=AF.Copy),
                     reads=[py], writes=[stage])
                b.op("act", lambda e, n=n, py=py: e.activation(out=sq[:, n * 512:(n + 1) * 512], in_=py[:], func=AF.Square),
                     reads=[py], writes=[sq])
            else:
                b.op("dve", lambda e, n=n, py=py, va=va: e.tensor_copy(
                    out=va[:, (n - 4) * 4:(n - 4) * 4 + 4, :], in_=py[:].rearrange("p (h d) -> p h d", d=128)),
                    reads=[py], writes=[va])
        for h in range(8):
            G0.put(h, 2048 + a * 128, 2048 + (a + 1) * 128, va[:, h, :], [va], f"st_v{a % 2}")

    def tail(a, junk, ss, hbf, hT, stage, sq, ssq, tmp, qkbf):
        headnorm_rope(b, stage, sq, ssq, 32, gqk, cos[:, a, :], sin[:, a, :], qkbf, tmp)
        kt = kT[a % 2]
        for i in range(16):
            pt = b.banks16[6] if i < 8 else b.banks16[7]
            b.op("pe", lambda e, i=i, pt=pt: e.transpose(out=pt[:, (i % 8) * 128:(i % 8 + 1) * 128],
                                                         in_=qkbf[:, i * 128:(i + 1) * 128], identity=idt[:]),
                 reads=[qkbf, idt], writes=[pt])
            if i == 7:
                b.op("dve", lambda e, pt=pt, a=a: e.tensor_copy(out=qT_res[:, a, :], in_=pt[:]), reads=[pt], writes=[qT_res])
            if i == 15:
                b.op("dve", lambda e, pt=pt, kt=kt: e.tensor_copy(out=kt[:].rearrange("p a b -> p (a b)"), in_=pt[:]),
                     reads=[pt], writes=[kt])
        for h in range(8):
            G0.put(h, a * 128, (a + 1) * 128, kt[:, h, :], [kt], f"st_k{a % 2}")
    def args(a):
        return [z[a % 2] for z in (junk_, ss_, hbf_, hT_, stage_, sq_, ssq_, tmp_, qkbf_)]
    import os
    if os.environ.get("PIPE1", "0") == "1":
        body(0, *args(0))
        for a in range(NS):
            if a + 1 < NS:
                body(a + 1, *args(a + 1))
            tail(a, *args(a))
    else:
        for a in range(NS):
            body(a, *args(a))
            tail(a, *args(a))
    for h in range(8):
        G0.place(h, "sp")
        G0.reduce(h)


def f_diff_attn(b, io, o_tok, qT, G0):
    LAM_INIT = 0.2
    maskT = b.sb([128, 512], BF16, "maskT")
    b.op("sp", lambda e: e.dma_start(out=maskT[:], in_=io["maskT"][:]), writes=[maskT], dma="ld_mask")
    lam = b.sb([128, 256], F32, "lam")
    b.op("sp", lambda e: e.dma_start(out=lam[:], in_=io["lam"][:]), writes=[lam], dma="ld_lam")
    gsub = b.sb([128, 128], F32, "gsub")
    b.op("sp", lambda e: e.dma_start(out=gsub[:], in_=io["gsub"][:]), writes=[gsub], dma="ld_gsub")
    b.op("dve", lambda e: e.tensor_scalar(out=gsub[:], in0=gsub[:], scalar1=1.0 - LAM_INIT, scalar2=None,
                                          op0=ALU.mult), reads=[gsub], writes=[gsub])
    lprod = b.sb([128, 128], F32, "lprod")
    l2 = b.sb([128, 2], F32, "l2")
    neglam = b.sb([128, 1], F32, "neglam")
    l4 = lam[:].rearrange("p (a b d) -> p a b d", a=2, b=2)
    b.op("dve", lambda e: e.tensor_tensor(out=lprod[:].rearrange("p (a d) -> p a d", a=2), in0=l4[:, :, 0, :],
                                          in1=l4[:, :, 1, :], op=ALU.mult), reads=[lam], writes=[lprod])
    b.op("dve", lambda e: e.tensor_reduce(out=l2[:], in_=lprod[:].rearrange("p (a d) -> p a d", a=2), axis=AX.X,
                                          op=ALU.add), reads=[lprod], writes=[l2])
    b.op("act", lambda e: e.activation(out=l2[:], in_=l2[:], func=AF.Exp), reads=[l2], writes=[l2])
    b.op("dve", lambda e: e.tensor_tensor(out=neglam[:], in0=l2[:, 1:2], in1=l2[:, 0:1], op=ALU.subtract),
         reads=[l2], writes=[neglam])
    b.op("dve", lambda e: e.tensor_scalar(out=neglam[:], in0=neglam[:], scalar1=-LAM_INIT, scalar2=None, op0=ALU.add),
         reads=[neglam], writes=[neglam])

    ktb = [b.sb([128, 8192], BF16, f"ktb{i}") for i in range(2)]
    vb = [b.sb([128, 64, 129], BF16, f"vb{i}") for i in range(2)]
    for i in range(2):
        b.op("dve", lambda e, i=i: e.memset(vb[i][:, :, 128:129], 1.0), writes=[vb[i]])
    pS = [[b.banks[c * 2 + i] for i in range(2)] for c in range(2)]
    pA = [[b.banks[4 + c * 2 + i] for i in range(2)] for c in range(2)]
    pT = [[b.sb([128, 512], BF16, f"pTs{c}{i}") for i in range(2)] for c in range(2)]
    rec = [b.sb([128, 2], F32, f"rec{i}") for i in range(2)]
    o32 = [b.sb([128, 128], F32, f"o32{i}") for i in range(2)]
    oj = b.sb([128, 128], BF16, "ojunk")
    ss1 = [b.sb([128, 1], F32, f"ss1{i}") for i in range(2)]
    units = [(h, a, blk) for h in range(8) for a in range(NS) for blk in range(a + 1)]

    def load_head(h):
        kb, vv = ktb[h % 2], vb[h % 2]
        yb = G0.yb[h]
        for r in range(4):
            b.op("sp", lambda e, kb=kb, r=r, yb=yb: e.dma_start(out=kb[:, r * 2048:(r + 1) * 2048],
                                                               in_=yb[r * 128:(r + 1) * 128, 0:2048]),
                 reads=[G0.yd[h]], writes=[kb], dma=f"ld_kt{h % 2}")
            b.op("sp", lambda e, vv=vv, r=r, yb=yb: e.dma_start(
                out=vv[:, r * 16:(r + 1) * 16, 0:128],
                in_=yb[r * 128:(r + 1) * 128, 2048:4096].rearrange("p (a e) -> p a e", e=128)),
                reads=[G0.yd[h]], writes=[vv], dma=f"ld_v{h % 2}")

    def qk(u, n):
        h, a, blk = u
        kb = ktb[h % 2]
        for i in range(4):
            for c in range(2):
                ps = pS[c][n % 2]
                kt = 16 * i + blk
                b.op("pe", lambda e, c=c, i=i, kt=kt, ps=ps, kb=kb, a=a, h=h: e.matmul(
                    ps[:, i * 128:(i + 1) * 128], lhsT=kb[64 * c:64 * c + 64, kt * 128:(kt + 1) * 128],
                    rhs=qT[64 * c:64 * c + 64, a, h * 128:(h + 1) * 128], start=True, stop=True),
                    reads=[kb, qT], writes=[ps])

    def evac(h, a):
        k = a % 2
        a0, a1 = pA[0][k], pA[1][k]
        r, o, s = rec[k], o32[k], ss1[k]
        b.op("dve", lambda e: e.reciprocal(out=r[:, 0:1], in_=a0[:, 128:129]), reads=[a0], writes=[r])
        b.op("dve", lambda e: e.reciprocal(out=r[:, 1:2], in_=a1[:, 128:129]), reads=[a1], writes=[r])
        b.op("dve", lambda e: e.tensor_tensor(out=r[:, 1:2], in0=r[:, 1:2], in1=neglam[:], op=ALU.mult),
             reads=[r, neglam], writes=[r])
        b.op("dve", lambda e: e.tensor_scalar(out=o[:], in0=a0[:, 0:128], scalar1=r[:, 0:1], scalar2=None, op0=ALU.mult),
             reads=[a0, r], writes=[o])
        b.op("dve", lambda e: e.scalar_tensor_tensor(out=o[:], in0=a1[:, 0:128], scalar=r[:, 1:2], in1=o[:],
                                                     op0=ALU.mult, op1=ALU.add), reads=[a1, r, o], writes=[o])
        b.op("act", lambda e: e.activation(out=oj[:], in_=o[:], func=AF.Square, accum_out=s[:, 0:1]),
             reads=[o], writes=[oj, s])
        rstd_from_ss(b, s, 1, 1.0 / 128)
        ot = o_tok[a]
        b.op("dve", lambda e: e.scalar_tensor_tensor(out=ot[:, h * 128:(h + 1) * 128], in0=o[:], scalar=s[:, 0:1],
                                                     in1=gsub[:], op0=ALU.mult, op1=ALU.mult),
             reads=[o, s, gsub], writes=[ot])

    def softmax_pv(u, n):
        h, a, blk = u
        vv = vb[h % 2]
        for c in range(2):
            ps = pS[c][n % 2]
            pt = pT[c][n % 2]
            acc = pA[c][a % 2]
            b.op("act", lambda e, ps=ps, pt=pt: e.activation(out=pt[:], in_=ps[:], func=AF.Exp),
                 reads=[ps], writes=[pt])
            if blk == a:
                b.op("dve", lambda e, pt=pt: e.tensor_tensor(out=pt[:], in0=pt[:], in1=maskT[:], op=ALU.mult),
                     reads=[pt, maskT], writes=[pt])
            for i in range(4):
                kt = 16 * i + blk
                b.op("pe", lambda e, i=i, kt=kt, pt=pt, acc=acc, vv=vv, blk=blk, a=a: e.matmul(
                    acc[:, 0:129], lhsT=pt[:, i * 128:(i + 1) * 128], rhs=vv[:, kt, :],
                    start=(blk == 0 and i == 0), stop=(blk == a and i == 3)),
                    reads=[pt, vv], writes=[acc])
        if blk == a:
            evac(h, a)

    load_head(0)
    qk(units[0], 0)
    for n, u in enumerate(units):
        if u[1] == 0 and u[2] == 0 and u[0] + 1 < 8:
            load_head(u[0] + 1)
        if n + 1 < len(units):
            qk(units[n + 1], n + 1)
        softmax_pv(u, n)


def dsa_proj_weights(b, io, top=False):
    w = b.sb([128, 8, 2376], BF16, "w_in2", top=top)
    wuq = b.sb([128, 2, 1024], BF16, "wuq", top=top)
    wuqi = b.sb([128, 2, 512], BF16, "wuqi", top=top)

    def load():
        load_w_bf16(b, w, io["w_in2"], 8, 2376, "ld_w2in", colchunk=792)
        load_w_bf16(b, wuq, io["w_uq"], 2, 1024, "ld_wuq")
        load_w_bf16(b, wuqi, io["w_uqi"], 2, 512, "ld_wuqi")
    return {"w": w, "wuq": wuq, "wuqi": wuqi, "load": load}


def f_dsa_proj(b, io, x_res, cos, sin, G1, GK, scr, pre=None):
    idt = b.ident()
    gmix = b.sb([128, D], F32, "gmix1")
    b.op("sp", lambda e: e.dma_start(out=gmix[:], in_=io["gmix1"][:]), writes=[gmix], dma="ld_gmix1")
    gqk = b.sb([128, 2048], F32, "gqk2")
    b.op("sp", lambda e: e.dma_start(out=gqk[:], in_=io["gqk2"][:]), writes=[gqk], dma="ld_gqk2")
    b.op("pool", lambda e: e.tensor_scalar(out=gqk[:, 0:1024], in0=gqk[:, 0:1024], scalar1=0.125, scalar2=None,
                                           op0=ALU.mult), reads=[gqk], writes=[gqk])
    gcq = b.sb([128, 256], F32, "gcq")
    b.op("sp", lambda e: e.dma_start(out=gcq[:], in_=io["gcq"][:]), writes=[gcq], dma="ld_gcq")
    if pre is None:
        pre = dsa_proj_weights(b, io)
        pre["load"]()
    w, wuq, wuqi = pre["w"], pre["wuq"], pre["wuqi"]
    junk_ = [b.sb([128, D], BF16, f"junk3{i}") for i in range(2)]
    ss_ = [b.sb([128, 1], F32, f"ss3{i}") for i in range(2)]
    hbf_ = [b.sb([128, D], BF16, f"hbf3{i}") for i in range(2)]
    hT_ = [b.sb([128, 8, 128], BF16, f"hT3{i}") for i in range(2)]
    stage_ = [b.sb([128, 2048], F32, f"stage3{i}") for i in range(2)]
    sq_ = [b.sb([128, 2048], BF16, f"sq3{i}") for i in range(2)]
    ssq_ = [b.sb([128, 32], F32, f"ssq3{i}") for i in range(2)]
    tmp1 = b.sb([128, 4 * 32 * 8], F32, "ropetmp3"); tmp_ = [tmp1, tmp1]
    qkbf_ = [b.sb([128, 2048], BF16, f"qkbf3{i}") for i in range(2)]
    qkT = [b.sb([128, 16, 128], BF16, f"qkT3{i}") for i in range(2)]
    vst = [b.sb([128, 16, 64], BF16, f"vst3{i}") for i in range(2)]
    cqs_ = [b.sb([128, 256], F32, f"cqs{i}") for i in range(2)]
    ssc_ = [b.sb([128, 1], F32, f"ssc{i}") for i in range(2)]
    cqbf_ = [b.sb([128, 256], BF16, f"cqbf{i}") for i in range(2)]
    cqT_ = [b.sb([128, 2, 128], BF16, f"cqT{i}") for i in range(2)]
    kis_ = [b.sb([128, 64], F32, f"kis{i}") for i in range(2)]
    ksq_ = [b.sb([128, 64], F32, f"ksq{i}") for i in range(2)]
    kss_ = [b.sb([128, 1], F32, f"kss{i}") for i in range(2)]
    kibf_ = [b.sb([128, 128], BF16, f"kibf{i}") for i in range(2)]
    kiT = [b.sb([128, 128], BF16, f"kiT{i}") for i in range(2)]
    wi_ = [b.sb([128, 8], F32, f"wi{i}") for i in range(2)]
    sgn = [b.sb([128, 8], F32, f"sgn{i}") for i in range(2)]
    aw_ = [b.sb([128, 8], F32, f"aw{i}") for i in range(2)]
    qis_ = [b.sb([128, 512], F32, f"qis{i}") for i in range(2)]
    qibf_ = [b.sb([128, 512], BF16, f"qibf{i}") for i in range(2)]
    qiT = [b.sb([128, 4, 128], BF16, f"qiT{i}") for i in range(2)]
    nbank = [0]

    def bank():
        nbank[0] += 1
        return b.banks[2 + nbank[0] % 4]

    def proj(py, lhs, nk, rhs_fn, ncol):
        for kc in range(nk):
            b.op("pe", lambda e, kc=kc: e.matmul(py[:, 0:ncol], lhsT=lhs[:, kc, :], rhs=rhs_fn(kc),
                                                 start=(kc == 0), stop=(kc == nk - 1)), reads=[lhs, w, wuq, wuqi], writes=[py])

    def body2(a, junk, ss, hbf, hT, stage, sq, ssq, tmp, qkbf, cqs, ssc, cqbf, cqT, kis, ksq, kss, kibf, wi, aw, qis, qibf):
        xr = x_res[a]
        rmsnorm_transpose(b, xr, gmix, hbf, hT, b.banks16[0], ss, junk)
        py = bank()
        proj(py, hT, 8, lambda kc: w[:, kc, 0:256], 256)
        b.op("act", lambda e, py=py: e.activation(out=cqs[:], in_=py[:, 0:256], func=AF.Copy), reads=[py], writes=[cqs])
        b.op("act", lambda e, py=py: e.activation(out=junk[:, 0:256], in_=py[:, 0:256], func=AF.Square, accum_out=ssc[:, 0:1]),
             reads=[py], writes=[junk, ssc])
        rstd_from_ss(b, ssc, 1, 1.0 / 256)
        b.op("dve", lambda e: e.scalar_tensor_tensor(out=cqbf[:], in0=cqs[:], scalar=ssc[:, 0:1], in1=gcq[:],
                                                     op0=ALU.mult, op1=ALU.mult), reads=[cqs, ssc, gcq], writes=[cqbf])
        p6 = b.banks16[6]
        for kc in range(2):
            b.op("pe", lambda e, kc=kc: e.transpose(out=p6[:, kc * 128:(kc + 1) * 128], in_=cqbf[:, kc * 128:(kc + 1) * 128],
                                                    identity=idt[:]), reads=[cqbf, idt], writes=[p6])
        b.op("dve", lambda e: e.tensor_copy(out=cqT[:].rearrange("p a b -> p (a b)"), in_=p6[:, 0:256]),
             reads=[p6], writes=[cqT])
        for n in range(2):
            py = bank()
            proj(py, hT, 8, lambda kc, n=n: w[:, kc, 256 + n * 512:256 + (n + 1) * 512], 512)
            b.op("act", lambda e, py=py, n=n: e.activation(out=stage[:, 1024 + n * 512:1024 + (n + 1) * 512], in_=py[:], func=AF.Copy),
                 reads=[py], writes=[stage])
            b.op("act", lambda e, py=py, n=n: e.activation(out=sq[:, 1024 + n * 512:1024 + (n + 1) * 512], in_=py[:], func=AF.Square),
                 reads=[py], writes=[sq])
        va = vst[a % 2]
        for n in range(2):
            py = bank()
            proj(py, hT, 8, lambda kc, n=n: w[:, kc, 1280 + n * 512:1280 + (n + 1) * 512], 512)
            b.op("dve", lambda e, py=py, n=n, va=va: e.tensor_copy(out=va[:, n * 8:(n + 1) * 8, :],
                                                                   in_=py[:].rearrange("p (h d) -> p h d", d=64)),
                 reads=[py], writes=[va])
        for hp in range(8):
            G1.put(hp, 2048 + a * 128, 2048 + (a + 1) * 128, va[:, 2 * hp:2 * hp + 2, :].rearrange("p h d -> p (h d)"),
                   [va], f"st_v2{a % 2}")
        py = bank()
        proj(py, hT, 8, lambda kc: w[:, kc, 2304:2376], 72)
        b.op("act", lambda e, py=py: e.activation(out=kis[:], in_=py[:, 0:64], func=AF.Copy), reads=[py], writes=[kis])
        b.op("act", lambda e, py=py: e.activation(out=ksq[:], in_=py[:, 0:64], func=AF.Square), reads=[py], writes=[ksq])
        b.op("dve", lambda e, py=py: e.tensor_copy(out=wi[:], in_=py[:, 64:72]), reads=[py], writes=[wi])
        for n in range(2):
            py = bank()
            proj(py, cqT, 2, lambda kc, n=n: wuq[:, kc, n * 512:(n + 1) * 512], 512)
            b.op("act", lambda e, py=py, n=n: e.activation(out=stage[:, n * 512:(n + 1) * 512], in_=py[:], func=AF.Copy),
                 reads=[py], writes=[stage])
            b.op("act", lambda e, py=py, n=n: e.activation(out=sq[:, n * 512:(n + 1) * 512], in_=py[:], func=AF.Square),
                 reads=[py], writes=[sq])
        py = bank()
        proj(py, cqT, 2, lambda kc: wuqi[:, kc, :], 512)
        b.op("act", lambda e, py=py: e.activation(out=qis[:], in_=py[:], func=AF.Copy), reads=[py], writes=[qis])

    def tail2(a, junk, ss, hbf, hT, stage, sq, ssq, tmp, qkbf, cqs, ssc, cqbf, cqT, kis, ksq, kss, kibf, wi, aw, qis, qibf):
        headnorm_rope(b, stage, sq, ssq, 32, gqk, cos[:, a, :], sin[:, a, :], qkbf, tmp)
        qt = qkT[a % 2]
        for i in range(16):
            pt = b.banks16[1] if i < 8 else b.banks16[7]
            b.op("pe", lambda e, i=i, pt=pt: e.transpose(out=pt[:, (i % 8) * 128:(i % 8 + 1) * 128],
                                                         in_=qkbf[:, i * 128:(i + 1) * 128], identity=idt[:]),
                 reads=[qkbf, idt], writes=[pt])
            if i % 8 == 7:
                b.op("dve", lambda e, i=i, pt=pt, qt=qt: e.tensor_copy(
                    out=qt[:, (i // 8) * 8:(i // 8) * 8 + 8, :].rearrange("p a b -> p (a b)"), in_=pt[:]),
                    reads=[pt], writes=[qt])
        b.op("sp", lambda e, a=a, qt=qt: e.dma_start(out=scr["QT2"][a][:], in_=qt[:, 0:8, :].rearrange("p a b -> p (a b)")),
             reads=[qt], writes=[scr["QT2"][a]], dma=f"st_q2{a % 2}")
        for hp in range(8):
            G1.put(hp, a * 128, (a + 1) * 128, qt[:, 8 + hp, :], [qt], f"st_k2{a % 2}")
        ki_half = T(kibf.t[:, 0:64], kibf.d)
        headnorm_rope(b, kis, ksq, kss, 1, None, cos[:, a, :], sin[:, a, :], ki_half, tmp)
        b.op("pool", lambda e: e.tensor_copy(out=kibf[:, 64:128], in_=kibf[:, 0:64]), reads=[kibf], writes=[kibf])
        p7 = b.banks16[7]
        b.op("pe", lambda e: e.transpose(out=p7[:, 0:128], in_=kibf[:], identity=idt[:]), reads=[kibf, idt], writes=[p7])
        kt_ = kiT[a % 2]
        b.op("dve", lambda e, kt_=kt_: e.tensor_copy(out=kt_[:], in_=p7[:, 0:128]), reads=[p7], writes=[kt_])
        GK.put(0, a * 128, (a + 1) * 128, kt_[:], [kt_], f"st_ki{a % 2}")
        sg = sgn[a % 2]
        b.op("act", lambda e, sg=sg: e.activation(out=sg[:], in_=wi[:], func=AF.Sign), reads=[wi], writes=[sg])
        b.op("dve", lambda e, sg=sg: e.scalar_tensor_tensor(out=aw[:], in0=wi[:], scalar=IDX_SCALE, in1=sg[:],
                                                           op0=ALU.mult, op1=ALU.mult), reads=[wi, sg], writes=[aw])
        b.op("sp", lambda e, a=a, sg=sg: e.dma_start(out=scr["SG"][a][:], in_=sg[:]), reads=[sg], writes=[scr["SG"][a]],
             dma=f"st_sg{a % 2}")
        headnorm_rope(b, qis, None, aw, 8, None, cos[:, a, :], sin[:, a, :], qibf, tmp, norm=False)
        qi_ = qiT[a % 2]
        for i in range(4):
            b.op("pe", lambda e, i=i: e.transpose(out=p7[:, 256 + i * 128:256 + (i + 1) * 128],
                                                  in_=qibf[:, i * 128:(i + 1) * 128], identity=idt[:]),
                 reads=[qibf, idt], writes=[p7])
        b.op("dve", lambda e, qi_=qi_: e.tensor_copy(out=qi_[:].rearrange("p a b -> p (a b)"), in_=p7[:, 256:768]),
             reads=[p7], writes=[qi_])
        b.op("sp", lambda e, a=a, qi_=qi_: e.dma_start(out=scr["QI"][a][:], in_=qi_[:].rearrange("p a b -> p (a b)")),
             reads=[qi_], writes=[scr["QI"][a]], dma=f"st_qi{a % 2}")
    def args2(a):
        return [z[a % 2] for z in (junk_, ss_, hbf_, hT_, stage_, sq_, ssq_, tmp_, qkbf_, cqs_, ssc_, cqbf_, cqT_, kis_, ksq_, kss_, kibf_, wi_, aw_, qis_, qibf_)]
    import os
    if os.environ.get("PIPE2", "0") == "1":
        body2(0, *args2(0))
        for a in range(NS):
            if a + 1 < NS:
                body2(a + 1, *args2(a + 1))
            tail2(a, *args2(a))
    else:
        for a in range(NS):
            body2(a, *args2(a))
            tail2(a, *args2(a))
    GK.place(0, "act")
    GK.reduce(0)
    for hp in range(8):
        G1.place(hp, "act")
        G1.reduce(hp)


NIT2 = 14


def f_dsa_attn(b, io, o_tok, G1, GK, scr):
    idt = b.ident()
    kia = b.sb([128, 8192], BF16, "kiall")
    for r in range(4):
        b.op("sp", lambda e, r=r: e.dma_start(out=kia[:, r * 2048:(r + 1) * 2048], in_=GK.yb[0][r * 128:(r + 1) * 128, :]),
             reads=[GK.yd[0]], writes=[kia], dma="ld_kia")
    kia4 = kia[:].rearrange("p (r a t) -> p r a t", r=4, a=16)
    negm = b.sb([128, 512], F32, "negm")
    b.op("sp", lambda e: e.dma_start(out=negm[:], in_=io["negmask"][:]), writes=[negm], dma="ld_negm")
    cW = b.sb([128, NIT2], F32, "cW")
    for i in range(NIT2):
        b.op("dve", lambda e, i=i: e.memset(cW[:, i:i + 1], 2.0 ** (-i)), writes=[cW])
    THR = b.sb([128, NS], F32, "THR")
    mA = b.mark()
    NSET = 3
    qiTa = [b.sb([128, 4, 128], BF16, f"qiTa{i}") for i in range(NSET)]
    sgn = [b.sb([128, 8], F32, f"sgna{i}") for i in range(NSET)]
    Dg = [b.sb([128, 8, 128], BF16, f"Dg{i}") for i in range(NSET)]
    tb = [[b.sb([128, 512], BF16, f"tb{s}{i}") for i in range(3)] for s in range(NSET)]
    pIs = [[b.banks[4], b.banks[6]], [b.banks[5], b.banks[0]], [b.banks[2], b.banks[2]]]
    pIas = [b.banks[7], b.banks[1], b.banks[3]]
    ctr = {"i": 0, "acc": 0, "kv": 0}

    def load_idx_small(a, s):
        b.op("sp", lambda e, a=a: e.dma_start(out=qiTa[s][:].rearrange("p a b -> p (a b)"), in_=scr["QI"][a][:]),
             reads=[scr["QI"][a]], writes=[qiTa[s]], dma=f"ld_qiTa{s}")
        b.op("sp", lambda e, a=a: e.dma_start(out=sgn[s][:], in_=scr["SG"][a][:]), reads=[scr["SG"][a]],
             writes=[sgn[s]], dma=f"ld_sgn{s}")

    def indexer_list(a, s, Ib_):
        L = []

        def add(eng, fn, reads=(), writes=()):
            L.append((eng, fn, reads, writes))
        nb = a + 1
        qi, sg, dg = qiTa[s], sgn[s], Dg[s]
        pys, pia = pIs[s], pIas[s]
        skew = pys[0] is not pys[1]
        add("dve", lambda e: e.tensor_tensor(out=dg[:], in0=idt[:].unsqueeze(1).to_broadcast([128, 8, 128]),
                                             in1=sg[:].unsqueeze(2).to_broadcast([128, 8, 128]), op=ALU.mult),
            [idt, sg], [dg])
        steps = [(blk, head) for blk in range(nb) for head in range(8)]
        S_ = len(steps)
        evq = []
        for k in range(S_ + 2):
            if k < S_:
                blk, head = steps[k]
                hp, hh = head // 2, head % 2
                py = pys[k % 2]
                add("pe", lambda e, hp=hp, hh=hh, blk=blk, py=py: e.matmul(
                    py[:].rearrange("p (r t) -> p r t", r=4), lhsT=qi[64 * hh:64 * hh + 64, hp, :],
                    rhs=kia4[64 * hh:64 * hh + 64, :, blk, :], start=True, stop=True), [qi, kia], [py])
            kr = k - 1 if skew else k
            if 0 <= kr < S_:
                py = pys[kr % 2]
                t = tb[s][kr % 3]
                add("act", lambda e, t=t, py=py: e.activation(out=t[:], in_=py[:], func=AF.Relu), [py], [t])
            if 0 <= k - 2 < S_:
                blk, head = steps[k - 2]
                t = tb[s][(k - 2) % 3]
                add("pe", lambda e, head=head, t=t: e.matmul(pia[:], lhsT=dg[:, head, :], rhs=t[:],
                                                             start=(head == 0), stop=(head == 7)), [dg, t], [pia])
                if head == 7:
                    add("act", lambda e, eb=blk: e.activation(out=Ib_[:, eb * 512:(eb + 1) * 512], in_=pia[:], func=AF.Copy),
                        [pia], [Ib_])
        for _, eb in evq:
            add("act", lambda e, eb=eb: e.activation(out=Ib_[:, eb * 512:(eb + 1) * 512], in_=pia[:], func=AF.Copy),
                [pia], [Ib_])
        return L

    IbA = [b.sb([128, 8192], F32, f"IbA{i}") for i in range(NSET)]
    MqA = [b.sb([128, 8192], BF16, f"MqA{i}") for i in range(NSET)]
    st = [{k: b.sb([128, n], F32, f"bs{k}{s}") for k, n in (("m1", 1), ("lo", 1), ("mid", 1), ("nmid", 1), ("cD", 1),
                                                           ("sA", 1), ("g", 1), ("W", NIT2))} for s in range(NSET)]

    def stageA_list(a, s):
        L = []

        def add(eng, fn, reads=(), writes=()):
            L.append((eng, fn, reads, writes))
        nb = a + 1
        nv = 512 * nb
        Ib_, Mq_ = IbA[s], MqA[s]
        S_ = st[s]
        m1, lo, mid, cD, sA, g, W = (S_[k] for k in ("m1", "lo", "mid", "cD", "sA", "g", "W"))
        add("dve", lambda e: e.tensor_reduce(out=m1[:], in_=Ib_[:, 0:nv], axis=AX.X, op=ALU.max, apply_absolute_value=True),
            [Ib_], [m1])
        add("dve", lambda e: e.tensor_tensor(out=Ib_[:, nv - 512:nv], in0=Ib_[:, nv - 512:nv], in1=negm[:], op=ALU.add),
            [Ib_, negm], [Ib_])
        add("dve", lambda e: e.tensor_scalar(out=W[:], in0=cW[:], scalar1=m1[:, 0:1], scalar2=None, op0=ALU.mult),
            [cW, m1], [W])
        add("dve", lambda e: e.tensor_tensor(out=W[:], in0=W[:], in1=cW[:], op=ALU.add), [W, cW], [W])
        add("dve", lambda e: e.memset(mid[:], 0.0), [], [mid])
        h = 512 * (nb // 2)
        thr = TOPK - 0.5 - 0.5 * h
        for i in range(NIT2):
            if h > 0:
                add("act", lambda e: e.activation(out=Mq_[:, 0:h], in_=Ib_[:, 0:h], func=AF.Sign, bias=mid[:, 0:1],
                                                  scale=-1.0, accum_out=sA[:, 0:1]), [Ib_, mid], [Mq_, sA])
            add("dve", lambda e: e.tensor_scalar(out=Mq_[:, h:nv], in0=Ib_[:, h:nv], scalar1=mid[:, 0:1], scalar2=0.0,
                                                 op0=ALU.is_ge, op1=ALU.add, accum_out=cD[:, 0:1]), [Ib_, mid], [Mq_, cD])
            if h > 0:
                add("dve", lambda e: e.scalar_tensor_tensor(out=cD[:], in0=sA[:], scalar=-0.5, in1=cD[:], op0=ALU.mult,
                                                            op1=ALU.add), [sA, cD], [cD])
            add("dve", lambda e, i=i: e.tensor_scalar(out=g[:], in0=cD[:], scalar1=thr, scalar2=W[:, i:i + 1],
                                                      op0=ALU.is_ge, op1=ALU.mult), [cD, W], [g])
            if i + 1 < NIT2:
                add("dve", lambda e, i=i: e.scalar_tensor_tensor(out=mid[:], in0=mid[:], scalar=W[:, i + 1:i + 2], in1=g[:],
                                                                 op0=ALU.subtract, op1=ALU.add), [mid, W, g], [mid])
            else:
                add("dve", lambda e, i=i: e.scalar_tensor_tensor(out=lo[:], in0=mid[:], scalar=W[:, i:i + 1], in1=g[:],
                                                                 op0=ALU.subtract, op1=ALU.add), [mid, W, g], [lo])
        for c0 in range(0, nv, 2048):
            c1 = min(nv, c0 + 2048)
            add("dve", lambda e, c0=c0, c1=c1: e.tensor_scalar(out=Mq_[:, c0:c1], in0=Ib_[:, c0:c1], scalar1=lo[:, 0:1],
                                                               scalar2=None, op0=ALU.is_ge), [Ib_, lo], [Mq_])
        add("sp", ("dma", lambda e: e.dma_start(out=scr["MQ"][a][:, 0:nv], in_=Mq_[:, 0:nv]), f"st_mq{s}"),
            [Mq_], [scr["MQ"][a]])
        return L

    def emit_item(it):
        eng, fn, reads, writes = it
        if isinstance(fn, tuple):
            b.op(eng, fn[1], reads, writes, dma=fn[2])
        else:
            b.op(eng, fn, reads, writes)

    def interleave(lists):
        lists = [l for l in lists if l]
        pos = [0] * len(lists)
        total = sum(len(l) for l in lists)
        for _ in range(total):
            best, bf = None, 2.0
            for li, l in enumerate(lists):
                if pos[li] < len(l):
                    f = pos[li] / len(l)
                    if f < bf:
                        best, bf = li, f
            emit_item(lists[best][pos[best]])
            pos[best] += 1

    for a in range(NS + 1):
        li = []
        if a < NS:
            load_idx_small(a, a % NSET)
            li = indexer_list(a, a % NSET, IbA[a % NSET])
        lb = stageA_list(a - 1, (a - 1) % NSET) if a >= 1 else []
        interleave([li, lb])
    b.release(mA)
    if o_tok is None:
        o_tok = [b.sb([128, 1024], BF16, f"otokb{a}", top=True) for a in range(NS)]

    Mqs = [b.sb([128, 8192], BF16, f"MqB{i}") for i in range(2)]
    MTs = [b.sb([128, 64, 128], BF16, f"MT{i}") for i in range(2)]
    ktp = [b.sb([128, 8192], BF16, f"ktp{i}") for i in range(2)]
    vp = [b.sb([128, 64, 130], BF16, f"vp{i}") for i in range(2)]
    for i in range(2):
        b.op("dve", lambda e, i=i: e.memset(vp[i][:, :, 0:1], 1.0), writes=[vp[i]])
        b.op("dve", lambda e, i=i: e.memset(vp[i][:, :, 129:130], 1.0), writes=[vp[i]])
    qTa = [b.sb([128, 8, 128], BF16, f"qTa{i}") for i in range(2)]
    pT = [b.sb([128, 512], BF16, f"pTd{i}") for i in range(3)]
    rec = [b.sb([128, 1], F32, f"recd{i}") for i in range(2)]
    pS = [b.banks[0], b.banks[1], b.banks[5]]
    pA = [b.banks[2], b.banks[3]]
    pMs = [b.banks16[6], b.banks16[7]]

    def load_slot_small(a):
        nv = 512 * (a + 1)
        b.op("sp", lambda e, a=a: e.dma_start(out=qTa[a % 2][:].rearrange("p a b -> p (a b)"), in_=scr["QT2"][a][:]),
             reads=[scr["QT2"][a]], writes=[qTa[a % 2]], dma=f"ld_qTa{a % 2}")
        b.op("sp", lambda e, a=a, nv=nv: e.dma_start(out=Mqs[a % 2][:, 0:nv], in_=scr["MQ"][a][:, 0:nv]),
             reads=[scr["MQ"][a]], writes=[Mqs[a % 2]], dma=f"ld_mq{a % 2}")

    def load_kv(a, hp):
        k = ctr["kv"] % 2
        ctr["kv"] += 1
        n = (a + 1) * 128
        yb = G1.yb[hp]
        b.op("sp", lambda e: e.dma_start(out=ktp[k][:].rearrange("p (r x) -> p r x", r=4)[:, :, 0:n],
                                         in_=yb[:, 0:n].rearrange("(r p) x -> p r x", p=128)),
             reads=[G1.yd[hp]], writes=[ktp[k]], dma=f"ld_ktp{k}")
        for r in range(4):
            b.op("sp", lambda e, r=r: e.dma_start(
                out=vp[k][:, r * 16:r * 16 + a + 1, 1:129],
                in_=yb[r * 128:(r + 1) * 128, 2048:2048 + n].rearrange("p (a e) -> p a e", e=128)),
                reads=[G1.yd[hp]], writes=[vp[k]], dma=f"ld_vp{k}")
        return k

    def pre_list(a):
        L = []
        Mq, MT = Mqs[a % 2], MTs[a % 2]
        nt = 4 * (a + 1)
        ngr = (nt + 7) // 8
        for gi in range(ngr + 1):
            if gi < ngr:
                pm = pMs[gi % 2]
                for kt in range(gi * 8, min(nt, gi * 8 + 8)):
                    L.append(("pe", lambda e, kt=kt, pm=pm: e.transpose(out=pm[:, (kt % 8) * 128:(kt % 8 + 1) * 128],
                                                                        in_=Mq[:, kt * 128:(kt + 1) * 128], identity=idt[:]),
                              [Mq, idt], [pm]))
            if gi >= 1:
                g0 = gi - 1
                pm = pMs[g0 % 2]
                k0 = g0 * 8
                n = min(nt, k0 + 8) - k0
                L.append(("act", lambda e, pm=pm, k0=k0, n=n: e.activation(
                    out=MT[:, k0:k0 + n, :].rearrange("p a b -> p (a b)"), in_=pm[:, 0:n * 128], func=AF.Copy),
                    [pm], [MT]))
        return L

    def qk(u, n, kbuf):
        a, hp, hh, blk = u
        ps = pS[n % 3]
        qa = qTa[a % 2]
        for i in range(4):
            kt = 16 * i + blk
            b.op("pe", lambda e, i=i, kt=kt, ps=ps, qa=qa, hh=hh, hp=hp, kbuf=kbuf: e.matmul(
                ps[:, i * 128:(i + 1) * 128], lhsT=ktp[kbuf][64 * hh:64 * hh + 64, kt * 128:(kt + 1) * 128],
                rhs=qa[64 * hh:64 * hh + 64, hp, :], start=True, stop=True), reads=[ktp[kbuf], qa], writes=[ps])

    def softmax_pv(u, n, kbuf):
        a, hp, hh, blk = u
        ps, pt = pS[n % 3], pT[n % 3]
        if blk == 0:
            ctr["acc"] += 1
        acc = pA[ctr["acc"] % 2]
        b.op("act", lambda e: e.activation(out=pt[:], in_=ps[:], func=AF.Exp), reads=[ps], writes=[pt])
        MT = MTs[a % 2]
        b.op("dve", lambda e: e.tensor_tensor(out=pt[:], in0=pt[:], in1=MT[:, 4 * blk:4 * blk + 4, :].rearrange("p a b -> p (a b)"),
                                              op=ALU.mult), reads=[pt, MT], writes=[pt])
        for i in range(4):
            kt = 16 * i + blk
            b.op("pe", lambda e, i=i, kt=kt: e.matmul(acc[:, 0:65], lhsT=pt[:, i * 128:(i + 1) * 128],
                                                      rhs=vp[kbuf][:, kt, hh * 65:(hh + 1) * 65],
                                                      start=(blk == 0 and i == 0), stop=(blk == a and i == 3)),
                 reads=[pt, vp[kbuf]], writes=[acc])
        if blk == a:
            head = 2 * hp + hh
            r = rec[ctr["acc"] % 2]
            sc, v0 = (0, 1) if hh == 0 else (64, 0)
            b.op("dve", lambda e: e.reciprocal(out=r[:], in_=acc[:, sc:sc + 1]), reads=[acc], writes=[r])
            b.op("dve", lambda e: e.tensor_scalar(out=o_tok[a][:, head * 64:(head + 1) * 64], in0=acc[:, v0:v0 + 64],
                                                  scalar1=r[:, 0:1], scalar2=None, op0=ALU.mult),
                 reads=[acc, r], writes=[o_tok[a]])

    load_slot_small(0)
    for it in pre_list(0):
        b.op(*it)
    for a in range(NS):
        if a + 1 < NS:
            load_slot_small(a + 1)
        kb_next = load_kv(a, 0)
        nxt = pre_list(a + 1) if a + 1 < NS else []
        done = 0
        units = [(a, hp, hh, blk) for hp in range(8) for hh in range(2) for blk in range(a + 1)]
        kbufs = {0: kb_next}
        kbufs[1] = load_kv(a, 1)
        qk(units[0], 0, kbufs[0])
        if len(units) > 1:
            qk(units[1], 1, kbufs[units[1][1]])
        for n, u in enumerate(units):
            _, hp, hh, blk = u
            if hh == 0 and blk == 0 and 1 <= hp and hp + 1 < 8:
                kbufs[hp + 1] = load_kv(a, hp + 1)
            if n + 2 < len(units):
                qk(units[n + 2], n + 2, kbufs[units[n + 2][1]])
            softmax_pv(u, n, kbufs[hp])
            want = (n + 1) * len(nxt) // len(units)
            while done < want:
                b.op(*nxt[done])
                done += 1
        while done < len(nxt):
            b.op(*nxt[done])
            done += 1
    return o_tok


BF = ml_dtypes.bfloat16


def rep(v, n=128):
    return np.ascontiguousarray(np.tile(np.asarray(v).reshape(1, -1), (n, 1)))


def own_tiles(arr_bs, c):
    bb, j = c // 4, c % 4
    a = arr_bs[bb]
    return np.ascontiguousarray(a.reshape(64, 128, *a.shape[1:])[j::4].reshape(2048, *a.shape[1:]))


def gather_tiles(per_core, bb):
    out = np.empty((64,) + per_core[0].shape[1:], per_core[0].dtype)
    for j in range(4):
        out[j::4] = per_core[bb * 4 + j]
    return out


def diff_masks(c):
    j = c % 4
    m = np.zeros((128, 4, 128), np.float32)
    for i in range(4):
        if i < j:
            m[:, i, :] = 1.0
        elif i == j:
            m[0:64, i, :] = 1.0
            m[64:128, i, 64:128] = 1.0
    return m.reshape(128, 512).astype(BF)


def dsa_negmask(c):
    return np.where(diff_masks(c).astype(np.float32).reshape(128, 4, 128).transpose(2, 1, 0).reshape(128, 512) > 0,
                    0.0, -1e30).astype(np.float32)


def build_L1():
    nc = bass.Bass("TRN2", target_bir_lowering=False)
    b = B(nc)
    io = {
        "x": b.dram("x", [2048, 1024], F32, "ExternalInput"),
        "pos": b.dram("pos", [128, 16], I32, "ExternalInput"),
        "gmix": b.dram("gmix", [128, 1024], F32, "ExternalInput"),
        "gqk": b.dram("gqk", [128, 2048], F32, "ExternalInput"),
        "w_in": b.dram("w_in", [1024, 3072], F32, "ExternalInput"),
        "QT": b.dram("QT", [16, 128, 1024], BF16, "ExternalOutput"),
        "KT": b.dram("KT", [16, 128, 1024], BF16, "ExternalOutput"),
        "V": b.dram("V", [16, 128, 8 * 129], BF16, "ExternalOutput"),
    }
    phase_diff_proj(b, io)
    b.finish()
    return nc


def build_L2():
    nc = bass.Bass("TRN2", target_bir_lowering=False)
    b = B(nc)
    io = {
        "QT": b.dram("QT", [16, 128, 1024], BF16, "ExternalInput"),
        "KTall": b.dram("KTall", [8, 128, 8192], BF16, "ExternalInput"),
        "Vall": b.dram("Vall", [8, 128, 64 * 129], BF16, "ExternalInput"),
        "maskT": b.dram("maskT", [128, 512], BF16, "ExternalInput"),
        "lam": b.dram("lam", [128, 256], F32, "ExternalInput"),
        "gsub": b.dram("gsub", [128, 128], F32, "ExternalInput"),
        "x": b.dram("x", [2048, 1024], F32, "ExternalInput"),
        "pos": b.dram("pos", [128, 16], I32, "ExternalInput"),
        "w_out": b.dram("w_out", [1024, 1024], F32, "ExternalInput"),
        "gmlp": b.dram("gmlp", [128, 1024], F32, "ExternalInput"),
        "w1": b.dram("w1", [1024, 4096], F32, "ExternalInput"),
        "w2": b.dram("w2", [4096, 1024], F32, "ExternalInput"),
        "gmix1": b.dram("gmix1", [128, 1024], F32, "ExternalInput"),
        "gqk2": b.dram("gqk2", [128, 2048], F32, "ExternalInput"),
        "gcq": b.dram("gcq", [128, 256], F32, "ExternalInput"),
        "w_in2": b.dram("w_in2", [1024, 2376], F32, "ExternalInput"),
        "w_uq": b.dram("w_uq", [256, 1024], F32, "ExternalInput"),
        "w_uqi": b.dram("w_uqi", [256, 512], F32, "ExternalInput"),
        "X2": b.dram("X2", [2048, 1024], F32, "ExternalOutput"),
        "QT2": b.dram("QT2", [16, 128, 1024], BF16, "ExternalOutput"),
        "KT2": b.dram("KT2", [16, 128, 1024], BF16, "ExternalOutput"),
        "V2": b.dram("V2", [16, 128, 16 * 65], BF16, "ExternalOutput"),
        "KI": b.dram("KI", [16, 128, 128], BF16, "ExternalOutput"),
        "QI": b.dram("QI", [16, 128, 512], BF16, "ExternalOutput"),
        "SG": b.dram("SG", [16, 128, 8], F32, "ExternalOutput"),
    }
    b.ident(); b.eps()
    pos_t = b.sb([128, NS], I32, "pos")
    b.op("sp", lambda e: e.dma_start(out=pos_t[:], in_=io["pos"][:]), writes=[pos_t], dma="ld_pos")
    cos, sin = rope_tables(b, pos_t)
    o_tok = [b.sb([128, 1024], BF16, f"otok{a}", top=True) for a in range(NS)]
    m1 = b.mark()
    phase_diff_attn(b, io, o_tok)
    b.release(m1)
    x_res = [b.sb([128, 1024], F32, f"xres{a}") for a in range(NS)]
    h2T = b.sb([128, 8, 2048], BF16, "h2T")
    m2 = b.mark()
    phase_post_attn(b, io, o_tok, x_res, h2T, io["w_out"][:], io["gmlp"][:], io["x"])
    b.release(m2)
    b.hi = ARENA_END
    m3 = b.mark()
    phase_mlp(b, x_res, h2T, io["w1"], io["w2"])
    b.release(m3)
    for a in range(NS):
        b.store("sp", lambda e, a=a: e.dma_start(out=io["X2"][a * 128:(a + 1) * 128, :], in_=x_res[a][:]),
                reads=[x_res[a]], dma="st_x2")
    phase_dsa_proj(b, io, x_res, cos, sin)
    b.finish()
    return nc


def l1_inputs(inp, c):
    return {"x": own_tiles(inp["x"], c),
            "pos": np.ascontiguousarray(own_tiles(inp["positions"], c).reshape(16, 128).T),
            "gmix": rep(inp["norm_mix"][0]),
            "gqk": rep(np.concatenate([np.tile(inp["diff_q_norm"][0], 16), np.tile(inp["diff_k_norm"][0], 16)])),
            "w_in": np.ascontiguousarray(inp["diff_w_in"][0])}


def l2_inputs(inp, r1):
    KTall = []; Vall = []
    for bb in range(2):
        kt = gather_tiles([r["KT"].reshape(16, 128, 8, 128) for r in r1], bb)
        KTall.append(np.ascontiguousarray(kt.transpose(2, 1, 0, 3).reshape(8, 128, 8192)))
        v = gather_tiles([r["V"].reshape(16, 128, 8, 129) for r in r1], bb)
        Vall.append(np.ascontiguousarray(v.transpose(2, 1, 0, 3).reshape(8, 128, 64 * 129)))
    lam = rep(np.concatenate([inp["diff_lam_q1"][0], inp["diff_lam_k1"][0], inp["diff_lam_q2"][0], inp["diff_lam_k2"][0]]))
    gqk2 = rep(np.concatenate([np.tile(inp["dsa_q_norm"][0], 16), np.tile(inp["dsa_k_norm"][0], 16)]))
    ins = []
    for c in range(8):
        ins.append({"QT": r1[c]["QT"], "KTall": KTall[c // 4], "Vall": Vall[c // 4], "maskT": diff_masks(c),
                    "lam": lam, "gsub": rep(inp["diff_subln"][0]),
                    "x": own_tiles(inp["x"], c),
                    "pos": np.ascontiguousarray(own_tiles(inp["positions"], c).reshape(16, 128).T),
                    "w_out": np.ascontiguousarray(inp["diff_w_out"][0]), "gmlp": rep(inp["norm_mlp"][0]),
                    "w1": np.ascontiguousarray(inp["mlp_w1"][0]), "w2": np.ascontiguousarray(inp["mlp_w2"][0]),
                    "gmix1": rep(inp["norm_mix"][1]), "gqk2": gqk2, "gcq": rep(inp["dsa_cq_norm"][0]),
                    "w_in2": np.ascontiguousarray(inp["dsa_w_in"][0]), "w_uq": np.ascontiguousarray(inp["dsa_w_uq"][0]),
                    "w_uqi": np.ascontiguousarray(inp["dsa_w_uq_idx"][0])})
    return ins


def run(nc, ins):
    res = run_bass_kernel_spmd(nc, ins, core_ids=list(range(8)))
    return [{k: np.asarray(v) for k, v in r.items()} for r in res.results]


def build_L3():
    nc = bass.Bass("TRN2", target_bir_lowering=False)
    b = B(nc)
    io = {
        "QT2": b.dram("QT2", [16, 128, 1024], BF16, "ExternalInput"),
        "QI": b.dram("QI", [16, 128, 512], BF16, "ExternalInput"),
        "SG": b.dram("SG", [16, 128, 8], F32, "ExternalInput"),
        "KIall": b.dram("KIall", [128, 8192], BF16, "ExternalInput"),
        "KT2all": b.dram("KT2all", [8, 128, 8192], BF16, "ExternalInput"),
        "V2all": b.dram("V2all", [8, 128, 64 * 130], BF16, "ExternalInput"),
        "negmask": b.dram("negmask", [128, 512], F32, "ExternalInput"),
        "X2": b.dram("X2", [2048, 1024], F32, "ExternalInput"),
        "w_out": b.dram("w_out", [1024, 1024], F32, "ExternalInput"),
        "gmlp": b.dram("gmlp", [128, 1024], F32, "ExternalInput"),
        "w1": b.dram("w1", [1024, 4096], F32, "ExternalInput"),
        "w2": b.dram("w2", [4096, 1024], F32, "ExternalInput"),
        "OUT": b.dram("OUT", [2048, 1024], F32, "ExternalOutput"),
    }
    b.ident(); b.eps()
    o_tok = [b.sb([128, 1024], BF16, f"otok{a}", top=True) for a in range(NS)]
    m1 = b.mark()
    phase_dsa_attn(b, io, o_tok)
    b.release(m1)
    x_res = [b.sb([128, 1024], F32, f"xres{a}") for a in range(NS)]
    h2T = b.sb([128, 8, 2048], BF16, "h2T")
    m2 = b.mark()
    phase_post_attn(b, io, o_tok, x_res, h2T, io["w_out"][:], io["gmlp"][:], io["X2"])
    b.release(m2)
    b.hi = ARENA_END
    m3 = b.mark()
    phase_mlp(b, x_res, h2T, io["w1"], io["w2"])
    b.release(m3)
    for a in range(NS):
        b.store("sp", lambda e, a=a: e.dma_start(out=io["OUT"][a * 128:(a + 1) * 128, :], in_=x_res[a][:]),
                reads=[x_res[a]], dma="st_out")
    b.finish()
    return nc


def l3_inputs(inp, r2):
    KIall = []; KTall = []; Vall = []
    for bb in range(2):
        ki = gather_tiles([r["KI"] for r in r2], bb)
        KIall.append(np.ascontiguousarray(ki.transpose(1, 0, 2).reshape(128, 8192)))
        kt = gather_tiles([r["KT2"].reshape(16, 128, 8, 128) for r in r2], bb)
        KTall.append(np.ascontiguousarray(kt.transpose(2, 1, 0, 3).reshape(8, 128, 8192)))
        v = gather_tiles([r["V2"].reshape(16, 128, 8, 130) for r in r2], bb)
        Vall.append(np.ascontiguousarray(v.transpose(2, 1, 0, 3).reshape(8, 128, 64 * 130)))
    ins = []
    for c in range(8):
        ins.append({"QT2": r2[c]["QT2"], "QI": r2[c]["QI"], "SG": r2[c]["SG"], "KIall": KIall[c // 4],
                    "KT2all": KTall[c // 4], "V2all": Vall[c // 4], "negmask": dsa_negmask(c),
                    "X2": r2[c]["X2"], "w_out": np.ascontiguousarray(inp["dsa_w_out"][0]), "gmlp": rep(inp["norm_mlp"][1]),
                    "w1": np.ascontiguousarray(inp["mlp_w1"][1]), "w2": np.ascontiguousarray(inp["mlp_w2"][1])})
    return ins


def assemble(r3):
    out = np.empty((2, 8192, 1024), np.float32)
    for bb in range(2):
        o = gather_tiles([r["OUT"].reshape(16, 128, 1024) for r in r3], bb)
        out[bb] = o.reshape(8192, 1024)
    return out


FUSED_IN = [
    ("x", [2048, 1024], F32), ("pos", [128, 16], I32), ("gmix", [128, 1024], F32), ("gqk", [128, 2048], F32),
    ("w_in", [1024, 3072], F32), ("maskT", [128, 512], BF16), ("lam", [128, 256], F32), ("gsub", [128, 128], F32),
    ("w_out", [1024, 1024], F32), ("gmlp", [128, 1024], F32), ("w1", [1024, 4096], F32), ("w2", [4096, 1024], F32),
    ("gmix1", [128, 1024], F32), ("gqk2", [128, 2048], F32), ("gcq", [128, 256], F32), ("w_in2", [1024, 2376], F32),
    ("w_uq", [256, 1024], F32), ("w_uqi", [256, 512], F32), ("negmask", [128, 512], F32),
    ("w_outb", [1024, 1024], F32), ("gmlpb", [128, 1024], F32), ("w1b", [1024, 4096], F32), ("w2b", [4096, 1024], F32),
]


def build_fused():
    nc = bass.Bass("TRN2", target_bir_lowering=False)
    _RANK.clear()
    b = B(nc)
    io = {n: b.dram(n, s, d, "ExternalInput") for (n, s, d) in FUSED_IN}
    io["OUT"] = b.dram("OUT", [2048, 1024], F32, "ExternalOutput")
    qt2 = nc.dram_tensor("scr_qt2", [16, 128, 1024], BF16).ap()
    qi = nc.dram_tensor("scr_qi", [16, 128, 512], BF16).ap()
    sg = nc.dram_tensor("scr_sg", [16, 128, 8], F32).ap()
    x2 = nc.dram_tensor("scr_x2", [2048, 1024], F32).ap()
    mqd = nc.dram_tensor("scr_mq", [16, 128, 8192], BF16).ap()
    scr = {"QT2": [T(qt2[a]) for a in range(NS)], "QI": [T(qi[a]) for a in range(NS)], "SG": [T(sg[a]) for a in range(NS)],
           "MQ": [T(mqd[a]) for a in range(NS)]}
    x2d = [T(x2[a * 128:(a + 1) * 128, :]) for a in range(NS)]

    b.ident(); b.eps()
    pos_t = b.sb([128, NS], I32, "pos")
    b.op("sp", lambda e: e.dma_start(out=pos_t[:], in_=io["pos"][:]), writes=[pos_t], dma="ld_pos")
    cos, sin = rope_tables(b, pos_t)
    o_tok = [b.sb([128, 1024], BF16, f"otok{a}", top=True) for a in range(NS)]
    hi_otok = b.hi
    qT_res = b.sb([128, NS, 1024], BF16, "qTres", top=True)
    zt = b.sb([128, 2048], BF16, "zeros")
    b.op("dve", lambda e: e.memset(zt[:], 0.0), writes=[zt])
    m0 = b.mark()
    G0 = Gather(b, "g0", 8, 4096, zt, "act")
    wst = [T(nc.alloc_sbuf_tensor_at(f"wstage{i}", [128, 8, 512], F32, offset=ARENA_END - (i + 1) * 16384)) for i in range(2)]
    f_diff_proj(b, io, qT_res, G0, cos, sin, wstage=wst)
    b.release(m0)
    G1 = Gather(b, "g1", 8, 4096, zt, "pool")
    GK = Gather(b, "gk", 1, 2048, zt, "pool")
    f_diff_attn(b, io, o_tok, qT_res, G0)
    b.release(m0)
    b.hi = hi_otok
    x_res = [b.sb([128, 1024], F32, f"xres{a}") for a in range(NS)]
    mh = b.mark()
    h2T = b.sb([128, 8, 2048], BF16, "h2T")
    m2 = b.mark()
    phase_post_attn(b, io, o_tok, x_res, h2T, io["w_out"][:], io["gmlp"][:], io["x"])
    b.release(m2)
    b.hi = ARENA_END
    pre = dsa_proj_weights(b, io, top=True)
    phase_mlp(b, x_res, h2T, io["w1"], io["w2"], hook=pre["load"])
    b.release((mh[0], b.hi))
    for a in range(NS):
        b.op("sp", lambda e, a=a: e.dma_start(out=x2d[a][:], in_=x_res[a][:]), reads=[x_res[a]], writes=[x2d[a]], dma="st_x2")
    io2 = dict(io)
    f_dsa_proj(b, io2, x_res, cos, sin, G1, GK, scr, pre=pre)
    b.release(m0)
    b.hi = ARENA_END
    m4 = b.mark()
    o_tok = f_dsa_attn(b, io, None, G1, GK, scr)
    b.release((m4[0], b.hi))
    x_res = [b.sb([128, 1024], F32, f"xresb{a}") for a in range(NS)]
    h2T = b.sb([128, 8, 2048], BF16, "h2Tb")
    m5 = b.mark()
    phase_post_attn(b, io, o_tok, x_res, h2T, io["w_outb"][:], io["gmlpb"][:], x2, xdeps=x2d)
    b.release(m5)
    b.hi = ARENA_END
    phase_mlp(b, x_res, h2T, io["w1b"], io["w2b"])
    for a in range(NS):
        b.store("sp", lambda e, a=a: e.dma_start(out=io["OUT"][a * 128:(a + 1) * 128, :], in_=x_res[a][:]),
                reads=[x_res[a]], dma="st_out")
    b.finish()
    return nc


def fused_inputs(inp):
    lam = rep(np.concatenate([inp["diff_lam_q1"][0], inp["diff_lam_k1"][0], inp["diff_lam_q2"][0], inp["diff_lam_k2"][0]]))
    gqk = rep(np.concatenate([np.tile(inp["diff_q_norm"][0], 16), np.tile(inp["diff_k_norm"][0], 16)]))
    gqk2 = rep(np.concatenate([np.tile(inp["dsa_q_norm"][0], 16), np.tile(inp["dsa_k_norm"][0], 16)]))
    c_ = np.ascontiguousarray
    shared = {"gmix": rep(inp["norm_mix"][0]), "gqk": gqk, "w_in": c_(inp["diff_w_in"][0]), "lam": lam,
              "gsub": rep(inp["diff_subln"][0]), "w_out": c_(inp["diff_w_out"][0]), "gmlp": rep(inp["norm_mlp"][0]),
              "w1": c_(inp["mlp_w1"][0]), "w2": c_(inp["mlp_w2"][0]), "gmix1": rep(inp["norm_mix"][1]), "gqk2": gqk2,
              "gcq": rep(inp["dsa_cq_norm"][0]), "w_in2": c_(inp["dsa_w_in"][0]), "w_uq": c_(inp["dsa_w_uq"][0]),
              "w_uqi": c_(inp["dsa_w_uq_idx"][0]), "w_outb": c_(inp["dsa_w_out"][0]), "gmlpb": rep(inp["norm_mlp"][1]),
              "w1b": c_(inp["mlp_w1"][1]), "w2b": c_(inp["mlp_w2"][1])}
    ins = []
    for c in range(8):
        d = dict(shared)
        d["x"] = own_tiles(inp["x"], c)
        d["pos"] = np.ascontiguousarray(own_tiles(inp["positions"], c).reshape(16, 128).T)
        d["maskT"] = diff_masks(c)
        d["negmask"] = dsa_negmask(c)
        ins.append(d)
    return ins


def kernel(**inputs):
    inp = {k: np.asarray(v) for k, v in inputs.items()}
    r = run(build_fused(), fused_inputs(inp))
    return assemble(r)
```

```python
import math
from contextlib import ExitStack
import numpy as np
import ml_dtypes
import concourse.bass as bass
import concourse.mybir as mybir
from concourse.bass_utils import run_bass_kernel_spmd


F32 = mybir.dt.float32
BF16 = mybir.dt.bfloat16
I32 = mybir.dt.int32
AF = mybir.ActivationFunctionType
ALU = mybir.AluOpType
AX = mybir.AxisListType


class Dep:
    __slots__ = ("w", "r")

    def __init__(self):
        self.w = None
        self.r = {}


class Sched:
    ENG = ("pe", "act", "dve", "pool", "sp")

    def __init__(self, nc):
        self.nc = nc
        self.ops = {e: [] for e in self.ENG}
        self.cnt = {e: 0 for e in self.ENG}
        self.known = {e: {} for e in self.ENG}
        self.dma_cnt = {}
        self.stack = ExitStack()
        self.nt = 0

    def sb(self, shape, dtype, name=None):
        self.nt += 1
        name = "sb_" + (name or f"t{self.nt}")
        return self.stack.enter_context(self.nc.sbuf_tensor(name, list(shape), dtype))

    def ps(self, shape, dtype, name=None):
        self.nt += 1
        name = "ps_" + (name or f"p{self.nt}")
        return self.stack.enter_context(self.nc.psum_tensor(name, list(shape), dtype))

    def op(self, eng, fn, reads=(), writes=(), dma=None, sem_inc=16):
        waits = {}

        def need(ev, raw):
            if ev is None:
                return
            key, val = ev
            if key == eng:
                if eng == "pe" or not raw:
                    return
            if waits.get(key, 0) < val:
                waits[key] = val

        for d in reads:
            need(d.w, True)
        for d in writes:
            need(d.w, False)
            for ev in d.r.items():
                need(ev, False)
        kn = self.known[eng]
        wl = []
        for key, val in waits.items():
            if kn.get(key, 0) >= val:
                continue
            kn[key] = val
            wl.append((key, val))
        if dma is not None:
            n = self.dma_cnt.get(dma, 0) + sem_inc
            self.dma_cnt[dma] = n
            ev = (dma, n)
            inc = (dma, sem_inc)
        else:
            self.cnt[eng] += 1
            ev = (eng, self.cnt[eng])
            inc = (eng, 1)
        self.ops[eng].append((wl, fn, inc))
        for d in reads:
            if d.r.get(ev[0], 0) < ev[1]:
                d.r[ev[0]] = ev[1]
        for d in writes:
            d.w = ev
            d.r = {}
        return ev

    def final_wait(self, eng, deps):
        waits = {}
        for d in deps:
            for ev in ([d.w] if d.w else []) + list(d.r.items()):
                if waits.get(ev[0], 0) < ev[1]:
                    waits[ev[0]] = ev[1]
        self.ops[eng].append((list(waits.items()), None, None))

    def barrier(self):
        waits = {e: self.cnt[e] for e in self.ENG if self.cnt[e] > 0}
        for k, n in self.dma_cnt.items():
            if not k.startswith("cc_"):
                waits[k] = n
        for e in self.ENG:
            kn = self.known[e]
            wl = []
            for key, val in waits.items():
                if key == e or kn.get(key, 0) >= val:
                    continue
                kn[key] = val
                wl.append((key, val))
            self.ops[e].append((wl, None, None))

    def final_events(self, eng, evs):
        waits = {}
        for ev in evs:
            if waits.get(ev[0], 0) < ev[1]:
                waits[ev[0]] = ev[1]
        self.ops[eng].append((list(waits.items()), None, None))

    def emit(self):
        nc = self.nc
        keys = set(self.ENG) | set(self.dma_cnt.keys())
        assert len(keys) <= 100, f"too many semaphores: {len(keys)}"
        sems = {}
        for k in sorted(keys):
            sems[k] = self.stack.enter_context(nc.semaphore("s_" + k))
        ops = self.ops

        def run(e, lst):
            for wl, fn, inc in lst:
                for key, val in wl:
                    e.wait_ge(sems[key], val)
                if fn is None:
                    continue
                ins = fn(e)
                if inc is not None:
                    ins.then_inc(sems[inc[0]], inc[1])

        with nc.Block() as block:
            @block.tensor
            def _(e):
                run(e, ops["pe"])

            @block.scalar
            def _(e):
                run(e, ops["act"])

            @block.vector
            def _(e):
                run(e, ops["dve"])

            @block.gpsimd
            def _(e):
                run(e, ops["pool"])

            @block.sync
            def _(e):
                run(e, ops["sp"])
        self.stack.close()


NS = 16
D = 1024
EPS = 1e-6
INV_FREQ = [500000.0 ** (-(2.0 * j) / 16.0) for j in range(8)]
TWO_PI_S = 6.28318
PI_S = 3.14159


class T:
    def __init__(self, t, d=None):
        self.t = t
        self.d = d if d is not None else Dep()

    def __getitem__(self, k):
        return self.t[k]


ARENA_BASE = 16512
ARENA_END = 16512 + 212736


class B:
    def __init__(self, nc):
        self.nc = nc
        self.S = Sched(nc)
        self._consts = {}
        self.final = []
        self.arena = nc.alloc_sbuf_tensor("arena", [128, ARENA_END - ARENA_BASE], mybir.dt.uint8)
        self.lo = ARENA_BASE
        self.hi = ARENA_END
        self.nt = 0
        self.banks = [T(nc.alloc_psum_tensor(f"bank{i}", [128, 512], F32)) for i in range(8)]
        self.banks16 = [T(bk.t[:].bitcast(BF16), bk.d) for bk in self.banks]

    def sb(self, shape, dt, name=None, top=False):
        self.nt += 1
        nm = f"sb{self.nt}_{name or 't'}"
        size = int(np.prod(shape[1:])) * mybir.dt.size(dt)
        size = (size + 31) // 32 * 32
        if top:
            self.hi -= size
            off = self.hi
        else:
            off = self.lo
            self.lo += size
        assert self.lo <= self.hi, f"SBUF arena overflow at {nm}: lo={self.lo} hi={self.hi}"
        return T(self.nc.alloc_sbuf_tensor_at(nm, list(shape), dt, offset=off))

    def mark(self):
        return (self.lo, self.hi)

    def release(self, m):
        self.lo, self.hi = m
        self.S.barrier()

    def dram(self, name, shape, dt, kind):
        t = T(self.nc.dram_tensor(name, list(shape), dt, kind=kind).ap())
        return t

    def op(self, eng, fn, reads=(), writes=(), dma=None, sem_inc=16):
        return self.S.op(eng, fn, [x.d for x in reads], [x.d for x in writes], dma, sem_inc)

    def store(self, eng, fn, reads, dma):
        ev = self.S.op(eng, fn, [x.d for x in reads], [], dma)
        self.final.append(ev)
        return ev

    def finish(self):
        self.S.final_events("sp", self.final)
        self.S.emit()

    def ident(self):
        if "ident" not in self._consts:
            idt = self.sb([128, 128], BF16, "ident")
            self.op("pool", lambda e: e.memset(idt[:], 0.0), writes=[idt])
            self.op("pool", lambda e: e.affine_select(out=idt[:], in_=idt[:], pattern=[[-1, 128]],
                                                     compare_op=ALU.not_equal, fill=1.0, base=0,
                                                     channel_multiplier=1), reads=[idt], writes=[idt])
            self._consts["ident"] = idt
        return self._consts["ident"]

    def eps(self):
        if "eps" not in self._consts:
            t = self.sb([128, 1], F32, "eps")
            self.op("pool", lambda e: e.memset(t[:], EPS), writes=[t])
            self._consts["eps"] = t
        return self._consts["eps"]


def rstd_from_ss(b, ss, n, scale):
    eps = b.eps()
    b.op("act", lambda e: e.activation(out=ss[:, 0:n], in_=ss[:, 0:n], func=AF.Ln, bias=eps[:], scale=scale),
         reads=[ss, eps], writes=[ss])
    b.op("act", lambda e: e.activation(out=ss[:, 0:n], in_=ss[:, 0:n], func=AF.Exp, scale=-0.5),
         reads=[ss], writes=[ss])


def rope_tables(b, pos_t):
    posf = b.sb([128, NS], F32, "posf")
    inv = b.sb([128, 8], F32, "invf")
    ang = b.sb([128, NS, 8], F32, "ang")
    ti = b.sb([128, NS, 8], I32, "angi")
    tf = b.sb([128, NS, 8], F32, "angf")
    neg = b.sb([128, NS, 8], F32, "angn")
    cos = b.sb([128, NS, 8], F32, "cos")
    sin = b.sb([128, NS, 8], F32, "sin")
    nb = b.sb([128, 1], F32, "negpi")
    b.op("pool", lambda e: e.memset(nb[:], -PI_S), writes=[nb])
    b.op("dve", lambda e: e.tensor_copy(out=posf[:], in_=pos_t[:]), reads=[pos_t], writes=[posf])
    for j in range(8):
        b.op("pool", lambda e, j=j: e.memset(inv[:, j:j + 1], INV_FREQ[j] / (2 * math.pi)), writes=[inv])
    b.op("dve", lambda e: e.tensor_tensor(out=ang[:], in0=posf[:].unsqueeze(2).to_broadcast([128, NS, 8]),
                                          in1=inv[:].unsqueeze(1).to_broadcast([128, NS, 8]), op=ALU.mult),
         reads=[posf, inv], writes=[ang])
    for (dst, off) in ((sin, 0.5), (cos, 0.75)):
        b.op("dve", lambda e, off=off: e.tensor_scalar(out=tf[:], in0=ang[:], scalar1=off, scalar2=None, op0=ALU.add),
             reads=[ang], writes=[tf])
        b.op("dve", lambda e: e.tensor_copy(out=ti[:], in_=tf[:]), reads=[tf], writes=[ti])
        b.op("dve", lambda e: e.tensor_copy(out=neg[:], in_=ti[:]), reads=[ti], writes=[neg])
        b.op("dve", lambda e: e.tensor_tensor(out=tf[:], in0=tf[:], in1=neg[:], op=ALU.subtract),
             reads=[tf, neg], writes=[tf])
        b.op("dve", lambda e: e.tensor_scalar(out=neg[:], in0=tf[:], scalar1=0.0, scalar2=None, op0=ALU.is_lt),
             reads=[tf], writes=[neg])
        b.op("dve", lambda e: e.tensor_tensor(out=tf[:], in0=tf[:], in1=neg[:], op=ALU.add),
             reads=[tf, neg], writes=[tf])
        b.op("act", lambda e, dst=dst: e.activation(out=dst[:], in_=tf[:], func=AF.Sin, bias=nb[:], scale=TWO_PI_S),
             reads=[tf, nb], writes=[dst])
    return cos, sin


def rmsnorm_transpose(b, xt, gt, hbf, hT, pT, ss, junk):
    idt = b.ident()
    b.op("act", lambda e: e.activation(out=junk[:], in_=xt[:], func=AF.Square, accum_out=ss[:, 0:1]),
         reads=[xt], writes=[junk, ss])
    rstd_from_ss(b, ss, 1, 1.0 / D)
    b.op("dve", lambda e: e.scalar_tensor_tensor(out=hbf[:], in0=xt[:], scalar=ss[:, 0:1], in1=gt[:],
                                                 op0=ALU.mult, op1=ALU.mult),
         reads=[xt, ss, gt], writes=[hbf])
    for kc in range(8):
        b.op("pe", lambda e, kc=kc: e.transpose(out=pT[:, kc * 128:(kc + 1) * 128],
                                                in_=hbf[:, kc * 128:(kc + 1) * 128], identity=idt[:]),
             reads=[hbf, idt], writes=[pT])
    b.op("dve", lambda e: e.tensor_copy(out=hT[:].rearrange("p a b -> p (a b)"), in_=pT[:]),
         reads=[pT], writes=[hT])


def headnorm_rope(b, stage, sq, ssq, nh, gain, cos_a, sin_a, outbf, tmp, norm=True):
    s3 = stage[:].rearrange("p (h d) -> p h d", d=64)
    if norm:
        b.op("dve", lambda e: e.tensor_reduce(out=ssq[:, 0:nh], in_=sq[:].rearrange("p (h d) -> p h d", d=64),
                                              axis=AX.X, op=ALU.add), reads=[sq], writes=[ssq])
        rstd_from_ss(b, ssq, nh, 1.0 / 64)
    b.op("dve", lambda e: e.tensor_tensor(out=s3, in0=s3, in1=ssq[:, 0:nh].unsqueeze(2).to_broadcast([128, nh, 64]),
                                          op=ALU.mult), reads=[stage, ssq], writes=[stage])
    if gain is not None:
        b.op("pool", lambda e: e.tensor_tensor(out=stage[:], in0=stage[:], in1=gain[:], op=ALU.mult),
             reads=[stage, gain], writes=[stage])
    b.op("act", lambda e: e.activation(out=outbf[:], in_=stage[:], func=AF.Copy), reads=[stage], writes=[outbf])
    o3 = outbf[:].rearrange("p (h d) -> p h d", d=64)
    x1 = s3[:, :, 0:8]
    x2 = s3[:, :, 8:16]
    cb = cos_a.unsqueeze(1).to_broadcast([128, nh, 8])
    sb_ = sin_a.unsqueeze(1).to_broadcast([128, nh, 8])
    t = tmp[:].rearrange("p (k h d) -> p k h d", k=4, d=8)
    eng = "pool"
    b.op(eng, lambda e: e.tensor_tensor(out=t[:, 0, 0:nh, :], in0=x1, in1=cb, op=ALU.mult), reads=[stage], writes=[tmp])
    b.op(eng, lambda e: e.tensor_tensor(out=t[:, 1, 0:nh, :], in0=x2, in1=sb_, op=ALU.mult), reads=[stage], writes=[tmp])
    b.op(eng, lambda e: e.tensor_tensor(out=t[:, 2, 0:nh, :], in0=x2, in1=cb, op=ALU.mult), reads=[stage], writes=[tmp])
    b.op(eng, lambda e: e.tensor_tensor(out=t[:, 3, 0:nh, :], in0=x1, in1=sb_, op=ALU.mult), reads=[stage], writes=[tmp])
    b.op(eng, lambda e: e.tensor_tensor(out=o3[:, :, 0:8], in0=t[:, 0, 0:nh, :], in1=t[:, 1, 0:nh, :], op=ALU.subtract),
         reads=[tmp], writes=[outbf])
    b.op(eng, lambda e: e.tensor_tensor(out=o3[:, :, 8:16], in0=t[:, 2, 0:nh, :], in1=t[:, 3, 0:nh, :], op=ALU.add),
         reads=[tmp], writes=[outbf])


def phase_diff_proj(b, io):
    idt = b.ident()
    pos_t = b.sb([128, NS], I32, "pos")
    b.op("sp", lambda e: e.dma_start(out=pos_t[:], in_=io["pos"][:]), writes=[pos_t], dma="ld_pos")
    gmix = b.sb([128, D], F32, "gmix")
    b.op("sp", lambda e: e.dma_start(out=gmix[:], in_=io["gmix"][:]), writes=[gmix], dma="ld_gmix")
    gqk = b.sb([128, 2048], F32, "gqk")
    b.op("sp", lambda e: e.dma_start(out=gqk[:], in_=io["gqk"][:]), writes=[gqk], dma="ld_gqk")
    b.op("pool", lambda e: e.tensor_scalar(out=gqk[:, 0:1024], in0=gqk[:, 0:1024], scalar1=0.125, scalar2=None,
                                           op0=ALU.mult), reads=[gqk], writes=[gqk])
    w = b.sb([128, 8, 3072], BF16, "w_in")
    for kc in range(8):
        for hf in range(3):
            b.op("pool", lambda e, kc=kc, hf=hf: e.dma_start(
                out=w[:, kc, hf * 1024:(hf + 1) * 1024],
                in_=io["w_in"][kc * 128:(kc + 1) * 128, hf * 1024:(hf + 1) * 1024]),
                writes=[w], dma="ld_w")
    cos, sin = rope_tables(b, pos_t)

    xts = [b.sb([128, D], F32, f"xt{i}") for i in range(2)]
    junk = b.sb([128, D], BF16, "junk")
    ss = b.sb([128, 1], F32, "ss")
    hbf = b.sb([128, D], BF16, "hbf")
    hT = b.sb([128, 8, 128], BF16, "hT")
    pT = [b.banks16[0], b.banks16[1]]
    pY = [b.banks[2 + i] for i in range(4)]
    stage = b.sb([128, 2048], F32, "stage")
    sq = b.sb([128, 2048], F32, "sq")
    ssq = b.sb([128, 32], F32, "ssq")
    tmp = b.sb([128, 4 * 32 * 8], F32, "ropetmp")
    qkbf = b.sb([128, 2048], BF16, "qkbf")
    qkT = [b.sb([128, 16, 128], BF16, f"qkT{i}") for i in range(2)]
    vaug = [b.sb([128, 8, 129], BF16, f"vaug{i}") for i in range(2)]
    for i in range(2):
        b.op("pool", lambda e, i=i: e.memset(vaug[i][:], 1.0), writes=[vaug[i]])

    for a in range(NS):
        xt = xts[a % 2]
        b.op("sp", lambda e, a=a, xt=xt: e.dma_start(out=xt[:], in_=io["x"][a * 128:(a + 1) * 128, :]),
             writes=[xt], dma=f"ld_x{a % 2}")
        rmsnorm_transpose(b, xt, gmix, hbf, hT, pT[0], ss, junk)
        for n in range(6):
            py = pY[n % 4]
            for kc in range(8):
                b.op("pe", lambda e, n=n, kc=kc, py=py: e.matmul(py[:], lhsT=hT[:, kc, :],
                                                                   rhs=w[:, kc, n * 512:(n + 1) * 512],
                                                                   start=(kc == 0), stop=(kc == 7)),
                     reads=[hT, w], writes=[py])
            if n < 4:
                b.op("act", lambda e, n=n, py=py: e.activation(out=stage[:, n * 512:(n + 1) * 512], in_=py[:], func=AF.Copy),
                     reads=[py], writes=[stage])
                b.op("act", lambda e, n=n, py=py: e.activation(out=sq[:, n * 512:(n + 1) * 512], in_=py[:], func=AF.Square),
                     reads=[py], writes=[sq])
            else:
                va = vaug[a % 2]
                b.op("dve", lambda e, n=n, py=py, va=va: e.tensor_copy(
                    out=va[:, (n - 4) * 4:(n - 4) * 4 + 4, 0:128], in_=py[:].rearrange("p (h d) -> p h d", d=128)),
                    reads=[py], writes=[va])
        va = vaug[a % 2]
        b.store("sp", lambda e, a=a, va=va: e.dma_start(out=io["V"][a], in_=va[:].rearrange("p h d -> p (h d)")),
                reads=[va], dma=f"st_v{a % 2}")
        headnorm_rope(b, stage, sq, ssq, 32, gqk, cos[:, a, :], sin[:, a, :], qkbf, tmp)
        qt = qkT[a % 2]
        for i in range(16):
            pt = pT[1] if i < 8 else pT[0]
            b.op("pe", lambda e, i=i, pt=pt: e.transpose(out=pt[:, (i % 8) * 128:(i % 8 + 1) * 128],
                                                         in_=qkbf[:, i * 128:(i + 1) * 128], identity=idt[:]),
                 reads=[qkbf, idt], writes=[pt])
            if i % 8 == 7:
                b.op("dve", lambda e, i=i, pt=pt, qt=qt: e.tensor_copy(
                    out=qt[:, (i // 8) * 8:(i // 8) * 8 + 8, :].rearrange("p a b -> p (a b)"), in_=pt[:]),
                    reads=[pt], writes=[qt])
        b.store("sp", lambda e, a=a, qt=qt: e.dma_start(out=io["QT"][a], in_=qt[:, 0:8, :].rearrange("p a b -> p (a b)")),
                reads=[qt], dma=f"st_q{a % 2}")
        b.store("sp", lambda e, a=a, qt=qt: e.dma_start(out=io["KT"][a], in_=qt[:, 8:16, :].rearrange("p a b -> p (a b)")),
                reads=[qt], dma=f"st_k{a % 2}")


def phase_diff_attn(b, io, o_tok):
    LAM_INIT = 0.2
    qT = b.sb([128, NS, 1024], BF16, "qT")
    for a in range(NS):
        b.op("sp", lambda e, a=a: e.dma_start(out=qT[:, a, :], in_=io["QT"][a]), writes=[qT], dma="ld_qT")
    maskT = b.sb([128, 512], BF16, "maskT")
    b.op("sp", lambda e: e.dma_start(out=maskT[:], in_=io["maskT"][:]), writes=[maskT], dma="ld_mask")
    lam = b.sb([128, 256], F32, "lam")
    b.op("sp", lambda e: e.dma_start(out=lam[:], in_=io["lam"][:]), writes=[lam], dma="ld_lam")
    gsub = b.sb([128, 128], F32, "gsub")
    b.op("sp", lambda e: e.dma_start(out=gsub[:], in_=io["gsub"][:]), writes=[gsub], dma="ld_gsub")
    b.op("pool", lambda e: e.tensor_scalar(out=gsub[:], in0=gsub[:], scalar1=1.0 - LAM_INIT, scalar2=None,
                                           op0=ALU.mult), reads=[gsub], writes=[gsub])
    lprod = b.sb([128, 128], F32, "lprod")
    l2 = b.sb([128, 2], F32, "l2")
    neglam = b.sb([128, 1], F32, "neglam")
    l4 = lam[:].rearrange("p (a b d) -> p a b d", a=2, b=2)
    b.op("dve", lambda e: e.tensor_tensor(out=lprod[:].rearrange("p (a d) -> p a d", a=2), in0=l4[:, :, 0, :],
                                          in1=l4[:, :, 1, :], op=ALU.mult), reads=[lam], writes=[lprod])
    b.op("dve", lambda e: e.tensor_reduce(out=l2[:], in_=lprod[:].rearrange("p (a d) -> p a d", a=2), axis=AX.X,
                                          op=ALU.add), reads=[lprod], writes=[l2])
    b.op("act", lambda e: e.activation(out=l2[:], in_=l2[:], func=AF.Exp), reads=[l2], writes=[l2])
    b.op("dve", lambda e: e.tensor_tensor(out=neglam[:], in0=l2[:, 1:2], in1=l2[:, 0:1], op=ALU.subtract),
         reads=[l2], writes=[neglam])
    b.op("dve", lambda e: e.tensor_scalar(out=neglam[:], in0=neglam[:], scalar1=-LAM_INIT, scalar2=None, op0=ALU.add),
         reads=[neglam], writes=[neglam])

    ktb = [b.sb([128, 8192], BF16, f"ktb{i}") for i in range(2)]
    vb = [b.sb([128, 64, 129], BF16, f"vb{i}") for i in range(2)]
    pS = [[b.banks[c * 2 + i] for i in range(2)] for c in range(2)]
    pA = [[b.banks[4 + c * 2 + i] for i in range(2)] for c in range(2)]
    pT = [[b.sb([128, 512], BF16, f"pTs{c}{i}") for i in range(2)] for c in range(2)]
    rec = [b.sb([128, 2], F32, f"rec{i}") for i in range(2)]
    o32 = [b.sb([128, 128], F32, f"o32{i}") for i in range(2)]
    oj = b.sb([128, 128], BF16, "ojunk")
    ss1 = [b.sb([128, 1], F32, f"ss1{i}") for i in range(2)]

    units = [(h, a, blk) for h in range(8) for a in range(NS) for blk in range(a + 1)]

    def load_head(h):
        kb, vv = ktb[h % 2], vb[h % 2]
        for part in range(4):
            b.op("sp", lambda e, h=h, kb=kb, part=part: e.dma_start(
                out=kb[:, part * 2048:(part + 1) * 2048], in_=io["KTall"][h][:, part * 2048:(part + 1) * 2048]),
                writes=[kb], dma=f"ld_kt{h % 2}")
            b.op("sp", lambda e, h=h, vv=vv, part=part: e.dma_start(
                out=vv[:, part * 16:(part + 1) * 16, :].rearrange("p a b -> p (a b)"),
                in_=io["Vall"][h][:, part * 16 * 129:(part + 1) * 16 * 129]),
                writes=[vv], dma=f"ld_v{h % 2}")

    def qk(u, n):
        h, a, blk = u
        kb = ktb[h % 2]
        for c in range(2):
            ps = pS[c][n % 2]
            for i in range(4):
                kt = 4 * blk + i
                b.op("pe", lambda e, c=c, i=i, kt=kt, ps=ps, kb=kb, a=a, h=h: e.matmul(
                    ps[:, i * 128:(i + 1) * 128], lhsT=kb[64 * c:64 * c + 64, kt * 128:(kt + 1) * 128],
                    rhs=qT[64 * c:64 * c + 64, a, h * 128:(h + 1) * 128], start=True, stop=True),
                    reads=[kb, qT], writes=[ps])

    def softmax_pv(u, n):
        h, a, blk = u
        vv = vb[h % 2]
        for c in range(2):
            ps = pS[c][n % 2]
            pt = pT[c][n % 2]
            acc = pA[c][a % 2]
            b.op("act", lambda e, ps=ps, pt=pt: e.activation(out=pt[:], in_=ps[:], func=AF.Exp),
                 reads=[ps], writes=[pt])
            if blk == a:
                b.op("pool", lambda e, pt=pt: e.tensor_tensor(out=pt[:], in0=pt[:], in1=maskT[:], op=ALU.mult),
                     reads=[pt, maskT], writes=[pt])
            for i in range(4):
                kt = 4 * blk + i
                b.op("pe", lambda e, i=i, kt=kt, pt=pt, acc=acc, vv=vv, blk=blk, a=a: e.matmul(
                    acc[:, 0:129], lhsT=pt[:, i * 128:(i + 1) * 128], rhs=vv[:, kt, :],
                    start=(blk == 0 and i == 0), stop=(blk == a and i == 3)),
                    reads=[pt, vv], writes=[acc])
        if blk == a:
            evac(h, a)

    def evac(h, a):
        k = a % 2
        a0, a1 = pA[0][k], pA[1][k]
        r, o, s = rec[k], o32[k], ss1[k]
        b.op("dve", lambda e: e.reciprocal(out=r[:, 0:1], in_=a0[:, 128:129]), reads=[a0], writes=[r])
        b.op("dve", lambda e: e.reciprocal(out=r[:, 1:2], in_=a1[:, 128:129]), reads=[a1], writes=[r])
        b.op("dve", lambda e: e.tensor_tensor(out=r[:, 1:2], in0=r[:, 1:2], in1=neglam[:], op=ALU.mult),
             reads=[r, neglam], writes=[r])
        b.op("dve", lambda e: e.tensor_scalar(out=o[:], in0=a0[:, 0:128], scalar1=r[:, 0:1], scalar2=None, op0=ALU.mult),
             reads=[a0, r], writes=[o])
        b.op("dve", lambda e: e.scalar_tensor_tensor(out=o[:], in0=a1[:, 0:128], scalar=r[:, 1:2], in1=o[:],
                                                     op0=ALU.mult, op1=ALU.add), reads=[a1, r, o], writes=[o])
        b.op("act", lambda e: e.activation(out=oj[:], in_=o[:], func=AF.Square, accum_out=s[:, 0:1]),
             reads=[o], writes=[oj, s])
        rstd_from_ss(b, s, 1, 1.0 / 128)
        ot = o_tok[a]
        b.op("dve", lambda e: e.scalar_tensor_tensor(out=ot[:, h * 128:(h + 1) * 128], in0=o[:], scalar=s[:, 0:1],
                                                     in1=gsub[:], op0=ALU.mult, op1=ALU.mult),
             reads=[o, s, gsub], writes=[ot])

    load_head(0)
    qk(units[0], 0)
    for n, u in enumerate(units):
        if u[1] == 0 and u[2] == 0 and u[0] + 1 < 8:
            load_head(u[0] + 1)
        if n + 1 < len(units):
            qk(units[n + 1], n + 1)
        softmax_pv(u, n)


def load_w_bf16(b, wt, src, nk, ncols, key, colchunk=1024):
    for c0 in range(0, ncols, colchunk):
        for kc in range(nk):
            c1 = min(ncols, c0 + colchunk)
            b.op("pool", lambda e, kc=kc, c0=c0, c1=c1: e.dma_start(
                out=wt[:, kc, c0:c1], in_=src[kc * 128:(kc + 1) * 128, c0:c1]), writes=[wt], dma=key)


def phase_post_attn(b, io, o_tok, x_res, h2T, wout_ap, gmlp_ap, xsrc, xdeps=None):
    idt = b.ident()
    m = b.mark()
    wout = b.sb([128, 8, 1024], BF16, "wout")
    load_w_bf16(b, wout, wout_ap, 8, 1024, "ld_wout")
    gm = b.sb([128, D], F32, "gmlp")
    b.op("sp", lambda e: e.dma_start(out=gm[:], in_=gmlp_ap), writes=[gm], dma="ld_gmlp")
    oT = [b.sb([128, 8, 128], BF16, f"oT{i}") for i in range(2)]
    junk = b.sb([128, D], BF16, "junk2")
    ss = [b.sb([128, 1], F32, f"ss2{i}") for i in range(2)]
    hbf = [b.sb([128, D], BF16, f"hbf2{i}") for i in range(2)]
    def head(a):
        xr = x_res[a]
        b.op("sp", lambda e, a=a, xr=xr: e.dma_start(out=xr[:], in_=xsrc[a * 128:(a + 1) * 128, :]),
             reads=([xdeps[a]] if xdeps else []), writes=[xr], dma="ld_xres")
        pt = b.banks16[a % 2]
        ot = oT[a % 2]
        for kc in range(8):
            b.op("pe", lambda e, kc=kc, pt=pt, a=a: e.transpose(out=pt[:, kc * 128:(kc + 1) * 128],
                                                                 in_=o_tok[a][:, kc * 128:(kc + 1) * 128], identity=idt[:]),
                 reads=[o_tok[a], idt], writes=[pt])
        b.op("act", lambda e, pt=pt, ot=ot: e.activation(out=ot[:].rearrange("p a b -> p (a b)"), in_=pt[:], func=AF.Copy),
             reads=[pt], writes=[ot])
        for n in range(2):
            py = b.banks[2 + (2 * a + n) % 4]
            for kc in range(8):
                b.op("pe", lambda e, kc=kc, n=n, py=py, ot=ot: e.matmul(py[:], lhsT=ot[:, kc, :],
                                                                         rhs=wout[:, kc, n * 512:(n + 1) * 512],
                                                                         start=(kc == 0), stop=(kc == 7)),
                     reads=[ot, wout], writes=[py])
            b.op("dve", lambda e, n=n, py=py, xr=xr: e.tensor_tensor(out=xr[:, n * 512:(n + 1) * 512],
                                                                     in0=xr[:, n * 512:(n + 1) * 512], in1=py[:], op=ALU.add),
                 reads=[xr, py], writes=[xr])

    def tail(a):
        rms_to_hT(b, x_res[a], gm, hbf[a % 2], h2T, a, b.banks16[6 + a % 2], ss[a % 2], junk)
    head(0)
    for a in range(NS):
        if a + 1 < NS:
            head(a + 1)
        tail(a)
    return m


def rms_to_hT(b, xr, gm, hbf, h2T, a, pt, ss, junk):
    idt = b.ident()
    b.op("act", lambda e: e.activation(out=junk[:], in_=xr[:], func=AF.Square, accum_out=ss[:, 0:1]),
         reads=[xr], writes=[junk, ss])
    rstd_from_ss(b, ss, 1, 1.0 / D)
    b.op("dve", lambda e: e.scalar_tensor_tensor(out=hbf[:], in0=xr[:], scalar=ss[:, 0:1], in1=gm[:],
                                                 op0=ALU.mult, op1=ALU.mult), reads=[xr, ss, gm], writes=[hbf])
    for kc in range(8):
        b.op("pe", lambda e, kc=kc: e.transpose(out=pt[:, kc * 128:(kc + 1) * 128],
                                                in_=hbf[:, kc * 128:(kc + 1) * 128], identity=idt[:]),
             reads=[hbf, idt], writes=[pt])
    b.op("act", lambda e: e.activation(out=h2T[:, :, a * 128:(a + 1) * 128],
                                       in_=pt[:].rearrange("p (k t) -> p k t", k=8), func=AF.Copy),
         reads=[pt], writes=[h2T])


def phase_mlp(b, x_res, h2T, w1_ap, w2_ap, hook=None):
    NFC = 8
    w1c = [b.sb([128, 8, 512], BF16, f"w1c{i}") for i in range(2)]
    w2c = [b.sb([128, 4, 1024], BF16, f"w2c{i}") for i in range(2)]
    rbuf = [b.sb([128, 512], F32, f"rbuf{i}") for i in range(2)]
    uT = [b.sb([128, 4, 512], BF16, f"uT{i}") for i in range(2)]
    pu = [b.banks[0], b.banks[1]]
    po = [b.banks[2 + i] for i in range(4)]

    def load_chunk(fc):
        w1, w2 = w1c[fc % 2], w2c[fc % 2]
        for kc in range(8):
            b.op("pool", lambda e, kc=kc, fc=fc, w1=w1: e.dma_start(
                out=w1[:, kc, :], in_=w1_ap[kc * 128:(kc + 1) * 128, fc * 512:(fc + 1) * 512]),
                writes=[w1], dma=f"ld_w1{fc % 2}")
        for ft in range(4):
            b.op("pool", lambda e, ft=ft, fc=fc, w2=w2: e.dma_start(
                out=w2[:, ft, :], in_=w2_ap[fc * 512 + ft * 128:fc * 512 + (ft + 1) * 128, :]),
                writes=[w2], dma=f"ld_w2{fc % 2}")

    steps = [(fc, tg) for fc in range(NFC) for tg in range(4)]
    cnt = {"u": 0, "o": 0}

    def stage_u(fc, tg):
        w1 = w1c[fc % 2]
        ut = uT[(fc * 4 + tg) % 2]
        for ft in range(4):
            p = pu[cnt["u"] % 2]
            r = rbuf[cnt["u"] % 2]
            cnt["u"] += 1
            for kc in range(8):
                b.op("pe", lambda e, kc=kc, ft=ft, p=p, w1=w1, tg=tg: e.matmul(
                    p[:], lhsT=w1[:, kc, ft * 128:(ft + 1) * 128], rhs=h2T[:, kc, tg * 512:(tg + 1) * 512],
                    start=(kc == 0), stop=(kc == 7)), reads=[w1, h2T], writes=[p])
            b.op("act", lambda e, p=p, r=r: e.activation(out=r[:], in_=p[:], func=AF.Relu), reads=[p], writes=[r])
            b.op("pool", lambda e, r=r, ut=ut, ft=ft: e.tensor_tensor(out=ut[:, ft, :], in0=r[:], in1=r[:], op=ALU.mult),
                 reads=[r], writes=[ut])

    def stage_o(fc, tg):
        w2 = w2c[fc % 2]
        ut = uT[(fc * 4 + tg) % 2]
        for tt in range(4):
            xr = x_res[tg * 4 + tt]
            for ch in range(2):
                p = po[cnt["o"] % 4]
                cnt["o"] += 1
                for ft in range(4):
                    b.op("pe", lambda e, ft=ft, tt=tt, ch=ch, p=p, ut=ut, w2=w2: e.matmul(
                        p[:], lhsT=ut[:, ft, tt * 128:(tt + 1) * 128], rhs=w2[:, ft, ch * 512:(ch + 1) * 512],
                        start=(ft == 0), stop=(ft == 3)), reads=[ut, w2], writes=[p])
                b.op("dve", lambda e, ch=ch, p=p, xr=xr: e.tensor_tensor(
                    out=xr[:, ch * 512:(ch + 1) * 512], in0=xr[:, ch * 512:(ch + 1) * 512], in1=p[:], op=ALU.add),
                    reads=[xr, p], writes=[xr])

    load_chunk(0)
    stage_u(*steps[0])
    for i, (fc, tg) in enumerate(steps):
        if tg == 0 and fc + 1 < NFC:
            load_chunk(fc + 1)
        if hook is not None and fc == 1 and tg == 1:
            hook()
        if i + 1 < len(steps):
            stage_u(*steps[i + 1])
        stage_o(fc, tg)


IDX_SCALE = (8 ** -0.5) * (64 ** -0.5)


def phase_dsa_proj(b, io, x_res, cos, sin):
    idt = b.ident()
    gmix = b.sb([128, D], F32, "gmix1")
    b.op("sp", lambda e: e.dma_start(out=gmix[:], in_=io["gmix1"][:]), writes=[gmix], dma="ld_gmix1")
    gqk = b.sb([128, 2048], F32, "gqk2")
    b.op("sp", lambda e: e.dma_start(out=gqk[:], in_=io["gqk2"][:]), writes=[gqk], dma="ld_gqk2")
    b.op("pool", lambda e: e.tensor_scalar(out=gqk[:, 0:1024], in0=gqk[:, 0:1024], scalar1=0.125, scalar2=None,
                                           op0=ALU.mult), reads=[gqk], writes=[gqk])
    gcq = b.sb([128, 256], F32, "gcq")
    b.op("sp", lambda e: e.dma_start(out=gcq[:], in_=io["gcq"][:]), writes=[gcq], dma="ld_gcq")
    w = b.sb([128, 8, 2376], BF16, "w_in2")
    load_w_bf16(b, w, io["w_in2"], 8, 2376, "ld_w2in", colchunk=792)
    wuq = b.sb([128, 2, 1024], BF16, "wuq")
    load_w_bf16(b, wuq, io["w_uq"], 2, 1024, "ld_wuq")
    wuqi = b.sb([128, 2, 512], BF16, "wuqi")
    load_w_bf16(b, wuqi, io["w_uqi"], 2, 512, "ld_wuqi")

    junk = b.sb([128, D], BF16, "junk3")
    ss = b.sb([128, 1], F32, "ss3")
    hbf = b.sb([128, D], BF16, "hbf3")
    hT = b.sb([128, 8, 128], BF16, "hT3")
    stage = b.sb([128, 2048], F32, "stage3")
    sq = b.sb([128, 2048], F32, "sq3")
    ssq = b.sb([128, 32], F32, "ssq3")
    tmp = b.sb([128, 4 * 32 * 8], F32, "ropetmp3")
    qkbf = b.sb([128, 2048], BF16, "qkbf3")
    qkT = [b.sb([128, 16, 128], BF16, f"qkT3{i}") for i in range(2)]
    vaug = [b.sb([128, 16, 65], BF16, f"vaug3{i}") for i in range(2)]
    for i in range(2):
        b.op("pool", lambda e, i=i: e.memset(vaug[i][:], 1.0), writes=[vaug[i]])
    cqs = b.sb([128, 256], F32, "cqs")
    ssc = b.sb([128, 1], F32, "ssc")
    cqbf = b.sb([128, 256], BF16, "cqbf")
    cqT = b.sb([128, 2, 128], BF16, "cqT")
    kis = b.sb([128, 64], F32, "kis")
    ksq = b.sb([128, 64], F32, "ksq")
    kss = b.sb([128, 1], F32, "kss")
    kibf = b.sb([128, 128], BF16, "kibf")
    kiT = [b.sb([128, 128], BF16, f"kiT{i}") for i in range(2)]
    wi = b.sb([128, 8], F32, "wi")
    sgn = [b.sb([128, 8], F32, f"sgn{i}") for i in range(2)]
    aw = b.sb([128, 8], F32, "aw")
    qis = b.sb([128, 512], F32, "qis")
    qibf = b.sb([128, 512], BF16, "qibf")
    qiT = [b.sb([128, 4, 128], BF16, f"qiT{i}") for i in range(2)]
    nbank = [0]

    def bank():
        nbank[0] += 1
        return b.banks[2 + nbank[0] % 4]

    def proj(py, lhs, nk, rhs_fn, ncol):
        for kc in range(nk):
            b.op("pe", lambda e, kc=kc: e.matmul(py[:, 0:ncol], lhsT=lhs[:, kc, :], rhs=rhs_fn(kc),
                                                 start=(kc == 0), stop=(kc == nk - 1)), reads=[lhs, w, wuq, wuqi], writes=[py])

    for a in range(NS):
        xr = x_res[a]
        rmsnorm_transpose(b, xr, gmix, hbf, hT, b.banks16[0], ss, junk)
        py = bank()
        proj(py, hT, 8, lambda kc: w[:, kc, 0:256], 256)
        b.op("act", lambda e, py=py: e.activation(out=cqs[:], in_=py[:, 0:256], func=AF.Copy), reads=[py], writes=[cqs])
        b.op("act", lambda e, py=py: e.activation(out=junk[:, 0:256], in_=py[:, 0:256], func=AF.Square, accum_out=ssc[:, 0:1]),
             reads=[py], writes=[junk, ssc])
        rstd_from_ss(b, ssc, 1, 1.0 / 256)
        b.op("dve", lambda e: e.scalar_tensor_tensor(out=cqbf[:], in0=cqs[:], scalar=ssc[:, 0:1], in1=gcq[:],
                                                     op0=ALU.mult, op1=ALU.mult), reads=[cqs, ssc, gcq], writes=[cqbf])
        p6 = b.banks16[6]
        for kc in range(2):
            b.op("pe", lambda e, kc=kc: e.transpose(out=p6[:, kc * 128:(kc + 1) * 128], in_=cqbf[:, kc * 128:(kc + 1) * 128],
                                                    identity=idt[:]), reads=[cqbf, idt], writes=[p6])
        b.op("dve", lambda e: e.tensor_copy(out=cqT[:].rearrange("p a b -> p (a b)"), in_=p6[:, 0:256]),
             reads=[p6], writes=[cqT])
        for n in range(2):
            py = bank()
            proj(py, hT, 8, lambda kc, n=n: w[:, kc, 256 + n * 512:256 + (n + 1) * 512], 512)
            b.op("act", lambda e, py=py, n=n: e.activation(out=stage[:, 1024 + n * 512:1024 + (n + 1) * 512], in_=py[:], func=AF.Copy),
                 reads=[py], writes=[stage])
            b.op("act", lambda e, py=py, n=n: e.activation(out=sq[:, 1024 + n * 512:1024 + (n + 1) * 512], in_=py[:], func=AF.Square),
                 reads=[py], writes=[sq])
        va = vaug[a % 2]
        for n in range(2):
            py = bank()
            proj(py, hT, 8, lambda kc, n=n: w[:, kc, 1280 + n * 512:1280 + (n + 1) * 512], 512)
            b.op("dve", lambda e, py=py, n=n, va=va: e.tensor_copy(out=va[:, n * 8:(n + 1) * 8, 0:64],
                                                                   in_=py[:].rearrange("p (h d) -> p h d", d=64)),
                 reads=[py], writes=[va])
        b.store("sp", lambda e, a=a, va=va: e.dma_start(out=io["V2"][a], in_=va[:].rearrange("p h d -> p (h d)")),
                reads=[va], dma=f"st_v2{a % 2}")
        py = bank()
        proj(py, hT, 8, lambda kc: w[:, kc, 2304:2376], 72)
        b.op("act", lambda e, py=py: e.activation(out=kis[:], in_=py[:, 0:64], func=AF.Copy), reads=[py], writes=[kis])
        b.op("act", lambda e, py=py: e.activation(out=ksq[:], in_=py[:, 0:64], func=AF.Square), reads=[py], writes=[ksq])
        b.op("dve", lambda e, py=py: e.tensor_copy(out=wi[:], in_=py[:, 64:72]), reads=[py], writes=[wi])
        for n in range(2):
            py = bank()
            proj(py, cqT, 2, lambda kc, n=n: wuq[:, kc, n * 512:(n + 1) * 512], 512)
            b.op("act", lambda e, py=py, n=n: e.activation(out=stage[:, n * 512:(n + 1) * 512], in_=py[:], func=AF.Copy),
                 reads=[py], writes=[stage])
            b.op("act", lambda e, py=py, n=n: e.activation(out=sq[:, n * 512:(n + 1) * 512], in_=py[:], func=AF.Square),
                 reads=[py], writes=[sq])
        py = bank()
        proj(py, cqT, 2, lambda kc: wuqi[:, kc, :], 512)
        b.op("act", lambda e, py=py: e.activation(out=qis[:], in_=py[:], func=AF.Copy), reads=[py], writes=[qis])
        headnorm_rope(b, stage, sq, ssq, 32, gqk, cos[:, a, :], sin[:, a, :], qkbf, tmp)
        qt = qkT[a % 2]
        for i in range(16):
            pt = b.banks16[1] if i < 8 else b.banks16[0]
            b.op("pe", lambda e, i=i, pt=pt: e.transpose(out=pt[:, (i % 8) * 128:(i % 8 + 1) * 128],
                                                         in_=qkbf[:, i * 128:(i + 1) * 128], identity=idt[:]),
                 reads=[qkbf, idt], writes=[pt])
            if i % 8 == 7:
                b.op("dve", lambda e, i=i, pt=pt, qt=qt: e.tensor_copy(
                    out=qt[:, (i // 8) * 8:(i // 8) * 8 + 8, :].rearrange("p a b -> p (a b)"), in_=pt[:]),
                    reads=[pt], writes=[qt])
        b.store("sp", lambda e, a=a, qt=qt: e.dma_start(out=io["QT2"][a], in_=qt[:, 0:8, :].rearrange("p a b -> p (a b)")),
                reads=[qt], dma=f"st_q2{a % 2}")
        b.store("sp", lambda e, a=a, qt=qt: e.dma_start(out=io["KT2"][a], in_=qt[:, 8:16, :].rearrange("p a b -> p (a b)")),
                reads=[qt], dma=f"st_k2{a % 2}")
        ki_half = T(kibf.t[:, 0:64], kibf.d)
        headnorm_rope(b, kis, ksq, kss, 1, None, cos[:, a, :], sin[:, a, :], ki_half, tmp)
        b.op("pool", lambda e: e.tensor_copy(out=kibf[:, 64:128], in_=kibf[:, 0:64]), reads=[kibf], writes=[kibf])
        p7 = b.banks16[7]
        b.op("pe", lambda e: e.transpose(out=p7[:, 0:128], in_=kibf[:], identity=idt[:]), reads=[kibf, idt], writes=[p7])
        kt_ = kiT[a % 2]
        b.op("dve", lambda e, kt_=kt_: e.tensor_copy(out=kt_[:], in_=p7[:, 0:128]), reads=[p7], writes=[kt_])
        b.store("sp", lambda e, a=a, kt_=kt_: e.dma_start(out=io["KI"][a], in_=kt_[:]), reads=[kt_], dma=f"st_ki{a % 2}")
        sg = sgn[a % 2]
        b.op("act", lambda e, sg=sg: e.activation(out=sg[:], in_=wi[:], func=AF.Sign), reads=[wi], writes=[sg])
        b.op("dve", lambda e, sg=sg: e.scalar_tensor_tensor(out=aw[:], in0=wi[:], scalar=IDX_SCALE, in1=sg[:],
                                                           op0=ALU.mult, op1=ALU.mult), reads=[wi, sg], writes=[aw])
        b.store("sp", lambda e, a=a, sg=sg: e.dma_start(out=io["SG"][a], in_=sg[:]), reads=[sg], dma=f"st_sg{a % 2}")
        headnorm_rope(b, qis, None, aw, 8, None, cos[:, a, :], sin[:, a, :], qibf, tmp, norm=False)
        qi_ = qiT[a % 2]
        for i in range(4):
            b.op("pe", lambda e, i=i: e.transpose(out=p7[:, 256 + i * 128:256 + (i + 1) * 128],
                                                  in_=qibf[:, i * 128:(i + 1) * 128], identity=idt[:]),
                 reads=[qibf, idt], writes=[p7])
        b.op("dve", lambda e, qi_=qi_: e.tensor_copy(out=qi_[:].rearrange("p a b -> p (a b)"), in_=p7[:, 256:768]),
             reads=[p7], writes=[qi_])
        b.store("sp", lambda e, a=a, qi_=qi_: e.dma_start(out=io["QI"][a], in_=qi_[:].rearrange("p a b -> p (a b)")),
                reads=[qi_], dma=f"st_qi{a % 2}")


NIT = 22
TOPK = 256


def phase_dsa_attn(b, io, o_tok):
    idt = b.ident()
    kia = b.sb([128, 8192], BF16, "kiall")
    for part in range(4):
        b.op("sp", lambda e, part=part: e.dma_start(out=kia[:, part * 2048:(part + 1) * 2048],
                                                    in_=io["KIall"][:, part * 2048:(part + 1) * 2048]),
             writes=[kia], dma="ld_kia")
    negm = b.sb([128, 512], F32, "negm")
    b.op("sp", lambda e: e.dma_start(out=negm[:], in_=io["negmask"][:]), writes=[negm], dma="ld_negm")
    cW = b.sb([128, NIT], F32, "cW")
    for i in range(NIT):
        b.op("pool", lambda e, i=i: e.memset(cW[:, i:i + 1], 2.0 ** (-i)), writes=[cW])
    Ib = b.sb([128, 8192], F32, "Ibuf")
    Mq = b.sb([128, 8192], BF16, "Mq")
    MT = b.sb([128, 64, 128], BF16, "MT")
    ktp = [b.sb([128, 8192], BF16, f"ktp{i}") for i in range(2)]
    vp = [b.sb([128, 64, 130], BF16, f"vp{i}") for i in range(2)]
    qTa = [b.sb([128, 8, 128], BF16, f"qTa{i}") for i in range(2)]
    qiTa = [b.sb([128, 4, 128], BF16, f"qiTa{i}") for i in range(2)]
    sgn = [b.sb([128, 8], F32, f"sgna{i}") for i in range(2)]
    tb = [b.sb([128, 512], F32, f"tb{i}") for i in range(2)]
    pT = [b.sb([128, 512], BF16, f"pTd{i}") for i in range(2)]
    m1 = b.sb([128, 1], F32, "bm1")
    lo = b.sb([128, 1], F32, "blo")
    mid = b.sb([128, 1], F32, "bmid")
    cnt = b.sb([128, 1], F32, "bcnt")
    g = b.sb([128, 1], F32, "bg")
    W = b.sb([128, NIT], F32, "bW")
    rec = [b.sb([128, 1], F32, f"recd{i}") for i in range(2)]
    pS = [b.banks[0], b.banks[1]]
    pA = [b.banks[2], b.banks[3]]
    pI = [b.banks[4], b.banks[5]]
    pM = [b.banks16[6], b.banks16[7]]
    ctr = {"i": 0, "s": 0, "acc": 0, "kv": 0}

    def load_slot_small(a):
        b.op("sp", lambda e, a=a: e.dma_start(out=qTa[a % 2][:].rearrange("p a b -> p (a b)"), in_=io["QT2"][a]),
             writes=[qTa[a % 2]], dma=f"ld_qTa{a % 2}")
        b.op("sp", lambda e, a=a: e.dma_start(out=qiTa[a % 2][:].rearrange("p a b -> p (a b)"), in_=io["QI"][a]),
             writes=[qiTa[a % 2]], dma=f"ld_qiTa{a % 2}")
        b.op("sp", lambda e, a=a: e.dma_start(out=sgn[a % 2][:], in_=io["SG"][a]), writes=[sgn[a % 2]], dma=f"ld_sgn{a % 2}")

    def load_kv(a, hp):
        k = ctr["kv"] % 2
        ctr["kv"] += 1
        nv = 512 * (a + 1)
        nt = 4 * (a + 1)
        b.op("sp", lambda e, k=k, hp=hp, nv=nv: e.dma_start(out=ktp[k][:, 0:nv], in_=io["KT2all"][hp][:, 0:nv]),
             writes=[ktp[k]], dma=f"ld_ktp{k}")
        b.op("sp", lambda e, k=k, hp=hp, nt=nt: e.dma_start(out=vp[k][:, 0:nt, :].rearrange("p a b -> p (a b)"),
                                                            in_=io["V2all"][hp][:, 0:nt * 130]),
             writes=[vp[k]], dma=f"ld_vp{k}")
        return k

    def indexer(a):
        nb = a + 1
        nv = 512 * nb
        qi, sg = qiTa[a % 2], sgn[a % 2]
        for blk in range(nb):
            for head in range(8):
                hp, hh = head // 2, head % 2
                py = pI[ctr["i"] % 2]
                t = tb[ctr["i"] % 2]
                ctr["i"] += 1
                b.op("pe", lambda e, py=py, hp=hp, hh=hh, blk=blk, qi=qi: e.matmul(
                    py[:], lhsT=qi[64 * hh:64 * hh + 64, hp, :], rhs=kia[64 * hh:64 * hh + 64, blk * 512:(blk + 1) * 512],
                    start=True, stop=True), reads=[qi, kia], writes=[py])
                b.op("act", lambda e, py=py, t=t: e.activation(out=t[:], in_=py[:], func=AF.Relu), reads=[py], writes=[t])
                if head == 0:
                    b.op("dve", lambda e, t=t, blk=blk, sg=sg: e.tensor_scalar(
                        out=Ib[:, blk * 512:(blk + 1) * 512], in0=t[:], scalar1=sg[:, 0:1], scalar2=None, op0=ALU.mult),
                        reads=[t, sg], writes=[Ib])
                else:
                    b.op("dve", lambda e, t=t, blk=blk, sg=sg, head=head: e.scalar_tensor_tensor(
                        out=Ib[:, blk * 512:(blk + 1) * 512], in0=t[:], scalar=sg[:, head:head + 1],
                        in1=Ib[:, blk * 512:(blk + 1) * 512], op0=ALU.mult, op1=ALU.add),
                        reads=[t, sg, Ib], writes=[Ib])
        b.op("dve", lambda e: e.tensor_reduce(out=m1[:], in_=Ib[:, 0:nv], axis=AX.X, op=ALU.max, apply_absolute_value=True),
             reads=[Ib], writes=[m1])
        b.op("dve", lambda e: e.tensor_tensor(out=Ib[:, nv - 512:nv], in0=Ib[:, nv - 512:nv], in1=negm[:], op=ALU.add),
             reads=[Ib, negm], writes=[Ib])
        b.op("dve", lambda e: e.tensor_scalar(out=m1[:], in0=m1[:], scalar1=1.0, scalar2=None, op0=ALU.add),
             reads=[m1], writes=[m1])
        b.op("dve", lambda e: e.tensor_scalar(out=lo[:], in0=m1[:], scalar1=-1.0, scalar2=None, op0=ALU.mult),
             reads=[m1], writes=[lo])
        b.op("dve", lambda e: e.tensor_scalar(out=W[:], in0=cW[:], scalar1=m1[:, 0:1], scalar2=None, op0=ALU.mult),
             reads=[cW, m1], writes=[W])
        b.op("dve", lambda e: e.tensor_tensor(out=mid[:], in0=lo[:], in1=W[:, 0:1], op=ALU.add), reads=[lo, W], writes=[mid])
        for i in range(NIT):
            b.op("dve", lambda e: e.tensor_scalar(out=Mq[:, 0:nv], in0=Ib[:, 0:nv], scalar1=mid[:, 0:1], scalar2=0.0,
                                                  op0=ALU.is_ge, op1=ALU.add, accum_out=cnt[:, 0:1]),
                 reads=[Ib, mid], writes=[Mq, cnt])
            b.op("dve", lambda e, i=i: e.tensor_scalar(out=g[:], in0=cnt[:], scalar1=TOPK - 0.5, scalar2=W[:, i:i + 1],
                                                       op0=ALU.is_ge, op1=ALU.mult), reads=[cnt, W], writes=[g])
            b.op("dve", lambda e: e.tensor_tensor(out=lo[:], in0=lo[:], in1=g[:], op=ALU.add), reads=[lo, g], writes=[lo])
            if i + 1 < NIT:
                b.op("dve", lambda e, i=i: e.tensor_tensor(out=mid[:], in0=lo[:], in1=W[:, i + 1:i + 2], op=ALU.add),
                     reads=[lo, W], writes=[mid])
        b.op("dve", lambda e: e.tensor_scalar(out=Mq[:, 0:nv], in0=Ib[:, 0:nv], scalar1=lo[:, 0:1], scalar2=None,
                                              op0=ALU.is_ge), reads=[Ib, lo], writes=[Mq])
        nt = 4 * nb
        for kt in range(nt):
            pm = pM[(kt // 8) % 2]
            b.op("pe", lambda e, kt=kt, pm=pm: e.transpose(out=pm[:, (kt % 8) * 128:(kt % 8 + 1) * 128],
                                                           in_=Mq[:, kt * 128:(kt + 1) * 128], identity=idt[:]),
                 reads=[Mq, idt], writes=[pm])
            if kt % 8 == 7 or kt == nt - 1:
                k0 = (kt // 8) * 8
                n = kt - k0 + 1
                b.op("act", lambda e, pm=pm, k0=k0, n=n: e.activation(
                    out=MT[:, k0:k0 + n, :].rearrange("p a b -> p (a b)"), in_=pm[:, 0:n * 128], func=AF.Copy),
                    reads=[pm], writes=[MT])

    def qk(u, n, kbuf):
        a, hp, hh, blk = u
        ps = pS[n % 2]
        qa = qTa[a % 2]
        for i in range(4):
            kt = 4 * blk + i
            b.op("pe", lambda e, i=i, kt=kt, ps=ps, qa=qa, hh=hh, hp=hp, kbuf=kbuf: e.matmul(
                ps[:, i * 128:(i + 1) * 128], lhsT=ktp[kbuf][64 * hh:64 * hh + 64, kt * 128:(kt + 1) * 128],
                rhs=qa[64 * hh:64 * hh + 64, hp, :], start=True, stop=True), reads=[ktp[kbuf], qa], writes=[ps])

    def softmax_pv(u, n, kbuf):
        a, hp, hh, blk = u
        ps, pt = pS[n % 2], pT[n % 2]
        if blk == 0:
            ctr["acc"] += 1
        acc = pA[ctr["acc"] % 2]
        b.op("act", lambda e: e.activation(out=pt[:], in_=ps[:], func=AF.Exp), reads=[ps], writes=[pt])
        b.op("dve", lambda e: e.tensor_tensor(out=pt[:], in0=pt[:], in1=MT[:, 4 * blk:4 * blk + 4, :].rearrange("p a b -> p (a b)"),
                                              op=ALU.mult), reads=[pt, MT], writes=[pt])
        for i in range(4):
            kt = 4 * blk + i
            b.op("pe", lambda e, i=i, kt=kt: e.matmul(acc[:, 0:65], lhsT=pt[:, i * 128:(i + 1) * 128],
                                                      rhs=vp[kbuf][:, kt, hh * 65:(hh + 1) * 65],
                                                      start=(blk == 0 and i == 0), stop=(blk == a and i == 3)),
                 reads=[pt, vp[kbuf]], writes=[acc])
        if blk == a:
            head = 2 * hp + hh
            r = rec[ctr["acc"] % 2]
            b.op("dve", lambda e: e.reciprocal(out=r[:], in_=acc[:, 64:65]), reads=[acc], writes=[r])
            b.op("dve", lambda e: e.tensor_scalar(out=o_tok[a][:, head * 64:(head + 1) * 64], in0=acc[:, 0:64],
                                                  scalar1=r[:, 0:1], scalar2=None, op0=ALU.mult),
                 reads=[acc, r], writes=[o_tok[a]])

    load_slot_small(0)
    for a in range(NS):
        if a + 1 < NS:
            load_slot_small(a + 1)
        kb_next = load_kv(a, 0)
        indexer(a)
        units = [(a, hp, hh, blk) for hp in range(8) for hh in range(2) for blk in range(a + 1)]
        kbufs = {}
        kbufs[0] = kb_next
        qk(units[0], 0, kbufs[0])
        for n, u in enumerate(units):
            _, hp, hh, blk = u
            if hh == 0 and blk == 0 and hp + 1 < 8:
                kbufs[hp + 1] = load_kv(a, hp + 1)
            if n + 1 < len(units):
                qk(units[n + 1], n + 1, kbufs[units[n + 1][1]])
            softmax_pv(u, n, kbufs[hp])


GROUPS = [[0, 1, 2, 3], [4, 5, 6, 7]]
_RANK = {}


class Gather:
    def __init__(self, b, name, nblk, cols, zt, zero_eng="act"):
        self.b, self.name, self.nblk, self.cols = b, name, nblk, cols
        nc = b.nc
        self.xb = nc.dram_tensor(name + "_xb", [nblk, 512, cols], BF16).ap()
        self.yb = nc.dram_tensor(name + "_yb", [nblk, 512, cols], BF16).ap()
        self.xo = nc.dram_tensor(name + "_xo", [nblk, 128, cols], BF16).ap()
        self.od = [T(self.xo[h]) for h in range(nblk)]
        self.xd = [T(self.xb[h]) for h in range(nblk)]
        self.yd = [T(self.yb[h]) for h in range(nblk)]
        for h in range(nblk):
            for r in range(4):
                for c0 in range(0, cols, 2048):
                    b.op(zero_eng, lambda e, h=h, r=r, c0=c0: e.dma_start(
                        out=self.xb[h, r * 128:(r + 1) * 128, c0:c0 + 2048], in_=zt[:, 0:2048]),
                        reads=[zt], writes=[self.xd[h]], dma="zero_" + name)

    def put(self, h, c0, c1, src_ap, reads, key):
        self.b.op("sp", lambda e: e.dma_start(out=self.xo[h, :, c0:c1], in_=src_ap), reads=reads,
                  writes=[self.od[h]], dma=key)

    def place(self, h, eng):
        def fn(e):
            if eng not in _RANK:
                _RANK[eng] = e.partition_id() % 4
            r = _RANK[eng]
            return e.dma_start(out=self.xb[h, bass.ds(r * 128, 128), :], in_=self.xo[h])
        self.b.op(eng, fn, reads=[self.od[h]], writes=[self.xd[h]], dma="place_" + self.name)

    def reduce(self, h):
        b = self.b
        b.op("pool", lambda e: e.collective_compute("AllReduce", ALU.add, replica_groups=GROUPS,
                                                    ins=[self.xb[h]], outs=[self.yb[h]]),
             reads=[self.xd[h]], writes=[self.yd[h]], dma=f"cc_{self.name}{h}", sem_inc=1)


def f_diff_proj(b, io, qT_res, G0, cos, sin, wstage=None):
    idt = b.ident()
    gmix = b.sb([128, D], F32, "gmix")
    b.op("sp", lambda e: e.dma_start(out=gmix[:], in_=io["gmix"][:]), writes=[gmix], dma="ld_gmix")
    gqk = b.sb([128, 2048], F32, "gqk")
    b.op("sp", lambda e: e.dma_start(out=gqk[:], in_=io["gqk"][:]), writes=[gqk], dma="ld_gqk")
    b.op("pool", lambda e: e.tensor_scalar(out=gqk[:, 0:1024], in0=gqk[:, 0:1024], scalar1=0.125, scalar2=None,
                                           op0=ALU.mult), reads=[gqk], writes=[gqk])
    xts = [b.sb([128, D], F32, f"xt{i}") for i in range(2)]

    def load_x(a):
        xt = xts[a % 2]
        b.op("sp", lambda e: e.dma_start(out=xt[:], in_=io["x"][a * 128:(a + 1) * 128, :]), writes=[xt], dma=f"ld_x{a % 2}")
    load_x(0)
    w = b.sb([128, 8, 3072], BF16, "w_in")
    if wstage is None:
        load_w_bf16(b, w, io["w_in"], 8, 3072, "ld_w")
    else:
        for n in range(6):
            stg = wstage[n % 2]
            b.op("sp", lambda e, n=n, stg=stg: e.dma_start(
                out=stg[:], in_=io["w_in"][:, n * 512:(n + 1) * 512].rearrange("(k p) c -> p k c", p=128)),
                writes=[stg], dma=f"ld_wst{n % 2}")
            b.op("act" if n % 2 == 0 else "dve",
                 (lambda e, n=n, stg=stg: e.activation(out=w[:, :, n * 512:(n + 1) * 512], in_=stg[:], func=AF.Copy)) if n % 2 == 0
                 else (lambda e, n=n, stg=stg: e.tensor_copy(out=w[:, :, n * 512:(n + 1) * 512], in_=stg[:])),
                 reads=[stg], writes=[w])
    junk_ = [b.sb([128, D], BF16, f"junk{i}") for i in range(2)]
    ss_ = [b.sb([128, 1], F32, f"ss{i}") for i in range(2)]
    hbf_ = [b.sb([128, D], BF16, f"hbf{i}") for i in range(2)]
    hT_ = [b.sb([128, 8, 128], BF16, f"hT{i}") for i in range(2)]
    pT = [b.banks16[0], b.banks16[1]]
    pY = [b.banks[2 + i] for i in range(4)]
    stage_ = [b.sb([128, 2048], F32, f"stage{i}") for i in range(2)]
    sq_ = [b.sb([128, 2048], BF16, f"sq{i}") for i in range(2)]
    ssq_ = [b.sb([128, 32], F32, f"ssq{i}") for i in range(2)]
    tmp1 = b.sb([128, 4 * 32 * 8], F32, "ropetmp"); tmp_ = [tmp1, tmp1]
    qkbf_ = [b.sb([128, 2048], BF16, f"qkbf{i}") for i in range(2)]
    kT = [b.sb([128, 8, 128], BF16, f"kTst{i}") for i in range(2)]
    vst = [b.sb([128, 8, 128], BF16, f"vst{i}") for i in range(2)]

    def body(a, junk, ss, hbf, hT, stage, sq, ssq, tmp, qkbf):
        xt = xts[a % 2]
        if a + 1 < NS:
            load_x(a + 1)
        rmsnorm_transpose(b, xt, gmix, hbf, hT, pT[0], ss, junk)
        va = vst[a % 2]
        for n in range(6):
            py = pY[n % 4]
            for kc in range(8):
                b.op("pe", lambda e, n=n, kc=kc, py=py: e.matmul(py[:], lhsT=hT[:, kc, :],
                                                                   rhs=w[:, kc, n * 512:(n + 1) * 512],
                                                                   start=(kc == 0), stop=(kc == 7)),
                     reads=[hT, w], writes=[py])
            if n < 4:
                b.op("act", lambda e, n=n, py=py: e.activation(out=stage[:, n * 512:(n + 1) * 512], in_=py[:], func=AF.Copy),
                     reads=[py], writes=[stage])
                b.op("act", lambda e, n=n, py=py: e.activation(out=sq[:, n * 512:(n + 1) * 512], in_=py[:], func=AF.Square),
                     reads=[py], writes=[sq])
            else:
                b.op("dve", lambda e, n=n, py=py, va=va: e.tensor_copy(
                    out=va[:, (n - 4) * 4:(n - 4) * 4 + 4, :], in_=py[:].rearrange("p (h d) -> p h d", d=128)),
                    reads=[py], writes=[va])
        for h in range(8):
            G0.put(h, 2048 + a * 128, 2048 + (a + 1) * 128, va[:, h, :], [va], f"st_v{a % 2}")

    def tail(a, junk, ss, hbf, hT, stage, sq, ssq, tmp, qkbf):
        headnorm_rope(b, stage, sq, ssq, 32, gqk, cos[:, a, :], sin[:, a, :], qkbf, tmp)
        kt = kT[a % 2]
        for i in range(16):
            pt = b.banks16[6] if i < 8 else b.banks16[7]
            b.op("pe", lambda e, i=i, pt=pt: e.transpose(out=pt[:, (i % 8) * 128:(i % 8 + 1) * 128],
                                                         in_=qkbf[:, i * 128:(i + 1) * 128], identity=idt[:]),
                 reads=[qkbf, idt], writes=[pt])
            if i == 7:
                b.op("dve", lambda e, pt=pt, a=a: e.tensor_copy(out=qT_res[:, a, :], in_=pt[:]), reads=[pt], writes=[qT_res])
            if i == 15:
                b.op("dve", lambda e, pt=pt, kt=kt: e.tensor_copy(out=kt[:].rearrange("p a b -> p (a b)"), in_=pt[:]),
                     reads=[pt], writes=[kt])
        for h in range(8):
            G0.put(h, a * 128, (a + 1) * 128, kt[:, h, :], [kt], f"st_k{a % 2}")
    def args(a):
        return [z[a % 2] for z in (junk_, ss_, hbf_, hT_, stage_, sq_, ssq_, tmp_, qkbf_)]
    import os
    if os.environ.get("PIPE1", "0") == "1":
        body(0, *args(0))
        for a in range(NS):
            if a + 1 < NS:
                body(a + 1, *args(a + 1))
            tail(a, *args(a))
    else:
        for a in range(NS):
            body(a, *args(a))
            tail(a, *args(a))
    for h in range(8):
        G0.place(h, "sp")
        G0.reduce(h)


def f_diff_attn(b, io, o_tok, qT, G0):
    LAM_INIT = 0.2
    maskT = b.sb([128, 512], BF16, "maskT")
    b.op("sp", lambda e: e.dma_start(out=maskT[:], in_=io["maskT"][:]), writes=[maskT], dma="ld_mask")
    lam = b.sb([128, 256], F32, "lam")
    b.op("sp", lambda e: e.dma_start(out=lam[:], in_=io["lam"][:]), writes=[lam], dma="ld_lam")
    gsub = b.sb([128, 128], F32, "gsub")
    b.op("sp", lambda e: e.dma_start(out=gsub[:], in_=io["gsub"][:]), writes=[gsub], dma="ld_gsub")
    b.op("dve", lambda e: e.tensor_scalar(out=gsub[:], in0=gsub[:], scalar1=1.0 - LAM_INIT, scalar2=None,
                                          op0=ALU.mult), reads=[gsub], writes=[gsub])
    lprod = b.sb([128, 128], F32, "lprod")
    l2 = b.sb([128, 2], F32, "l2")
    neglam = b.sb([128, 1], F32, "neglam")
    l4 = lam[:].rearrange("p (a b d) -> p a b d", a=2, b=2)
    b.op("dve", lambda e: e.tensor_tensor(out=lprod[:].rearrange("p (a d) -> p a d", a=2), in0=l4[:, :, 0, :],
                                          in1=l4[:, :, 1, :], op=ALU.mult), reads=[lam], writes=[lprod])
    b.op("dve", lambda e: e.tensor_reduce(out=l2[:], in_=lprod[:].rearrange("p (a d) -> p a d", a=2), axis=AX.X,
                                          op=ALU.add), reads=[lprod], writes=[l2])
    b.op("act", lambda e: e.activation(out=l2[:], in_=l2[:], func=AF.Exp), reads=[l2], writes=[l2])
    b.op("dve", lambda e: e.tensor_tensor(out=neglam[:], in0=l2[:, 1:2], in1=l2[:, 0:1], op=ALU.subtract),
         reads=[l2], writes=[neglam])
    b.op("dve", lambda e: e.tensor_scalar(out=neglam[:], in0=neglam[:], scalar1=-LAM_INIT, scalar2=None, op0=ALU.add),
         reads=[neglam], writes=[neglam])

    ktb = [b.sb([128, 8192], BF16, f"ktb{i}") for i in range(2)]
    vb = [b.sb([128, 64, 129], BF16, f"vb{i}") for i in range(2)]
    for i in range(2):
        b.op("dve", lambda e, i=i: e.memset(vb[i][:, :, 128:129], 1.0), writes=[vb[i]])
    pS = [[b.banks[c * 2 + i] for i in range(2)] for c in range(2)]
    pA = [[b.banks[4 + c * 2 + i] for i in range(2)] for c in range(2)]
    pT = [[b.sb([128, 512], BF16, f"pTs{c}{i}") for i in range(2)] for c in range(2)]
    rec = [b.sb([128, 2], F32, f"rec{i}") for i in range(2)]
    o32 = [b.sb([128, 128], F32, f"o32{i}") for i in range(2)]
    oj = b.sb([128, 128], BF16, "ojunk")
    ss1 = [b.sb([128, 1], F32, f"ss1{i}") for i in range(2)]
    units = [(h, a, blk) for h in range(8) for a in range(NS) for blk in range(a + 1)]

    def load_head(h):
        kb, vv = ktb[h % 2], vb[h % 2]
        yb = G0.yb[h]
        for r in range(4):
            b.op("sp", lambda e, kb=kb, r=r, yb=yb: e.dma_start(out=kb[:, r * 2048:(r + 1) * 2048],
                                                               in_=yb[r * 128:(r + 1) * 128, 0:2048]),
                 reads=[G0.yd[h]], writes=[kb], dma=f"ld_kt{h % 2}")
            b.op("sp", lambda e, vv=vv, r=r, yb=yb: e.dma_start(
                out=vv[:, r * 16:(r + 1) * 16, 0:128],
                in_=yb[r * 128:(r + 1) * 128, 2048:4096].rearrange("p (a e) -> p a e", e=128)),
                reads=[G0.yd[h]], writes=[vv], dma=f"ld_v{h % 2}")

    def qk(u, n):
        h, a, blk = u
        kb = ktb[h % 2]
        for i in range(4):
            for c in range(2):
                ps = pS[c][n % 2]
                kt = 16 * i + blk
                b.op("pe", lambda e, c=c, i=i, kt=kt, ps=ps, kb=kb, a=a, h=h: e.matmul(
                    ps[:, i * 128:(i + 1) * 128], lhsT=kb[64 * c:64 * c + 64, kt * 128:(kt + 1) * 128],
                    rhs=qT[64 * c:64 * c + 64, a, h * 128:(h + 1) * 128], start=True, stop=True),
                    reads=[kb, qT], writes=[ps])

    def evac(h, a):
        k = a % 2
        a0, a1 = pA[0][k], pA[1][k]
        r, o, s = rec[k], o32[k], ss1[k]
        b.op("dve", lambda e: e.reciprocal(out=r[:, 0:1], in_=a0[:, 128:129]), reads=[a0], writes=[r])
        b.op("dve", lambda e: e.reciprocal(out=r[:, 1:2], in_=a1[:, 128:129]), reads=[a1], writes=[r])
        b.op("dve", lambda e: e.tensor_tensor(out=r[:, 1:2], in0=r[:, 1:2], in1=neglam[:], op=ALU.mult),
             reads=[r, neglam], writes=[r])
        b.op("dve", lambda e: e.tensor_scalar(out=o[:], in0=a0[:, 0:128], scalar1=r[:, 0:1], scalar2=None, op0=ALU.mult),
             reads=[a0, r], writes=[o])
        b.op("dve", lambda e: e.scalar_tensor_tensor(out=o[:], in0=a1[:, 0:128], scalar=r[:, 1:2], in1=o[:],
                                                     op0=ALU.mult, op1=ALU.add), reads=[a1, r, o], writes=[o])
        b.op("act", lambda e: e.activation(out=oj[:], in_=o[:], func=AF.Square, accum_out=s[:, 0:1]),
             reads=[o], writes=[oj, s])
        rstd_from_ss(b, s, 1, 1.0 / 128)
        ot = o_tok[a]
        b.op("dve", lambda e: e.scalar_tensor_tensor(out=ot[:, h * 128:(h + 1) * 128], in0=o[:], scalar=s[:, 0:1],
                                                     in1=gsub[:], op0=ALU.mult, op1=ALU.mult),
             reads=[o, s, gsub], writes=[ot])

    def softmax_pv(u, n):
        h, a, blk = u
        vv = vb[h % 2]
        for c in range(2):
            ps = pS[c][n % 2]
            pt = pT[c][n % 2]
            acc = pA[c][a % 2]
            b.op("act", lambda e, ps=ps, pt=pt: e.activation(out=pt[:], in_=ps[:], func=AF.Exp),
                 reads=[ps], writes=[pt])
            if blk == a:
                b.op("dve", lambda e, pt=pt: e.tensor_tensor(out=pt[:], in0=pt[:], in1=maskT[:], op=ALU.mult),
                     reads=[pt, maskT], writes=[pt])
            for i in range(4):
                kt = 16 * i + blk
                b.op("pe", lambda e, i=i, kt=kt, pt=pt, acc=acc, vv=vv, blk=blk, a=a: e.matmul(
                    acc[:, 0:129], lhsT=pt[:, i * 128:(i + 1) * 128], rhs=vv[:, kt, :],
                    start=(blk == 0 and i == 0), stop=(blk == a and i == 3)),
                    reads=[pt, vv], writes=[acc])
        if blk == a:
            evac(h, a)

    load_head(0)
    qk(units[0], 0)
    for n, u in enumerate(units):
        if u[1] == 0 and u[2] == 0 and u[0] + 1 < 8:
            load_head(u[0] + 1)
        if n + 1 < len(units):
            qk(units[n + 1], n + 1)
        softmax_pv(u, n)


def dsa_proj_weights(b, io, top=False):
    w = b.sb([128, 8, 2376], BF16, "w_in2", top=top)
    wuq = b.sb([128, 2, 1024], BF16, "wuq", top=top)
    wuqi = b.sb([128, 2, 512], BF16, "wuqi", top=top)

    def load():
        load_w_bf16(b, w, io["w_in2"], 8, 2376, "ld_w2in", colchunk=792)
        load_w_bf16(b, wuq, io["w_uq"], 2, 1024, "ld_wuq")
        load_w_bf16(b, wuqi, io["w_uqi"], 2, 512, "ld_wuqi")
    return {"w": w, "wuq": wuq, "wuqi": wuqi, "load": load}


def f_dsa_proj(b, io, x_res, cos, sin, G1, GK, scr, pre=None):
    idt = b.ident()
    gmix = b.sb([128, D], F32, "gmix1")
    b.op("sp", lambda e: e.dma_start(out=gmix[:], in_=io["gmix1"][:]), writes=[gmix], dma="ld_gmix1")
    gqk = b.sb([128, 2048], F32, "gqk2")
    b.op("sp", lambda e: e.dma_start(out=gqk[:], in_=io["gqk2"][:]), writes=[gqk], dma="ld_gqk2")
    b.op("pool", lambda e: e.tensor_scalar(out=gqk[:, 0:1024], in0=gqk[:, 0:1024], scalar1=0.125, scalar2=None,
                                           op0=ALU.mult), reads=[gqk], writes=[gqk])
    gcq = b.sb([128, 256], F32, "gcq")
    b.op("sp", lambda e: e.dma_start(out=gcq[:], in_=io["gcq"][:]), writes=[gcq], dma="ld_gcq")
    if pre is None:
        pre = dsa_proj_weights(b, io)
        pre["load"]()
    w, wuq, wuqi = pre["w"], pre["wuq"], pre["wuqi"]
    junk_ = [b.sb([128, D], BF16, f"junk3{i}") for i in range(2)]
    ss_ = [b.sb([128, 1], F32, f"ss3{i}") for i in range(2)]
    hbf_ = [b.sb([128, D], BF16, f"hbf3{i}") for i in range(2)]
    hT_ = [b.sb([128, 8, 128], BF16, f"hT3{i}") for i in range(2)]
    stage_ = [b.sb([128, 2048], F32, f"stage3{i}") for i in range(2)]
    sq_ = [b.sb([128, 2048], BF16, f"sq3{i}") for i in range(2)]
    ssq_ = [b.sb([128, 32], F32, f"ssq3{i}") for i in range(2)]
    tmp1 = b.sb([128, 4 * 32 * 8], F32, "ropetmp3"); tmp_ = [tmp1, tmp1]
    qkbf_ = [b.sb([128, 2048], BF16, f"qkbf3{i}") for i in range(2)]
    qkT = [b.sb([128, 16, 128], BF16, f"qkT3{i}") for i in range(2)]
    vst = [b.sb([128, 16, 64], BF16, f"vst3{i}") for i in range(2)]
    cqs_ = [b.sb([128, 256], F32, f"cqs{i}") for i in range(2)]
    ssc_ = [b.sb([128, 1], F32, f"ssc{i}") for i in range(2)]
    cqbf_ = [b.sb([128, 256], BF16, f"cqbf{i}") for i in range(2)]
    cqT_ = [b.sb([128, 2, 128], BF16, f"cqT{i}") for i in range(2)]
    kis_ = [b.sb([128, 64], F32, f"kis{i}") for i in range(2)]
    ksq_ = [b.sb([128, 64], F32, f"ksq{i}") for i in range(2)]
    kss_ = [b.sb([128, 1], F32, f"kss{i}") for i in range(2)]
    kibf_ = [b.sb([128, 128], BF16, f"kibf{i}") for i in range(2)]
    kiT = [b.sb([128, 128], BF16, f"kiT{i}") for i in range(2)]
    wi_ = [b.sb([128, 8], F32, f"wi{i}") for i in range(2)]
    sgn = [b.sb([128, 8], F32, f"sgn{i}") for i in range(2)]
    aw_ = [b.sb([128, 8], F32, f"aw{i}") for i in range(2)]
    qis_ = [b.sb([128, 512], F32, f"qis{i}") for i in range(2)]
    qibf_ = [b.sb([128, 512], BF16, f"qibf{i}") for i in range(2)]
    qiT = [b.sb([128, 4, 128], BF16, f"qiT{i}") for i in range(2)]
    nbank = [0]

    def bank():
        nbank[0] += 1
        return b.banks[2 + nbank[0] % 4]

    def proj(py, lhs, nk, rhs_fn, ncol):
        for kc in range(nk):
            b.op("pe", lambda e, kc=kc: e.matmul(py[:, 0:ncol], lhsT=lhs[:, kc, :], rhs=rhs_fn(kc),
                                                 start=(kc == 0), stop=(kc == nk - 1)), reads=[lhs, w, wuq, wuqi], writes=[py])

    def body2(a, junk, ss, hbf, hT, stage, sq, ssq, tmp, qkbf, cqs, ssc, cqbf, cqT, kis, ksq, kss, kibf, wi, aw, qis, qibf):
        xr = x_res[a]
        rmsnorm_transpose(b, xr, gmix, hbf, hT, b.banks16[0], ss, junk)
        py = bank()
        proj(py, hT, 8, lambda kc: w[:, kc, 0:256], 256)
        b.op("act", lambda e, py=py: e.activation(out=cqs[:], in_=py[:, 0:256], func=AF.Copy), reads=[py], writes=[cqs])
        b.op("act", lambda e, py=py: e.activation(out=junk[:, 0:256], in_=py[:, 0:256], func=AF.Square, accum_out=ssc[:, 0:1]),
             reads=[py], writes=[junk, ssc])
        rstd_from_ss(b, ssc, 1, 1.0 / 256)
        b.op("dve", lambda e: e.scalar_tensor_tensor(out=cqbf[:], in0=cqs[:], scalar=ssc[:, 0:1], in1=gcq[:],
                                                     op0=ALU.mult, op1=ALU.mult), reads=[cqs, ssc, gcq], writes=[cqbf])
        p6 = b.banks16[6]
        for kc in range(2):
            b.op("pe", lambda e, kc=kc: e.transpose(out=p6[:, kc * 128:(kc + 1) * 128], in_=cqbf[:, kc * 128:(kc + 1) * 128],
                                                    identity=idt[:]), reads=[cqbf, idt], writes=[p6])
        b.op("dve", lambda e: e.tensor_copy(out=cqT[:].rearrange("p a b -> p (a b)"), in_=p6[:, 0:256]),
             reads=[p6], writes=[cqT])
        for n in range(2):
            py = bank()
            proj(py, hT, 8, lambda kc, n=n: w[:, kc, 256 + n * 512:256 + (n + 1) * 512], 512)
            b.op("act", lambda e, py=py, n=n: e.activation(out=stage[:, 1024 + n * 512:1024 + (n + 1) * 512], in_=py[:], func=AF.Copy),
                 reads=[py], writes=[stage])
            b.op("act", lambda e, py=py, n=n: e.activation(out=sq[:, 1024 + n * 512:1024 + (n + 1) * 512], in_=py[:], func=AF.Square),
                 reads=[py], writes=[sq])
        va = vst[a % 2]
        for n in range(2):
            py = bank()
            proj(py, hT, 8, lambda kc, n=n: w[:, kc, 1280 + n * 512:1280 + (n + 1) * 512], 512)
            b.op("dve", lambda e, py=py, n=n, va=va: e.tensor_copy(out=va[:, n * 8:(n + 1) * 8, :],
                                                                   in_=py[:].rearrange("p (h d) -> p h d", d=64)),
                 reads=[py], writes=[va])
        for hp in range(8):
            G1.put(hp, 2048 + a * 128, 2048 + (a + 1) * 128, va[:, 2 * hp:2 * hp + 2, :].rearrange("p h d -> p (h d)"),
                   [va], f"st_v2{a % 2}")
        py = bank()
        proj(py, hT, 8, lambda kc: w[:, kc, 2304:2376], 72)
        b.op("act", lambda e, py=py: e.activation(out=kis[:], in_=py[:, 0:64], func=AF.Copy), reads=[py], writes=[kis])
        b.op("act", lambda e, py=py: e.activation(out=ksq[:], in_=py[:, 0:64], func=AF.Square), reads=[py], writes=[ksq])
        b.op("dve", lambda e, py=py: e.tensor_copy(out=wi[:], in_=py[:, 64:72]), reads=[py], writes=[wi])
        for n in range(2):
            py = bank()
            proj(py, cqT, 2, lambda kc, n=n: wuq[:, kc, n * 512:(n + 1) * 512], 512)
            b.op("act", lambda e, py=py, n=n: e.activation(out=stage[:, n * 512:(n + 1) * 512], in_=py[:], func=AF.Copy),
                 reads=[py], writes=[stage])
            b.op("act", lambda e, py=py, n=n: e.activation(out=sq[:, n * 512:(n + 1) * 512], in_=py[:], func=AF.Square),
                 reads=[py], writes=[sq])
        py = bank()
        proj(py, cqT, 2, lambda kc: wuqi[:, kc, :], 512)
        b.op("act", lambda e, py=py: e.activation(out=qis[:], in_=py[:], func=AF.Copy), reads=[py], writes=[qis])

    def tail2(a, junk, ss, hbf, hT, stage, sq, ssq, tmp, qkbf, cqs, ssc, cqbf, cqT, kis, ksq, kss, kibf, wi, aw, qis, qibf):
        headnorm_rope(b, stage, sq, ssq, 32, gqk, cos[:, a, :], sin[:, a, :], qkbf, tmp)
        qt = qkT[a % 2]
        for i in range(16):
            pt = b.banks16[1] if i < 8 else b.banks16[7]
            b.op("pe", lambda e, i=i, pt=pt: e.transpose(out=pt[:, (i % 8) * 128:(i % 8 + 1) * 128],
                                                         in_=qkbf[:, i * 128:(i + 1) * 128], identity=idt[:]),
                 reads=[qkbf, idt], writes=[pt])
            if i % 8 == 7:
                b.op("dve", lambda e, i=i, pt=pt, qt=qt: e.tensor_copy(
                    out=qt[:, (i // 8) * 8:(i // 8) * 8 + 8, :].rearrange("p a b -> p (a b)"), in_=pt[:]),
                    reads=[pt], writes=[qt])
        b.op("sp", lambda e, a=a, qt=qt: e.dma_start(out=scr["QT2"][a][:], in_=qt[:, 0:8, :].rearrange("p a b -> p (a b)")),
             reads=[qt], writes=[scr["QT2"][a]], dma=f"st_q2{a % 2}")
        for hp in range(8):
            G1.put(hp, a * 128, (a + 1) * 128, qt[:, 8 + hp, :], [qt], f"st_k2{a % 2}")
        ki_half = T(kibf.t[:, 0:64], kibf.d)
        headnorm_rope(b, kis, ksq, kss, 1, None, cos[:, a, :], sin[:, a, :], ki_half, tmp)
        b.op("pool", lambda e: e.tensor_copy(out=kibf[:, 64:128], in_=kibf[:, 0:64]), reads=[kibf], writes=[kibf])
        p7 = b.banks16[7]
        b.op("pe", lambda e: e.transpose(out=p7[:, 0:128], in_=kibf[:], identity=idt[:]), reads=[kibf, idt], writes=[p7])
        kt_ = kiT[a % 2]
        b.op("dve", lambda e, kt_=kt_: e.tensor_copy(out=kt_[:], in_=p7[:, 0:128]), reads=[p7], writes=[kt_])
        GK.put(0, a * 128, (a + 1) * 128, kt_[:], [kt_], f"st_ki{a % 2}")
        sg = sgn[a % 2]
        b.op("act", lambda e, sg=sg: e.activation(out=sg[:], in_=wi[:], func=AF.Sign), reads=[wi], writes=[sg])
        b.op("dve", lambda e, sg=sg: e.scalar_tensor_tensor(out=aw[:], in0=wi[:], scalar=IDX_SCALE, in1=sg[:],
                                                           op0=ALU.mult, op1=ALU.mult), reads=[wi, sg], writes=[aw])
        b.op("sp", lambda e, a=a, sg=sg: e.dma_start(out=scr["SG"][a][:], in_=sg[:]), reads=[sg], writes=[scr["SG"][a]],
             dma=f"st_sg{a % 2}")
        headnorm_rope(b, qis, None, aw, 8, None, cos[:, a, :], sin[:, a, :], qibf, tmp, norm=False)
        qi_ = qiT[a % 2]
        for i in range(4):
            b.op("pe", lambda e, i=i: e.transpose(out=p7[:, 256 + i * 128:256 + (i + 1) * 128],
                                                  in_=qibf[:, i * 128:(i + 1) * 128], identity=idt[:]),
                 reads=[qibf, idt], writes=[p7])
        b.op("dve", lambda e, qi_=qi_: e.tensor_copy(out=qi_[:].rearrange("p a b -> p (a b)"), in_=p7[:, 256:768]),
             reads=[p7], writes=[qi_])
        b.op("sp", lambda e, a=a, qi_=qi_: e.dma_start(out=scr["QI"][a][:], in_=qi_[:].rearrange("p a b -> p (a b)")),
             reads=[qi_], writes=[scr["QI"][a]], dma=f"st_qi{a % 2}")
    def args2(a):
        return [z[a % 2] for z in (junk_, ss_, hbf_, hT_, stage_, sq_, ssq_, tmp_, qkbf_, cqs_, ssc_, cqbf_, cqT_, kis_, ksq_, kss_, kibf_, wi_, aw_, qis_, qibf_)]
    import os
    if os.environ.get("PIPE2", "0") == "1":
        body2(0, *args2(0))
        for a in range(NS):
            if a + 1 < NS:
                body2(a + 1, *args2(a + 1))
            tail2(a, *args2(a))
    else:
        for a in range(NS):
            body2(a, *args2(a))
            tail2(a, *args2(a))
    GK.place(0, "act")
    GK.reduce(0)
    for hp in range(8):
        G1.place(hp, "act")
        G1.reduce(hp)


NIT2 = 14


def f_dsa_attn(b, io, o_tok, G1, GK, scr):
    idt = b.ident()
    kia = b.sb([128, 8192], BF16, "kiall")
    for r in range(4):
        b.op("sp", lambda e, r=r: e.dma_start(out=kia[:, r * 2048:(r + 1) * 2048], in_=GK.yb[0][r * 128:(r + 1) * 128, :]),
             reads=[GK.yd[0]], writes=[kia], dma="ld_kia")
    kia4 = kia[:].rearrange("p (r a t) -> p r a t", r=4, a=16)
    negm = b.sb([128, 512], F32, "negm")
    b.op("sp", lambda e: e.dma_start(out=negm[:], in_=io["negmask"][:]), writes=[negm], dma="ld_negm")
    cW = b.sb([128, NIT2], F32, "cW")
    for i in range(NIT2):
        b.op("dve", lambda e, i=i: e.memset(cW[:, i:i + 1], 2.0 ** (-i)), writes=[cW])
    THR = b.sb([128, NS], F32, "THR")
    mA = b.mark()
    NSET = 3
    qiTa = [b.sb([128, 4, 128], BF16, f"qiTa{i}") for i in range(NSET)]
    sgn = [b.sb([128, 8], F32, f"sgna{i}") for i in range(NSET)]
    Dg = [b.sb([128, 8, 128], BF16, f"Dg{i}") for i in range(NSET)]
    tb = [[b.sb([128, 512], BF16, f"tb{s}{i}") for i in range(3)] for s in range(NSET)]
    pIs = [[b.banks[4], b.banks[6]], [b.banks[5], b.banks[0]], [b.banks[2], b.banks[2]]]
    pIas = [b.banks[7], b.banks[1], b.banks[3]]
    ctr = {"i": 0, "acc": 0, "kv": 0}

    def load_idx_small(a, s):
        b.op("sp", lambda e, a=a: e.dma_start(out=qiTa[s][:].rearrange("p a b -> p (a b)"), in_=scr["QI"][a][:]),
             reads=[scr["QI"][a]], writes=[qiTa[s]], dma=f"ld_qiTa{s}")
        b.op("sp", lambda e, a=a: e.dma_start(out=sgn[s][:], in_=scr["SG"][a][:]), reads=[scr["SG"][a]],
             writes=[sgn[s]], dma=f"ld_sgn{s}")

    def indexer_list(a, s, Ib_):
        L = []

        def add(eng, fn, reads=(), writes=()):
            L.append((eng, fn, reads, writes))
        nb = a + 1
        qi, sg, dg = qiTa[s], sgn[s], Dg[s]
        pys, pia = pIs[s], pIas[s]
        skew = pys[0] is not pys[1]
        add("dve", lambda e: e.tensor_tensor(out=dg[:], in0=idt[:].unsqueeze(1).to_broadcast([128, 8, 128]),
                                             in1=sg[:].unsqueeze(2).to_broadcast([128, 8, 128]), op=ALU.mult),
            [idt, sg], [dg])
        steps = [(blk, head) for blk in range(nb) for head in range(8)]
        S_ = len(steps)
        evq = []
        for k in range(S_ + 2):
            if k < S_:
                blk, head = steps[k]
                hp, hh = head // 2, head % 2
                py = pys[k % 2]
                add("pe", lambda e, hp=hp, hh=hh, blk=blk, py=py: e.matmul(
                    py[:].rearrange("p (r t) -> p r t", r=4), lhsT=qi[64 * hh:64 * hh + 64, hp, :],
                    rhs=kia4[64 * hh:64 * hh + 64, :, blk, :], start=True, stop=True), [qi, kia], [py])
            kr = k - 1 if skew else k
            if 0 <= kr < S_:
                py = pys[kr % 2]
                t = tb[s][kr % 3]
                add("act", lambda e, t=t, py=py: e.activation(out=t[:], in_=py[:], func=AF.Relu), [py], [t])
            if 0 <= k - 2 < S_:
                blk, head = steps[k - 2]
                t = tb[s][(k - 2) % 3]
                add("pe", lambda e, head=head, t=t: e.matmul(pia[:], lhsT=dg[:, head, :], rhs=t[:],
                                                             start=(head == 0), stop=(head == 7)), [dg, t], [pia])
                if head == 7:
                    add("act", lambda e, eb=blk: e.activation(out=Ib_[:, eb * 512:(eb + 1) * 512], in_=pia[:], func=AF.Copy),
                        [pia], [Ib_])
        for _, eb in evq:
            add("act", lambda e, eb=eb: e.activation(out=Ib_[:, eb * 512:(eb + 1) * 512], in_=pia[:], func=AF.Copy),
                [pia], [Ib_])
        return L

    IbA = [b.sb([128, 8192], F32, f"IbA{i}") for i in range(NSET)]
    MqA = [b.sb([128, 8192], BF16, f"MqA{i}") for i in range(NSET)]
    st = [{k: b.sb([128, n], F32, f"bs{k}{s}") for k, n in (("m1", 1), ("lo", 1), ("mid", 1), ("nmid", 1), ("cD", 1),
                                                           ("sA", 1), ("g", 1), ("W", NIT2))} for s in range(NSET)]

    def stageA_list(a, s):
        L = []

        def add(eng, fn, reads=(), writes=()):
            L.append((eng, fn, reads, writes))
        nb = a + 1
        nv = 512 * nb
        Ib_, Mq_ = IbA[s], MqA[s]
        S_ = st[s]
        m1, lo, mid, cD, sA, g, W = (S_[k] for k in ("m1", "lo", "mid", "cD", "sA", "g", "W"))
        add("dve", lambda e: e.tensor_reduce(out=m1[:], in_=Ib_[:, 0:nv], axis=AX.X, op=ALU.max, apply_absolute_value=True),
            [Ib_], [m1])
        add("dve", lambda e: e.tensor_tensor(out=Ib_[:, nv - 512:nv], in0=Ib_[:, nv - 512:nv], in1=negm[:], op=ALU.add),
            [Ib_, negm], [Ib_])
        add("dve", lambda e: e.tensor_scalar(out=W[:], in0=cW[:], scalar1=m1[:, 0:1], scalar2=None, op0=ALU.mult),
            [cW, m1], [W])
        add("dve", lambda e: e.tensor_tensor(out=W[:], in0=W[:], in1=cW[:], op=ALU.add), [W, cW], [W])
        add("dve", lambda e: e.memset(mid[:], 0.0), [], [mid])
        h = 512 * (nb // 2)
        thr = TOPK - 0.5 - 0.5 * h
        for i in range(NIT2):
            if h > 0:
                add("act", lambda e: e.activation(out=Mq_[:, 0:h], in_=Ib_[:, 0:h], func=AF.Sign, bias=mid[:, 0:1],
                                                  scale=-1.0, accum_out=sA[:, 0:1]), [Ib_, mid], [Mq_, sA])
            add("dve", lambda e: e.tensor_scalar(out=Mq_[:, h:nv], in0=Ib_[:, h:nv], scalar1=mid[:, 0:1], scalar2=0.0,
                                                 op0=ALU.is_ge, op1=ALU.add, accum_out=cD[:, 0:1]), [Ib_, mid], [Mq_, cD])
            if h > 0:
                add("dve", lambda e: e.scalar_tensor_tensor(out=cD[:], in0=sA[:], scalar=-0.5, in1=cD[:], op0=ALU.mult,
                                                            op1=ALU.add), [sA, cD], [cD])
            add("dve", lambda e, i=i: e.tensor_scalar(out=g[:], in0=cD[:], scalar1=thr, scalar2=W[:, i:i + 1],
                                                      op0=ALU.is_ge, op1=ALU.mult), [cD, W], [g])
            if i + 1 < NIT2:
                add("dve", lambda e, i=i: e.scalar_tensor_tensor(out=mid[:], in0=mid[:], scalar=W[:, i + 1:i + 2], in1=g[:],
                                                                 op0=ALU.subtract, op1=ALU.add), [mid, W, g], [mid])
            else:
                add("dve", lambda e, i=i: e.scalar_tensor_tensor(out=lo[:], in0=mid[:], scalar=W[:, i:i + 1], in1=g[:],
                                                                 op0=ALU.subtract, op1=ALU.add), [mid, W, g], [lo])
        for c0 in range(0, nv, 2048):
            c1 = min(nv, c0 + 2048)
            add("dve", lambda e, c0=c0, c1=c1: e.tensor_scalar(out=Mq_[:, c0:c1], in0=Ib_[:, c0:c1], scalar1=lo[:, 0:1],
                                                               scalar2=None, op0=ALU.is_ge), [Ib_, lo], [Mq_])
        add("sp", ("dma", lambda e: e.dma_start(out=scr["MQ"][a][:, 0:nv], in_=Mq_[:, 0:nv]), f"st_mq{s}"),
            [Mq_], [scr["MQ"][a]])
        return L

    def emit_item(it):
        eng, fn, reads, writes = it
        if isinstance(fn, tuple):
            b.op(eng, fn[1], reads, writes, dma=fn[2])
        else:
            b.op(eng, fn, reads, writes)

    def interleave(lists):
        lists = [l for l in lists if l]
        pos = [0] * len(lists)
        total = sum(len(l) for l in lists)
        for _ in range(total):
            best, bf = None, 2.0
            for li, l in enumerate(lists):
                if pos[li] < len(l):
                    f = pos[li] / len(l)
                    if f < bf:
                        best, bf = li, f
            emit_item(lists[best][pos[best]])
            pos[best] += 1

    for a in range(NS + 1):
        li = []
        if a < NS:
            load_idx_small(a, a % NSET)
            li = indexer_list(a, a % NSET, IbA[a % NSET])
        lb = stageA_list(a - 1, (a - 1) % NSET) if a >= 1 else []
        interleave([li, lb])
    b.release(mA)
    if o_tok is None:
        o_tok = [b.sb([128, 1024], BF16, f"otokb{a}", top=True) for a in range(NS)]

    Mqs = [b.sb([128, 8192], BF16, f"MqB{i}") for i in range(2)]
    MTs = [b.sb([128, 64, 128], BF16, f"MT{i}") for i in range(2)]
    ktp = [b.sb([128, 8192], BF16, f"ktp{i}") for i in range(2)]
    vp = [b.sb([128, 64, 130], BF16, f"vp{i}") for i in range(2)]
    for i in range(2):
        b.op("dve", lambda e, i=i: e.memset(vp[i][:, :, 0:1], 1.0), writes=[vp[i]])
        b.op("dve", lambda e, i=i: e.memset(vp[i][:, :, 129:130], 1.0), writes=[vp[i]])
    qTa = [b.sb([128, 8, 128], BF16, f"qTa{i}") for i in range(2)]
    pT = [b.sb([128, 512], BF16, f"pTd{i}") for i in range(3)]
    rec = [b.sb([128, 1], F32, f"recd{i}") for i in range(2)]
    pS = [b.banks[0], b.banks[1], b.banks[5]]
    pA = [b.banks[2], b.banks[3]]
    pMs = [b.banks16[6], b.banks16[7]]

    def load_slot_small(a):
        nv = 512 * (a + 1)
        b.op("sp", lambda e, a=a: e.dma_start(out=qTa[a % 2][:].rearrange("p a b -> p (a b)"), in_=scr["QT2"][a][:]),
             reads=[scr["QT2"][a]], writes=[qTa[a % 2]], dma=f"ld_qTa{a % 2}")
        b.op("sp", lambda e, a=a, nv=nv: e.dma_start(out=Mqs[a % 2][:, 0:nv], in_=scr["MQ"][a][:, 0:nv]),
             reads=[scr["MQ"][a]], writes=[Mqs[a % 2]], dma=f"ld_mq{a % 2}")

    def load_kv(a, hp):
        k = ctr["kv"] % 2
        ctr["kv"] += 1
        n = (a + 1) * 128
        yb = G1.yb[hp]
        b.op("sp", lambda e: e.dma_start(out=ktp[k][:].rearrange("p (r x) -> p r x", r=4)[:, :, 0:n],
                                         in_=yb[:, 0:n].rearrange("(r p) x -> p r x", p=128)),
             reads=[G1.yd[hp]], writes=[ktp[k]], dma=f"ld_ktp{k}")
        for r in range(4):
            b.op("sp", lambda e, r=r: e.dma_start(
                out=vp[k][:, r * 16:r * 16 + a + 1, 1:129],
                in_=yb[r * 128:(r + 1) * 128, 2048:2048 + n].rearrange("p (a e) -> p a e", e=128)),
                reads=[G1.yd[hp]], writes=[vp[k]], dma=f"ld_vp{k}")
        return k

    def pre_list(a):
        L = []
        Mq, MT = Mqs[a % 2], MTs[a % 2]
        nt = 4 * (a + 1)
        ngr = (nt + 7) // 8
        for gi in range(ngr + 1):
            if gi < ngr:
                pm = pMs[gi % 2]
                for kt in range(gi * 8, min(nt, gi * 8 + 8)):
                    L.append(("pe", lambda e, kt=kt, pm=pm: e.transpose(out=pm[:, (kt % 8) * 128:(kt % 8 + 1) * 128],
                                                                        in_=Mq[:, kt * 128:(kt + 1) * 128], identity=idt[:]),
                              [Mq, idt], [pm]))
            if gi >= 1:
                g0 = gi - 1
                pm = pMs[g0 % 2]
                k0 = g0 * 8
                n = min(nt, k0 + 8) - k0
                L.append(("act", lambda e, pm=pm, k0=k0, n=n: e.activation(
                    out=MT[:, k0:k0 + n, :].rearrange("p a b -> p (a b)"), in_=pm[:, 0:n * 128], func=AF.Copy),
                    [pm], [MT]))
        return L

    def qk(u, n, kbuf):
        a, hp, hh, blk = u
        ps = pS[n % 3]
        qa = qTa[a % 2]
        for i in range(4):
            kt = 16 * i + blk
            b.op("pe", lambda e, i=i, kt=kt, ps=ps, qa=qa, hh=hh, hp=hp, kbuf=kbuf: e.matmul(
                ps[:, i * 128:(i + 1) * 128], lhsT=ktp[kbuf][64 * hh:64 * hh + 64, kt * 128:(kt + 1) * 128],
                rhs=qa[64 * hh:64 * hh + 64, hp, :], start=True, stop=True), reads=[ktp[kbuf], qa], writes=[ps])

    def softmax_pv(u, n, kbuf):
        a, hp, hh, blk = u
        ps, pt = pS[n % 3], pT[n % 3]
        if blk == 0:
            ctr["acc"] += 1
        acc = pA[ctr["acc"] % 2]
        b.op("act", lambda e: e.activation(out=pt[:], in_=ps[:], func=AF.Exp), reads=[ps], writes=[pt])
        MT = MTs[a % 2]
        b.op("dve", lambda e: e.tensor_tensor(out=pt[:], in0=pt[:], in1=MT[:, 4 * blk:4 * blk + 4, :].rearrange("p a b -> p (a b)"),
                                              op=ALU.mult), reads=[pt, MT], writes=[pt])
        for i in range(4):
            kt = 16 * i + blk
            b.op("pe", lambda e, i=i, kt=kt: e.matmul(acc[:, 0:65], lhsT=pt[:, i * 128:(i + 1) * 128],
                                                      rhs=vp[kbuf][:, kt, hh * 65:(hh + 1) * 65],
                                                      start=(blk == 0 and i == 0), stop=(blk == a and i == 3)),
                 reads=[pt, vp[kbuf]], writes=[acc])
        if blk == a:
            head = 2 * hp + hh
            r = rec[ctr["acc"] % 2]
            sc, v0 = (0, 1) if hh == 0 else (64, 0)
            b.op("dve", lambda e: e.reciprocal(out=r[:], in_=acc[:, sc:sc + 1]), reads=[acc], writes=[r])
            b.op("dve", lambda e: e.tensor_scalar(out=o_tok[a][:, head * 64:(head + 1) * 64], in0=acc[:, v0:v0 + 64],
                                                  scalar1=r[:, 0:1], scalar2=None, op0=ALU.mult),
                 reads=[acc, r], writes=[o_tok[a]])

    load_slot_small(0)
    for it in pre_list(0):
        b.op(*it)
    for a in range(NS):
        if a + 1 < NS:
            load_slot_small(a + 1)
        kb_next = load_kv(a, 0)
        nxt = pre_list(a + 1) if a + 1 < NS else []
        done = 0
        units = [(a, hp, hh, blk) for hp in range(8) for hh in range(2) for blk in range(a + 1)]
        kbufs = {0: kb_next}
        kbufs[1] = load_kv(a, 1)
        qk(units[0], 0, kbufs[0])
        if len(units) > 1:
            qk(units[1], 1, kbufs[units[1][1]])
        for n, u in enumerate(units):
            _, hp, hh, blk = u
            if hh == 0 and blk == 0 and 1 <= hp and hp + 1 < 8:
                kbufs[hp + 1] = load_kv(a, hp + 1)
            if n + 2 < len(units):
                qk(units[n + 2], n + 2, kbufs[units[n + 2][1]])
            softmax_pv(u, n, kbufs[hp])
            want = (n + 1) * len(nxt) // len(units)
            while done < want:
                b.op(*nxt[done])
                done += 1
        while done < len(nxt):
            b.op(*nxt[done])
            done += 1
    return o_tok


BF = ml_dtypes.bfloat16


def rep(v, n=128):
    return np.ascontiguousarray(np.tile(np.asarray(v).reshape(1, -1), (n, 1)))


def own_tiles(arr_bs, c):
    bb, j = c // 4, c % 4
    a = arr_bs[bb]
    return np.ascontiguousarray(a.reshape(64, 128, *a.shape[1:])[j::4].reshape(2048, *a.shape[1:]))


def gather_tiles(per_core, bb):
    out = np.empty((64,) + per_core[0].shape[1:], per_core[0].dtype)
    for j in range(4):
        out[j::4] = per_core[bb * 4 + j]
    return out


def diff_masks(c):
    j = c % 4
    m = np.zeros((128, 4, 128), np.float32)
    for i in range(4):
        if i < j:
            m[:, i, :] = 1.0
        elif i == j:
            m[0:64, i, :] = 1.0
            m[64:128, i, 64:128] = 1.0
    return m.reshape(128, 512).astype(BF)


def dsa_negmask(c):
    return np.where(diff_masks(c).astype(np.float32).reshape(128, 4, 128).transpose(2, 1, 0).reshape(128, 512) > 0,
                    0.0, -1e30).astype(np.float32)


def build_L1():
    nc = bass.Bass("TRN2", target_bir_lowering=False)
    b = B(nc)
    io = {
        "x": b.dram("x", [2048, 1024], F32, "ExternalInput"),
        "pos": b.dram("pos", [128, 16], I32, "ExternalInput"),
        "gmix": b.dram("gmix", [128, 1024], F32, "ExternalInput"),
        "gqk": b.dram("gqk", [128, 2048], F32, "ExternalInput"),
        "w_in": b.dram("w_in", [1024, 3072], F32, "ExternalInput"),
        "QT": b.dram("QT", [16, 128, 1024], BF16, "ExternalOutput"),
        "KT": b.dram("KT", [16, 128, 1024], BF16, "ExternalOutput"),
        "V": b.dram("V", [16, 128, 8 * 129], BF16, "ExternalOutput"),
    }
    phase_diff_proj(b, io)
    b.finish()
    return nc


def build_L2():
    nc = bass.Bass("TRN2", target_bir_lowering=False)
    b = B(nc)
    io = {
        "QT": b.dram("QT", [16, 128, 1024], BF16, "ExternalInput"),
        "KTall": b.dram("KTall", [8, 128, 8192], BF16, "ExternalInput"),
        "Vall": b.dram("Vall", [8, 128, 64 * 129], BF16, "ExternalInput"),
        "maskT": b.dram("maskT", [128, 512], BF16, "ExternalInput"),
        "lam": b.dram("lam", [128, 256], F32, "ExternalInput"),
        "gsub": b.dram("gsub", [128, 128], F32, "ExternalInput"),
        "x": b.dram("x", [2048, 1024], F32, "ExternalInput"),
        "pos": b.dram("pos", [128, 16], I32, "ExternalInput"),
        "w_out": b.dram("w_out", [1024, 1024], F32, "ExternalInput"),
        "gmlp": b.dram("gmlp", [128, 1024], F32, "ExternalInput"),
        "w1": b.dram("w1", [1024, 4096], F32, "ExternalInput"),
        "w2": b.dram("w2", [4096, 1024], F32, "ExternalInput"),
        "gmix1": b.dram("gmix1", [128, 1024], F32, "ExternalInput"),
        "gqk2": b.dram("gqk2", [128, 2048], F32, "ExternalInput"),
        "gcq": b.dram("gcq", [128, 256], F32, "ExternalInput"),
        "w_in2": b.dram("w_in2", [1024, 2376], F32, "ExternalInput"),
        "w_uq": b.dram("w_uq", [256, 1024], F32, "ExternalInput"),
        "w_uqi": b.dram("w_uqi", [256, 512], F32, "ExternalInput"),
        "X2": b.dram("X2", [2048, 1024], F32, "ExternalOutput"),
        "QT2": b.dram("QT2", [16, 128, 1024], BF16, "ExternalOutput"),
        "KT2": b.dram("KT2", [16, 128, 1024], BF16, "ExternalOutput"),
        "V2": b.dram("V2", [16, 128, 16 * 65], BF16, "ExternalOutput"),
        "KI": b.dram("KI", [16, 128, 128], BF16, "ExternalOutput"),
        "QI": b.dram("QI", [16, 128, 512], BF16, "ExternalOutput"),
        "SG": b.dram("SG", [16, 128, 8], F32, "ExternalOutput"),
    }
    b.ident(); b.eps()
    pos_t = b.sb([128, NS], I32, "pos")
    b.op("sp", lambda e: e.dma_start(out=pos_t[:], in_=io["pos"][:]), writes=[pos_t], dma="ld_pos")
    cos, sin = rope_tables(b, pos_t)
    o_tok = [b.sb([128, 1024], BF16, f"otok{a}", top=True) for a in range(NS)]
    m1 = b.mark()
    phase_diff_attn(b, io, o_tok)
    b.release(m1)
    x_res = [b.sb([128, 1024], F32, f"xres{a}") for a in range(NS)]
    h2T = b.sb([128, 8, 2048], BF16, "h2T")
    m2 = b.mark()
    phase_post_attn(b, io, o_tok, x_res, h2T, io["w_out"][:], io["gmlp"][:], io["x"])
    b.release(m2)
    b.hi = ARENA_END
    m3 = b.mark()
    phase_mlp(b, x_res, h2T, io["w1"], io["w2"])
    b.release(m3)
    for a in range(NS):
        b.store("sp", lambda e, a=a: e.dma_start(out=io["X2"][a * 128:(a + 1) * 128, :], in_=x_res[a][:]),
                reads=[x_res[a]], dma="st_x2")
    phase_dsa_proj(b, io, x_res, cos, sin)
    b.finish()
    return nc


def l1_inputs(inp, c):
    return {"x": own_tiles(inp["x"], c),
            "pos": np.ascontiguousarray(own_tiles(inp["positions"], c).reshape(16, 128).T),
            "gmix": rep(inp["norm_mix"][0]),
            "gqk": rep(np.concatenate([np.tile(inp["diff_q_norm"][0], 16), np.tile(inp["diff_k_norm"][0], 16)])),
            "w_in": np.ascontiguousarray(inp["diff_w_in"][0])}


def l2_inputs(inp, r1):
    KTall = []; Vall = []
    for bb in range(2):
        kt = gather_tiles([r["KT"].reshape(16, 128, 8, 128) for r in r1], bb)
        KTall.append(np.ascontiguousarray(kt.transpose(2, 1, 0, 3).reshape(8, 128, 8192)))
        v = gather_tiles([r["V"].reshape(16, 128, 8, 129) for r in r1], bb)
        Vall.append(np.ascontiguousarray(v.transpose(2, 1, 0, 3).reshape(8, 128, 64 * 129)))
    lam = rep(np.concatenate([inp["diff_lam_q1"][0], inp["diff_lam_k1"][0], inp["diff_lam_q2"][0], inp["diff_lam_k2"][0]]))
    gqk2 = rep(np.concatenate([np.tile(inp["dsa_q_norm"][0], 16), np.tile(inp["dsa_k_norm"][0], 16)]))
    ins = []
    for c in range(8):
        ins.append({"QT": r1[c]["QT"], "KTall": KTall[c // 4], "Vall": Vall[c // 4], "maskT": diff_masks(c),
                    "lam": lam, "gsub": rep(inp["diff_subln"][0]),
                    "x": own_tiles(inp["x"], c),
                    "pos": np.ascontiguousarray(own_tiles(inp["positions"], c).reshape(16, 128).T),
                    "w_out": np.ascontiguousarray(inp["diff_w_out"][0]), "gmlp": rep(inp["norm_mlp"][0]),
                    "w1": np.ascontiguousarray(inp["mlp_w1"][0]), "w2": np.ascontiguousarray(inp["mlp_w2"][0]),
                    "gmix1": rep(inp["norm_mix"][1]), "gqk2": gqk2, "gcq": rep(inp["dsa_cq_norm"][0]),
                    "w_in2": np.ascontiguousarray(inp["dsa_w_in"][0]), "w_uq": np.ascontiguousarray(inp["dsa_w_uq"][0]),
                    "w_uqi": np.ascontiguousarray(inp["dsa_w_uq_idx"][0])})
    return ins


def run(nc, ins):
    res = run_bass_kernel_spmd(nc, ins, core_ids=list(range(8)))
    return [{k: np.asarray(v) for k, v in r.items()} for r in res.results]


def build_L3():
    nc = bass.Bass("TRN2", target_bir_lowering=False)
    b = B(nc)
    io = {
        "QT2": b.dram("QT2", [16, 128, 1024], BF16, "ExternalInput"),
        "QI": b.dram("QI", [16, 128, 512], BF16, "ExternalInput"),
        "SG": b.dram("SG", [16, 128, 8], F32, "ExternalInput"),
        "KIall": b.dram("KIall", [128, 8192], BF16, "ExternalInput"),
        "KT2all": b.dram("KT2all", [8, 128, 8192], BF16, "ExternalInput"),
        "V2all": b.dram("V2all", [8, 128, 64 * 130], BF16, "ExternalInput"),
        "negmask": b.dram("negmask", [128, 512], F32, "ExternalInput"),
        "X2": b.dram("X2", [2048, 1024], F32, "ExternalInput"),
        "w_out": b.dram("w_out", [1024, 1024], F32, "ExternalInput"),
        "gmlp": b.dram("gmlp", [128, 1024], F32, "ExternalInput"),
        "w1": b.dram("w1", [1024, 4096], F32, "ExternalInput"),
        "w2": b.dram("w2", [4096, 1024], F32, "ExternalInput"),
        "OUT": b.dram("OUT", [2048, 1024], F32, "ExternalOutput"),
    }
    b.ident(); b.eps()
    o_tok = [b.sb([128, 1024], BF16, f"otok{a}", top=True) for a in range(NS)]
    m1 = b.mark()
    phase_dsa_attn(b, io, o_tok)
    b.release(m1)
    x_res = [b.sb([128, 1024], F32, f"xres{a}") for a in range(NS)]
    h2T = b.sb([128, 8, 2048], BF16, "h2T")
    m2 = b.mark()
    phase_post_attn(b, io, o_tok, x_res, h2T, io["w_out"][:], io["gmlp"][:], io["X2"])
    b.release(m2)
    b.hi = ARENA_END
    m3 = b.mark()
    phase_mlp(b, x_res, h2T, io["w1"], io["w2"])
    b.release(m3)
    for a in range(NS):
        b.store("sp", lambda e, a=a: e.dma_start(out=io["OUT"][a * 128:(a + 1) * 128, :], in_=x_res[a][:]),
                reads=[x_res[a]], dma="st_out")
    b.finish()
    return nc


def l3_inputs(inp, r2):
    KIall = []; KTall = []; Vall = []
    for bb in range(2):
        ki = gather_tiles([r["KI"] for r in r2], bb)
        KIall.append(np.ascontiguousarray(ki.transpose(1, 0, 2).reshape(128, 8192)))
        kt = gather_tiles([r["KT2"].reshape(16, 128, 8, 128) for r in r2], bb)
        KTall.append(np.ascontiguousarray(kt.transpose(2, 1, 0, 3).reshape(8, 128, 8192)))
        v = gather_tiles([r["V2"].reshape(16, 128, 8, 130) for r in r2], bb)
        Vall.append(np.ascontiguousarray(v.transpose(2, 1, 0, 3).reshape(8, 128, 64 * 130)))
    ins = []
    for c in range(8):
        ins.append({"QT2": r2[c]["QT2"], "QI": r2[c]["QI"], "SG": r2[c]["SG"], "KIall": KIall[c // 4],
                    "KT2all": KTall[c // 4], "V2all": Vall[c // 4], "negmask": dsa_negmask(c),
                    "X2": r2[c]["X2"], "w_out": np.ascontiguousarray(inp["dsa_w_out"][0]), "gmlp": rep(inp["norm_mlp"][1]),
                    "w1": np.ascontiguousarray(inp["mlp_w1"][1]), "w2": np.ascontiguousarray(inp["mlp_w2"][1])})
    return ins


def assemble(r3):
    out = np.empty((2, 8192, 1024), np.float32)
    for bb in range(2):
        o = gather_tiles([r["OUT"].reshape(16, 128, 1024) for r in r3], bb)
        out[bb] = o.reshape(8192, 1024)
    return out


FUSED_IN = [
    ("x", [2048, 1024], F32), ("pos", [128, 16], I32), ("gmix", [128, 1024], F32), ("gqk", [128, 2048], F32),
    ("w_in", [1024, 3072], F32), ("maskT", [128, 512], BF16), ("lam", [128, 256], F32), ("gsub", [128, 128], F32),
    ("w_out", [1024, 1024], F32), ("gmlp", [128, 1024], F32), ("w1", [1024, 4096], F32), ("w2", [4096, 1024], F32),
    ("gmix1", [128, 1024], F32), ("gqk2", [128, 2048], F32), ("gcq", [128, 256], F32), ("w_in2", [1024, 2376], F32),
    ("w_uq", [256, 1024], F32), ("w_uqi", [256, 512], F32), ("negmask", [128, 512], F32),
    ("w_outb", [1024, 1024], F32), ("gmlpb", [128, 1024], F32), ("w1b", [1024, 4096], F32), ("w2b", [4096, 1024], F32),
]


def build_fused():
    nc = bass.Bass("TRN2", target_bir_lowering=False)
    _RANK.clear()
    b = B(nc)
    io = {n: b.dram(n, s, d, "ExternalInput") for (n, s, d) in FUSED_IN}
    io["OUT"] = b.dram("OUT", [2048, 1024], F32, "ExternalOutput")
    qt2 = nc.dram_tensor("scr_qt2", [16, 128, 1024], BF16).ap()
    qi = nc.dram_tensor("scr_qi", [16, 128, 512], BF16).ap()
    sg = nc.dram_tensor("scr_sg", [16, 128, 8], F32).ap()
    x2 = nc.dram_tensor("scr_x2", [2048, 1024], F32).ap()
    mqd = nc.dram_tensor("scr_mq", [16, 128, 8192], BF16).ap()
    scr = {"QT2": [T(qt2[a]) for a in range(NS)], "QI": [T(qi[a]) for a in range(NS)], "SG": [T(sg[a]) for a in range(NS)],
           "MQ": [T(mqd[a]) for a in range(NS)]}
    x2d = [T(x2[a * 128:(a + 1) * 128, :]) for a in range(NS)]

    b.ident(); b.eps()
    pos_t = b.sb([128, NS], I32, "pos")
    b.op("sp", lambda e: e.dma_start(out=pos_t[:], in_=io["pos"][:]), writes=[pos_t], dma="ld_pos")
    cos, sin = rope_tables(b, pos_t)
    o_tok = [b.sb([128, 1024], BF16, f"otok{a}", top=True) for a in range(NS)]
    hi_otok = b.hi
    qT_res = b.sb([128, NS, 1024], BF16, "qTres", top=True)
    zt = b.sb([128, 2048], BF16, "zeros")
    b.op("dve", lambda e: e.memset(zt[:], 0.0), writes=[zt])
    m0 = b.mark()
    G0 = Gather(b, "g0", 8, 4096, zt, "act")
    wst = [T(nc.alloc_sbuf_tensor_at(f"wstage{i}", [128, 8, 512], F32, offset=ARENA_END - (i + 1) * 16384)) for i in range(2)]
    f_diff_proj(b, io, qT_res, G0, cos, sin, wstage=wst)
    b.release(m0)
    G1 = Gather(b, "g1", 8, 4096, zt, "pool")
    GK = Gather(b, "gk", 1, 2048, zt, "pool")
    f_diff_attn(b, io, o_tok, qT_res, G0)
    b.release(m0)
    b.hi = hi_otok
    x_res = [b.sb([128, 1024], F32, f"xres{a}") for a in range(NS)]
    mh = b.mark()
    h2T = b.sb([128, 8, 2048], BF16, "h2T")
    m2 = b.mark()
    phase_post_attn(b, io, o_tok, x_res, h2T, io["w_out"][:], io["gmlp"][:], io["x"])
    b.release(m2)
    b.hi = ARENA_END
    pre = dsa_proj_weights(b, io, top=True)
    phase_mlp(b, x_res, h2T, io["w1"], io["w2"], hook=pre["load"])
    b.release((mh[0], b.hi))
    for a in range(NS):
        b.op("sp", lambda e, a=a: e.dma_start(out=x2d[a][:], in_=x_res[a][:]), reads=[x_res[a]], writes=[x2d[a]], dma="st_x2")
    io2 = dict(io)
    f_dsa_proj(b, io2, x_res, cos, sin, G1, GK, scr, pre=pre)
    b.release(m0)
    b.hi = ARENA_END
    m4 = b.mark()
    o_tok = f_dsa_attn(b, io, None, G1, GK, scr)
    b.release((m4[0], b.hi))
    x_res = [b.sb([128, 1024], F32, f"xresb{a}") for a in range(NS)]
    h2T = b.sb([128, 8, 2048], BF16, "h2Tb")
    m5 = b.mark()
    phase_post_attn(b, io, o_tok, x_res, h2T, io["w_outb"][:], io["gmlpb"][:], x2, xdeps=x2d)
    b.release(m5)
    b.hi = ARENA_END
    phase_mlp(b, x_res, h2T, io["w1b"], io["w2b"])
    for a in range(NS):
        b.store("sp", lambda e, a=a: e.dma_start(out=io["OUT"][a * 128:(a + 1) * 128, :], in_=x_res[a][:]),
                reads=[x_res[a]], dma="st_out")
    b.finish()
    return nc


def fused_inputs(inp):
    lam = rep(np.concatenate([inp["diff_lam_q1"][0], inp["diff_lam_k1"][0], inp["diff_lam_q2"][0], inp["diff_lam_k2"][0]]))
    gqk = rep(np.concatenate([np.tile(inp["diff_q_norm"][0], 16), np.tile(inp["diff_k_norm"][0], 16)]))
    gqk2 = rep(np.concatenate([np.tile(inp["dsa_q_norm"][0], 16), np.tile(inp["dsa_k_norm"][0], 16)]))
    c_ = np.ascontiguousarray
    shared = {"gmix": rep(inp["norm_mix"][0]), "gqk": gqk, "w_in": c_(inp["diff_w_in"][0]), "lam": lam,
              "gsub": rep(inp["diff_subln"][0]), "w_out": c_(inp["diff_w_out"][0]), "gmlp": rep(inp["norm_mlp"][0]),
              "w1": c_(inp["mlp_w1"][0]), "w2": c_(inp["mlp_w2"][0]), "gmix1": rep(inp["norm_mix"][1]), "gqk2": gqk2,
              "gcq": rep(inp["dsa_cq_norm"][0]), "w_in2": c_(inp["dsa_w_in"][0]), "w_uq": c_(inp["dsa_w_uq"][0]),
              "w_uqi": c_(inp["dsa_w_uq_idx"][0]), "w_outb": c_(inp["dsa_w_out"][0]), "gmlpb": rep(inp["norm_mlp"][1]),
              "w1b": c_(inp["mlp_w1"][1]), "w2b": c_(inp["mlp_w2"][1])}
    ins = []
    for c in range(8):
        d = dict(shared)
        d["x"] = own_tiles(inp["x"], c)
        d["pos"] = np.ascontiguousarray(own_tiles(inp["positions"], c).reshape(16, 128).T)
        d["maskT"] = diff_masks(c)
        d["negmask"] = dsa_negmask(c)
        ins.append(d)
    return ins


def kernel(**inputs):
    inp = {k: np.asarray(v) for k, v in inputs.items()}
    r = run(build_fused(), fused_inputs(inp))
    return assemble(r)
```

```python
import math
from contextlib import ExitStack
import numpy as np
import ml_dtypes
import concourse.bass as bass
import concourse.mybir as mybir
from concourse.bass_utils import run_bass_kernel_spmd


F32 = mybir.dt.float32
BF16 = mybir.dt.bfloat16
I32 = mybir.dt.int32
AF = mybir.ActivationFunctionType
ALU = mybir.AluOpType
AX = mybir.AxisListType


class Dep:
    __slots__ = ("w", "r")

    def __init__(self):
        self.w = None
        self.r = {}


class Sched:
    ENG = ("pe", "act", "dve", "pool", "sp")

    def __init__(self, nc):
        self.nc = nc
        self.ops = {e: [] for e in self.ENG}
        self.cnt = {e: 0 for e in self.ENG}
        self.known = {e: {} for e in self.ENG}
        self.dma_cnt = {}
        self.stack = ExitStack()
        self.nt = 0

    def sb(self, shape, dtype, name=None):
        self.nt += 1
        name = "sb_" + (name or f"t{self.nt}")
        return self.stack.enter_context(self.nc.sbuf_tensor(name, list(shape), dtype))

    def ps(self, shape, dtype, name=None):
        self.nt += 1
        name = "ps_" + (name or f"p{self.nt}")
        return self.stack.enter_context(self.nc.psum_tensor(name, list(shape), dtype))

    def op(self, eng, fn, reads=(), writes=(), dma=None, sem_inc=16):
        waits = {}

        def need(ev, raw):
            if ev is None:
                return
            key, val = ev
            if key == eng:
                if eng == "pe" or not raw:
                    return
            if waits.get(key, 0) < val:
                waits[key] = val

        for d in reads:
            need(d.w, True)
        for d in writes:
            need(d.w, False)
            for ev in d.r.items():
                need(ev, False)
        kn = self.known[eng]
        wl = []
        for key, val in waits.items():
            if kn.get(key, 0) >= val:
                continue
            kn[key] = val
            wl.append((key, val))
        if dma is not None:
            n = self.dma_cnt.get(dma, 0) + sem_inc
            self.dma_cnt[dma] = n
            ev = (dma, n)
            inc = (dma, sem_inc)
        else:
            self.cnt[eng] += 1
            ev = (eng, self.cnt[eng])
            inc = (eng, 1)
        self.ops[eng].append((wl, fn, inc))
        for d in reads:
            if d.r.get(ev[0], 0) < ev[1]:
                d.r[ev[0]] = ev[1]
        for d in writes:
            d.w = ev
            d.r = {}
        return ev

    def final_wait(self, eng, deps):
        waits = {}
        for d in deps:
            for ev in ([d.w] if d.w else []) + list(d.r.items()):
                if waits.get(ev[0], 0) < ev[1]:
                    waits[ev[0]] = ev[1]
        self.ops[eng].append((list(waits.items()), None, None))

    def barrier(self):
        waits = {e: self.cnt[e] for e in self.ENG if self.cnt[e] > 0}
        for k, n in self.dma_cnt.items():
            if not k.startswith("cc_"):
                waits[k] = n
        for e in self.ENG:
            kn = self.known[e]
            wl = []
            for key, val in waits.items():
                if key == e or kn.get(key, 0) >= val:
                    continue
                kn[key] = val
                wl.append((key, val))
            self.ops[e].append((wl, None, None))

    def final_events(self, eng, evs):
        waits = {}
        for ev in evs:
            if waits.get(ev[0], 0) < ev[1]:
                waits[ev[0]] = ev[1]
        self.ops[eng].append((list(waits.items()), None, None))

    def emit(self):
        nc = self.nc
        keys = set(self.ENG) | set(self.dma_cnt.keys())
        assert len(keys) <= 100, f"too many semaphores: {len(keys)}"
        sems = {}
        for k in sorted(keys):
            sems[k] = self.stack.enter_context(nc.semaphore("s_" + k))
        ops = self.ops

        def run(e, lst):
            for wl, fn, inc in lst:
                for key, val in wl:
                    e.wait_ge(sems[key], val)
                if fn is None:
                    continue
                ins = fn(e)
                if inc is not None:
                    ins.then_inc(sems[inc[0]], inc[1])

        with nc.Block() as block:
            @block.tensor
            def _(e):
                run(e, ops["pe"])

            @block.scalar
            def _(e):
                run(e, ops["act"])

            @block.vector
            def _(e):
                run(e, ops["dve"])

            @block.gpsimd
            def _(e):
                run(e, ops["pool"])

            @block.sync
            def _(e):
                run(e, ops["sp"])
        self.stack.close()


NS = 16
D = 1024
EPS = 1e-6
INV_FREQ = [500000.0 ** (-(2.0 * j) / 16.0) for j in range(8)]
TWO_PI_S = 6.28318
PI_S = 3.14159


class T:
    def __init__(self, t, d=None):
        self.t = t
        self.d = d if d is not None else Dep()

    def __getitem__(self, k):
        return self.t[k]


ARENA_BASE = 16512
ARENA_END = 16512 + 212736


class B:
    def __init__(self, nc):
        self.nc = nc
        self.S = Sched(nc)
        self._consts = {}
        self.final = []
        self.arena = nc.alloc_sbuf_tensor("arena", [128, ARENA_END - ARENA_BASE], mybir.dt.uint8)
        self.lo = ARENA_BASE
        self.hi = ARENA_END
        self.nt = 0
        self.banks = [T(nc.alloc_psum_tensor(f"bank{i}", [128, 512], F32)) for i in range(8)]
        self.banks16 = [T(bk.t[:].bitcast(BF16), bk.d) for bk in self.banks]

    def sb(self, shape, dt, name=None, top=False):
        self.nt += 1
        nm = f"sb{self.nt}_{name or 't'}"
        size = int(np.prod(shape[1:])) * mybir.dt.size(dt)
        size = (size + 31) // 32 * 32
        if top:
            self.hi -= size
            off = self.hi
        else:
            off = self.lo
            self.lo += size
        assert self.lo <= self.hi, f"SBUF arena overflow at {nm}: lo={self.lo} hi={self.hi}"
        return T(self.nc.alloc_sbuf_tensor_at(nm, list(shape), dt, offset=off))

    def mark(self):
        return (self.lo, self.hi)

    def release(self, m):
        self.lo, self.hi = m
        self.S.barrier()

    def dram(self, name, shape, dt, kind):
        t = T(self.nc.dram_tensor(name, list(shape), dt, kind=kind).ap())
        return t

    def op(self, eng, fn, reads=(), writes=(), dma=None, sem_inc=16):
        return self.S.op(eng, fn, [x.d for x in reads], [x.d for x in writes], dma, sem_inc)

    def store(self, eng, fn, reads, dma):
        ev = self.S.op(eng, fn, [x.d for x in reads], [], dma)
        self.final.append(ev)
        return ev

    def finish(self):
        self.S.final_events("sp", self.final)
        self.S.emit()

    def ident(self):
        if "ident" not in self._consts:
            idt = self.sb([128, 128], BF16, "ident")
            self.op("pool", lambda e: e.memset(idt[:], 0.0), writes=[idt])
            self.op("pool", lambda e: e.affine_select(out=idt[:], in_=idt[:], pattern=[[-1, 128]],
                                                     compare_op=ALU.not_equal, fill=1.0, base=0,
                                                     channel_multiplier=1), reads=[idt], writes=[idt])
            self._consts["ident"] = idt
        return self._consts["ident"]

    def eps(self):
        if "eps" not in self._consts:
            t = self.sb([128, 1], F32, "eps")
            self.op("pool", lambda e: e.memset(t[:], EPS), writes=[t])
            self._consts["eps"] = t
        return self._consts["eps"]


def rstd_from_ss(b, ss, n, scale):
    eps = b.eps()
    b.op("act", lambda e: e.activation(out=ss[:, 0:n], in_=ss[:, 0:n], func=AF.Ln, bias=eps[:], scale=scale),
         reads=[ss, eps], writes=[ss])
    b.op("act", lambda e: e.activation(out=ss[:, 0:n], in_=ss[:, 0:n], func=AF.Exp, scale=-0.5),
         reads=[ss], writes=[ss])


def rope_tables(b, pos_t):
    posf = b.sb([128, NS], F32, "posf")
    inv = b.sb([128, 8], F32, "invf")
    ang = b.sb([128, NS, 8], F32, "ang")
    ti = b.sb([128, NS, 8], I32, "angi")
    tf = b.sb([128, NS, 8], F32, "angf")
    neg = b.sb([128, NS, 8], F32, "angn")
    cos = b.sb([128, NS, 8], F32, "cos")
    sin = b.sb([128, NS, 8], F32, "sin")
    nb = b.sb([128, 1], F32, "negpi")
    b.op("pool", lambda e: e.memset(nb[:], -PI_S), writes=[nb])
    b.op("dve", lambda e: e.tensor_copy(out=posf[:], in_=pos_t[:]), reads=[pos_t], writes=[posf])
    for j in range(8):
        b.op("pool", lambda e, j=j: e.memset(inv[:, j:j + 1], INV_FREQ[j] / (2 * math.pi)), writes=[inv])
    b.op("dve", lambda e: e.tensor_tensor(out=ang[:], in0=posf[:].unsqueeze(2).to_broadcast([128, NS, 8]),
                                          in1=inv[:].unsqueeze(1).to_broadcast([128, NS, 8]), op=ALU.mult),
         reads=[posf, inv], writes=[ang])
    for (dst, off) in ((sin, 0.5), (cos, 0.75)):
        b.op("dve", lambda e, off=off: e.tensor_scalar(out=tf[:], in0=ang[:], scalar1=off, scalar2=None, op0=ALU.add),
             reads=[ang], writes=[tf])
        b.op("dve", lambda e: e.tensor_copy(out=ti[:], in_=tf[:]), reads=[tf], writes=[ti])
        b.op("dve", lambda e: e.tensor_copy(out=neg[:], in_=ti[:]), reads=[ti], writes=[neg])
        b.op("dve", lambda e: e.tensor_tensor(out=tf[:], in0=tf[:], in1=neg[:], op=ALU.subtract),
             reads=[tf, neg], writes=[tf])
        b.op("dve", lambda e: e.tensor_scalar(out=neg[:], in0=tf[:], scalar1=0.0, scalar2=None, op0=ALU.is_lt),
             reads=[tf], writes=[neg])
        b.op("dve", lambda e: e.tensor_tensor(out=tf[:], in0=tf[:], in1=neg[:], op=ALU.add),
             reads=[tf, neg], writes=[tf])
        b.op("act", lambda e, dst=dst: e.activation(out=dst[:], in_=tf[:], func=AF.Sin, bias=nb[:], scale=TWO_PI_S),
             reads=[tf, nb], writes=[dst])
    return cos, sin


def rmsnorm_transpose(b, xt, gt, hbf, hT, pT, ss, junk):
    idt = b.ident()
    b.op("act", lambda e: e.activation(out=junk[:], in_=xt[:], func=AF.Square, accum_out=ss[:, 0:1]),
         reads=[xt], writes=[junk, ss])
    rstd_from_ss(b, ss, 1, 1.0 / D)
    b.op("dve", lambda e: e.scalar_tensor_tensor(out=hbf[:], in0=xt[:], scalar=ss[:, 0:1], in1=gt[:],
                                                 op0=ALU.mult, op1=ALU.mult),
         reads=[xt, ss, gt], writes=[hbf])
    for kc in range(8):
        b.op("pe", lambda e, kc=kc: e.transpose(out=pT[:, kc * 128:(kc + 1) * 128],
                                                in_=hbf[:, kc * 128:(kc + 1) * 128], identity=idt[:]),
             reads=[hbf, idt], writes=[pT])
    b.op("dve", lambda e: e.tensor_copy(out=hT[:].rearrange("p a b -> p (a b)"), in_=pT[:]),
         reads=[pT], writes=[hT])


def headnorm_rope(b, stage, sq, ssq, nh, gain, cos_a, sin_a, outbf, tmp, norm=True):
    s3 = stage[:].rearrange("p (h d) -> p h d", d=64)
    if norm:
        b.op("dve", lambda e: e.tensor_reduce(out=ssq[:, 0:nh], in_=sq[:].rearrange("p (h d) -> p h d", d=64),
                                              axis=AX.X, op=ALU.add), reads=[sq], writes=[ssq])
        rstd_from_ss(b, ssq, nh, 1.0 / 64)
    b.op("dve", lambda e: e.tensor_tensor(out=s3, in0=s3, in1=ssq[:, 0:nh].unsqueeze(2).to_broadcast([128, nh, 64]),
                                          op=ALU.mult), reads=[stage, ssq], writes=[stage])
    if gain is not None:
        b.op("pool", lambda e: e.tensor_tensor(out=stage[:], in0=stage[:], in1=gain[:], op=ALU.mult),
             reads=[stage, gain], writes=[stage])
    b.op("act", lambda e: e.activation(out=outbf[:], in_=stage[:], func=AF.Copy), reads=[stage], writes=[outbf])
    o3 = outbf[:].rearrange("p (h d) -> p h d", d=64)
    x1 = s3[:, :, 0:8]
    x2 = s3[:, :, 8:16]
    cb = cos_a.unsqueeze(1).to_broadcast([128, nh, 8])
    sb_ = sin_a.unsqueeze(1).to_broadcast([128, nh, 8])
    t = tmp[:].rearrange("p (k h d) -> p k h d", k=4, d=8)
    eng = "pool"
    b.op(eng, lambda e: e.tensor_tensor(out=t[:, 0, 0:nh, :], in0=x1, in1=cb, op=ALU.mult), reads=[stage], writes=[tmp])
    b.op(eng, lambda e: e.tensor_tensor(out=t[:, 1, 0:nh, :], in0=x2, in1=sb_, op=ALU.mult), reads=[stage], writes=[tmp])
    b.op(eng, lambda e: e.tensor_tensor(out=t[:, 2, 0:nh, :], in0=x2, in1=cb, op=ALU.mult), reads=[stage], writes=[tmp])
    b.op(eng, lambda e: e.tensor_tensor(out=t[:, 3, 0:nh, :], in0=x1, in1=sb_, op=ALU.mult), reads=[stage], writes=[tmp])
    b.op(eng, lambda e: e.tensor_tensor(out=o3[:, :, 0:8], in0=t[:, 0, 0:nh, :], in1=t[:, 1, 0:nh, :], op=ALU.subtract),
         reads=[tmp], writes=[outbf])
    b.op(eng, lambda e: e.tensor_tensor(out=o3[:, :, 8:16], in0=t[:, 2, 0:nh, :], in1=t[:, 3, 0:nh, :], op=ALU.add),
         reads=[tmp], writes=[outbf])


def phase_diff_proj(b, io):
    idt = b.ident()
    pos_t = b.sb([128, NS], I32, "pos")
    b.op("sp", lambda e: e.dma_start(out=pos_t[:], in_=io["pos"][:]), writes=[pos_t], dma="ld_pos")
    gmix = b.sb([128, D], F32, "gmix")
    b.op("sp", lambda e: e.dma_start(out=gmix[:], in_=io["gmix"][:]), writes=[gmix], dma="ld_gmix")
    gqk = b.sb([128, 2048], F32, "gqk")
    b.op("sp", lambda e: e.dma_start(out=gqk[:], in_=io["gqk"][:]), writes=[gqk], dma="ld_gqk")
    b.op("pool", lambda e: e.tensor_scalar(out=gqk[:, 0:1024], in0=gqk[:, 0:1024], scalar1=0.125, scalar2=None,
                                           op0=ALU.mult), reads=[gqk], writes=[gqk])
    w = b.sb([128, 8, 3072], BF16, "w_in")
    for kc in range(8):
        for hf in range(3):
            b.op("pool", lambda e, kc=kc, hf=hf: e.dma_start(
                out=w[:, kc, hf * 1024:(hf + 1) * 1024],
                in_=io["w_in"][kc * 128:(kc + 1) * 128, hf * 1024:(hf + 1) * 1024]),
                writes=[w], dma="ld_w")
    cos, sin = rope_tables(b, pos_t)

    xts = [b.sb([128, D], F32, f"xt{i}") for i in range(2)]
    junk = b.sb([128, D], BF16, "junk")
    ss = b.sb([128, 1], F32, "ss")
    hbf = b.sb([128, D], BF16, "hbf")
    hT = b.sb([128, 8, 128], BF16, "hT")
    pT = [b.banks16[0], b.banks16[1]]
    pY = [b.banks[2 + i] for i in range(4)]
    stage = b.sb([128, 2048], F32, "stage")
    sq = b.sb([128, 2048], F32, "sq")
    ssq = b.sb([128, 32], F32, "ssq")
    tmp = b.sb([128, 4 * 32 * 8], F32, "ropetmp")
    qkbf = b.sb([128, 2048], BF16, "qkbf")
    qkT = [b.sb([128, 16, 128], BF16, f"qkT{i}") for i in range(2)]
    vaug = [b.sb([128, 8, 129], BF16, f"vaug{i}") for i in range(2)]
    for i in range(2):
        b.op("pool", lambda e, i=i: e.memset(vaug[i][:], 1.0), writes=[vaug[i]])

    for a in range(NS):
        xt = xts[a % 2]
        b.op("sp", lambda e, a=a, xt=xt: e.dma_start(out=xt[:], in_=io["x"][a * 128:(a + 1) * 128, :]),
             writes=[xt], dma=f"ld_x{a % 2}")
        rmsnorm_transpose(b, xt, gmix, hbf, hT, pT[0], ss, junk)
        for n in range(6):
            py = pY[n % 4]
            for kc in range(8):
                b.op("pe", lambda e, n=n, kc=kc, py=py: e.matmul(py[:], lhsT=hT[:, kc, :],
                                                                   rhs=w[:, kc, n * 512:(n + 1) * 512],
                                                                   start=(kc == 0), stop=(kc == 7)),
                     reads=[hT, w], writes=[py])
            if n < 4:
                b.op("act", lambda e, n=n, py=py: e.activation(out=stage[:, n * 512:(n + 1) * 512], in_=py[:], func=AF.Copy),
                     reads=[py], writes=[stage])
                b.op("act", lambda e, n=n, py=py: e.activation(out=sq[:, n * 512:(n + 1) * 512], in_=py[:], func=AF.Square),
                     reads=[py], writes=[sq])
            else:
                va = vaug[a % 2]
                b.op("dve", lambda e, n=n, py=py, va=va: e.tensor_copy(
                    out=va[:, (n - 4) * 4:(n - 4) * 4 + 4, 0:128], in_=py[:].rearrange("p (h d) -> p h d", d=128)),
                    reads=[py], writes=[va])
        va = vaug[a % 2]
        b.store("sp", lambda e, a=a, va=va: e.dma_start(out=io["V"][a], in_=va[:].rearrange("p h d -> p (h d)")),
                reads=[va], dma=f"st_v{a % 2}")
        headnorm_rope(b, stage, sq, ssq, 32, gqk, cos[:, a, :], sin[:, a, :], qkbf, tmp)
        qt = qkT[a % 2]
        for i in range(16):
            pt = pT[1] if i < 8 else pT[0]
            b.op("pe", lambda e, i=i, pt=pt: e.transpose(out=pt[:, (i % 8) * 128:(i % 8 + 1) * 128],
                                                         in_=qkbf[:, i * 128:(i + 1) * 128], identity=idt[:]),
                 reads=[qkbf, idt], writes=[pt])
            if i % 8 == 7:
                b.op("dve", lambda e, i=i, pt=pt, qt=qt: e.tensor_copy(
                    out=qt[:, (i // 8) * 8:(i // 8) * 8 + 8, :].rearrange("p a b -> p (a b)"), in_=pt[:]),
                    reads=[pt], writes=[qt])
        b.store("sp", lambda e, a=a, qt=qt: e.dma_start(out=io["QT"][a], in_=qt[:, 0:8, :].rearrange("p a b -> p (a b)")),
                reads=[qt], dma=f"st_q{a % 2}")
        b.store("sp", lambda e, a=a, qt=qt: e.dma_start(out=io["KT"][a], in_=qt[:, 8:16, :].rearrange("p a b -> p (a b)")),
                reads=[qt], dma=f"st_k{a % 2}")


def phase_diff_attn(b, io, o_tok):
    LAM_INIT = 0.2
    qT = b.sb([128, NS, 1024], BF16, "qT")
    for a in range(NS):
        b.op("sp", lambda e, a=a: e.dma_start(out=qT[:, a, :], in_=io["QT"][a]), writes=[qT], dma="ld_qT")
    maskT = b.sb([128, 512], BF16, "maskT")
    b.op("sp", lambda e: e.dma_start(out=maskT[:], in_=io["maskT"][:]), writes=[maskT], dma="ld_mask")
    lam = b.sb([128, 256], F32, "lam")
    b.op("sp", lambda e: e.dma_start(out=lam[:], in_=io["lam"][:]), writes=[lam], dma="ld_lam")
    gsub = b.sb([128, 128], F32, "gsub")
    b.op("sp", lambda e: e.dma_start(out=gsub[:], in_=io["gsub"][:]), writes=[gsub], dma="ld_gsub")
    b.op("pool", lambda e: e.tensor_scalar(out=gsub[:], in0=gsub[:], scalar1=1.0 - LAM_INIT, scalar2=None,
                                           op0=ALU.mult), reads=[gsub], writes=[gsub])
    lprod = b.sb([128, 128], F32, "lprod")
    l2 = b.sb([128, 2], F32, "l2")
    neglam = b.sb([128, 1], F32, "neglam")
    l4 = lam[:].rearrange("p (a b d) -> p a b d", a=2, b=2)
    b.op("dve", lambda e: e.tensor_tensor(out=lprod[:].rearrange("p (a d) -> p a d", a=2), in0=l4[:, :, 0, :],
                                          in1=l4[:, :, 1, :], op=ALU.mult), reads=[lam], writes=[lprod])
    b.op("dve", lambda e: e.tensor_reduce(out=l2[:], in_=lprod[:].rearrange("p (a d) -> p a d", a=2), axis=AX.X,
                                          op=ALU.add), reads=[lprod], writes=[l2])
    b.op("act", lambda e: e.activation(out=l2[:], in_=l2[:], func=AF.Exp), reads=[l2], writes=[l2])
    b.op("dve", lambda e: e.tensor_tensor(out=neglam[:], in0=l2[:, 1:2], in1=l2[:, 0:1], op=ALU.subtract),
         reads=[l2], writes=[neglam])
    b.op("dve", lambda e: e.tensor_scalar(out=neglam[:], in0=neglam[:], scalar1=-LAM_INIT, scalar2=None, op0=ALU.add),
         reads=[neglam], writes=[neglam])

    ktb = [b.sb([128, 8192], BF16, f"ktb{i}") for i in range(2)]
    vb = [b.sb([128, 64, 129], BF16, f"vb{i}") for i in range(2)]
    pS = [[b.banks[c * 2 + i] for i in range(2)] for c in range(2)]
    pA = [[b.banks[4 + c * 2 + i] for i in range(2)] for c in range(2)]
    pT = [[b.sb([128, 512], BF16, f"pTs{c}{i}") for i in range(2)] for c in range(2)]
    rec = [b.sb([128, 2], F32, f"rec{i}") for i in range(2)]
    o32 = [b.sb([128, 128], F32, f"o32{i}") for i in range(2)]
    oj = b.sb([128, 128], BF16, "ojunk")
    ss1 = [b.sb([128, 1], F32, f"ss1{i}") for i in range(2)]

    units = [(h, a, blk) for h in range(8) for a in range(NS) for blk in range(a + 1)]

    def load_head(h):
        kb, vv = ktb[h % 2], vb[h % 2]
        for part in range(4):
            b.op("sp", lambda e, h=h, kb=kb, part=part: e.dma_start(
                out=kb[:, part * 2048:(part + 1) * 2048], in_=io["KTall"][h][:, part * 2048:(part + 1) * 2048]),
                writes=[kb], dma=f"ld_kt{h % 2}")
            b.op("sp", lambda e, h=h, vv=vv, part=part: e.dma_start(
                out=vv[:, part * 16:(part + 1) * 16, :].rearrange("p a b -> p (a b)"),
                in_=io["Vall"][h][:, part * 16 * 129:(part + 1) * 16 * 129]),
                writes=[vv], dma=f"ld_v{h % 2}")

    def qk(u, n):
        h, a, blk = u
        kb = ktb[h % 2]
        for c in range(2):
            ps = pS[c][n % 2]
            for i in range(4):
                kt = 4 * blk + i
                b.op("pe", lambda e, c=c, i=i, kt=kt, ps=ps, kb=kb, a=a, h=h: e.matmul(
                    ps[:, i * 128:(i + 1) * 128], lhsT=kb[64 * c:64 * c + 64, kt * 128:(kt + 1) * 128],
                    rhs=qT[64 * c:64 * c + 64, a, h * 128:(h + 1) * 128], start=True, stop=True),
                    reads=[kb, qT], writes=[ps])

    def softmax_pv(u, n):
        h, a, blk = u
        vv = vb[h % 2]
        for c in range(2):
            ps = pS[c][n % 2]
            pt = pT[c][n % 2]
            acc = pA[c][a % 2]
            b.op("act", lambda e, ps=ps, pt=pt: e.activation(out=pt[:], in_=ps[:], func=AF.Exp),
                 reads=[ps], writes=[pt])
            if blk == a:
                b.op("pool", lambda e, pt=pt: e.tensor_tensor(out=pt[:], in0=pt[:], in1=maskT[:], op=ALU.mult),
                     reads=[pt, maskT], writes=[pt])
            for i in range(4):
                kt = 4 * blk + i
                b.op("pe", lambda e, i=i, kt=kt, pt=pt, acc=acc, vv=vv, blk=blk, a=a: e.matmul(
                    acc[:, 0:129], lhsT=pt[:, i * 128:(i + 1) * 128], rhs=vv[:, kt, :],
                    start=(blk == 0 and i == 0), stop=(blk == a and i == 3)),
                    reads=[pt, vv], writes=[acc])
        if blk == a:
            evac(h, a)

    def evac(h, a):
        k = a % 2
        a0, a1 = pA[0][k], pA[1][k]
        r, o, s = rec[k], o32[k], ss1[k]
        b.op("dve", lambda e: e.reciprocal(out=r[:, 0:1], in_=a0[:, 128:129]), reads=[a0], writes=[r])
        b.op("dve", lambda e: e.reciprocal(out=r[:, 1:2], in_=a1[:, 128:129]), reads=[a1], writes=[r])
        b.op("dve", lambda e: e.tensor_tensor(out=r[:, 1:2], in0=r[:, 1:2], in1=neglam[:], op=ALU.mult),
             reads=[r, neglam], writes=[r])
        b.op("dve", lambda e: e.tensor_scalar(out=o[:], in0=a0[:, 0:128], scalar1=r[:, 0:1], scalar2=None, op0=ALU.mult),
             reads=[a0, r], writes=[o])
        b.op("dve", lambda e: e.scalar_tensor_tensor(out=o[:], in0=a1[:, 0:128], scalar=r[:, 1:2], in1=o[:],
                                                     op0=ALU.mult, op1=ALU.add), reads=[a1, r, o], writes=[o])
        b.op("act", lambda e: e.activation(out=oj[:], in_=o[:], func=AF.Square, accum_out=s[:, 0:1]),
             reads=[o], writes=[oj, s])
        rstd_from_ss(b, s, 1, 1.0 / 128)
        ot = o_tok[a]
        b.op("dve", lambda e: e.scalar_tensor_tensor(out=ot[:, h * 128:(h + 1) * 128], in0=o[:], scalar=s[:, 0:1],
                                                     in1=gsub[:], op0=ALU.mult, op1=ALU.mult),
             reads=[o, s, gsub], writes=[ot])

    load_head(0)
    qk(units[0], 0)
    for n, u in enumerate(units):
        if u[1] == 0 and u[2] == 0 and u[0] + 1 < 8:
            load_head(u[0] + 1)
        if n + 1 < len(units):
            qk(units[n + 1], n + 1)
        softmax_pv(u, n)


def load_w_bf16(b, wt, src, nk, ncols, key, colchunk=1024):
    for c0 in range(0, ncols, colchunk):
        for kc in range(nk):
            c1 = min(ncols, c0 + colchunk)
            b.op("pool", lambda e, kc=kc, c0=c0, c1=c1: e.dma_start(
                out=wt[:, kc, c0:c1], in_=src[kc * 128:(kc + 1) * 128, c0:c1]), writes=[wt], dma=key)


def phase_post_attn(b, io, o_tok, x_res, h2T, wout_ap, gmlp_ap, xsrc, xdeps=None):
    idt = b.ident()
    m = b.mark()
    wout = b.sb([128, 8, 1024], BF16, "wout")
    load_w_bf16(b, wout, wout_ap, 8, 1024, "ld_wout")
    gm = b.sb([128, D], F32, "gmlp")
    b.op("sp", lambda e: e.dma_start(out=gm[:], in_=gmlp_ap), writes=[gm], dma="ld_gmlp")
    oT = [b.sb([128, 8, 128], BF16, f"oT{i}") for i in range(2)]
    junk = b.sb([128, D], BF16, "junk2")
    ss = [b.sb([128, 1], F32, f"ss2{i}") for i in range(2)]
    hbf = [b.sb([128, D], BF16, f"hbf2{i}") for i in range(2)]
    def head(a):
        xr = x_res[a]
        b.op("sp", lambda e, a=a, xr=xr: e.dma_start(out=xr[:], in_=xsrc[a * 128:(a + 1) * 128, :]),
             reads=([xdeps[a]] if xdeps else []), writes=[xr], dma="ld_xres")
        pt = b.banks16[a % 2]
        ot = oT[a % 2]
        for kc in range(8):
            b.op("pe", lambda e, kc=kc, pt=pt, a=a: e.transpose(out=pt[:, kc * 128:(kc + 1) * 128],
                                                                 in_=o_tok[a][:, kc * 128:(kc + 1) * 128], identity=idt[:]),
                 reads=[o_tok[a], idt], writes=[pt])
        b.op("act", lambda e, pt=pt, ot=ot: e.activation(out=ot[:].rearrange("p a b -> p (a b)"), in_=pt[:], func=AF.Copy),
             reads=[pt], writes=[ot])
        for n in range(2):
            py = b.banks[2 + (2 * a + n) % 4]
            for kc in range(8):
                b.op("pe", lambda e, kc=kc, n=n, py=py, ot=ot: e.matmul(py[:], lhsT=ot[:, kc, :],
                                                                         rhs=wout[:, kc, n * 512:(n + 1) * 512],
                                                                         start=(kc == 0), stop=(kc == 7)),
                     reads=[ot, wout], writes=[py])
            b.op("dve", lambda e, n=n, py=py, xr=xr: e.tensor_tensor(out=xr[:, n * 512:(n + 1) * 512],
                                                                     in0=xr[:, n * 512:(n + 1) * 512], in1=py[:], op=ALU.add),
                 reads=[xr, py], writes=[xr])

    def tail(a):
        rms_to_hT(b, x_res[a], gm, hbf[a % 2], h2T, a, b.banks16[6 + a % 2], ss[a % 2], junk)
    head(0)
    for a in range(NS):
        if a + 1 < NS:
            head(a + 1)
        tail(a)
    return m


def rms_to_hT(b, xr, gm, hbf, h2T, a, pt, ss, junk):
    idt = b.ident()
    b.op("act", lambda e: e.activation(out=junk[:], in_=xr[:], func=AF.Square, accum_out=ss[:, 0:1]),
         reads=[xr], writes=[junk, ss])
    rstd_from_ss(b, ss, 1, 1.0 / D)
    b.op("dve", lambda e: e.scalar_tensor_tensor(out=hbf[:], in0=xr[:], scalar=ss[:, 0:1], in1=gm[:],
                                                 op0=ALU.mult, op1=ALU.mult), reads=[xr, ss, gm], writes=[hbf])
    for kc in range(8):
        b.op("pe", lambda e, kc=kc: e.transpose(out=pt[:, kc * 128:(kc + 1) * 128],
                                                in_=hbf[:, kc * 128:(kc + 1) * 128], identity=idt[:]),
             reads=[hbf, idt], writes=[pt])
    b.op("act", lambda e: e.activation(out=h2T[:, :, a * 128:(a + 1) * 128],
                                       in_=pt[:].rearrange("p (k t) -> p k t", k=8), func=AF.Copy),
         reads=[pt], writes=[h2T])


def phase_mlp(b, x_res, h2T, w1_ap, w2_ap, hook=None):
    NFC = 8
    w1c = [b.sb([128, 8, 512], BF16, f"w1c{i}") for i in range(2)]
    w2c = [b.sb([128, 4, 1024], BF16, f"w2c{i}") for i in range(2)]
    rbuf = [b.sb([128, 512], F32, f"rbuf{i}") for i in range(2)]
    uT = [b.sb([128, 4, 512], BF16, f"uT{i}") for i in range(2)]
    pu = [b.banks[0], b.banks[1]]
    po = [b.banks[2 + i] for i in range(4)]

    def load_chunk(fc):
        w1, w2 = w1c[fc % 2], w2c[fc % 2]
        for kc in range(8):
            b.op("pool", lambda e, kc=kc, fc=fc, w1=w1: e.dma_start(
                out=w1[:, kc, :], in_=w1_ap[kc * 128:(kc + 1) * 128, fc * 512:(fc + 1) * 512]),
                writes=[w1], dma=f"ld_w1{fc % 2}")
        for ft in range(4):
            b.op("pool", lambda e, ft=ft, fc=fc, w2=w2: e.dma_start(
                out=w2[:, ft, :], in_=w2_ap[fc * 512 + ft * 128:fc * 512 + (ft + 1) * 128, :]),
                writes=[w2], dma=f"ld_w2{fc % 2}")

    steps = [(fc, tg) for fc in range(NFC) for tg in range(4)]
    cnt = {"u": 0, "o": 0}

    def stage_u(fc, tg):
        w1 = w1c[fc % 2]
        ut = uT[(fc * 4 + tg) % 2]
        for ft in range(4):
            p = pu[cnt["u"] % 2]
            r = rbuf[cnt["u"] % 2]
            cnt["u"] += 1
            for kc in range(8):
                b.op("pe", lambda e, kc=kc, ft=ft, p=p, w1=w1, tg=tg: e.matmul(
                    p[:], lhsT=w1[:, kc, ft * 128:(ft + 1) * 128], rhs=h2T[:, kc, tg * 512:(tg + 1) * 512],
                    start=(kc == 0), stop=(kc == 7)), reads=[w1, h2T], writes=[p])
            b.op("act", lambda e, p=p, r=r: e.activation(out=r[:], in_=p[:], func=AF.Relu), reads=[p], writes=[r])
            b.op("pool", lambda e, r=r, ut=ut, ft=ft: e.tensor_tensor(out=ut[:, ft, :], in0=r[:], in1=r[:], op=ALU.mult),
                 reads=[r], writes=[ut])

    def stage_o(fc, tg):
        w2 = w2c[fc % 2]
        ut = uT[(fc * 4 + tg) % 2]
        for tt in range(4):
            xr = x_res[tg * 4 + tt]
            for ch in range(2):
                p = po[cnt["o"] % 4]
                cnt["o"] += 1
                for ft in range(4):
                    b.op("pe", lambda e, ft=ft, tt=tt, ch=ch, p=p, ut=ut, w2=w2: e.matmul(
                        p[:], lhsT=ut[:, ft, tt * 128:(tt + 1) * 128], rhs=w2[:, ft, ch * 512:(ch + 1) * 512],
                        start=(ft == 0), stop=(ft == 3)), reads=[ut, w2], writes=[p])
                b.op("dve", lambda e, ch=ch, p=p, xr=xr: e.tensor_tensor(
                    out=xr[:, ch * 512:(ch + 1) * 512], in0=xr[:, ch * 512:(ch + 1) * 512], in1=p[:], op=ALU.add),
                    reads=[xr, p], writes=[xr])

    load_chunk(0)
    stage_u(*steps[0])
    for i, (fc, tg) in enumerate(steps):
        if tg == 0 and fc + 1 < NFC:
            load_chunk(fc + 1)
        if hook is not None and fc == 1 and tg == 1:
            hook()
        if i + 1 < len(steps):
            stage_u(*steps[i + 1])
        stage_o(fc, tg)


IDX_SCALE = (8 ** -0.5) * (64 ** -0.5)


def phase_dsa_proj(b, io, x_res, cos, sin):
    idt = b.ident()
    gmix = b.sb([128, D], F32, "gmix1")
    b.op("sp", lambda e: e.dma_start(out=gmix[:], in_=io["gmix1"][:]), writes=[gmix], dma="ld_gmix1")
    gqk = b.sb([128, 2048], F32, "gqk2")
    b.op("sp", lambda e: e.dma_start(out=gqk[:], in_=io["gqk2"][:]), writes=[gqk], dma="ld_gqk2")
    b.op("pool", lambda e: e.tensor_scalar(out=gqk[:, 0:1024], in0=gqk[:, 0:1024], scalar1=0.125, scalar2=None,
                                           op0=ALU.mult), reads=[gqk], writes=[gqk])
    gcq = b.sb([128, 256], F32, "gcq")
    b.op("sp", lambda e: e.dma_start(out=gcq[:], in_=io["gcq"][:]), writes=[gcq], dma="ld_gcq")
    w = b.sb([128, 8, 2376], BF16, "w_in2")
    load_w_bf16(b, w, io["w_in2"], 8, 2376, "ld_w2in", colchunk=792)
    wuq = b.sb([128, 2, 1024], BF16, "wuq")
    load_w_bf16(b, wuq, io["w_uq"], 2, 1024, "ld_wuq")
    wuqi = b.sb([128, 2, 512], BF16, "wuqi")
    load_w_bf16(b, wuqi, io["w_uqi"], 2, 512, "ld_wuqi")

    junk = b.sb([128, D], BF16, "junk3")
    ss = b.sb([128, 1], F32, "ss3")
    hbf = b.sb([128, D], BF16, "hbf3")
    hT = b.sb([128, 8, 128], BF16, "hT3")
    stage = b.sb([128, 2048], F32, "stage3")
    sq = b.sb([128, 2048], F32, "sq3")
    ssq = b.sb([128, 32], F32, "ssq3")
    tmp = b.sb([128, 4 * 32 * 8], F32, "ropetmp3")
    qkbf = b.sb([128, 2048], BF16, "qkbf3")
    qkT = [b.sb([128, 16, 128], BF16, f"qkT3{i}") for i in range(2)]
    vaug = [b.sb([128, 16, 65], BF16, f"vaug3{i}") for i in range(2)]
    for i in range(2):
        b.op("pool", lambda e, i=i: e.memset(vaug[i][:], 1.0), writes=[vaug[i]])
    cqs = b.sb([128, 256], F32, "cqs")
    ssc = b.sb([128, 1], F32, "ssc")
    cqbf = b.sb([128, 256], BF16, "cqbf")
    cqT = b.sb([128, 2, 128], BF16, "cqT")
    kis = b.sb([128, 64], F32, "kis")
    ksq = b.sb([128, 64], F32, "ksq")
    kss = b.sb([128, 1], F32, "kss")
    kibf = b.sb([128, 128], BF16, "kibf")
    kiT = [b.sb([128, 128], BF16, f"kiT{i}") for i in range(2)]
    wi = b.sb([128, 8], F32, "wi")
    sgn = [b.sb([128, 8], F32, f"sgn{i}") for i in range(2)]
    aw = b.sb([128, 8], F32, "aw")
    qis = b.sb([128, 512], F32, "qis")
    qibf = b.sb([128, 512], BF16, "qibf")
    qiT = [b.sb([128, 4, 128], BF16, f"qiT{i}") for i in range(2)]
    nbank = [0]

    def bank():
        nbank[0] += 1
        return b.banks[2 + nbank[0] % 4]

    def proj(py, lhs, nk, rhs_fn, ncol):
        for kc in range(nk):
            b.op("pe", lambda e, kc=kc: e.matmul(py[:, 0:ncol], lhsT=lhs[:, kc, :], rhs=rhs_fn(kc),
                                                 start=(kc == 0), stop=(kc == nk - 1)), reads=[lhs, w, wuq, wuqi], writes=[py])

    for a in range(NS):
        xr = x_res[a]
        rmsnorm_transpose(b, xr, gmix, hbf, hT, b.banks16[0], ss, junk)
        py = bank()
        proj(py, hT, 8, lambda kc: w[:, kc, 0:256], 256)
        b.op("act", lambda e, py=py: e.activation(out=cqs[:], in_=py[:, 0:256], func=AF.Copy), reads=[py], writes=[cqs])
        b.op("act", lambda e, py=py: e.activation(out=junk[:, 0:256], in_=py[:, 0:256], func=AF.Square, accum_out=ssc[:, 0:1]),
             reads=[py], writes=[junk, ssc])
        rstd_from_ss(b, ssc, 1, 1.0 / 256)
        b.op("dve", lambda e: e.scalar_tensor_tensor(out=cqbf[:], in0=cqs[:], scalar=ssc[:, 0:1], in1=gcq[:],
                                                     op0=ALU.mult, op1=ALU.mult), reads=[cqs, ssc, gcq], writes=[cqbf])
        p6 = b.banks16[6]
        for kc in range(2):
            b.op("pe", lambda e, kc=kc: e.transpose(out=p6[:, kc * 128:(kc + 1) * 128], in_=cqbf[:, kc * 128:(kc + 1) * 128],
                                                    identity=idt[:]), reads=[cqbf, idt], writes=[p6])
        b.op("dve", lambda e: e.tensor_copy(out=cqT[:].rearrange("p a b -> p (a b)"), in_=p6[:, 0:256]),
             reads=[p6], writes=[cqT])
        for n in range(2):
            py = bank()
            proj(py, hT, 8, lambda kc, n=n: w[:, kc, 256 + n * 512:256 + (n + 1) * 512], 512)
            b.op("act", lambda e, py=py, n=n: e.activation(out=stage[:, 1024 + n * 512:1024 + (n + 1) * 512], in_=py[:], func=AF.Copy),
                 reads=[py], writes=[stage])
            b.op("act", lambda e, py=py, n=n: e.activation(out=sq[:, 1024 + n * 512:1024 + (n + 1) * 512], in_=py[:], func=AF.Square),
                 reads=[py], writes=[sq])
        va = vaug[a % 2]
        for n in range(2):
            py = bank()
            proj(py, hT, 8, lambda kc, n=n: w[:, kc, 1280 + n * 512:1280 + (n + 1) * 512], 512)
            b.op("dve", lambda e, py=py, n=n, va=va: e.tensor_copy(out=va[:, n * 8:(n + 1) * 8, 0:64],
                                                                   in_=py[:].rearrange("p (h d) -> p h d", d=64)),
                 reads=[py], writes=[va])
        b.store("sp", lambda e, a=a, va=va: e.dma_start(out=io["V2"][a], in_=va[:].rearrange("p h d -> p (h d)")),
                reads=[va], dma=f"st_v2{a % 2}")
        py = bank()
        proj(py, hT, 8, lambda kc: w[:, kc, 2304:2376], 72)
        b.op("act", lambda e, py=py: e.activation(out=kis[:], in_=py[:, 0:64], func=AF.Copy), reads=[py], writes=[kis])
        b.op("act", lambda e, py=py: e.activation(out=ksq[:], in_=py[:, 0:64], func=AF.Square), reads=[py], writes=[ksq])
        b.op("dve", lambda e, py=py: e.tensor_copy(out=wi[:], in_=py[:, 64:72]), reads=[py], writes=[wi])
        for n in range(2):
            py = bank()
            proj(py, cqT, 2, lambda kc, n=n: wuq[:, kc, n * 512:(n + 1) * 512], 512)
            b.op("act", lambda e, py=py, n=n: e.activation(out=stage[:, n * 512:(n + 1) * 512], in_=py[:], func=AF.Copy),
                 reads=[py], writes=[stage])
            b.op("act", lambda e, py=py, n=n: e.activation(out=sq[:, n * 512:(n + 1) * 512], in_=py[:], func=AF.Square),
                 reads=[py], writes=[sq])
        py = bank()
        proj(py, cqT, 2, lambda kc: wuqi[:, kc, :], 512)
        b.op("act", lambda e, py=py: e.activation(out=qis[:], in_=py[:], func=AF.Copy), reads=[py], writes=[qis])
        headnorm_rope(b, stage, sq, ssq, 32, gqk, cos[:, a, :], sin[:, a, :], qkbf, tmp)
        qt = qkT[a % 2]
        for i in range(16):
            pt = b.banks16[1] if i < 8 else b.banks16[0]
            b.op("pe", lambda e, i=i, pt=pt: e.transpose(out=pt[:, (i % 8) * 128:(i % 8 + 1) * 128],
                                                         in_=qkbf[:, i * 128:(i + 1) * 128], identity=idt[:]),
                 reads=[qkbf, idt], writes=[pt])
            if i % 8 == 7:
                b.op("dve", lambda e, i=i, pt=pt, qt=qt: e.tensor_copy(
                    out=qt[:, (i // 8) * 8:(i // 8) * 8 + 8, :].rearrange("p a b -> p (a b)"), in_=pt[:]),
                    reads=[pt], writes=[qt])
        b.store("sp", lambda e, a=a, qt=qt: e.dma_start(out=io["QT2"][a], in_=qt[:, 0:8, :].rearrange("p a b -> p (a b)")),
                reads=[qt], dma=f"st_q2{a % 2}")
        b.store("sp", lambda e, a=a, qt=qt: e.dma_start(out=io["KT2"][a], in_=qt[:, 8:16, :].rearrange("p a b -> p (a b)")),
                reads=[qt], dma=f"st_k2{a % 2}")
        ki_half = T(kibf.t[:, 0:64], kibf.d)
        headnorm_rope(b, kis, ksq, kss, 1, None, cos[:, a, :], sin[:, a, :], ki_half, tmp)
        b.op("pool", lambda e: e.tensor_copy(out=kibf[:, 64:128], in_=kibf[:, 0:64]), reads=[kibf], writes=[kibf])
        p7 = b.banks16[7]
        b.op("pe", lambda e: e.transpose(out=p7[:, 0:128], in_=kibf[:], identity=idt[:]), reads=[kibf, idt], writes=[p7])
        kt_ = kiT[a % 2]
        b.op("dve", lambda e, kt_=kt_: e.tensor_copy(out=kt_[:], in_=p7[:, 0:128]), reads=[p7], writes=[kt_])
        b.store("sp", lambda e, a=a, kt_=kt_: e.dma_start(out=io["KI"][a], in_=kt_[:]), reads=[kt_], dma=f"st_ki{a % 2}")
        sg = sgn[a % 2]
        b.op("act", lambda e, sg=sg: e.activation(out=sg[:], in_=wi[:], func=AF.Sign), reads=[wi], writes=[sg])
        b.op("dve", lambda e, sg=sg: e.scalar_tensor_tensor(out=aw[:], in0=wi[:], scalar=IDX_SCALE, in1=sg[:],
                                                           op0=ALU.mult, op1=ALU.mult), reads=[wi, sg], writes=[aw])
        b.store("sp", lambda e, a=a, sg=sg: e.dma_start(out=io["SG"][a], in_=sg[:]), reads=[sg], dma=f"st_sg{a % 2}")
        headnorm_rope(b, qis, None, aw, 8, None, cos[:, a, :], sin[:, a, :], qibf, tmp, norm=False)
        qi_ = qiT[a % 2]
        for i in range(4):
            b.op("pe", lambda e, i=i: e.transpose(out=p7[:, 256 + i * 128:256 + (i + 1) * 128],
                                                  in_=qibf[:, i * 128:(i + 1) * 128], identity=idt[:]),
                 reads=[qibf, idt], writes=[p7])
        b.op("dve", lambda e, qi_=qi_: e.tensor_copy(out=qi_[:].rearrange("p a b -> p (a b)"), in_=p7[:, 256:768]),
             reads=[p7], writes=[qi_])
        b.store("sp", lambda e, a=a, qi_=qi_: e.dma_start(out=io["QI"][a], in_=qi_[:].rearrange("p a b -> p (a b)")),
                reads=[qi_], dma=f"st_qi{a % 2}")


NIT = 22
TOPK = 256


def phase_dsa_attn(b, io, o_tok):
    idt = b.ident()
    kia = b.sb([128, 8192], BF16, "kiall")
    for part in range(4):
        b.op("sp", lambda e, part=part: e.dma_start(out=kia[:, part * 2048:(part + 1) * 2048],
                                                    in_=io["KIall"][:, part * 2048:(part + 1) * 2048]),
             writes=[kia], dma="ld_kia")
    negm = b.sb([128, 512], F32, "negm")
    b.op("sp", lambda e: e.dma_start(out=negm[:], in_=io["negmask"][:]), writes=[negm], dma="ld_negm")
    cW = b.sb([128, NIT], F32, "cW")
    for i in range(NIT):
        b.op("pool", lambda e, i=i: e.memset(cW[:, i:i + 1], 2.0 ** (-i)), writes=[cW])
    Ib = b.sb([128, 8192], F32, "Ibuf")
    Mq = b.sb([128, 8192], BF16, "Mq")
    MT = b.sb([128, 64, 128], BF16, "MT")
    ktp = [b.sb([128, 8192], BF16, f"ktp{i}") for i in range(2)]
    vp = [b.sb([128, 64, 130], BF16, f"vp{i}") for i in range(2)]
    qTa = [b.sb([128, 8, 128], BF16, f"qTa{i}") for i in range(2)]
    qiTa = [b.sb([128, 4, 128], BF16, f"qiTa{i}") for i in range(2)]
    sgn = [b.sb([128, 8], F32, f"sgna{i}") for i in range(2)]
    tb = [b.sb([128, 512], F32, f"tb{i}") for i in range(2)]
    pT = [b.sb([128, 512], BF16, f"pTd{i}") for i in range(2)]
    m1 = b.sb([128, 1], F32, "bm1")
    lo = b.sb([128, 1], F32, "blo")
    mid = b.sb([128, 1], F32, "bmid")
    cnt = b.sb([128, 1], F32, "bcnt")
    g = b.sb([128, 1], F32, "bg")
    W = b.sb([128, NIT], F32, "bW")
    rec = [b.sb([128, 1], F32, f"recd{i}") for i in range(2)]
    pS = [b.banks[0], b.banks[1]]
    pA = [b.banks[2], b.banks[3]]
    pI = [b.banks[4], b.banks[5]]
    pM = [b.banks16[6], b.banks16[7]]
    ctr = {"i": 0, "s": 0, "acc": 0, "kv": 0}

    def load_slot_small(a):
        b.op("sp", lambda e, a=a: e.dma_start(out=qTa[a % 2][:].rearrange("p a b -> p (a b)"), in_=io["QT2"][a]),
             writes=[qTa[a % 2]], dma=f"ld_qTa{a % 2}")
        b.op("sp", lambda e, a=a: e.dma_start(out=qiTa[a % 2][:].rearrange("p a b -> p (a b)"), in_=io["QI"][a]),
             writes=[qiTa[a % 2]], dma=f"ld_qiTa{a % 2}")
        b.op("sp", lambda e, a=a: e.dma_start(out=sgn[a % 2][:], in_=io["SG"][a]), writes=[sgn[a % 2]], dma=f"ld_sgn{a % 2}")

    def load_kv(a, hp):
        k = ctr["kv"] % 2
        ctr["kv"] += 1
        nv = 512 * (a + 1)
        nt = 4 * (a + 1)
        b.op("sp", lambda e, k=k, hp=hp, nv=nv: e.dma_start(out=ktp[k][:, 0:nv], in_=io["KT2all"][hp][:, 0:nv]),
             writes=[ktp[k]], dma=f"ld_ktp{k}")
        b.op("sp", lambda e, k=k, hp=hp, nt=nt: e.dma_start(out=vp[k][:, 0:nt, :].rearrange("p a b -> p (a b)"),
                                                            in_=io["V2all"][hp][:, 0:nt * 130]),
             writes=[vp[k]], dma=f"ld_vp{k}")
        return k

    def indexer(a):
        nb = a + 1
        nv = 512 * nb
        qi, sg = qiTa[a % 2], sgn[a % 2]
        for blk in range(nb):
            for head in range(8):
                hp, hh = head // 2, head % 2
                py = pI[ctr["i"] % 2]
                t = tb[ctr["i"] % 2]
                ctr["i"] += 1
                b.op("pe", lambda e, py=py, hp=hp, hh=hh, blk=blk, qi=qi: e.matmul(
                    py[:], lhsT=qi[64 * hh:64 * hh + 64, hp, :], rhs=kia[64 * hh:64 * hh + 64, blk * 512:(blk + 1) * 512],
                    start=True, stop=True), reads=[qi, kia], writes=[py])
                b.op("act", lambda e, py=py, t=t: e.activation(out=t[:], in_=py[:], func=AF.Relu), reads=[py], writes=[t])
                if head == 0:
                    b.op("dve", lambda e, t=t, blk=blk, sg=sg: e.tensor_scalar(
                        out=Ib[:, blk * 512:(blk + 1) * 512], in0=t[:], scalar1=sg[:, 0:1], scalar2=None, op0=ALU.mult),
                        reads=[t, sg], writes=[Ib])
                else:
                    b.op("dve", lambda e, t=t, blk=blk, sg=sg, head=head: e.scalar_tensor_tensor(
                        out=Ib[:, blk * 512:(blk + 1) * 512], in0=t[:], scalar=sg[:, head:head + 1],
                        in1=Ib[:, blk * 512:(blk + 1) * 512], op0=ALU.mult, op1=ALU.add),
                        reads=[t, sg, Ib], writes=[Ib])
        b.op("dve", lambda e: e.tensor_reduce(out=m1[:], in_=Ib[:, 0:nv], axis=AX.X, op=ALU.max, apply_absolute_value=True),
             reads=[Ib], writes=[m1])
        b.op("dve", lambda e: e.tensor_tensor(out=Ib[:, nv - 512:nv], in0=Ib[:, nv - 512:nv], in1=negm[:], op=ALU.add),
             reads=[Ib, negm], writes=[Ib])
        b.op("dve", lambda e: e.tensor_scalar(out=m1[:], in0=m1[:], scalar1=1.0, scalar2=None, op0=ALU.add),
             reads=[m1], writes=[m1])
        b.op("dve", lambda e: e.tensor_scalar(out=lo[:], in0=m1[:], scalar1=-1.0, scalar2=None, op0=ALU.mult),
             reads=[m1], writes=[lo])
        b.op("dve", lambda e: e.tensor_scalar(out=W[:], in0=cW[:], scalar1=m1[:, 0:1], scalar2=None, op0=ALU.mult),
             reads=[cW, m1], writes=[W])
        b.op("dve", lambda e: e.tensor_tensor(out=mid[:], in0=lo[:], in1=W[:, 0:1], op=ALU.add), reads=[lo, W], writes=[mid])
        for i in range(NIT):
            b.op("dve", lambda e: e.tensor_scalar(out=Mq[:, 0:nv], in0=Ib[:, 0:nv], scalar1=mid[:, 0:1], scalar2=0.0,
                                                  op0=ALU.is_ge, op1=ALU.add, accum_out=cnt[:, 0:1]),
                 reads=[Ib, mid], writes=[Mq, cnt])
            b.op("dve", lambda e, i=i: e.tensor_scalar(out=g[:], in0=cnt[:], scalar1=TOPK - 0.5, scalar2=W[:, i:i + 1],
                                                       op0=ALU.is_ge, op1=ALU.mult), reads=[cnt, W], writes=[g])
            b.op("dve", lambda e: e.tensor_tensor(out=lo[:], in0=lo[:], in1=g[:], op=ALU.add), reads=[lo, g], writes=[lo])
            if i + 1 < NIT:
                b.op("dve", lambda e, i=i: e.tensor_tensor(out=mid[:], in0=lo[:], in1=W[:, i + 1:i + 2], op=ALU.add),
                     reads=[lo, W], writes=[mid])
        b.op("dve", lambda e: e.tensor_scalar(out=Mq[:, 0:nv], in0=Ib[:, 0:nv], scalar1=lo[:, 0:1], scalar2=None,
                                              op0=ALU.is_ge), reads=[Ib, lo], writes=[Mq])
        nt = 4 * nb
        for kt in range(nt):
            pm = pM[(kt // 8) % 2]
            b.op("pe", lambda e, kt=kt, pm=pm: e.transpose(out=pm[:, (kt % 8) * 128:(kt % 8 + 1) * 128],
                                                           in_=Mq[:, kt * 128:(kt + 1) * 128], identity=idt[:]),
                 reads=[Mq, idt], writes=[pm])
            if kt % 8 == 7 or kt == nt - 1:
                k0 = (kt // 8) * 8
                n = kt - k0 + 1
                b.op("act", lambda e, pm=pm, k0=k0, n=n: e.activation(
                    out=MT[:, k0:k0 + n, :].rearrange("p a b -> p (a b)"), in_=pm[:, 0:n * 128], func=AF.Copy),
                    reads=[pm], writes=[MT])

    def qk(u, n, kbuf):
        a, hp, hh, blk = u
        ps = pS[n % 2]
        qa = qTa[a % 2]
        for i in range(4):
            kt = 4 * blk + i
            b.op("pe", lambda e, i=i, kt=kt, ps=ps, qa=qa, hh=hh, hp=hp, kbuf=kbuf: e.matmul(
                ps[:, i * 128:(i + 1) * 128], lhsT=ktp[kbuf][64 * hh:64 * hh + 64, kt * 128:(kt + 1) * 128],
                rhs=qa[64 * hh:64 * hh + 64, hp, :], start=True, stop=True), reads=[ktp[kbuf], qa], writes=[ps])

    def softmax_pv(u, n, kbuf):
        a, hp, hh, blk = u
        ps, pt = pS[n % 2], pT[n % 2]
        if blk == 0:
            ctr["acc"] += 1
        acc = pA[ctr["acc"] % 2]
        b.op("act", lambda e: e.activation(out=pt[:], in_=ps[:], func=AF.Exp), reads=[ps], writes=[pt])
        b.op("dve", lambda e: e.tensor_tensor(out=pt[:], in0=pt[:], in1=MT[:, 4 * blk:4 * blk + 4, :].rearrange("p a b -> p (a b)"),
                                              op=ALU.mult), reads=[pt, MT], writes=[pt])
        for i in range(4):
            kt = 4 * blk + i
            b.op("pe", lambda e, i=i, kt=kt: e.matmul(acc[:, 0:65], lhsT=pt[:, i * 128:(i + 1) * 128],
                                                      rhs=vp[kbuf][:, kt, hh * 65:(hh + 1) * 65],
                                                      start=(blk == 0 and i == 0), stop=(blk == a and i == 3)),
                 reads=[pt, vp[kbuf]], writes=[acc])
        if blk == a:
            head = 2 * hp + hh
            r = rec[ctr["acc"] % 2]
            b.op("dve", lambda e: e.reciprocal(out=r[:], in_=acc[:, 64:65]), reads=[acc], writes=[r])
            b.op("dve", lambda e: e.tensor_scalar(out=o_tok[a][:, head * 64:(head + 1) * 64], in0=acc[:, 0:64],
                                                  scalar1=r[:, 0:1], scalar2=None, op0=ALU.mult),
                 reads=[acc, r], writes=[o_tok[a]])

    load_slot_small(0)
    for a in range(NS):
        if a + 1 < NS:
            load_slot_small(a + 1)
        kb_next = load_kv(a, 0)
        indexer(a)
        units = [(a, hp, hh, blk) for hp in range(8) for hh in range(2) for blk in range(a + 1)]
        kbufs = {}
        kbufs[0] = kb_next
        qk(units[0], 0, kbufs[0])
        for n, u in enumerate(units):
            _, hp, hh, blk = u
            if hh == 0 and blk == 0 and hp + 1 < 8:
                kbufs[hp + 1] = load_kv(a, hp + 1)
            if n + 1 < len(units):
                qk(units[n + 1], n + 1, kbufs[units[n + 1][1]])
            softmax_pv(u, n, kbufs[hp])


GROUPS = [[0, 1, 2, 3], [4, 5, 6, 7]]
_RANK = {}


class Gather:
    def __init__(self, b, name, nblk, cols, zt, zero_eng="act"):
        self.b, self.name, self.nblk, self.cols = b, name, nblk, cols
        nc = b.nc
        self.xb = nc.dram_tensor(name + "_xb", [nblk, 512, cols], BF16).ap()
        self.yb = nc.dram_tensor(name + "_yb", [nblk, 512, cols], BF16).ap()
        self.xo = nc.dram_tensor(name + "_xo", [nblk, 128, cols], BF16).ap()
        self.od = [T(self.xo[h]) for h in range(nblk)]
        self.xd = [T(self.xb[h]) for h in range(nblk)]
        self.yd = [T(self.yb[h]) for h in range(nblk)]
        self.zlist = [(h, r, c0) for h in range(nblk) for r in range(4) for c0 in range(0, cols, 2048)]
        self.zt = zt
        if zero_eng is not None:
            self.zero(zero_eng, len(self.zlist))

    def zero(self, eng, n):
        b, zt = self.b, self.zt
        for _ in range(min(n, len(self.zlist))):
            h, r, c0 = self.zlist.pop(0)
            b.op(eng, lambda e, h=h, r=r, c0=c0: e.dma_start(
                out=self.xb[h, r * 128:(r + 1) * 128, c0:c0 + 2048], in_=zt[:, 0:2048]),
                reads=[zt], writes=[self.xd[h]], dma="zero_" + self.name)

    def put(self, h, c0, c1, src_ap, reads, key):
        self.b.op("sp", lambda e: e.dma_start(out=self.xo[h, :, c0:c1], in_=src_ap), reads=reads,
                  writes=[self.od[h]], dma=key)

    def place(self, h, eng):
        def fn(e):
            if eng not in _RANK:
                _RANK[eng] = e.partition_id() % 4
            r = _RANK[eng]
            return e.dma_start(out=self.xb[h, bass.ds(r * 128, 128), :], in_=self.xo[h])
        self.b.op(eng, fn, reads=[self.od[h]], writes=[self.xd[h]], dma="place_" + self.name)

    def reduce(self, h):
        b = self.b
        b.op("pool", lambda e: e.collective_compute("AllReduce", ALU.add, replica_groups=GROUPS,
                                                    ins=[self.xb[h]], outs=[self.yb[h]]),
             reads=[self.xd[h]], writes=[self.yd[h]], dma=f"cc_{self.name}{h}", sem_inc=1)


def f_diff_proj(b, io, qT_res, G0, cos, sin, wstage=None):
    idt = b.ident()
    gmix = b.sb([128, D], F32, "gmix")
    b.op("sp", lambda e: e.dma_start(out=gmix[:], in_=io["gmix"][:]), writes=[gmix], dma="ld_gmix")
    gqk = b.sb([128, 2048], F32, "gqk")
    b.op("sp", lambda e: e.dma_start(out=gqk[:], in_=io["gqk"][:]), writes=[gqk], dma="ld_gqk")
    b.op("pool", lambda e: e.tensor_scalar(out=gqk[:, 0:1024], in0=gqk[:, 0:1024], scalar1=0.125, scalar2=None,
                                           op0=ALU.mult), reads=[gqk], writes=[gqk])
    xts = [b.sb([128, D], F32, f"xt{i}") for i in range(2)]

    def load_x(a):
        xt = xts[a % 2]
        b.op("sp", lambda e: e.dma_start(out=xt[:], in_=io["x"][a * 128:(a + 1) * 128, :]), writes=[xt], dma=f"ld_x{a % 2}")
    load_x(0)
    w = b.sb([128, 8, 3072], BF16, "w_in")
    if wstage is None:
        load_w_bf16(b, w, io["w_in"], 8, 3072, "ld_w")
    else:
        for n in range(6):
            stg = wstage[n % 2]
            b.op("sp", lambda e, n=n, stg=stg: e.dma_start(
                out=stg[:], in_=io["w_in"][:, n * 512:(n + 1) * 512].rearrange("(k p) c -> p k c", p=128)),
                writes=[stg], dma=f"ld_wst{n % 2}")
            b.op("act" if n % 2 == 0 else "dve",
                 (lambda e, n=n, stg=stg: e.activation(out=w[:, :, n * 512:(n + 1) * 512], in_=stg[:], func=AF.Copy)) if n % 2 == 0
                 else (lambda e, n=n, stg=stg: e.tensor_copy(out=w[:, :, n * 512:(n + 1) * 512], in_=stg[:])),
                 reads=[stg], writes=[w])
    junk_ = [b.sb([128, D], BF16, f"junk{i}") for i in range(2)]
    ss_ = [b.sb([128, 1], F32, f"ss{i}") for i in range(2)]
    hbf_ = [b.sb([128, D], BF16, f"hbf{i}") for i in range(2)]
    hT_ = [b.sb([128, 8, 128], BF16, f"hT{i}") for i in range(2)]
    pT = [b.banks16[0], b.banks16[1]]
    pY = [b.banks[2 + i] for i in range(4)]
    stage_ = [b.sb([128, 2048], F32, f"stage{i}") for i in range(2)]
    sq_ = [b.sb([128, 2048], BF16, f"sq{i}") for i in range(2)]
    ssq_ = [b.sb([128, 32], F32, f"ssq{i}") for i in range(2)]
    tmp1 = b.sb([128, 4 * 32 * 8], F32, "ropetmp"); tmp_ = [tmp1, tmp1]
    qkbf_ = [b.sb([128, 2048], BF16, f"qkbf{i}") for i in range(2)]
    kT = [b.sb([128, 8, 128], BF16, f"kTst{i}") for i in range(2)]
    vst = [b.sb([128, 8, 128], BF16, f"vst{i}") for i in range(2)]

    def body(a, junk, ss, hbf, hT, stage, sq, ssq, tmp, qkbf):
        xt = xts[a % 2]
        if a + 1 < NS:
            load_x(a + 1)
        rmsnorm_transpose(b, xt, gmix, hbf, hT, pT[0], ss, junk)
        va = vst[a % 2]
        for n in range(6):
            py = pY[n % 4]
            for kc in range(8):
                b.op("pe", lambda e, n=n, kc=kc, py=py: e.matmul(py[:], lhsT=hT[:, kc, :],
                                                                   rhs=w[:, kc, n * 512:(n + 1) * 512],
                                                                   start=(kc == 0), stop=(kc == 7)),
                     reads=[hT, w], writes=[py])
            if n < 4:
                b.op("act", lambda e, n=n, py=py: e.activation(out=stage[:, n * 512:(n + 1) * 512], in_=py[:], func=AF.Copy),
                     reads=[py], writes=[stage])
                b.op("act", lambda e, n=n, py=py: e.activation(out=sq[:, n * 512:(n + 1) * 512], in_=py[:], func=AF.Square),
                     reads=[py], writes=[sq])
            else:
                b.op("dve", lambda e, n=n, py=py, va=va: e.tensor_copy(
                    out=va[:, (n - 4) * 4:(n - 4) * 4 + 4, :], in_=py[:].rearrange("p (h d) -> p h d", d=128)),
                    reads=[py], writes=[va])
        for h in range(8):
            G0.put(h, 2048 + a * 128, 2048 + (a + 1) * 128, va[:, h, :], [va], f"st_v{a % 2}")

    def tail(a, junk, ss, hbf, hT, stage, sq, ssq, tmp, qkbf):
        headnorm_rope(b, stage, sq, ssq, 32, gqk, cos[:, a, :], sin[:, a, :], qkbf, tmp)
        kt = kT[a % 2]
        for i in range(16):
            pt = b.banks16[6] if i < 8 else b.banks16[7]
            b.op("pe", lambda e, i=i, pt=pt: e.transpose(out=pt[:, (i % 8) * 128:(i % 8 + 1) * 128],
                                                         in_=qkbf[:, i * 128:(i + 1) * 128], identity=idt[:]),
                 reads=[qkbf, idt], writes=[pt])
            if i == 7:
                b.op("dve", lambda e, pt=pt, a=a: e.tensor_copy(out=qT_res[:, a, :], in_=pt[:]), reads=[pt], writes=[qT_res])
            if i == 15:
                b.op("dve", lambda e, pt=pt, kt=kt: e.tensor_copy(out=kt[:].rearrange("p a b -> p (a b)"), in_=pt[:]),
                     reads=[pt], writes=[kt])
        for h in range(8):
            G0.put(h, a * 128, (a + 1) * 128, kt[:, h, :], [kt], f"st_k{a % 2}")
        G0.zero("act", 4)
    def args(a):
        return [z[a % 2] for z in (junk_, ss_, hbf_, hT_, stage_, sq_, ssq_, tmp_, qkbf_)]
    import os
    if os.environ.get("PIPE1", "0") == "1":
        body(0, *args(0))
        for a in range(NS):
            if a + 1 < NS:
                body(a + 1, *args(a + 1))
            tail(a, *args(a))
    else:
        for a in range(NS):
            body(a, *args(a))
            tail(a, *args(a))
    G0.zero("act", 1000)
    for h in range(8):
        G0.place(h, "sp")
        G0.reduce(h)


def f_diff_attn(b, io, o_tok, qT, G0):
    LAM_INIT = 0.2
    maskT = b.sb([128, 512], BF16, "maskT")
    b.op("sp", lambda e: e.dma_start(out=maskT[:], in_=io["maskT"][:]), writes=[maskT], dma="ld_mask")
    lam = b.sb([128, 256], F32, "lam")
    b.op("sp", lambda e: e.dma_start(out=lam[:], in_=io["lam"][:]), writes=[lam], dma="ld_lam")
    gsub = b.sb([128, 128], F32, "gsub")
    b.op("sp", lambda e: e.dma_start(out=gsub[:], in_=io["gsub"][:]), writes=[gsub], dma="ld_gsub")
    b.op("dve", lambda e: e.tensor_scalar(out=gsub[:], in0=gsub[:], scalar1=1.0 - LAM_INIT, scalar2=None,
                                          op0=ALU.mult), reads=[gsub], writes=[gsub])
    lprod = b.sb([128, 128], F32, "lprod")
    l2 = b.sb([128, 2], F32, "l2")
    neglam = b.sb([128, 1], F32, "neglam")
    l4 = lam[:].rearrange("p (a b d) -> p a b d", a=2, b=2)
    b.op("dve", lambda e: e.tensor_tensor(out=lprod[:].rearrange("p (a d) -> p a d", a=2), in0=l4[:, :, 0, :],
                                          in1=l4[:, :, 1, :], op=ALU.mult), reads=[lam], writes=[lprod])
    b.op("dve", lambda e: e.tensor_reduce(out=l2[:], in_=lprod[:].rearrange("p (a d) -> p a d", a=2), axis=AX.X,
                                          op=ALU.add), reads=[lprod], writes=[l2])
    b.op("act", lambda e: e.activation(out=l2[:], in_=l2[:], func=AF.Exp), reads=[l2], writes=[l2])
    b.op("dve", lambda e: e.tensor_tensor(out=neglam[:], in0=l2[:, 1:2], in1=l2[:, 0:1], op=ALU.subtract),
         reads=[l2], writes=[neglam])
    b.op("dve", lambda e: e.tensor_scalar(out=neglam[:], in0=neglam[:], scalar1=-LAM_INIT, scalar2=None, op0=ALU.add),
         reads=[neglam], writes=[neglam])

    ktb = [b.sb([128, 8192], BF16, f"ktb{i}") for i in range(2)]
    vb = [b.sb([128, 64, 129], BF16, f"vb{i}") for i in range(2)]
    for i in range(2):
        b.op("dve", lambda e, i=i: e.memset(vb[i][:, :, 128:129], 1.0), writes=[vb[i]])
    pS = [[b.banks[c * 2 + i] for i in range(2)] for c in range(2)]
    pA = [[b.banks[4 + c * 2 + i] for i in range(2)] for c in range(2)]
    pT = [[b.sb([128, 512], BF16, f"pTs{c}{i}") for i in range(2)] for c in range(2)]
    rec = [b.sb([128, 2], F32, f"rec{i}") for i in range(2)]
    o32 = [b.sb([128, 128], F32, f"o32{i}") for i in range(2)]
    oj = b.sb([128, 128], BF16, "ojunk")
    ss1 = [b.sb([128, 1], F32, f"ss1{i}") for i in range(2)]
    units = [(h, a, blk) for h in range(8) for a in range(NS) for blk in range(a + 1)]

    def load_head(h):
        kb, vv = ktb[h % 2], vb[h % 2]
        yb = G0.yb[h]
        for r in range(4):
            b.op("sp", lambda e, kb=kb, r=r, yb=yb: e.dma_start(out=kb[:, r * 2048:(r + 1) * 2048],
                                                               in_=yb[r * 128:(r + 1) * 128, 0:2048]),
                 reads=[G0.yd[h]], writes=[kb], dma=f"ld_kt{h % 2}")
            b.op("sp", lambda e, vv=vv, r=r, yb=yb: e.dma_start(
                out=vv[:, r * 16:(r + 1) * 16, 0:128],
                in_=yb[r * 128:(r + 1) * 128, 2048:4096].rearrange("p (a e) -> p a e", e=128)),
                reads=[G0.yd[h]], writes=[vv], dma=f"ld_v{h % 2}")

    def qk(u, n):
        h, a, blk = u
        kb = ktb[h % 2]
        for i in range(4):
            for c in range(2):
                ps = pS[c][n % 2]
                kt = 16 * i + blk
                b.op("pe", lambda e, c=c, i=i, kt=kt, ps=ps, kb=kb, a=a, h=h: e.matmul(
                    ps[:, i * 128:(i + 1) * 128], lhsT=kb[64 * c:64 * c + 64, kt * 128:(kt + 1) * 128],
                    rhs=qT[64 * c:64 * c + 64, a, h * 128:(h + 1) * 128], start=True, stop=True),
                    reads=[kb, qT], writes=[ps])

    def evac(h, a):
        k = a % 2
        a0, a1 = pA[0][k], pA[1][k]
        r, o, s = rec[k], o32[k], ss1[k]
        b.op("dve", lambda e: e.reciprocal(out=r[:, 0:1], in_=a0[:, 128:129]), reads=[a0], writes=[r])
        b.op("dve", lambda e: e.reciprocal(out=r[:, 1:2], in_=a1[:, 128:129]), reads=[a1], writes=[r])
        b.op("dve", lambda e: e.tensor_tensor(out=r[:, 1:2], in0=r[:, 1:2], in1=neglam[:], op=ALU.mult),
             reads=[r, neglam], writes=[r])
        b.op("dve", lambda e: e.tensor_scalar(out=o[:], in0=a0[:, 0:128], scalar1=r[:, 0:1], scalar2=None, op0=ALU.mult),
             reads=[a0, r], writes=[o])
        b.op("dve", lambda e: e.scalar_tensor_tensor(out=o[:], in0=a1[:, 0:128], scalar=r[:, 1:2], in1=o[:],
                                                     op0=ALU.mult, op1=ALU.add), reads=[a1, r, o], writes=[o])
        b.op("act", lambda e: e.activation(out=oj[:], in_=o[:], func=AF.Square, accum_out=s[:, 0:1]),
             reads=[o], writes=[oj, s])
        rstd_from_ss(b, s, 1, 1.0 / 128)
        ot = o_tok[a]
        b.op("dve", lambda e: e.scalar_tensor_tensor(out=ot[:, h * 128:(h + 1) * 128], in0=o[:], scalar=s[:, 0:1],
                                                     in1=gsub[:], op0=ALU.mult, op1=ALU.mult),
             reads=[o, s, gsub], writes=[ot])

    def softmax_pv(u, n):
        h, a, blk = u
        vv = vb[h % 2]
        for c in range(2):
            ps = pS[c][n % 2]
            pt = pT[c][n % 2]
            acc = pA[c][a % 2]
            b.op("act", lambda e, ps=ps, pt=pt: e.activation(out=pt[:], in_=ps[:], func=AF.Exp),
                 reads=[ps], writes=[pt])
            if blk == a:
                b.op("dve", lambda e, pt=pt: e.tensor_tensor(out=pt[:], in0=pt[:], in1=maskT[:], op=ALU.mult),
                     reads=[pt, maskT], writes=[pt])
            for i in range(4):
                kt = 16 * i + blk
                b.op("pe", lambda e, i=i, kt=kt, pt=pt, acc=acc, vv=vv, blk=blk, a=a: e.matmul(
                    acc[:, 0:129], lhsT=pt[:, i * 128:(i + 1) * 128], rhs=vv[:, kt, :],
                    start=(blk == 0 and i == 0), stop=(blk == a and i == 3)),
                    reads=[pt, vv], writes=[acc])
        if blk == a:
            evac(h, a)

    load_head(0)
    qk(units[0], 0)
    for n, u in enumerate(units):
        if u[1] == 0 and u[2] == 0 and u[0] + 1 < 8:
            load_head(u[0] + 1)
        if n + 1 < len(units):
            qk(units[n + 1], n + 1)
        softmax_pv(u, n)


def dsa_proj_weights(b, io, top=False):
    w = b.sb([128, 8, 2376], BF16, "w_in2", top=top)
    wuq = b.sb([128, 2, 1024], BF16, "wuq", top=top)
    wuqi = b.sb([128, 2, 512], BF16, "wuqi", top=top)

    def load():
        load_w_bf16(b, w, io["w_in2"], 8, 2376, "ld_w2in", colchunk=792)
        load_w_bf16(b, wuq, io["w_uq"], 2, 1024, "ld_wuq")
        load_w_bf16(b, wuqi, io["w_uqi"], 2, 512, "ld_wuqi")
    return {"w": w, "wuq": wuq, "wuqi": wuqi, "load": load}


def f_dsa_proj(b, io, x_res, cos, sin, G1, GK, scr, pre=None):
    idt = b.ident()
    gmix = b.sb([128, D], F32, "gmix1")
    b.op("sp", lambda e: e.dma_start(out=gmix[:], in_=io["gmix1"][:]), writes=[gmix], dma="ld_gmix1")
    gqk = b.sb([128, 2048], F32, "gqk2")
    b.op("sp", lambda e: e.dma_start(out=gqk[:], in_=io["gqk2"][:]), writes=[gqk], dma="ld_gqk2")
    b.op("pool", lambda e: e.tensor_scalar(out=gqk[:, 0:1024], in0=gqk[:, 0:1024], scalar1=0.125, scalar2=None,
                                           op0=ALU.mult), reads=[gqk], writes=[gqk])
    gcq = b.sb([128, 256], F32, "gcq")
    b.op("sp", lambda e: e.dma_start(out=gcq[:], in_=io["gcq"][:]), writes=[gcq], dma="ld_gcq")
    if pre is None:
        pre = dsa_proj_weights(b, io)
        pre["load"]()
    w, wuq, wuqi = pre["w"], pre["wuq"], pre["wuqi"]
    junk_ = [b.sb([128, D], BF16, f"junk3{i}") for i in range(2)]
    ss_ = [b.sb([128, 1], F32, f"ss3{i}") for i in range(2)]
    hbf_ = [b.sb([128, D], BF16, f"hbf3{i}") for i in range(2)]
    hT_ = [b.sb([128, 8, 128], BF16, f"hT3{i}") for i in range(2)]
    stage_ = [b.sb([128, 2048], F32, f"stage3{i}") for i in range(2)]
    sq_ = [b.sb([128, 2048], BF16, f"sq3{i}") for i in range(2)]
    ssq_ = [b.sb([128, 32], F32, f"ssq3{i}") for i in range(2)]
    tmp1 = b.sb([128, 4 * 32 * 8], F32, "ropetmp3"); tmp_ = [tmp1, tmp1]
    qkbf_ = [b.sb([128, 2048], BF16, f"qkbf3{i}") for i in range(2)]
    qkT = [b.sb([128, 16, 128], BF16, f"qkT3{i}") for i in range(2)]
    vst = [b.sb([128, 16, 64], BF16, f"vst3{i}") for i in range(2)]
    cqs_ = [b.sb([128, 256], F32, f"cqs{i}") for i in range(2)]
    ssc_ = [b.sb([128, 1], F32, f"ssc{i}") for i in range(2)]
    cqbf_ = [b.sb([128, 256], BF16, f"cqbf{i}") for i in range(2)]
    cqT_ = [b.sb([128, 2, 128], BF16, f"cqT{i}") for i in range(2)]
    kis_ = [b.sb([128, 64], F32, f"kis{i}") for i in range(2)]
    ksq_ = [b.sb([128, 64], F32, f"ksq{i}") for i in range(2)]
    kss_ = [b.sb([128, 1], F32, f"kss{i}") for i in range(2)]
    kibf_ = [b.sb([128, 128], BF16, f"kibf{i}") for i in range(2)]
    kiT = [b.sb([128, 128], BF16, f"kiT{i}") for i in range(2)]
    wi_ = [b.sb([128, 8], F32, f"wi{i}") for i in range(2)]
    sgn = [b.sb([128, 8], F32, f"sgn{i}") for i in range(2)]
    aw_ = [b.sb([128, 8], F32, f"aw{i}") for i in range(2)]
    qis_ = [b.sb([128, 512], F32, f"qis{i}") for i in range(2)]
    qibf_ = [b.sb([128, 512], BF16, f"qibf{i}") for i in range(2)]
    qiT = [b.sb([128, 4, 128], BF16, f"qiT{i}") for i in range(2)]
    nbank = [0]

    def bank():
        nbank[0] += 1
        return b.banks[2 + nbank[0] % 4]

    def proj(py, lhs, nk, rhs_fn, ncol):
        for kc in range(nk):
            b.op("pe", lambda e, kc=kc: e.matmul(py[:, 0:ncol], lhsT=lhs[:, kc, :], rhs=rhs_fn(kc),
                                                 start=(kc == 0), stop=(kc == nk - 1)), reads=[lhs, w, wuq, wuqi], writes=[py])

    def body2(a, junk, ss, hbf, hT, stage, sq, ssq, tmp, qkbf, cqs, ssc, cqbf, cqT, kis, ksq, kss, kibf, wi, aw, qis, qibf):
        xr = x_res[a]
        rmsnorm_transpose(b, xr, gmix, hbf, hT, b.banks16[0], ss, junk)
        py = bank()
        proj(py, hT, 8, lambda kc: w[:, kc, 0:256], 256)
        b.op("act", lambda e, py=py: e.activation(out=cqs[:], in_=py[:, 0:256], func=AF.Copy), reads=[py], writes=[cqs])
        b.op("act", lambda e, py=py: e.activation(out=junk[:, 0:256], in_=py[:, 0:256], func=AF.Square, accum_out=ssc[:, 0:1]),
             reads=[py], writes=[junk, ssc])
        rstd_from_ss(b, ssc, 1, 1.0 / 256)
        b.op("dve", lambda e: e.scalar_tensor_tensor(out=cqbf[:], in0=cqs[:], scalar=ssc[:, 0:1], in1=gcq[:],
                                                     op0=ALU.mult, op1=ALU.mult), reads=[cqs, ssc, gcq], writes=[cqbf])
        p6 = b.banks16[6]
        for kc in range(2):
            b.op("pe", lambda e, kc=kc: e.transpose(out=p6[:, kc * 128:(kc + 1) * 128], in_=cqbf[:, kc * 128:(kc + 1) * 128],
                                                    identity=idt[:]), reads=[cqbf, idt], writes=[p6])
        b.op("dve", lambda e: e.tensor_copy(out=cqT[:].rearrange("p a b -> p (a b)"), in_=p6[:, 0:256]),
             reads=[p6], writes=[cqT])
        for n in range(2):
            py = bank()
            proj(py, hT, 8, lambda kc, n=n: w[:, kc, 256 + n * 512:256 + (n + 1) * 512], 512)
            b.op("act", lambda e, py=py, n=n: e.activation(out=stage[:, 1024 + n * 512:1024 + (n + 1) * 512], in_=py[:], func=AF.Copy),
                 reads=[py], writes=[stage])
            b.op("act", lambda e, py=py, n=n: e.activation(out=sq[:, 1024 + n * 512:1024 + (n + 1) * 512], in_=py[:], func=AF.Square),
                 reads=[py], writes=[sq])
        va = vst[a % 2]
        for n in range(2):
            py = bank()
            proj(py, hT, 8, lambda kc, n=n: w[:, kc, 1280 + n * 512:1280 + (n + 1) * 512], 512)
            b.op("dve", lambda e, py=py, n=n, va=va: e.tensor_copy(out=va[:, n * 8:(n + 1) * 8, :],
                                                                   in_=py[:].rearrange("p (h d) -> p h d", d=64)),
                 reads=[py], writes=[va])
        for hp in range(8):
            G1.put(hp, 2048 + a * 128, 2048 + (a + 1) * 128, va[:, 2 * hp:2 * hp + 2, :].rearrange("p h d -> p (h d)"),
                   [va], f"st_v2{a % 2}")
        py = bank()
        proj(py, hT, 8, lambda kc: w[:, kc, 2304:2376], 72)
        b.op("act", lambda e, py=py: e.activation(out=kis[:], in_=py[:, 0:64], func=AF.Copy), reads=[py], writes=[kis])
        b.op("act", lambda e, py=py: e.activation(out=ksq[:], in_=py[:, 0:64], func=AF.Square), reads=[py], writes=[ksq])
        b.op("dve", lambda e, py=py: e.tensor_copy(out=wi[:], in_=py[:, 64:72]), reads=[py], writes=[wi])
        for n in range(2):
            py = bank()
            proj(py, cqT, 2, lambda kc, n=n: wuq[:, kc, n * 512:(n + 1) * 512], 512)
            b.op("act", lambda e, py=py, n=n: e.activation(out=stage[:, n * 512:(n + 1) * 512], in_=py[:], func=AF.Copy),
                 reads=[py], writes=[stage])
            b.op("act", lambda e, py=py, n=n: e.activation(out=sq[:, n * 512:(n + 1) * 512], in_=py[:], func=AF.Square),
                 reads=[py], writes=[sq])
        py = bank()
        proj(py, cqT, 2, lambda kc: wuqi[:, kc, :], 512)
        b.op("act", lambda e, py=py: e.activation(out=qis[:], in_=py[:], func=AF.Copy), reads=[py], writes=[qis])

    def tail2(a, junk, ss, hbf, hT, stage, sq, ssq, tmp, qkbf, cqs, ssc, cqbf, cqT, kis, ksq, kss, kibf, wi, aw, qis, qibf):
        headnorm_rope(b, stage, sq, ssq, 32, gqk, cos[:, a, :], sin[:, a, :], qkbf, tmp)
        qt = qkT[a % 2]
        for i in range(16):
            pt = b.banks16[1] if i < 8 else b.banks16[7]
            b.op("pe", lambda e, i=i, pt=pt: e.transpose(out=pt[:, (i % 8) * 128:(i % 8 + 1) * 128],
                                                         in_=qkbf[:, i * 128:(i + 1) * 128], identity=idt[:]),
                 reads=[qkbf, idt], writes=[pt])
            if i % 8 == 7:
                b.op("dve", lambda e, i=i, pt=pt, qt=qt: e.tensor_copy(
                    out=qt[:, (i // 8) * 8:(i // 8) * 8 + 8, :].rearrange("p a b -> p (a b)"), in_=pt[:]),
                    reads=[pt], writes=[qt])
        b.op("sp", lambda e, a=a, qt=qt: e.dma_start(out=scr["QT2"][a][:], in_=qt[:, 0:8, :].rearrange("p a b -> p (a b)")),
             reads=[qt], writes=[scr["QT2"][a]], dma=f"st_q2{a % 2}")
        for hp in range(8):
            G1.put(hp, a * 128, (a + 1) * 128, qt[:, 8 + hp, :], [qt], f"st_k2{a % 2}")
        ki_half = T(kibf.t[:, 0:64], kibf.d)
        headnorm_rope(b, kis, ksq, kss, 1, None, cos[:, a, :], sin[:, a, :], ki_half, tmp)
        b.op("pool", lambda e: e.tensor_copy(out=kibf[:, 64:128], in_=kibf[:, 0:64]), reads=[kibf], writes=[kibf])
        p7 = b.banks16[7]
        b.op("pe", lambda e: e.transpose(out=p7[:, 0:128], in_=kibf[:], identity=idt[:]), reads=[kibf, idt], writes=[p7])
        kt_ = kiT[a % 2]
        b.op("dve", lambda e, kt_=kt_: e.tensor_copy(out=kt_[:], in_=p7[:, 0:128]), reads=[p7], writes=[kt_])
        GK.put(0, a * 128, (a + 1) * 128, kt_[:], [kt_], f"st_ki{a % 2}")
        sg = sgn[a % 2]
        b.op("act", lambda e, sg=sg: e.activation(out=sg[:], in_=wi[:], func=AF.Sign), reads=[wi], writes=[sg])
        b.op("dve", lambda e, sg=sg: e.scalar_tensor_tensor(out=aw[:], in0=wi[:], scalar=IDX_SCALE, in1=sg[:],
                                                           op0=ALU.mult, op1=ALU.mult), reads=[wi, sg], writes=[aw])
        b.op("sp", lambda e, a=a, sg=sg: e.dma_start(out=scr["SG"][a][:], in_=sg[:]), reads=[sg], writes=[scr["SG"][a]],
             dma=f"st_sg{a % 2}")
        headnorm_rope(b, qis, None, aw, 8, None, cos[:, a, :], sin[:, a, :], qibf, tmp, norm=False)
        qi_ = qiT[a % 2]
        for i in range(4):
            b.op("pe", lambda e, i=i: e.transpose(out=p7[:, 256 + i * 128:256 + (i + 1) * 128],
                                                  in_=qibf[:, i * 128:(i + 1) * 128], identity=idt[:]),
                 reads=[qibf, idt], writes=[p7])
        b.op("dve", lambda e, qi_=qi_: e.tensor_copy(out=qi_[:].rearrange("p a b -> p (a b)"), in_=p7[:, 256:768]),
             reads=[p7], writes=[qi_])
        b.op("sp", lambda e, a=a, qi_=qi_: e.dma_start(out=scr["QI"][a][:], in_=qi_[:].rearrange("p a b -> p (a b)")),
             reads=[qi_], writes=[scr["QI"][a]], dma=f"st_qi{a % 2}")
    def args2(a):
        return [z[a % 2] for z in (junk_, ss_, hbf_, hT_, stage_, sq_, ssq_, tmp_, qkbf_, cqs_, ssc_, cqbf_, cqT_, kis_, ksq_, kss_, kibf_, wi_, aw_, qis_, qibf_)]
    import os
    if os.environ.get("PIPE2", "0") == "1":
        body2(0, *args2(0))
        for a in range(NS):
            if a + 1 < NS:
                body2(a + 1, *args2(a + 1))
            tail2(a, *args2(a))
    else:
        for a in range(NS):
            body2(a, *args2(a))
            tail2(a, *args2(a))
    GK.place(0, "act")
    GK.reduce(0)
    for hp in range(8):
        G1.place(hp, "act")
        G1.reduce(hp)


NIT2 = 14


def f_dsa_attn(b, io, o_tok, G1, GK, scr):
    idt = b.ident()
    kia = b.sb([128, 8192], BF16, "kiall")
    for r in range(4):
        b.op("sp", lambda e, r=r: e.dma_start(out=kia[:, r * 2048:(r + 1) * 2048], in_=GK.yb[0][r * 128:(r + 1) * 128, :]),
             reads=[GK.yd[0]], writes=[kia], dma="ld_kia")
    kia4 = kia[:].rearrange("p (r a t) -> p r a t", r=4, a=16)
    negm = b.sb([128, 512], F32, "negm")
    b.op("sp", lambda e: e.dma_start(out=negm[:], in_=io["negmask"][:]), writes=[negm], dma="ld_negm")
    cW = b.sb([128, NIT2], F32, "cW")
    for i in range(NIT2):
        b.op("dve", lambda e, i=i: e.memset(cW[:, i:i + 1], 2.0 ** (-i)), writes=[cW])
    THR = b.sb([128, NS], F32, "THR")
    mA = b.mark()
    NSET = 3
    qiTa = [b.sb([128, 4, 128], BF16, f"qiTa{i}") for i in range(NSET)]
    sgn = [b.sb([128, 8], F32, f"sgna{i}") for i in range(NSET)]
    Dg = [b.sb([128, 8, 128], BF16, f"Dg{i}") for i in range(NSET)]
    tb = [[b.sb([128, 512], BF16, f"tb{s}{i}") for i in range(3)] for s in range(NSET)]
    pIs = [[b.banks[4], b.banks[6]], [b.banks[5], b.banks[0]], [b.banks[2], b.banks[2]]]
    pIas = [b.banks[7], b.banks[1], b.banks[3]]
    ctr = {"i": 0, "acc": 0, "kv": 0}

    def load_idx_small(a, s):
        b.op("sp", lambda e, a=a: e.dma_start(out=qiTa[s][:].rearrange("p a b -> p (a b)"), in_=scr["QI"][a][:]),
             reads=[scr["QI"][a]], writes=[qiTa[s]], dma=f"ld_qiTa{s}")
        b.op("sp", lambda e, a=a: e.dma_start(out=sgn[s][:], in_=scr["SG"][a][:]), reads=[scr["SG"][a]],
             writes=[sgn[s]], dma=f"ld_sgn{s}")

    def indexer_list(a, s, Ib_):
        L = []

        def add(eng, fn, reads=(), writes=()):
            L.append((eng, fn, reads, writes))
        nb = a + 1
        qi, sg, dg = qiTa[s], sgn[s], Dg[s]
        pys, pia = pIs[s], pIas[s]
        skew = pys[0] is not pys[1]
        add("dve", lambda e: e.tensor_tensor(out=dg[:], in0=idt[:].unsqueeze(1).to_broadcast([128, 8, 128]),
                                             in1=sg[:].unsqueeze(2).to_broadcast([128, 8, 128]), op=ALU.mult),
            [idt, sg], [dg])
        steps = [(blk, head) for blk in range(nb) for head in range(8)]
        S_ = len(steps)
        evq = []
        for k in range(S_ + 2):
            if k < S_:
                blk, head = steps[k]
                hp, hh = head // 2, head % 2
                py = pys[k % 2]
                add("pe", lambda e, hp=hp, hh=hh, blk=blk, py=py: e.matmul(
                    py[:].rearrange("p (r t) -> p r t", r=4), lhsT=qi[64 * hh:64 * hh + 64, hp, :],
                    rhs=kia4[64 * hh:64 * hh + 64, :, blk, :], start=True, stop=True), [qi, kia], [py])
            kr = k - 1 if skew else k
            if 0 <= kr < S_:
                py = pys[kr % 2]
                t = tb[s][kr % 3]
                add("act", lambda e, t=t, py=py: e.activation(out=t[:], in_=py[:], func=AF.Relu), [py], [t])
            if 0 <= k - 2 < S_:
                blk, head = steps[k - 2]
                t = tb[s][(k - 2) % 3]
                add("pe", lambda e, head=head, t=t: e.matmul(pia[:], lhsT=dg[:, head, :], rhs=t[:],
                                                             start=(head == 0), stop=(head == 7)), [dg, t], [pia])
                if head == 7:
                    add("act", lambda e, eb=blk: e.activation(out=Ib_[:, eb * 512:(eb + 1) * 512], in_=pia[:], func=AF.Copy),
                        [pia], [Ib_])
        for _, eb in evq:
            add("act", lambda e, eb=eb: e.activation(out=Ib_[:, eb * 512:(eb + 1) * 512], in_=pia[:], func=AF.Copy),
                [pia], [Ib_])
        return L

    IbA = [b.sb([128, 8192], F32, f"IbA{i}") for i in range(NSET)]
    MqA = [b.sb([128, 8192], BF16, f"MqA{i}") for i in range(NSET)]
    st = [{k: b.sb([128, n], F32, f"bs{k}{s}") for k, n in (("m1", 1), ("lo", 1), ("mid", 1), ("nmid", 1), ("cD", 1),
                                                           ("sA", 1), ("g", 1), ("W", NIT2))} for s in range(NSET)]

    def stageA_list(a, s):
        L = []

        def add(eng, fn, reads=(), writes=()):
            L.append((eng, fn, reads, writes))
        nb = a + 1
        nv = 512 * nb
        Ib_, Mq_ = IbA[s], MqA[s]
        S_ = st[s]
        m1, lo, mid, cD, sA, g, W = (S_[k] for k in ("m1", "lo", "mid", "cD", "sA", "g", "W"))
        add("dve", lambda e: e.tensor_reduce(out=m1[:], in_=Ib_[:, 0:nv], axis=AX.X, op=ALU.max, apply_absolute_value=True),
            [Ib_], [m1])
        add("dve", lambda e: e.tensor_tensor(out=Ib_[:, nv - 512:nv], in0=Ib_[:, nv - 512:nv], in1=negm[:], op=ALU.add),
            [Ib_, negm], [Ib_])
        add("dve", lambda e: e.tensor_scalar(out=W[:], in0=cW[:], scalar1=m1[:, 0:1], scalar2=None, op0=ALU.mult),
            [cW, m1], [W])
        add("dve", lambda e: e.tensor_tensor(out=W[:], in0=W[:], in1=cW[:], op=ALU.add), [W, cW], [W])
        add("dve", lambda e: e.memset(mid[:], 0.0), [], [mid])
        h = 512 * (nb // 2)
        thr = TOPK - 0.5 - 0.5 * h
        for i in range(NIT2):
            if h > 0:
                add("act", lambda e: e.activation(out=Mq_[:, 0:h], in_=Ib_[:, 0:h], func=AF.Sign, bias=mid[:, 0:1],
                                                  scale=-1.0, accum_out=sA[:, 0:1]), [Ib_, mid], [Mq_, sA])
            add("dve", lambda e: e.tensor_scalar(out=Mq_[:, h:nv], in0=Ib_[:, h:nv], scalar1=mid[:, 0:1], scalar2=0.0,
                                                 op0=ALU.is_ge, op1=ALU.add, accum_out=cD[:, 0:1]), [Ib_, mid], [Mq_, cD])
            if h > 0:
                add("dve", lambda e: e.scalar_tensor_tensor(out=cD[:], in0=sA[:], scalar=-0.5, in1=cD[:], op0=ALU.mult,
                                                            op1=ALU.add), [sA, cD], [cD])
            add("dve", lambda e, i=i: e.tensor_scalar(out=g[:], in0=cD[:], scalar1=thr, scalar2=W[:, i:i + 1],
                                                      op0=ALU.is_ge, op1=ALU.mult), [cD, W], [g])
            if i + 1 < NIT2:
                add("dve", lambda e, i=i: e.scalar_tensor_tensor(out=mid[:], in0=mid[:], scalar=W[:, i + 1:i + 2], in1=g[:],
                                                                 op0=ALU.subtract, op1=ALU.add), [mid, W, g], [mid])
            else:
                add("dve", lambda e, i=i: e.scalar_tensor_tensor(out=lo[:], in0=mid[:], scalar=W[:, i:i + 1], in1=g[:],
                                                                 op0=ALU.subtract, op1=ALU.add), [mid, W, g], [lo])
        for c0 in range(0, nv, 2048):
            c1 = min(nv, c0 + 2048)
            add("dve", lambda e, c0=c0, c1=c1: e.tensor_scalar(out=Mq_[:, c0:c1], in0=Ib_[:, c0:c1], scalar1=lo[:, 0:1],
                                                               scalar2=None, op0=ALU.is_ge), [Ib_, lo], [Mq_])
        add("sp", ("dma", lambda e: e.dma_start(out=scr["MQ"][a][:, 0:nv], in_=Mq_[:, 0:nv]), f"st_mq{s}"),
            [Mq_], [scr["MQ"][a]])
        return L

    def emit_item(it):
        eng, fn, reads, writes = it
        if isinstance(fn, tuple):
            b.op(eng, fn[1], reads, writes, dma=fn[2])
        else:
            b.op(eng, fn, reads, writes)

    def interleave(lists):
        lists = [l for l in lists if l]
        pos = [0] * len(lists)
        total = sum(len(l) for l in lists)
        for _ in range(total):
            best, bf = None, 2.0
            for li, l in enumerate(lists):
                if pos[li] < len(l):
                    f = pos[li] / len(l)
                    if f < bf:
                        best, bf = li, f
            emit_item(lists[best][pos[best]])
            pos[best] += 1

    for a in range(NS + 1):
        li = []
        if a < NS:
            load_idx_small(a, a % NSET)
            li = indexer_list(a, a % NSET, IbA[a % NSET])
        lb = stageA_list(a - 1, (a - 1) % NSET) if a >= 1 else []
        interleave([li, lb])
    b.release(mA)
    if o_tok is None:
        o_tok = [b.sb([128, 1024], BF16, f"otokb{a}", top=True) for a in range(NS)]

    Mqs = [b.sb([128, 8192], BF16, f"MqB{i}") for i in range(2)]
    MTs = [b.sb([128, 64, 128], BF16, f"MT{i}") for i in range(2)]
    ktp = [b.sb([128, 8192], BF16, f"ktp{i}") for i in range(2)]
    vp = [b.sb([128, 64, 130], BF16, f"vp{i}") for i in range(2)]
    for i in range(2):
        b.op("dve", lambda e, i=i: e.memset(vp[i][:, :, 0:1], 1.0), writes=[vp[i]])
        b.op("dve", lambda e, i=i: e.memset(vp[i][:, :, 129:130], 1.0), writes=[vp[i]])
    qTa = [b.sb([128, 8, 128], BF16, f"qTa{i}") for i in range(2)]
    pT = [b.sb([128, 512], BF16, f"pTd{i}") for i in range(3)]
    rec = [b.sb([128, 1], F32, f"recd{i}") for i in range(2)]
    pS = [b.banks[0], b.banks[1], b.banks[5]]
    pA = [b.banks[2], b.banks[3]]
    pMs = [b.banks16[6], b.banks16[7]]

    def load_slot_small(a):
        nv = 512 * (a + 1)
        b.op("sp", lambda e, a=a: e.dma_start(out=qTa[a % 2][:].rearrange("p a b -> p (a b)"), in_=scr["QT2"][a][:]),
             reads=[scr["QT2"][a]], writes=[qTa[a % 2]], dma=f"ld_qTa{a % 2}")
        b.op("sp", lambda e, a=a, nv=nv: e.dma_start(out=Mqs[a % 2][:, 0:nv], in_=scr["MQ"][a][:, 0:nv]),
             reads=[scr["MQ"][a]], writes=[Mqs[a % 2]], dma=f"ld_mq{a % 2}")

    def load_kv(a, hp):
        k = ctr["kv"] % 2
        ctr["kv"] += 1
        n = (a + 1) * 128
        yb = G1.yb[hp]
        b.op("sp", lambda e: e.dma_start(out=ktp[k][:].rearrange("p (r x) -> p r x", r=4)[:, :, 0:n],
                                         in_=yb[:, 0:n].rearrange("(r p) x -> p r x", p=128)),
             reads=[G1.yd[hp]], writes=[ktp[k]], dma=f"ld_ktp{k}")
        for r in range(4):
            b.op("sp", lambda e, r=r: e.dma_start(
                out=vp[k][:, r * 16:r * 16 + a + 1, 1:129],
                in_=yb[r * 128:(r + 1) * 128, 2048:2048 + n].rearrange("p (a e) -> p a e", e=128)),
                reads=[G1.yd[hp]], writes=[vp[k]], dma=f"ld_vp{k}")
        return k

    def pre_list(a):
        L = []
        Mq, MT = Mqs[a % 2], MTs[a % 2]
        nt = 4 * (a + 1)
        ngr = (nt + 7) // 8
        for gi in range(ngr + 1):
            if gi < ngr:
                pm = pMs[gi % 2]
                for kt in range(gi * 8, min(nt, gi * 8 + 8)):
                    L.append(("pe", lambda e, kt=kt, pm=pm: e.transpose(out=pm[:, (kt % 8) * 128:(kt % 8 + 1) * 128],
                                                                        in_=Mq[:, kt * 128:(kt + 1) * 128], identity=idt[:]),
                              [Mq, idt], [pm]))
            if gi >= 1:
                g0 = gi - 1
                pm = pMs[g0 % 2]
                k0 = g0 * 8
                n = min(nt, k0 + 8) - k0
                L.append(("act", lambda e, pm=pm, k0=k0, n=n: e.activation(
                    out=MT[:, k0:k0 + n, :].rearrange("p a b -> p (a b)"), in_=pm[:, 0:n * 128], func=AF.Copy),
                    [pm], [MT]))
        return L

    def qk(u, n, kbuf):
        a, hp, hh, blk = u
        ps = pS[n % 3]
        qa = qTa[a % 2]
        for i in range(4):
            kt = 16 * i + blk
            b.op("pe", lambda e, i=i, kt=kt, ps=ps, qa=qa, hh=hh, hp=hp, kbuf=kbuf: e.matmul(
                ps[:, i * 128:(i + 1) * 128], lhsT=ktp[kbuf][64 * hh:64 * hh + 64, kt * 128:(kt + 1) * 128],
                rhs=qa[64 * hh:64 * hh + 64, hp, :], start=True, stop=True), reads=[ktp[kbuf], qa], writes=[ps])

    def softmax_pv(u, n, kbuf):
        a, hp, hh, blk = u
        ps, pt = pS[n % 3], pT[n % 3]
        if blk == 0:
            ctr["acc"] += 1
        acc = pA[ctr["acc"] % 2]
        b.op("act", lambda e: e.activation(out=pt[:], in_=ps[:], func=AF.Exp), reads=[ps], writes=[pt])
        MT = MTs[a % 2]
        b.op("dve", lambda e: e.tensor_tensor(out=pt[:], in0=pt[:], in1=MT[:, 4 * blk:4 * blk + 4, :].rearrange("p a b -> p (a b)"),
                                              op=ALU.mult), reads=[pt, MT], writes=[pt])
        for i in range(4):
            kt = 16 * i + blk
            b.op("pe", lambda e, i=i, kt=kt: e.matmul(acc[:, 0:65], lhsT=pt[:, i * 128:(i + 1) * 128],
                                                      rhs=vp[kbuf][:, kt, hh * 65:(hh + 1) * 65],
                                                      start=(blk == 0 and i == 0), stop=(blk == a and i == 3)),
                 reads=[pt, vp[kbuf]], writes=[acc])
        if blk == a:
            head = 2 * hp + hh
            r = rec[ctr["acc"] % 2]
            sc, v0 = (0, 1) if hh == 0 else (64, 0)
            b.op("dve", lambda e: e.reciprocal(out=r[:], in_=acc[:, sc:sc + 1]), reads=[acc], writes=[r])
            b.op("dve", lambda e: e.tensor_scalar(out=o_tok[a][:, head * 64:(head + 1) * 64], in0=acc[:, v0:v0 + 64],
                                                  scalar1=r[:, 0:1], scalar2=None, op0=ALU.mult),
                 reads=[acc, r], writes=[o_tok[a]])

    load_slot_small(0)
    for it in pre_list(0):
        b.op(*it)
    for a in range(NS):
        if a + 1 < NS:
            load_slot_small(a + 1)
        kb_next = load_kv(a, 0)
        nxt = pre_list(a + 1) if a + 1 < NS else []
        done = 0
        units = [(a, hp, hh, blk) for hp in range(8) for hh in range(2) for blk in range(a + 1)]
        kbufs = {0: kb_next}
        kbufs[1] = load_kv(a, 1)
        qk(units[0], 0, kbufs[0])
        if len(units) > 1:
            qk(units[1], 1, kbufs[units[1][1]])
        for n, u in enumerate(units):
            _, hp, hh, blk = u
            if hh == 0 and blk == 0 and 1 <= hp and hp + 1 < 8:
                kbufs[hp + 1] = load_kv(a, hp + 1)
            if n + 2 < len(units):
                qk(units[n + 2], n + 2, kbufs[units[n + 2][1]])
            softmax_pv(u, n, kbufs[hp])
            want = (n + 1) * len(nxt) // len(units)
            while done < want:
                b.op(*nxt[done])
                done += 1
        while done < len(nxt):
            b.op(*nxt[done])
            done += 1
    return o_tok


BF = ml_dtypes.bfloat16


def rep(v, n=128):
    return np.ascontiguousarray(np.tile(np.asarray(v).reshape(1, -1), (n, 1)))


def own_tiles(arr_bs, c):
    bb, j = c // 4, c % 4
    a = arr_bs[bb]
    return np.ascontiguousarray(a.reshape(64, 128, *a.shape[1:])[j::4].reshape(2048, *a.shape[1:]))


def gather_tiles(per_core, bb):
    out = np.empty((64,) + per_core[0].shape[1:], per_core[0].dtype)
    for j in range(4):
        out[j::4] = per_core[bb * 4 + j]
    return out


def diff_masks(c):
    j = c % 4
    m = np.zeros((128, 4, 128), np.float32)
    for i in range(4):
        if i < j:
            m[:, i, :] = 1.0
        elif i == j:
            m[0:64, i, :] = 1.0
            m[64:128, i, 64:128] = 1.0
    return m.reshape(128, 512).astype(BF)


def dsa_negmask(c):
    return np.where(diff_masks(c).astype(np.float32).reshape(128, 4, 128).transpose(2, 1, 0).reshape(128, 512) > 0,
                    0.0, -1e30).astype(np.float32)


def build_L1():
    nc = bass.Bass("TRN2", target_bir_lowering=False)
    b = B(nc)
    io = {
        "x": b.dram("x", [2048, 1024], F32, "ExternalInput"),
        "pos": b.dram("pos", [128, 16], I32, "ExternalInput"),
        "gmix": b.dram("gmix", [128, 1024], F32, "ExternalInput"),
        "gqk": b.dram("gqk", [128, 2048], F32, "ExternalInput"),
        "w_in": b.dram("w_in", [1024, 3072], F32, "ExternalInput"),
        "QT": b.dram("QT", [16, 128, 1024], BF16, "ExternalOutput"),
        "KT": b.dram("KT", [16, 128, 1024], BF16, "ExternalOutput"),
        "V": b.dram("V", [16, 128, 8 * 129], BF16, "ExternalOutput"),
    }
    phase_diff_proj(b, io)
    b.finish()
    return nc


def build_L2():
    nc = bass.Bass("TRN2", target_bir_lowering=False)
    b = B(nc)
    io = {
        "QT": b.dram("QT", [16, 128, 1024], BF16, "ExternalInput"),
        "KTall": b.dram("KTall", [8, 128, 8192], BF16, "ExternalInput"),
        "Vall": b.dram("Vall", [8, 128, 64 * 129], BF16, "ExternalInput"),
        "maskT": b.dram("maskT", [128, 512], BF16, "ExternalInput"),
        "lam": b.dram("lam", [128, 256], F32, "ExternalInput"),
        "gsub": b.dram("gsub", [128, 128], F32, "ExternalInput"),
        "x": b.dram("x", [2048, 1024], F32, "ExternalInput"),
        "pos": b.dram("pos", [128, 16], I32, "ExternalInput"),
        "w_out": b.dram("w_out", [1024, 1024], F32, "ExternalInput"),
        "gmlp": b.dram("gmlp", [128, 1024], F32, "ExternalInput"),
        "w1": b.dram("w1", [1024, 4096], F32, "ExternalInput"),
        "w2": b.dram("w2", [4096, 1024], F32, "ExternalInput"),
        "gmix1": b.dram("gmix1", [128, 1024], F32, "ExternalInput"),
        "gqk2": b.dram("gqk2", [128, 2048], F32, "ExternalInput"),
        "gcq": b.dram("gcq", [128, 256], F32, "ExternalInput"),
        "w_in2": b.dram("w_in2", [1024, 2376], F32, "ExternalInput"),
        "w_uq": b.dram("w_uq", [256, 1024], F32, "ExternalInput"),
        "w_uqi": b.dram("w_uqi", [256, 512], F32, "ExternalInput"),
        "X2": b.dram("X2", [2048, 1024], F32, "ExternalOutput"),
        "QT2": b.dram("QT2", [16, 128, 1024], BF16, "ExternalOutput"),
        "KT2": b.dram("KT2", [16, 128, 1024], BF16, "ExternalOutput"),
        "V2": b.dram("V2", [16, 128, 16 * 65], BF16, "ExternalOutput"),
        "KI": b.dram("KI", [16, 128, 128], BF16, "ExternalOutput"),
        "QI": b.dram("QI", [16, 128, 512], BF16, "ExternalOutput"),
        "SG": b.dram("SG", [16, 128, 8], F32, "ExternalOutput"),
    }
    b.ident(); b.eps()
    pos_t = b.sb([128, NS], I32, "pos")
    b.op("sp", lambda e: e.dma_start(out=pos_t[:], in_=io["pos"][:]), writes=[pos_t], dma="ld_pos")
    cos, sin = rope_tables(b, pos_t)
    o_tok = [b.sb([128, 1024], BF16, f"otok{a}", top=True) for a in range(NS)]
    m1 = b.mark()
    phase_diff_attn(b, io, o_tok)
    b.release(m1)
    x_res = [b.sb([128, 1024], F32, f"xres{a}") for a in range(NS)]
    h2T = b.sb([128, 8, 2048], BF16, "h2T")
    m2 = b.mark()
    phase_post_attn(b, io, o_tok, x_res, h2T, io["w_out"][:], io["gmlp"][:], io["x"])
    b.release(m2)
    b.hi = ARENA_END
    m3 = b.mark()
    phase_mlp(b, x_res, h2T, io["w1"], io["w2"])
    b.release(m3)
    for a in range(NS):
        b.store("sp", lambda e, a=a: e.dma_start(out=io["X2"][a * 128:(a + 1) * 128, :], in_=x_res[a][:]),
                reads=[x_res[a]], dma="st_x2")
    phase_dsa_proj(b, io, x_res, cos, sin)
    b.finish()
    return nc


def l1_inputs(inp, c):
    return {"x": own_tiles(inp["x"], c),
            "pos": np.ascontiguousarray(own_tiles(inp["positions"], c).reshape(16, 128).T),
            "gmix": rep(inp["norm_mix"][0]),
            "gqk": rep(np.concatenate([np.tile(inp["diff_q_norm"][0], 16), np.tile(inp["diff_k_norm"][0], 16)])),
            "w_in": np.ascontiguousarray(inp["diff_w_in"][0])}


def l2_inputs(inp, r1):
    KTall = []; Vall = []
    for bb in range(2):
        kt = gather_tiles([r["KT"].reshape(16, 128, 8, 128) for r in r1], bb)
        KTall.append(np.ascontiguousarray(kt.transpose(2, 1, 0, 3).reshape(8, 128, 8192)))
        v = gather_tiles([r["V"].reshape(16, 128, 8, 129) for r in r1], bb)
        Vall.append(np.ascontiguousarray(v.transpose(2, 1, 0, 3).reshape(8, 128, 64 * 129)))
    lam = rep(np.concatenate([inp["diff_lam_q1"][0], inp["diff_lam_k1"][0], inp["diff_lam_q2"][0], inp["diff_lam_k2"][0]]))
    gqk2 = rep(np.concatenate([np.tile(inp["dsa_q_norm"][0], 16), np.tile(inp["dsa_k_norm"][0], 16)]))
    ins = []
    for c in range(8):
        ins.append({"QT": r1[c]["QT"], "KTall": KTall[c // 4], "Vall": Vall[c // 4], "maskT": diff_masks(c),
                    "lam": lam, "gsub": rep(inp["diff_subln"][0]),
                    "x": own_tiles(inp["x"], c),
                    "pos": np.ascontiguousarray(own_tiles(inp["positions"], c).reshape(16, 128).T),
                    "w_out": np.ascontiguousarray(inp["diff_w_out"][0]), "gmlp": rep(inp["norm_mlp"][0]),
                    "w1": np.ascontiguousarray(inp["mlp_w1"][0]), "w2": np.ascontiguousarray(inp["mlp_w2"][0]),
                    "gmix1": rep(inp["norm_mix"][1]), "gqk2": gqk2, "gcq": rep(inp["dsa_cq_norm"][0]),
                    "w_in2": np.ascontiguousarray(inp["dsa_w_in"][0]), "w_uq": np.ascontiguousarray(inp["dsa_w_uq"][0]),
                    "w_uqi": np.ascontiguousarray(inp["dsa_w_uq_idx"][0])})
    return ins


def run(nc, ins):
    res = run_bass_kernel_spmd(nc, ins, core_ids=list(range(8)))
    return [{k: np.asarray(v) for k, v in r.items()} for r in res.results]


def build_L3():
    nc = bass.Bass("TRN2", target_bir_lowering=False)
    b = B(nc)
    io = {
        "QT2": b.dram("QT2", [16, 128, 1024], BF16, "ExternalInput"),
        "QI": b.dram("QI", [16, 128, 512], BF16, "ExternalInput"),
        "SG": b.dram("SG", [16, 128, 8], F32, "ExternalInput"),
        "KIall": b.dram("KIall", [128, 8192], BF16, "ExternalInput"),
        "KT2all": b.dram("KT2all", [8, 128, 8192], BF16, "ExternalInput"),
        "V2all": b.dram("V2all", [8, 128, 64 * 130], BF16, "ExternalInput"),
        "negmask": b.dram("negmask", [128, 512], F32, "ExternalInput"),
        "X2": b.dram("X2", [2048, 1024], F32, "ExternalInput"),
        "w_out": b.dram("w_out", [1024, 1024], F32, "ExternalInput"),
        "gmlp": b.dram("gmlp", [128, 1024], F32, "ExternalInput"),
        "w1": b.dram("w1", [1024, 4096], F32, "ExternalInput"),
        "w2": b.dram("w2", [4096, 1024], F32, "ExternalInput"),
        "OUT": b.dram("OUT", [2048, 1024], F32, "ExternalOutput"),
    }
    b.ident(); b.eps()
    o_tok = [b.sb([128, 1024], BF16, f"otok{a}", top=True) for a in range(NS)]
    m1 = b.mark()
    phase_dsa_attn(b, io, o_tok)
    b.release(m1)
    x_res = [b.sb([128, 1024], F32, f"xres{a}") for a in range(NS)]
    h2T = b.sb([128, 8, 2048], BF16, "h2T")
    m2 = b.mark()
    phase_post_attn(b, io, o_tok, x_res, h2T, io["w_out"][:], io["gmlp"][:], io["X2"])
    b.release(m2)
    b.hi = ARENA_END
    m3 = b.mark()
    phase_mlp(b, x_res, h2T, io["w1"], io["w2"])
    b.release(m3)
    for a in range(NS):
        b.store("sp", lambda e, a=a: e.dma_start(out=io["OUT"][a * 128:(a + 1) * 128, :], in_=x_res[a][:]),
                reads=[x_res[a]], dma="st_out")
    b.finish()
    return nc


def l3_inputs(inp, r2):
    KIall = []; KTall = []; Vall = []
    for bb in range(2):
        ki = gather_tiles([r["KI"] for r in r2], bb)
        KIall.append(np.ascontiguousarray(ki.transpose(1, 0, 2).reshape(128, 8192)))
        kt = gather_tiles([r["KT2"].reshape(16, 128, 8, 128) for r in r2], bb)
        KTall.append(np.ascontiguousarray(kt.transpose(2, 1, 0, 3).reshape(8, 128, 8192)))
        v = gather_tiles([r["V2"].reshape(16, 128, 8, 130) for r in r2], bb)
        Vall.append(np.ascontiguousarray(v.transpose(2, 1, 0, 3).reshape(8, 128, 64 * 130)))
    ins = []
    for c in range(8):
        ins.append({"QT2": r2[c]["QT2"], "QI": r2[c]["QI"], "SG": r2[c]["SG"], "KIall": KIall[c // 4],
                    "KT2all": KTall[c // 4], "V2all": Vall[c // 4], "negmask": dsa_negmask(c),
                    "X2": r2[c]["X2"], "w_out": np.ascontiguousarray(inp["dsa_w_out"][0]), "gmlp": rep(inp["norm_mlp"][1]),
                    "w1": np.ascontiguousarray(inp["mlp_w1"][1]), "w2": np.ascontiguousarray(inp["mlp_w2"][1])})
    return ins


def assemble(r3):
    out = np.empty((2, 8192, 1024), np.float32)
    for bb in range(2):
        o = gather_tiles([r["OUT"].reshape(16, 128, 1024) for r in r3], bb)
        out[bb] = o.reshape(8192, 1024)
    return out


FUSED_IN = [
    ("x", [2048, 1024], F32), ("pos", [128, 16], I32), ("gmix", [128, 1024], F32), ("gqk", [128, 2048], F32),
    ("w_in", [1024, 3072], F32), ("maskT", [128, 512], BF16), ("lam", [128, 256], F32), ("gsub", [128, 128], F32),
    ("w_out", [1024, 1024], F32), ("gmlp", [128, 1024], F32), ("w1", [1024, 4096], F32), ("w2", [4096, 1024], F32),
    ("gmix1", [128, 1024], F32), ("gqk2", [128, 2048], F32), ("gcq", [128, 256], F32), ("w_in2", [1024, 2376], F32),
    ("w_uq", [256, 1024], F32), ("w_uqi", [256, 512], F32), ("negmask", [128, 512], F32),
    ("w_outb", [1024, 1024], F32), ("gmlpb", [128, 1024], F32), ("w1b", [1024, 4096], F32), ("w2b", [4096, 1024], F32),
]


def build_fused():
    nc = bass.Bass("TRN2", target_bir_lowering=False)
    _RANK.clear()
    b = B(nc)
    io = {n: b.dram(n, s, d, "ExternalInput") for (n, s, d) in FUSED_IN}
    io["OUT"] = b.dram("OUT", [2048, 1024], F32, "ExternalOutput")
    qt2 = nc.dram_tensor("scr_qt2", [16, 128, 1024], BF16).ap()
    qi = nc.dram_tensor("scr_qi", [16, 128, 512], BF16).ap()
    sg = nc.dram_tensor("scr_sg", [16, 128, 8], F32).ap()
    x2 = nc.dram_tensor("scr_x2", [2048, 1024], F32).ap()
    mqd = nc.dram_tensor("scr_mq", [16, 128, 8192], BF16).ap()
    scr = {"QT2": [T(qt2[a]) for a in range(NS)], "QI": [T(qi[a]) for a in range(NS)], "SG": [T(sg[a]) for a in range(NS)],
           "MQ": [T(mqd[a]) for a in range(NS)]}
    x2d = [T(x2[a * 128:(a + 1) * 128, :]) for a in range(NS)]

    b.ident(); b.eps()
    pos_t = b.sb([128, NS], I32, "pos")
    b.op("sp", lambda e: e.dma_start(out=pos_t[:], in_=io["pos"][:]), writes=[pos_t], dma="ld_pos")
    cos, sin = rope_tables(b, pos_t)
    o_tok = [b.sb([128, 1024], BF16, f"otok{a}", top=True) for a in range(NS)]
    hi_otok = b.hi
    qT_res = b.sb([128, NS, 1024], BF16, "qTres", top=True)
    zt = b.sb([128, 2048], BF16, "zeros")
    b.op("dve", lambda e: e.memset(zt[:], 0.0), writes=[zt])
    m0 = b.mark()
    G0 = Gather(b, "g0", 8, 4096, zt, None)
    wst = [T(nc.alloc_sbuf_tensor_at(f"wstage{i}", [128, 8, 512], F32, offset=ARENA_END - (i + 1) * 16384)) for i in range(2)]
    f_diff_proj(b, io, qT_res, G0, cos, sin, wstage=wst)
    b.release(m0)
    G1 = Gather(b, "g1", 8, 4096, zt, "pool")
    GK = Gather(b, "gk", 1, 2048, zt, "pool")
    f_diff_attn(b, io, o_tok, qT_res, G0)
    b.release(m0)
    b.hi = hi_otok
    x_res = [b.sb([128, 1024], F32, f"xres{a}") for a in range(NS)]
    mh = b.mark()
    h2T = b.sb([128, 8, 2048], BF16, "h2T")
    m2 = b.mark()
    phase_post_attn(b, io, o_tok, x_res, h2T, io["w_out"][:], io["gmlp"][:], io["x"])
    b.release(m2)
    b.hi = ARENA_END
    pre = dsa_proj_weights(b, io, top=True)
    phase_mlp(b, x_res, h2T, io["w1"], io["w2"], hook=pre["load"])
    b.release((mh[0], b.hi))
    for a in range(NS):
        b.op("sp", lambda e, a=a: e.dma_start(out=x2d[a][:], in_=x_res[a][:]), reads=[x_res[a]], writes=[x2d[a]], dma="st_x2")
    io2 = dict(io)
    f_dsa_proj(b, io2, x_res, cos, sin, G1, GK, scr, pre=pre)
    b.release(m0)
    b.hi = ARENA_END
    m4 = b.mark()
    o_tok = f_dsa_attn(b, io, None, G1, GK, scr)
    b.release((m4[0], b.hi))
    x_res = [b.sb([128, 1024], F32, f"xresb{a}") for a in range(NS)]
    h2T = b.sb([128, 8, 2048], BF16, "h2Tb")
    m5 = b.mark()
    phase_post_attn(b, io, o_tok, x_res, h2T, io["w_outb"][:], io["gmlpb"][:], x2, xdeps=x2d)
    b.release(m5)
    b.hi = ARENA_END
    phase_mlp(b, x_res, h2T, io["w1b"], io["w2b"])
    for a in range(NS):
        b.store("sp", lambda e, a=a: e.dma_start(out=io["OUT"][a * 128:(a + 1) * 128, :], in_=x_res[a][:]),
                reads=[x_res[a]], dma="st_out")
    b.finish()
    return nc


def fused_inputs(inp):
    lam = rep(np.concatenate([inp["diff_lam_q1"][0], inp["diff_lam_k1"][0], inp["diff_lam_q2"][0], inp["diff_lam_k2"][0]]))
    gqk = rep(np.concatenate([np.tile(inp["diff_q_norm"][0], 16), np.tile(inp["diff_k_norm"][0], 16)]))
    gqk2 = rep(np.concatenate([np.tile(inp["dsa_q_norm"][0], 16), np.tile(inp["dsa_k_norm"][0], 16)]))
    c_ = np.ascontiguousarray
    shared = {"gmix": rep(inp["norm_mix"][0]), "gqk": gqk, "w_in": c_(inp["diff_w_in"][0]), "lam": lam,
              "gsub": rep(inp["diff_subln"][0]), "w_out": c_(inp["diff_w_out"][0]), "gmlp": rep(inp["norm_mlp"][0]),
              "w1": c_(inp["mlp_w1"][0]), "w2": c_(inp["mlp_w2"][0]), "gmix1": rep(inp["norm_mix"][1]), "gqk2": gqk2,
              "gcq": rep(inp["dsa_cq_norm"][0]), "w_in2": c_(inp["dsa_w_in"][0]), "w_uq": c_(inp["dsa_w_uq"][0]),
              "w_uqi": c_(inp["dsa_w_uq_idx"][0]), "w_outb": c_(inp["dsa_w_out"][0]), "gmlpb": rep(inp["norm_mlp"][1]),
              "w1b": c_(inp["mlp_w1"][1]), "w2b": c_(inp["mlp_w2"][1])}
    ins = []
    for c in range(8):
        d = dict(shared)
        d["x"] = own_tiles(inp["x"], c)
        d["pos"] = np.ascontiguousarray(own_tiles(inp["positions"], c).reshape(16, 128).T)
        d["maskT"] = diff_masks(c)
        d["negmask"] = dsa_negmask(c)
        ins.append(d)
    return ins


def kernel(**inputs):
    inp = {k: np.asarray(v) for k, v in inputs.items()}
    r = run(build_fused(), fused_inputs(inp))
    return assemble(r)
```
